# Optimizing a Trainium2 kernel written in Bass

```python
import math
import jax, jax.numpy as jnp
from jax import lax
import numpy as np

D_MODEL = 1024
BATCH = 8
SEQ = 2048
DEPTH = 1
DEC_BATCH = 128
DEC_SEQ = 4
PAST_LEN = 16384
PAGE_SIZE = 128

HGRN_DK = 128
HGRN_DV = 128
HGRN_HEADS = D_MODEL // HGRN_DK
HGRN_WIDTH = HGRN_HEADS * HGRN_DK
HGRN_VWIDTH = HGRN_HEADS * HGRN_DV
HGRN_CHUNK = 64
RWKV_N = 64
RWKV_HEADS = D_MODEL // RWKV_N
RWKV_WIDTH = RWKV_HEADS * RWKV_N
W_LORA = max(32, int(round(1.8 * D_MODEL ** 0.5 / 32)) * 32)
A_LORA = max(32, int(round(1.8 * D_MODEL ** 0.5 / 32)) * 32)
G_LORA = max(32, int(round(0.6 * D_MODEL ** 0.8 / 32)) * 32)
RWKV_SIZES = (RWKV_WIDTH, RWKV_WIDTH, RWKV_WIDTH, W_LORA, A_LORA, G_LORA)
RWKV_COLS = 3 * RWKV_WIDTH + W_LORA + A_LORA + G_LORA
IN_SIZES = (HGRN_WIDTH, HGRN_WIDTH, HGRN_VWIDTH, HGRN_VWIDTH, RWKV_COLS, D_MODEL, D_MODEL)
IN_COLS = 2 * HGRN_WIDTH + 2 * HGRN_VWIDTH + RWKV_COLS + 2 * D_MODEL
D_FF = 4 * D_MODEL
RMS_EPS = 1e-6
LNX_EPS = 64e-5

kernel_name = 'hybrid_hgrn2_rwkv7_adaln_step'


def _split(a, sizes):
    idx = np.cumsum(np.array(sizes))[:-1].tolist()
    return jnp.split(a, idx, axis=-1)


def rms_norm(x, w):
    xf = x.astype(jnp.float32)
    y = xf * lax.rsqrt(jnp.mean(xf * xf, axis=-1, keepdims=True) + RMS_EPS)
    return (y * w.astype(jnp.float32)).astype(x.dtype)


def hgrn2_chunked(q, log_f, k, v, s0):
    B, T, H, K = q.shape
    V = v.shape[-1]
    C = min(HGRN_CHUNK, T)
    pad = (-T) % C
    if pad:
        pw = ((0, 0), (0, pad), (0, 0), (0, 0))
        q, log_f, k, v = (jnp.pad(a, pw) for a in (q, log_f, k, v))
    n = (T + pad) // C

    def to_chunks(a):
        return a.reshape(B, n, C, H, a.shape[-1]).transpose(1, 0, 3, 2, 4)

    causal = jnp.tril(jnp.ones((C, C), dtype=bool))[:, :, None]

    def step(S, inp):
        qc, lfc, kc, vc = inp
        b = jnp.cumsum(lfc, axis=2)
        diff = b[:, :, :, None, :] - b[:, :, None, :, :]
        decay = jnp.exp(jnp.where(causal, diff, -jnp.inf))
        attn = jnp.einsum('bhtk,bhsk,bhtsk->bhts', qc, kc, decay)
        o_intra = jnp.einsum('bhts,bhsv->bhtv', attn, vc)
        o_inter = jnp.einsum('bhtk,bhkv->bhtv', qc * jnp.exp(b), S)
        b_last = b[:, :, -1:, :]
        S_new = jnp.exp(b_last[:, :, 0, :])[..., None] * S + jnp.einsum(
            'bhsk,bhsv->bhkv', kc * jnp.exp(b_last - b), vc)
        return S_new, o_intra + o_inter

    S, o = lax.scan(step, s0, (to_chunks(q), to_chunks(log_f), to_chunks(k), to_chunks(v)))
    o = o.transpose(1, 0, 3, 2, 4).reshape(B, n * C, H, V)[:, :T]
    return o, S


def rwkv7_scan(r, log_w, k, v, kk, a, s0):
    def step(S, inp):
        r_t, lw_t, k_t, v_t, kk_t, a_t = inp
        sk = jnp.einsum('bhij,bhj->bhi', S, kk_t)
        S = (S * jnp.exp(lw_t)[:, :, None, :]
             - sk[..., None] * (kk_t * a_t)[:, :, None, :]
             + v_t[..., None] * k_t[:, :, None, :])
        o = jnp.einsum('bhij,bhj->bhi', S, r_t)
        return S, o

    xs = tuple(jnp.moveaxis(t, 1, 0) for t in (r, log_w, k, v, kk, a))
    S, o = lax.scan(step, s0, xs)
    return jnp.moveaxis(o, 0, 1), S


def _layer(x, c, s_hgrn, s_rwkv, s_shift, lb, p):
    (norm1_w, norm2_w, ada_w, ada_b, w_in, hgrn_norm_w, mu, w0, w2, a0, a2, g2,
     k_k, k_a, r_k, lnx_w, lnx_b, w_out, w_up, w_down) = p
    f32 = jnp.float32
    B, T, _ = x.shape
    dt = x.dtype
    mod = (jax.nn.silu(c) @ ada_w + ada_b)[:, None, :]
    sh1, sc1, gt1, sh2, sc2, gt2 = jnp.split(mod, 6, axis=-1)

    h = rms_norm(x, norm1_w) * (1 + sc1) + sh1
    P = h @ w_in
    q, f_pre, i_val, og, rc, gate_a, gate_b = _split(P, IN_SIZES)

    f = lb + (1.0 - lb) * jax.nn.sigmoid(f_pre.astype(f32))
    log_f = jnp.log(f)
    k_in = 1.0 - f
    heads_a = lambda t, n: t.astype(f32).reshape(B, T, HGRN_HEADS, n)
    o_a, s_hgrn_new = hgrn2_chunked(heads_a(q, HGRN_DK), log_f.reshape(B, T, HGRN_HEADS, HGRN_DK),
                                    k_in.reshape(B, T, HGRN_HEADS, HGRN_DK), heads_a(i_val, HGRN_DV),
                                    s_hgrn.astype(f32))
    o_a = (o_a * lax.rsqrt(jnp.mean(o_a * o_a, axis=-1, keepdims=True) + RMS_EPS)
           * hgrn_norm_w.astype(f32) * jax.nn.silu(heads_a(og, HGRN_DV)))
    o_a = o_a.reshape(B, T, HGRN_VWIDTH).astype(dt)

    prev = jnp.concatenate([s_shift[:, None, :].astype(rc.dtype), rc[:, :-1]], axis=1)
    xs = rc + mu * (prev - rc)
    r, kb, v, xw, xa, xg = _split(xs, RWKV_SIZES)
    w = (w0 + jnp.tanh(xw) @ w2).astype(f32)
    log_w = -jnp.exp(-jax.nn.softplus(-w) - 0.5)
    a = jax.nn.sigmoid((a0 + xa @ a2).astype(f32))
    g = (jax.nn.sigmoid(xg) @ g2).astype(f32)
    heads_b = lambda t: t.astype(f32).reshape(B, T, RWKV_HEADS, RWKV_N)
    kk = heads_b(kb * k_k)
    kk = kk / jnp.maximum(jnp.sqrt(jnp.sum(kk * kk, axis=-1, keepdims=True)), 1e-12)
    kmod = kb.astype(f32) * (1.0 + (a - 1.0) * k_a.astype(f32))
    r_h, k_h, v_h, a_h = heads_b(r), heads_b(kmod), heads_b(v), heads_b(a)
    o_b, s_rwkv_new = rwkv7_scan(r_h, heads_b(log_w), k_h, v_h, kk, a_h, s_rwkv.astype(f32))
    mean = jnp.mean(o_b, axis=-1, keepdims=True)
    var = jnp.mean(jnp.square(o_b - mean), axis=-1, keepdims=True)
    o_b = ((o_b - mean) * lax.rsqrt(var + LNX_EPS)).reshape(B, T, RWKV_WIDTH)
    o_b = o_b * lnx_w.astype(f32) + lnx_b.astype(f32)
    bonus = jnp.sum(r_h * k_h * r_k.astype(f32), axis=-1, keepdims=True) * v_h
    o_b = ((o_b + bonus.reshape(B, T, RWKV_WIDTH)) * g).astype(dt)

    m = jax.nn.sigmoid(gate_a) * o_a + jax.nn.sigmoid(gate_b) * o_b
    x = x + gt1 * (m @ w_out)

    h2 = rms_norm(x, norm2_w) * (1 + sc2) + sh2
    x = x + gt2 * (jnp.square(jax.nn.relu(h2 @ w_up)) @ w_down)
    return (x, s_hgrn_new.astype(s_hgrn.dtype), s_rwkv_new.astype(s_rwkv.dtype),
            rc[:, -1].astype(s_shift.dtype))


def _run(x, c, s_h, s_r, s_s, lb_all, layer_params, final_norm_w):
    hs, rs, ss = [], [], []
    for l in range(DEPTH):
        x, h_new, r_new, s_new = _layer(x, c, s_h[l], s_r[l], s_s[l], lb_all[l],
                                        tuple(p[l] for p in layer_params))
        hs.append(h_new)
        rs.append(r_new)
        ss.append(s_new)
    return rms_norm(x, final_norm_w), jnp.stack(hs), jnp.stack(rs), jnp.stack(ss)


def setup_inputs(seed: int = 0) -> dict:
    key = jax.random.key(seed)
    ks = iter(list(jax.random.split(key, 48)))
    L, D = DEPTH, D_MODEL

    def nrm(shape, scale):
        return scale * jax.random.normal(next(ks), shape, jnp.float32)

    def gain(shape):
        return 1.0 + nrm(shape, 0.02)

    return {
        'x_prompt': nrm((BATCH, SEQ, D), 1.0),
        'x_sample': nrm((DEC_BATCH, DEC_SEQ, D), 1.0),
        'state_hgrn': nrm((L, DEC_BATCH, HGRN_HEADS, HGRN_DK, HGRN_DV), 0.5),
        'state_rwkv': nrm((L, DEC_BATCH, RWKV_HEADS, RWKV_N, RWKV_N), 0.5),
        'state_shift': nrm((L, DEC_BATCH, RWKV_COLS), 1.0),
        'c_prompt': nrm((BATCH, D), 1.0),
        'c_sample': nrm((DEC_BATCH, D), 1.0),
        'norm1_w': gain((L, D)),
        'norm2_w': gain((L, D)),
        'ada_w': nrm((L, D, 6 * D), 0.3 * D ** -0.5),
        'ada_b': nrm((L, 6 * D), 0.02),
        'w_in': nrm((L, D, IN_COLS), D ** -0.5),
        'lb_logits': nrm((L + 1, HGRN_WIDTH), 0.5),
        'hgrn_norm_w': gain((L, HGRN_DV)),
        'rwkv_mu': jax.random.uniform(next(ks), (L, RWKV_COLS), jnp.float32, 0.1, 0.9),
        'rwkv_w0': -2.5 + nrm((L, RWKV_WIDTH), 1.0),
        'rwkv_w2': nrm((L, W_LORA, RWKV_WIDTH), 0.1 * W_LORA ** -0.5),
        'rwkv_a0': nrm((L, RWKV_WIDTH), 0.5),
        'rwkv_a2': nrm((L, A_LORA, RWKV_WIDTH), 0.5 * A_LORA ** -0.5),
        'rwkv_g2': nrm((L, G_LORA, RWKV_WIDTH), G_LORA ** -0.5),
        'rwkv_k_k': 0.85 + nrm((L, RWKV_WIDTH), 0.05),
        'rwkv_k_a': 1.0 + nrm((L, RWKV_WIDTH), 0.05),
        'rwkv_r_k': nrm((L, RWKV_HEADS, RWKV_N), 0.1),
        'rwkv_lnx_w': gain((L, RWKV_WIDTH)),
        'rwkv_lnx_b': nrm((L, RWKV_WIDTH), 0.02),
        'w_out': nrm((L, D, D), D ** -0.5),
        'w_up': nrm((L, D, D_FF), D ** -0.5),
        'w_down': nrm((L, D_FF, D), D_FF ** -0.5),
        'final_norm_w': gain((D,)),
    }


def reference(x_prompt, x_sample, state_hgrn, state_rwkv, state_shift, c_prompt, c_sample,
              norm1_w, norm2_w, ada_w, ada_b, w_in, lb_logits, hgrn_norm_w, rwkv_mu, rwkv_w0,
              rwkv_w2, rwkv_a0, rwkv_a2, rwkv_g2, rwkv_k_k, rwkv_k_a, rwkv_r_k, rwkv_lnx_w,
              rwkv_lnx_b, w_out, w_up, w_down, final_norm_w):
    layer_params = (norm1_w, norm2_w, ada_w, ada_b, w_in, hgrn_norm_w, rwkv_mu, rwkv_w0, rwkv_w2,
                    rwkv_a0, rwkv_a2, rwkv_g2, rwkv_k_k, rwkv_k_a, rwkv_r_k, rwkv_lnx_w,
                    rwkv_lnx_b, w_out, w_up, w_down)
    lb_all = jnp.cumsum(jax.nn.softmax(lb_logits.astype(jnp.float32), axis=0), axis=0)
    zh = jnp.zeros((DEPTH, BATCH, HGRN_HEADS, HGRN_DK, HGRN_DV), state_hgrn.dtype)
    zr = jnp.zeros((DEPTH, BATCH, RWKV_HEADS, RWKV_N, RWKV_N), state_rwkv.dtype)
    zs = jnp.zeros((DEPTH, BATCH, RWKV_COLS), state_shift.dtype)
    y_prompt, hgrn_p, rwkv_p, shift_p = _run(x_prompt, c_prompt, zh, zr, zs, lb_all,
                                             layer_params, final_norm_w)
    y_sample, hgrn_s, rwkv_s, shift_s = _run(x_sample, c_sample, state_hgrn, state_rwkv,
                                             state_shift, lb_all, layer_params, final_norm_w)
    return (y_prompt, y_sample, hgrn_p, rwkv_p, shift_p, hgrn_s, rwkv_s, shift_s)
```

```python
import contextlib
import numpy as np
import concourse.bass as bass
import concourse.mybir as mybir
from concourse.bass_utils import run_bass_kernel_spmd

F32 = mybir.dt.float32
BF16 = mybir.dt.bfloat16
AF = mybir.ActivationFunctionType
ALU = mybir.AluOpType
AX = mybir.AxisListType

D = 1024
SEQ = 2048
NSEQ_S = 16
TS = 4
NT = SEQ + NSEQ_S * TS
INC = 9504
RC = 3360
NDMASEM = 6
RMS_EPS = 1e-6
LNX_EPS = 64e-5
C1 = -0.5 * float(np.exp(-0.5))


class Prog:
    def __init__(self, nc):
        self.nc = nc
        self.ops = []
        self.last_w = {}
        self.readers = {}
        self.bar = False

    @staticmethod
    def key(x):
        if isinstance(x, tuple):
            if isinstance(x[0], str):
                return x
            return (x[0].tensor.name, x[1])
        if isinstance(x, str):
            return x
        return x.tensor.name

    def add(self, eng, fn, r=(), w=(), dma=False):
        deps = set()
        rk = [self.key(x) for x in r]
        wk = [self.key(x) for x in w]
        if self.bar:
            rk.append("__bar")
        for k in rk:
            if k in self.last_w:
                deps.add(self.last_w[k])
        for k in wk:
            if k in self.last_w:
                deps.add(self.last_w[k])
            deps |= self.readers.get(k, set())
        idx = len(self.ops)
        self.ops.append(dict(eng=eng, fn=fn, deps=deps, dma=dma, users=0))
        for k in rk:
            self.readers.setdefault(k, set()).add(idx)
        for k in wk:
            self.last_w[k] = idx
            self.readers[k] = set()
        return idx

    def barrier(self, dummy):
        allk = set(self.last_w.keys()) | set(self.readers.keys())
        allk.discard("__bar")
        self.bar = False
        self.add("dve", lambda e: e.memset(dummy, 0.0), r=(), w=list(allk) + ["__bar"])
        self.last_w = {"__bar": self.last_w["__bar"]}
        self.readers = {}
        self.bar = True

    def emit(self):
        nc = self.nc
        engs = {"pe": nc.tensor, "act": nc.scalar, "dve": nc.vector, "pool": nc.gpsimd, "sp": nc.sync}
        ops = self.ops

        def skip(d, o):
            return ops[d]["eng"] == "pe" and o["eng"] == "pe" and not ops[d]["dma"] and not o["dma"]

        for o in ops:
            for d in o["deps"]:
                if not skip(d, o):
                    ops[d]["users"] += 1
        with contextlib.ExitStack() as st:
            csem = {e: st.enter_context(nc.semaphore("c_" + e)) for e in engs}
            dsem = {e: [st.enter_context(nc.semaphore(f"d_{e}{i}")) for i in range(NDMASEM)]
                    for e in ("act", "pool", "sp")}
            ccnt = {e: 0 for e in engs}
            dcnt = {e: [0] * NDMASEM for e in dsem}
            drr = {e: 0 for e in dsem}
            waited = {e: {} for e in engs}
            sig = {}

            def wait(e, sem, val):
                k = id(sem)
                if val <= 0 or waited[e].get(k, 0) >= val:
                    return
                engs[e].wait_ge(sem, val)
                waited[e][k] = val

            for i, o in enumerate(ops):
                e = o["eng"]
                for d in sorted(o["deps"]):
                    if d not in sig or skip(d, o):
                        continue
                    s, v = sig[d]
                    wait(e, s, v)
                if o["dma"]:
                    j = drr[e]
                    drr[e] = (j + 1) % NDMASEM
                    s = dsem[e][j]
                    wait(e, s, dcnt[e][j])
                    ins = o["fn"](engs[e])
                    dcnt[e][j] += 16
                    ins.then_inc(s, 16)
                    sig[i] = (s, dcnt[e][j])
                else:
                    ins = o["fn"](engs[e])
                    if o["users"] > 0:
                        ccnt[e] += 1
                        ins.then_inc(csem[e], 1)
                        sig[i] = (csem[e], ccnt[e])
            for e in dsem:
                for j in range(NDMASEM):
                    if dcnt[e][j] > 0:
                        nc.sync.wait_ge(dsem[e][j], dcnt[e][j])
        return len(ops)


def build(debug=None):
    nc = bass.Bass("TRN2", target_bir_lowering=False)
    P = Prog(nc)
    din = lambda n, s: nc.dram_tensor(n, list(s), F32, kind="ExternalInput").ap()
    dout = lambda n, s: nc.dram_tensor(n, list(s), F32, kind="ExternalOutput").ap()
    xp = din("xp", [SEQ, D]); xs = din("xs", [64, D])
    sh_in = din("sh", [16, 8, 128, 128]); sr_in = din("sr", [16, 16, 64, 64]); ss_in = din("ss", [16, RC])
    cpr = din("cp", [1, D]); csm = din("cs", [16, D])
    norm1_w = din("norm1_w", [8, 128]); norm2_w = din("norm2_w", [8, 128])
    ada_w = din("ada_w", [D, 6 * D]); ada_b = din("ada_b", [48, 128])
    w_in = din("w_in", [D, INC]); lb_logits = din("lb_logits", [16, 128])
    hgrn_norm_w = din("hgrn_norm_w", [1, 128]); mu = din("rwkv_mu", [1, RC])
    w0 = din("rwkv_w0", [8, 128]); w2 = din("rwkv_w2", [64, D]); a0 = din("rwkv_a0", [8, 128])
    a2 = din("rwkv_a2", [64, D]); g2 = din("rwkv_g2", [160, D]); k_k = din("rwkv_k_k", [8, 128])
    k_a = din("rwkv_k_a", [8, 128]); r_k = din("rwkv_r_k", [8, 128]); lnx_w = din("rwkv_lnx_w", [8, 128])
    lnx_b = din("rwkv_lnx_b", [8, 128]); w_out = din("w_out", [D, D]); w_up = din("w_up", [D, 4 * D])
    w_down = din("w_down", [4 * D, D]); fnw = din("final_norm_w", [8, 128])
    yp = dout("yp", [SEQ, D]); ys = dout("ys", [64, D])
    hp = dout("hp", [8, 128, 128]); rp = dout("rp", [16, 64, 64]); shp = dout("shp", [1, RC])
    hs = dout("hs", [16, 8, 128, 128]); rs = dout("rs", [16, 16, 64, 64]); shs = dout("shs", [16, RC])
    dbg = {}

    cnt = [0]

    def uname(n):
        cnt[0] += 1
        return f"{n}_{cnt[0]}"

    class Scope:
        def __init__(self):
            self.st = contextlib.ExitStack()

        def sb(self, name, shape, dt=F32):
            return self.st.enter_context(nc.sbuf_tensor(uname(name), list(shape), dt)).ap()

        def close(self):
            self.st.close()

    G = Scope()
    banks = [nc.alloc_psum_tensor(f"bank{i}", [128, 512], F32).ap() for i in range(8)]
    bk = [0]

    def bank():
        b = banks[bk[0] % 8]
        bk[0] += 1
        return b

    def dump(name, ap, shape, keys=None):
        if debug is None or (name not in debug and "all" not in debug):
            return
        d = nc.dram_tensor("dbg_" + name, list(shape), F32, kind="ExternalOutput").ap()
        dbg[name] = d
        q = "pool" if ap.dtype != F32 else "sp"
        rk = [ap] + [(ap, t_) for t_ in range(16)] if keys is None else keys
        P.add(q, lambda e: e.dma_start(out=d, in_=ap), r=rk, w=[d], dma=True)

    def dma(q, out, in_, r=(), w=()):
        P.add(q, lambda e: e.dma_start(out=out, in_=in_), r=list(r) or [in_], w=list(w) or [out], dma=True)

    def act(out, in_, func, bias=0.0, scale=1.0, r=(), w=(), accum=None):
        rr = list(r) or [in_]
        if not isinstance(bias, float):
            rr.append(bias)
        if not isinstance(scale, float):
            rr.append(scale)
        ww = list(w) or [out]
        if accum is not None:
            ww.append(accum)
        if accum is not None:
            P.add("act", lambda e: e.activation(out=out, in_=in_, func=func, bias=bias, scale=scale, accum_out=accum), r=rr, w=ww)
        else:
            P.add("act", lambda e: e.activation(out=out, in_=in_, func=func, bias=bias, scale=scale), r=rr, w=ww)

    def tt(out, in0, in1, op, eng="dve", r=(), w=()):
        P.add(eng, lambda e: e.tensor_tensor(out=out, in0=in0, in1=in1, op=op), r=list(r) or [in0, in1], w=list(w) or [out])

    def ts(out, in0, s1, s2, op0, op1=None, eng="dve", r=(), w=()):
        rr = list(r) or [in0]
        for s in (s1, s2):
            if s is not None and not isinstance(s, float):
                rr.append(s)
        if op1 is None:
            P.add(eng, lambda e: e.tensor_scalar(out=out, in0=in0, scalar1=s1, scalar2=None, op0=op0), r=rr, w=list(w) or [out])
        else:
            P.add(eng, lambda e: e.tensor_scalar(out=out, in0=in0, scalar1=s1, scalar2=s2, op0=op0, op1=op1), r=rr, w=list(w) or [out])

    def stt(out, in0, scalar, in1, op0, op1, r=(), w=()):
        rr = list(r) or [in0, in1]
        if not isinstance(scalar, float):
            rr.append(scalar)
        P.add("dve", lambda e: e.scalar_tensor_tensor(out=out, in0=in0, scalar=scalar, in1=in1, op0=op0, op1=op1), r=rr, w=list(w) or [out])

    def cp(out, in_, eng="dve", r=(), w=()):
        if eng == "act":
            act(out, in_, AF.Copy, r=r, w=w)
        else:
            P.add(eng, lambda e: e.tensor_copy(out=out, in_=in_), r=list(r) or [in_], w=list(w) or [out])

    def memset(ap, val, eng="pool", w=()):
        P.add(eng, lambda e: e.memset(ap, val), w=list(w) or [ap])

    def mm(out, lhsT, rhs, start, stop, r=(), w=()):
        P.add("pe", lambda e: e.matmul(out, lhsT, rhs, start=start, stop=stop), r=list(r) or [lhsT, rhs], w=list(w) or [out])

    def tr(out, in_, ident, r=(), w=()):
        P.add("pe", lambda e: e.transpose(out=out, in_=in_, identity=ident), r=list(r) or [in_, ident], w=list(w) or [out])

    I32 = mybir.dt.int32

    def rsqrt_dve(y, x, t1, keys):
        P.add("dve", lambda e: e.tensor_scalar(out=y.bitcast(I32), in0=x.bitcast(I32), scalar1=-0.5, scalar2=1597463007.0,
                                               op0=ALU.mult, op1=ALU.add), r=keys, w=keys)
        for _ in range(2):
            tt(t1, y, y, ALU.mult, r=keys, w=keys)
            stt(t1, t1, -0.5, x, ALU.mult, ALU.mult, r=keys, w=keys)
            stt(y, t1, 1.5, y, ALU.add, ALU.mult, r=keys, w=keys)

    def rsqrt_(out, in_, scale, eps, tA, tB):
        ts(tA, in_, scale, eps, ALU.mult, ALU.add)
        rsqrt_dve(out, tA, tB, [out, tA, tB])

    ones128 = G.sb("ones128", [128, 128]); identF = G.sb("identF", [128, 128]); identB = G.sb("identB", [128, 128], BF16)
    memset(ones128, 1.0)
    P.add("pool", lambda e: e.affine_select(out=identF, in_=ones128, pattern=[[1, 128]], compare_op=ALU.is_equal,
                                           fill=0.0, base=0, channel_multiplier=-1), r=[ones128], w=[identF])
    cp(identB, identF)
    onesB = G.sb("onesB", [128, 128], BF16)
    cp(onesB, ones128)
    blkones = G.sb("blkones", [128, 128], BF16)
    memset(blkones, 0.0)
    memset(blkones[0:64, 0:64], 1.0, w=[blkones]); memset(blkones[64:128, 64:128], 1.0, w=[blkones])
    blkneg = G.sb("blkneg", [128, 128])
    memset(blkneg, 0.0)
    memset(blkneg[0:64, 0:64], -1.0, w=[blkneg]); memset(blkneg[64:128, 64:128], -1.0, w=[blkneg])
    blkpos = G.sb("blkpos", [128, 128])
    memset(blkpos, 0.0)
    memset(blkpos[0:64, 0:64], 1.0, w=[blkpos]); memset(blkpos[64:128, 64:128], 1.0, w=[blkpos])
    mhalf = G.sb("mhalf", [128, 512]); memset(mhalf, -0.5)
    zeros = G.sb("zeros", [128, 64]); memset(zeros, 0.0)
    dummy = G.sb("dummy", [128, 1])
    masks = {}
    for C in (64, 4):
        mi = G.sb(f"mi{C}", [C, C]); ms = G.sb(f"ms{C}", [C, C]); mls = G.sb(f"mls{C}", [C, C])
        P.add("pool", lambda e, mi=mi, C=C: e.affine_select(out=mi, in_=ones128[0:C, 0:C], pattern=[[1, C]], compare_op=ALU.is_ge,
                                                         fill=0.0, base=0, channel_multiplier=-1), r=[ones128], w=[mi])
        P.add("pool", lambda e, ms=ms, C=C: e.affine_select(out=ms, in_=ones128[0:C, 0:C], pattern=[[1, C]], compare_op=ALU.is_gt,
                                                         fill=0.0, base=0, channel_multiplier=-1), r=[ones128], w=[ms])
        P.add("pool", lambda e, mls=mls, C=C: e.affine_select(out=mls, in_=ones128[0:C, 0:C], pattern=[[-1, C]], compare_op=ALU.is_gt,
                                                           fill=0.0, base=0, channel_multiplier=1), r=[ones128], w=[mls])
        msn = G.sb(f"msn{C}", [C, C]); mlsn = G.sb(f"mlsn{C}", [C, C])
        ts(msn, ms, -1.0, None, ALU.mult); ts(mlsn, mls, -1.0, None, ALU.mult)
        masks[C] = dict(incl=mi, strict=ms, nstrict=msn, nlstrict=mlsn)

    stgA = G.sb("stgA", [128, 128]); stgB = G.sb("stgB", [48, 128])
    memset(stgA, 0.0)
    rows = {}
    r0 = 0
    for nm, src, n in (("n1", norm1_w, 8), ("n2", norm2_w, 8), ("lb", lb_logits, 16), ("w0", w0, 8), ("a0", a0, 8),
                       ("kk", k_k, 8), ("ka", k_a, 8), ("rk", r_k, 8), ("lw", lnx_w, 8), ("lbb", lnx_b, 8), ("fn", fnw, 8)):
        dma("sp", stgA[r0:r0 + n, :], src, w=[stgA]); rows[nm] = r0; r0 += n
    dma("sp", stgA[r0:r0 + 26, :], mu[:, 0:3328].rearrange("o (n p) -> (o n) p", p=128), w=[stgA]); rows["mu"] = r0; r0 += 26
    dma("sp", stgA[r0:r0 + 1, 0:32], mu[:, 3328:3360], w=[stgA]); rows["mu2"] = r0; r0 += 1
    assert r0 <= 128
    dma("sp", stgB, ada_b)
    pv = G.sb("pv", [128, 128]); pvB = G.sb("pvB", [128, 48])
    b_ = bank(); tr(b_[:, 0:128], stgA, identF); cp(pv, b_[:, 0:128])
    b_ = bank(); tr(b_[:, 0:48], stgB, identF[0:48, 0:48]); cp(pvB, b_[:, 0:48])
    col = lambda nm, i=0: pv[:, rows[nm] + i: rows[nm] + i + 1]
    hs1 = G.sb("hs1", [128, 8]); hs2 = G.sb("hs2", [128, 8]); hs1n = G.sb("hs1n", [128, 8]); hs2c = G.sb("hs2c", [128, 8])
    tt(hs1, pv[:, rows["lb"]:rows["lb"] + 8], pv[:, rows["lb"] + 8:rows["lb"] + 16], ALU.subtract)
    act(hs2, hs1, AF.Tanh, scale=0.5)
    ts(hs1, hs2, -0.25, 0.25, ALU.mult, ALU.add)
    ts(hs2, hs2, 0.25, 0.75, ALU.mult, ALU.add)
    ts(hs1n, hs1, -1.0, None, ALU.mult)
    ts(hs2c, hs2, -1.0, 1.0, ALU.mult, ALU.add)
    hw0 = G.sb("hw0", [128, 8]); ha0 = G.sb("ha0", [128, 8]); omka = G.sb("omka", [128, 8])
    ts(hw0, pv[:, rows["w0"]:rows["w0"] + 8], 0.5, None, ALU.mult)
    ts(ha0, pv[:, rows["a0"]:rows["a0"] + 8], 0.5, None, ALU.mult)
    ts(omka, pv[:, rows["ka"]:rows["ka"] + 8], -1.0, 1.0, ALU.mult, ALU.add)
    nwbc = G.sb("nwbc", [64, 128]); dma("sp", nwbc, hgrn_norm_w.partition_broadcast(64))

    modT = G.sb("modT", [128, 48, 17])
    g1p = G.sb("g1p", [128, 8]); g2p = G.sb("g2p", [128, 8])
    sh1p = G.sb("sh1p", [128, 8]); sh2p = G.sb("sh2p", [128, 8]); gt1p = G.sb("gt1p", [128, 8]); gt2p = G.sb("gt2p", [128, 8])
    g1s = G.sb("g1s", [128, 8, 64]); g2s = G.sb("g2s", [128, 8, 64]); sh1s = G.sb("sh1s", [128, 8, 64]); sh2s = G.sb("sh2s", [128, 8, 64])
    gt1s = G.sb("gt1s", [128, 8, 64]); gt2s = G.sb("gt2s", [128, 8, 64])
    mT = G.sb("mT", [128, 8, NT], BF16)
    S0 = Scope()
    c17 = S0.sb("c17", [17, D]); s17 = S0.sb("s17", [17, D]); s17b = S0.sb("s17b", [17, D], BF16); scT = S0.sb("scT", [128, 8, 32], BF16)
    dma("sp", c17[0:1, :], cpr, w=[c17]); dma("sp", c17[1:17, :], csm, w=[c17])
    act(s17, c17, AF.Tanh, scale=0.5)
    ts(s17, s17, 0.5, 0.5, ALU.mult, ALU.add)
    tt(s17b, s17, c17, ALU.mult)
    b_ = bank(); bb = b_.bitcast(BF16)
    for k in range(8):
        tr(bb[:, k * 32:k * 32 + 17], s17b[:, k * 128:(k + 1) * 128], identB[0:17, 0:17], w=[b_])
    cp(scT[:, :, 0:17], bb[:, 0:256].rearrange("p (k c) -> p k c", c=32)[:, :, 0:17], r=[b_])
    awb = [S0.sb("awb", [128, 8, 512], BF16) for _ in range(2)]
    for g in range(12):
        wt = awb[g % 2]
        for k in range(8):
            dma("pool", wt[:, k, :], ada_w[k * 128:(k + 1) * 128, g * 512:(g + 1) * 512], w=[wt])
        b_ = bank()
        for m in range(4):
            for k in range(8):
                mm(b_[:, m * 32:m * 32 + 17], wt[:, k, m * 128:(m + 1) * 128], scT[:, k, 0:17], k == 0, k == 7, w=[b_])
        for m in range(4):
            ts(modT[:, g * 4 + m, :], b_[:, m * 32:m * 32 + 17], pvB[:, g * 4 + m:g * 4 + m + 1], None, ALU.add, r=[b_, pvB], w=[modT])
    def modp(part):
        return modT[:, part * 8:(part + 1) * 8, 0]
    def mods(part):
        return modT[:, part * 8:(part + 1) * 8, 1:17].unsqueeze(3).broadcast_to([128, 8, 16, 4])
    stt(g1p, modp(1), 1.0, pv[:, rows["n1"]:rows["n1"] + 8], ALU.add, ALU.mult, r=[modT, pv])
    stt(g2p, modp(4), 1.0, pv[:, rows["n2"]:rows["n2"] + 8], ALU.add, ALU.mult, r=[modT, pv])
    for dst, part in ((sh1p, 0), (gt1p, 2), (sh2p, 3), (gt2p, 5)):
        cp(dst, modp(part), r=[modT])
    def mods3(part):
        return modT[:, part * 8:(part + 1) * 8, 1:17]
    for t4 in range(4):
        for dst, part in ((sh1s, 0), (gt1s, 2), (sh2s, 3), (gt2s, 5)):
            cp(dst[:, :, t4::4], mods3(part), r=[modT], w=[dst])
        for dst, part, nm in ((g1s, 1, "n1"), (g2s, 4, "n2")):
            stt(dst[:, :, t4::4], mods3(part), 1.0, pv[:, rows[nm]:rows[nm] + 8].unsqueeze(2).broadcast_to([128, 8, 16]),
                ALU.add, ALU.mult, r=[modT, pv], w=[dst])
    S0.close()
    P.barrier(dummy)

    tiles = [(i * 256, 256, 64, 4, False) for i in range(8)] + [(SEQ + 16 * i, 16, 4, 4, True) for i in range(4)]

    def norm_to(hT_dst, xT_t, T, gp, shp_, gs, shs_, is_s, ti):
        sq = nbuf["sq"]
        tt(sq[:, :, 0:T], xT_t[:, :, 0:T], xT_t[:, :, 0:T], ALU.mult, r=[xT_t], w=[sq])
        b_ = bank()
        for fc in range(8):
            mm(b_[:, 0:T], onesB, sq[:, fc, 0:T], fc == 0, fc == 7, w=[b_])
        rstd = nbuf["rstd"]
        ts(rstd[:, 0:T], b_[:, 0:T], 1.0 / D, RMS_EPS, ALU.mult, ALU.add, r=[b_], w=[rstd])
        r1 = nbuf["r1"]; r2 = nbuf["r2"]
        cp(r1[:, 0:T], rstd[:, 0:T], eng="act", r=[rstd], w=[r1])
        rsqrt_dve(rstd[:, 0:T], r1[:, 0:T], r2[:, 0:T], [rstd, r1, r2])
        tmp = nbuf["tmp"]
        tt(tmp[:, :, 0:T], xT_t[:, :, 0:T], rstd[:, 0:T].unsqueeze(1).broadcast_to([128, 8, T]), ALU.mult, r=[xT_t, rstd], w=[tmp])
        if not is_s:
            for fc in range(8):
                ts(hT_dst[:, fc, :], tmp[:, fc, 0:T], gp[:, fc:fc + 1], shp_[:, fc:fc + 1], ALU.mult, ALU.add,
                   r=[tmp, gp, shp_], w=[(hT_dst, ti)])
        else:
            tt(tmp[:, :, 0:T], tmp[:, :, 0:T], gs, ALU.mult, r=[tmp, gs], w=[tmp])
            tt(hT_dst, tmp[:, :, 0:T], shs_, ALU.add, r=[tmp, shs_], w=[(hT_dst, ti)])

    def load_xT(xT_t, c0, T, ti):
        for sub in range(0, T, 128):
            n = min(128, T - sub)
            xtm = nbuf["xtm"][nbuf["i"] % 2]; nbuf["i"] += 1
            src = xp[c0 + sub:c0 + sub + n, :] if c0 < SEQ else xs[c0 - SEQ + sub:c0 - SEQ + sub + n, :]
            dma("sp", xtm[0:n, :], src, w=[xtm])
            for half in range(2):
                b_ = bank()
                for f4 in range(4):
                    fc = half * 4 + f4
                    tr(b_[:, f4 * 128:f4 * 128 + n], xtm[0:n, fc * 128:(fc + 1) * 128], identF[0:n, 0:n], w=[b_])
                cp(xT_t[:, half * 4:half * 4 + 4, sub:sub + n], b_.rearrange("p (f t) -> p f t", t=128)[:, :, 0:n], eng="act", r=[b_], w=[xT_t])

    S2 = Scope()
    hT = S2.sb("hT", [128, 8, NT], BF16)
    S1 = Scope()
    nbuf = dict(sq=S1.sb("sq", [128, 8, 512], BF16), rstd=S1.sb("rstd", [128, 512]), tmp=S1.sb("ntmp", [128, 8, 512]), r1=S1.sb("r1", [128, 512]), r2=S1.sb("r2", [128, 512]),
                xtm=[S1.sb("xtm", [128, D]) for _ in range(2)], i=0)
    xT_ts = [S1.sb("xT_t", [128, 8, 512]) for _ in range(2)]
    for ti, (c0, T, C, NCH, is_s) in enumerate(tiles):
        xT_t = xT_ts[ti % 2]
        load_xT(xT_t, c0, T, ti)
        so = max(c0 - SEQ, 0)
        norm_to(hT[:, :, c0:c0 + T], xT_t, T, g1p, sh1p, g1s[:, :, so:so + min(T, 16)], sh1s[:, :, so:so + min(T, 16)], is_s, ti)
    S1.close()
    P.barrier(dummy)

    W = lambda name, shape, dt=F32: S2.sb(name, shape, dt)
    wl = W("wl", [128, 8, 288], BF16)
    wbs = [W("wb", [128, 8, 1152], BF16)] * 2
    w2b = W("w2b", [128, D], BF16)
    g2b = W("g2b", [128, D], BF16); g2c = W("g2c", [32, D], BF16)
    TW = W("TW", [128, NT], BF16); SG1 = W("SG1", [128, NT], BF16); SG2 = W("SG2", [32, NT], BF16)
    srow = W("srow", [17, 512]); ssT = W("ssT", [128, 27, 16]); lastc = W("lastc", [128, 27, 17])
    prevs = W("prevs", [128, 8])
    rcS = W("rcS", [128, 256]); dsh = W("dsh", [128, 256])
    S0all = W("S0all", [128, 4, 128]); S0bf = W("S0bf", [128, 4, 128], BF16); Souts = S0all
    Sh = W("Sh", [128, 128]); Shbf = W("Shbf", [128, 128], BF16)
    R0in = W("R0in", [128, 4, 128]); R0blk = W("R0blk", [128, 4, 128]); R0bf = W("R0bf", [128, 4, 128], BF16); Routs = R0blk
    Sr = W("Sr", [128, 128]); Srbf = W("Srbf", [128, 128], BF16); Srt = W("Srt", [128, 128])
    f32t = {n: W(n, [128, 256]) for n in ("th", "fS", "kS", "Pc", "rP", "Ga", "t1", "t2", "Gmm", "kap", "kmod", "bS", "bon", "gS", "Gb", "oT", "n1", "n2")}
    for a_, b__ in (("xr", "th"), ("xk", "fS"), ("xv", "kS"), ("Gm", "Pc"), ("rG", "rP"), ("aS", "Ga")):
        f32t[a_] = f32t[b__]
    bft = {n: W(n, [128, 256], BF16) for n in ("qt", "kt", "kh", "vT", "sqb", "kpt", "bt", "bh")}
    for a_, b__ in (("rt", "qt"), ("ktl", "kt"), ("khr", "kh"), ("vTr", "vT")):
        bft[a_] = bft[b__]
    KR = W("KR", [128, 4, 128], BF16)
    tm = {n: W(n, [64, 4, 128], BF16) for n in ("Vtm", "khtm", "Ktm", "bhtm", "khrtm", "W1", "Un", "Ktp", "onb")}
    msk = {n: W(n, [64, 4, 128], BF16) for n in ("PbT", "PkT")}
    Abm = {n: W(n, [64, 4, 128], BF16) for n in ("Y", "Z", "Y2", "Z2", "TTa", "TTb")}
    attnT = W("attnT", [64, 4, 64], BF16)
    Osb = W("Osb", [64, 4, 128]); Osq = W("Osq", [64, 4, 128]); st_ = W("st_", [64, 64]); st2 = W("st2", [64, 64]); stA = W("stA", [64, 64]); stB = W("stB", [64, 64])
    Mblk = W("Mblk", [128, 4, 128], BF16); Ncomb = W("Ncomb", [128, 4, 128]); RpT = W("RpT", [128, 256], BF16)

    dma("pool", w2b[0:64, :], w2, w=[w2b]); dma("pool", w2b[64:128, :], a2, w=[w2b])
    dma("pool", g2b, g2[0:128, :]); dma("pool", g2c, g2[128:160, :])
    for k in range(8):
        dma("pool", wl[:, k, :], w_in[k * 128:(k + 1) * 128, 7168:7456], w=[wl])
    for p4 in range(0, 27, 4):
        n4 = min(4, 27 - p4); wdt = min(n4 * 128, RC - p4 * 128)
        memset(srow, 0.0, eng="dve")
        dma("sp", srow[0:16, 0:wdt], ss_in[:, p4 * 128:p4 * 128 + wdt], r=[ss_in], w=[srow])
        b_ = bank()
        for q in range(n4):
            tr(b_[:, q * 16:(q + 1) * 16], srow[0:16, q * 128:(q + 1) * 128], identF[0:16, 0:16], r=[srow, identF], w=[b_])
        cp(ssT[:, p4:p4 + n4, :], b_[:, 0:n4 * 16].rearrange("p (q s) -> p q s", s=16), r=[b_], w=[ssT])
    memset(prevs, 0.0); memset(lastc, 0.0)

    def proj(wt, c_lo, c_hi, ti, c0, T):
        b_ = bank(); n = c_hi - c_lo
        for k in range(8):
            mm(b_[0:n, 0:T], wt[:, k, c_lo:c_hi], hT[:, k, c0:c0 + T], k == 0, k == 7, r=[wt, (hT, ti)], w=[b_])
        return b_

    def shift_mix(dst, ps, n, T, is_s, mucol, pcol, part, last_tile, s0=0):
        cp(rcS[0:n, 0:T], ps[0:n, 0:T], eng="act", r=[ps], w=[rcS])
        if not is_s:
            tt(dsh[0:n, 1:T], rcS[0:n, 0:T - 1], rcS[0:n, 1:T], ALU.subtract, r=[rcS], w=[dsh])
            tt(dsh[0:n, 0:1], prevs[0:n, pcol:pcol + 1], rcS[0:n, 0:1], ALU.subtract, r=[rcS, prevs], w=[dsh])
            cp(prevs[0:n, pcol:pcol + 1], rcS[0:n, T - 1:T], r=[rcS, dsh], w=[prevs])
            if last_tile:
                cp(lastc[0:n, part, 0:1], rcS[0:n, T - 1:T], r=[rcS], w=[lastc])
        else:
            r3 = rcS[0:n, 0:T].rearrange("p (s t) -> p s t", t=4); d3 = dsh[0:n, 0:T].rearrange("p (s t) -> p s t", t=4)
            tt(d3[:, :, 1:4], r3[:, :, 0:3], r3[:, :, 1:4], ALU.subtract, r=[rcS], w=[dsh])
            tt(d3[:, :, 0], ssT[0:n, part, s0:s0 + T // 4], r3[:, :, 0], ALU.subtract, r=[rcS, ssT], w=[dsh])
            cp(lastc[0:n, part, 1 + s0:1 + s0 + T // 4], r3[:, :, 3], r=[rcS], w=[lastc])
        stt(dst[0:n, 0:T], dsh[0:n, 0:T], mucol[0:n, :], rcS[0:n, 0:T], ALU.mult, ALU.add, r=[dsh, rcS, pv], w=[dst])

    def sigm(dst, src, r=(), w=(), bias=0.0):
        act(dst, src, AF.Tanh, bias=bias, scale=0.5, r=r, w=w)
        ts(dst, dst, 0.5, 0.5, ALU.mult, ALU.add)

    for ti, (c0, T, C, NCH, is_s) in enumerate(tiles):
        last = (c0 + T == SEQ)
        b_ = proj(wl, 0, 128, ti, c0, T)
        shift_mix(f32t["t1"], b_, 128, T, is_s, col("mu", 24), 0, 24, last, max(c0 - SEQ, 0) // 4)
        act(TW[0:64, c0:c0 + T], f32t["t1"][0:64, 0:T], AF.Tanh, r=[f32t["t1"]], w=[(TW, ti)])
        cp(TW[64:128, c0:c0 + T], f32t["t1"][64:128, 0:T], eng="act", r=[f32t["t1"]], w=[(TW, ti)])
        b_ = proj(wl, 128, 256, ti, c0, T)
        shift_mix(f32t["t1"], b_, 128, T, is_s, col("mu", 25), 1, 25, last, max(c0 - SEQ, 0) // 4)
        sigm(f32t["t2"][:, 0:T], f32t["t1"][:, 0:T], r=[f32t["t1"]], w=[f32t["t2"]])
        cp(SG1[:, c0:c0 + T], f32t["t2"][:, 0:T], r=[f32t["t2"]], w=[(SG1, ti)])
        b_ = proj(wl, 256, 288, ti, c0, T)
        shift_mix(f32t["t1"], b_, 32, T, is_s, col("mu2"), 2, 26, last, max(c0 - SEQ, 0) // 4)
        sigm(f32t["t2"][0:32, 0:T], f32t["t1"][0:32, 0:T], r=[f32t["t1"]], w=[f32t["t2"]])
        cp(SG2[:, c0:c0 + T], f32t["t2"][0:32, 0:T], r=[f32t["t2"]], w=[(SG2, ti)])

    if debug is not None and debug.get("_stop") == "lora":
        dump("TW", TW, [128, NT]); dump("SG1", SG1, [128, NT]); dump("SG2", SG2, [32, NT]); dump("lastc", lastc.rearrange("p a b -> p (a b)"), [128, 27 * 17])
        n = P.emit()
        return nc, dbg, n

    def chunk3(ap, C):
        return ap.rearrange("p (c t) -> p c t", t=C)

    def to_tm(dst, src_bf, T, C, NCH):
        for c4 in range(0, NCH, 8):
            b_ = bank(); bb = b_.bitcast(BF16)
            for c in range(c4, min(c4 + 8, NCH)):
                tr(bb[0:C, (c - c4) * 128:(c - c4 + 1) * 128], src_bf[:, c * C:(c + 1) * C], identB, r=[src_bf, identB], w=[b_])
            n8 = min(8, NCH - c4)
            cp(dst[0:C, c4:c4 + n8, :], bb[0:C, 0:n8 * 128].rearrange("p (c f) -> p c f", f=128), eng="act", r=[b_], w=[dst])

    def cumprod(dst, src, T, C, NCH):
        for c in range(NCH):
            sl = slice(c * C, (c + 1) * C)
            P.add("dve", lambda e, sl=sl: e.tensor_tensor_scan(out=dst[:, sl], data0=src[:, sl], data1=zeros[:, 0:C], initial=1.0,
                                                              op0=ALU.mult, op1=ALU.add), r=[src, zeros], w=[dst])

    def lastbc(ap, T, C, NCH):
        return chunk3(ap[:, 0:T], C)[:, :, C - 1:C].broadcast_to([128, NCH, C])


    def rwkv_block(j, ti, c0, T, C, NCH, is_s, last, wb):
        F_ = f32t; B_ = bft; M_ = masks[C]; jc = slice(j * 128, (j + 1) * 128)
        L = 6 if C == 64 else 2
        br = proj(wb, 512, 640, ti, c0, T); shift_mix(F_["xr"], br, 128, T, is_s, col("mu", j), 3, j, last, max(c0 - SEQ, 0) // 4)
        bk_ = proj(wb, 640, 768, ti, c0, T); shift_mix(F_["xk"], bk_, 128, T, is_s, col("mu", 8 + j), 4, 8 + j, last, max(c0 - SEQ, 0) // 4)
        bv = proj(wb, 768, 896, ti, c0, T); shift_mix(F_["xv"], bv, 128, T, is_s, col("mu", 16 + j), 5, 16 + j, last, max(c0 - SEQ, 0) // 4)
        bgb = proj(wb, 1024, 1152, ti, c0, T)
        bw = bank(); mm(bw[:, 0:T], w2b[0:64, jc], TW[0:64, c0:c0 + T], True, True, r=[w2b, (TW, ti)], w=[bw])
        ba = bank(); mm(ba[:, 0:T], w2b[64:128, jc], TW[64:128, c0:c0 + T], True, True, r=[w2b, (TW, ti)], w=[ba])
        bg = bank(); mm(bg[:, 0:T], g2b[:, jc], SG1[:, c0:c0 + T], True, False, r=[g2b, (SG1, ti)], w=[bg])
        mm(bg[:, 0:T], g2c[:, jc], SG2[:, c0:c0 + T], False, True, r=[g2c, (SG2, ti)], w=[bg])
        X = lambda n: F_[n][:, 0:T]
        Bx = lambda n: B_[n][:, 0:T]
        act(X("t1"), bw[:, 0:T], AF.Tanh, bias=hw0[:, j:j + 1], scale=0.5, r=[bw], w=[F_["t1"]])
        ts(X("t1"), X("t1"), C1, C1, ALU.mult, ALU.add)
        act(X("t1"), X("t1"), AF.Exp)
        cumprod(F_["Gm"], F_["t1"], T, C, NCH)
        P.add("dve", lambda e: e.reciprocal(out=X("rG"), in_=X("Gm")), r=[F_["Gm"]], w=[F_["rG"]])
        cp(chunk3(X("Gmm"), C)[:, :, 1:C], chunk3(X("Gm"), C)[:, :, 0:C - 1], r=[F_["Gm"]], w=[F_["Gmm"]])
        memset(chunk3(X("Gmm"), C)[:, :, 0:1], 1.0, eng="dve", w=[F_["Gmm"]])
        sigm(X("aS"), ba[:, 0:T], r=[ba], w=[F_["aS"]], bias=ha0[:, j:j + 1])
        cp(X("gS"), bg[:, 0:T], eng="act", r=[bg], w=[F_["gS"]])
        sigm(X("Gb"), bgb[:, 0:T], r=[bgb], w=[F_["Gb"]])
        tt(X("Gb"), X("Gb"), X("gS"), ALU.mult)
        ts(X("kap"), X("xk"), col("kk", j), None, ALU.mult, r=[F_["xk"], pv])
        tt(Bx("sqb"), X("kap"), X("kap"), ALU.mult)
        bn = bank(); mm(bn[:, 0:T], blkones, Bx("sqb"), True, True, w=[bn])
        ts(X("n1"), bn[:, 0:T], 1.0, 1e-24, ALU.mult, ALU.add, r=[bn], w=[F_["n1"]])
        rsqrt_dve(X("t2"), X("n1"), X("n2"), [F_["t2"], F_["n1"], F_["n2"]])
        tt(X("kap"), X("kap"), X("t2"), ALU.mult)
        ts(X("t2"), X("aS"), col("ka", j), omka[:, j:j + 1], ALU.mult, ALU.add, r=[F_["aS"], pv, omka], w=[F_["t2"]])
        tt(X("kmod"), X("xk"), X("t2"), ALU.mult)
        tt(X("bS"), X("aS"), X("kap"), ALU.mult)
        stt(Bx("sqb"), X("xr"), col("rk", j), X("kmod"), ALU.mult, ALU.mult, r=[F_["xr"], F_["kmod"], pv], w=[B_["sqb"]])
        bn2 = bank(); mm(bn2[:, 0:T], blkones, Bx("sqb"), True, True, w=[bn2])
        tt(X("bon"), bn2[:, 0:T], X("xv"), ALU.mult, r=[bn2, F_["xv"]], w=[F_["bon"]])
        tt(Bx("kpt"), X("kap"), X("Gmm"), ALU.mult)
        tt(Bx("rt"), X("xr"), X("Gm"), ALU.mult)
        cp(KR[:, 0:NCH, 0:C], chunk3(Bx("kpt"), C), eng="pool", r=[B_["kpt"]], w=[KR])
        cp(KR[:, 0:NCH, 64:64 + C], chunk3(Bx("rt"), C), eng="pool", r=[B_["rt"]], w=[KR])
        tt(X("t2"), X("kmod"), X("rG"), ALU.mult); cp(Bx("ktl"), X("t2"), eng="act")
        tt(chunk3(Bx("khr"), C), chunk3(X("t2"), C), lastbc(F_["Gm"], T, C, NCH), ALU.mult, r=[F_["t2"], F_["Gm"]], w=[B_["khr"]])
        tt(X("t2"), X("bS"), X("rG"), ALU.mult); cp(Bx("bt"), X("t2"), eng="act")
        tt(chunk3(Bx("bh"), C), chunk3(X("t2"), C), lastbc(F_["Gm"], T, C, NCH), ALU.mult, r=[F_["t2"], F_["Gm"]], w=[B_["bh"]])
        cp(Bx("vTr"), X("xv"), eng="act")
        to_tm(tm["Vtm"], B_["vTr"], T, C, NCH); to_tm(tm["Ktm"], B_["kpt"], T, C, NCH)
        to_tm(tm["bhtm"], B_["bh"], T, C, NCH); to_tm(tm["khrtm"], B_["khr"], T, C, NCH)
        if debug is not None and debug.get('rstop', 99) <= 1:
            return
        hv = lambda t_, c_lo, c_hi: t_[0:C, c_lo:c_hi, :].rearrange("p c (h t) -> p c h t", t=64)[:, :, :, 0:C]
        for c4 in range(0, NCH, 4):
            for h in range(2):
                hs_ = slice(64 * h, 64 * h + 64)
                b1 = bank(); b2 = bank(); b3 = bank()
                for q in range(4):
                    c = c4 + q; csl = slice(c * C, (c + 1) * C); o = q * 128
                    mm(b1[0:C, o:o + 128], B_["bt"][hs_, csl], KR[hs_, c, :], True, True, r=[B_["bt"], KR], w=[b1])
                    mm(b2[0:C, o:o + 128], B_["ktl"][hs_, csl], KR[hs_, c, :], True, True, r=[B_["ktl"], KR], w=[b2])
                    mm(b3[0:C, o:o + C], B_["kpt"][hs_, csl], B_["bt"][hs_, csl], True, True, r=[B_["kpt"], B_["bt"]], w=[b3])
                v4_ = lambda b: b[0:C, :].rearrange("p (c x t) -> p c x t", x=2, t=64)
                mk = lambda m_: m_.unsqueeze(1).broadcast_to([C, 4, C])
                dv = lambda t_: t_[0:C, c4:c4 + 4, h * 64:h * 64 + C]
                tt(dv(Abm["Z"]), v4_(b1)[:, :, 0, 0:C], mk(M_["nstrict"]), ALU.mult, r=[b1, M_["nstrict"]], w=[Abm["Z"]])
                tt(dv(msk["PbT"]), v4_(b1)[:, :, 1, 0:C], mk(M_["incl"]), ALU.mult, r=[b1, M_["incl"]], w=[msk["PbT"]])
                tt(dv(tm["W1"]), v4_(b2)[:, :, 0, 0:C], mk(M_["strict"]), ALU.mult, r=[b2, M_["strict"]], w=[tm["W1"]])
                tt(dv(msk["PkT"]), v4_(b2)[:, :, 1, 0:C], mk(M_["incl"]), ALU.mult, r=[b2, M_["incl"]], w=[msk["PkT"]])
                tt(dv(Abm["Y"]), v4_(b3)[:, :, 0, 0:C], mk(M_["nlstrict"]), ALU.mult, r=[b3, M_["nlstrict"]], w=[Abm["Y"]])
        if debug is not None and debug.get('rstop', 99) <= 2:
            return
        tt(hv(Abm["TTa"], 0, NCH), hv(Abm["Z"], 0, NCH), identF[0:C, 0:C].unsqueeze(1).unsqueeze(1).broadcast_to([C, NCH, 2, C]), ALU.add,
           r=[Abm["Z"], identF], w=[Abm["TTa"]])
        Yc, Zc, Yn, Zn, TTc, TTn = Abm["Y"], Abm["Z"], Abm["Y2"], Abm["Z2"], Abm["TTa"], Abm["TTb"]
        pr = [(c, h) for c in range(NCH) for h in range(2)]

        def batch_mm(dst, lhs, rhs, wid_l, wid_r, post):
            for p8 in range(0, len(pr), 8):
                b_ = bank(); grp = pr[p8:p8 + 8]
                for q, (c, h) in enumerate(grp):
                    mm(b_[0:C, q * 64:q * 64 + wid_r], lhs[0:C, c, h * 64:h * 64 + wid_l], rhs[0:C, c, h * 64:h * 64 + wid_r], True, True,
                       r=[lhs, rhs], w=[b_])
                cA = grp[0][0]; cB = grp[-1][0] + 1
                src = b_[0:C, 0:(cB - cA) * 128].rearrange("p (c h t) -> p c h t", h=2, t=64)[:, :, :, 0:wid_r]
                d = dst[0:C, cA:cB, :].rearrange("p c (h t) -> p c h t", t=64)[:, :, :, 0:wid_r]
                post(d, src, b_)

        cpy = lambda d, s_, b_: cp(d, s_, eng="act", r=[b_], w=[d])
        for n in range(1, L):
            batch_mm(Yn, Zc, Yc, C, C, cpy)
            if n < L - 1:
                batch_mm(Zn, Yc, Zc, C, C, cpy)
            addT = lambda d, s_, b_, TTc=TTc: tt(d, s_, TTc[0:C, 0:NCH, :].rearrange("p c (h t) -> p c h t", t=64)[:, :, :, 0:C][:, (d.offset - d.offset):, :, :] if False else s_, s_, ALU.add) if False else None
            for p8 in range(0, len(pr), 8):
                b_ = bank(); grp = pr[p8:p8 + 8]
                for q, (c, h) in enumerate(grp):
                    mm(b_[0:C, q * 64:q * 64 + C], Yn[0:C, c, h * 64:h * 64 + C], TTc[0:C, c, h * 64:h * 64 + C], True, True, r=[Yn, TTc], w=[b_])
                cA = grp[0][0]; cB = grp[-1][0] + 1
                src = b_[0:C, 0:(cB - cA) * 128].rearrange("p (c h t) -> p c h t", h=2, t=64)[:, :, :, 0:C]
                tt(hv(TTn, cA, cB), src, hv(TTc, cA, cB), ALU.add, r=[b_, TTc], w=[TTn])
            Yc, Yn = Yn, Yc; Zc, Zn = Zn, Zc; TTc, TTn = TTn, TTc
        if debug is not None and debug.get('rstop', 99) <= 3:
            return
        TTf = TTc
        AkT = Yn
        cp(AkT[0:C, 0:NCH, :], tm["W1"][0:C, 0:NCH, :], eng="pool", r=[tm["W1"]], w=[AkT])
        batch_mm(tm["W1"], AkT, tm["Vtm"], C, 64, cpy)
        neg = lambda d, s_, b_: ts(d, s_, -1.0, None, ALU.mult, r=[b_], w=[d])
        batch_mm(tm["Un"], TTf, tm["W1"], C, 64, neg)
        batch_mm(tm["Ktp"], TTf, tm["Ktm"], C, 64, cpy)
        if debug is not None and debug.get('rstop', 99) <= 4:
            return
        for c4 in range(0, NCH, 4):
            bM = bank(); bN = bank(); bR = bank()
            for q in range(4):
                c = c4 + q
                mm(bM[:, q * 128:(q + 1) * 128], tm["Ktp"][0:C, c, :], tm["bhtm"][0:C, c, :], True, True, r=[tm["Ktp"], tm["bhtm"]], w=[bM])
                mm(bN[:, q * 128:(q + 1) * 128], tm["khrtm"][0:C, c, :], tm["Vtm"][0:C, c, :], True, False, r=[tm["khrtm"], tm["Vtm"]], w=[bN])
                mm(bN[:, q * 128:(q + 1) * 128], tm["bhtm"][0:C, c, :], tm["Un"][0:C, c, :], False, True, r=[tm["bhtm"], tm["Un"]], w=[bN])
                mm(bR[:, q * 128:(q + 1) * 128], tm["Ktp"][0:C, c, :], msk["PbT"][0:C, c, :], True, True, r=[tm["Ktp"], msk["PbT"]], w=[bR])
            v3 = lambda b: b.rearrange("p (c f) -> p c f", f=128)
            tt(Mblk[:, c4:c4 + 4, :], v3(bM), blkneg.unsqueeze(1).broadcast_to([128, 4, 128]), ALU.mult, r=[bM, blkneg], w=[Mblk])
            tt(Ncomb[:, c4:c4 + 4, :], v3(bN), blkpos.unsqueeze(1).broadcast_to([128, 4, 128]), ALU.mult, r=[bN, blkpos], w=[Ncomb])
            for h in range(2):
                hs_ = slice(64 * h, 64 * h + 64)
                tt(chunk3(RpT[hs_, c4 * C:(c4 + 4) * C], C), chunk3(B_["rt"][hs_, c4 * C:(c4 + 4) * C], C),
                   v3(bR)[hs_, :, h * 64:h * 64 + C], ALU.subtract, r=[B_["rt"], bR], w=[RpT])
        if debug is not None and debug.get('rstop', 99) <= 5:
            return
        for c in range(NCH):
            csl = slice(c * C, (c + 1) * C)
            sbf = R0bf[:, c, :] if is_s else Srbf
            sin = R0blk[:, c, :] if is_s else Sr
            sout = Routs[:, c, :] if is_s else Sr
            kS_ = [R0bf] if is_s else [Srbf]
            if c % 4 == 0:
                bo = bank()
            oo = bo[0:C, (c % 4) * 128:(c % 4 + 1) * 128]
            mm(oo, RpT[:, csl], sbf, True, False, r=[RpT] + kS_, w=[bo])
            for h in range(2):
                hb = slice(h * 64, (h + 1) * 64)
                mm(oo[:, hb], msk["PkT"][0:C, c, h * 64:h * 64 + C], tm["Vtm"][0:C, c, hb], False, False, r=[msk["PkT"], tm["Vtm"]], w=[bo])
                mm(oo[:, hb], msk["PbT"][0:C, c, h * 64:h * 64 + C], tm["Un"][0:C, c, hb], False, h == 1, r=[msk["PbT"], tm["Un"]], w=[bo])
            b3 = bank()
            mm(b3[:, 0:128], Mblk[:, c, :], sbf, True, True, r=[Mblk] + kS_, w=[b3])
            stt(Srt, sin, F_["Gm"][:, (c + 1) * C - 1:(c + 1) * C], Ncomb[:, c, :], ALU.mult, ALU.add,
                r=[R0blk if is_s else Sr, F_["Gm"], Ncomb], w=[Srt])
            tt(sout, Srt, b3[:, 0:128], ALU.add, r=[Srt, b3], w=[Routs if is_s else Sr])
            if not is_s:
                cp(Srbf, Sr, eng="act")
            if c % 4 == 3:
                cp(Osb[0:C, c - 3:c + 1, :], bo.rearrange("p (c f) -> p c f", f=128)[0:C], eng="act", r=[bo], w=[Osb])
        if debug is not None and debug.get('rstop', 99) <= 6:
            return
        if is_s:
            s0 = (c0 - SEQ) // 4
            for s4 in range(0, 4, 4):
                b_ = bank()
                for q in range(4):
                    tr(b_[:, q * 128:(q + 1) * 128], Routs[:, s4 + q, :], identF, w=[b_])
                cp(R0in[:, s4:s4 + 4, :], b_.rearrange("p (q f) -> p q f", f=128), r=[b_], w=[R0in])
            dma("sp", rs[s0:s0 + 4, 2 * j].rearrange("s i j -> i s j"), R0in[0:64, :, 0:64], r=[R0in], w=[rs])
            dma("sp", rs[s0:s0 + 4, 2 * j + 1].rearrange("s i j -> i s j"), R0in[64:128, :, 64:128], r=[R0in], w=[rs])
        elif last:
            b_ = bank(); tr(b_[:, 0:128], Sr, identF, w=[b_]); cp(Srt, b_[:, 0:128])
            dma("sp", rp[2 * j], Srt[0:64, 0:64], r=[Srt], w=[rp]); dma("sp", rp[2 * j + 1], Srt[64:128, 64:128], r=[Srt], w=[rp])
        if debug is not None and debug.get('rstop', 99) <= 7:
            return
        G2 = NCH * 2
        og = Osb[0:C, 0:NCH, :].rearrange("p c (h i) -> p (c h) i", i=64)
        oq = Osq[0:C, 0:NCH, :].rearrange("p c (h i) -> p (c h) i", i=64)
        P.add("dve", lambda e: e.reduce_sum(out=st_[0:C, 0:G2], in_=og, axis=AX.X), r=[Osb], w=[st_])
        ts(st_[0:C, 0:G2], st_[0:C, 0:G2], 1.0 / 64, None, ALU.mult)
        tt(og, og, st_[0:C, 0:G2].unsqueeze(2).broadcast_to([C, G2, 64]), ALU.subtract, r=[Osb, st_], w=[Osb])
        tt(oq, og, og, ALU.mult, r=[Osb], w=[Osq])
        P.add("dve", lambda e: e.reduce_sum(out=st2[0:C, 0:G2], in_=oq, axis=AX.X), r=[Osq], w=[st2])
        rsqrt_(st2[0:C, 0:G2], st2[0:C, 0:G2], 1.0 / 64, LNX_EPS, stA[0:C, 0:G2], stB[0:C, 0:G2])
        tt(tm["onb"][0:C, 0:NCH, :].rearrange("p c (h i) -> p (c h) i", i=64), og, st2[0:C, 0:G2].unsqueeze(2).broadcast_to([C, G2, 64]), ALU.mult,
           r=[Osb, st2], w=[tm["onb"]])
        b_ = bank(); bb = b_.bitcast(BF16)
        for c in range(NCH):
            tr(bb[:, c * C:(c + 1) * C], tm["onb"][0:C, c, :], identB[0:C, 0:C], r=[tm["onb"], identB], w=[b_])
        ts(X("t2"), bb[:, 0:T], col("lw", j), col("lbb", j), ALU.mult, ALU.add, r=[b_, pv], w=[F_["t2"]])
        tt(X("t2"), X("t2"), X("bon"), ALU.add)
        tt(X("t2"), X("t2"), X("Gb"), ALU.mult)
        tt(mT[:, j, c0:c0 + T], X("t2"), X("oT"), ALU.add, r=[F_["t2"], F_["oT"]], w=[(mT, ti)])

    NBLK = 8 if debug is None else debug.get('nblk', 8)
    for j in range(NBLK):
        wb = wbs[j % 2]
        for k in range(8):
            rows_ = w_in[k * 128:(k + 1) * 128, :]
            dma("pool", wb[:, k, 0:896].rearrange("p (s c) -> p s c", c=128),
                rows_[:, 0:7168].rearrange("p (s c) -> p s c", c=1024)[:, :, j * 128:(j + 1) * 128], r=[w_in], w=[wb])
            dma("pool", wb[:, k, 896:1152].rearrange("p (s c) -> p s c", c=128),
                rows_[:, 7456:9504].rearrange("p (s c) -> p s c", c=1024)[:, :, j * 128:(j + 1) * 128], r=[w_in], w=[wb])
        memset(Sh, 0.0, eng="dve"); memset(Shbf, 0.0, eng="dve"); memset(Sr, 0.0, eng="dve"); memset(Srbf, 0.0, eng="dve")
        memset(prevs[:, 3:6], 0.0, eng="dve", w=[prevs])
        for ti, (c0, T, C, NCH, is_s) in enumerate(tiles):
            if debug is not None and 'tsel' in debug and ti not in debug['tsel']:
                continue
            last = (c0 + T == SEQ)
            M_ = masks[C]
            F_ = f32t; B_ = bft
            if is_s:
                s0 = (c0 - SEQ) // 4
                dma("sp", S0all, sh_in[s0:s0 + 4, j].rearrange("s k v -> k s v"), r=[sh_in], w=[S0all])
                cp(S0bf, S0all, eng="pool")
                memset(R0in, 0.0)
                dma("sp", R0in[0:64, :, 0:64], sr_in[s0:s0 + 4, 2 * j].rearrange("s i j -> i s j"), r=[sr_in], w=[R0in])
                dma("sp", R0in[64:128, :, 64:128], sr_in[s0:s0 + 4, 2 * j + 1].rearrange("s i j -> i s j"), r=[sr_in], w=[R0in])
                for s4 in range(0, 4, 4):
                    b_ = bank()
                    for q in range(4):
                        tr(b_[:, q * 128:(q + 1) * 128], R0in[:, s4 + q, :], identF, w=[b_])
                    cp(R0blk[:, s4:s4 + 4, :], b_.rearrange("p (q f) -> p q f", f=128), r=[b_], w=[R0blk])
                cp(R0bf, R0blk, eng="pool")
            bq = proj(wb, 0, 128, ti, c0, T); bf_ = proj(wb, 128, 256, ti, c0, T)
            bi = proj(wb, 256, 384, ti, c0, T); bog = proj(wb, 384, 512, ti, c0, T)
            bga = proj(wb, 896, 1024, ti, c0, T)
            act(F_["th"][:, 0:T], bf_[:, 0:T], AF.Tanh, scale=0.5, r=[bf_], w=[F_["th"]])
            ts(F_["fS"][:, 0:T], F_["th"][:, 0:T], hs1[:, j:j + 1], hs2[:, j:j + 1], ALU.mult, ALU.add, r=[F_["th"], hs1, hs2], w=[F_["fS"]])
            ts(F_["kS"][:, 0:T], F_["th"][:, 0:T], hs1n[:, j:j + 1], hs2c[:, j:j + 1], ALU.mult, ALU.add, r=[F_["th"], hs1n, hs2c], w=[F_["kS"]])
            cumprod(F_["Pc"], F_["fS"], T, C, NCH)
            P.add("dve", lambda e, T=T: e.reciprocal(out=F_["rP"][:, 0:T], in_=F_["Pc"][:, 0:T]), r=[F_["Pc"]], w=[F_["rP"]])
            tt(B_["qt"][:, 0:T], bq[:, 0:T], F_["Pc"][:, 0:T], ALU.mult, r=[bq, F_["Pc"]], w=[B_["qt"]])
            tt(F_["t1"][:, 0:T], F_["kS"][:, 0:T], F_["rP"][:, 0:T], ALU.mult, r=[F_["kS"], F_["rP"]], w=[F_["t1"]])
            cp(B_["kt"][:, 0:T], F_["t1"][:, 0:T], r=[F_["t1"]], w=[B_["kt"]])
            tt(chunk3(B_["kh"][:, 0:T], C), chunk3(F_["t1"][:, 0:T], C), lastbc(F_["Pc"], T, C, NCH), ALU.mult, r=[F_["t1"], F_["Pc"]], w=[B_["kh"]])
            cp(B_["vT"][:, 0:T], bi[:, 0:T], eng="act", r=[bi], w=[B_["vT"]])
            sigm(F_["t2"][:, 0:T], bog[:, 0:T], r=[bog], w=[F_["t2"]])
            tt(F_["t2"][:, 0:T], bog[:, 0:T], F_["t2"][:, 0:T], ALU.mult, r=[bog, F_["t2"]], w=[F_["t2"]])
            sigm(F_["Ga"][:, 0:T], bga[:, 0:T], r=[bga], w=[F_["Ga"]])
            tt(F_["Ga"][:, 0:T], F_["Ga"][:, 0:T], F_["t2"][:, 0:T], ALU.mult, r=[F_["Ga"], F_["t2"]], w=[F_["Ga"]])
            to_tm(tm["Vtm"], B_["vT"], T, C, NCH); to_tm(tm["khtm"], B_["kh"], T, C, NCH)
            for c8 in range(0, NCH, 8):
                b_ = bank()
                for c in range(c8, min(c8 + 8, NCH)):
                    mm(b_[0:C, (c - c8) * C:(c - c8 + 1) * C], B_["kt"][:, c * C:(c + 1) * C], B_["qt"][:, c * C:(c + 1) * C], True, True, w=[b_])
                n8 = min(8, NCH - c8)
                tt(attnT[0:C, c8:c8 + n8, 0:C], b_[0:C, 0:n8 * C].rearrange("p (c t) -> p c t", t=C),
                   M_["incl"].unsqueeze(1).broadcast_to([C, n8, C]), ALU.mult, r=[b_, M_["incl"]], w=[attnT])
            for c in range(NCH):
                csl = slice(c * C, (c + 1) * C)
                sbf = S0bf[:, c, :] if is_s else Shbf
                sin = S0all[:, c, :] if is_s else Sh
                sout = Souts[:, c, :] if is_s else Sh
                if c % 4 == 0:
                    bo = bank()
                oo = bo[0:C, (c % 4) * 128:(c % 4 + 1) * 128]
                mm(oo, attnT[0:C, c, 0:C], tm["Vtm"][0:C, c, :], True, False, r=[attnT, tm["Vtm"]], w=[bo])
                mm(oo, B_["qt"][:, csl], sbf, False, True, r=[B_["qt"], S0bf if is_s else Shbf], w=[bo])
                b2 = bank()
                mm(b2[:, 0:128], tm["khtm"][0:C, c, :], tm["Vtm"][0:C, c, :], True, True, r=[tm["khtm"], tm["Vtm"]], w=[b2])
                stt(sout, sin, F_["Pc"][:, (c + 1) * C - 1:(c + 1) * C], b2[:, 0:128], ALU.mult, ALU.add,
                    r=[S0all if is_s else Sh, F_["Pc"], b2], w=[Souts if is_s else Sh])
                if not is_s:
                    cp(Shbf, Sh, eng="act")
                if c % 4 == 3:
                    cp(Osb[0:C, c - 3:c + 1, :], bo.rearrange("p (c f) -> p c f", f=128)[0:C], eng="act", r=[bo], w=[Osb])
            if is_s:
                dma("sp", hs[s0:s0 + 4, j].rearrange("s k v -> k s v"), Souts, r=[Souts], w=[hs])
            elif last:
                dma("sp", hp[j], Sh, r=[Sh], w=[hp])
            tt(Osq[0:C, 0:NCH, :], Osb[0:C, 0:NCH, :], Osb[0:C, 0:NCH, :], ALU.mult, r=[Osb], w=[Osq])
            P.add("dve", lambda e, C=C, NCH=NCH: e.reduce_sum(out=st_[0:C, 0:NCH], in_=Osq[0:C, 0:NCH, :], axis=AX.X), r=[Osq], w=[st_])
            rsqrt_(st_[0:C, 0:NCH], st_[0:C, 0:NCH], 1.0 / 128, RMS_EPS, stA[0:C, 0:NCH], stB[0:C, 0:NCH])
            tt(Osq[0:C, 0:NCH, :], Osb[0:C, 0:NCH, :], st_[0:C, 0:NCH].unsqueeze(2).broadcast_to([C, NCH, 128]), ALU.mult, r=[Osb, st_], w=[Osq])
            tt(tm["onb"][0:C, 0:NCH, :], Osq[0:C, 0:NCH, :], nwbc[0:C, :].unsqueeze(1).broadcast_to([C, NCH, 128]), ALU.mult, r=[Osq, nwbc], w=[tm["onb"]])
            b_ = bank(); bb = b_.bitcast(BF16)
            for c in range(NCH):
                tr(bb[:, c * C:(c + 1) * C], tm["onb"][0:C, c, :], identB[0:C, 0:C], r=[tm["onb"], identB], w=[b_])
            tt(F_["oT"][:, 0:T], bb[:, 0:T], F_["Ga"][:, 0:T], ALU.mult, r=[b_, F_["Ga"]], w=[F_["oT"]])
            if debug is not None and debug.get('no_rwkv'):
                dump(f'oT{j}_{ti}', F_['oT'][:, 0:T], [128, T])
                continue
            rwkv_block(j, ti, c0, T, C, NCH, is_s, last, wb)

    dump("mT", mT.rearrange("p a b -> p (a b)"), [128, 8 * NT])
    for p4 in range(0, 27, 4):
        b_ = bank(); n4 = min(4, 27 - p4); wdt = min(n4 * 128, RC - p4 * 128)
        for q in range(n4):
            tr(b_[0:17, q * 128:(q + 1) * 128], lastc[:, p4 + q, :], identF, r=[lastc, identF], w=[b_])
        cp(srow[:, 0:n4 * 128], b_[0:17, 0:n4 * 128], r=[b_], w=[srow])
        dma("sp", shp[:, p4 * 128:p4 * 128 + wdt], srow[0:1, 0:wdt], r=[srow], w=[shp])
        dma("sp", shs[:, p4 * 128:p4 * 128 + wdt], srow[1:17, 0:wdt], r=[srow], w=[shs])
    if debug is not None and debug.get("_stop") == 2:
        n = P.emit()
        return nc, dbg, n
    S2.close()
    P.barrier(dummy)
    S1 = Scope()
    xT = S1.sb("xT", [128, 8, NT]); h2T = S1.sb("h2T", [128, 8, NT], BF16)
    nbuf = dict(sq=S1.sb("sq", [128, 8, 512], BF16), rstd=S1.sb("rstd", [128, 512]), tmp=S1.sb("ntmp", [128, 8, 512]), r1=S1.sb("r1", [128, 512]), r2=S1.sb("r2", [128, 512]),
                xtm=[S1.sb("xtm", [128, D]) for _ in range(2)], i=0)
    wo = S1.sb("wo", [128, 8, 128], BF16); wu = S1.sb("wu", [128, 8, 256], BF16); wd = S1.sb("wd", [128, 2, D], BF16)
    aT = S1.sb("aT", [128, 2, 512], BF16); rT = S1.sb("rT", [128, 512], BF16); t64 = S1.sb("t64", [128, 64])
    tiles34 = [(i * 512, 512, False) for i in range(4)] + [(SEQ, 64, True)]
    for ti, (c0, T, is_s) in enumerate(tiles34):
        load_xT(xT[:, :, c0:c0 + T], c0, T, ti)

    def resid_add(f, c0, T, is_s, ps, gtp, gts):
        if not is_s:
            stt(xT[:, f, c0:c0 + T], ps[:, 0:T], gtp[:, f:f + 1], xT[:, f, c0:c0 + T], ALU.mult, ALU.add, r=[ps, gtp, xT], w=[xT])
        else:
            tt(t64[:, 0:T], ps[:, 0:T], gts[:, f, :], ALU.mult, r=[ps, gts], w=[t64])
            tt(xT[:, f, c0:c0 + T], xT[:, f, c0:c0 + T], t64[:, 0:T], ALU.add, r=[xT, t64], w=[xT])

    for f in range(8):
        for k in range(8):
            dma("pool", wo[:, k, :], w_out[k * 128:(k + 1) * 128, f * 128:(f + 1) * 128], r=[w_out], w=[wo])
        for ti, (c0, T, is_s) in enumerate(tiles34):
            b_ = bank()
            for k in range(8):
                mm(b_[:, 0:T], wo[:, k, :], mT[:, k, c0:c0 + T], k == 0, k == 7, r=[wo, mT], w=[b_])
            resid_add(f, c0, T, is_s, b_, gt1p, gt1s)
    for ti, (c0, T, is_s) in enumerate(tiles34):
        norm_to(h2T[:, :, c0:c0 + T], xT[:, :, c0:c0 + T], T, g2p, sh2p, g2s, sh2s, is_s, 0)
    for g in range(16):
        for k in range(8):
            dma("pool", wu[:, k, :], w_up[k * 128:(k + 1) * 128, g * 256:(g + 1) * 256], r=[w_up], w=[wu])
        for q in range(2):
            dma("pool", wd[:, q, :], w_down[g * 256 + q * 128:g * 256 + (q + 1) * 128, :], r=[w_down], w=[wd])
        for ti, (c0, T, is_s) in enumerate(tiles34):
            for q in range(2):
                b_ = bank()
                for k in range(8):
                    mm(b_[:, 0:T], wu[:, k, q * 128:(q + 1) * 128], h2T[:, k, c0:c0 + T], k == 0, k == 7, r=[wu, (h2T, 0)], w=[b_])
                act(rT[:, 0:T], b_[:, 0:T], AF.Relu, r=[b_], w=[rT])
                tt(aT[:, q, 0:T], rT[:, 0:T], rT[:, 0:T], ALU.mult, r=[rT], w=[aT])
            for f in range(8):
                b_ = bank()
                for q in range(2):
                    mm(b_[:, 0:T], wd[:, q, f * 128:(f + 1) * 128], aT[:, q, 0:T], q == 0, q == 1, r=[wd, aT], w=[b_])
                resid_add(f, c0, T, is_s, b_, gt2p, gt2s)
    fnp = pv[:, rows["fn"]:rows["fn"] + 8]
    for ti, (c0, T, is_s) in enumerate(tiles34):
        sq = nbuf["sq"]; rstd = nbuf["rstd"]; tmp = nbuf["tmp"]
        xv_ = xT[:, :, c0:c0 + T]
        tt(sq[:, :, 0:T], xv_, xv_, ALU.mult, r=[xT], w=[sq])
        b_ = bank()
        for fc in range(8):
            mm(b_[:, 0:T], onesB, sq[:, fc, 0:T], fc == 0, fc == 7, w=[b_])
        ts(rstd[:, 0:T], b_[:, 0:T], 1.0 / D, RMS_EPS, ALU.mult, ALU.add, r=[b_], w=[rstd])
        r1 = nbuf["r1"]; r2 = nbuf["r2"]
        cp(r1[:, 0:T], rstd[:, 0:T], eng="act", r=[rstd], w=[r1])
        rsqrt_dve(rstd[:, 0:T], r1[:, 0:T], r2[:, 0:T], [rstd, r1, r2])
        tt(tmp[:, :, 0:T], xv_, rstd[:, 0:T].unsqueeze(1).broadcast_to([128, 8, T]), ALU.mult, r=[xT, rstd], w=[tmp])
        tt(tmp[:, :, 0:T], tmp[:, :, 0:T], fnp.unsqueeze(2).broadcast_to([128, 8, T]), ALU.mult, r=[tmp, pv], w=[tmp])
        for sub in range(0, T, 128):
            n = min(128, T - sub)
            ytm = nbuf["xtm"][nbuf["i"] % 2]; nbuf["i"] += 1
            for half in range(2):
                b_ = bank()
                for f4 in range(4):
                    fc = half * 4 + f4
                    tr(b_[0:n, f4 * 128:(f4 + 1) * 128], tmp[:, fc, sub:sub + n], identF, r=[tmp, identF], w=[b_])
                cp(ytm[0:n, half * 512:(half + 1) * 512], b_[0:n, :], eng="act", r=[b_], w=[ytm])
            dst = yp[c0 + sub:c0 + sub + n, :] if not is_s else ys[sub:sub + n, :]
            dma("sp", dst, ytm[0:n, :], r=[ytm], w=[yp if not is_s else ys])
    n = P.emit()
    S1.close(); G.close()
    return nc, dbg, n


_CACHE = {}


def _core_inputs(inp, c):
    f = lambda a: np.ascontiguousarray(a, dtype=np.float32)
    m = {"xp": inp["x_prompt"][c], "xs": inp["x_sample"][16 * c:16 * c + 16].reshape(64, D),
         "sh": inp["state_hgrn"][0, 16 * c:16 * c + 16], "sr": inp["state_rwkv"][0, 16 * c:16 * c + 16],
         "ss": inp["state_shift"][0, 16 * c:16 * c + 16], "cp": inp["c_prompt"][c:c + 1], "cs": inp["c_sample"][16 * c:16 * c + 16]}
    for k in ("norm1_w", "norm2_w", "rwkv_w0", "rwkv_a0", "rwkv_k_k", "rwkv_k_a", "rwkv_lnx_w", "rwkv_lnx_b", "rwkv_r_k", "final_norm_w"):
        m[k] = inp[k].reshape(8, 128)
    m["ada_w"] = inp["ada_w"][0]; m["ada_b"] = inp["ada_b"].reshape(48, 128); m["w_in"] = inp["w_in"][0]
    m["lb_logits"] = inp["lb_logits"].reshape(16, 128); m["hgrn_norm_w"] = inp["hgrn_norm_w"].reshape(1, 128)
    m["rwkv_mu"] = inp["rwkv_mu"].reshape(1, RC); m["rwkv_w2"] = inp["rwkv_w2"][0]; m["rwkv_a2"] = inp["rwkv_a2"][0]
    m["rwkv_g2"] = inp["rwkv_g2"][0]; m["w_out"] = inp["w_out"][0]; m["w_up"] = inp["w_up"][0]; m["w_down"] = inp["w_down"][0]
    return {k: f(v) for k, v in m.items()}


def kernel(**inputs):
    inp = {k: np.asarray(v) for k, v in inputs.items()}
    if "nc" not in _CACHE:
        _CACHE["nc"] = build()[0]
    nc = _CACHE["nc"]
    in_maps = [_core_inputs(inp, c) for c in range(8)]
    res = run_bass_kernel_spmd(nc, in_maps, core_ids=list(range(8)))
    R = res.results
    g = lambda n: [np.asarray(R[c][n], dtype=np.float32) for c in range(8)]
    y_prompt = np.stack(g("yp"), 0)
    y_sample = np.concatenate([a.reshape(16, 4, D) for a in g("ys")], 0)
    hgrn_p = np.stack(g("hp"), 0)[None]
    rwkv_p = np.stack(g("rp"), 0)[None]
    shift_p = np.concatenate(g("shp"), 0)[None]
    hgrn_s = np.concatenate(g("hs"), 0)[None]
    rwkv_s = np.concatenate(g("rs"), 0)[None]
    shift_s = np.concatenate(g("shs"), 0)[None]
    return (y_prompt, y_sample, hgrn_p, rwkv_p, shift_p, hgrn_s, rwkv_s, shift_s)
```

```python
import contextlib
import numpy as np
import concourse.bass as bass
import concourse.mybir as mybir
from concourse.bass_utils import run_bass_kernel_spmd

F32 = mybir.dt.float32
BF16 = mybir.dt.bfloat16
AF = mybir.ActivationFunctionType
ALU = mybir.AluOpType
AX = mybir.AxisListType

D = 1024
SEQ = 2048
NSEQ_S = 16
TS = 4
NT = SEQ + NSEQ_S * TS
INC = 9504
RC = 3360
NDMASEM = 6
RMS_EPS = 1e-6
LNX_EPS = 64e-5
C1 = -0.5 * float(np.exp(-0.5))


class Prog:
    def __init__(self, nc):
        self.nc = nc
        self.ops = []
        self.last_w = {}
        self.readers = {}
        self.bar = False
        self.sink = None

    def merge(self, A, B, lead=0):
        done = set(); ia = ib = 0
        na = max(1, sum(1 for x in A if x[0] not in ("mark", "need"))); nb = max(1, sum(1 for x in B if x[0] not in ("mark", "need")))
        ca = cb = 0
        def step(L, i):
            x = L[i]
            if x[0] == "mark":
                done.add(x[1]); return True
            if x[0] == "need":
                return x[1] in done
            self.tag = "A" if L is A else "B"
            self.add(*x); self.tag = None; return True
        while ia < len(A) or ib < len(B):
            pickA = ib >= len(B) or (ia < len(A) and (ca - lead) * nb <= cb * na)
            for _ in range(2):
                if pickA and ia < len(A):
                    isop = A[ia][0] not in ("mark", "need")
                    if step(A, ia):
                        ia += 1; ca += isop; break
                elif (not pickA) and ib < len(B):
                    isop = B[ib][0] not in ("mark", "need")
                    if step(B, ib):
                        ib += 1; cb += isop; break
                pickA = not pickA
            else:
                raise RuntimeError("merge deadlock")

    @staticmethod
    def key(x):
        if isinstance(x, tuple):
            if isinstance(x[0], str):
                return x
            return (x[0].tensor.name, x[1])
        if isinstance(x, str):
            return x
        return x.tensor.name

    def add(self, eng, fn, r=(), w=(), dma=False):
        if self.sink is not None:
            if eng in ("mark", "need"):
                self.sink.append((eng, fn))
            else:
                self.sink.append((eng, fn, list(r), list(w), dma))
            return None
        deps = set()
        rk = [self.key(x) for x in r]
        wk = [self.key(x) for x in w]
        if self.bar:
            rk.append("__bar")
        for k in rk:
            if k in self.last_w:
                deps.add(self.last_w[k])
        for k in wk:
            if k in self.last_w:
                deps.add(self.last_w[k])
            deps |= self.readers.get(k, set())
        idx = len(self.ops)
        self.ops.append(dict(eng=eng, fn=fn, deps=deps, dma=dma, users=0, stream=getattr(self, "tag", None), rk=rk, wk=wk))
        for k in rk:
            self.readers.setdefault(k, set()).add(idx)
        for k in wk:
            self.last_w[k] = idx
            self.readers[k] = set()
        return idx

    def barrier(self, dummy):
        allk = set(self.last_w.keys()) | set(self.readers.keys())
        allk.discard("__bar")
        self.bar = False
        self.add("dve", lambda e: e.memset(dummy, 0.0), r=(), w=list(allk) + ["__bar"])
        self.last_w = {"__bar": self.last_w["__bar"]}
        self.readers = {}
        self.bar = True

    def emit(self):
        nc = self.nc
        engs = {"pe": nc.tensor, "act": nc.scalar, "dve": nc.vector, "pool": nc.gpsimd, "sp": nc.sync}
        ops = self.ops

        def skip(d, o):
            return ops[d]["eng"] == "pe" and o["eng"] == "pe" and not ops[d]["dma"] and not o["dma"]

        for o in ops:
            for d in o["deps"]:
                if not skip(d, o):
                    ops[d]["users"] += 1
        with contextlib.ExitStack() as st:
            csem = {e: st.enter_context(nc.semaphore("c_" + e)) for e in engs}
            dsem = {e: [st.enter_context(nc.semaphore(f"d_{e}{i}")) for i in range(NDMASEM)]
                    for e in ("act", "pool", "sp")}
            ccnt = {e: 0 for e in engs}
            dcnt = {e: [0] * NDMASEM for e in dsem}
            drr = {e: 0 for e in dsem}
            waited = {e: {} for e in engs}
            sig = {}

            def wait(e, sem, val):
                k = id(sem)
                if val <= 0 or waited[e].get(k, 0) >= val:
                    return
                engs[e].wait_ge(sem, val)
                waited[e][k] = val

            for i, o in enumerate(ops):
                e = o["eng"]
                for d in sorted(o["deps"]):
                    if d not in sig or skip(d, o):
                        continue
                    s, v = sig[d]
                    wait(e, s, v)
                if o["dma"]:
                    j = drr[e]
                    drr[e] = (j + 1) % NDMASEM
                    s = dsem[e][j]
                    wait(e, s, dcnt[e][j])
                    ins = o["fn"](engs[e])
                    dcnt[e][j] += 16
                    ins.then_inc(s, 16)
                    sig[i] = (s, dcnt[e][j])
                else:
                    ins = o["fn"](engs[e])
                    if o["users"] > 0:
                        ccnt[e] += 1
                        ins.then_inc(csem[e], 1)
                        sig[i] = (csem[e], ccnt[e])
            for e in dsem:
                for j in range(NDMASEM):
                    if dcnt[e][j] > 0:
                        nc.sync.wait_ge(dsem[e][j], dcnt[e][j])
        return len(ops)


def build(debug=None):
    nc = bass.Bass("TRN2", target_bir_lowering=False)
    P = Prog(nc)
    din = lambda n, s: nc.dram_tensor(n, list(s), F32, kind="ExternalInput").ap()
    dout = lambda n, s: nc.dram_tensor(n, list(s), F32, kind="ExternalOutput").ap()
    xp = din("xp", [SEQ, D]); xs = din("xs", [64, D])
    sh_in = din("sh", [16, 8, 128, 128]); sr_in = din("sr", [16, 16, 64, 64]); ss_in = din("ss", [16, RC])
    cpr = din("cp", [1, D]); csm = din("cs", [16, D])
    norm1_w = din("norm1_w", [8, 128]); norm2_w = din("norm2_w", [8, 128])
    ada_w = din("ada_w", [D, 6 * D]); ada_b = din("ada_b", [48, 128])
    w_in = din("w_in", [D, INC]); lb_logits = din("lb_logits", [16, 128])
    hgrn_norm_w = din("hgrn_norm_w", [1, 128]); mu = din("rwkv_mu", [1, RC])
    w0 = din("rwkv_w0", [8, 128]); w2 = din("rwkv_w2", [64, D]); a0 = din("rwkv_a0", [8, 128])
    a2 = din("rwkv_a2", [64, D]); g2 = din("rwkv_g2", [160, D]); k_k = din("rwkv_k_k", [8, 128])
    k_a = din("rwkv_k_a", [8, 128]); r_k = din("rwkv_r_k", [8, 128]); lnx_w = din("rwkv_lnx_w", [8, 128])
    lnx_b = din("rwkv_lnx_b", [8, 128]); w_out = din("w_out", [D, D]); w_up = din("w_up", [D, 4 * D])
    w_down = din("w_down", [4 * D, D]); fnw = din("final_norm_w", [8, 128])
    yp = dout("yp", [SEQ, D]); ys = dout("ys", [64, D])
    hp = dout("hp", [8, 128, 128]); rp = dout("rp", [16, 64, 64]); shp = dout("shp", [1, RC])
    hs = dout("hs", [16, 8, 128, 128]); rs = dout("rs", [16, 16, 64, 64]); shs = dout("shs", [16, RC])
    dbg = {}

    cnt = [0]

    def uname(n):
        cnt[0] += 1
        return f"{n}_{cnt[0]}"

    SBTOT = [0]; SBMAX = [0]

    class Scope:
        def __init__(self):
            self.st = contextlib.ExitStack()

        def sb(self, name, shape, dt=F32):
            nb = int(np.prod(shape[1:])) * (4 if dt == F32 else 2)
            SBTOT[0] += nb; self.tot = getattr(self, "tot", 0) + nb
            SBMAX[0] = max(SBMAX[0], SBTOT[0])
            return self.st.enter_context(nc.sbuf_tensor(uname(name), list(shape), dt)).ap()

        def close(self):
            SBTOT[0] -= getattr(self, "tot", 0)
            self.st.close()

    G = Scope()
    banks = [nc.alloc_psum_tensor(f"bank{i}", [128, 512], F32).ap() for i in range(8)]
    bk = [0]

    pools = {None: list(range(8)), "H": [1, 2], "R": [3, 4, 5, 6, 7]}
    cur = [None]
    bkp = {"H": 0, "R": 0}

    def bank():
        if cur[0] is None:
            b = banks[bk[0] % 8]
            bk[0] += 1
            return b
        pl = pools[cur[0]]
        b = banks[pl[bkp[cur[0]] % len(pl)]]
        bkp[cur[0]] += 1
        return b

    def dump(name, ap, shape, keys=None):
        if debug is None or (name not in debug and "all" not in debug):
            return
        d = nc.dram_tensor("dbg_" + name, list(shape), F32, kind="ExternalOutput").ap()
        dbg[name] = d
        q = "pool" if ap.dtype != F32 else "sp"
        rk = [ap] + [(ap, t_) for t_ in range(16)] if keys is None else keys
        P.add(q, lambda e: e.dma_start(out=d, in_=ap), r=rk, w=[d], dma=True)

    def dma(q, out, in_, r=(), w=()):
        P.add(q, lambda e: e.dma_start(out=out, in_=in_), r=list(r) or [in_], w=list(w) or [out], dma=True)

    def act(out, in_, func, bias=0.0, scale=1.0, r=(), w=(), accum=None):
        rr = list(r) or [in_]
        if not isinstance(bias, float):
            rr.append(bias)
        if not isinstance(scale, float):
            rr.append(scale)
        ww = list(w) or [out]
        if accum is not None:
            ww.append(accum)
        if accum is not None:
            P.add("act", lambda e: e.activation(out=out, in_=in_, func=func, bias=bias, scale=scale, accum_out=accum), r=rr, w=ww)
        else:
            P.add("act", lambda e: e.activation(out=out, in_=in_, func=func, bias=bias, scale=scale), r=rr, w=ww)

    def tt(out, in0, in1, op, eng="dve", r=(), w=()):
        P.add(eng, lambda e: e.tensor_tensor(out=out, in0=in0, in1=in1, op=op), r=list(r) or [in0, in1], w=list(w) or [out])

    def ts(out, in0, s1, s2, op0, op1=None, eng="dve", r=(), w=()):
        rr = list(r) or [in0]
        for s in (s1, s2):
            if s is not None and not isinstance(s, float):
                rr.append(s)
        if op1 is None:
            P.add(eng, lambda e: e.tensor_scalar(out=out, in0=in0, scalar1=s1, scalar2=None, op0=op0), r=rr, w=list(w) or [out])
        else:
            P.add(eng, lambda e: e.tensor_scalar(out=out, in0=in0, scalar1=s1, scalar2=s2, op0=op0, op1=op1), r=rr, w=list(w) or [out])

    def aff(out, in_, scale, bias, r=(), w=()):
        act(out, in_, AF.Identity, bias=bias, scale=scale, r=r, w=w)

    def stt(out, in0, scalar, in1, op0, op1, r=(), w=()):
        rr = list(r) or [in0, in1]
        if not isinstance(scalar, float):
            rr.append(scalar)
        P.add("dve", lambda e: e.scalar_tensor_tensor(out=out, in0=in0, scalar=scalar, in1=in1, op0=op0, op1=op1), r=rr, w=list(w) or [out])

    def cp(out, in_, eng="dve", r=(), w=()):
        if eng == "act":
            act(out, in_, AF.Copy, r=r, w=w)
        else:
            P.add(eng, lambda e: e.tensor_copy(out=out, in_=in_), r=list(r) or [in_], w=list(w) or [out])

    def memset(ap, val, eng="pool", w=()):
        P.add(eng, lambda e: e.memset(ap, val), w=list(w) or [ap])

    def mm(out, lhsT, rhs, start, stop, r=(), w=()):
        P.add("pe", lambda e: e.matmul(out, lhsT, rhs, start=start, stop=stop), r=list(r) or [lhsT, rhs], w=list(w) or [out])

    def tr(out, in_, ident, r=(), w=()):
        P.add("pe", lambda e: e.transpose(out=out, in_=in_, identity=ident), r=list(r) or [in_, ident], w=list(w) or [out])

    I32 = mybir.dt.int32

    def rsqrt_dve(y, x, t1, keys):
        P.add("dve", lambda e: e.tensor_scalar(out=y.bitcast(I32), in0=x.bitcast(I32), scalar1=-0.5, scalar2=1597463007.0,
                                               op0=ALU.mult, op1=ALU.add), r=keys, w=keys)
        for _ in range(2):
            tt(t1, y, y, ALU.mult, r=keys, w=keys)
            stt(t1, t1, -0.5, x, ALU.mult, ALU.mult, r=keys, w=keys)
            stt(y, t1, 1.5, y, ALU.add, ALU.mult, r=keys, w=keys)

    def rsqrt_(out, in_, scale, eps, tA, tB):
        ts(tA, in_, scale, eps, ALU.mult, ALU.add)
        rsqrt_dve(out, tA, tB, [out, tA, tB])

    ones128 = G.sb("ones128", [128, 128]); identF = G.sb("identF", [128, 128]); identB = G.sb("identB", [128, 128], BF16)
    memset(ones128, 1.0)
    P.add("pool", lambda e: e.affine_select(out=identF, in_=ones128, pattern=[[1, 128]], compare_op=ALU.is_equal,
                                           fill=0.0, base=0, channel_multiplier=-1), r=[ones128], w=[identF])
    cp(identB, identF)
    onesB = G.sb("onesB", [128, 128], BF16)
    cp(onesB, ones128)
    blkones = G.sb("blkones", [128, 128], BF16)
    memset(blkones, 0.0)
    memset(blkones[0:64, 0:64], 1.0, w=[blkones]); memset(blkones[64:128, 64:128], 1.0, w=[blkones])
    blkneg = G.sb("blkneg", [128, 128])
    memset(blkneg, 0.0)
    memset(blkneg[0:64, 0:64], -1.0, w=[blkneg]); memset(blkneg[64:128, 64:128], -1.0, w=[blkneg])
    blkpos = G.sb("blkpos", [128, 128])
    memset(blkpos, 0.0)
    memset(blkpos[0:64, 0:64], 1.0, w=[blkpos]); memset(blkpos[64:128, 64:128], 1.0, w=[blkpos])
    zeros = G.sb("zeros", [128, 64]); memset(zeros, 0.0)
    dummy = G.sb("dummy", [128, 1])
    masks = {}
    for C in (64, 4):
        mi = G.sb(f"mi{C}", [C, C]); ms = G.sb(f"ms{C}", [C, C]); mls = G.sb(f"mls{C}", [C, C])
        P.add("pool", lambda e, mi=mi, C=C: e.affine_select(out=mi, in_=ones128[0:C, 0:C], pattern=[[1, C]], compare_op=ALU.is_ge,
                                                         fill=0.0, base=0, channel_multiplier=-1), r=[ones128], w=[mi])
        P.add("pool", lambda e, ms=ms, C=C: e.affine_select(out=ms, in_=ones128[0:C, 0:C], pattern=[[1, C]], compare_op=ALU.is_gt,
                                                         fill=0.0, base=0, channel_multiplier=-1), r=[ones128], w=[ms])
        P.add("pool", lambda e, mls=mls, C=C: e.affine_select(out=mls, in_=ones128[0:C, 0:C], pattern=[[-1, C]], compare_op=ALU.is_gt,
                                                           fill=0.0, base=0, channel_multiplier=1), r=[ones128], w=[mls])
        msn = G.sb(f"msn{C}", [C, C]); mlsn = G.sb(f"mlsn{C}", [C, C])
        ts(msn, ms, -1.0, None, ALU.mult); ts(mlsn, mls, -1.0, None, ALU.mult)
        masks[C] = dict(incl=mi, strict=ms, nstrict=msn, nlstrict=mlsn)

    stgA = G.sb("stgA", [128, 128]); stgB = G.sb("stgB", [48, 128])
    memset(stgA, 0.0)
    rows = {}
    r0 = 0
    for nm, src, n in (("n1", norm1_w, 8), ("n2", norm2_w, 8), ("lb", lb_logits, 16), ("w0", w0, 8), ("a0", a0, 8),
                       ("kk", k_k, 8), ("ka", k_a, 8), ("rk", r_k, 8), ("lw", lnx_w, 8), ("lbb", lnx_b, 8), ("fn", fnw, 8)):
        dma("sp", stgA[r0:r0 + n, :], src, w=[stgA]); rows[nm] = r0; r0 += n
    dma("sp", stgA[r0:r0 + 26, :], mu[:, 0:3328].rearrange("o (n p) -> (o n) p", p=128), w=[stgA]); rows["mu"] = r0; r0 += 26
    dma("sp", stgA[r0:r0 + 1, 0:32], mu[:, 3328:3360], w=[stgA]); rows["mu2"] = r0; r0 += 1
    assert r0 <= 128
    dma("sp", stgB, ada_b)
    pv = G.sb("pv", [128, 128]); pvB = G.sb("pvB", [128, 48])
    b_ = bank(); tr(b_[:, 0:128], stgA, identF); cp(pv, b_[:, 0:128])
    b_ = bank(); tr(b_[:, 0:48], stgB, identF[0:48, 0:48]); cp(pvB, b_[:, 0:48])
    col = lambda nm, i=0: pv[:, rows[nm] + i: rows[nm] + i + 1]
    hs1 = G.sb("hs1", [128, 8]); hs2 = G.sb("hs2", [128, 8]); hs1n = G.sb("hs1n", [128, 8]); hs2c = G.sb("hs2c", [128, 8])
    tt(hs1, pv[:, rows["lb"]:rows["lb"] + 8], pv[:, rows["lb"] + 8:rows["lb"] + 16], ALU.subtract)
    act(hs2, hs1, AF.Tanh, scale=0.5)
    ts(hs1, hs2, -0.25, 0.25, ALU.mult, ALU.add)
    ts(hs2, hs2, 0.25, 0.75, ALU.mult, ALU.add)
    ts(hs1n, hs1, -1.0, None, ALU.mult)
    ts(hs2c, hs2, -1.0, 1.0, ALU.mult, ALU.add)
    hw0 = G.sb("hw0", [128, 8]); ha0 = G.sb("ha0", [128, 8]); omka = G.sb("omka", [128, 8])
    ts(hw0, pv[:, rows["w0"]:rows["w0"] + 8], 0.5, None, ALU.mult)
    ts(ha0, pv[:, rows["a0"]:rows["a0"] + 8], 0.5, None, ALU.mult)
    ts(omka, pv[:, rows["ka"]:rows["ka"] + 8], -1.0, 1.0, ALU.mult, ALU.add)
    nwbc = G.sb("nwbc", [64, 128]); dma("sp", nwbc, hgrn_norm_w.partition_broadcast(64))

    modT = G.sb("modT", [128, 48, 17])
    g1p = G.sb("g1p", [128, 8]); g2p = G.sb("g2p", [128, 8])
    sh1p = G.sb("sh1p", [128, 8]); sh2p = G.sb("sh2p", [128, 8]); gt1p = G.sb("gt1p", [128, 8]); gt2p = G.sb("gt2p", [128, 8])
    g1s = G.sb("g1s", [128, 8, 64]); g2s = G.sb("g2s", [128, 8, 64]); sh1s = G.sb("sh1s", [128, 8, 64]); sh2s = G.sb("sh2s", [128, 8, 64])
    gt1s = G.sb("gt1s", [128, 8, 64]); gt2s = G.sb("gt2s", [128, 8, 64])
    mT = G.sb("mT", [128, 8, NT], BF16)
    S0 = Scope()
    c17 = S0.sb("c17", [17, D]); s17 = S0.sb("s17", [17, D]); s17b = S0.sb("s17b", [17, D], BF16); scT = S0.sb("scT", [128, 8, 32], BF16)
    dma("sp", c17[0:1, :], cpr, w=[c17]); dma("sp", c17[1:17, :], csm, w=[c17])
    act(s17, c17, AF.Tanh, scale=0.5)
    ts(s17, s17, 0.5, 0.5, ALU.mult, ALU.add)
    tt(s17b, s17, c17, ALU.mult)
    b_ = bank(); bb = b_.bitcast(BF16)
    for k in range(8):
        tr(bb[:, k * 32:k * 32 + 17], s17b[:, k * 128:(k + 1) * 128], identB[0:17, 0:17], w=[b_])
    cp(scT[:, :, 0:17], bb[:, 0:256].rearrange("p (k c) -> p k c", c=32)[:, :, 0:17], r=[b_])
    awb = [S0.sb("awb", [128, 8, 512], BF16) for _ in range(2)]
    for g in range(12):
        wt = awb[g % 2]
        for k in range(8):
            dma("pool", wt[:, k, :], ada_w[k * 128:(k + 1) * 128, g * 512:(g + 1) * 512], w=[wt])
        b_ = bank()
        for m in range(4):
            for k in range(8):
                mm(b_[:, m * 32:m * 32 + 17], wt[:, k, m * 128:(m + 1) * 128], scT[:, k, 0:17], k == 0, k == 7, w=[b_])
        for m in range(4):
            ts(modT[:, g * 4 + m, :], b_[:, m * 32:m * 32 + 17], pvB[:, g * 4 + m:g * 4 + m + 1], None, ALU.add, r=[b_, pvB], w=[modT])
    def modp(part):
        return modT[:, part * 8:(part + 1) * 8, 0]
    def mods(part):
        return modT[:, part * 8:(part + 1) * 8, 1:17].unsqueeze(3).broadcast_to([128, 8, 16, 4])
    stt(g1p, modp(1), 1.0, pv[:, rows["n1"]:rows["n1"] + 8], ALU.add, ALU.mult, r=[modT, pv])
    stt(g2p, modp(4), 1.0, pv[:, rows["n2"]:rows["n2"] + 8], ALU.add, ALU.mult, r=[modT, pv])
    for dst, part in ((sh1p, 0), (gt1p, 2), (sh2p, 3), (gt2p, 5)):
        cp(dst, modp(part), r=[modT])
    def mods3(part):
        return modT[:, part * 8:(part + 1) * 8, 1:17]
    for t4 in range(4):
        for dst, part in ((sh1s, 0), (gt1s, 2), (sh2s, 3), (gt2s, 5)):
            cp(dst[:, :, t4::4], mods3(part), r=[modT], w=[dst])
        for dst, part, nm in ((g1s, 1, "n1"), (g2s, 4, "n2")):
            stt(dst[:, :, t4::4], mods3(part), 1.0, pv[:, rows[nm]:rows[nm] + 8].unsqueeze(2).broadcast_to([128, 8, 16]),
                ALU.add, ALU.mult, r=[modT, pv], w=[dst])
    S0.close()
    P.barrier(dummy)

    tiles = [(i * 256, 256, 64, 4, False) for i in range(8)] + [(SEQ + 16 * i, 16, 4, 4, True) for i in range(4)]

    def norm_to(hT_dst, xT_t, T, gp, shp_, gs, shs_, is_s, ti):
        sq = nbuf["sq"]
        tt(sq[:, :, 0:T], xT_t[:, :, 0:T], xT_t[:, :, 0:T], ALU.mult, r=[xT_t], w=[sq])
        b_ = bank()
        for fc in range(8):
            mm(b_[:, 0:T], onesB, sq[:, fc, 0:T], fc == 0, fc == 7, w=[b_])
        rstd = nbuf["rstd"]
        ts(rstd[:, 0:T], b_[:, 0:T], 1.0 / D, RMS_EPS, ALU.mult, ALU.add, r=[b_], w=[rstd])
        r1 = nbuf["r1"]; r2 = nbuf["r2"]
        cp(r1[:, 0:T], rstd[:, 0:T], eng="act", r=[rstd], w=[r1])
        rsqrt_dve(rstd[:, 0:T], r1[:, 0:T], r2[:, 0:T], [rstd, r1, r2])
        tmp = nbuf["tmp"]
        tt(tmp[:, :, 0:T], xT_t[:, :, 0:T], rstd[:, 0:T].unsqueeze(1).broadcast_to([128, 8, T]), ALU.mult, r=[xT_t, rstd], w=[tmp])
        if not is_s:
            for fc in range(8):
                ts(hT_dst[:, fc, :], tmp[:, fc, 0:T], gp[:, fc:fc + 1], shp_[:, fc:fc + 1], ALU.mult, ALU.add,
                   r=[tmp, gp, shp_], w=[(hT_dst, ti)])
        else:
            tt(tmp[:, :, 0:T], tmp[:, :, 0:T], gs, ALU.mult, r=[tmp, gs], w=[tmp])
            tt(hT_dst, tmp[:, :, 0:T], shs_, ALU.add, r=[tmp, shs_], w=[(hT_dst, ti)])

    def load_xT(xT_t, c0, T, ti):
        for sub in range(0, T, 128):
            n = min(128, T - sub)
            xtm = nbuf["xtm"][nbuf["i"] % 2]; nbuf["i"] += 1
            src = xp[c0 + sub:c0 + sub + n, :] if c0 < SEQ else xs[c0 - SEQ + sub:c0 - SEQ + sub + n, :]
            dma("sp", xtm[0:n, :], src, w=[xtm])
            for half in range(2):
                b_ = bank()
                for f4 in range(4):
                    fc = half * 4 + f4
                    tr(b_[:, f4 * 128:f4 * 128 + n], xtm[0:n, fc * 128:(fc + 1) * 128], identF[0:n, 0:n], w=[b_])
                cp(xT_t[:, half * 4:half * 4 + 4, sub:sub + n], b_.rearrange("p (f t) -> p f t", t=128)[:, :, 0:n], eng="act", r=[b_], w=[xT_t])

    S2 = Scope()
    hT = S2.sb("hT", [128, 8, NT], BF16)
    S1 = Scope()
    nbuf = dict(sq=S1.sb("sq", [128, 8, 512], BF16), rstd=S1.sb("rstd", [128, 512]), tmp=S1.sb("ntmp", [128, 8, 512]), r1=S1.sb("r1", [128, 512]), r2=S1.sb("r2", [128, 512]),
                xtm=[S1.sb("xtm", [128, D]) for _ in range(2)], i=0)
    xT_ts = [S1.sb("xT_t", [128, 8, 512]) for _ in range(2)]
    for ti, (c0, T, C, NCH, is_s) in enumerate(tiles):
        xT_t = xT_ts[ti % 2]
        load_xT(xT_t, c0, T, ti)
        so = max(c0 - SEQ, 0)
        norm_to(hT[:, :, c0:c0 + T], xT_t, T, g1p, sh1p, g1s[:, :, so:so + min(T, 16)], sh1s[:, :, so:so + min(T, 16)], is_s, ti)
    S1.close()
    P.barrier(dummy)

    W = lambda name, shape, dt=F32: S2.sb(name, shape, dt)
    wbH = W("wbH", [128, 8, 640], BF16); wbR = W("wbR", [128, 8, 512], BF16)
    wl = wbH[:, :, 0:288]
    w2b = W("w2b", [128, D], BF16)
    g2b = W("g2b", [128, D], BF16); g2c = W("g2c", [32, D], BF16)
    TW = W("TW", [128, NT], BF16); SG1 = W("SG1", [128, NT], BF16); SG2 = W("SG2", [32, NT], BF16)
    srow = W("srow", [17, 512]); ssT = W("ssT", [128, 27, 16]); lastc = W("lastc", [128, 27, 17])
    prevs = W("prevs", [128, 8])
    rcS = W("rcS", [128, 256]); dsh = W("dsh", [128, 256])
    S0all = W("S0all", [128, 4, 128]); S0bf = W("S0bf", [128, 4, 128], BF16); Souts = S0all
    Sh = W("Sh", [128, 128]); Shbf = W("Shbf", [128, 128], BF16)
    R0in = W("R0in", [128, 4, 128]); R0blk = W("R0blk", [128, 4, 128]); R0bf = W("R0bf", [128, 4, 128], BF16); Routs = R0blk
    Sr = W("Sr", [128, 128]); Srbf = W("Srbf", [128, 128], BF16); Srt = W("Srt", [128, 128])
    f32t = {n: W(n, [128, 256]) for n in ("th", "fS", "kS", "Pc", "rP", "Ga", "t1", "t2", "Gmm", "kap", "kmod", "bS", "bon", "gS", "Gb",
                                          "xr", "xk", "xv", "Gm", "rG", "aS", "u1", "u2")}
    f32t["n1"] = f32t["bon"]; f32t["n2"] = f32t["u1"]
    oTs = [W("oT", [128, 256]) for _ in range(2)]
    bft = {n: W(n, [128, 256], BF16) for n in ("qt", "kt", "kh", "vT", "sqb", "kpt", "bt", "bh", "rt", "ktl", "khr", "vTr")}
    KR = W("KR", [128, 4, 128], BF16)
    tm = {n: W(n, [64, 4, 128], BF16) for n in ("Vtm", "khtm", "Ktm", "bhtm", "khrtm", "W1", "Un", "Ktp", "onb", "Vtr", "onr")}
    msk = {n: W(n, [64, 4, 128], BF16) for n in ("PbT", "PkT")}
    Abm = {n: W(n, [64, 4, 128], BF16) for n in ("Y", "Z", "Y2", "Z2", "TTa", "TTb")}
    attnT = W("attnT", [64, 4, 64], BF16)
    Osb = W("Osb", [64, 4, 128]); Osq = W("Osq", [64, 4, 128]); st_ = W("st_", [64, 16]); stA = W("stA", [64, 16]); stB = W("stB", [64, 16])
    Osb2 = W("Osb2", [64, 4, 128]); Osq2 = W("Osq2", [64, 4, 128]); st3 = W("st3", [64, 16]); st4 = W("st4", [64, 16]); stC = W("stC", [64, 16]); stD = W("stD", [64, 16])
    Mblk = W("Mblk", [128, 4, 128], BF16); Ncomb = W("Ncomb", [128, 4, 128]); RpT = W("RpT", [128, 256], BF16)

    dma("pool", w2b[0:64, :], w2, w=[w2b]); dma("pool", w2b[64:128, :], a2, w=[w2b])
    dma("pool", g2b, g2[0:128, :]); dma("pool", g2c, g2[128:160, :])
    for k in range(8):
        dma("pool", wl[:, k, :], w_in[k * 128:(k + 1) * 128, 7168:7456], w=[wl])
    for p4 in range(0, 27, 4):
        n4 = min(4, 27 - p4); wdt = min(n4 * 128, RC - p4 * 128)
        memset(srow, 0.0, eng="dve")
        dma("sp", srow[0:16, 0:wdt], ss_in[:, p4 * 128:p4 * 128 + wdt], r=[ss_in], w=[srow])
        b_ = bank()
        for q in range(n4):
            tr(b_[:, q * 16:(q + 1) * 16], srow[0:16, q * 128:(q + 1) * 128], identF[0:16, 0:16], r=[srow, identF], w=[b_])
        cp(ssT[:, p4:p4 + n4, :], b_[:, 0:n4 * 16].rearrange("p (q s) -> p q s", s=16), r=[b_], w=[ssT])
    memset(prevs, 0.0); memset(lastc, 0.0)

    def proj(wt, c_lo, c_hi, ti, c0, T):
        b_ = bank(); n = c_hi - c_lo
        for k in range(8):
            mm(b_[0:n, 0:T], wt[:, k, c_lo:c_hi], hT[:, k, c0:c0 + T], k == 0, k == 7, r=[wt, (hT, ti)], w=[b_])
        return b_

    def shift_mix(dst, ps, n, T, is_s, mucol, pcol, part, last_tile, s0=0):
        cp(rcS[0:n, 0:T], ps[0:n, 0:T], eng="act", r=[ps], w=[rcS])
        if not is_s:
            tt(dsh[0:n, 1:T], rcS[0:n, 0:T - 1], rcS[0:n, 1:T], ALU.subtract, r=[rcS], w=[dsh])
            tt(dsh[0:n, 0:1], prevs[0:n, pcol:pcol + 1], rcS[0:n, 0:1], ALU.subtract, r=[rcS, prevs], w=[dsh])
            cp(prevs[0:n, pcol:pcol + 1], rcS[0:n, T - 1:T], r=[rcS, dsh], w=[prevs])
            if last_tile:
                cp(lastc[0:n, part, 0:1], rcS[0:n, T - 1:T], r=[rcS], w=[lastc])
        else:
            r3 = rcS[0:n, 0:T].rearrange("p (s t) -> p s t", t=4); d3 = dsh[0:n, 0:T].rearrange("p (s t) -> p s t", t=4)
            tt(d3[:, :, 1:4], r3[:, :, 0:3], r3[:, :, 1:4], ALU.subtract, r=[rcS], w=[dsh])
            tt(d3[:, :, 0], ssT[0:n, part, s0:s0 + T // 4], r3[:, :, 0], ALU.subtract, r=[rcS, ssT], w=[dsh])
            cp(lastc[0:n, part, 1 + s0:1 + s0 + T // 4], r3[:, :, 3], r=[rcS], w=[lastc])
        stt(dst[0:n, 0:T], dsh[0:n, 0:T], mucol[0:n, :], rcS[0:n, 0:T], ALU.mult, ALU.add, r=[dsh, rcS, pv], w=[dst])

    def sigm(dst, src, r=(), w=(), bias=0.0):
        act(dst, src, AF.Tanh, bias=bias, scale=0.5, r=r, w=w)
        aff(dst, dst, 0.5, 0.5)

    for ti, (c0, T, C, NCH, is_s) in enumerate(tiles):
        last = (c0 + T == SEQ)
        b_ = proj(wl, 0, 128, ti, c0, T)
        shift_mix(f32t["t1"], b_, 128, T, is_s, col("mu", 24), 0, 24, last, max(c0 - SEQ, 0) // 4)
        act(TW[0:64, c0:c0 + T], f32t["t1"][0:64, 0:T], AF.Tanh, r=[f32t["t1"]], w=[(TW, ti)])
        cp(TW[64:128, c0:c0 + T], f32t["t1"][64:128, 0:T], eng="act", r=[f32t["t1"]], w=[(TW, ti)])
        b_ = proj(wl, 128, 256, ti, c0, T)
        shift_mix(f32t["t1"], b_, 128, T, is_s, col("mu", 25), 1, 25, last, max(c0 - SEQ, 0) // 4)
        sigm(f32t["t2"][:, 0:T], f32t["t1"][:, 0:T], r=[f32t["t1"]], w=[f32t["t2"]])
        cp(SG1[:, c0:c0 + T], f32t["t2"][:, 0:T], r=[f32t["t2"]], w=[(SG1, ti)])
        b_ = proj(wl, 256, 288, ti, c0, T)
        shift_mix(f32t["t1"], b_, 32, T, is_s, col("mu2"), 2, 26, last, max(c0 - SEQ, 0) // 4)
        sigm(f32t["t2"][0:32, 0:T], f32t["t1"][0:32, 0:T], r=[f32t["t1"]], w=[f32t["t2"]])
        cp(SG2[:, c0:c0 + T], f32t["t2"][0:32, 0:T], r=[f32t["t2"]], w=[(SG2, ti)])

    if debug is not None and debug.get("_stop") == "lora":
        dump("TW", TW, [128, NT]); dump("SG1", SG1, [128, NT]); dump("SG2", SG2, [32, NT]); dump("lastc", lastc.rearrange("p a b -> p (a b)"), [128, 27 * 17])
        n = P.emit()
        return nc, dbg, n

    def chunk3(ap, C):
        return ap.rearrange("p (c t) -> p c t", t=C)

    def to_tm(dst, src_bf, T, C, NCH):
        for c4 in range(0, NCH, 8):
            b_ = bank(); bb = b_.bitcast(BF16)
            for c in range(c4, min(c4 + 8, NCH)):
                tr(bb[0:C, (c - c4) * 128:(c - c4 + 1) * 128], src_bf[:, c * C:(c + 1) * C], identB, r=[src_bf, identB], w=[b_])
            n8 = min(8, NCH - c4)
            cp(dst[0:C, c4:c4 + n8, :], bb[0:C, 0:n8 * 128].rearrange("p (c f) -> p c f", f=128), eng="act", r=[b_], w=[dst])

    def cumprod(dst, src, T, C, NCH):
        for c in range(NCH):
            sl = slice(c * C, (c + 1) * C)
            P.add("dve", lambda e, sl=sl: e.tensor_tensor_scan(out=dst[:, sl], data0=src[:, sl], data1=zeros[:, 0:C], initial=1.0,
                                                              op0=ALU.mult, op1=ALU.add), r=[src, zeros], w=[dst])

    def lastbc(ap, T, C, NCH):
        return chunk3(ap[:, 0:T], C)[:, :, C - 1:C].broadcast_to([128, NCH, C])


    def rwkv_block(j, ti, c0, T, C, NCH, is_s, last, wb, oT, gi):
        F_ = f32t; B_ = bft; M_ = masks[C]; jc = slice(j * 128, (j + 1) * 128)
        L = 6 if C == 64 else 2
        if is_s:
            s0 = (c0 - SEQ) // 4
            memset(R0in, 0.0)
            dma("sp", R0in[0:64, :, 0:64], sr_in[s0:s0 + 4, 2 * j].rearrange("s i j -> i s j"), r=[sr_in], w=[R0in])
            dma("sp", R0in[64:128, :, 64:128], sr_in[s0:s0 + 4, 2 * j + 1].rearrange("s i j -> i s j"), r=[sr_in], w=[R0in])
            b_ = bank()
            for q in range(4):
                tr(b_[:, q * 128:(q + 1) * 128], R0in[:, q, :], identF, w=[b_])
            cp(R0blk[:, 0:4, :], b_.rearrange("p (q f) -> p q f", f=128), r=[b_], w=[R0blk])
            cp(R0bf, R0blk, eng="pool")
        br = proj(wb, 0, 128, ti, c0, T); shift_mix(F_["xr"], br, 128, T, is_s, col("mu", j), 3, j, last, max(c0 - SEQ, 0) // 4)
        bk_ = proj(wb, 128, 256, ti, c0, T); shift_mix(F_["xk"], bk_, 128, T, is_s, col("mu", 8 + j), 4, 8 + j, last, max(c0 - SEQ, 0) // 4)
        bv = proj(wb, 256, 384, ti, c0, T); shift_mix(F_["xv"], bv, 128, T, is_s, col("mu", 16 + j), 5, 16 + j, last, max(c0 - SEQ, 0) // 4)
        bgb = proj(wb, 384, 512, ti, c0, T)
        sigm(F_["Gb"][:, 0:T], bgb[:, 0:T], r=[bgb], w=[F_["Gb"]])
        bw = bank(); mm(bw[:, 0:T], w2b[0:64, jc], TW[0:64, c0:c0 + T], True, True, r=[w2b, (TW, ti)], w=[bw])
        ba = bank(); mm(ba[:, 0:T], w2b[64:128, jc], TW[64:128, c0:c0 + T], True, True, r=[w2b, (TW, ti)], w=[ba])
        bg = bank(); mm(bg[:, 0:T], g2b[:, jc], SG1[:, c0:c0 + T], True, False, r=[g2b, (SG1, ti)], w=[bg])
        mm(bg[:, 0:T], g2c[:, jc], SG2[:, c0:c0 + T], False, True, r=[g2c, (SG2, ti)], w=[bg])
        X = lambda n: F_[n][:, 0:T]
        Bx = lambda n: B_[n][:, 0:T]
        act(X("u1"), bw[:, 0:T], AF.Tanh, bias=hw0[:, j:j + 1], scale=0.5, r=[bw], w=[F_["u1"]])
        act(X("u1"), X("u1"), AF.Exp, bias=C1, scale=C1)
        cumprod(F_["Gm"], F_["u1"], T, C, NCH)
        P.add("dve", lambda e: e.reciprocal(out=X("rG"), in_=X("Gm")), r=[F_["Gm"]], w=[F_["rG"]])
        cp(chunk3(X("Gmm"), C)[:, :, 1:C], chunk3(X("Gm"), C)[:, :, 0:C - 1], r=[F_["Gm"]], w=[F_["Gmm"]])
        memset(chunk3(X("Gmm"), C)[:, :, 0:1], 1.0, eng="dve", w=[F_["Gmm"]])
        sigm(X("aS"), ba[:, 0:T], r=[ba], w=[F_["aS"]], bias=ha0[:, j:j + 1])
        cp(X("gS"), bg[:, 0:T], eng="act", r=[bg], w=[F_["gS"]])
        tt(X("Gb"), X("Gb"), X("gS"), ALU.mult, eng="pool")
        aff(X("kap"), X("xk"), col("kk", j), 0.0, r=[F_["xk"], pv], w=[F_["kap"]])
        tt(Bx("sqb"), X("kap"), X("kap"), ALU.mult, eng="pool")
        bn = bank(); mm(bn[:, 0:T], blkones, Bx("sqb"), True, True, w=[bn])
        ts(X("n1"), bn[:, 0:T], 1.0, 1e-24, ALU.mult, ALU.add, r=[bn], w=[F_["n1"]])
        rsqrt_dve(X("u2"), X("n1"), X("n2"), [F_["u2"], F_["n1"], F_["n2"]])
        tt(X("kap"), X("kap"), X("u2"), ALU.mult)
        aff(X("u2"), X("aS"), col("ka", j), omka[:, j:j + 1], r=[F_["aS"], pv, omka], w=[F_["u2"]])
        tt(X("kmod"), X("xk"), X("u2"), ALU.mult, eng="pool")
        tt(X("bS"), X("aS"), X("kap"), ALU.mult)
        stt(Bx("sqb"), X("xr"), col("rk", j), X("kmod"), ALU.mult, ALU.mult, r=[F_["xr"], F_["kmod"], pv], w=[B_["sqb"]])
        bn2 = bank(); mm(bn2[:, 0:T], blkones, Bx("sqb"), True, True, w=[bn2])
        tt(X("bon"), bn2[:, 0:T], X("xv"), ALU.mult, r=[bn2, F_["xv"]], w=[F_["bon"]])
        tt(Bx("kpt"), X("kap"), X("Gmm"), ALU.mult)
        tt(Bx("rt"), X("xr"), X("Gm"), ALU.mult, eng="pool")
        cp(KR[:, 0:NCH, 0:C], chunk3(Bx("kpt"), C), eng="pool", r=[B_["kpt"]], w=[KR])
        cp(KR[:, 0:NCH, 64:64 + C], chunk3(Bx("rt"), C), eng="pool", r=[B_["rt"]], w=[KR])
        tt(X("u2"), X("kmod"), X("rG"), ALU.mult); cp(Bx("ktl"), X("u2"), eng="act")
        tt(chunk3(Bx("khr"), C), chunk3(X("u2"), C), lastbc(F_["Gm"], T, C, NCH), ALU.mult, r=[F_["u2"], F_["Gm"]], w=[B_["khr"]])
        tt(X("u2"), X("bS"), X("rG"), ALU.mult); cp(Bx("bt"), X("u2"), eng="act")
        tt(chunk3(Bx("bh"), C), chunk3(X("u2"), C), lastbc(F_["Gm"], T, C, NCH), ALU.mult, r=[F_["u2"], F_["Gm"]], w=[B_["bh"]])
        cp(Bx("vTr"), X("xv"), eng="act")
        to_tm(tm["Vtr"], B_["vTr"], T, C, NCH); to_tm(tm["Ktm"], B_["kpt"], T, C, NCH)
        to_tm(tm["bhtm"], B_["bh"], T, C, NCH); to_tm(tm["khrtm"], B_["khr"], T, C, NCH)
        if debug is not None and debug.get('rstop', 99) <= 1:
            return
        hv = lambda t_, c_lo, c_hi: t_[0:C, c_lo:c_hi, :].rearrange("p c (h t) -> p c h t", t=64)[:, :, :, 0:C]
        for c4 in range(0, NCH, 4):
            for h in range(2):
                hs_ = slice(64 * h, 64 * h + 64)
                b1 = bank(); b2 = bank(); b3 = bank()
                for q in range(4):
                    c = c4 + q; csl = slice(c * C, (c + 1) * C); o = q * 128
                    mm(b1[0:C, o:o + 128], B_["bt"][hs_, csl], KR[hs_, c, :], True, True, r=[B_["bt"], KR], w=[b1])
                    mm(b2[0:C, o:o + 128], B_["ktl"][hs_, csl], KR[hs_, c, :], True, True, r=[B_["ktl"], KR], w=[b2])
                    mm(b3[0:C, o:o + C], B_["kpt"][hs_, csl], B_["bt"][hs_, csl], True, True, r=[B_["kpt"], B_["bt"]], w=[b3])
                v4_ = lambda b: b[0:C, :].rearrange("p (c x t) -> p c x t", x=2, t=64)
                mk = lambda m_: m_.unsqueeze(1).broadcast_to([C, 4, C])
                dv = lambda t_: t_[0:C, c4:c4 + 4, h * 64:h * 64 + C]
                tt(dv(Abm["Z"]), v4_(b1)[:, :, 0, 0:C], mk(M_["nstrict"]), ALU.mult, r=[b1, M_["nstrict"]], w=[Abm["Z"]])
                tt(dv(msk["PbT"]), v4_(b1)[:, :, 1, 0:C], mk(M_["incl"]), ALU.mult, r=[b1, M_["incl"]], w=[msk["PbT"]])
                tt(dv(tm["W1"]), v4_(b2)[:, :, 0, 0:C], mk(M_["strict"]), ALU.mult, r=[b2, M_["strict"]], w=[tm["W1"]])
                tt(dv(msk["PkT"]), v4_(b2)[:, :, 1, 0:C], mk(M_["incl"]), ALU.mult, r=[b2, M_["incl"]], w=[msk["PkT"]])
                tt(dv(Abm["Y"]), v4_(b3)[:, :, 0, 0:C], mk(M_["nlstrict"]), ALU.mult, r=[b3, M_["nlstrict"]], w=[Abm["Y"]])
        if debug is not None and debug.get('rstop', 99) <= 2:
            return
        tt(hv(Abm["TTa"], 0, NCH), hv(Abm["Z"], 0, NCH), identF[0:C, 0:C].unsqueeze(1).unsqueeze(1).broadcast_to([C, NCH, 2, C]), ALU.add,
           r=[Abm["Z"], identF], w=[Abm["TTa"]])
        Yc, Zc, Yn, Zn, TTc, TTn = Abm["Y"], Abm["Z"], Abm["Y2"], Abm["Z2"], Abm["TTa"], Abm["TTb"]
        pr = [(c, h) for c in range(NCH) for h in range(2)]

        def batch_mm(dst, lhs, rhs, wid_l, wid_r, post):
            for p8 in range(0, len(pr), 8):
                b_ = bank(); grp = pr[p8:p8 + 8]
                for q, (c, h) in enumerate(grp):
                    mm(b_[0:C, q * 64:q * 64 + wid_r], lhs[0:C, c, h * 64:h * 64 + wid_l], rhs[0:C, c, h * 64:h * 64 + wid_r], True, True,
                       r=[lhs, rhs], w=[b_])
                cA = grp[0][0]; cB = grp[-1][0] + 1
                src = b_[0:C, 0:(cB - cA) * 128].rearrange("p (c h t) -> p c h t", h=2, t=64)[:, :, :, 0:wid_r]
                d = dst[0:C, cA:cB, :].rearrange("p c (h t) -> p c h t", t=64)[:, :, :, 0:wid_r]
                post(d, src, b_)

        cpy = lambda d, s_, b_: cp(d, s_, eng="act", r=[b_], w=[d])
        for n in range(1, L):
            batch_mm(Yn, Zc, Yc, C, C, cpy)
            if n < L - 1:
                batch_mm(Zn, Yc, Zc, C, C, cpy)
            addT = lambda d, s_, b_, TTc=TTc: tt(d, s_, TTc[0:C, 0:NCH, :].rearrange("p c (h t) -> p c h t", t=64)[:, :, :, 0:C][:, (d.offset - d.offset):, :, :] if False else s_, s_, ALU.add) if False else None
            for p8 in range(0, len(pr), 8):
                b_ = bank(); grp = pr[p8:p8 + 8]
                for q, (c, h) in enumerate(grp):
                    mm(b_[0:C, q * 64:q * 64 + C], Yn[0:C, c, h * 64:h * 64 + C], TTc[0:C, c, h * 64:h * 64 + C], True, True, r=[Yn, TTc], w=[b_])
                cA = grp[0][0]; cB = grp[-1][0] + 1
                src = b_[0:C, 0:(cB - cA) * 128].rearrange("p (c h t) -> p c h t", h=2, t=64)[:, :, :, 0:C]
                tt(hv(TTn, cA, cB), src, hv(TTc, cA, cB), ALU.add, r=[b_, TTc], w=[TTn])
            Yc, Yn = Yn, Yc; Zc, Zn = Zn, Zc; TTc, TTn = TTn, TTc
        if debug is not None and debug.get('rstop', 99) <= 3:
            return
        TTf = TTc
        AkT = Yn
        cp(AkT[0:C, 0:NCH, :], tm["W1"][0:C, 0:NCH, :], eng="pool", r=[tm["W1"]], w=[AkT])
        batch_mm(tm["W1"], AkT, tm["Vtr"], C, 64, cpy)
        neg = lambda d, s_, b_: ts(d, s_, -1.0, None, ALU.mult, r=[b_], w=[d])
        batch_mm(tm["Un"], TTf, tm["W1"], C, 64, neg)
        batch_mm(tm["Ktp"], TTf, tm["Ktm"], C, 64, cpy)
        if debug is not None and debug.get('rstop', 99) <= 4:
            return
        for c4 in range(0, NCH, 4):
            bM = bank(); bN = bank(); bR = bank()
            for q in range(4):
                c = c4 + q
                mm(bM[:, q * 128:(q + 1) * 128], tm["Ktp"][0:C, c, :], tm["bhtm"][0:C, c, :], True, True, r=[tm["Ktp"], tm["bhtm"]], w=[bM])
                mm(bN[:, q * 128:(q + 1) * 128], tm["khrtm"][0:C, c, :], tm["Vtr"][0:C, c, :], True, False, r=[tm["khrtm"], tm["Vtr"]], w=[bN])
                mm(bN[:, q * 128:(q + 1) * 128], tm["bhtm"][0:C, c, :], tm["Un"][0:C, c, :], False, True, r=[tm["bhtm"], tm["Un"]], w=[bN])
                mm(bR[:, q * 128:(q + 1) * 128], tm["Ktp"][0:C, c, :], msk["PbT"][0:C, c, :], True, True, r=[tm["Ktp"], msk["PbT"]], w=[bR])
            v3 = lambda b: b.rearrange("p (c f) -> p c f", f=128)
            tt(Mblk[:, c4:c4 + 4, :], v3(bM), blkneg.unsqueeze(1).broadcast_to([128, 4, 128]), ALU.mult, r=[bM, blkneg], w=[Mblk])
            tt(Ncomb[:, c4:c4 + 4, :], v3(bN), blkpos.unsqueeze(1).broadcast_to([128, 4, 128]), ALU.mult, r=[bN, blkpos], w=[Ncomb])
            for h in range(2):
                hs_ = slice(64 * h, 64 * h + 64)
                tt(chunk3(RpT[hs_, c4 * C:(c4 + 4) * C], C), chunk3(B_["rt"][hs_, c4 * C:(c4 + 4) * C], C),
                   v3(bR)[hs_, :, h * 64:h * 64 + C], ALU.subtract, r=[B_["rt"], bR], w=[RpT])
        if debug is not None and debug.get('rstop', 99) <= 5:
            return
        for c in range(NCH):
            csl = slice(c * C, (c + 1) * C)
            sbf = R0bf[:, c, :] if is_s else Srbf
            sin = R0blk[:, c, :] if is_s else Sr
            sout = Routs[:, c, :] if is_s else Sr
            kS_ = [R0bf] if is_s else [Srbf]
            if c % 4 == 0:
                bo = bank()
            oo = bo[0:C, (c % 4) * 128:(c % 4 + 1) * 128]
            mm(oo, RpT[:, csl], sbf, True, False, r=[RpT] + kS_, w=[bo])
            for h in range(2):
                hb = slice(h * 64, (h + 1) * 64)
                mm(oo[:, hb], msk["PkT"][0:C, c, h * 64:h * 64 + C], tm["Vtr"][0:C, c, hb], False, False, r=[msk["PkT"], tm["Vtr"]], w=[bo])
                mm(oo[:, hb], msk["PbT"][0:C, c, h * 64:h * 64 + C], tm["Un"][0:C, c, hb], False, h == 1, r=[msk["PbT"], tm["Un"]], w=[bo])
            b3 = bank()
            mm(b3[:, 0:128], Mblk[:, c, :], sbf, True, True, r=[Mblk] + kS_, w=[b3])
            stt(Srt, sin, F_["Gm"][:, (c + 1) * C - 1:(c + 1) * C], Ncomb[:, c, :], ALU.mult, ALU.add,
                r=[R0blk if is_s else Sr, F_["Gm"], Ncomb], w=[Srt])
            tt(sout, Srt, b3[:, 0:128], ALU.add, r=[Srt, b3], w=[Routs if is_s else Sr])
            if not is_s:
                cp(Srbf, Sr, eng="act")
            if c % 4 == 3:
                cp(Osb2[0:C, c - 3:c + 1, :], bo.rearrange("p (c f) -> p c f", f=128)[0:C], eng="act", r=[bo], w=[Osb2])
        if debug is not None and debug.get('rstop', 99) <= 6:
            return
        if is_s:
            s0 = (c0 - SEQ) // 4
            for s4 in range(0, 4, 4):
                b_ = bank()
                for q in range(4):
                    tr(b_[:, q * 128:(q + 1) * 128], Routs[:, s4 + q, :], identF, w=[b_])
                cp(R0in[:, s4:s4 + 4, :], b_.rearrange("p (q f) -> p q f", f=128), r=[b_], w=[R0in])
            dma("sp", rs[s0:s0 + 4, 2 * j].rearrange("s i j -> i s j"), R0in[0:64, :, 0:64], r=[R0in], w=[rs])
            dma("sp", rs[s0:s0 + 4, 2 * j + 1].rearrange("s i j -> i s j"), R0in[64:128, :, 64:128], r=[R0in], w=[rs])
        elif last:
            b_ = bank(); tr(b_[:, 0:128], Sr, identF, w=[b_]); cp(Srt, b_[:, 0:128])
            dma("sp", rp[2 * j], Srt[0:64, 0:64], r=[Srt], w=[rp]); dma("sp", rp[2 * j + 1], Srt[64:128, 64:128], r=[Srt], w=[rp])
        if debug is not None and debug.get('rstop', 99) <= 7:
            return
        G2 = NCH * 2
        og = Osb2[0:C, 0:NCH, :].rearrange("p c (h i) -> p (c h) i", i=64)
        oq = Osq2[0:C, 0:NCH, :].rearrange("p c (h i) -> p (c h) i", i=64)
        P.add("dve", lambda e: e.reduce_sum(out=st3[0:C, 0:G2], in_=og, axis=AX.X), r=[Osb2], w=[st3])
        ts(st3[0:C, 0:G2], st3[0:C, 0:G2], 1.0 / 64, None, ALU.mult)
        tt(og, og, st3[0:C, 0:G2].unsqueeze(2).broadcast_to([C, G2, 64]), ALU.subtract, r=[Osb2, st3], w=[Osb2])
        tt(oq, og, og, ALU.mult, eng="pool", r=[Osb2], w=[Osq2])
        P.add("dve", lambda e: e.reduce_sum(out=st4[0:C, 0:G2], in_=oq, axis=AX.X), r=[Osq2], w=[st4])
        rsqrt_(st4[0:C, 0:G2], st4[0:C, 0:G2], 1.0 / 64, LNX_EPS, stC[0:C, 0:G2], stD[0:C, 0:G2])
        tt(tm["onr"][0:C, 0:NCH, :].rearrange("p c (h i) -> p (c h) i", i=64), og, st4[0:C, 0:G2].unsqueeze(2).broadcast_to([C, G2, 64]), ALU.mult,
           r=[Osb2, st4], w=[tm["onr"]])
        b_ = bank(); bb = b_.bitcast(BF16)
        for c in range(NCH):
            tr(bb[:, c * C:(c + 1) * C], tm["onr"][0:C, c, :], identB[0:C, 0:C], r=[tm["onr"], identB], w=[b_])
        ts(X("u2"), bb[:, 0:T], col("lw", j), col("lbb", j), ALU.mult, ALU.add, r=[b_, pv], w=[F_["u2"]])
        tt(X("u2"), X("u2"), X("bon"), ALU.add, eng="pool")
        tt(X("u2"), X("u2"), X("Gb"), ALU.mult, eng="pool")
        P.add("need", ("oTw", gi))
        tt(mT[:, j, c0:c0 + T], X("u2"), oT[:, 0:T], ALU.add, r=[F_["u2"], oT], w=[(mT, ti)])
        P.add("mark", ("oTr", gi))

    NBLK = 8 if debug is None else debug.get('nblk', 8)

    def hgrn_tile(j, ti, c0, T, C, NCH, is_s, last, wb, oT, gi):
        M_ = masks[C]; F_ = f32t; B_ = bft
        if is_s:
            s0 = (c0 - SEQ) // 4
            dma("sp", S0all, sh_in[s0:s0 + 4, j].rearrange("s k v -> k s v"), r=[sh_in], w=[S0all])
            cp(S0bf, S0all, eng="pool")
        bf_ = proj(wb, 128, 256, ti, c0, T)
        act(F_["th"][:, 0:T], bf_[:, 0:T], AF.Tanh, scale=0.5, r=[bf_], w=[F_["th"]])
        aff(F_["fS"][:, 0:T], F_["th"][:, 0:T], hs1[:, j:j + 1], hs2[:, j:j + 1], r=[F_["th"], hs1, hs2], w=[F_["fS"]])
        aff(F_["kS"][:, 0:T], F_["th"][:, 0:T], hs1n[:, j:j + 1], hs2c[:, j:j + 1], r=[F_["th"], hs1n, hs2c], w=[F_["kS"]])
        cumprod(F_["Pc"], F_["fS"], T, C, NCH)
        P.add("dve", lambda e, T=T: e.reciprocal(out=F_["rP"][:, 0:T], in_=F_["Pc"][:, 0:T]), r=[F_["Pc"]], w=[F_["rP"]])
        bq = proj(wb, 0, 128, ti, c0, T)
        tt(B_["qt"][:, 0:T], bq[:, 0:T], F_["Pc"][:, 0:T], ALU.mult, r=[bq, F_["Pc"]], w=[B_["qt"]])
        tt(F_["t1"][:, 0:T], F_["kS"][:, 0:T], F_["rP"][:, 0:T], ALU.mult, eng="pool", r=[F_["kS"], F_["rP"]], w=[F_["t1"]])
        cp(B_["kt"][:, 0:T], F_["t1"][:, 0:T], eng="act", r=[F_["t1"]], w=[B_["kt"]])
        tt(chunk3(B_["kh"][:, 0:T], C), chunk3(F_["t1"][:, 0:T], C), lastbc(F_["Pc"], T, C, NCH), ALU.mult, r=[F_["t1"], F_["Pc"]], w=[B_["kh"]])
        bi = proj(wb, 256, 384, ti, c0, T)
        cp(B_["vT"][:, 0:T], bi[:, 0:T], eng="act", r=[bi], w=[B_["vT"]])
        bog = proj(wb, 384, 512, ti, c0, T)
        sigm(F_["t2"][:, 0:T], bog[:, 0:T], r=[bog], w=[F_["t2"]])
        tt(F_["t2"][:, 0:T], bog[:, 0:T], F_["t2"][:, 0:T], ALU.mult, r=[bog, F_["t2"]], w=[F_["t2"]])
        bga = proj(wb, 512, 640, ti, c0, T)
        sigm(F_["Ga"][:, 0:T], bga[:, 0:T], r=[bga], w=[F_["Ga"]])
        tt(F_["Ga"][:, 0:T], F_["Ga"][:, 0:T], F_["t2"][:, 0:T], ALU.mult, eng="pool", r=[F_["Ga"], F_["t2"]], w=[F_["Ga"]])
        to_tm(tm["Vtm"], B_["vT"], T, C, NCH); to_tm(tm["khtm"], B_["kh"], T, C, NCH)
        b_ = bank()
        for c in range(NCH):
            mm(b_[0:C, c * C:(c + 1) * C], B_["kt"][:, c * C:(c + 1) * C], B_["qt"][:, c * C:(c + 1) * C], True, True, w=[b_])
        tt(attnT[0:C, 0:NCH, 0:C], b_[0:C, 0:NCH * C].rearrange("p (c t) -> p c t", t=C),
           M_["incl"].unsqueeze(1).broadcast_to([C, NCH, C]), ALU.mult, r=[b_, M_["incl"]], w=[attnT])
        bo = banks[0]
        for c in range(NCH):
            csl = slice(c * C, (c + 1) * C)
            sbf = S0bf[:, c, :] if is_s else Shbf
            sin = S0all[:, c, :] if is_s else Sh
            sout = Souts[:, c, :] if is_s else Sh
            oo = bo[0:C, c * 128:(c + 1) * 128]
            mm(oo, attnT[0:C, c, 0:C], tm["Vtm"][0:C, c, :], True, False, r=[attnT, tm["Vtm"]], w=[bo])
            mm(oo, B_["qt"][:, csl], sbf, False, True, r=[B_["qt"], S0bf if is_s else Shbf], w=[bo])
            b2 = bank()
            mm(b2[:, 0:128], tm["khtm"][0:C, c, :], tm["Vtm"][0:C, c, :], True, True, r=[tm["khtm"], tm["Vtm"]], w=[b2])
            stt(sout, sin, F_["Pc"][:, (c + 1) * C - 1:(c + 1) * C], b2[:, 0:128], ALU.mult, ALU.add,
                r=[S0all if is_s else Sh, F_["Pc"], b2], w=[Souts if is_s else Sh])
            if not is_s:
                cp(Shbf, Sh, eng="act")
        cp(Osb[0:C, 0:NCH, :], bo.rearrange("p (c f) -> p c f", f=128)[0:C], eng="act", r=[bo], w=[Osb])
        if is_s:
            dma("sp", hs[s0:s0 + 4, j].rearrange("s k v -> k s v"), Souts, r=[Souts], w=[hs])
        elif last:
            dma("sp", hp[j], Sh, r=[Sh], w=[hp])
        tt(Osq[0:C, 0:NCH, :], Osb[0:C, 0:NCH, :], Osb[0:C, 0:NCH, :], ALU.mult, eng="pool", r=[Osb], w=[Osq])
        P.add("dve", lambda e, C=C, NCH=NCH: e.reduce_sum(out=st_[0:C, 0:NCH], in_=Osq[0:C, 0:NCH, :], axis=AX.X), r=[Osq], w=[st_])
        rsqrt_(st_[0:C, 0:NCH], st_[0:C, 0:NCH], 1.0 / 128, RMS_EPS, stA[0:C, 0:NCH], stB[0:C, 0:NCH])
        tt(Osq[0:C, 0:NCH, :], Osb[0:C, 0:NCH, :], st_[0:C, 0:NCH].unsqueeze(2).broadcast_to([C, NCH, 128]), ALU.mult, r=[Osb, st_], w=[Osq])
        tt(tm["onb"][0:C, 0:NCH, :], Osq[0:C, 0:NCH, :], nwbc[0:C, :].unsqueeze(1).broadcast_to([C, NCH, 128]), ALU.mult, r=[Osq, nwbc], w=[tm["onb"]])
        b_ = bank(); bb = b_.bitcast(BF16)
        for c in range(NCH):
            tr(bb[:, c * C:(c + 1) * C], tm["onb"][0:C, c, :], identB[0:C, 0:C], r=[tm["onb"], identB], w=[b_])
        if gi >= 2:
            P.add("need", ("oTr", gi - 2))
        tt(oT[:, 0:T], bb[:, 0:T], F_["Ga"][:, 0:T], ALU.mult, r=[b_, F_["Ga"]], w=[oT])
        P.add("mark", ("oTw", gi))

    def stream(kind):
        gi = 0
        for j in range(NBLK):
            for k in range(8):
                rows_ = w_in[k * 128:(k + 1) * 128, :]
                if kind == "H":
                    dma("pool", wbH[:, k, 0:512].rearrange("p (s c) -> p s c", c=128),
                        rows_[:, 0:4096].rearrange("p (s c) -> p s c", c=1024)[:, :, j * 128:(j + 1) * 128], r=[w_in], w=[wbH])
                    dma("pool", wbH[:, k, 512:640], rows_[:, 7456 + j * 128:7456 + (j + 1) * 128], r=[w_in], w=[wbH])
                else:
                    dma("pool", wbR[:, k, 0:384].rearrange("p (s c) -> p s c", c=128),
                        rows_[:, 4096:7168].rearrange("p (s c) -> p s c", c=1024)[:, :, j * 128:(j + 1) * 128], r=[w_in], w=[wbR])
                    dma("pool", wbR[:, k, 384:512], rows_[:, 8480 + j * 128:8480 + (j + 1) * 128], r=[w_in], w=[wbR])
            if kind == "H":
                memset(Sh, 0.0, eng="dve"); memset(Shbf, 0.0, eng="dve")
            else:
                memset(Sr, 0.0, eng="dve"); memset(Srbf, 0.0, eng="dve"); memset(prevs[:, 3:6], 0.0, eng="dve", w=[prevs])
            for ti, (c0, T, C, NCH, is_s) in enumerate(tiles):
                if debug is not None and 'tsel' in debug and ti not in debug['tsel']:
                    continue
                last = (c0 + T == SEQ)
                if kind == "H":
                    hgrn_tile(j, ti, c0, T, C, NCH, is_s, last, wbH, oTs[gi % 2], gi)
                else:
                    rwkv_block(j, ti, c0, T, C, NCH, is_s, last, wbR, oTs[gi % 2], gi)
                gi += 1

    opsH = []; opsR = []
    P.sink = opsH; cur[0] = "H"; stream("H")
    P.sink = opsR; cur[0] = "R"; stream("R")
    P.sink = None; cur[0] = None
    if debug is not None and debug.get("only") == "R":
        opsH = [x for x in opsH if x[0] == "mark"]
    if debug is not None and debug.get("only") == "H":
        opsR = [x for x in opsR if x[0] == "mark"]
    _n0 = len(P.ops)
    P.merge(opsH, opsR, lead=(150 if debug is None else debug.get('lead', 150)))
    if debug is not None and debug.get("xdeps"):
        from collections import Counter
        cx = Counter()
        for i in range(_n0, len(P.ops)):
            o = P.ops[i]
            for d in o["deps"]:
                od = P.ops[d]
                if d >= _n0 and od["stream"] != o["stream"]:
                    shared = (set(o["rk"]) | set(o["wk"])) & (set(od["rk"]) | set(od["wk"]))
                    cx[(o["stream"], o["eng"], od["eng"], str(sorted(map(str, shared)))[:80])] += 1
        for k, v in cx.most_common(30):
            print("XDEP", v, k)
    dump("mT", mT.rearrange("p a b -> p (a b)"), [128, 8 * NT])
    for p4 in range(0, 27, 4):
        b_ = bank(); n4 = min(4, 27 - p4); wdt = min(n4 * 128, RC - p4 * 128)
        for q in range(n4):
            tr(b_[0:17, q * 128:(q + 1) * 128], lastc[:, p4 + q, :], identF, r=[lastc, identF], w=[b_])
        cp(srow[:, 0:n4 * 128], b_[0:17, 0:n4 * 128], r=[b_], w=[srow])
        dma("sp", shp[:, p4 * 128:p4 * 128 + wdt], srow[0:1, 0:wdt], r=[srow], w=[shp])
        dma("sp", shs[:, p4 * 128:p4 * 128 + wdt], srow[1:17, 0:wdt], r=[srow], w=[shs])
    if debug is not None and debug.get("_stop") == 2:
        n = P.emit()
        return nc, dbg, n
    print('SBUF phase2 total', SBTOT[0], 'S2', S2.tot)
    S2.close()
    P.barrier(dummy)
    S1 = Scope()
    xT = S1.sb("xT", [128, 8, NT]); h2T = S1.sb("h2T", [128, 8, NT], BF16)
    nbuf = dict(sq=S1.sb("sq", [128, 8, 512], BF16), rstd=S1.sb("rstd", [128, 512]), tmp=S1.sb("ntmp", [128, 8, 512]), r1=S1.sb("r1", [128, 512]), r2=S1.sb("r2", [128, 512]),
                xtm=[S1.sb("xtm", [128, D]) for _ in range(2)], i=0)
    wo = S1.sb("wo", [128, 8, 128], BF16); wu = S1.sb("wu", [128, 8, 256], BF16); wd = S1.sb("wd", [128, 2, D], BF16)
    aT = S1.sb("aT", [128, 2, 512], BF16); rT = S1.sb("rT", [128, 512], BF16); t64 = S1.sb("t64", [128, 64])
    tiles34 = [(i * 512, 512, False) for i in range(4)] + [(SEQ, 64, True)]
    for ti, (c0, T, is_s) in enumerate(tiles34):
        load_xT(xT[:, :, c0:c0 + T], c0, T, ti)

    def resid_add(f, c0, T, is_s, ps, gtp, gts):
        if not is_s:
            stt(xT[:, f, c0:c0 + T], ps[:, 0:T], gtp[:, f:f + 1], xT[:, f, c0:c0 + T], ALU.mult, ALU.add, r=[ps, gtp, xT], w=[xT])
        else:
            tt(t64[:, 0:T], ps[:, 0:T], gts[:, f, :], ALU.mult, r=[ps, gts], w=[t64])
            tt(xT[:, f, c0:c0 + T], xT[:, f, c0:c0 + T], t64[:, 0:T], ALU.add, r=[xT, t64], w=[xT])

    for f in range(8):
        for k in range(8):
            dma("pool", wo[:, k, :], w_out[k * 128:(k + 1) * 128, f * 128:(f + 1) * 128], r=[w_out], w=[wo])
        for ti, (c0, T, is_s) in enumerate(tiles34):
            b_ = bank()
            for k in range(8):
                mm(b_[:, 0:T], wo[:, k, :], mT[:, k, c0:c0 + T], k == 0, k == 7, r=[wo, mT], w=[b_])
            resid_add(f, c0, T, is_s, b_, gt1p, gt1s)
    for ti, (c0, T, is_s) in enumerate(tiles34):
        norm_to(h2T[:, :, c0:c0 + T], xT[:, :, c0:c0 + T], T, g2p, sh2p, g2s, sh2s, is_s, 0)
    for g in range(16):
        for k in range(8):
            dma("pool", wu[:, k, :], w_up[k * 128:(k + 1) * 128, g * 256:(g + 1) * 256], r=[w_up], w=[wu])
        for q in range(2):
            dma("pool", wd[:, q, :], w_down[g * 256 + q * 128:g * 256 + (q + 1) * 128, :], r=[w_down], w=[wd])
        for ti, (c0, T, is_s) in enumerate(tiles34):
            for q in range(2):
                b_ = bank()
                for k in range(8):
                    mm(b_[:, 0:T], wu[:, k, q * 128:(q + 1) * 128], h2T[:, k, c0:c0 + T], k == 0, k == 7, r=[wu, (h2T, 0)], w=[b_])
                act(rT[:, 0:T], b_[:, 0:T], AF.Relu, r=[b_], w=[rT])
                tt(aT[:, q, 0:T], rT[:, 0:T], rT[:, 0:T], ALU.mult, r=[rT], w=[aT])
            for f in range(8):
                b_ = bank()
                for q in range(2):
                    mm(b_[:, 0:T], wd[:, q, f * 128:(f + 1) * 128], aT[:, q, 0:T], q == 0, q == 1, r=[wd, aT], w=[b_])
                resid_add(f, c0, T, is_s, b_, gt2p, gt2s)
    fnp = pv[:, rows["fn"]:rows["fn"] + 8]
    for ti, (c0, T, is_s) in enumerate(tiles34):
        sq = nbuf["sq"]; rstd = nbuf["rstd"]; tmp = nbuf["tmp"]
        xv_ = xT[:, :, c0:c0 + T]
        tt(sq[:, :, 0:T], xv_, xv_, ALU.mult, r=[xT], w=[sq])
        b_ = bank()
        for fc in range(8):
            mm(b_[:, 0:T], onesB, sq[:, fc, 0:T], fc == 0, fc == 7, w=[b_])
        ts(rstd[:, 0:T], b_[:, 0:T], 1.0 / D, RMS_EPS, ALU.mult, ALU.add, r=[b_], w=[rstd])
        r1 = nbuf["r1"]; r2 = nbuf["r2"]
        cp(r1[:, 0:T], rstd[:, 0:T], eng="act", r=[rstd], w=[r1])
        rsqrt_dve(rstd[:, 0:T], r1[:, 0:T], r2[:, 0:T], [rstd, r1, r2])
        tt(tmp[:, :, 0:T], xv_, rstd[:, 0:T].unsqueeze(1).broadcast_to([128, 8, T]), ALU.mult, r=[xT, rstd], w=[tmp])
        tt(tmp[:, :, 0:T], tmp[:, :, 0:T], fnp.unsqueeze(2).broadcast_to([128, 8, T]), ALU.mult, r=[tmp, pv], w=[tmp])
        for sub in range(0, T, 128):
            n = min(128, T - sub)
            ytm = nbuf["xtm"][nbuf["i"] % 2]; nbuf["i"] += 1
            for half in range(2):
                b_ = bank()
                for f4 in range(4):
                    fc = half * 4 + f4
                    tr(b_[0:n, f4 * 128:(f4 + 1) * 128], tmp[:, fc, sub:sub + n], identF, r=[tmp, identF], w=[b_])
                cp(ytm[0:n, half * 512:(half + 1) * 512], b_[0:n, :], eng="act", r=[b_], w=[ytm])
            dst = yp[c0 + sub:c0 + sub + n, :] if not is_s else ys[sub:sub + n, :]
            dma("sp", dst, ytm[0:n, :], r=[ytm], w=[yp if not is_s else ys])
    n = P.emit()
    print('SBUF bytes/partition: G', G.tot, 'max', SBMAX[0])
    S1.close(); G.close()
    return nc, dbg, n


_CACHE = {}


def _core_inputs(inp, c):
    f = lambda a: np.ascontiguousarray(a, dtype=np.float32)
    m = {"xp": inp["x_prompt"][c], "xs": inp["x_sample"][16 * c:16 * c + 16].reshape(64, D),
         "sh": inp["state_hgrn"][0, 16 * c:16 * c + 16], "sr": inp["state_rwkv"][0, 16 * c:16 * c + 16],
         "ss": inp["state_shift"][0, 16 * c:16 * c + 16], "cp": inp["c_prompt"][c:c + 1], "cs": inp["c_sample"][16 * c:16 * c + 16]}
    for k in ("norm1_w", "norm2_w", "rwkv_w0", "rwkv_a0", "rwkv_k_k", "rwkv_k_a", "rwkv_lnx_w", "rwkv_lnx_b", "rwkv_r_k", "final_norm_w"):
        m[k] = inp[k].reshape(8, 128)
    m["ada_w"] = inp["ada_w"][0]; m["ada_b"] = inp["ada_b"].reshape(48, 128); m["w_in"] = inp["w_in"][0]
    m["lb_logits"] = inp["lb_logits"].reshape(16, 128); m["hgrn_norm_w"] = inp["hgrn_norm_w"].reshape(1, 128)
    m["rwkv_mu"] = inp["rwkv_mu"].reshape(1, RC); m["rwkv_w2"] = inp["rwkv_w2"][0]; m["rwkv_a2"] = inp["rwkv_a2"][0]
    m["rwkv_g2"] = inp["rwkv_g2"][0]; m["w_out"] = inp["w_out"][0]; m["w_up"] = inp["w_up"][0]; m["w_down"] = inp["w_down"][0]
    return {k: f(v) for k, v in m.items()}


def kernel(**inputs):
    inp = {k: np.asarray(v) for k, v in inputs.items()}
    if "nc" not in _CACHE:
        _CACHE["nc"] = build()[0]
    nc = _CACHE["nc"]
    in_maps = [_core_inputs(inp, c) for c in range(8)]
    res = run_bass_kernel_spmd(nc, in_maps, core_ids=list(range(8)))
    R = res.results
    g = lambda n: [np.asarray(R[c][n], dtype=np.float32) for c in range(8)]
    y_prompt = np.stack(g("yp"), 0)
    y_sample = np.concatenate([a.reshape(16, 4, D) for a in g("ys")], 0)
    hgrn_p = np.stack(g("hp"), 0)[None]
    rwkv_p = np.stack(g("rp"), 0)[None]
    shift_p = np.concatenate(g("shp"), 0)[None]
    hgrn_s = np.concatenate(g("hs"), 0)[None]
    rwkv_s = np.concatenate(g("rs"), 0)[None]
    shift_s = np.concatenate(g("shs"), 0)[None]
    return (y_prompt, y_sample, hgrn_p, rwkv_p, shift_p, hgrn_s, rwkv_s, shift_s)
```

```python
import contextlib
import numpy as np
import concourse.bass as bass
import concourse.mybir as mybir
from concourse.bass_utils import run_bass_kernel_spmd

F32 = mybir.dt.float32
BF16 = mybir.dt.bfloat16
AF = mybir.ActivationFunctionType
ALU = mybir.AluOpType
AX = mybir.AxisListType

D = 1024
SEQ = 2048
NSEQ_S = 16
TS = 4
NT = SEQ + NSEQ_S * TS
INC = 9504
RC = 3360
NDMASEM = 12
RMS_EPS = 1e-6
LNX_EPS = 64e-5
C1 = -0.5 * float(np.exp(-0.5))


class Prog:
    def __init__(self, nc):
        self.nc = nc
        self.ops = []
        self.last_w = {}
        self.readers = {}
        self.bar = False
        self.sink = None

    def merge(self, A, B, lead=0):
        done = set(); ia = ib = 0
        na = max(1, sum(1 for x in A if x[0] not in ("mark", "need"))); nb = max(1, sum(1 for x in B if x[0] not in ("mark", "need")))
        ca = cb = 0
        def step(L, i):
            x = L[i]
            if x[0] == "mark":
                done.add(x[1]); return True
            if x[0] == "need":
                return x[1] in done
            self.tag = "A" if L is A else "B"
            self.add(*x); self.tag = None; return True
        while ia < len(A) or ib < len(B):
            pickA = ib >= len(B) or (ia < len(A) and (ca - lead) * nb <= cb * na)
            for _ in range(2):
                if pickA and ia < len(A):
                    isop = A[ia][0] not in ("mark", "need")
                    if step(A, ia):
                        ia += 1; ca += isop; break
                elif (not pickA) and ib < len(B):
                    isop = B[ib][0] not in ("mark", "need")
                    if step(B, ib):
                        ib += 1; cb += isop; break
                pickA = not pickA
            else:
                raise RuntimeError("merge deadlock")

    @staticmethod
    def key(x):
        if isinstance(x, tuple):
            if isinstance(x[0], str):
                return x
            return (x[0].tensor.name, x[1])
        if isinstance(x, str):
            return x
        return x.tensor.name

    def add(self, eng, fn, r=(), w=(), dma=False):
        if self.sink is not None:
            if eng in ("mark", "need"):
                self.sink.append((eng, fn))
            else:
                self.sink.append((eng, fn, list(r), list(w), dma))
            return None
        deps = set()
        rk = [self.key(x) for x in r]
        wk = [self.key(x) for x in w]
        if self.bar:
            rk.append("__bar")
        for k in rk:
            if k in self.last_w:
                deps.add(self.last_w[k])
        for k in wk:
            if k in self.last_w:
                deps.add(self.last_w[k])
            deps |= self.readers.get(k, set())
        idx = len(self.ops)
        self.ops.append(dict(eng=eng, fn=fn, deps=deps, dma=dma, users=0, stream=getattr(self, "tag", None), rk=rk, wk=wk))
        for k in rk:
            self.readers.setdefault(k, set()).add(idx)
        for k in wk:
            self.last_w[k] = idx
            self.readers[k] = set()
        return idx

    def barrier(self, dummy):
        allk = set(self.last_w.keys()) | set(self.readers.keys())
        allk.discard("__bar")
        self.bar = False
        self.add("dve", lambda e: e.memset(dummy, 0.0), r=(), w=list(allk) + ["__bar"])
        self.last_w = {"__bar": self.last_w["__bar"]}
        self.readers = {}
        self.bar = True

    def emit(self):
        nc = self.nc
        engs = {"pe": nc.tensor, "act": nc.scalar, "dve": nc.vector, "pool": nc.gpsimd, "sp": nc.sync}
        ops = self.ops

        def skip(d, o):
            return ops[d]["eng"] == "pe" and o["eng"] == "pe" and not ops[d]["dma"] and not o["dma"]

        for o in ops:
            for d in o["deps"]:
                if not skip(d, o):
                    ops[d]["users"] += 1
        with contextlib.ExitStack() as st:
            csem = {e: st.enter_context(nc.semaphore("c_" + e)) for e in engs}
            dsem = {e: [st.enter_context(nc.semaphore(f"d_{e}{i}")) for i in range(NDMASEM)]
                    for e in ("act", "pool", "sp")}
            ccnt = {e: 0 for e in engs}
            dcnt = {e: [0] * NDMASEM for e in dsem}
            drr = {e: 0 for e in dsem}
            waited = {e: {} for e in engs}
            sig = {}

            def wait(e, sem, val):
                k = id(sem)
                if val <= 0 or waited[e].get(k, 0) >= val:
                    return
                engs[e].wait_ge(sem, val)
                waited[e][k] = val

            for i, o in enumerate(ops):
                e = o["eng"]
                for d in sorted(o["deps"]):
                    if d not in sig or skip(d, o):
                        continue
                    s, v = sig[d]
                    wait(e, s, v)
                if o["dma"]:
                    j = drr[e]
                    drr[e] = (j + 1) % NDMASEM
                    s = dsem[e][j]
                    wait(e, s, dcnt[e][j])
                    ins = o["fn"](engs[e])
                    dcnt[e][j] += 16
                    ins.then_inc(s, 16)
                    sig[i] = (s, dcnt[e][j])
                else:
                    ins = o["fn"](engs[e])
                    if o["users"] > 0:
                        ccnt[e] += 1
                        ins.then_inc(csem[e], 1)
                        sig[i] = (csem[e], ccnt[e])
            for e in dsem:
                for j in range(NDMASEM):
                    if dcnt[e][j] > 0:
                        nc.sync.wait_ge(dsem[e][j], dcnt[e][j])
        return len(ops)


def build(debug=None):
    nc = bass.Bass("TRN2", target_bir_lowering=False)
    P = Prog(nc)
    din = lambda n, s: nc.dram_tensor(n, list(s), F32, kind="ExternalInput").ap()
    dout = lambda n, s: nc.dram_tensor(n, list(s), F32, kind="ExternalOutput").ap()
    xp = din("xp", [SEQ, D]); xs = din("xs", [64, D])
    sh_in = din("sh", [16, 8, 128, 128]); sr_in = din("sr", [16, 16, 64, 64]); ss_in = din("ss", [16, RC])
    cpr = din("cp", [1, D]); csm = din("cs", [16, D])
    norm1_w = din("norm1_w", [8, 128]); norm2_w = din("norm2_w", [8, 128])
    ada_w = din("ada_w", [D, 6 * D]); ada_b = din("ada_b", [48, 128])
    w_in = din("w_in", [D, INC]); lb_logits = din("lb_logits", [16, 128])
    hgrn_norm_w = din("hgrn_norm_w", [1, 128]); mu = din("rwkv_mu", [1, RC])
    w0 = din("rwkv_w0", [8, 128]); w2 = din("rwkv_w2", [64, D]); a0 = din("rwkv_a0", [8, 128])
    a2 = din("rwkv_a2", [64, D]); g2 = din("rwkv_g2", [160, D]); k_k = din("rwkv_k_k", [8, 128])
    k_a = din("rwkv_k_a", [8, 128]); r_k = din("rwkv_r_k", [8, 128]); lnx_w = din("rwkv_lnx_w", [8, 128])
    lnx_b = din("rwkv_lnx_b", [8, 128]); w_out = din("w_out", [D, D]); w_up = din("w_up", [D, 4 * D])
    w_down = din("w_down", [4 * D, D]); fnw = din("final_norm_w", [8, 128])
    yp = dout("yp", [SEQ, D]); ys = dout("ys", [64, D])
    hp = dout("hp", [8, 128, 128]); rp = dout("rp", [16, 64, 64]); shp = dout("shp", [1, RC])
    hs = dout("hs", [16, 8, 128, 128]); rs = dout("rs", [16, 16, 64, 64]); shs = dout("shs", [16, RC])
    dbg = {}

    cnt = [0]

    def uname(n):
        cnt[0] += 1
        return f"{n}_{cnt[0]}"

    SBTOT = [0]; SBMAX = [0]

    class Scope:
        def __init__(self):
            self.st = contextlib.ExitStack()

        def sb(self, name, shape, dt=F32):
            nb = int(np.prod(shape[1:])) * (4 if dt == F32 else 2)
            SBTOT[0] += nb; self.tot = getattr(self, "tot", 0) + nb
            SBMAX[0] = max(SBMAX[0], SBTOT[0])
            return self.st.enter_context(nc.sbuf_tensor(uname(name), list(shape), dt)).ap()

        def close(self):
            SBTOT[0] -= getattr(self, "tot", 0)
            self.st.close()

    G = Scope()
    banks = [nc.alloc_psum_tensor(f"bank{i}", [128, 512], F32).ap() for i in range(8)]
    bk = [0]

    pools = {None: list(range(8)), "H": [1, 2], "R": [3, 4, 5, 6, 7]}
    cur = [None]
    bkp = {"H": 0, "R": 0}

    def bank():
        if cur[0] is None:
            b = banks[bk[0] % 8]
            bk[0] += 1
            return b
        pl = pools[cur[0]]
        b = banks[pl[bkp[cur[0]] % len(pl)]]
        bkp[cur[0]] += 1
        return b

    def dump(name, ap, shape, keys=None):
        if debug is None or (name not in debug and "all" not in debug):
            return
        d = nc.dram_tensor("dbg_" + name, list(shape), F32, kind="ExternalOutput").ap()
        dbg[name] = d
        q = "pool" if ap.dtype != F32 else "sp"
        rk = [ap] + [(ap, t_) for t_ in range(16)] if keys is None else keys
        P.add(q, lambda e: e.dma_start(out=d, in_=ap), r=rk, w=[d], dma=True)

    def dma(q, out, in_, r=(), w=()):
        P.add(q, lambda e: e.dma_start(out=out, in_=in_), r=list(r) or [in_], w=list(w) or [out], dma=True)

    def act(out, in_, func, bias=0.0, scale=1.0, r=(), w=(), accum=None):
        rr = list(r) or [in_]
        if not isinstance(bias, float):
            rr.append(bias)
        if not isinstance(scale, float):
            rr.append(scale)
        ww = list(w) or [out]
        if accum is not None:
            ww.append(accum)
        if accum is not None:
            P.add("act", lambda e: e.activation(out=out, in_=in_, func=func, bias=bias, scale=scale, accum_out=accum), r=rr, w=ww)
        else:
            P.add("act", lambda e: e.activation(out=out, in_=in_, func=func, bias=bias, scale=scale), r=rr, w=ww)

    def tt(out, in0, in1, op, eng="dve", r=(), w=()):
        P.add(eng, lambda e: e.tensor_tensor(out=out, in0=in0, in1=in1, op=op), r=list(r) or [in0, in1], w=list(w) or [out])

    def ts(out, in0, s1, s2, op0, op1=None, eng="dve", r=(), w=()):
        rr = list(r) or [in0]
        for s in (s1, s2):
            if s is not None and not isinstance(s, float):
                rr.append(s)
        if op1 is None:
            P.add(eng, lambda e: e.tensor_scalar(out=out, in0=in0, scalar1=s1, scalar2=None, op0=op0), r=rr, w=list(w) or [out])
        else:
            P.add(eng, lambda e: e.tensor_scalar(out=out, in0=in0, scalar1=s1, scalar2=s2, op0=op0, op1=op1), r=rr, w=list(w) or [out])

    def aff(out, in_, scale, bias, r=(), w=()):
        act(out, in_, AF.Identity, bias=bias, scale=scale, r=r, w=w)

    def stt(out, in0, scalar, in1, op0, op1, r=(), w=()):
        rr = list(r) or [in0, in1]
        if not isinstance(scalar, float):
            rr.append(scalar)
        P.add("dve", lambda e: e.scalar_tensor_tensor(out=out, in0=in0, scalar=scalar, in1=in1, op0=op0, op1=op1), r=rr, w=list(w) or [out])

    def cp(out, in_, eng="dve", r=(), w=()):
        if eng == "act":
            act(out, in_, AF.Copy, r=r, w=w)
        else:
            P.add(eng, lambda e: e.tensor_copy(out=out, in_=in_), r=list(r) or [in_], w=list(w) or [out])

    def memset(ap, val, eng="pool", w=()):
        P.add(eng, lambda e: e.memset(ap, val), w=list(w) or [ap])

    def mm(out, lhsT, rhs, start, stop, r=(), w=()):
        P.add("pe", lambda e: e.matmul(out, lhsT, rhs, start=start, stop=stop), r=list(r) or [lhsT, rhs], w=list(w) or [out])

    def tr(out, in_, ident, r=(), w=()):
        P.add("pe", lambda e: e.transpose(out=out, in_=in_, identity=ident), r=list(r) or [in_, ident], w=list(w) or [out])

    I32 = mybir.dt.int32

    def rsqrt_dve(y, x, t1, keys):
        P.add("dve", lambda e: e.tensor_scalar(out=y.bitcast(I32), in0=x.bitcast(I32), scalar1=-0.5, scalar2=1597463007.0,
                                               op0=ALU.mult, op1=ALU.add), r=keys, w=keys)
        for _ in range(2):
            tt(t1, y, y, ALU.mult, r=keys, w=keys)
            stt(t1, t1, -0.5, x, ALU.mult, ALU.mult, r=keys, w=keys)
            stt(y, t1, 1.5, y, ALU.add, ALU.mult, r=keys, w=keys)

    def rsqrt_(out, in_, scale, eps, tA, tB):
        ts(tA, in_, scale, eps, ALU.mult, ALU.add)
        rsqrt_dve(out, tA, tB, [out, tA, tB])

    ones128 = G.sb("ones128", [128, 128]); identF = G.sb("identF", [128, 128]); identB = G.sb("identB", [128, 128], BF16)
    memset(ones128, 1.0)
    P.add("pool", lambda e: e.affine_select(out=identF, in_=ones128, pattern=[[1, 128]], compare_op=ALU.is_equal,
                                           fill=0.0, base=0, channel_multiplier=-1), r=[ones128], w=[identF])
    cp(identB, identF)
    onesB = G.sb("onesB", [128, 128], BF16)
    cp(onesB, ones128)
    blkones = G.sb("blkones", [128, 128], BF16)
    memset(blkones, 0.0)
    memset(blkones[0:64, 0:64], 1.0, w=[blkones]); memset(blkones[64:128, 64:128], 1.0, w=[blkones])
    blkneg = G.sb("blkneg", [128, 128])
    memset(blkneg, 0.0)
    memset(blkneg[0:64, 0:64], -1.0, w=[blkneg]); memset(blkneg[64:128, 64:128], -1.0, w=[blkneg])
    blkpos = G.sb("blkpos", [128, 128])
    memset(blkpos, 0.0)
    memset(blkpos[0:64, 0:64], 1.0, w=[blkpos]); memset(blkpos[64:128, 64:128], 1.0, w=[blkpos])
    zeros = G.sb("zeros", [128, 64]); memset(zeros, 0.0)
    dummy = G.sb("dummy", [128, 1])
    masks = {}
    for C in (64, 4):
        mi = G.sb(f"mi{C}", [C, C]); ms = G.sb(f"ms{C}", [C, C]); mls = G.sb(f"mls{C}", [C, C])
        P.add("pool", lambda e, mi=mi, C=C: e.affine_select(out=mi, in_=ones128[0:C, 0:C], pattern=[[1, C]], compare_op=ALU.is_ge,
                                                         fill=0.0, base=0, channel_multiplier=-1), r=[ones128], w=[mi])
        P.add("pool", lambda e, ms=ms, C=C: e.affine_select(out=ms, in_=ones128[0:C, 0:C], pattern=[[1, C]], compare_op=ALU.is_gt,
                                                         fill=0.0, base=0, channel_multiplier=-1), r=[ones128], w=[ms])
        P.add("pool", lambda e, mls=mls, C=C: e.affine_select(out=mls, in_=ones128[0:C, 0:C], pattern=[[-1, C]], compare_op=ALU.is_gt,
                                                           fill=0.0, base=0, channel_multiplier=1), r=[ones128], w=[mls])
        msn = G.sb(f"msn{C}", [C, C]); mlsn = G.sb(f"mlsn{C}", [C, C])
        ts(msn, ms, -1.0, None, ALU.mult); ts(mlsn, mls, -1.0, None, ALU.mult)
        masks[C] = dict(incl=mi, strict=ms, nstrict=msn, nlstrict=mlsn)

    stgA = G.sb("stgA", [128, 128]); stgB = G.sb("stgB", [48, 128])
    memset(stgA, 0.0)
    rows = {}
    r0 = 0
    for nm, src, n in (("n1", norm1_w, 8), ("n2", norm2_w, 8), ("lb", lb_logits, 16), ("w0", w0, 8), ("a0", a0, 8),
                       ("kk", k_k, 8), ("ka", k_a, 8), ("rk", r_k, 8), ("lw", lnx_w, 8), ("lbb", lnx_b, 8), ("fn", fnw, 8)):
        dma("sp", stgA[r0:r0 + n, :], src, w=[stgA]); rows[nm] = r0; r0 += n
    dma("sp", stgA[r0:r0 + 26, :], mu[:, 0:3328].rearrange("o (n p) -> (o n) p", p=128), w=[stgA]); rows["mu"] = r0; r0 += 26
    dma("sp", stgA[r0:r0 + 1, 0:32], mu[:, 3328:3360], w=[stgA]); rows["mu2"] = r0; r0 += 1
    assert r0 <= 128
    dma("sp", stgB, ada_b)
    pv = G.sb("pv", [128, 128]); pvB = G.sb("pvB", [128, 48])
    b_ = bank(); tr(b_[:, 0:128], stgA, identF); cp(pv, b_[:, 0:128])
    b_ = bank(); tr(b_[:, 0:48], stgB, identF[0:48, 0:48]); cp(pvB, b_[:, 0:48])
    col = lambda nm, i=0: pv[:, rows[nm] + i: rows[nm] + i + 1]
    hs1 = G.sb("hs1", [128, 8]); hs2 = G.sb("hs2", [128, 8]); hs1n = G.sb("hs1n", [128, 8]); hs2c = G.sb("hs2c", [128, 8])
    tt(hs1, pv[:, rows["lb"]:rows["lb"] + 8], pv[:, rows["lb"] + 8:rows["lb"] + 16], ALU.subtract)
    act(hs2, hs1, AF.Tanh, scale=0.5)
    ts(hs1, hs2, -0.25, 0.25, ALU.mult, ALU.add)
    ts(hs2, hs2, 0.25, 0.75, ALU.mult, ALU.add)
    ts(hs1n, hs1, -1.0, None, ALU.mult)
    ts(hs2c, hs2, -1.0, 1.0, ALU.mult, ALU.add)
    hw0 = G.sb("hw0", [128, 8]); ha0 = G.sb("ha0", [128, 8]); omka = G.sb("omka", [128, 8])
    ts(hw0, pv[:, rows["w0"]:rows["w0"] + 8], 0.5, None, ALU.mult)
    ts(ha0, pv[:, rows["a0"]:rows["a0"] + 8], 0.5, None, ALU.mult)
    ts(omka, pv[:, rows["ka"]:rows["ka"] + 8], -1.0, 1.0, ALU.mult, ALU.add)
    nwbc = G.sb("nwbc", [64, 128]); dma("sp", nwbc, hgrn_norm_w.partition_broadcast(64))

    modT = G.sb("modT", [128, 48, 17])
    g1p = G.sb("g1p", [128, 8]); g2p = G.sb("g2p", [128, 8])
    sh1p = G.sb("sh1p", [128, 8]); sh2p = G.sb("sh2p", [128, 8]); gt1p = G.sb("gt1p", [128, 8]); gt2p = G.sb("gt2p", [128, 8])
    g1s = G.sb("g1s", [128, 8, 64]); g2s = G.sb("g2s", [128, 8, 64]); sh1s = G.sb("sh1s", [128, 8, 64]); sh2s = G.sb("sh2s", [128, 8, 64])
    gt1s = G.sb("gt1s", [128, 8, 64]); gt2s = G.sb("gt2s", [128, 8, 64])
    mT = G.sb("mT", [128, 8, NT], BF16)
    S0 = Scope()
    c17 = S0.sb("c17", [17, D]); s17 = S0.sb("s17", [17, D]); s17b = S0.sb("s17b", [17, D], BF16); scT = S0.sb("scT", [128, 8, 32], BF16)
    dma("sp", c17[0:1, :], cpr, w=[c17]); dma("sp", c17[1:17, :], csm, w=[c17])
    act(s17, c17, AF.Tanh, scale=0.5)
    ts(s17, s17, 0.5, 0.5, ALU.mult, ALU.add)
    tt(s17b, s17, c17, ALU.mult)
    b_ = bank(); bb = b_.bitcast(BF16)
    for k in range(8):
        tr(bb[:, k * 32:k * 32 + 17], s17b[:, k * 128:(k + 1) * 128], identB[0:17, 0:17], w=[b_])
    cp(scT[:, :, 0:17], bb[:, 0:256].rearrange("p (k c) -> p k c", c=32)[:, :, 0:17], r=[b_])
    awb = [S0.sb("awb", [128, 8, 512], BF16) for _ in range(2)]
    for g in range(12):
        wt = awb[g % 2]
        dma("pool", wt, ada_w[:, g * 512:(g + 1) * 512].rearrange("(k p) c -> p k c", p=128), r=[ada_w], w=[wt])
        b_ = bank()
        for m in range(4):
            for k in range(8):
                mm(b_[:, m * 32:m * 32 + 17], wt[:, k, m * 128:(m + 1) * 128], scT[:, k, 0:17], k == 0, k == 7, w=[b_])
        for m in range(4):
            ts(modT[:, g * 4 + m, :], b_[:, m * 32:m * 32 + 17], pvB[:, g * 4 + m:g * 4 + m + 1], None, ALU.add, r=[b_, pvB], w=[modT])
    def modp(part):
        return modT[:, part * 8:(part + 1) * 8, 0]
    def mods(part):
        return modT[:, part * 8:(part + 1) * 8, 1:17].unsqueeze(3).broadcast_to([128, 8, 16, 4])
    stt(g1p, modp(1), 1.0, pv[:, rows["n1"]:rows["n1"] + 8], ALU.add, ALU.mult, r=[modT, pv])
    stt(g2p, modp(4), 1.0, pv[:, rows["n2"]:rows["n2"] + 8], ALU.add, ALU.mult, r=[modT, pv])
    for dst, part in ((sh1p, 0), (gt1p, 2), (sh2p, 3), (gt2p, 5)):
        cp(dst, modp(part), r=[modT])
    def mods3(part):
        return modT[:, part * 8:(part + 1) * 8, 1:17]
    for t4 in range(4):
        for dst, part in ((sh1s, 0), (gt1s, 2), (sh2s, 3), (gt2s, 5)):
            cp(dst[:, :, t4::4], mods3(part), r=[modT], w=[dst])
        for dst, part, nm in ((g1s, 1, "n1"), (g2s, 4, "n2")):
            stt(dst[:, :, t4::4], mods3(part), 1.0, pv[:, rows[nm]:rows[nm] + 8].unsqueeze(2).broadcast_to([128, 8, 16]),
                ALU.add, ALU.mult, r=[modT, pv], w=[dst])
    S0.close()
    P.barrier(dummy)

    tiles = [(i * 256, 256, 64, 4, False) for i in range(8)] + [(SEQ + 16 * i, 16, 4, 4, True) for i in range(4)]

    def norm_to(hT_dst, xT_t, T, gp, shp_, gs, shs_, is_s, ti):
        sq = nbuf["sq"]
        tt(sq[:, :, 0:T], xT_t[:, :, 0:T], xT_t[:, :, 0:T], ALU.mult, r=[xT_t], w=[sq])
        b_ = bank()
        for fc in range(8):
            mm(b_[:, 0:T], onesB, sq[:, fc, 0:T], fc == 0, fc == 7, w=[b_])
        rstd = nbuf["rstd"]
        ts(rstd[:, 0:T], b_[:, 0:T], 1.0 / D, RMS_EPS, ALU.mult, ALU.add, r=[b_], w=[rstd])
        r1 = nbuf["r1"]; r2 = nbuf["r2"]
        cp(r1[:, 0:T], rstd[:, 0:T], eng="act", r=[rstd], w=[r1])
        rsqrt_dve(rstd[:, 0:T], r1[:, 0:T], r2[:, 0:T], [rstd, r1, r2])
        tmp = nbuf["tmp"]
        tt(tmp[:, :, 0:T], xT_t[:, :, 0:T], rstd[:, 0:T].unsqueeze(1).broadcast_to([128, 8, T]), ALU.mult, r=[xT_t, rstd], w=[tmp])
        if not is_s:
            for fc in range(8):
                ts(hT_dst[:, fc, :], tmp[:, fc, 0:T], gp[:, fc:fc + 1], shp_[:, fc:fc + 1], ALU.mult, ALU.add,
                   r=[tmp, gp, shp_], w=[(hT_dst, ti)])
        else:
            tt(tmp[:, :, 0:T], tmp[:, :, 0:T], gs, ALU.mult, r=[tmp, gs], w=[tmp])
            tt(hT_dst, tmp[:, :, 0:T], shs_, ALU.add, r=[tmp, shs_], w=[(hT_dst, ti)])

    def load_xT(xT_t, c0, T, ti):
        for sub in range(0, T, 128):
            n = min(128, T - sub)
            xtm = nbuf["xtm"][nbuf["i"] % 2]; nbuf["i"] += 1
            src = xp[c0 + sub:c0 + sub + n, :] if c0 < SEQ else xs[c0 - SEQ + sub:c0 - SEQ + sub + n, :]
            dma("sp", xtm[0:n, :], src, w=[xtm])
            for half in range(2):
                b_ = bank()
                for f4 in range(4):
                    fc = half * 4 + f4
                    tr(b_[:, f4 * 128:f4 * 128 + n], xtm[0:n, fc * 128:(fc + 1) * 128], identF[0:n, 0:n], w=[b_])
                cp(xT_t[:, half * 4:half * 4 + 4, sub:sub + n], b_.rearrange("p (f t) -> p f t", t=128)[:, :, 0:n], eng="act", r=[b_], w=[xT_t])

    S2 = Scope()
    hT = S2.sb("hT", [128, 8, NT], BF16)
    S1 = Scope()
    nbuf = dict(sq=S1.sb("sq", [128, 8, 512], BF16), rstd=S1.sb("rstd", [128, 512]), tmp=S1.sb("ntmp", [128, 8, 512]), r1=S1.sb("r1", [128, 512]), r2=S1.sb("r2", [128, 512]),
                xtm=[S1.sb("xtm", [128, D]) for _ in range(2)], i=0)
    xT_ts = [S1.sb("xT_t", [128, 8, 512]) for _ in range(2)]
    for ti, (c0, T, C, NCH, is_s) in enumerate(tiles):
        xT_t = xT_ts[ti % 2]
        load_xT(xT_t, c0, T, ti)
        so = max(c0 - SEQ, 0)
        norm_to(hT[:, :, c0:c0 + T], xT_t, T, g1p, sh1p, g1s[:, :, so:so + min(T, 16)], sh1s[:, :, so:so + min(T, 16)], is_s, ti)
    S1.close()
    P.barrier(dummy)

    W = lambda name, shape, dt=F32: S2.sb(name, shape, dt)
    wbH = W("wbH", [128, 8, 640], BF16); wbR = W("wbR", [128, 8, 512], BF16)
    wl = wbH[:, :, 0:288]
    w2b = W("w2b", [128, D], BF16)
    g2b = W("g2b", [128, D], BF16); g2c = W("g2c", [32, D], BF16)
    TW = W("TW", [128, NT], BF16); SG1 = W("SG1", [128, NT], BF16); SG2 = W("SG2", [32, NT], BF16)
    srow = W("srow", [17, 512]); ssT = W("ssT", [128, 27, 16]); lastc = W("lastc", [128, 27, 17])
    prevs = W("prevs", [128, 8])
    rcS = W("rcS", [128, 256]); dsh = W("dsh", [128, 256])
    S0all = W("S0all", [128, 4, 128]); S0bf = W("S0bf", [128, 4, 128], BF16); Souts = S0all
    Sh = W("Sh", [128, 128]); Shbf = W("Shbf", [128, 128], BF16)
    R0in = W("R0in", [128, 4, 128]); R0blk = W("R0blk", [128, 4, 128]); R0bf = W("R0bf", [128, 4, 128], BF16); Routs = R0blk
    Sr = W("Sr", [128, 128]); Srbf = W("Srbf", [128, 128], BF16); Srt = W("Srt", [128, 128])
    f32t = {n: W(n, [128, 256]) for n in ("th", "fS", "kS", "Pc", "rP", "Ga", "t1", "t2", "Gmm", "kap", "kmod", "bS", "bon", "gS", "Gb",
                                          "xr", "xk", "xv", "Gm", "rG", "aS", "u1", "u2")}
    f32t["n1"] = f32t["bon"]; f32t["n2"] = f32t["u1"]
    oTs = [W("oT", [128, 256]) for _ in range(2)]
    bft = {n: W(n, [128, 256], BF16) for n in ("qt", "kt", "kh", "vT", "sqb", "kpt", "bt", "bh", "rt", "ktl", "khr", "vTr")}
    KR = W("KR", [128, 4, 128], BF16)
    tm = {n: W(n, [64, 4, 128], BF16) for n in ("Vtm", "khtm", "Ktm", "bhtm", "khrtm", "W1", "Un", "Ktp", "onb", "Vtr", "onr")}
    msk = {n: W(n, [64, 4, 128], BF16) for n in ("PbT", "PkT")}
    Abm = {n: W(n, [64, 4, 128], BF16) for n in ("Y", "Z", "Y2", "Z2", "TTa", "TTb")}
    attnT = W("attnT", [64, 4, 64], BF16)
    Osb = W("Osb", [64, 4, 128]); Osq = W("Osq", [64, 4, 128]); st_ = W("st_", [64, 16]); stA = W("stA", [64, 16]); stB = W("stB", [64, 16])
    Osb2 = W("Osb2", [64, 4, 128]); Osq2 = W("Osq2", [64, 4, 128]); st3 = W("st3", [64, 16]); st4 = W("st4", [64, 16]); stC = W("stC", [64, 16]); stD = W("stD", [64, 16])
    Mblk = W("Mblk", [128, 4, 128], BF16); Ncomb = W("Ncomb", [128, 4, 128]); RpT = W("RpT", [128, 256], BF16)

    dma("pool", w2b[0:64, :], w2, w=[w2b]); dma("pool", w2b[64:128, :], a2, w=[w2b])
    dma("pool", g2b, g2[0:128, :]); dma("pool", g2c, g2[128:160, :])
    dma("pool", wl, w_in[:, 7168:7456].rearrange("(k p) c -> p k c", p=128), r=[w_in], w=[wl])
    for p4 in range(0, 27, 4):
        n4 = min(4, 27 - p4); wdt = min(n4 * 128, RC - p4 * 128)
        memset(srow, 0.0, eng="dve")
        dma("sp", srow[0:16, 0:wdt], ss_in[:, p4 * 128:p4 * 128 + wdt], r=[ss_in], w=[srow])
        b_ = bank()
        for q in range(n4):
            tr(b_[:, q * 16:(q + 1) * 16], srow[0:16, q * 128:(q + 1) * 128], identF[0:16, 0:16], r=[srow, identF], w=[b_])
        cp(ssT[:, p4:p4 + n4, :], b_[:, 0:n4 * 16].rearrange("p (q s) -> p q s", s=16), r=[b_], w=[ssT])
    memset(prevs, 0.0); memset(lastc, 0.0)

    def proj(wt, c_lo, c_hi, ti, c0, T):
        b_ = bank(); n = c_hi - c_lo
        for k in range(8):
            mm(b_[0:n, 0:T], wt[:, k, c_lo:c_hi], hT[:, k, c0:c0 + T], k == 0, k == 7, r=[wt, (hT, ti)], w=[b_])
        return b_

    def shift_mix(dst, ps, n, T, is_s, mucol, pcol, part, last_tile, s0=0):
        cp(rcS[0:n, 0:T], ps[0:n, 0:T], eng="act", r=[ps], w=[rcS])
        if not is_s:
            tt(dsh[0:n, 1:T], rcS[0:n, 0:T - 1], rcS[0:n, 1:T], ALU.subtract, r=[rcS], w=[dsh])
            tt(dsh[0:n, 0:1], prevs[0:n, pcol:pcol + 1], rcS[0:n, 0:1], ALU.subtract, r=[rcS, prevs], w=[dsh])
            cp(prevs[0:n, pcol:pcol + 1], rcS[0:n, T - 1:T], r=[rcS, dsh], w=[prevs])
            if last_tile:
                cp(lastc[0:n, part, 0:1], rcS[0:n, T - 1:T], r=[rcS], w=[lastc])
        else:
            r3 = rcS[0:n, 0:T].rearrange("p (s t) -> p s t", t=4); d3 = dsh[0:n, 0:T].rearrange("p (s t) -> p s t", t=4)
            tt(d3[:, :, 1:4], r3[:, :, 0:3], r3[:, :, 1:4], ALU.subtract, r=[rcS], w=[dsh])
            tt(d3[:, :, 0], ssT[0:n, part, s0:s0 + T // 4], r3[:, :, 0], ALU.subtract, r=[rcS, ssT], w=[dsh])
            cp(lastc[0:n, part, 1 + s0:1 + s0 + T // 4], r3[:, :, 3], r=[rcS], w=[lastc])
        stt(dst[0:n, 0:T], dsh[0:n, 0:T], mucol[0:n, :], rcS[0:n, 0:T], ALU.mult, ALU.add, r=[dsh, rcS, pv], w=[dst])

    def sigm(dst, src, r=(), w=(), bias=0.0):
        act(dst, src, AF.Tanh, bias=bias, scale=0.5, r=r, w=w)
        aff(dst, dst, 0.5, 0.5)

    for ti, (c0, T, C, NCH, is_s) in enumerate(tiles):
        last = (c0 + T == SEQ)
        b_ = proj(wl, 0, 128, ti, c0, T)
        shift_mix(f32t["t1"], b_, 128, T, is_s, col("mu", 24), 0, 24, last, max(c0 - SEQ, 0) // 4)
        act(TW[0:64, c0:c0 + T], f32t["t1"][0:64, 0:T], AF.Tanh, r=[f32t["t1"]], w=[(TW, ti)])
        cp(TW[64:128, c0:c0 + T], f32t["t1"][64:128, 0:T], eng="act", r=[f32t["t1"]], w=[(TW, ti)])
        b_ = proj(wl, 128, 256, ti, c0, T)
        shift_mix(f32t["t1"], b_, 128, T, is_s, col("mu", 25), 1, 25, last, max(c0 - SEQ, 0) // 4)
        sigm(f32t["t2"][:, 0:T], f32t["t1"][:, 0:T], r=[f32t["t1"]], w=[f32t["t2"]])
        cp(SG1[:, c0:c0 + T], f32t["t2"][:, 0:T], r=[f32t["t2"]], w=[(SG1, ti)])
        b_ = proj(wl, 256, 288, ti, c0, T)
        shift_mix(f32t["t1"], b_, 32, T, is_s, col("mu2"), 2, 26, last, max(c0 - SEQ, 0) // 4)
        sigm(f32t["t2"][0:32, 0:T], f32t["t1"][0:32, 0:T], r=[f32t["t1"]], w=[f32t["t2"]])
        cp(SG2[:, c0:c0 + T], f32t["t2"][0:32, 0:T], r=[f32t["t2"]], w=[(SG2, ti)])

    if debug is not None and debug.get("_stop") == "lora":
        dump("TW", TW, [128, NT]); dump("SG1", SG1, [128, NT]); dump("SG2", SG2, [32, NT]); dump("lastc", lastc.rearrange("p a b -> p (a b)"), [128, 27 * 17])
        n = P.emit()
        return nc, dbg, n

    def chunk3(ap, C):
        return ap.rearrange("p (c t) -> p c t", t=C)

    def to_tm(dst, src_bf, T, C, NCH):
        for c4 in range(0, NCH, 8):
            b_ = bank(); bb = b_.bitcast(BF16)
            for c in range(c4, min(c4 + 8, NCH)):
                tr(bb[0:C, (c - c4) * 128:(c - c4 + 1) * 128], src_bf[:, c * C:(c + 1) * C], identB, r=[src_bf, identB], w=[b_])
            n8 = min(8, NCH - c4)
            cp(dst[0:C, c4:c4 + n8, :], bb[0:C, 0:n8 * 128].rearrange("p (c f) -> p c f", f=128), eng="act", r=[b_], w=[dst])

    def cumprod(dst, src, T, C, NCH):
        for c in range(NCH):
            sl = slice(c * C, (c + 1) * C)
            P.add("dve", lambda e, sl=sl: e.tensor_tensor_scan(out=dst[:, sl], data0=src[:, sl], data1=zeros[:, 0:C], initial=1.0,
                                                              op0=ALU.mult, op1=ALU.add), r=[src, zeros], w=[dst])

    def lastbc(ap, T, C, NCH):
        return chunk3(ap[:, 0:T], C)[:, :, C - 1:C].broadcast_to([128, NCH, C])


    def rwkv_block(j, ti, c0, T, C, NCH, is_s, last, wb, oT, gi):
        F_ = f32t; B_ = bft; M_ = masks[C]; jc = slice(j * 128, (j + 1) * 128)
        L = 6 if C == 64 else 2
        if is_s:
            s0 = (c0 - SEQ) // 4
            memset(R0in, 0.0)
            dma("sp", R0in[0:64, :, 0:64], sr_in[s0:s0 + 4, 2 * j].rearrange("s i j -> i s j"), r=[sr_in], w=[R0in])
            dma("sp", R0in[64:128, :, 64:128], sr_in[s0:s0 + 4, 2 * j + 1].rearrange("s i j -> i s j"), r=[sr_in], w=[R0in])
            b_ = bank()
            for q in range(4):
                tr(b_[:, q * 128:(q + 1) * 128], R0in[:, q, :], identF, w=[b_])
            cp(R0blk[:, 0:4, :], b_.rearrange("p (q f) -> p q f", f=128), r=[b_], w=[R0blk])
            cp(R0bf, R0blk, eng="pool")
        br = proj(wb, 0, 128, ti, c0, T); shift_mix(F_["xr"], br, 128, T, is_s, col("mu", j), 3, j, last, max(c0 - SEQ, 0) // 4)
        bk_ = proj(wb, 128, 256, ti, c0, T); shift_mix(F_["xk"], bk_, 128, T, is_s, col("mu", 8 + j), 4, 8 + j, last, max(c0 - SEQ, 0) // 4)
        bv = proj(wb, 256, 384, ti, c0, T); shift_mix(F_["xv"], bv, 128, T, is_s, col("mu", 16 + j), 5, 16 + j, last, max(c0 - SEQ, 0) // 4)
        bgb = proj(wb, 384, 512, ti, c0, T)
        sigm(F_["Gb"][:, 0:T], bgb[:, 0:T], r=[bgb], w=[F_["Gb"]])
        bw = bank(); mm(bw[:, 0:T], w2b[0:64, jc], TW[0:64, c0:c0 + T], True, True, r=[w2b, (TW, ti)], w=[bw])
        ba = bank(); mm(ba[:, 0:T], w2b[64:128, jc], TW[64:128, c0:c0 + T], True, True, r=[w2b, (TW, ti)], w=[ba])
        bg = bank(); mm(bg[:, 0:T], g2b[:, jc], SG1[:, c0:c0 + T], True, False, r=[g2b, (SG1, ti)], w=[bg])
        mm(bg[:, 0:T], g2c[:, jc], SG2[:, c0:c0 + T], False, True, r=[g2c, (SG2, ti)], w=[bg])
        X = lambda n: F_[n][:, 0:T]
        Bx = lambda n: B_[n][:, 0:T]
        act(X("u1"), bw[:, 0:T], AF.Tanh, bias=hw0[:, j:j + 1], scale=0.5, r=[bw], w=[F_["u1"]])
        act(X("u1"), X("u1"), AF.Exp, bias=C1, scale=C1)
        cumprod(F_["Gm"], F_["u1"], T, C, NCH)
        P.add("dve", lambda e: e.reciprocal(out=X("rG"), in_=X("Gm")), r=[F_["Gm"]], w=[F_["rG"]])
        cp(chunk3(X("Gmm"), C)[:, :, 1:C], chunk3(X("Gm"), C)[:, :, 0:C - 1], r=[F_["Gm"]], w=[F_["Gmm"]])
        memset(chunk3(X("Gmm"), C)[:, :, 0:1], 1.0, eng="dve", w=[F_["Gmm"]])
        sigm(X("aS"), ba[:, 0:T], r=[ba], w=[F_["aS"]], bias=ha0[:, j:j + 1])
        cp(X("gS"), bg[:, 0:T], eng="act", r=[bg], w=[F_["gS"]])
        tt(X("Gb"), X("Gb"), X("gS"), ALU.mult, eng="pool")
        aff(X("kap"), X("xk"), col("kk", j), 0.0, r=[F_["xk"], pv], w=[F_["kap"]])
        tt(Bx("sqb"), X("kap"), X("kap"), ALU.mult, eng="pool")
        bn = bank(); mm(bn[:, 0:T], blkones, Bx("sqb"), True, True, w=[bn])
        ts(X("n1"), bn[:, 0:T], 1.0, 1e-24, ALU.mult, ALU.add, r=[bn], w=[F_["n1"]])
        rsqrt_dve(X("u2"), X("n1"), X("n2"), [F_["u2"], F_["n1"], F_["n2"]])
        tt(X("kap"), X("kap"), X("u2"), ALU.mult)
        aff(X("u2"), X("aS"), col("ka", j), omka[:, j:j + 1], r=[F_["aS"], pv, omka], w=[F_["u2"]])
        tt(X("kmod"), X("xk"), X("u2"), ALU.mult, eng="pool")
        tt(X("bS"), X("aS"), X("kap"), ALU.mult)
        stt(Bx("sqb"), X("xr"), col("rk", j), X("kmod"), ALU.mult, ALU.mult, r=[F_["xr"], F_["kmod"], pv], w=[B_["sqb"]])
        bn2 = bank(); mm(bn2[:, 0:T], blkones, Bx("sqb"), True, True, w=[bn2])
        tt(X("bon"), bn2[:, 0:T], X("xv"), ALU.mult, r=[bn2, F_["xv"]], w=[F_["bon"]])
        tt(Bx("kpt"), X("kap"), X("Gmm"), ALU.mult)
        tt(Bx("rt"), X("xr"), X("Gm"), ALU.mult, eng="pool")
        cp(KR[:, 0:NCH, 0:C], chunk3(Bx("kpt"), C), eng="pool", r=[B_["kpt"]], w=[KR])
        cp(KR[:, 0:NCH, 64:64 + C], chunk3(Bx("rt"), C), eng="pool", r=[B_["rt"]], w=[KR])
        tt(X("u2"), X("kmod"), X("rG"), ALU.mult); cp(Bx("ktl"), X("u2"), eng="act")
        tt(chunk3(Bx("khr"), C), chunk3(X("u2"), C), lastbc(F_["Gm"], T, C, NCH), ALU.mult, r=[F_["u2"], F_["Gm"]], w=[B_["khr"]])
        tt(X("u2"), X("bS"), X("rG"), ALU.mult); cp(Bx("bt"), X("u2"), eng="act")
        tt(chunk3(Bx("bh"), C), chunk3(X("u2"), C), lastbc(F_["Gm"], T, C, NCH), ALU.mult, r=[F_["u2"], F_["Gm"]], w=[B_["bh"]])
        cp(Bx("vTr"), X("xv"), eng="act")
        to_tm(tm["Vtr"], B_["vTr"], T, C, NCH); to_tm(tm["Ktm"], B_["kpt"], T, C, NCH)
        to_tm(tm["bhtm"], B_["bh"], T, C, NCH); to_tm(tm["khrtm"], B_["khr"], T, C, NCH)
        if debug is not None and debug.get('rstop', 99) <= 1:
            return
        hv = lambda t_, c_lo, c_hi: t_[0:C, c_lo:c_hi, :].rearrange("p c (h t) -> p c h t", t=64)[:, :, :, 0:C]
        for c4 in range(0, NCH, 4):
            for h in range(2):
                hs_ = slice(64 * h, 64 * h + 64)
                b1 = bank(); b2 = bank(); b3 = bank()
                for q in range(4):
                    c = c4 + q; csl = slice(c * C, (c + 1) * C); o = q * 128
                    mm(b1[0:C, o:o + 128], B_["bt"][hs_, csl], KR[hs_, c, :], True, True, r=[B_["bt"], KR], w=[b1])
                    mm(b2[0:C, o:o + 128], B_["ktl"][hs_, csl], KR[hs_, c, :], True, True, r=[B_["ktl"], KR], w=[b2])
                    mm(b3[0:C, o:o + C], B_["kpt"][hs_, csl], B_["bt"][hs_, csl], True, True, r=[B_["kpt"], B_["bt"]], w=[b3])
                v4_ = lambda b: b[0:C, :].rearrange("p (c x t) -> p c x t", x=2, t=64)
                mk = lambda m_: m_.unsqueeze(1).broadcast_to([C, 4, C])
                dv = lambda t_: t_[0:C, c4:c4 + 4, h * 64:h * 64 + C]
                tt(dv(Abm["Z"]), v4_(b1)[:, :, 0, 0:C], mk(M_["nstrict"]), ALU.mult, r=[b1, M_["nstrict"]], w=[Abm["Z"]])
                tt(dv(msk["PbT"]), v4_(b1)[:, :, 1, 0:C], mk(M_["incl"]), ALU.mult, r=[b1, M_["incl"]], w=[msk["PbT"]])
                tt(dv(tm["W1"]), v4_(b2)[:, :, 0, 0:C], mk(M_["strict"]), ALU.mult, r=[b2, M_["strict"]], w=[tm["W1"]])
                tt(dv(msk["PkT"]), v4_(b2)[:, :, 1, 0:C], mk(M_["incl"]), ALU.mult, r=[b2, M_["incl"]], w=[msk["PkT"]])
                tt(dv(Abm["Y"]), v4_(b3)[:, :, 0, 0:C], mk(M_["nlstrict"]), ALU.mult, r=[b3, M_["nlstrict"]], w=[Abm["Y"]])
        if debug is not None and debug.get('rstop', 99) <= 2:
            return
        tt(hv(Abm["TTa"], 0, NCH), hv(Abm["Z"], 0, NCH), identF[0:C, 0:C].unsqueeze(1).unsqueeze(1).broadcast_to([C, NCH, 2, C]), ALU.add,
           r=[Abm["Z"], identF], w=[Abm["TTa"]])
        Yc, Zc, Yn, Zn, TTc, TTn = Abm["Y"], Abm["Z"], Abm["Y2"], Abm["Z2"], Abm["TTa"], Abm["TTb"]
        pr = [(c, h) for c in range(NCH) for h in range(2)]

        def batch_mm(dst, lhs, rhs, wid_l, wid_r, post):
            for p8 in range(0, len(pr), 8):
                b_ = bank(); grp = pr[p8:p8 + 8]
                for q, (c, h) in enumerate(grp):
                    mm(b_[0:C, q * 64:q * 64 + wid_r], lhs[0:C, c, h * 64:h * 64 + wid_l], rhs[0:C, c, h * 64:h * 64 + wid_r], True, True,
                       r=[lhs, rhs], w=[b_])
                cA = grp[0][0]; cB = grp[-1][0] + 1
                src = b_[0:C, 0:(cB - cA) * 128].rearrange("p (c h t) -> p c h t", h=2, t=64)[:, :, :, 0:wid_r]
                d = dst[0:C, cA:cB, :].rearrange("p c (h t) -> p c h t", t=64)[:, :, :, 0:wid_r]
                post(d, src, b_)

        cpy = lambda d, s_, b_: cp(d, s_, eng="act", r=[b_], w=[d])
        for n in range(1, L):
            batch_mm(Yn, Zc, Yc, C, C, cpy)
            if n < L - 1:
                batch_mm(Zn, Yc, Zc, C, C, cpy)
            addT = lambda d, s_, b_, TTc=TTc: tt(d, s_, TTc[0:C, 0:NCH, :].rearrange("p c (h t) -> p c h t", t=64)[:, :, :, 0:C][:, (d.offset - d.offset):, :, :] if False else s_, s_, ALU.add) if False else None
            for p8 in range(0, len(pr), 8):
                b_ = bank(); grp = pr[p8:p8 + 8]
                for q, (c, h) in enumerate(grp):
                    mm(b_[0:C, q * 64:q * 64 + C], Yn[0:C, c, h * 64:h * 64 + C], TTc[0:C, c, h * 64:h * 64 + C], True, True, r=[Yn, TTc], w=[b_])
                cA = grp[0][0]; cB = grp[-1][0] + 1
                src = b_[0:C, 0:(cB - cA) * 128].rearrange("p (c h t) -> p c h t", h=2, t=64)[:, :, :, 0:C]
                tt(hv(TTn, cA, cB), src, hv(TTc, cA, cB), ALU.add, r=[b_, TTc], w=[TTn])
            Yc, Yn = Yn, Yc; Zc, Zn = Zn, Zc; TTc, TTn = TTn, TTc
        if debug is not None and debug.get('rstop', 99) <= 3:
            return
        TTf = TTc
        AkT = Yn
        cp(AkT[0:C, 0:NCH, :], tm["W1"][0:C, 0:NCH, :], eng="pool", r=[tm["W1"]], w=[AkT])
        batch_mm(tm["W1"], AkT, tm["Vtr"], C, 64, cpy)
        neg = lambda d, s_, b_: ts(d, s_, -1.0, None, ALU.mult, r=[b_], w=[d])
        batch_mm(tm["Un"], TTf, tm["W1"], C, 64, neg)
        batch_mm(tm["Ktp"], TTf, tm["Ktm"], C, 64, cpy)
        if debug is not None and debug.get('rstop', 99) <= 4:
            return
        for c4 in range(0, NCH, 4):
            bM = bank(); bN = bank(); bR = bank()
            for q in range(4):
                c = c4 + q
                mm(bM[:, q * 128:(q + 1) * 128], tm["Ktp"][0:C, c, :], tm["bhtm"][0:C, c, :], True, True, r=[tm["Ktp"], tm["bhtm"]], w=[bM])
                mm(bN[:, q * 128:(q + 1) * 128], tm["khrtm"][0:C, c, :], tm["Vtr"][0:C, c, :], True, False, r=[tm["khrtm"], tm["Vtr"]], w=[bN])
                mm(bN[:, q * 128:(q + 1) * 128], tm["bhtm"][0:C, c, :], tm["Un"][0:C, c, :], False, True, r=[tm["bhtm"], tm["Un"]], w=[bN])
                mm(bR[:, q * 128:(q + 1) * 128], tm["Ktp"][0:C, c, :], msk["PbT"][0:C, c, :], True, True, r=[tm["Ktp"], msk["PbT"]], w=[bR])
            v3 = lambda b: b.rearrange("p (c f) -> p c f", f=128)
            tt(Mblk[:, c4:c4 + 4, :], v3(bM), blkneg.unsqueeze(1).broadcast_to([128, 4, 128]), ALU.mult, r=[bM, blkneg], w=[Mblk])
            tt(Ncomb[:, c4:c4 + 4, :], v3(bN), blkpos.unsqueeze(1).broadcast_to([128, 4, 128]), ALU.mult, r=[bN, blkpos], w=[Ncomb])
            for h in range(2):
                hs_ = slice(64 * h, 64 * h + 64)
                tt(chunk3(RpT[hs_, c4 * C:(c4 + 4) * C], C), chunk3(B_["rt"][hs_, c4 * C:(c4 + 4) * C], C),
                   v3(bR)[hs_, :, h * 64:h * 64 + C], ALU.subtract, r=[B_["rt"], bR], w=[RpT])
        if debug is not None and debug.get('rstop', 99) <= 5:
            return
        for c in range(NCH):
            csl = slice(c * C, (c + 1) * C)
            sbf = R0bf[:, c, :] if is_s else Srbf
            sin = R0blk[:, c, :] if is_s else Sr
            sout = Routs[:, c, :] if is_s else Sr
            kS_ = [R0bf] if is_s else [Srbf]
            if c % 4 == 0:
                bo = bank()
            oo = bo[0:C, (c % 4) * 128:(c % 4 + 1) * 128]
            mm(oo, RpT[:, csl], sbf, True, False, r=[RpT] + kS_, w=[bo])
            for h in range(2):
                hb = slice(h * 64, (h + 1) * 64)
                mm(oo[:, hb], msk["PkT"][0:C, c, h * 64:h * 64 + C], tm["Vtr"][0:C, c, hb], False, False, r=[msk["PkT"], tm["Vtr"]], w=[bo])
                mm(oo[:, hb], msk["PbT"][0:C, c, h * 64:h * 64 + C], tm["Un"][0:C, c, hb], False, h == 1, r=[msk["PbT"], tm["Un"]], w=[bo])
            b3 = bank()
            mm(b3[:, 0:128], Mblk[:, c, :], sbf, True, True, r=[Mblk] + kS_, w=[b3])
            stt(Srt, sin, F_["Gm"][:, (c + 1) * C - 1:(c + 1) * C], Ncomb[:, c, :], ALU.mult, ALU.add,
                r=[R0blk if is_s else Sr, F_["Gm"], Ncomb], w=[Srt])
            tt(sout, Srt, b3[:, 0:128], ALU.add, r=[Srt, b3], w=[Routs if is_s else Sr])
            if not is_s:
                cp(Srbf, Sr, eng="act")
            if c % 4 == 3:
                cp(Osb2[0:C, c - 3:c + 1, :], bo.rearrange("p (c f) -> p c f", f=128)[0:C], eng="act", r=[bo], w=[Osb2])
        if debug is not None and debug.get('rstop', 99) <= 6:
            return
        if is_s:
            s0 = (c0 - SEQ) // 4
            for s4 in range(0, 4, 4):
                b_ = bank()
                for q in range(4):
                    tr(b_[:, q * 128:(q + 1) * 128], Routs[:, s4 + q, :], identF, w=[b_])
                cp(R0in[:, s4:s4 + 4, :], b_.rearrange("p (q f) -> p q f", f=128), r=[b_], w=[R0in])
            dma("sp", rs[s0:s0 + 4, 2 * j].rearrange("s i j -> i s j"), R0in[0:64, :, 0:64], r=[R0in], w=[rs])
            dma("sp", rs[s0:s0 + 4, 2 * j + 1].rearrange("s i j -> i s j"), R0in[64:128, :, 64:128], r=[R0in], w=[rs])
        elif last:
            b_ = bank(); tr(b_[:, 0:128], Sr, identF, w=[b_]); cp(Srt, b_[:, 0:128])
            dma("sp", rp[2 * j], Srt[0:64, 0:64], r=[Srt], w=[rp]); dma("sp", rp[2 * j + 1], Srt[64:128, 64:128], r=[Srt], w=[rp])
        if debug is not None and debug.get('rstop', 99) <= 7:
            return
        G2 = NCH * 2
        og = Osb2[0:C, 0:NCH, :].rearrange("p c (h i) -> p (c h) i", i=64)
        oq = Osq2[0:C, 0:NCH, :].rearrange("p c (h i) -> p (c h) i", i=64)
        P.add("dve", lambda e: e.reduce_sum(out=st3[0:C, 0:G2], in_=og, axis=AX.X), r=[Osb2], w=[st3])
        ts(st3[0:C, 0:G2], st3[0:C, 0:G2], 1.0 / 64, None, ALU.mult)
        tt(og, og, st3[0:C, 0:G2].unsqueeze(2).broadcast_to([C, G2, 64]), ALU.subtract, r=[Osb2, st3], w=[Osb2])
        tt(oq, og, og, ALU.mult, eng="pool", r=[Osb2], w=[Osq2])
        P.add("dve", lambda e: e.reduce_sum(out=st4[0:C, 0:G2], in_=oq, axis=AX.X), r=[Osq2], w=[st4])
        rsqrt_(st4[0:C, 0:G2], st4[0:C, 0:G2], 1.0 / 64, LNX_EPS, stC[0:C, 0:G2], stD[0:C, 0:G2])
        tt(tm["onr"][0:C, 0:NCH, :].rearrange("p c (h i) -> p (c h) i", i=64), og, st4[0:C, 0:G2].unsqueeze(2).broadcast_to([C, G2, 64]), ALU.mult,
           r=[Osb2, st4], w=[tm["onr"]])
        b_ = bank(); bb = b_.bitcast(BF16)
        for c in range(NCH):
            tr(bb[:, c * C:(c + 1) * C], tm["onr"][0:C, c, :], identB[0:C, 0:C], r=[tm["onr"], identB], w=[b_])
        ts(X("u2"), bb[:, 0:T], col("lw", j), col("lbb", j), ALU.mult, ALU.add, r=[b_, pv], w=[F_["u2"]])
        tt(X("u2"), X("u2"), X("bon"), ALU.add, eng="pool")
        tt(X("u2"), X("u2"), X("Gb"), ALU.mult, eng="pool")
        P.add("need", ("oTw", gi))
        tt(mT[:, j, c0:c0 + T], X("u2"), oT[:, 0:T], ALU.add, r=[F_["u2"], oT], w=[(mT, ti)])
        P.add("mark", ("oTr", gi))

    NBLK = 8 if debug is None else debug.get('nblk', 8)

    def hgrn_tile(j, ti, c0, T, C, NCH, is_s, last, wb, oT, gi):
        M_ = masks[C]; F_ = f32t; B_ = bft
        if is_s:
            s0 = (c0 - SEQ) // 4
            dma("sp", S0all, sh_in[s0:s0 + 4, j].rearrange("s k v -> k s v"), r=[sh_in], w=[S0all])
            cp(S0bf, S0all, eng="pool")
        bf_ = proj(wb, 128, 256, ti, c0, T)
        act(F_["th"][:, 0:T], bf_[:, 0:T], AF.Tanh, scale=0.5, r=[bf_], w=[F_["th"]])
        aff(F_["fS"][:, 0:T], F_["th"][:, 0:T], hs1[:, j:j + 1], hs2[:, j:j + 1], r=[F_["th"], hs1, hs2], w=[F_["fS"]])
        aff(F_["kS"][:, 0:T], F_["th"][:, 0:T], hs1n[:, j:j + 1], hs2c[:, j:j + 1], r=[F_["th"], hs1n, hs2c], w=[F_["kS"]])
        cumprod(F_["Pc"], F_["fS"], T, C, NCH)
        P.add("dve", lambda e, T=T: e.reciprocal(out=F_["rP"][:, 0:T], in_=F_["Pc"][:, 0:T]), r=[F_["Pc"]], w=[F_["rP"]])
        bq = proj(wb, 0, 128, ti, c0, T)
        tt(B_["qt"][:, 0:T], bq[:, 0:T], F_["Pc"][:, 0:T], ALU.mult, r=[bq, F_["Pc"]], w=[B_["qt"]])
        tt(F_["t1"][:, 0:T], F_["kS"][:, 0:T], F_["rP"][:, 0:T], ALU.mult, eng="pool", r=[F_["kS"], F_["rP"]], w=[F_["t1"]])
        cp(B_["kt"][:, 0:T], F_["t1"][:, 0:T], eng="act", r=[F_["t1"]], w=[B_["kt"]])
        tt(chunk3(B_["kh"][:, 0:T], C), chunk3(F_["t1"][:, 0:T], C), lastbc(F_["Pc"], T, C, NCH), ALU.mult, r=[F_["t1"], F_["Pc"]], w=[B_["kh"]])
        bi = proj(wb, 256, 384, ti, c0, T)
        cp(B_["vT"][:, 0:T], bi[:, 0:T], eng="act", r=[bi], w=[B_["vT"]])
        bog = proj(wb, 384, 512, ti, c0, T)
        sigm(F_["t2"][:, 0:T], bog[:, 0:T], r=[bog], w=[F_["t2"]])
        tt(F_["t2"][:, 0:T], bog[:, 0:T], F_["t2"][:, 0:T], ALU.mult, r=[bog, F_["t2"]], w=[F_["t2"]])
        bga = proj(wb, 512, 640, ti, c0, T)
        sigm(F_["Ga"][:, 0:T], bga[:, 0:T], r=[bga], w=[F_["Ga"]])
        tt(F_["Ga"][:, 0:T], F_["Ga"][:, 0:T], F_["t2"][:, 0:T], ALU.mult, eng="pool", r=[F_["Ga"], F_["t2"]], w=[F_["Ga"]])
        to_tm(tm["Vtm"], B_["vT"], T, C, NCH); to_tm(tm["khtm"], B_["kh"], T, C, NCH)
        b_ = bank()
        for c in range(NCH):
            mm(b_[0:C, c * C:(c + 1) * C], B_["kt"][:, c * C:(c + 1) * C], B_["qt"][:, c * C:(c + 1) * C], True, True, w=[b_])
        tt(attnT[0:C, 0:NCH, 0:C], b_[0:C, 0:NCH * C].rearrange("p (c t) -> p c t", t=C),
           M_["incl"].unsqueeze(1).broadcast_to([C, NCH, C]), ALU.mult, r=[b_, M_["incl"]], w=[attnT])
        bo = banks[0]
        for c in range(NCH):
            csl = slice(c * C, (c + 1) * C)
            sbf = S0bf[:, c, :] if is_s else Shbf
            sin = S0all[:, c, :] if is_s else Sh
            sout = Souts[:, c, :] if is_s else Sh
            oo = bo[0:C, c * 128:(c + 1) * 128]
            mm(oo, attnT[0:C, c, 0:C], tm["Vtm"][0:C, c, :], True, False, r=[attnT, tm["Vtm"]], w=[bo])
            mm(oo, B_["qt"][:, csl], sbf, False, True, r=[B_["qt"], S0bf if is_s else Shbf], w=[bo])
            b2 = bank()
            mm(b2[:, 0:128], tm["khtm"][0:C, c, :], tm["Vtm"][0:C, c, :], True, True, r=[tm["khtm"], tm["Vtm"]], w=[b2])
            stt(sout, sin, F_["Pc"][:, (c + 1) * C - 1:(c + 1) * C], b2[:, 0:128], ALU.mult, ALU.add,
                r=[S0all if is_s else Sh, F_["Pc"], b2], w=[Souts if is_s else Sh])
            if not is_s:
                cp(Shbf, Sh, eng="act")
        cp(Osb[0:C, 0:NCH, :], bo.rearrange("p (c f) -> p c f", f=128)[0:C], eng="act", r=[bo], w=[Osb])
        if is_s:
            dma("sp", hs[s0:s0 + 4, j].rearrange("s k v -> k s v"), Souts, r=[Souts], w=[hs])
        elif last:
            dma("sp", hp[j], Sh, r=[Sh], w=[hp])
        tt(Osq[0:C, 0:NCH, :], Osb[0:C, 0:NCH, :], Osb[0:C, 0:NCH, :], ALU.mult, eng="pool", r=[Osb], w=[Osq])
        P.add("dve", lambda e, C=C, NCH=NCH: e.reduce_sum(out=st_[0:C, 0:NCH], in_=Osq[0:C, 0:NCH, :], axis=AX.X), r=[Osq], w=[st_])
        rsqrt_(st_[0:C, 0:NCH], st_[0:C, 0:NCH], 1.0 / 128, RMS_EPS, stA[0:C, 0:NCH], stB[0:C, 0:NCH])
        tt(Osq[0:C, 0:NCH, :], Osb[0:C, 0:NCH, :], st_[0:C, 0:NCH].unsqueeze(2).broadcast_to([C, NCH, 128]), ALU.mult, r=[Osb, st_], w=[Osq])
        tt(tm["onb"][0:C, 0:NCH, :], Osq[0:C, 0:NCH, :], nwbc[0:C, :].unsqueeze(1).broadcast_to([C, NCH, 128]), ALU.mult, r=[Osq, nwbc], w=[tm["onb"]])
        b_ = bank(); bb = b_.bitcast(BF16)
        for c in range(NCH):
            tr(bb[:, c * C:(c + 1) * C], tm["onb"][0:C, c, :], identB[0:C, 0:C], r=[tm["onb"], identB], w=[b_])
        if gi >= 2:
            P.add("need", ("oTr", gi - 2))
        tt(oT[:, 0:T], bb[:, 0:T], F_["Ga"][:, 0:T], ALU.mult, r=[b_, F_["Ga"]], w=[oT])
        P.add("mark", ("oTw", gi))

    def stream(kind):
        gi = 0
        for j in range(NBLK):
            seg = lambda col0: w_in[:, col0 + j * 128:col0 + (j + 1) * 128].rearrange("(k p) c -> p k c", p=128)
            if kind == "H":
                for si, col0 in enumerate((0, 1024, 2048, 3072, 7456)):
                    dma("pool", wbH[:, :, si * 128:(si + 1) * 128], seg(col0), r=[w_in], w=[wbH])
            else:
                for si, col0 in enumerate((4096, 5120, 6144, 8480)):
                    dma("pool", wbR[:, :, si * 128:(si + 1) * 128], seg(col0), r=[w_in], w=[wbR])
            if kind == "H":
                memset(Sh, 0.0, eng="dve"); memset(Shbf, 0.0, eng="dve")
            else:
                memset(Sr, 0.0, eng="dve"); memset(Srbf, 0.0, eng="dve"); memset(prevs[:, 3:6], 0.0, eng="dve", w=[prevs])
            for ti, (c0, T, C, NCH, is_s) in enumerate(tiles):
                if debug is not None and 'tsel' in debug and ti not in debug['tsel']:
                    continue
                last = (c0 + T == SEQ)
                if kind == "H":
                    hgrn_tile(j, ti, c0, T, C, NCH, is_s, last, wbH, oTs[gi % 2], gi)
                else:
                    rwkv_block(j, ti, c0, T, C, NCH, is_s, last, wbR, oTs[gi % 2], gi)
                gi += 1

    opsH = []; opsR = []
    P.sink = opsH; cur[0] = "H"; stream("H")
    P.sink = opsR; cur[0] = "R"; stream("R")
    P.sink = None; cur[0] = None
    if debug is not None and debug.get("only") == "R":
        opsH = [x for x in opsH if x[0] == "mark"]
    if debug is not None and debug.get("only") == "H":
        opsR = [x for x in opsR if x[0] == "mark"]
    _n0 = len(P.ops)
    P.merge(opsH, opsR, lead=(150 if debug is None else debug.get('lead', 150)))
    if debug is not None and debug.get("xdeps"):
        from collections import Counter
        cx = Counter()
        for i in range(_n0, len(P.ops)):
            o = P.ops[i]
            for d in o["deps"]:
                od = P.ops[d]
                if d >= _n0 and od["stream"] != o["stream"]:
                    shared = (set(o["rk"]) | set(o["wk"])) & (set(od["rk"]) | set(od["wk"]))
                    cx[(o["stream"], o["eng"], od["eng"], str(sorted(map(str, shared)))[:80])] += 1
        for k, v in cx.most_common(30):
            print("XDEP", v, k)
    dump("mT", mT.rearrange("p a b -> p (a b)"), [128, 8 * NT])
    for p4 in range(0, 27, 4):
        b_ = bank(); n4 = min(4, 27 - p4); wdt = min(n4 * 128, RC - p4 * 128)
        for q in range(n4):
            tr(b_[0:17, q * 128:(q + 1) * 128], lastc[:, p4 + q, :], identF, r=[lastc, identF], w=[b_])
        cp(srow[:, 0:n4 * 128], b_[0:17, 0:n4 * 128], r=[b_], w=[srow])
        dma("sp", shp[:, p4 * 128:p4 * 128 + wdt], srow[0:1, 0:wdt], r=[srow], w=[shp])
        dma("sp", shs[:, p4 * 128:p4 * 128 + wdt], srow[1:17, 0:wdt], r=[srow], w=[shs])
    if debug is not None and debug.get("_stop") == 2:
        n = P.emit()
        return nc, dbg, n
    print('SBUF phase2 total', SBTOT[0], 'S2', S2.tot)
    S2.close()
    P.barrier(dummy)
    S1 = Scope()
    xT = S1.sb("xT", [128, 8, NT]); h2T = S1.sb("h2T", [128, 8, NT], BF16)
    nbuf = dict(sq=S1.sb("sq", [128, 8, 512], BF16), rstd=S1.sb("rstd", [128, 512]), tmp=S1.sb("ntmp", [128, 8, 512]),
                xtm=[S1.sb("xtm", [128, D]) for _ in range(2)], i=0)
    nbuf["r1"] = nbuf["tmp"][:, 1, :]; nbuf["r2"] = nbuf["tmp"][:, 0, :]
    wo = S1.sb("wo", [128, 8, 128], BF16); wus = [S1.sb("wu", [128, 8, 256], BF16) for _ in range(2)]; wds = [S1.sb("wd", [128, 2, D], BF16)] * 2
    aTs = [S1.sb("aT", [128, 2, 512], BF16) for _ in range(2)]; rTs = [S1.sb("rT", [128, 1, 512], BF16)] * 2; t64 = S1.sb("t64", [128, 64])
    tiles34 = [(i * 512, 512, False) for i in range(4)] + [(SEQ, 64, True)]
    for ti, (c0, T, is_s) in enumerate(tiles34):
        load_xT(xT[:, :, c0:c0 + T], c0, T, ti)

    def resid_add(f, c0, T, is_s, ps, gtp, gts):
        if not is_s:
            stt(xT[:, f, c0:c0 + T], ps[:, 0:T], gtp[:, f:f + 1], xT[:, f, c0:c0 + T], ALU.mult, ALU.add, r=[ps, gtp, xT], w=[xT])
        else:
            tt(t64[:, 0:T], ps[:, 0:T], gts[:, f, :], ALU.mult, r=[ps, gts], w=[t64])
            tt(xT[:, f, c0:c0 + T], xT[:, f, c0:c0 + T], t64[:, 0:T], ALU.add, r=[xT, t64], w=[xT])

    for f in range(8):
        dma("pool", wo, w_out[:, f * 128:(f + 1) * 128].rearrange("(k p) c -> p k c", p=128), r=[w_out], w=[wo])
        for ti, (c0, T, is_s) in enumerate(tiles34):
            b_ = bank()
            for k in range(8):
                mm(b_[:, 0:T], wo[:, k, :], mT[:, k, c0:c0 + T], k == 0, k == 7, r=[wo, mT], w=[b_])
            resid_add(f, c0, T, is_s, b_, gt1p, gt1s)
    for ti, (c0, T, is_s) in enumerate(tiles34):
        norm_to(h2T[:, :, c0:c0 + T], xT[:, :, c0:c0 + T], T, g2p, sh2p, g2s, sh2s, is_s, 0)
    steps = [(g, ti) for g in range(16) for ti in range(len(tiles34))]
    upb = [0]; dnb = [0]

    def load_wu(g):
        wu_ = wus[g % 2]
        dma("pool", wu_, w_up[:, g * 256:(g + 1) * 256].rearrange("(k p) c -> p k c", p=128), r=[w_up], w=[wu_])

    def load_wd(g):
        wd_ = wds[0]
        dma("pool", wd_, w_down[g * 256:(g + 1) * 256, :].rearrange("(q p) c -> p q c", p=128), r=[w_down], w=[wd_])

    def up(k):
        g, ti = steps[k]; c0, T, is_s = tiles34[ti]
        if ti == 0 and g + 1 < 16:
            load_wu(g + 1)
        wu_ = wus[g % 2]; aT_ = aTs[k % 2]; rT_ = rTs[k % 2]
        for q in range(2):
            b_ = banks[upb[0] % 4]; upb[0] += 1
            for kk_ in range(8):
                mm(b_[:, 0:T], wu_[:, kk_, q * 128:(q + 1) * 128], h2T[:, kk_, c0:c0 + T], kk_ == 0, kk_ == 7, r=[wu_, (h2T, 0)], w=[b_])
            act(rT_[:, 0, 0:T], b_[:, 0:T], AF.Relu, r=[b_], w=[rT_])
            tt(aT_[:, q, 0:T], rT_[:, 0, 0:T], rT_[:, 0, 0:T], ALU.mult, eng=("pool" if q == 1 else "dve"), r=[rT_], w=[aT_])

    def down(k):
        g, ti = steps[k]; c0, T, is_s = tiles34[ti]
        wd_ = wds[g % 2]; aT_ = aTs[k % 2]
        for f in range(8):
            b_ = banks[4 + dnb[0] % 4]; dnb[0] += 1
            for q in range(2):
                mm(b_[:, 0:T], wd_[:, q, f * 128:(f + 1) * 128], aT_[:, q, 0:T], q == 0, q == 1, r=[wd_, aT_], w=[b_])
            resid_add(f, c0, T, is_s, b_, gt2p, gt2s)

    load_wu(0); load_wd(0)
    up(0)
    for k in range(len(steps)):
        if k + 1 < len(steps):
            up(k + 1)
        down(k)
        g_, ti_ = steps[k]
        if ti_ == len(tiles34) - 1 and g_ + 1 < 16:
            load_wd(g_ + 1)
    fnp = pv[:, rows["fn"]:rows["fn"] + 8]
    for ti, (c0, T, is_s) in enumerate(tiles34):
        sq = nbuf["sq"]; rstd = nbuf["rstd"]; tmp = nbuf["tmp"]
        xv_ = xT[:, :, c0:c0 + T]
        tt(sq[:, :, 0:T], xv_, xv_, ALU.mult, r=[xT], w=[sq])
        b_ = bank()
        for fc in range(8):
            mm(b_[:, 0:T], onesB, sq[:, fc, 0:T], fc == 0, fc == 7, w=[b_])
        ts(rstd[:, 0:T], b_[:, 0:T], 1.0 / D, RMS_EPS, ALU.mult, ALU.add, r=[b_], w=[rstd])
        r1 = nbuf["r1"]; r2 = nbuf["r2"]
        cp(r1[:, 0:T], rstd[:, 0:T], eng="act", r=[rstd], w=[r1])
        rsqrt_dve(rstd[:, 0:T], r1[:, 0:T], r2[:, 0:T], [rstd, r1, r2])
        tt(tmp[:, :, 0:T], xv_, rstd[:, 0:T].unsqueeze(1).broadcast_to([128, 8, T]), ALU.mult, r=[xT, rstd], w=[tmp])
        tt(tmp[:, :, 0:T], tmp[:, :, 0:T], fnp.unsqueeze(2).broadcast_to([128, 8, T]), ALU.mult, r=[tmp, pv], w=[tmp])
        for sub in range(0, T, 128):
            n = min(128, T - sub)
            ytm = nbuf["xtm"][nbuf["i"] % 2]; nbuf["i"] += 1
            for half in range(2):
                b_ = bank()
                for f4 in range(4):
                    fc = half * 4 + f4
                    tr(b_[0:n, f4 * 128:(f4 + 1) * 128], tmp[:, fc, sub:sub + n], identF, r=[tmp, identF], w=[b_])
                cp(ytm[0:n, half * 512:(half + 1) * 512], b_[0:n, :], eng="act", r=[b_], w=[ytm])
            dst = yp[c0 + sub:c0 + sub + n, :] if not is_s else ys[sub:sub + n, :]
            dma("sp", dst, ytm[0:n, :], r=[ytm], w=[yp if not is_s else ys])
    n = P.emit()
    print('SBUF bytes/partition: G', G.tot, 'max', SBMAX[0])
    S1.close(); G.close()
    return nc, dbg, n


_CACHE = {}


def _core_inputs(inp, c):
    f = lambda a: np.ascontiguousarray(a, dtype=np.float32)
    m = {"xp": inp["x_prompt"][c], "xs": inp["x_sample"][16 * c:16 * c + 16].reshape(64, D),
         "sh": inp["state_hgrn"][0, 16 * c:16 * c + 16], "sr": inp["state_rwkv"][0, 16 * c:16 * c + 16],
         "ss": inp["state_shift"][0, 16 * c:16 * c + 16], "cp": inp["c_prompt"][c:c + 1], "cs": inp["c_sample"][16 * c:16 * c + 16]}
    for k in ("norm1_w", "norm2_w", "rwkv_w0", "rwkv_a0", "rwkv_k_k", "rwkv_k_a", "rwkv_lnx_w", "rwkv_lnx_b", "rwkv_r_k", "final_norm_w"):
        m[k] = inp[k].reshape(8, 128)
    m["ada_w"] = inp["ada_w"][0]; m["ada_b"] = inp["ada_b"].reshape(48, 128); m["w_in"] = inp["w_in"][0]
    m["lb_logits"] = inp["lb_logits"].reshape(16, 128); m["hgrn_norm_w"] = inp["hgrn_norm_w"].reshape(1, 128)
    m["rwkv_mu"] = inp["rwkv_mu"].reshape(1, RC); m["rwkv_w2"] = inp["rwkv_w2"][0]; m["rwkv_a2"] = inp["rwkv_a2"][0]
    m["rwkv_g2"] = inp["rwkv_g2"][0]; m["w_out"] = inp["w_out"][0]; m["w_up"] = inp["w_up"][0]; m["w_down"] = inp["w_down"][0]
    return {k: f(v) for k, v in m.items()}


def kernel(**inputs):
    inp = {k: np.asarray(v) for k, v in inputs.items()}
    if "nc" not in _CACHE:
        _CACHE["nc"] = build()[0]
    nc = _CACHE["nc"]
    in_maps = [_core_inputs(inp, c) for c in range(8)]
    res = run_bass_kernel_spmd(nc, in_maps, core_ids=list(range(8)))
    R = res.results
    g = lambda n: [np.asarray(R[c][n], dtype=np.float32) for c in range(8)]
    y_prompt = np.stack(g("yp"), 0)
    y_sample = np.concatenate([a.reshape(16, 4, D) for a in g("ys")], 0)
    hgrn_p = np.stack(g("hp"), 0)[None]
    rwkv_p = np.stack(g("rp"), 0)[None]
    shift_p = np.concatenate(g("shp"), 0)[None]
    hgrn_s = np.concatenate(g("hs"), 0)[None]
    rwkv_s = np.concatenate(g("rs"), 0)[None]
    shift_s = np.concatenate(g("shs"), 0)[None]
    return (y_prompt, y_sample, hgrn_p, rwkv_p, shift_p, hgrn_s, rwkv_s, shift_s)
```

```python
import contextlib
import numpy as np
import concourse.bass as bass
import concourse.mybir as mybir
from concourse.bass_utils import run_bass_kernel_spmd

F32 = mybir.dt.float32
BF16 = mybir.dt.bfloat16
AF = mybir.ActivationFunctionType
ALU = mybir.AluOpType
AX = mybir.AxisListType

D = 1024
SEQ = 2048
NSEQ_S = 16
TS = 4
NT = SEQ + NSEQ_S * TS
INC = 9504
RC = 3360
NDMASEM = 12
SAME_ENG_RELAX = ()
RMS_EPS = 1e-6
LNX_EPS = 64e-5
C1 = -0.5 * float(np.exp(-0.5))


class Prog:
    def __init__(self, nc):
        self.nc = nc
        self.ops = []
        self.last_w = {}
        self.readers = {}
        self.bar = False
        self.sink = None

    def peek(self, r, w):
        deps = set()
        rk = [self.key(x) for x in r]; wk = [self.key(x) for x in w]
        if self.bar:
            rk.append("__bar")
        for k in rk:
            if k in self.last_w:
                deps.add(self.last_w[k])
        for k in wk:
            if k in self.last_w:
                deps.add(self.last_w[k])
            deps |= self.readers.get(k, set())
        return deps

    @staticmethod
    def dur(eng, dma, n):
        if dma:
            return 0.1
        if eng == "pe":
            return 0.06 + n / 1200.0
        if eng == "dve":
            return 0.10 + n / 960.0
        if eng == "act":
            return 0.25 + n / 1200.0
        if eng == "pool":
            return 0.30 + n / 400.0
        return 0.1

    def merge(self, A, B, lead=0):
        done = set(); pos = [0, 0]; L = [A, B]
        end = {}; free = {}
        tot = [max(1, len(A)), max(1, len(B))]

        def est_start(x):
            eng, fn, r, w, dma, n = x
            deps = self.peek(r, w)
            t = free.get(eng, 0.0)
            for d in deps:
                if d in end:
                    lat = 0.2 if self.ops[d]["eng"] == eng else 0.3
                    if self.ops[d]["eng"] == "pe" and eng == "pe":
                        lat = 0.0
                    t = max(t, end[d] + lat)
            return t

        while pos[0] < len(A) or pos[1] < len(B):
            cand = []
            for si in (0, 1):
                while pos[si] < len(L[si]) and L[si][pos[si]][0] == "mark":
                    done.add(L[si][pos[si]][1]); pos[si] += 1
            for si in (0, 1):
                if pos[si] >= len(L[si]):
                    continue
                x = L[si][pos[si]]
                if x[0] == "need":
                    if x[1] in done:
                        pos[si] += 1
                        cand = None
                        break
                    continue
                cand.append((est_start(x), pos[si] / tot[si], si))
            if cand is None:
                continue
            if not cand:
                if pos[0] >= len(A) and pos[1] >= len(B):
                    break
                raise RuntimeError("merge deadlock")
            t, _, si = min(cand)
            x = L[si][pos[si]]; pos[si] += 1
            eng, fn, r, w, dma, n = x
            idx = self.add(eng, fn, r, w, dma, n)
            d_ = self.dur(eng, dma, n)
            free[eng] = t + d_
            end[idx] = t + d_ + (2.5 if dma else 0.0)

    @staticmethod
    def key(x):
        if isinstance(x, tuple):
            if isinstance(x[0], str):
                return x
            return (x[0].tensor.name, x[1])
        if isinstance(x, str):
            return x
        return x.tensor.name

    def add(self, eng, fn, r=(), w=(), dma=False, n=256):
        if self.sink is not None:
            if eng in ("mark", "need"):
                self.sink.append((eng, fn))
            else:
                self.sink.append((eng, fn, list(r), list(w), dma, n))
            return None
        deps = set()
        rk = [self.key(x) for x in r]
        wk = [self.key(x) for x in w]
        if self.bar:
            rk.append("__bar")
        for k in rk:
            if k in self.last_w:
                deps.add(self.last_w[k])
        for k in wk:
            if k in self.last_w:
                deps.add(self.last_w[k])
            deps |= self.readers.get(k, set())
        idx = len(self.ops)
        self.ops.append(dict(eng=eng, fn=fn, deps=deps, dma=dma, users=0, stream=getattr(self, "tag", None), rk=rk, wk=wk))
        for k in rk:
            self.readers.setdefault(k, set()).add(idx)
        for k in wk:
            self.last_w[k] = idx
            self.readers[k] = set()
        return idx

    def barrier(self, dummy):
        allk = set(self.last_w.keys()) | set(self.readers.keys())
        allk.discard("__bar")
        self.bar = False
        self.add("dve", lambda e: e.memset(dummy, 0.0), r=(), w=list(allk) + ["__bar"])
        self.last_w = {"__bar": self.last_w["__bar"]}
        self.readers = {}
        self.bar = True

    def emit(self):
        nc = self.nc
        engs = {"pe": nc.tensor, "act": nc.scalar, "dve": nc.vector, "pool": nc.gpsimd, "sp": nc.sync}
        ops = self.ops

        seqn = {}
        for i_, o_ in enumerate(ops):
            if not o_["dma"]:
                seqn[o_["eng"]] = seqn.get(o_["eng"], 0) + 1
                o_["seq"] = seqn[o_["eng"]]

        def skip(d, o):
            if ops[d]["dma"] or o["dma"]:
                return False
            if ops[d]["eng"] != o["eng"]:
                return False
            if o["eng"] == "pe":
                return True
            if o["eng"] in SAME_ENG_RELAX and o["seq"] - ops[d]["seq"] >= 2:
                return True
            return False

        for o in ops:
            for d in o["deps"]:
                if not skip(d, o):
                    ops[d]["users"] += 1
        with contextlib.ExitStack() as st:
            csem = {e: st.enter_context(nc.semaphore("c_" + e)) for e in engs}
            dsem = {e: [st.enter_context(nc.semaphore(f"d_{e}{i}")) for i in range(NDMASEM)]
                    for e in ("act", "pool", "sp")}
            ccnt = {e: 0 for e in engs}
            dcnt = {e: [0] * NDMASEM for e in dsem}
            drr = {e: 0 for e in dsem}
            waited = {e: {} for e in engs}
            sig = {}

            def wait(e, sem, val):
                k = id(sem)
                if val <= 0 or waited[e].get(k, 0) >= val:
                    return
                engs[e].wait_ge(sem, val)
                waited[e][k] = val

            for i, o in enumerate(ops):
                e = o["eng"]
                for d in sorted(o["deps"]):
                    if d not in sig or skip(d, o):
                        continue
                    s, v = sig[d]
                    wait(e, s, v)
                if o["dma"]:
                    j = drr[e]
                    drr[e] = (j + 1) % NDMASEM
                    s = dsem[e][j]
                    wait(e, s, dcnt[e][j])
                    ins = o["fn"](engs[e])
                    dcnt[e][j] += 16
                    ins.then_inc(s, 16)
                    sig[i] = (s, dcnt[e][j])
                else:
                    ins = o["fn"](engs[e])
                    if o["users"] > 0:
                        ccnt[e] += 1
                        ins.then_inc(csem[e], 1)
                        sig[i] = (csem[e], ccnt[e])
            for e in dsem:
                for j in range(NDMASEM):
                    if dcnt[e][j] > 0:
                        nc.sync.wait_ge(dsem[e][j], dcnt[e][j])
        return len(ops)


def build(debug=None):
    nc = bass.Bass("TRN2", target_bir_lowering=False)
    P = Prog(nc)
    din = lambda n, s: nc.dram_tensor(n, list(s), F32, kind="ExternalInput").ap()
    dout = lambda n, s: nc.dram_tensor(n, list(s), F32, kind="ExternalOutput").ap()
    xp = din("xp", [SEQ, D]); xs = din("xs", [64, D])
    sh_in = din("sh", [16, 8, 128, 128]); sr_in = din("sr", [16, 16, 64, 64]); ss_in = din("ss", [16, RC])
    cpr = din("cp", [1, D]); csm = din("cs", [16, D])
    norm1_w = din("norm1_w", [8, 128]); norm2_w = din("norm2_w", [8, 128])
    ada_w = din("ada_w", [D, 6 * D]); ada_b = din("ada_b", [48, 128])
    w_in = din("w_in", [D, INC]); lb_logits = din("lb_logits", [16, 128])
    hgrn_norm_w = din("hgrn_norm_w", [1, 128]); mu = din("rwkv_mu", [1, RC])
    w0 = din("rwkv_w0", [8, 128]); w2 = din("rwkv_w2", [64, D]); a0 = din("rwkv_a0", [8, 128])
    a2 = din("rwkv_a2", [64, D]); g2 = din("rwkv_g2", [160, D]); k_k = din("rwkv_k_k", [8, 128])
    k_a = din("rwkv_k_a", [8, 128]); r_k = din("rwkv_r_k", [8, 128]); lnx_w = din("rwkv_lnx_w", [8, 128])
    lnx_b = din("rwkv_lnx_b", [8, 128]); w_out = din("w_out", [D, D]); w_up = din("w_up", [D, 4 * D])
    w_down = din("w_down", [4 * D, D]); fnw = din("final_norm_w", [8, 128])
    yp = dout("yp", [SEQ, D]); ys = dout("ys", [64, D])
    hp = dout("hp", [8, 128, 128]); rp = dout("rp", [16, 64, 64]); shp = dout("shp", [1, RC])
    hs = dout("hs", [16, 8, 128, 128]); rs = dout("rs", [16, 16, 64, 64]); shs = dout("shs", [16, RC])
    dbg = {}

    cnt = [0]

    def uname(n):
        cnt[0] += 1
        return f"{n}_{cnt[0]}"

    SBTOT = [0]; SBMAX = [0]

    class Scope:
        def __init__(self):
            self.st = contextlib.ExitStack()

        def sb(self, name, shape, dt=F32):
            nb = int(np.prod(shape[1:])) * (4 if dt == F32 else 2)
            SBTOT[0] += nb; self.tot = getattr(self, "tot", 0) + nb
            SBMAX[0] = max(SBMAX[0], SBTOT[0])
            return self.st.enter_context(nc.sbuf_tensor(uname(name), list(shape), dt)).ap()

        def close(self):
            SBTOT[0] -= getattr(self, "tot", 0)
            self.st.close()

    G = Scope()
    banks = [nc.alloc_psum_tensor(f"bank{i}", [128, 512], F32).ap() for i in range(8)]
    bk = [0]

    pools = {None: list(range(8)), "H": [1, 2], "R": [3, 4, 5, 6, 7]}
    cur = [None]
    bkp = {"H": 0, "R": 0}

    def bank():
        if cur[0] is None:
            b = banks[bk[0] % 8]
            bk[0] += 1
            return b
        pl = pools[cur[0]]
        b = banks[pl[bkp[cur[0]] % len(pl)]]
        bkp[cur[0]] += 1
        return b

    def dump(name, ap, shape, keys=None):
        if debug is None or (name not in debug and "all" not in debug):
            return
        d = nc.dram_tensor("dbg_" + name, list(shape), F32, kind="ExternalOutput").ap()
        dbg[name] = d
        q = "pool" if ap.dtype != F32 else "sp"
        rk = [ap] + [(ap, t_) for t_ in range(16)] if keys is None else keys
        P.add(q, lambda e: e.dma_start(out=d, in_=ap), r=rk, w=[d], dma=True)

    def dma(q, out, in_, r=(), w=()):
        P.add(q, lambda e: e.dma_start(out=out, in_=in_), r=list(r) or [in_], w=list(w) or [out], dma=True)

    def act(out, in_, func, bias=0.0, scale=1.0, r=(), w=(), accum=None):
        rr = list(r) or [in_]
        if not isinstance(bias, float):
            rr.append(bias)
        if not isinstance(scale, float):
            rr.append(scale)
        ww = list(w) or [out]
        if accum is not None:
            ww.append(accum)
        if accum is not None:
            P.add("act", lambda e: e.activation(out=out, in_=in_, func=func, bias=bias, scale=scale, accum_out=accum), r=rr, w=ww, n=out.free_size())
        else:
            P.add("act", lambda e: e.activation(out=out, in_=in_, func=func, bias=bias, scale=scale), r=rr, w=ww, n=out.free_size())

    def tt(out, in0, in1, op, eng="dve", r=(), w=()):
        P.add(eng, lambda e: e.tensor_tensor(out=out, in0=in0, in1=in1, op=op), r=list(r) or [in0, in1], w=list(w) or [out], n=out.free_size())

    def ts(out, in0, s1, s2, op0, op1=None, eng="dve", r=(), w=()):
        rr = list(r) or [in0]
        for s in (s1, s2):
            if s is not None and not isinstance(s, float):
                rr.append(s)
        if op1 is None:
            P.add(eng, lambda e: e.tensor_scalar(out=out, in0=in0, scalar1=s1, scalar2=None, op0=op0), r=rr, w=list(w) or [out], n=out.free_size())
        else:
            P.add(eng, lambda e: e.tensor_scalar(out=out, in0=in0, scalar1=s1, scalar2=s2, op0=op0, op1=op1), r=rr, w=list(w) or [out], n=out.free_size())

    def aff(out, in_, scale, bias, r=(), w=()):
        act(out, in_, AF.Identity, bias=bias, scale=scale, r=r, w=w)

    def stt(out, in0, scalar, in1, op0, op1, r=(), w=()):
        rr = list(r) or [in0, in1]
        if not isinstance(scalar, float):
            rr.append(scalar)
        P.add("dve", lambda e: e.scalar_tensor_tensor(out=out, in0=in0, scalar=scalar, in1=in1, op0=op0, op1=op1), r=rr, w=list(w) or [out], n=out.free_size())

    def cp(out, in_, eng="dve", r=(), w=()):
        if eng == "act":
            act(out, in_, AF.Copy, r=r, w=w)
        else:
            P.add(eng, lambda e: e.tensor_copy(out=out, in_=in_), r=list(r) or [in_], w=list(w) or [out], n=out.free_size())

    def memset(ap, val, eng="pool", w=()):
        P.add(eng, lambda e: e.memset(ap, val), w=list(w) or [ap], n=ap.free_size())

    def mm(out, lhsT, rhs, start, stop, r=(), w=()):
        P.add("pe", lambda e: e.matmul(out, lhsT, rhs, start=start, stop=stop), r=list(r) or [lhsT, rhs], w=list(w) or [out], n=rhs.free_size() + 60)

    def tr(out, in_, ident, r=(), w=()):
        P.add("pe", lambda e: e.transpose(out=out, in_=in_, identity=ident), r=list(r) or [in_, ident], w=list(w) or [out], n=ident.free_size() + 60)

    I32 = mybir.dt.int32

    def rsqrt_dve(y, x, t1, keys):
        P.add("dve", lambda e: e.tensor_scalar(out=y.bitcast(I32), in0=x.bitcast(I32), scalar1=-0.5, scalar2=1597463007.0,
                                               op0=ALU.mult, op1=ALU.add), r=keys, w=keys)
        for _ in range(2):
            tt(t1, y, y, ALU.mult, r=keys, w=keys)
            stt(t1, t1, -0.5, x, ALU.mult, ALU.mult, r=keys, w=keys)
            stt(y, t1, 1.5, y, ALU.add, ALU.mult, r=keys, w=keys)

    def rsqrt_(out, in_, scale, eps, tA, tB):
        ts(tA, in_, scale, eps, ALU.mult, ALU.add)
        rsqrt_dve(out, tA, tB, [out, tA, tB])

    ones128 = G.sb("ones128", [128, 128]); identF = G.sb("identF", [128, 128]); identB = G.sb("identB", [128, 128], BF16)
    memset(ones128, 1.0)
    P.add("pool", lambda e: e.affine_select(out=identF, in_=ones128, pattern=[[1, 128]], compare_op=ALU.is_equal,
                                           fill=0.0, base=0, channel_multiplier=-1), r=[ones128], w=[identF])
    cp(identB, identF)
    onesB = G.sb("onesB", [128, 128], BF16)
    cp(onesB, ones128)
    blkones = G.sb("blkones", [128, 128], BF16)
    memset(blkones, 0.0)
    memset(blkones[0:64, 0:64], 1.0, w=[blkones]); memset(blkones[64:128, 64:128], 1.0, w=[blkones])
    blkneg = G.sb("blkneg", [128, 128])
    memset(blkneg, 0.0)
    memset(blkneg[0:64, 0:64], -1.0, w=[blkneg]); memset(blkneg[64:128, 64:128], -1.0, w=[blkneg])
    blkpos = G.sb("blkpos", [128, 128])
    memset(blkpos, 0.0)
    memset(blkpos[0:64, 0:64], 1.0, w=[blkpos]); memset(blkpos[64:128, 64:128], 1.0, w=[blkpos])
    dummy = G.sb("dummy", [128, 1])
    masks = {}
    for C in (64, 4):
        mi = G.sb(f"mi{C}", [C, C]); ms = G.sb(f"ms{C}", [C, C]); mls = G.sb(f"mls{C}", [C, C])
        P.add("pool", lambda e, mi=mi, C=C: e.affine_select(out=mi, in_=ones128[0:C, 0:C], pattern=[[1, C]], compare_op=ALU.is_ge,
                                                         fill=0.0, base=0, channel_multiplier=-1), r=[ones128], w=[mi])
        P.add("pool", lambda e, ms=ms, C=C: e.affine_select(out=ms, in_=ones128[0:C, 0:C], pattern=[[1, C]], compare_op=ALU.is_gt,
                                                         fill=0.0, base=0, channel_multiplier=-1), r=[ones128], w=[ms])
        P.add("pool", lambda e, mls=mls, C=C: e.affine_select(out=mls, in_=ones128[0:C, 0:C], pattern=[[-1, C]], compare_op=ALU.is_gt,
                                                           fill=0.0, base=0, channel_multiplier=1), r=[ones128], w=[mls])
        msn = G.sb(f"msn{C}", [C, C]); mlsn = G.sb(f"mlsn{C}", [C, C])
        ts(msn, ms, -1.0, None, ALU.mult); ts(mlsn, mls, -1.0, None, ALU.mult)
        masks[C] = dict(incl=mi, strict=ms, nstrict=msn, nlstrict=mlsn)

    stgA = G.sb("stgA", [128, 128]); stgB = G.sb("stgB", [48, 128])
    memset(stgA, 0.0)
    rows = {}
    r0 = 0
    for nm, src, n in (("n1", norm1_w, 8), ("n2", norm2_w, 8), ("lb", lb_logits, 16), ("w0", w0, 8), ("a0", a0, 8),
                       ("kk", k_k, 8), ("ka", k_a, 8), ("rk", r_k, 8), ("lw", lnx_w, 8), ("lbb", lnx_b, 8), ("fn", fnw, 8)):
        dma("sp", stgA[r0:r0 + n, :], src, w=[stgA]); rows[nm] = r0; r0 += n
    dma("sp", stgA[r0:r0 + 26, :], mu[:, 0:3328].rearrange("o (n p) -> (o n) p", p=128), w=[stgA]); rows["mu"] = r0; r0 += 26
    dma("sp", stgA[r0:r0 + 1, 0:32], mu[:, 3328:3360], w=[stgA]); rows["mu2"] = r0; r0 += 1
    assert r0 <= 128
    dma("sp", stgB, ada_b)
    pv = G.sb("pv", [128, 128]); pvB = G.sb("pvB", [128, 48])
    b_ = bank(); tr(b_[:, 0:128], stgA, identF); cp(pv, b_[:, 0:128])
    b_ = bank(); tr(b_[:, 0:48], stgB, identF[0:48, 0:48]); cp(pvB, b_[:, 0:48])
    col = lambda nm, i=0: pv[:, rows[nm] + i: rows[nm] + i + 1]
    pv2 = G.sb("pv2", [128, 128]); ts(pv2, pv, -1.0, 1.0, ALU.mult, ALU.add)
    col2 = lambda nm, i=0: pv2[:, rows[nm] + i: rows[nm] + i + 1]
    hs1 = G.sb("hs1", [128, 8]); hs2 = G.sb("hs2", [128, 8]); hs1n = G.sb("hs1n", [128, 8]); hs2c = G.sb("hs2c", [128, 8])
    tt(hs1, pv[:, rows["lb"]:rows["lb"] + 8], pv[:, rows["lb"] + 8:rows["lb"] + 16], ALU.subtract)
    act(hs2, hs1, AF.Tanh, scale=0.5)
    ts(hs1, hs2, -0.25, 0.25, ALU.mult, ALU.add)
    ts(hs2, hs2, 0.25, 0.75, ALU.mult, ALU.add)
    ts(hs1n, hs1, -1.0, None, ALU.mult)
    ts(hs2c, hs2, -1.0, 1.0, ALU.mult, ALU.add)
    hw0 = G.sb("hw0", [128, 8]); ha0 = G.sb("ha0", [128, 8]); omka = G.sb("omka", [128, 8])
    ts(hw0, pv[:, rows["w0"]:rows["w0"] + 8], 0.5, None, ALU.mult)
    ts(ha0, pv[:, rows["a0"]:rows["a0"] + 8], 0.5, None, ALU.mult)
    ts(omka, pv[:, rows["ka"]:rows["ka"] + 8], -1.0, 1.0, ALU.mult, ALU.add)
    nwbc = G.sb("nwbc", [64, 128]); dma("sp", nwbc, hgrn_norm_w.partition_broadcast(64))

    modT = G.sb("modT", [128, 48, 17])
    g1p = G.sb("g1p", [128, 8]); g2p = G.sb("g2p", [128, 8])
    sh1p = G.sb("sh1p", [128, 8]); sh2p = G.sb("sh2p", [128, 8]); gt1p = G.sb("gt1p", [128, 8]); gt2p = G.sb("gt2p", [128, 8])
    g1s = G.sb("g1s", [128, 8, 64]); g2s = G.sb("g2s", [128, 8, 64]); sh1s = G.sb("sh1s", [128, 8, 64]); sh2s = G.sb("sh2s", [128, 8, 64])
    gt1s = G.sb("gt1s", [128, 8, 64]); gt2s = G.sb("gt2s", [128, 8, 64])
    mT = G.sb("mT", [128, 8, NT], BF16)
    S0 = Scope()
    c17 = S0.sb("c17", [17, D]); s17 = S0.sb("s17", [17, D]); s17b = S0.sb("s17b", [17, D], BF16); scT = S0.sb("scT", [128, 8, 32], BF16)
    dma("sp", c17[0:1, :], cpr, w=[c17]); dma("sp", c17[1:17, :], csm, w=[c17])
    act(s17, c17, AF.Tanh, scale=0.5)
    ts(s17, s17, 0.5, 0.5, ALU.mult, ALU.add)
    tt(s17b, s17, c17, ALU.mult)
    b_ = bank(); bb = b_.bitcast(BF16)
    for k in range(8):
        tr(bb[:, k * 32:k * 32 + 17], s17b[:, k * 128:(k + 1) * 128], identB[0:17, 0:17], w=[b_])
    cp(scT[:, :, 0:17], bb[:, 0:256].rearrange("p (k c) -> p k c", c=32)[:, :, 0:17], r=[b_])
    awb = [S0.sb("awb", [128, 8, 512], BF16) for _ in range(2)]
    for g in range(12):
        wt = awb[g % 2]
        dma("pool", wt, ada_w[:, g * 512:(g + 1) * 512].rearrange("(k p) c -> p k c", p=128), r=[ada_w], w=[wt])
        b_ = bank()
        for m in range(4):
            for k in range(8):
                mm(b_[:, m * 32:m * 32 + 17], wt[:, k, m * 128:(m + 1) * 128], scT[:, k, 0:17], k == 0, k == 7, w=[b_])
        for m in range(4):
            ts(modT[:, g * 4 + m, :], b_[:, m * 32:m * 32 + 17], pvB[:, g * 4 + m:g * 4 + m + 1], None, ALU.add, r=[b_, pvB], w=[modT])
    def modp(part):
        return modT[:, part * 8:(part + 1) * 8, 0]
    def mods(part):
        return modT[:, part * 8:(part + 1) * 8, 1:17].unsqueeze(3).broadcast_to([128, 8, 16, 4])
    stt(g1p, modp(1), 1.0, pv[:, rows["n1"]:rows["n1"] + 8], ALU.add, ALU.mult, r=[modT, pv])
    stt(g2p, modp(4), 1.0, pv[:, rows["n2"]:rows["n2"] + 8], ALU.add, ALU.mult, r=[modT, pv])
    for dst, part in ((sh1p, 0), (gt1p, 2), (sh2p, 3), (gt2p, 5)):
        cp(dst, modp(part), r=[modT])
    def mods3(part):
        return modT[:, part * 8:(part + 1) * 8, 1:17]
    for t4 in range(4):
        for dst, part in ((sh1s, 0), (gt1s, 2), (sh2s, 3), (gt2s, 5)):
            cp(dst[:, :, t4::4], mods3(part), r=[modT], w=[dst])
        for dst, part, nm in ((g1s, 1, "n1"), (g2s, 4, "n2")):
            stt(dst[:, :, t4::4], mods3(part), 1.0, pv[:, rows[nm]:rows[nm] + 8].unsqueeze(2).broadcast_to([128, 8, 16]),
                ALU.add, ALU.mult, r=[modT, pv], w=[dst])
    S0.close()
    P.barrier(dummy)

    tiles = [(i * 256, 256, 64, 4, False) for i in range(8)] + [(SEQ + 16 * i, 16, 4, 4, True) for i in range(4)]

    def norm_to(hT_dst, xT_t, T, gp, shp_, gs, shs_, is_s, ti):
        sq = nbuf["sq"]
        tt(sq[:, :, 0:T], xT_t[:, :, 0:T], xT_t[:, :, 0:T], ALU.mult, r=[xT_t], w=[sq])
        b_ = bank()
        for fc in range(8):
            mm(b_[:, 0:T], onesB, sq[:, fc, 0:T], fc == 0, fc == 7, w=[b_])
        rstd = nbuf["rstd"]
        ts(rstd[:, 0:T], b_[:, 0:T], 1.0 / D, RMS_EPS, ALU.mult, ALU.add, r=[b_], w=[rstd])
        r1 = nbuf["r1"]; r2 = nbuf["r2"]
        cp(r1[:, 0:T], rstd[:, 0:T], eng="act", r=[rstd], w=[r1])
        rsqrt_dve(rstd[:, 0:T], r1[:, 0:T], r2[:, 0:T], [rstd, r1, r2])
        tmp = nbuf["tmp"]
        tt(tmp[:, :, 0:T], xT_t[:, :, 0:T], rstd[:, 0:T].unsqueeze(1).broadcast_to([128, 8, T]), ALU.mult, r=[xT_t, rstd], w=[tmp])
        if not is_s:
            for fc in range(8):
                ts(hT_dst[:, fc, :], tmp[:, fc, 0:T], gp[:, fc:fc + 1], shp_[:, fc:fc + 1], ALU.mult, ALU.add,
                   r=[tmp, gp, shp_], w=[(hT_dst, ti)])
        else:
            tt(tmp[:, :, 0:T], tmp[:, :, 0:T], gs, ALU.mult, r=[tmp, gs], w=[tmp])
            tt(hT_dst, tmp[:, :, 0:T], shs_, ALU.add, r=[tmp, shs_], w=[(hT_dst, ti)])

    def load_xT(xT_t, c0, T, ti):
        for sub in range(0, T, 128):
            n = min(128, T - sub)
            xtm = nbuf["xtm"][nbuf["i"] % 2]; nbuf["i"] += 1
            src = xp[c0 + sub:c0 + sub + n, :] if c0 < SEQ else xs[c0 - SEQ + sub:c0 - SEQ + sub + n, :]
            dma("sp", xtm[0:n, :], src, w=[xtm])
            for half in range(2):
                b_ = bank()
                for f4 in range(4):
                    fc = half * 4 + f4
                    tr(b_[:, f4 * 128:f4 * 128 + n], xtm[0:n, fc * 128:(fc + 1) * 128], identF[0:n, 0:n], w=[b_])
                cp(xT_t[:, half * 4:half * 4 + 4, sub:sub + n], b_.rearrange("p (f t) -> p f t", t=128)[:, :, 0:n], eng="act", r=[b_], w=[xT_t])

    S2 = Scope()
    hT = S2.sb("hT", [128, 8, NT], BF16)
    S1 = Scope()
    nbuf = dict(sq=S1.sb("sq", [128, 8, 512], BF16), rstd=S1.sb("rstd", [128, 512]), tmp=S1.sb("ntmp", [128, 8, 512]), r1=S1.sb("r1", [128, 512]), r2=S1.sb("r2", [128, 512]),
                xtm=[S1.sb("xtm", [128, D]) for _ in range(2)], i=0)
    xT_ts = [S1.sb("xT_t", [128, 8, 512]) for _ in range(2)]
    for ti, (c0, T, C, NCH, is_s) in enumerate(tiles):
        xT_t = xT_ts[ti % 2]
        load_xT(xT_t, c0, T, ti)
        so = max(c0 - SEQ, 0)
        norm_to(hT[:, :, c0:c0 + T], xT_t, T, g1p, sh1p, g1s[:, :, so:so + min(T, 16)], sh1s[:, :, so:so + min(T, 16)], is_s, ti)
    S1.close()
    P.barrier(dummy)

    W = lambda name, shape, dt=F32: S2.sb(name, shape, dt)
    wbH = W("wbH", [128, 8, 640], BF16); wbR = W("wbR", [128, 8, 512], BF16)
    wl = wbH[:, :, 0:288]
    w2b = W("w2b", [128, D], BF16)
    g2b = W("g2b", [128, D], BF16); g2c = W("g2c", [32, D], BF16)
    TW = W("TW", [128, NT], BF16); SG1 = W("SG1", [128, NT], BF16); SG2 = W("SG2", [32, NT], BF16)
    srow = W("srow", [17, 512]); ssT = W("ssT", [128, 27, 16]); lastc = W("lastc", [128, 27, 17])
    prevs = W("prevs", [128, 8])
    dsh = W("dsh", [128, 256])
    S0all = W("S0all", [128, 4, 128]); S0bf = W("S0bf", [128, 4, 128], BF16); Souts = S0all
    Sh = W("Sh", [128, 128]); Shbf = W("Shbf", [128, 128], BF16)
    R0in = W("R0in", [128, 4, 128]); R0blk = W("R0blk", [128, 4, 128]); R0bf = W("R0bf", [128, 4, 128], BF16); Routs = R0blk
    Sr = W("Sr", [128, 128]); Srbf = W("Srbf", [128, 128], BF16); Srt = W("Srt", [128, 128])
    f32t = {n: W(n, [128, 256]) for n in ("th", "fS", "kS", "Pc", "rP", "Ga", "t1", "t2", "Gmm", "kap", "kmod", "bS", "bon", "gS", "Gb",
                                          "xr", "xk", "xv", "Gm", "rG", "aS", "u1", "u2")}
    f32t["n1"] = f32t["bon"]; f32t["n2"] = f32t["u1"]
    oTs = [W("oT", [128, 256]) for _ in range(2)]
    rstH = W("rstH", [128, 256]); rstR = W("rstR", [128, 256]); memset(rstH, 0.0); memset(rstR, 0.0)
    bft = {n: W(n, [128, 256], BF16) for n in ("qt", "kt", "kh", "vT", "sqb", "kpt", "bt", "bh", "rt", "ktl", "khr", "vTr")}
    KR = W("KR", [128, 4, 128], BF16)
    tm = {n: W(n, [64, 4, 128], BF16) for n in ("Vtm", "khtm", "Ktm", "bhtm", "khrtm", "W1", "Un", "Ktp", "onb", "Vtr", "onr")}
    msk = {n: W(n, [64, 4, 128], BF16) for n in ("PbT", "PkT")}
    Abm = {n: W(n, [64, 4, 128], BF16) for n in ("Y", "Z", "Y2", "Z2", "TTa", "TTb")}
    attnT = W("attnT", [64, 4, 64], BF16)
    Osb = W("Osb", [64, 4, 128]); Osq = W("Osq", [64, 4, 128]); st_ = W("st_", [64, 16]); stA = W("stA", [64, 16]); stB = W("stB", [64, 16])
    Osb2 = W("Osb2", [64, 4, 128]); Osq2 = W("Osq2", [64, 4, 128]); st3 = W("st3", [64, 16]); st4 = W("st4", [64, 16]); stC = W("stC", [64, 16]); stD = W("stD", [64, 16])
    Mblk = W("Mblk", [128, 4, 128], BF16); Ncomb = W("Ncomb", [128, 4, 128]); RpT = W("RpT", [128, 256], BF16)

    dma("pool", w2b[0:64, :], w2, w=[w2b]); dma("pool", w2b[64:128, :], a2, w=[w2b])
    dma("pool", g2b, g2[0:128, :]); dma("pool", g2c, g2[128:160, :])
    dma("pool", wl, w_in[:, 7168:7456].rearrange("(k p) c -> p k c", p=128), r=[w_in], w=[wl])
    for p4 in range(0, 27, 4):
        n4 = min(4, 27 - p4); wdt = min(n4 * 128, RC - p4 * 128)
        memset(srow, 0.0, eng="dve")
        dma("sp", srow[0:16, 0:wdt], ss_in[:, p4 * 128:p4 * 128 + wdt], r=[ss_in], w=[srow])
        b_ = bank()
        for q in range(n4):
            tr(b_[:, q * 16:(q + 1) * 16], srow[0:16, q * 128:(q + 1) * 128], identF[0:16, 0:16], r=[srow, identF], w=[b_])
        cp(ssT[:, p4:p4 + n4, :], b_[:, 0:n4 * 16].rearrange("p (q s) -> p q s", s=16), r=[b_], w=[ssT])
    memset(prevs, 0.0); memset(lastc, 0.0)

    def proj(wt, c_lo, c_hi, ti, c0, T):
        b_ = bank(); n = c_hi - c_lo
        for k in range(8):
            mm(b_[0:n, 0:T], wt[:, k, c_lo:c_hi], hT[:, k, c0:c0 + T], k == 0, k == 7, r=[wt, (hT, ti)], w=[b_])
        return b_

    def shift_mix(dst, ps, n, T, is_s, mucol, pcol, part, last_tile, s0=0, omucol=None):
        aff(dsh[0:n, 0:T], ps[0:n, 0:T], omucol[0:n, :], 0.0, r=[ps, pv2], w=[dsh])
        if not is_s:
            stt(dst[0:n, 1:T], ps[0:n, 0:T - 1], mucol[0:n, :], dsh[0:n, 1:T], ALU.mult, ALU.add, r=[ps, dsh, pv], w=[dst])
            stt(dst[0:n, 0:1], prevs[0:n, pcol:pcol + 1], mucol[0:n, :], dsh[0:n, 0:1], ALU.mult, ALU.add, r=[prevs, dsh, pv], w=[dst])
            cp(prevs[0:n, pcol:pcol + 1], ps[0:n, T - 1:T], eng="act", r=[ps, dst], w=[prevs])
            if last_tile:
                cp(lastc[0:n, part, 0:1], ps[0:n, T - 1:T], eng="act", r=[ps], w=[lastc])
        else:
            p3 = ps[0:n, 0:T].rearrange("p (s t) -> p s t", t=4); d3 = dsh[0:n, 0:T].rearrange("p (s t) -> p s t", t=4)
            o3 = dst[0:n, 0:T].rearrange("p (s t) -> p s t", t=4)
            stt(o3[:, :, 1:4], p3[:, :, 0:3], mucol[0:n, :], d3[:, :, 1:4], ALU.mult, ALU.add, r=[ps, dsh, pv], w=[dst])
            stt(o3[:, :, 0], ssT[0:n, part, s0:s0 + T // 4], mucol[0:n, :], d3[:, :, 0], ALU.mult, ALU.add, r=[ssT, dsh, pv], w=[dst])
            cp(lastc[0:n, part, 1 + s0:1 + s0 + T // 4], p3[:, :, 3], eng="act", r=[ps], w=[lastc])

    def sigm(dst, src, r=(), w=(), bias=0.0):
        act(dst, src, AF.Tanh, bias=bias, scale=0.5, r=r, w=w)
        aff(dst, dst, 0.5, 0.5)

    for ti, (c0, T, C, NCH, is_s) in enumerate(tiles):
        last = (c0 + T == SEQ)
        b_ = proj(wl, 0, 128, ti, c0, T)
        shift_mix(f32t["t1"], b_, 128, T, is_s, col("mu", 24), 0, 24, last, max(c0 - SEQ, 0) // 4, omucol=col2("mu", 24))
        act(TW[0:64, c0:c0 + T], f32t["t1"][0:64, 0:T], AF.Tanh, r=[f32t["t1"]], w=[(TW, ti)])
        cp(TW[64:128, c0:c0 + T], f32t["t1"][64:128, 0:T], eng="act", r=[f32t["t1"]], w=[(TW, ti)])
        b_ = proj(wl, 128, 256, ti, c0, T)
        shift_mix(f32t["t1"], b_, 128, T, is_s, col("mu", 25), 1, 25, last, max(c0 - SEQ, 0) // 4, omucol=col2("mu", 25))
        sigm(f32t["t2"][:, 0:T], f32t["t1"][:, 0:T], r=[f32t["t1"]], w=[f32t["t2"]])
        cp(SG1[:, c0:c0 + T], f32t["t2"][:, 0:T], r=[f32t["t2"]], w=[(SG1, ti)])
        b_ = proj(wl, 256, 288, ti, c0, T)
        shift_mix(f32t["t1"], b_, 32, T, is_s, col("mu2"), 2, 26, last, max(c0 - SEQ, 0) // 4, omucol=col2("mu2"))
        sigm(f32t["t2"][0:32, 0:T], f32t["t1"][0:32, 0:T], r=[f32t["t1"]], w=[f32t["t2"]])
        cp(SG2[:, c0:c0 + T], f32t["t2"][0:32, 0:T], r=[f32t["t2"]], w=[(SG2, ti)])

    if debug is not None and debug.get("_stop") == "lora":
        dump("TW", TW, [128, NT]); dump("SG1", SG1, [128, NT]); dump("SG2", SG2, [32, NT]); dump("lastc", lastc.rearrange("p a b -> p (a b)"), [128, 27 * 17])
        n = P.emit()
        return nc, dbg, n

    def chunk3(ap, C):
        return ap.rearrange("p (c t) -> p c t", t=C)

    def to_tm(dst, src_bf, T, C, NCH):
        for c4 in range(0, NCH, 8):
            b_ = bank(); bb = b_.bitcast(BF16)
            for c in range(c4, min(c4 + 8, NCH)):
                tr(bb[0:C, (c - c4) * 128:(c - c4 + 1) * 128], src_bf[:, c * C:(c + 1) * C], identB, r=[src_bf, identB], w=[b_])
            n8 = min(8, NCH - c4)
            cp(dst[0:C, c4:c4 + n8, :], bb[0:C, 0:n8 * 128].rearrange("p (c f) -> p c f", f=128), eng="act", r=[b_], w=[dst])

    def cumprod(dst, src, T, C, NCH, rst):
        cp(chunk3(rst[:, 0:T], C)[:, :, 0:1], chunk3(src[:, 0:T], C)[:, :, 0:1], eng="act", r=[src], w=[rst])
        P.add("dve", lambda e: e.tensor_tensor_scan(out=dst[:, 0:T], data0=src[:, 0:T], data1=rst[:, 0:T], initial=1.0,
                                                    op0=ALU.mult, op1=ALU.max), r=[src, rst], w=[dst])
        if C != 64:
            memset(chunk3(rst[:, 0:T], C)[:, :, 0:1], 0.0, eng="pool", w=[rst])

    def lastbc(ap, T, C, NCH):
        return chunk3(ap[:, 0:T], C)[:, :, C - 1:C].broadcast_to([128, NCH, C])


    def rwkv_block(j, ti, c0, T, C, NCH, is_s, last, wb, oT, gi):
        F_ = f32t; B_ = bft; M_ = masks[C]; jc = slice(j * 128, (j + 1) * 128)
        L = 6 if C == 64 else 2
        if is_s:
            s0 = (c0 - SEQ) // 4
            memset(R0in, 0.0)
            dma("sp", R0in[0:64, :, 0:64], sr_in[s0:s0 + 4, 2 * j].rearrange("s i j -> i s j"), r=[sr_in], w=[R0in])
            dma("sp", R0in[64:128, :, 64:128], sr_in[s0:s0 + 4, 2 * j + 1].rearrange("s i j -> i s j"), r=[sr_in], w=[R0in])
            b_ = bank()
            for q in range(4):
                tr(b_[:, q * 128:(q + 1) * 128], R0in[:, q, :], identF, w=[b_])
            cp(R0blk[:, 0:4, :], b_.rearrange("p (q f) -> p q f", f=128), r=[b_], w=[R0blk])
            cp(R0bf, R0blk, eng="pool")
        br = proj(wb, 0, 128, ti, c0, T); shift_mix(F_["xr"], br, 128, T, is_s, col("mu", j), 3, j, last, max(c0 - SEQ, 0) // 4, omucol=col2("mu", j))
        bk_ = proj(wb, 128, 256, ti, c0, T); shift_mix(F_["xk"], bk_, 128, T, is_s, col("mu", 8 + j), 4, 8 + j, last, max(c0 - SEQ, 0) // 4, omucol=col2("mu", 8 + j))
        bv = proj(wb, 256, 384, ti, c0, T); shift_mix(F_["xv"], bv, 128, T, is_s, col("mu", 16 + j), 5, 16 + j, last, max(c0 - SEQ, 0) // 4, omucol=col2("mu", 16 + j))
        bgb = proj(wb, 384, 512, ti, c0, T)
        sigm(F_["Gb"][:, 0:T], bgb[:, 0:T], r=[bgb], w=[F_["Gb"]])
        bw = bank(); mm(bw[:, 0:T], w2b[0:64, jc], TW[0:64, c0:c0 + T], True, True, r=[w2b, (TW, ti)], w=[bw])
        ba = bank(); mm(ba[:, 0:T], w2b[64:128, jc], TW[64:128, c0:c0 + T], True, True, r=[w2b, (TW, ti)], w=[ba])
        bg = bank(); mm(bg[:, 0:T], g2b[:, jc], SG1[:, c0:c0 + T], True, False, r=[g2b, (SG1, ti)], w=[bg])
        mm(bg[:, 0:T], g2c[:, jc], SG2[:, c0:c0 + T], False, True, r=[g2c, (SG2, ti)], w=[bg])
        X = lambda n: F_[n][:, 0:T]
        Bx = lambda n: B_[n][:, 0:T]
        act(X("u1"), bw[:, 0:T], AF.Tanh, bias=hw0[:, j:j + 1], scale=0.5, r=[bw], w=[F_["u1"]])
        act(X("u1"), X("u1"), AF.Exp, bias=C1, scale=C1)
        cumprod(F_["Gm"], F_["u1"], T, C, NCH, rstR)
        P.add("dve", lambda e: e.reciprocal(out=X("rG"), in_=X("Gm")), r=[F_["Gm"]], w=[F_["rG"]])
        cp(chunk3(X("Gmm"), C)[:, :, 1:C], chunk3(X("Gm"), C)[:, :, 0:C - 1], r=[F_["Gm"]], w=[F_["Gmm"]])
        memset(chunk3(X("Gmm"), C)[:, :, 0:1], 1.0, eng="dve", w=[F_["Gmm"]])
        sigm(X("aS"), ba[:, 0:T], r=[ba], w=[F_["aS"]], bias=ha0[:, j:j + 1])
        cp(X("gS"), bg[:, 0:T], eng="act", r=[bg], w=[F_["gS"]])
        tt(X("Gb"), X("Gb"), X("gS"), ALU.mult, eng="pool")
        aff(X("kap"), X("xk"), col("kk", j), 0.0, r=[F_["xk"], pv], w=[F_["kap"]])
        tt(Bx("sqb"), X("kap"), X("kap"), ALU.mult, eng="pool")
        bn = bank(); mm(bn[:, 0:T], blkones, Bx("sqb"), True, True, w=[bn])
        ts(X("n1"), bn[:, 0:T], 1.0, 1e-24, ALU.mult, ALU.add, r=[bn], w=[F_["n1"]])
        rsqrt_dve(X("u2"), X("n1"), X("n2"), [F_["u2"], F_["n1"], F_["n2"]])
        tt(X("kap"), X("kap"), X("u2"), ALU.mult)
        aff(X("u2"), X("aS"), col("ka", j), omka[:, j:j + 1], r=[F_["aS"], pv, omka], w=[F_["u2"]])
        tt(X("kmod"), X("xk"), X("u2"), ALU.mult, eng="pool")
        tt(X("bS"), X("aS"), X("kap"), ALU.mult)
        stt(Bx("sqb"), X("xr"), col("rk", j), X("kmod"), ALU.mult, ALU.mult, r=[F_["xr"], F_["kmod"], pv], w=[B_["sqb"]])
        bn2 = bank(); mm(bn2[:, 0:T], blkones, Bx("sqb"), True, True, w=[bn2])
        tt(X("bon"), bn2[:, 0:T], X("xv"), ALU.mult, r=[bn2, F_["xv"]], w=[F_["bon"]])
        tt(Bx("kpt"), X("kap"), X("Gmm"), ALU.mult)
        tt(Bx("rt"), X("xr"), X("Gm"), ALU.mult, eng="pool")
        cp(KR[:, 0:NCH, 0:C], chunk3(Bx("kpt"), C), eng="pool", r=[B_["kpt"]], w=[KR])
        cp(KR[:, 0:NCH, 64:64 + C], chunk3(Bx("rt"), C), eng="pool", r=[B_["rt"]], w=[KR])
        tt(X("u2"), X("kmod"), X("rG"), ALU.mult); cp(Bx("ktl"), X("u2"), eng="act")
        tt(chunk3(Bx("khr"), C), chunk3(X("u2"), C), lastbc(F_["Gm"], T, C, NCH), ALU.mult, r=[F_["u2"], F_["Gm"]], w=[B_["khr"]])
        tt(X("u2"), X("bS"), X("rG"), ALU.mult); cp(Bx("bt"), X("u2"), eng="act")
        tt(chunk3(Bx("bh"), C), chunk3(X("u2"), C), lastbc(F_["Gm"], T, C, NCH), ALU.mult, r=[F_["u2"], F_["Gm"]], w=[B_["bh"]])
        cp(Bx("vTr"), X("xv"), eng="act")
        to_tm(tm["Vtr"], B_["vTr"], T, C, NCH); to_tm(tm["Ktm"], B_["kpt"], T, C, NCH)
        to_tm(tm["bhtm"], B_["bh"], T, C, NCH); to_tm(tm["khrtm"], B_["khr"], T, C, NCH)
        if debug is not None and debug.get('rstop', 99) <= 1:
            return
        hv = lambda t_, c_lo, c_hi: t_[0:C, c_lo:c_hi, :].rearrange("p c (h t) -> p c h t", t=64)[:, :, :, 0:C]
        for c4 in range(0, NCH, 4):
            for h in range(2):
                hs_ = slice(64 * h, 64 * h + 64)
                b1 = bank(); b2 = bank(); b3 = bank()
                for q in range(4):
                    c = c4 + q; csl = slice(c * C, (c + 1) * C); o = q * 128
                    mm(b1[0:C, o:o + 128], B_["bt"][hs_, csl], KR[hs_, c, :], True, True, r=[B_["bt"], KR], w=[b1])
                    mm(b2[0:C, o:o + 128], B_["ktl"][hs_, csl], KR[hs_, c, :], True, True, r=[B_["ktl"], KR], w=[b2])
                    mm(b3[0:C, o:o + C], B_["kpt"][hs_, csl], B_["bt"][hs_, csl], True, True, r=[B_["kpt"], B_["bt"]], w=[b3])
                v4_ = lambda b: b[0:C, :].rearrange("p (c x t) -> p c x t", x=2, t=64)
                mk = lambda m_: m_.unsqueeze(1).broadcast_to([C, 4, C])
                dv = lambda t_: t_[0:C, c4:c4 + 4, h * 64:h * 64 + C]
                tt(dv(Abm["Z"]), v4_(b1)[:, :, 0, 0:C], mk(M_["nstrict"]), ALU.mult, r=[b1, M_["nstrict"]], w=[Abm["Z"]])
                tt(dv(msk["PbT"]), v4_(b1)[:, :, 1, 0:C], mk(M_["incl"]), ALU.mult, r=[b1, M_["incl"]], w=[msk["PbT"]])
                tt(dv(tm["W1"]), v4_(b2)[:, :, 0, 0:C], mk(M_["strict"]), ALU.mult, r=[b2, M_["strict"]], w=[tm["W1"]])
                tt(dv(msk["PkT"]), v4_(b2)[:, :, 1, 0:C], mk(M_["incl"]), ALU.mult, r=[b2, M_["incl"]], w=[msk["PkT"]])
                tt(dv(Abm["Y"]), v4_(b3)[:, :, 0, 0:C], mk(M_["nlstrict"]), ALU.mult, r=[b3, M_["nlstrict"]], w=[Abm["Y"]])
        if debug is not None and debug.get('rstop', 99) <= 2:
            return
        tt(hv(Abm["TTa"], 0, NCH), hv(Abm["Z"], 0, NCH), identF[0:C, 0:C].unsqueeze(1).unsqueeze(1).broadcast_to([C, NCH, 2, C]), ALU.add,
           r=[Abm["Z"], identF], w=[Abm["TTa"]])
        Yc, Zc, Yn, Zn, TTc, TTn = Abm["Y"], Abm["Z"], Abm["Y2"], Abm["Z2"], Abm["TTa"], Abm["TTb"]
        pr = [(c, h) for c in range(NCH) for h in range(2)]

        def batch_mm(dst, lhs, rhs, wid_l, wid_r, post):
            for p8 in range(0, len(pr), 8):
                b_ = bank(); grp = pr[p8:p8 + 8]
                for q, (c, h) in enumerate(grp):
                    mm(b_[0:C, q * 64:q * 64 + wid_r], lhs[0:C, c, h * 64:h * 64 + wid_l], rhs[0:C, c, h * 64:h * 64 + wid_r], True, True,
                       r=[lhs, rhs], w=[b_])
                cA = grp[0][0]; cB = grp[-1][0] + 1
                src = b_[0:C, 0:(cB - cA) * 128].rearrange("p (c h t) -> p c h t", h=2, t=64)[:, :, :, 0:wid_r]
                d = dst[0:C, cA:cB, :].rearrange("p c (h t) -> p c h t", t=64)[:, :, :, 0:wid_r]
                post(d, src, b_)

        cpy = lambda d, s_, b_: cp(d, s_, eng="act", r=[b_], w=[d])
        for n in range(1, L):
            batch_mm(Yn, Zc, Yc, C, C, cpy)
            if n < L - 1:
                batch_mm(Zn, Yc, Zc, C, C, cpy)
            addT = lambda d, s_, b_, TTc=TTc: tt(d, s_, TTc[0:C, 0:NCH, :].rearrange("p c (h t) -> p c h t", t=64)[:, :, :, 0:C][:, (d.offset - d.offset):, :, :] if False else s_, s_, ALU.add) if False else None
            for p8 in range(0, len(pr), 8):
                b_ = bank(); grp = pr[p8:p8 + 8]
                for q, (c, h) in enumerate(grp):
                    mm(b_[0:C, q * 64:q * 64 + C], Yn[0:C, c, h * 64:h * 64 + C], TTc[0:C, c, h * 64:h * 64 + C], True, True, r=[Yn, TTc], w=[b_])
                cA = grp[0][0]; cB = grp[-1][0] + 1
                src = b_[0:C, 0:(cB - cA) * 128].rearrange("p (c h t) -> p c h t", h=2, t=64)[:, :, :, 0:C]
                tt(hv(TTn, cA, cB), src, hv(TTc, cA, cB), ALU.add, r=[b_, TTc], w=[TTn])
            Yc, Yn = Yn, Yc; Zc, Zn = Zn, Zc; TTc, TTn = TTn, TTc
        if debug is not None and debug.get('rstop', 99) <= 3:
            return
        TTf = TTc
        AkT = Yn
        cp(AkT[0:C, 0:NCH, :], tm["W1"][0:C, 0:NCH, :], eng="pool", r=[tm["W1"]], w=[AkT])
        batch_mm(tm["W1"], AkT, tm["Vtr"], C, 64, cpy)
        neg = lambda d, s_, b_: ts(d, s_, -1.0, None, ALU.mult, r=[b_], w=[d])
        batch_mm(tm["Un"], TTf, tm["W1"], C, 64, neg)
        batch_mm(tm["Ktp"], TTf, tm["Ktm"], C, 64, cpy)
        if debug is not None and debug.get('rstop', 99) <= 4:
            return
        for c4 in range(0, NCH, 4):
            bM = bank(); bN = bank(); bR = bank()
            for q in range(4):
                c = c4 + q
                mm(bM[:, q * 128:(q + 1) * 128], tm["Ktp"][0:C, c, :], tm["bhtm"][0:C, c, :], True, True, r=[tm["Ktp"], tm["bhtm"]], w=[bM])
                mm(bN[:, q * 128:(q + 1) * 128], tm["khrtm"][0:C, c, :], tm["Vtr"][0:C, c, :], True, False, r=[tm["khrtm"], tm["Vtr"]], w=[bN])
                mm(bN[:, q * 128:(q + 1) * 128], tm["bhtm"][0:C, c, :], tm["Un"][0:C, c, :], False, True, r=[tm["bhtm"], tm["Un"]], w=[bN])
                mm(bR[:, q * 128:(q + 1) * 128], tm["Ktp"][0:C, c, :], msk["PbT"][0:C, c, :], True, True, r=[tm["Ktp"], msk["PbT"]], w=[bR])
            v3 = lambda b: b.rearrange("p (c f) -> p c f", f=128)
            tt(Mblk[:, c4:c4 + 4, :], v3(bM), blkneg.unsqueeze(1).broadcast_to([128, 4, 128]), ALU.mult, r=[bM, blkneg], w=[Mblk])
            tt(Ncomb[:, c4:c4 + 4, :], v3(bN), blkpos.unsqueeze(1).broadcast_to([128, 4, 128]), ALU.mult, r=[bN, blkpos], w=[Ncomb])
            for h in range(2):
                hs_ = slice(64 * h, 64 * h + 64)
                tt(chunk3(RpT[hs_, c4 * C:(c4 + 4) * C], C), chunk3(B_["rt"][hs_, c4 * C:(c4 + 4) * C], C),
                   v3(bR)[hs_, :, h * 64:h * 64 + C], ALU.subtract, r=[B_["rt"], bR], w=[RpT])
        if debug is not None and debug.get('rstop', 99) <= 5:
            return
        for c in range(NCH):
            csl = slice(c * C, (c + 1) * C)
            sbf = R0bf[:, c, :] if is_s else Srbf
            sin = R0blk[:, c, :] if is_s else Sr
            sout = Routs[:, c, :] if is_s else Sr
            kS_ = [R0bf] if is_s else [Srbf]
            if c % 4 == 0:
                bo = bank()
            oo = bo[0:C, (c % 4) * 128:(c % 4 + 1) * 128]
            mm(oo, RpT[:, csl], sbf, True, False, r=[RpT] + kS_, w=[bo])
            for h in range(2):
                hb = slice(h * 64, (h + 1) * 64)
                mm(oo[:, hb], msk["PkT"][0:C, c, h * 64:h * 64 + C], tm["Vtr"][0:C, c, hb], False, False, r=[msk["PkT"], tm["Vtr"]], w=[bo])
                mm(oo[:, hb], msk["PbT"][0:C, c, h * 64:h * 64 + C], tm["Un"][0:C, c, hb], False, h == 1, r=[msk["PbT"], tm["Un"]], w=[bo])
            b3 = bank()
            mm(b3[:, 0:128], Mblk[:, c, :], sbf, True, True, r=[Mblk] + kS_, w=[b3])
            stt(Srt, sin, F_["Gm"][:, (c + 1) * C - 1:(c + 1) * C], Ncomb[:, c, :], ALU.mult, ALU.add,
                r=[R0blk if is_s else Sr, F_["Gm"], Ncomb], w=[Srt])
            tt(sout, Srt, b3[:, 0:128], ALU.add, r=[Srt, b3], w=[Routs if is_s else Sr])
            if not is_s:
                cp(Srbf, Sr, eng="act")
            if c % 4 == 3:
                cp(Osb2[0:C, c - 3:c + 1, :], bo.rearrange("p (c f) -> p c f", f=128)[0:C], eng="act", r=[bo], w=[Osb2])
        if debug is not None and debug.get('rstop', 99) <= 6:
            return
        if is_s:
            s0 = (c0 - SEQ) // 4
            for s4 in range(0, 4, 4):
                b_ = bank()
                for q in range(4):
                    tr(b_[:, q * 128:(q + 1) * 128], Routs[:, s4 + q, :], identF, w=[b_])
                cp(R0in[:, s4:s4 + 4, :], b_.rearrange("p (q f) -> p q f", f=128), r=[b_], w=[R0in])
            dma("sp", rs[s0:s0 + 4, 2 * j].rearrange("s i j -> i s j"), R0in[0:64, :, 0:64], r=[R0in], w=[rs])
            dma("sp", rs[s0:s0 + 4, 2 * j + 1].rearrange("s i j -> i s j"), R0in[64:128, :, 64:128], r=[R0in], w=[rs])
        elif last:
            b_ = bank(); tr(b_[:, 0:128], Sr, identF, w=[b_]); cp(Srt, b_[:, 0:128])
            dma("sp", rp[2 * j], Srt[0:64, 0:64], r=[Srt], w=[rp]); dma("sp", rp[2 * j + 1], Srt[64:128, 64:128], r=[Srt], w=[rp])
        if debug is not None and debug.get('rstop', 99) <= 7:
            return
        G2 = NCH * 2
        og = Osb2[0:C, 0:NCH, :].rearrange("p c (h i) -> p (c h) i", i=64)
        oq = Osq2[0:C, 0:NCH, :].rearrange("p c (h i) -> p (c h) i", i=64)
        P.add("dve", lambda e: e.reduce_sum(out=st3[0:C, 0:G2], in_=og, axis=AX.X), r=[Osb2], w=[st3])
        ts(st3[0:C, 0:G2], st3[0:C, 0:G2], 1.0 / 64, None, ALU.mult)
        tt(og, og, st3[0:C, 0:G2].unsqueeze(2).broadcast_to([C, G2, 64]), ALU.subtract, r=[Osb2, st3], w=[Osb2])
        tt(oq, og, og, ALU.mult, eng="pool", r=[Osb2], w=[Osq2])
        P.add("dve", lambda e: e.reduce_sum(out=st4[0:C, 0:G2], in_=oq, axis=AX.X), r=[Osq2], w=[st4])
        rsqrt_(st4[0:C, 0:G2], st4[0:C, 0:G2], 1.0 / 64, LNX_EPS, stC[0:C, 0:G2], stD[0:C, 0:G2])
        tt(tm["onr"][0:C, 0:NCH, :].rearrange("p c (h i) -> p (c h) i", i=64), og, st4[0:C, 0:G2].unsqueeze(2).broadcast_to([C, G2, 64]), ALU.mult,
           r=[Osb2, st4], w=[tm["onr"]])
        b_ = bank(); bb = b_.bitcast(BF16)
        for c in range(NCH):
            tr(bb[:, c * C:(c + 1) * C], tm["onr"][0:C, c, :], identB[0:C, 0:C], r=[tm["onr"], identB], w=[b_])
        ts(X("u2"), bb[:, 0:T], col("lw", j), col("lbb", j), ALU.mult, ALU.add, r=[b_, pv], w=[F_["u2"]])
        tt(X("u2"), X("u2"), X("bon"), ALU.add, eng="pool")
        tt(X("u2"), X("u2"), X("Gb"), ALU.mult, eng="pool")
        P.add("need", ("oTw", gi))
        tt(mT[:, j, c0:c0 + T], X("u2"), oT[:, 0:T], ALU.add, r=[F_["u2"], oT], w=[(mT, ti)])
        P.add("mark", ("oTr", gi))

    NBLK = 8 if debug is None else debug.get('nblk', 8)

    def hgrn_tile(j, ti, c0, T, C, NCH, is_s, last, wb, oT, gi):
        M_ = masks[C]; F_ = f32t; B_ = bft
        if is_s:
            s0 = (c0 - SEQ) // 4
            dma("sp", S0all, sh_in[s0:s0 + 4, j].rearrange("s k v -> k s v"), r=[sh_in], w=[S0all])
            cp(S0bf, S0all, eng="pool")
        bf_ = proj(wb, 128, 256, ti, c0, T)
        act(F_["th"][:, 0:T], bf_[:, 0:T], AF.Tanh, scale=0.5, r=[bf_], w=[F_["th"]])
        aff(F_["fS"][:, 0:T], F_["th"][:, 0:T], hs1[:, j:j + 1], hs2[:, j:j + 1], r=[F_["th"], hs1, hs2], w=[F_["fS"]])
        aff(F_["kS"][:, 0:T], F_["th"][:, 0:T], hs1n[:, j:j + 1], hs2c[:, j:j + 1], r=[F_["th"], hs1n, hs2c], w=[F_["kS"]])
        cumprod(F_["Pc"], F_["fS"], T, C, NCH, rstH)
        P.add("dve", lambda e, T=T: e.reciprocal(out=F_["rP"][:, 0:T], in_=F_["Pc"][:, 0:T]), r=[F_["Pc"]], w=[F_["rP"]])
        bq = proj(wb, 0, 128, ti, c0, T)
        tt(B_["qt"][:, 0:T], bq[:, 0:T], F_["Pc"][:, 0:T], ALU.mult, r=[bq, F_["Pc"]], w=[B_["qt"]])
        tt(F_["t1"][:, 0:T], F_["kS"][:, 0:T], F_["rP"][:, 0:T], ALU.mult, eng="pool", r=[F_["kS"], F_["rP"]], w=[F_["t1"]])
        cp(B_["kt"][:, 0:T], F_["t1"][:, 0:T], eng="act", r=[F_["t1"]], w=[B_["kt"]])
        tt(chunk3(B_["kh"][:, 0:T], C), chunk3(F_["t1"][:, 0:T], C), lastbc(F_["Pc"], T, C, NCH), ALU.mult, r=[F_["t1"], F_["Pc"]], w=[B_["kh"]])
        bi = proj(wb, 256, 384, ti, c0, T)
        cp(B_["vT"][:, 0:T], bi[:, 0:T], eng="act", r=[bi], w=[B_["vT"]])
        bog = proj(wb, 384, 512, ti, c0, T)
        sigm(F_["t2"][:, 0:T], bog[:, 0:T], r=[bog], w=[F_["t2"]])
        tt(F_["t2"][:, 0:T], bog[:, 0:T], F_["t2"][:, 0:T], ALU.mult, r=[bog, F_["t2"]], w=[F_["t2"]])
        bga = proj(wb, 512, 640, ti, c0, T)
        sigm(F_["Ga"][:, 0:T], bga[:, 0:T], r=[bga], w=[F_["Ga"]])
        tt(F_["Ga"][:, 0:T], F_["Ga"][:, 0:T], F_["t2"][:, 0:T], ALU.mult, eng="pool", r=[F_["Ga"], F_["t2"]], w=[F_["Ga"]])
        to_tm(tm["Vtm"], B_["vT"], T, C, NCH); to_tm(tm["khtm"], B_["kh"], T, C, NCH)
        b_ = bank()
        for c in range(NCH):
            mm(b_[0:C, c * C:(c + 1) * C], B_["kt"][:, c * C:(c + 1) * C], B_["qt"][:, c * C:(c + 1) * C], True, True, w=[b_])
        tt(attnT[0:C, 0:NCH, 0:C], b_[0:C, 0:NCH * C].rearrange("p (c t) -> p c t", t=C),
           M_["incl"].unsqueeze(1).broadcast_to([C, NCH, C]), ALU.mult, r=[b_, M_["incl"]], w=[attnT])
        bo = banks[0]
        for c in range(NCH):
            csl = slice(c * C, (c + 1) * C)
            sbf = S0bf[:, c, :] if is_s else Shbf
            sin = S0all[:, c, :] if is_s else Sh
            sout = Souts[:, c, :] if is_s else Sh
            oo = bo[0:C, c * 128:(c + 1) * 128]
            mm(oo, attnT[0:C, c, 0:C], tm["Vtm"][0:C, c, :], True, False, r=[attnT, tm["Vtm"]], w=[bo])
            mm(oo, B_["qt"][:, csl], sbf, False, True, r=[B_["qt"], S0bf if is_s else Shbf], w=[bo])
            b2 = bank()
            mm(b2[:, 0:128], tm["khtm"][0:C, c, :], tm["Vtm"][0:C, c, :], True, True, r=[tm["khtm"], tm["Vtm"]], w=[b2])
            stt(sout, sin, F_["Pc"][:, (c + 1) * C - 1:(c + 1) * C], b2[:, 0:128], ALU.mult, ALU.add,
                r=[S0all if is_s else Sh, F_["Pc"], b2], w=[Souts if is_s else Sh])
            if not is_s:
                cp(Shbf, Sh, eng="act")
        cp(Osb[0:C, 0:NCH, :], bo.rearrange("p (c f) -> p c f", f=128)[0:C], eng="act", r=[bo], w=[Osb])
        if is_s:
            dma("sp", hs[s0:s0 + 4, j].rearrange("s k v -> k s v"), Souts, r=[Souts], w=[hs])
        elif last:
            dma("sp", hp[j], Sh, r=[Sh], w=[hp])
        tt(Osq[0:C, 0:NCH, :], Osb[0:C, 0:NCH, :], Osb[0:C, 0:NCH, :], ALU.mult, eng="pool", r=[Osb], w=[Osq])
        P.add("dve", lambda e, C=C, NCH=NCH: e.reduce_sum(out=st_[0:C, 0:NCH], in_=Osq[0:C, 0:NCH, :], axis=AX.X), r=[Osq], w=[st_])
        rsqrt_(st_[0:C, 0:NCH], st_[0:C, 0:NCH], 1.0 / 128, RMS_EPS, stA[0:C, 0:NCH], stB[0:C, 0:NCH])
        tt(Osq[0:C, 0:NCH, :], Osb[0:C, 0:NCH, :], st_[0:C, 0:NCH].unsqueeze(2).broadcast_to([C, NCH, 128]), ALU.mult, r=[Osb, st_], w=[Osq])
        tt(tm["onb"][0:C, 0:NCH, :], Osq[0:C, 0:NCH, :], nwbc[0:C, :].unsqueeze(1).broadcast_to([C, NCH, 128]), ALU.mult, r=[Osq, nwbc], w=[tm["onb"]])
        b_ = bank(); bb = b_.bitcast(BF16)
        for c in range(NCH):
            tr(bb[:, c * C:(c + 1) * C], tm["onb"][0:C, c, :], identB[0:C, 0:C], r=[tm["onb"], identB], w=[b_])
        if gi >= 2:
            P.add("need", ("oTr", gi - 2))
        tt(oT[:, 0:T], bb[:, 0:T], F_["Ga"][:, 0:T], ALU.mult, r=[b_, F_["Ga"]], w=[oT])
        P.add("mark", ("oTw", gi))

    def stream(kind):
        gi = 0
        for j in range(NBLK):
            seg = lambda col0: w_in[:, col0 + j * 128:col0 + (j + 1) * 128].rearrange("(k p) c -> p k c", p=128)
            if kind == "H":
                for si, col0 in enumerate((0, 1024, 2048, 3072, 7456)):
                    dma("pool", wbH[:, :, si * 128:(si + 1) * 128], seg(col0), r=[w_in], w=[wbH])
            else:
                for si, col0 in enumerate((4096, 5120, 6144, 8480)):
                    dma("pool", wbR[:, :, si * 128:(si + 1) * 128], seg(col0), r=[w_in], w=[wbR])
            if kind == "H":
                memset(Sh, 0.0, eng="dve"); memset(Shbf, 0.0, eng="dve")
            else:
                memset(Sr, 0.0, eng="dve"); memset(Srbf, 0.0, eng="dve"); memset(prevs[:, 3:6], 0.0, eng="dve", w=[prevs])
            for ti, (c0, T, C, NCH, is_s) in enumerate(tiles):
                if debug is not None and 'tsel' in debug and ti not in debug['tsel']:
                    continue
                last = (c0 + T == SEQ)
                if kind == "H":
                    hgrn_tile(j, ti, c0, T, C, NCH, is_s, last, wbH, oTs[gi % 2], gi)
                else:
                    rwkv_block(j, ti, c0, T, C, NCH, is_s, last, wbR, oTs[gi % 2], gi)
                gi += 1

    opsH = []; opsR = []
    P.sink = opsH; cur[0] = "H"; stream("H")
    P.sink = opsR; cur[0] = "R"; stream("R")
    P.sink = None; cur[0] = None
    if debug is not None and debug.get("only") == "R":
        opsH = [x for x in opsH if x[0] == "mark"]
    if debug is not None and debug.get("only") == "H":
        opsR = [x for x in opsR if x[0] == "mark"]
    _n0 = len(P.ops)
    P.merge(opsH, opsR, lead=(150 if debug is None else debug.get('lead', 150)))
    if debug is not None and debug.get("xdeps"):
        from collections import Counter
        cx = Counter()
        for i in range(_n0, len(P.ops)):
            o = P.ops[i]
            for d in o["deps"]:
                od = P.ops[d]
                if d >= _n0 and od["stream"] != o["stream"]:
                    shared = (set(o["rk"]) | set(o["wk"])) & (set(od["rk"]) | set(od["wk"]))
                    cx[(o["stream"], o["eng"], od["eng"], str(sorted(map(str, shared)))[:80])] += 1
        for k, v in cx.most_common(30):
            print("XDEP", v, k)
    dump("mT", mT.rearrange("p a b -> p (a b)"), [128, 8 * NT])
    for p4 in range(0, 27, 4):
        b_ = bank(); n4 = min(4, 27 - p4); wdt = min(n4 * 128, RC - p4 * 128)
        for q in range(n4):
            tr(b_[0:17, q * 128:(q + 1) * 128], lastc[:, p4 + q, :], identF, r=[lastc, identF], w=[b_])
        cp(srow[:, 0:n4 * 128], b_[0:17, 0:n4 * 128], r=[b_], w=[srow])
        dma("sp", shp[:, p4 * 128:p4 * 128 + wdt], srow[0:1, 0:wdt], r=[srow], w=[shp])
        dma("sp", shs[:, p4 * 128:p4 * 128 + wdt], srow[1:17, 0:wdt], r=[srow], w=[shs])
    if debug is not None and debug.get("_stop") == 2:
        n = P.emit()
        return nc, dbg, n
    print('SBUF phase2 total', SBTOT[0], 'S2', S2.tot)
    S2.close()
    P.barrier(dummy)
    S1 = Scope()
    xT = S1.sb("xT", [128, 8, NT]); h2T = S1.sb("h2T", [128, 8, NT], BF16)
    nbuf = dict(sq=S1.sb("sq", [128, 8, 512], BF16), rstd=S1.sb("rstd", [128, 512]), tmp=S1.sb("ntmp", [128, 8, 512]),
                xtm=[S1.sb("xtm", [128, D]) for _ in range(2)], i=0)
    nbuf["r1"] = nbuf["tmp"][:, 1, :]; nbuf["r2"] = nbuf["tmp"][:, 0, :]
    wo = S1.sb("wo", [128, 8, 128], BF16); wus = [S1.sb("wu", [128, 8, 256], BF16) for _ in range(2)]; wds = [S1.sb("wd", [128, 2, D], BF16)] * 2
    aTs = [S1.sb("aT", [128, 2, 512], BF16) for _ in range(2)]; rTs = [S1.sb("rT", [128, 1, 512], BF16)] * 2; t64 = S1.sb("t64", [128, 64])
    tiles34 = [(i * 512, 512, False) for i in range(4)] + [(SEQ, 64, True)]
    for ti, (c0, T, is_s) in enumerate(tiles34):
        load_xT(xT[:, :, c0:c0 + T], c0, T, ti)

    def resid_add(f, c0, T, is_s, ps, gtp, gts):
        if not is_s:
            stt(xT[:, f, c0:c0 + T], ps[:, 0:T], gtp[:, f:f + 1], xT[:, f, c0:c0 + T], ALU.mult, ALU.add, r=[ps, gtp, xT], w=[xT])
        else:
            tt(t64[:, 0:T], ps[:, 0:T], gts[:, f, :], ALU.mult, r=[ps, gts], w=[t64])
            tt(xT[:, f, c0:c0 + T], xT[:, f, c0:c0 + T], t64[:, 0:T], ALU.add, r=[xT, t64], w=[xT])

    for f in range(8):
        dma("pool", wo, w_out[:, f * 128:(f + 1) * 128].rearrange("(k p) c -> p k c", p=128), r=[w_out], w=[wo])
        for ti, (c0, T, is_s) in enumerate(tiles34):
            b_ = bank()
            for k in range(8):
                mm(b_[:, 0:T], wo[:, k, :], mT[:, k, c0:c0 + T], k == 0, k == 7, r=[wo, mT], w=[b_])
            resid_add(f, c0, T, is_s, b_, gt1p, gt1s)
    for ti, (c0, T, is_s) in enumerate(tiles34):
        norm_to(h2T[:, :, c0:c0 + T], xT[:, :, c0:c0 + T], T, g2p, sh2p, g2s, sh2s, is_s, 0)
    steps = [(g, ti) for g in range(16) for ti in range(len(tiles34))]
    upb = [0]; dnb = [0]

    def load_wu(g):
        wu_ = wus[g % 2]
        dma("pool", wu_, w_up[:, g * 256:(g + 1) * 256].rearrange("(k p) c -> p k c", p=128), r=[w_up], w=[wu_])

    def load_wd(g):
        wd_ = wds[0]
        dma("pool", wd_, w_down[g * 256:(g + 1) * 256, :].rearrange("(q p) c -> p q c", p=128), r=[w_down], w=[wd_])

    def up(k):
        g, ti = steps[k]; c0, T, is_s = tiles34[ti]
        if ti == 0 and g + 1 < 16:
            load_wu(g + 1)
        wu_ = wus[g % 2]; aT_ = aTs[k % 2]; rT_ = rTs[k % 2]
        for q in range(2):
            b_ = banks[upb[0] % 4]; upb[0] += 1
            for kk_ in range(8):
                mm(b_[:, 0:T], wu_[:, kk_, q * 128:(q + 1) * 128], h2T[:, kk_, c0:c0 + T], kk_ == 0, kk_ == 7, r=[wu_, (h2T, 0)], w=[b_])
            act(rT_[:, 0, 0:T], b_[:, 0:T], AF.Relu, r=[b_], w=[rT_])
            tt(aT_[:, q, 0:T], rT_[:, 0, 0:T], rT_[:, 0, 0:T], ALU.mult, eng=("pool" if q == 1 else "dve"), r=[rT_], w=[aT_])

    def down(k):
        g, ti = steps[k]; c0, T, is_s = tiles34[ti]
        wd_ = wds[g % 2]; aT_ = aTs[k % 2]
        for f in range(8):
            b_ = banks[4 + dnb[0] % 4]; dnb[0] += 1
            for q in range(2):
                mm(b_[:, 0:T], wd_[:, q, f * 128:(f + 1) * 128], aT_[:, q, 0:T], q == 0, q == 1, r=[wd_, aT_], w=[b_])
            resid_add(f, c0, T, is_s, b_, gt2p, gt2s)

    load_wu(0); load_wd(0)
    up(0)
    for k in range(len(steps)):
        if k + 1 < len(steps):
            up(k + 1)
        down(k)
        g_, ti_ = steps[k]
        if ti_ == len(tiles34) - 1 and g_ + 1 < 16:
            load_wd(g_ + 1)
    fnp = pv[:, rows["fn"]:rows["fn"] + 8]
    for ti, (c0, T, is_s) in enumerate(tiles34):
        sq = nbuf["sq"]; rstd = nbuf["rstd"]; tmp = nbuf["tmp"]
        xv_ = xT[:, :, c0:c0 + T]
        tt(sq[:, :, 0:T], xv_, xv_, ALU.mult, r=[xT], w=[sq])
        b_ = bank()
        for fc in range(8):
            mm(b_[:, 0:T], onesB, sq[:, fc, 0:T], fc == 0, fc == 7, w=[b_])
        ts(rstd[:, 0:T], b_[:, 0:T], 1.0 / D, RMS_EPS, ALU.mult, ALU.add, r=[b_], w=[rstd])
        r1 = nbuf["r1"]; r2 = nbuf["r2"]
        cp(r1[:, 0:T], rstd[:, 0:T], eng="act", r=[rstd], w=[r1])
        rsqrt_dve(rstd[:, 0:T], r1[:, 0:T], r2[:, 0:T], [rstd, r1, r2])
        tt(tmp[:, :, 0:T], xv_, rstd[:, 0:T].unsqueeze(1).broadcast_to([128, 8, T]), ALU.mult, r=[xT, rstd], w=[tmp])
        tt(tmp[:, :, 0:T], tmp[:, :, 0:T], fnp.unsqueeze(2).broadcast_to([128, 8, T]), ALU.mult, r=[tmp, pv], w=[tmp])
        for sub in range(0, T, 128):
            n = min(128, T - sub)
            ytm = nbuf["xtm"][nbuf["i"] % 2]; nbuf["i"] += 1
            for half in range(2):
                b_ = bank()
                for f4 in range(4):
                    fc = half * 4 + f4
                    tr(b_[0:n, f4 * 128:(f4 + 1) * 128], tmp[:, fc, sub:sub + n], identF, r=[tmp, identF], w=[b_])
                cp(ytm[0:n, half * 512:(half + 1) * 512], b_[0:n, :], eng="act", r=[b_], w=[ytm])
            dst = yp[c0 + sub:c0 + sub + n, :] if not is_s else ys[sub:sub + n, :]
            dma("sp", dst, ytm[0:n, :], r=[ytm], w=[yp if not is_s else ys])
    n = P.emit()
    print('SBUF bytes/partition: G', G.tot, 'max', SBMAX[0])
    S1.close(); G.close()
    return nc, dbg, n


_CACHE = {}


def _core_inputs(inp, c):
    f = lambda a: np.ascontiguousarray(a, dtype=np.float32)
    m = {"xp": inp["x_prompt"][c], "xs": inp["x_sample"][16 * c:16 * c + 16].reshape(64, D),
         "sh": inp["state_hgrn"][0, 16 * c:16 * c + 16], "sr": inp["state_rwkv"][0, 16 * c:16 * c + 16],
         "ss": inp["state_shift"][0, 16 * c:16 * c + 16], "cp": inp["c_prompt"][c:c + 1], "cs": inp["c_sample"][16 * c:16 * c + 16]}
    for k in ("norm1_w", "norm2_w", "rwkv_w0", "rwkv_a0", "rwkv_k_k", "rwkv_k_a", "rwkv_lnx_w", "rwkv_lnx_b", "rwkv_r_k", "final_norm_w"):
        m[k] = inp[k].reshape(8, 128)
    m["ada_w"] = inp["ada_w"][0]; m["ada_b"] = inp["ada_b"].reshape(48, 128); m["w_in"] = inp["w_in"][0]
    m["lb_logits"] = inp["lb_logits"].reshape(16, 128); m["hgrn_norm_w"] = inp["hgrn_norm_w"].reshape(1, 128)
    m["rwkv_mu"] = inp["rwkv_mu"].reshape(1, RC); m["rwkv_w2"] = inp["rwkv_w2"][0]; m["rwkv_a2"] = inp["rwkv_a2"][0]
    m["rwkv_g2"] = inp["rwkv_g2"][0]; m["w_out"] = inp["w_out"][0]; m["w_up"] = inp["w_up"][0]; m["w_down"] = inp["w_down"][0]
    return {k: f(v) for k, v in m.items()}


def kernel(**inputs):
    inp = {k: np.asarray(v) for k, v in inputs.items()}
    if "nc" not in _CACHE:
        _CACHE["nc"] = build()[0]
    nc = _CACHE["nc"]
    in_maps = [_core_inputs(inp, c) for c in range(8)]
    res = run_bass_kernel_spmd(nc, in_maps, core_ids=list(range(8)))
    R = res.results
    g = lambda n: [np.asarray(R[c][n], dtype=np.float32) for c in range(8)]
    y_prompt = np.stack(g("yp"), 0)
    y_sample = np.concatenate([a.reshape(16, 4, D) for a in g("ys")], 0)
    hgrn_p = np.stack(g("hp"), 0)[None]
    rwkv_p = np.stack(g("rp"), 0)[None]
    shift_p = np.concatenate(g("shp"), 0)[None]
    hgrn_s = np.concatenate(g("hs"), 0)[None]
    rwkv_s = np.concatenate(g("rs"), 0)[None]
    shift_s = np.concatenate(g("shs"), 0)[None]
    return (y_prompt, y_sample, hgrn_p, rwkv_p, shift_p, hgrn_s, rwkv_s, shift_s)
```

```python
import contextlib
import numpy as np
import concourse.bass as bass
import concourse.mybir as mybir
from concourse.bass_utils import run_bass_kernel_spmd

F32 = mybir.dt.float32
BF16 = mybir.dt.bfloat16
AF = mybir.ActivationFunctionType
ALU = mybir.AluOpType
AX = mybir.AxisListType

D = 1024
SEQ = 2048
NSEQ_S = 16
TS = 4
NT = SEQ + NSEQ_S * TS
INC = 9504
RC = 3360
NDMASEM = 12
SAME_ENG_RELAX = ()
RMS_EPS = 1e-6
LNX_EPS = 64e-5
C1 = -0.5 * float(np.exp(-0.5))


class Prog:
    def __init__(self, nc):
        self.nc = nc
        self.ops = []
        self.last_w = {}
        self.readers = {}
        self.bar = False
        self.sink = None
        self.lat_same = 0.2
        self.lat_cross = 0.3

    def peek(self, r, w):
        deps = set()
        rk = [self.key(x) for x in r]; wk = [self.key(x) for x in w]
        if self.bar:
            rk.append("__bar")
        for k in rk:
            if k in self.last_w:
                deps.add(self.last_w[k])
        for k in wk:
            if k in self.last_w:
                deps.add(self.last_w[k])
            deps |= self.readers.get(k, set())
        return deps

    @staticmethod
    def dur(eng, dma, n):
        if dma:
            return 0.1
        if eng == "pe":
            return 0.06 + n / 1200.0
        if eng == "dve":
            return 0.10 + n / 960.0
        if eng == "act":
            return 0.25 + n / 1200.0
        if eng == "pool":
            return 0.30 + n / 400.0
        return 0.1

    def merge(self, lists, lead=0):
        L = list(lists); K = len(L)
        done = set(); pos = [0] * K
        end = {}; free = {}
        tot = [max(1, len(x)) for x in L]

        def est_start(x):
            eng, fn, r, w, dma, n = x
            deps = self.peek(r, w)
            t = free.get(eng, 0.0)
            for d in deps:
                if d in end:
                    lat = self.lat_same if self.ops[d]["eng"] == eng else self.lat_cross
                    if self.ops[d]["eng"] == "pe" and eng == "pe":
                        lat = 0.0
                    t = max(t, end[d] + lat)
            return t

        while any(pos[i] < len(L[i]) for i in range(K)):
            progressed = False
            for si in range(K):
                while pos[si] < len(L[si]) and (L[si][pos[si]][0] == "mark" or (L[si][pos[si]][0] == "need" and L[si][pos[si]][1] in done)):
                    if L[si][pos[si]][0] == "mark":
                        done.add(L[si][pos[si]][1])
                    pos[si] += 1; progressed = True
            cand = []
            for si in range(K):
                if pos[si] >= len(L[si]):
                    continue
                x = L[si][pos[si]]
                if x[0] == "need":
                    continue
                cand.append((est_start(x), pos[si] / tot[si], si))
            if not cand:
                if progressed:
                    continue
                if all(pos[i] >= len(L[i]) for i in range(K)):
                    break
                raise RuntimeError("merge deadlock")
            t, _, si = min(cand)
            x = L[si][pos[si]]; pos[si] += 1
            eng, fn, r, w, dma, n = x
            idx = self.add(eng, fn, r, w, dma, n)
            d_ = self.dur(eng, dma, n)
            free[eng] = t + d_
            end[idx] = t + d_ + (2.5 if dma else 0.0)

    @staticmethod
    def key(x):
        if isinstance(x, tuple):
            if isinstance(x[0], str):
                return x
            return (x[0].tensor.name, x[1])
        if isinstance(x, str):
            return x
        return x.tensor.name

    def add(self, eng, fn, r=(), w=(), dma=False, n=256):
        if self.sink is not None:
            if eng in ("mark", "need"):
                self.sink.append((eng, fn))
            else:
                self.sink.append((eng, fn, list(r), list(w), dma, n))
            return None
        deps = set()
        rk = [self.key(x) for x in r]
        wk = [self.key(x) for x in w]
        if self.bar:
            rk.append("__bar")
        for k in rk:
            if k in self.last_w:
                deps.add(self.last_w[k])
        for k in wk:
            if k in self.last_w:
                deps.add(self.last_w[k])
            deps |= self.readers.get(k, set())
        idx = len(self.ops)
        self.ops.append(dict(eng=eng, fn=fn, deps=deps, dma=dma, users=0, stream=getattr(self, "tag", None), rk=rk, wk=wk))
        for k in rk:
            self.readers.setdefault(k, set()).add(idx)
        for k in wk:
            self.last_w[k] = idx
            self.readers[k] = set()
        return idx

    def barrier(self, dummy):
        allk = set(self.last_w.keys()) | set(self.readers.keys())
        allk.discard("__bar")
        self.bar = False
        self.add("dve", lambda e: e.memset(dummy, 0.0), r=(), w=list(allk) + ["__bar"])
        self.last_w = {"__bar": self.last_w["__bar"]}
        self.readers = {}
        self.bar = True

    def emit(self):
        nc = self.nc
        engs = {"pe": nc.tensor, "act": nc.scalar, "dve": nc.vector, "pool": nc.gpsimd, "sp": nc.sync}
        ops = self.ops

        seqn = {}
        for i_, o_ in enumerate(ops):
            if not o_["dma"]:
                seqn[o_["eng"]] = seqn.get(o_["eng"], 0) + 1
                o_["seq"] = seqn[o_["eng"]]

        def skip(d, o):
            if ops[d]["dma"] or o["dma"]:
                return False
            if ops[d]["eng"] != o["eng"]:
                return False
            if o["eng"] == "pe":
                return True
            if o["eng"] in SAME_ENG_RELAX and o["seq"] - ops[d]["seq"] >= 2:
                return True
            return False

        for o in ops:
            for d in o["deps"]:
                if not skip(d, o):
                    ops[d]["users"] += 1
        with contextlib.ExitStack() as st:
            csem = {e: st.enter_context(nc.semaphore("c_" + e)) for e in engs}
            dsem = {e: [st.enter_context(nc.semaphore(f"d_{e}{i}")) for i in range(NDMASEM)]
                    for e in ("act", "pool", "sp")}
            ccnt = {e: 0 for e in engs}
            dcnt = {e: [0] * NDMASEM for e in dsem}
            drr = {e: 0 for e in dsem}
            waited = {e: {} for e in engs}
            sig = {}

            def wait(e, sem, val):
                k = id(sem)
                if val <= 0 or waited[e].get(k, 0) >= val:
                    return
                engs[e].wait_ge(sem, val)
                waited[e][k] = val

            for i, o in enumerate(ops):
                e = o["eng"]
                for d in sorted(o["deps"]):
                    if d not in sig or skip(d, o):
                        continue
                    s, v = sig[d]
                    wait(e, s, v)
                if o["dma"]:
                    j = drr[e]
                    drr[e] = (j + 1) % NDMASEM
                    s = dsem[e][j]
                    wait(e, s, dcnt[e][j])
                    ins = o["fn"](engs[e])
                    dcnt[e][j] += 16
                    ins.then_inc(s, 16)
                    sig[i] = (s, dcnt[e][j])
                else:
                    ins = o["fn"](engs[e])
                    if o["users"] > 0:
                        ccnt[e] += 1
                        ins.then_inc(csem[e], 1)
                        sig[i] = (csem[e], ccnt[e])
            for e in dsem:
                for j in range(NDMASEM):
                    if dcnt[e][j] > 0:
                        nc.sync.wait_ge(dsem[e][j], dcnt[e][j])
        return len(ops)


def build(debug=None):
    nc = bass.Bass("TRN2", target_bir_lowering=False)
    P = Prog(nc)
    din = lambda n, s: nc.dram_tensor(n, list(s), F32, kind="ExternalInput").ap()
    dout = lambda n, s: nc.dram_tensor(n, list(s), F32, kind="ExternalOutput").ap()
    xp = din("xp", [SEQ, D]); xs = din("xs", [64, D])
    sh_in = din("sh", [16, 8, 128, 128]); sr_in = din("sr", [16, 16, 64, 64]); ss_in = din("ss", [16, RC])
    cpr = din("cp", [1, D]); csm = din("cs", [16, D])
    norm1_w = din("norm1_w", [8, 128]); norm2_w = din("norm2_w", [8, 128])
    ada_w = din("ada_w", [D, 6 * D]); ada_b = din("ada_b", [48, 128])
    w_in = din("w_in", [D, INC]); lb_logits = din("lb_logits", [16, 128])
    hgrn_norm_w = din("hgrn_norm_w", [1, 128]); mu = din("rwkv_mu", [1, RC])
    w0 = din("rwkv_w0", [8, 128]); w2 = din("rwkv_w2", [64, D]); a0 = din("rwkv_a0", [8, 128])
    a2 = din("rwkv_a2", [64, D]); g2 = din("rwkv_g2", [160, D]); k_k = din("rwkv_k_k", [8, 128])
    k_a = din("rwkv_k_a", [8, 128]); r_k = din("rwkv_r_k", [8, 128]); lnx_w = din("rwkv_lnx_w", [8, 128])
    lnx_b = din("rwkv_lnx_b", [8, 128]); w_out = din("w_out", [D, D]); w_up = din("w_up", [D, 4 * D])
    w_down = din("w_down", [4 * D, D]); fnw = din("final_norm_w", [8, 128])
    yp = dout("yp", [SEQ, D]); ys = dout("ys", [64, D])
    hp = dout("hp", [8, 128, 128]); rp = dout("rp", [16, 64, 64]); shp = dout("shp", [1, RC])
    hs = dout("hs", [16, 8, 128, 128]); rs = dout("rs", [16, 16, 64, 64]); shs = dout("shs", [16, RC])
    dbg = {}

    cnt = [0]

    def uname(n):
        cnt[0] += 1
        return f"{n}_{cnt[0]}"

    SBTOT = [0]; SBMAX = [0]

    class Scope:
        def __init__(self):
            self.st = contextlib.ExitStack()

        def sb(self, name, shape, dt=F32):
            nb = int(np.prod(shape[1:])) * (4 if dt == F32 else 2)
            SBTOT[0] += nb; self.tot = getattr(self, "tot", 0) + nb
            SBMAX[0] = max(SBMAX[0], SBTOT[0])
            return self.st.enter_context(nc.sbuf_tensor(uname(name), list(shape), dt)).ap()

        def close(self):
            SBTOT[0] -= getattr(self, "tot", 0)
            self.st.close()

    G = Scope()
    banks = [nc.alloc_psum_tensor(f"bank{i}", [128, 512], F32).ap() for i in range(8)]
    bk = [0]

    pools = {None: list(range(8)), "H": [1, 2], "RF": [3, 4], "RB": [6, 7]}
    cur = [None]
    bkp = {"H": 0, "RF": 0, "RB": 0}

    def bank():
        if cur[0] is None:
            b = banks[bk[0] % 8]
            bk[0] += 1
            return b
        pl = pools[cur[0]]
        b = banks[pl[bkp[cur[0]] % len(pl)]]
        bkp[cur[0]] += 1
        return b

    def dump(name, ap, shape, keys=None):
        if debug is None or (name not in debug and "all" not in debug):
            return
        d = nc.dram_tensor("dbg_" + name, list(shape), F32, kind="ExternalOutput").ap()
        dbg[name] = d
        q = "pool" if ap.dtype != F32 else "sp"
        rk = [ap] + [(ap, t_) for t_ in range(16)] if keys is None else keys
        P.add(q, lambda e: e.dma_start(out=d, in_=ap), r=rk, w=[d], dma=True)

    def dma(q, out, in_, r=(), w=()):
        P.add(q, lambda e: e.dma_start(out=out, in_=in_), r=list(r) or [in_], w=list(w) or [out], dma=True)

    def act(out, in_, func, bias=0.0, scale=1.0, r=(), w=(), accum=None):
        rr = list(r) or [in_]
        if not isinstance(bias, float):
            rr.append(bias)
        if not isinstance(scale, float):
            rr.append(scale)
        ww = list(w) or [out]
        if accum is not None:
            ww.append(accum)
        if accum is not None:
            P.add("act", lambda e: e.activation(out=out, in_=in_, func=func, bias=bias, scale=scale, accum_out=accum), r=rr, w=ww, n=out.free_size())
        else:
            P.add("act", lambda e: e.activation(out=out, in_=in_, func=func, bias=bias, scale=scale), r=rr, w=ww, n=out.free_size())

    def tt(out, in0, in1, op, eng="dve", r=(), w=()):
        P.add(eng, lambda e: e.tensor_tensor(out=out, in0=in0, in1=in1, op=op), r=list(r) or [in0, in1], w=list(w) or [out], n=out.free_size())

    def ts(out, in0, s1, s2, op0, op1=None, eng="dve", r=(), w=()):
        rr = list(r) or [in0]
        for s in (s1, s2):
            if s is not None and not isinstance(s, float):
                rr.append(s)
        if op1 is None:
            P.add(eng, lambda e: e.tensor_scalar(out=out, in0=in0, scalar1=s1, scalar2=None, op0=op0), r=rr, w=list(w) or [out], n=out.free_size())
        else:
            P.add(eng, lambda e: e.tensor_scalar(out=out, in0=in0, scalar1=s1, scalar2=s2, op0=op0, op1=op1), r=rr, w=list(w) or [out], n=out.free_size())

    def aff(out, in_, scale, bias, r=(), w=()):
        act(out, in_, AF.Identity, bias=bias, scale=scale, r=r, w=w)

    def stt(out, in0, scalar, in1, op0, op1, r=(), w=()):
        rr = list(r) or [in0, in1]
        if not isinstance(scalar, float):
            rr.append(scalar)
        P.add("dve", lambda e: e.scalar_tensor_tensor(out=out, in0=in0, scalar=scalar, in1=in1, op0=op0, op1=op1), r=rr, w=list(w) or [out], n=out.free_size())

    def cp(out, in_, eng="dve", r=(), w=()):
        if eng == "act":
            act(out, in_, AF.Copy, r=r, w=w)
        else:
            P.add(eng, lambda e: e.tensor_copy(out=out, in_=in_), r=list(r) or [in_], w=list(w) or [out], n=out.free_size())

    def memset(ap, val, eng="pool", w=()):
        P.add(eng, lambda e: e.memset(ap, val), w=list(w) or [ap], n=ap.free_size())

    def mm(out, lhsT, rhs, start, stop, r=(), w=()):
        P.add("pe", lambda e: e.matmul(out, lhsT, rhs, start=start, stop=stop), r=list(r) or [lhsT, rhs], w=list(w) or [out], n=rhs.free_size() + 60)

    def tr(out, in_, ident, r=(), w=()):
        P.add("pe", lambda e: e.transpose(out=out, in_=in_, identity=ident), r=list(r) or [in_, ident], w=list(w) or [out], n=ident.free_size() + 60)

    I32 = mybir.dt.int32

    def rsqrt_dve(y, x, t1, keys):
        P.add("dve", lambda e: e.tensor_scalar(out=y.bitcast(I32), in0=x.bitcast(I32), scalar1=-0.5, scalar2=1597463007.0,
                                               op0=ALU.mult, op1=ALU.add), r=keys, w=keys)
        for _ in range(2):
            tt(t1, y, y, ALU.mult, r=keys, w=keys)
            stt(t1, t1, -0.5, x, ALU.mult, ALU.mult, r=keys, w=keys)
            stt(y, t1, 1.5, y, ALU.add, ALU.mult, r=keys, w=keys)

    def rsqrt_(out, in_, scale, eps, tA, tB):
        ts(tA, in_, scale, eps, ALU.mult, ALU.add)
        rsqrt_dve(out, tA, tB, [out, tA, tB])

    ones128 = G.sb("ones128", [128, 128]); identF = G.sb("identF", [128, 128]); identB = G.sb("identB", [128, 128], BF16)
    memset(ones128, 1.0)
    P.add("pool", lambda e: e.affine_select(out=identF, in_=ones128, pattern=[[1, 128]], compare_op=ALU.is_equal,
                                           fill=0.0, base=0, channel_multiplier=-1), r=[ones128], w=[identF])
    cp(identB, identF)
    onesB = G.sb("onesB", [128, 128], BF16)
    cp(onesB, ones128)
    blkones = G.sb("blkones", [128, 128], BF16)
    memset(blkones, 0.0)
    memset(blkones[0:64, 0:64], 1.0, w=[blkones]); memset(blkones[64:128, 64:128], 1.0, w=[blkones])
    blkneg = G.sb("blkneg", [128, 128])
    memset(blkneg, 0.0)
    memset(blkneg[0:64, 0:64], -1.0, w=[blkneg]); memset(blkneg[64:128, 64:128], -1.0, w=[blkneg])
    blkpos = G.sb("blkpos", [128, 128])
    memset(blkpos, 0.0)
    memset(blkpos[0:64, 0:64], 1.0, w=[blkpos]); memset(blkpos[64:128, 64:128], 1.0, w=[blkpos])
    dummy = G.sb("dummy", [128, 1])
    masks = {}
    for C in (64, 4):
        mi = G.sb(f"mi{C}", [C, C]); ms = G.sb(f"ms{C}", [C, C]); mls = G.sb(f"mls{C}", [C, C])
        P.add("pool", lambda e, mi=mi, C=C: e.affine_select(out=mi, in_=ones128[0:C, 0:C], pattern=[[1, C]], compare_op=ALU.is_ge,
                                                         fill=0.0, base=0, channel_multiplier=-1), r=[ones128], w=[mi])
        P.add("pool", lambda e, ms=ms, C=C: e.affine_select(out=ms, in_=ones128[0:C, 0:C], pattern=[[1, C]], compare_op=ALU.is_gt,
                                                         fill=0.0, base=0, channel_multiplier=-1), r=[ones128], w=[ms])
        P.add("pool", lambda e, mls=mls, C=C: e.affine_select(out=mls, in_=ones128[0:C, 0:C], pattern=[[-1, C]], compare_op=ALU.is_gt,
                                                           fill=0.0, base=0, channel_multiplier=1), r=[ones128], w=[mls])
        msn = G.sb(f"msn{C}", [C, C]); mlsn = G.sb(f"mlsn{C}", [C, C])
        ts(msn, ms, -1.0, None, ALU.mult); ts(mlsn, mls, -1.0, None, ALU.mult)
        masks[C] = dict(incl=mi, strict=ms, nstrict=msn, nlstrict=mlsn)

    stgA = G.sb("stgA", [128, 128]); stgB = G.sb("stgB", [48, 128])
    memset(stgA, 0.0)
    rows = {}
    r0 = 0
    for nm, src, n in (("n1", norm1_w, 8), ("n2", norm2_w, 8), ("lb", lb_logits, 16), ("w0", w0, 8), ("a0", a0, 8),
                       ("kk", k_k, 8), ("ka", k_a, 8), ("rk", r_k, 8), ("lw", lnx_w, 8), ("lbb", lnx_b, 8), ("fn", fnw, 8)):
        dma("sp", stgA[r0:r0 + n, :], src, w=[stgA]); rows[nm] = r0; r0 += n
    dma("sp", stgA[r0:r0 + 26, :], mu[:, 0:3328].rearrange("o (n p) -> (o n) p", p=128), w=[stgA]); rows["mu"] = r0; r0 += 26
    dma("sp", stgA[r0:r0 + 1, 0:32], mu[:, 3328:3360], w=[stgA]); rows["mu2"] = r0; r0 += 1
    assert r0 <= 128
    dma("sp", stgB, ada_b)
    pv = G.sb("pv", [128, 128]); pvB = G.sb("pvB", [128, 48])
    b_ = bank(); tr(b_[:, 0:128], stgA, identF); cp(pv, b_[:, 0:128])
    b_ = bank(); tr(b_[:, 0:48], stgB, identF[0:48, 0:48]); cp(pvB, b_[:, 0:48])
    col = lambda nm, i=0: pv[:, rows[nm] + i: rows[nm] + i + 1]
    pv2 = G.sb("pv2", [128, 128]); ts(pv2, pv, -1.0, 1.0, ALU.mult, ALU.add)
    col2 = lambda nm, i=0: pv2[:, rows[nm] + i: rows[nm] + i + 1]
    hs1 = G.sb("hs1", [128, 8]); hs2 = G.sb("hs2", [128, 8]); hs1n = G.sb("hs1n", [128, 8]); hs2c = G.sb("hs2c", [128, 8])
    tt(hs1, pv[:, rows["lb"]:rows["lb"] + 8], pv[:, rows["lb"] + 8:rows["lb"] + 16], ALU.subtract)
    act(hs2, hs1, AF.Tanh, scale=0.5)
    ts(hs1, hs2, -0.25, 0.25, ALU.mult, ALU.add)
    ts(hs2, hs2, 0.25, 0.75, ALU.mult, ALU.add)
    ts(hs1n, hs1, -1.0, None, ALU.mult)
    ts(hs2c, hs2, -1.0, 1.0, ALU.mult, ALU.add)
    hw0 = G.sb("hw0", [128, 8]); ha0 = G.sb("ha0", [128, 8]); omka = G.sb("omka", [128, 8])
    ts(hw0, pv[:, rows["w0"]:rows["w0"] + 8], 0.5, None, ALU.mult)
    ts(ha0, pv[:, rows["a0"]:rows["a0"] + 8], 0.5, None, ALU.mult)
    ts(omka, pv[:, rows["ka"]:rows["ka"] + 8], -1.0, 1.0, ALU.mult, ALU.add)
    nwbc = G.sb("nwbc", [64, 128]); dma("sp", nwbc, hgrn_norm_w.partition_broadcast(64))

    modT = G.sb("modT", [128, 48, 17])
    g1p = G.sb("g1p", [128, 8]); g2p = G.sb("g2p", [128, 8])
    sh1p = G.sb("sh1p", [128, 8]); sh2p = G.sb("sh2p", [128, 8]); gt1p = G.sb("gt1p", [128, 8]); gt2p = G.sb("gt2p", [128, 8])
    mT = G.sb("mT", [128, 8, NT], BF16)
    S0 = Scope()
    c17 = S0.sb("c17", [17, D]); s17 = S0.sb("s17", [17, D]); s17b = S0.sb("s17b", [17, D], BF16); scT = S0.sb("scT", [128, 8, 32], BF16)
    dma("sp", c17[0:1, :], cpr, w=[c17]); dma("sp", c17[1:17, :], csm, w=[c17])
    act(s17, c17, AF.Tanh, scale=0.5)
    ts(s17, s17, 0.5, 0.5, ALU.mult, ALU.add)
    tt(s17b, s17, c17, ALU.mult)
    b_ = bank(); bb = b_.bitcast(BF16)
    for k in range(8):
        tr(bb[:, k * 32:k * 32 + 17], s17b[:, k * 128:(k + 1) * 128], identB[0:17, 0:17], w=[b_])
    cp(scT[:, :, 0:17], bb[:, 0:256].rearrange("p (k c) -> p k c", c=32)[:, :, 0:17], r=[b_])
    awb = [S0.sb("awb", [128, 8, 512], BF16) for _ in range(2)]
    for g in range(12):
        wt = awb[g % 2]
        dma("pool", wt, ada_w[:, g * 512:(g + 1) * 512].rearrange("(k p) c -> p k c", p=128), r=[ada_w], w=[wt])
        b_ = bank()
        for m in range(4):
            for k in range(8):
                mm(b_[:, m * 32:m * 32 + 17], wt[:, k, m * 128:(m + 1) * 128], scT[:, k, 0:17], k == 0, k == 7, w=[b_])
        for m in range(4):
            ts(modT[:, g * 4 + m, :], b_[:, m * 32:m * 32 + 17], pvB[:, g * 4 + m:g * 4 + m + 1], None, ALU.add, r=[b_, pvB], w=[modT])
    def modp(part):
        return modT[:, part * 8:(part + 1) * 8, 0]
    def mods(part):
        return modT[:, part * 8:(part + 1) * 8, 1:17].unsqueeze(3).broadcast_to([128, 8, 16, 4])
    stt(g1p, modp(1), 1.0, pv[:, rows["n1"]:rows["n1"] + 8], ALU.add, ALU.mult, r=[modT, pv])
    stt(g2p, modp(4), 1.0, pv[:, rows["n2"]:rows["n2"] + 8], ALU.add, ALU.mult, r=[modT, pv])
    for dst, part in ((sh1p, 0), (gt1p, 2), (sh2p, 3), (gt2p, 5)):
        cp(dst, modp(part), r=[modT])
    def mods3(part):
        return modT[:, part * 8:(part + 1) * 8, 1:17]

    def mk_mods(dst, part, nm=None):
        for t4 in range(4):
            if nm is None:
                cp(dst[:, :, t4::4], mods3(part), r=[modT], w=[dst])
            else:
                stt(dst[:, :, t4::4], mods3(part), 1.0, pv[:, rows[nm]:rows[nm] + 8].unsqueeze(2).broadcast_to([128, 8, 16]),
                    ALU.add, ALU.mult, r=[modT, pv], w=[dst])
    S0.close()
    P.barrier(dummy)

    tiles = [(i * 256, 256, 64, 4, False) for i in range(8)] + [(SEQ + 16 * i, 16, 4, 4, True) for i in range(4)]

    def norm_to(hT_dst, xT_t, T, gp, shp_, gs, shs_, is_s, ti):
        sq = nbuf["sq"]
        tt(sq[:, :, 0:T], xT_t[:, :, 0:T], xT_t[:, :, 0:T], ALU.mult, r=[xT_t], w=[sq])
        b_ = bank()
        for fc in range(8):
            mm(b_[:, 0:T], onesB, sq[:, fc, 0:T], fc == 0, fc == 7, w=[b_])
        rstd = nbuf["rstd"]
        ts(rstd[:, 0:T], b_[:, 0:T], 1.0 / D, RMS_EPS, ALU.mult, ALU.add, r=[b_], w=[rstd])
        r1 = nbuf["r1"]; r2 = nbuf["r2"]
        cp(r1[:, 0:T], rstd[:, 0:T], eng="act", r=[rstd], w=[r1])
        rsqrt_dve(rstd[:, 0:T], r1[:, 0:T], r2[:, 0:T], [rstd, r1, r2])
        tmp = nbuf["tmp"]
        tt(tmp[:, :, 0:T], xT_t[:, :, 0:T], rstd[:, 0:T].unsqueeze(1).broadcast_to([128, 8, T]), ALU.mult, r=[xT_t, rstd], w=[tmp])
        if not is_s:
            for fc in range(8):
                ts(hT_dst[:, fc, :], tmp[:, fc, 0:T], gp[:, fc:fc + 1], shp_[:, fc:fc + 1], ALU.mult, ALU.add,
                   r=[tmp, gp, shp_], w=[(hT_dst, ti)])
        else:
            tt(tmp[:, :, 0:T], tmp[:, :, 0:T], gs, ALU.mult, r=[tmp, gs], w=[tmp])
            tt(hT_dst, tmp[:, :, 0:T], shs_, ALU.add, r=[tmp, shs_], w=[(hT_dst, ti)])

    def load_xT(xT_t, c0, T, ti):
        for sub in range(0, T, 128):
            n = min(128, T - sub)
            xtm = nbuf["xtm"][nbuf["i"] % 2]; nbuf["i"] += 1
            src = xp[c0 + sub:c0 + sub + n, :] if c0 < SEQ else xs[c0 - SEQ + sub:c0 - SEQ + sub + n, :]
            dma("sp", xtm[0:n, :], src, w=[xtm])
            for half in range(2):
                b_ = bank()
                for f4 in range(4):
                    fc = half * 4 + f4
                    tr(b_[:, f4 * 128:f4 * 128 + n], xtm[0:n, fc * 128:(fc + 1) * 128], identF[0:n, 0:n], w=[b_])
                cp(xT_t[:, half * 4:half * 4 + 4, sub:sub + n], b_.rearrange("p (f t) -> p f t", t=128)[:, :, 0:n], eng="act", r=[b_], w=[xT_t])

    S2 = Scope()
    hT = S2.sb("hT", [128, 8, NT], BF16)
    S1 = Scope()
    g1s = S1.sb("g1s", [128, 8, 64]); sh1s = S1.sb("sh1s", [128, 8, 64])
    mk_mods(sh1s, 0); mk_mods(g1s, 1, "n1")
    nbuf = dict(sq=S1.sb("sq", [128, 8, 512], BF16), rstd=S1.sb("rstd", [128, 512]), tmp=S1.sb("ntmp", [128, 8, 512]), r1=S1.sb("r1", [128, 512]), r2=S1.sb("r2", [128, 512]),
                xtm=[S1.sb("xtm", [128, D]) for _ in range(2)], i=0)
    xT_ts = [S1.sb("xT_t", [128, 8, 512]) for _ in range(2)]
    for ti, (c0, T, C, NCH, is_s) in enumerate(tiles):
        xT_t = xT_ts[ti % 2]
        load_xT(xT_t, c0, T, ti)
        so = max(c0 - SEQ, 0)
        norm_to(hT[:, :, c0:c0 + T], xT_t, T, g1p, sh1p, g1s[:, :, so:so + min(T, 16)], sh1s[:, :, so:so + min(T, 16)], is_s, ti)
    S1.close()
    P.barrier(dummy)

    W = lambda name, shape, dt=F32: S2.sb(name, shape, dt)
    wbH = W("wbH", [128, 8, 640], BF16); wbR = W("wbR", [128, 8, 512], BF16)
    wl = wbH[:, :, 0:288]
    w2b = W("w2b", [128, D], BF16)
    g2b = W("g2b", [128, D], BF16); g2c = W("g2c", [32, D], BF16)
    TW = W("TW", [128, NT], BF16); SG1 = W("SG1", [128, NT], BF16); SG2 = W("SG2", [32, NT], BF16)
    ssT = W("ssT", [128, 27, 16]); lastc = W("lastc", [128, 27, 17])
    prevs = W("prevs", [128, 8])
    dsh = W("dsh", [128, 256])
    S0all = W("S0all", [128, 4, 128]); S0bf = W("S0bf", [128, 4, 128], BF16); Souts = S0all
    Sh = W("Sh", [128, 128]); Shbf = W("Shbf", [128, 128], BF16)
    R0in = W("R0in", [128, 4, 128]); R0blk = W("R0blk", [128, 4, 128]); R0bf = W("R0bf", [128, 4, 128], BF16); Routs = R0blk
    Sr = W("Sr", [128, 128]); Srbf = W("Srbf", [128, 128], BF16); Srt = W("Srt", [128, 128])
    f32t = {n: W(n, [128, 256]) for n in ("th", "fS", "kS", "Pc", "rP", "Ga", "t1", "t2", "Gmm", "kap", "kmod", "bS", "gS",
                                          "xr", "xk", "xv", "rG", "aS", "u1", "u2", "e1")}
    f32t["n1"] = f32t["kmod"]; f32t["n2"] = f32t["u1"]
    oTs = [W("oT", [128, 256]) for _ in range(2)]
    rstH = W("rstH", [128, 256]); rstR = W("rstR", [128, 256]); memset(rstH, 0.0); memset(rstR, 0.0)
    bft = {n: W(n, [128, 256], BF16) for n in ("qt", "kt", "kh", "vT", "sqb", "kpt", "bt", "bh", "ktl", "khr", "vTr")}
    KR = W("KR", [128, 4, 128], BF16)
    tm = {n: W(n, [64, 4, 128], BF16) for n in ("Vtm", "khtm", "W1", "Un", "Ktp", "onb", "onr")}
    Abm = {n: W(n, [64, 4, 128], BF16) for n in ("Y2", "Z2", "TTa", "TTb")}
    IFs = [dict(**{n: W(n, [64, 4, 128], BF16) for n in ("Z", "Y", "PbT", "PkT", "AkT", "Vtr", "Ktm", "bhtm", "khrtm")},
                rt=W("rt", [128, 256], BF16), **{n: W(n, [128, 256]) for n in ("Gm", "bon", "Gb")}) for _ in range(2)]
    attnT = W("attnT", [64, 4, 64], BF16)
    Osb = W("Osb", [64, 4, 128]); Osq = W("Osq", [64, 4, 128]); st_ = W("st_", [64, 16]); stA = W("stA", [64, 16]); stB = W("stB", [64, 16])
    srow = Osq.rearrange("p c f -> p (c f)")[0:17, :]
    Osb2 = W("Osb2", [64, 4, 128]); Osq2 = W("Osq2", [64, 4, 128]); st3 = W("st3", [64, 16]); st4 = W("st4", [64, 16]); stC = W("stC", [64, 16]); stD = W("stD", [64, 16])
    Mblk = W("Mblk", [128, 4, 128], BF16); Ncomb = W("Ncomb", [128, 4, 128]); RpT = W("RpT", [128, 256], BF16)

    dma("pool", w2b[0:64, :], w2, w=[w2b]); dma("pool", w2b[64:128, :], a2, w=[w2b])
    dma("pool", g2b, g2[0:128, :]); dma("pool", g2c, g2[128:160, :])
    dma("pool", wl, w_in[:, 7168:7456].rearrange("(k p) c -> p k c", p=128), r=[w_in], w=[wl])
    for p4 in range(0, 27, 4):
        n4 = min(4, 27 - p4); wdt = min(n4 * 128, RC - p4 * 128)
        memset(srow, 0.0, eng="dve")
        dma("sp", srow[0:16, 0:wdt], ss_in[:, p4 * 128:p4 * 128 + wdt], r=[ss_in], w=[srow])
        b_ = bank()
        for q in range(n4):
            tr(b_[:, q * 16:(q + 1) * 16], srow[0:16, q * 128:(q + 1) * 128], identF[0:16, 0:16], r=[srow, identF], w=[b_])
        cp(ssT[:, p4:p4 + n4, :], b_[:, 0:n4 * 16].rearrange("p (q s) -> p q s", s=16), r=[b_], w=[ssT])
    memset(prevs, 0.0); memset(lastc, 0.0)

    def proj(wt, c_lo, c_hi, ti, c0, T):
        b_ = bank(); n = c_hi - c_lo
        for k in range(8):
            mm(b_[0:n, 0:T], wt[:, k, c_lo:c_hi], hT[:, k, c0:c0 + T], k == 0, k == 7, r=[wt, (hT, ti)], w=[b_])
        return b_

    def shift_mix(dst, ps, n, T, is_s, mucol, pcol, part, last_tile, s0=0, omucol=None):
        aff(dsh[0:n, 0:T], ps[0:n, 0:T], omucol[0:n, :], 0.0, r=[ps, pv2], w=[dsh])
        if not is_s:
            stt(dst[0:n, 1:T], ps[0:n, 0:T - 1], mucol[0:n, :], dsh[0:n, 1:T], ALU.mult, ALU.add, r=[ps, dsh, pv], w=[dst])
            stt(dst[0:n, 0:1], prevs[0:n, pcol:pcol + 1], mucol[0:n, :], dsh[0:n, 0:1], ALU.mult, ALU.add, r=[prevs, dsh, pv], w=[dst])
            cp(prevs[0:n, pcol:pcol + 1], ps[0:n, T - 1:T], eng="act", r=[ps, dst], w=[prevs])
            if last_tile:
                cp(lastc[0:n, part, 0:1], ps[0:n, T - 1:T], eng="act", r=[ps], w=[lastc])
        else:
            p3 = ps[0:n, 0:T].rearrange("p (s t) -> p s t", t=4); d3 = dsh[0:n, 0:T].rearrange("p (s t) -> p s t", t=4)
            o3 = dst[0:n, 0:T].rearrange("p (s t) -> p s t", t=4)
            stt(o3[:, :, 1:4], p3[:, :, 0:3], mucol[0:n, :], d3[:, :, 1:4], ALU.mult, ALU.add, r=[ps, dsh, pv], w=[dst])
            stt(o3[:, :, 0], ssT[0:n, part, s0:s0 + T // 4], mucol[0:n, :], d3[:, :, 0], ALU.mult, ALU.add, r=[ssT, dsh, pv], w=[dst])
            cp(lastc[0:n, part, 1 + s0:1 + s0 + T // 4], p3[:, :, 3], eng="act", r=[ps], w=[lastc])

    def sigm(dst, src, r=(), w=(), bias=0.0):
        act(dst, src, AF.Tanh, bias=bias, scale=0.5, r=r, w=w)
        aff(dst, dst, 0.5, 0.5)

    for ti, (c0, T, C, NCH, is_s) in enumerate(tiles):
        last = (c0 + T == SEQ)
        b_ = proj(wl, 0, 128, ti, c0, T)
        shift_mix(f32t["t1"], b_, 128, T, is_s, col("mu", 24), 0, 24, last, max(c0 - SEQ, 0) // 4, omucol=col2("mu", 24))
        act(TW[0:64, c0:c0 + T], f32t["t1"][0:64, 0:T], AF.Tanh, r=[f32t["t1"]], w=[(TW, ti)])
        cp(TW[64:128, c0:c0 + T], f32t["t1"][64:128, 0:T], eng="act", r=[f32t["t1"]], w=[(TW, ti)])
        b_ = proj(wl, 128, 256, ti, c0, T)
        shift_mix(f32t["t1"], b_, 128, T, is_s, col("mu", 25), 1, 25, last, max(c0 - SEQ, 0) // 4, omucol=col2("mu", 25))
        sigm(f32t["t2"][:, 0:T], f32t["t1"][:, 0:T], r=[f32t["t1"]], w=[f32t["t2"]])
        cp(SG1[:, c0:c0 + T], f32t["t2"][:, 0:T], r=[f32t["t2"]], w=[(SG1, ti)])
        b_ = proj(wl, 256, 288, ti, c0, T)
        shift_mix(f32t["t1"], b_, 32, T, is_s, col("mu2"), 2, 26, last, max(c0 - SEQ, 0) // 4, omucol=col2("mu2"))
        sigm(f32t["t2"][0:32, 0:T], f32t["t1"][0:32, 0:T], r=[f32t["t1"]], w=[f32t["t2"]])
        cp(SG2[:, c0:c0 + T], f32t["t2"][0:32, 0:T], r=[f32t["t2"]], w=[(SG2, ti)])

    if debug is not None and debug.get("_stop") == "lora":
        dump("TW", TW, [128, NT]); dump("SG1", SG1, [128, NT]); dump("SG2", SG2, [32, NT]); dump("lastc", lastc.rearrange("p a b -> p (a b)"), [128, 27 * 17])
        n = P.emit()
        return nc, dbg, n

    def chunk3(ap, C):
        return ap.rearrange("p (c t) -> p c t", t=C)

    def to_tm(dst, src_bf, T, C, NCH):
        for c4 in range(0, NCH, 8):
            b_ = bank(); bb = b_.bitcast(BF16)
            for c in range(c4, min(c4 + 8, NCH)):
                tr(bb[0:C, (c - c4) * 128:(c - c4 + 1) * 128], src_bf[:, c * C:(c + 1) * C], identB, r=[src_bf, identB], w=[b_])
            n8 = min(8, NCH - c4)
            cp(dst[0:C, c4:c4 + n8, :], bb[0:C, 0:n8 * 128].rearrange("p (c f) -> p c f", f=128), eng="act", r=[b_], w=[dst])

    def cumprod(dst, src, T, C, NCH, rst):
        cp(chunk3(rst[:, 0:T], C)[:, :, 0:1], chunk3(src[:, 0:T], C)[:, :, 0:1], eng="act", r=[src], w=[rst])
        P.add("dve", lambda e: e.tensor_tensor_scan(out=dst[:, 0:T], data0=src[:, 0:T], data1=rst[:, 0:T], initial=1.0,
                                                    op0=ALU.mult, op1=ALU.max), r=[src, rst], w=[dst])
        if C != 64:
            memset(chunk3(rst[:, 0:T], C)[:, :, 0:1], 0.0, eng="pool", w=[rst])

    def lastbc(ap, T, C, NCH):
        return chunk3(ap[:, 0:T], C)[:, :, C - 1:C].broadcast_to([128, NCH, C])


    def rwkv_front(j, ti, c0, T, C, NCH, is_s, last, wb, gi):
        F_ = f32t; B_ = bft; M_ = masks[C]; jc = slice(j * 128, (j + 1) * 128)
        I_ = IFs[gi % 2]
        if gi >= 2:
            P.add("need", ("B", gi - 2))
        X = lambda n: F_[n][:, 0:T]
        Bx = lambda n: B_[n][:, 0:T]
        Gm = I_["Gm"]; bon = I_["bon"]; Gb = I_["Gb"]; rt = I_["rt"]
        so_ = max(c0 - SEQ, 0) // 4
        br = proj(wb, 0, 128, ti, c0, T); shift_mix(F_["xr"], br, 128, T, is_s, col("mu", j), 3, j, last, so_, omucol=col2("mu", j))
        bk_ = proj(wb, 128, 256, ti, c0, T); shift_mix(F_["xk"], bk_, 128, T, is_s, col("mu", 8 + j), 4, 8 + j, last, so_, omucol=col2("mu", 8 + j))
        bv = proj(wb, 256, 384, ti, c0, T); shift_mix(F_["xv"], bv, 128, T, is_s, col("mu", 16 + j), 5, 16 + j, last, so_, omucol=col2("mu", 16 + j))
        bgb = proj(wb, 384, 512, ti, c0, T)
        sigm(Gb[:, 0:T], bgb[:, 0:T], r=[bgb], w=[Gb])
        bw = bank(); mm(bw[:, 0:T], w2b[0:64, jc], TW[0:64, c0:c0 + T], True, True, r=[w2b, (TW, ti)], w=[bw])
        act(X("u1"), bw[:, 0:T], AF.Tanh, bias=hw0[:, j:j + 1], scale=0.5, r=[bw], w=[F_["u1"]])
        act(X("u1"), X("u1"), AF.Exp, bias=C1, scale=C1)
        ba = bank(); mm(ba[:, 0:T], w2b[64:128, jc], TW[64:128, c0:c0 + T], True, True, r=[w2b, (TW, ti)], w=[ba])
        sigm(X("aS"), ba[:, 0:T], r=[ba], w=[F_["aS"]], bias=ha0[:, j:j + 1])
        bg = bank(); mm(bg[:, 0:T], g2b[:, jc], SG1[:, c0:c0 + T], True, False, r=[g2b, (SG1, ti)], w=[bg])
        mm(bg[:, 0:T], g2c[:, jc], SG2[:, c0:c0 + T], False, True, r=[g2c, (SG2, ti)], w=[bg])
        cp(X("gS"), bg[:, 0:T], eng="act", r=[bg], w=[F_["gS"]])
        tt(Gb[:, 0:T], Gb[:, 0:T], X("gS"), ALU.mult, eng="pool", r=[Gb, F_["gS"]], w=[Gb])
        cumprod(Gm, F_["u1"], T, C, NCH, rstR)
        P.add("dve", lambda e: e.reciprocal(out=X("rG"), in_=Gm[:, 0:T]), r=[Gm], w=[F_["rG"]])
        cp(chunk3(X("Gmm"), C)[:, :, 1:C], chunk3(Gm[:, 0:T], C)[:, :, 0:C - 1], r=[Gm], w=[F_["Gmm"]])
        memset(chunk3(X("Gmm"), C)[:, :, 0:1], 1.0, eng="dve", w=[F_["Gmm"]])
        aff(X("kap"), X("xk"), col("kk", j), 0.0, r=[F_["xk"], pv], w=[F_["kap"]])
        tt(Bx("sqb"), X("kap"), X("kap"), ALU.mult, eng="pool")
        bn = bank(); mm(bn[:, 0:T], blkones, Bx("sqb"), True, True, w=[bn])
        ts(X("n1"), bn[:, 0:T], 1.0, 1e-24, ALU.mult, ALU.add, r=[bn], w=[F_["n1"]])
        rsqrt_dve(X("u2"), X("n1"), X("n2"), [F_["u2"], F_["n1"], F_["n2"]])
        tt(X("kap"), X("kap"), X("u2"), ALU.mult)
        aff(X("u2"), X("aS"), col("ka", j), omka[:, j:j + 1], r=[F_["aS"], pv, omka], w=[F_["u2"]])
        tt(X("kmod"), X("xk"), X("u2"), ALU.mult, eng="pool")
        tt(X("bS"), X("aS"), X("kap"), ALU.mult)
        stt(Bx("sqb"), X("xr"), col("rk", j), X("kmod"), ALU.mult, ALU.mult, r=[F_["xr"], F_["kmod"], pv], w=[B_["sqb"]])
        bn2 = bank(); mm(bn2[:, 0:T], blkones, Bx("sqb"), True, True, w=[bn2])
        tt(bon[:, 0:T], bn2[:, 0:T], X("xv"), ALU.mult, r=[bn2, F_["xv"]], w=[bon])
        tt(Bx("kpt"), X("kap"), X("Gmm"), ALU.mult)
        tt(rt[:, 0:T], X("xr"), Gm[:, 0:T], ALU.mult, eng="pool", r=[F_["xr"], Gm], w=[rt])
        cp(KR[:, 0:NCH, 0:C], chunk3(Bx("kpt"), C), eng="pool", r=[B_["kpt"]], w=[KR])
        cp(KR[:, 0:NCH, 64:64 + C], chunk3(rt[:, 0:T], C), eng="pool", r=[rt], w=[KR])
        tt(X("u2"), X("kmod"), X("rG"), ALU.mult); cp(Bx("ktl"), X("u2"), eng="act")
        tt(chunk3(Bx("khr"), C), chunk3(X("u2"), C), lastbc(Gm, T, C, NCH), ALU.mult, r=[F_["u2"], Gm], w=[B_["khr"]])
        tt(X("u2"), X("bS"), X("rG"), ALU.mult); cp(Bx("bt"), X("u2"), eng="act")
        tt(chunk3(Bx("bh"), C), chunk3(X("u2"), C), lastbc(Gm, T, C, NCH), ALU.mult, r=[F_["u2"], Gm], w=[B_["bh"]])
        cp(Bx("vTr"), X("xv"), eng="act")
        to_tm(I_["Vtr"], B_["vTr"], T, C, NCH); to_tm(I_["Ktm"], B_["kpt"], T, C, NCH)
        to_tm(I_["bhtm"], B_["bh"], T, C, NCH); to_tm(I_["khrtm"], B_["khr"], T, C, NCH)
        v4_ = lambda b: b[0:C, :].rearrange("p (c x t) -> p c x t", x=2, t=64)
        mk = lambda m_: m_.unsqueeze(1).broadcast_to([C, NCH, C])
        for h in range(2):
            hs_ = slice(64 * h, 64 * h + 64)
            dv = lambda t_: t_[0:C, 0:NCH, h * 64:h * 64 + C]
            b1 = bank()
            for c in range(NCH):
                mm(b1[0:C, c * 128:(c + 1) * 128], B_["bt"][hs_, c * C:(c + 1) * C], KR[hs_, c, :], True, True, r=[B_["bt"], KR], w=[b1])
            tt(dv(I_["Z"]), v4_(b1)[:, 0:NCH, 0, 0:C], mk(M_["nstrict"]), ALU.mult, r=[b1, M_["nstrict"]], w=[I_["Z"]])
            tt(dv(I_["PbT"]), v4_(b1)[:, 0:NCH, 1, 0:C], mk(M_["incl"]), ALU.mult, r=[b1, M_["incl"]], w=[I_["PbT"]])
            b2 = bank()
            for c in range(NCH):
                mm(b2[0:C, c * 128:(c + 1) * 128], B_["ktl"][hs_, c * C:(c + 1) * C], KR[hs_, c, :], True, True, r=[B_["ktl"], KR], w=[b2])
            tt(dv(I_["AkT"]), v4_(b2)[:, 0:NCH, 0, 0:C], mk(M_["strict"]), ALU.mult, r=[b2, M_["strict"]], w=[I_["AkT"]])
            tt(dv(I_["PkT"]), v4_(b2)[:, 0:NCH, 1, 0:C], mk(M_["incl"]), ALU.mult, r=[b2, M_["incl"]], w=[I_["PkT"]])
            b3 = bank()
            for c in range(NCH):
                mm(b3[0:C, c * 128:c * 128 + C], B_["kpt"][hs_, c * C:(c + 1) * C], B_["bt"][hs_, c * C:(c + 1) * C], True, True, r=[B_["kpt"], B_["bt"]], w=[b3])
            tt(dv(I_["Y"]), v4_(b3)[:, 0:NCH, 0, 0:C], mk(M_["nlstrict"]), ALU.mult, r=[b3, M_["nlstrict"]], w=[I_["Y"]])
        P.add("mark", ("F", gi))

    def rwkv_back(j, ti, c0, T, C, NCH, is_s, last, oT, gi):
        F_ = f32t; M_ = masks[C]
        I_ = IFs[gi % 2]
        L = 6 if C == 64 else 2
        X = lambda n: F_[n][:, 0:T]
        Gm = I_["Gm"]; bon = I_["bon"]; Gb = I_["Gb"]; rt = I_["rt"]
        if is_s:
            s0 = (c0 - SEQ) // 4
            memset(R0in, 0.0)
            dma("sp", R0in[0:64, :, 0:64], sr_in[s0:s0 + 4, 2 * j].rearrange("s i j -> i s j"), r=[sr_in], w=[R0in])
            dma("sp", R0in[64:128, :, 64:128], sr_in[s0:s0 + 4, 2 * j + 1].rearrange("s i j -> i s j"), r=[sr_in], w=[R0in])
            b_ = bank()
            for q in range(4):
                tr(b_[:, q * 128:(q + 1) * 128], R0in[:, q, :], identF, w=[b_])
            cp(R0blk[:, 0:4, :], b_.rearrange("p (q f) -> p q f", f=128), r=[b_], w=[R0blk])
            cp(R0bf, R0blk, eng="pool")
        P.add("need", ("F", gi))
        hv = lambda t_, c_lo, c_hi: t_[0:C, c_lo:c_hi, :].rearrange("p c (h t) -> p c h t", t=64)[:, :, :, 0:C]
        tt(hv(Abm["TTa"], 0, NCH), hv(I_["Z"], 0, NCH), identF[0:C, 0:C].unsqueeze(1).unsqueeze(1).broadcast_to([C, NCH, 2, C]), ALU.add,
           r=[I_["Z"], identF], w=[Abm["TTa"]])
        Yc, Zc, Yn, Zn, TTc, TTn = I_["Y"], I_["Z"], Abm["Y2"], Abm["Z2"], Abm["TTa"], Abm["TTb"]
        pr = [(c, h) for c in range(NCH) for h in range(2)]

        def batch_mm(dst, lhs, rhs, wid_l, wid_r, post):
            for p8 in range(0, len(pr), 8):
                b_ = bank(); grp = pr[p8:p8 + 8]
                for q, (c, h) in enumerate(grp):
                    mm(b_[0:C, q * 64:q * 64 + wid_r], lhs[0:C, c, h * 64:h * 64 + wid_l], rhs[0:C, c, h * 64:h * 64 + wid_r], True, True,
                       r=[lhs, rhs], w=[b_])
                cA = grp[0][0]; cB = grp[-1][0] + 1
                src = b_[0:C, 0:(cB - cA) * 128].rearrange("p (c h t) -> p c h t", h=2, t=64)[:, :, :, 0:wid_r]
                d = dst[0:C, cA:cB, :].rearrange("p c (h t) -> p c h t", t=64)[:, :, :, 0:wid_r]
                post(d, src, b_)

        cpy = lambda d, s_, b_: cp(d, s_, eng="act", r=[b_], w=[d])
        for n in range(1, L):
            batch_mm(Yn, Zc, Yc, C, C, cpy)
            if n < L - 1:
                batch_mm(Zn, Yc, Zc, C, C, cpy)
            for p8 in range(0, len(pr), 8):
                b_ = bank(); grp = pr[p8:p8 + 8]
                for q, (c, h) in enumerate(grp):
                    mm(b_[0:C, q * 64:q * 64 + C], Yn[0:C, c, h * 64:h * 64 + C], TTc[0:C, c, h * 64:h * 64 + C], True, True, r=[Yn, TTc], w=[b_])
                cA = grp[0][0]; cB = grp[-1][0] + 1
                src = b_[0:C, 0:(cB - cA) * 128].rearrange("p (c h t) -> p c h t", h=2, t=64)[:, :, :, 0:C]
                tt(hv(TTn, cA, cB), src, hv(TTc, cA, cB), ALU.add, r=[b_, TTc], w=[TTn])
            Yc, Yn = Yn, Yc; Zc, Zn = Zn, Zc; TTc, TTn = TTn, TTc
        TTf = TTc
        batch_mm(tm["W1"], I_["AkT"], I_["Vtr"], C, 64, cpy)
        batch_mm(tm["Ktp"], TTf, I_["Ktm"], C, 64, cpy)
        neg = lambda d, s_, b_: ts(d, s_, -1.0, None, ALU.mult, r=[b_], w=[d])
        batch_mm(tm["Un"], TTf, tm["W1"], C, 64, neg)
        v3 = lambda b: b.rearrange("p (c f) -> p c f", f=128)
        bM = bank()
        for c in range(NCH):
            mm(bM[:, c * 128:(c + 1) * 128], tm["Ktp"][0:C, c, :], I_["bhtm"][0:C, c, :], True, True, r=[tm["Ktp"], I_["bhtm"]], w=[bM])
        tt(Mblk[:, 0:NCH, :], v3(bM)[:, 0:NCH, :], blkneg.unsqueeze(1).broadcast_to([128, NCH, 128]), ALU.mult, r=[bM, blkneg], w=[Mblk])
        bR = bank()
        for c in range(NCH):
            mm(bR[:, c * 128:(c + 1) * 128], tm["Ktp"][0:C, c, :], I_["PbT"][0:C, c, :], True, True, r=[tm["Ktp"], I_["PbT"]], w=[bR])
        for h in range(2):
            hs_ = slice(64 * h, 64 * h + 64)
            tt(chunk3(RpT[hs_, 0:T], C), chunk3(rt[hs_, 0:T], C), v3(bR)[hs_, 0:NCH, h * 64:h * 64 + C], ALU.subtract, r=[rt, bR], w=[RpT])
        bN = bank()
        for c in range(NCH):
            mm(bN[:, c * 128:(c + 1) * 128], I_["khrtm"][0:C, c, :], I_["Vtr"][0:C, c, :], True, False, r=[I_["khrtm"], I_["Vtr"]], w=[bN])
            mm(bN[:, c * 128:(c + 1) * 128], I_["bhtm"][0:C, c, :], tm["Un"][0:C, c, :], False, True, r=[I_["bhtm"], tm["Un"]], w=[bN])
        tt(Ncomb[:, 0:NCH, :], v3(bN)[:, 0:NCH, :], blkpos.unsqueeze(1).broadcast_to([128, NCH, 128]), ALU.mult, r=[bN, blkpos], w=[Ncomb])
        bo = banks[5]
        for c in range(NCH):
            csl = slice(c * C, (c + 1) * C)
            sbf = R0bf[:, c, :] if is_s else Srbf
            sin = R0blk[:, c, :] if is_s else Sr
            sout = Routs[:, c, :] if is_s else Sr
            kS_ = [R0bf] if is_s else [Srbf]
            oo = bo[0:C, c * 128:(c + 1) * 128]
            mm(oo, RpT[:, csl], sbf, True, False, r=[RpT] + kS_, w=[bo])
            for h in range(2):
                hb = slice(h * 64, (h + 1) * 64)
                mm(oo[:, hb], I_["PkT"][0:C, c, h * 64:h * 64 + C], I_["Vtr"][0:C, c, hb], False, False, r=[I_["PkT"], I_["Vtr"]], w=[bo])
                mm(oo[:, hb], I_["PbT"][0:C, c, h * 64:h * 64 + C], tm["Un"][0:C, c, hb], False, h == 1, r=[I_["PbT"], tm["Un"]], w=[bo])
            b3 = bank()
            mm(b3[:, 0:128], Mblk[:, c, :], sbf, True, True, r=[Mblk] + kS_, w=[b3])
            stt(Srt, sin, Gm[:, (c + 1) * C - 1:(c + 1) * C], Ncomb[:, c, :], ALU.mult, ALU.add,
                r=[R0blk if is_s else Sr, Gm, Ncomb], w=[Srt])
            tt(sout, Srt, b3[:, 0:128], ALU.add, r=[Srt, b3], w=[Routs if is_s else Sr])
            if not is_s:
                cp(Srbf, Sr, eng="act")
        cp(Osb2[0:C, 0:NCH, :], bo.rearrange("p (c f) -> p c f", f=128)[0:C, 0:NCH, :], eng="act", r=[bo], w=[Osb2])
        if is_s:
            b_ = bank()
            for q in range(4):
                tr(b_[:, q * 128:(q + 1) * 128], Routs[:, q, :], identF, w=[b_])
            cp(R0in[:, 0:4, :], b_.rearrange("p (q f) -> p q f", f=128), r=[b_], w=[R0in])
            dma("sp", rs[s0:s0 + 4, 2 * j].rearrange("s i j -> i s j"), R0in[0:64, :, 0:64], r=[R0in], w=[rs])
            dma("sp", rs[s0:s0 + 4, 2 * j + 1].rearrange("s i j -> i s j"), R0in[64:128, :, 64:128], r=[R0in], w=[rs])
        elif last:
            b_ = bank(); tr(b_[:, 0:128], Sr, identF, w=[b_]); cp(Srt, b_[:, 0:128])
            dma("sp", rp[2 * j], Srt[0:64, 0:64], r=[Srt], w=[rp]); dma("sp", rp[2 * j + 1], Srt[64:128, 64:128], r=[Srt], w=[rp])
        G2 = NCH * 2
        og = Osb2[0:C, 0:NCH, :].rearrange("p c (h i) -> p (c h) i", i=64)
        oq = Osq2[0:C, 0:NCH, :].rearrange("p c (h i) -> p (c h) i", i=64)
        P.add("dve", lambda e: e.reduce_sum(out=st3[0:C, 0:G2], in_=og, axis=AX.X), r=[Osb2], w=[st3])
        ts(st3[0:C, 0:G2], st3[0:C, 0:G2], 1.0 / 64, None, ALU.mult)
        tt(og, og, st3[0:C, 0:G2].unsqueeze(2).broadcast_to([C, G2, 64]), ALU.subtract, r=[Osb2, st3], w=[Osb2])
        tt(oq, og, og, ALU.mult, eng="pool", r=[Osb2], w=[Osq2])
        P.add("dve", lambda e: e.reduce_sum(out=st4[0:C, 0:G2], in_=oq, axis=AX.X), r=[Osq2], w=[st4])
        rsqrt_(st4[0:C, 0:G2], st4[0:C, 0:G2], 1.0 / 64, LNX_EPS, stC[0:C, 0:G2], stD[0:C, 0:G2])
        tt(tm["onr"][0:C, 0:NCH, :].rearrange("p c (h i) -> p (c h) i", i=64), og, st4[0:C, 0:G2].unsqueeze(2).broadcast_to([C, G2, 64]), ALU.mult,
           r=[Osb2, st4], w=[tm["onr"]])
        b_ = bank(); bb = b_.bitcast(BF16)
        for c in range(NCH):
            tr(bb[:, c * C:(c + 1) * C], tm["onr"][0:C, c, :], identB[0:C, 0:C], r=[tm["onr"], identB], w=[b_])
        ts(X("e1"), bb[:, 0:T], col("lw", j), col("lbb", j), ALU.mult, ALU.add, r=[b_, pv], w=[F_["e1"]])
        tt(X("e1"), X("e1"), bon[:, 0:T], ALU.add, eng="pool", r=[F_["e1"], bon], w=[F_["e1"]])
        tt(X("e1"), X("e1"), Gb[:, 0:T], ALU.mult, eng="pool", r=[F_["e1"], Gb], w=[F_["e1"]])
        P.add("need", ("oTw", gi))
        tt(mT[:, j, c0:c0 + T], X("e1"), oT[:, 0:T], ALU.add, r=[F_["e1"], oT], w=[(mT, ti)])
        P.add("mark", ("oTr", gi))
        P.add("mark", ("B", gi))

    NBLK = 8 if debug is None else debug.get('nblk', 8)

    def hgrn_tile(j, ti, c0, T, C, NCH, is_s, last, wb, oT, gi):
        M_ = masks[C]; F_ = f32t; B_ = bft
        if is_s:
            s0 = (c0 - SEQ) // 4
            dma("sp", S0all, sh_in[s0:s0 + 4, j].rearrange("s k v -> k s v"), r=[sh_in], w=[S0all])
            cp(S0bf, S0all, eng="pool")
        bf_ = proj(wb, 128, 256, ti, c0, T)
        act(F_["th"][:, 0:T], bf_[:, 0:T], AF.Tanh, scale=0.5, r=[bf_], w=[F_["th"]])
        aff(F_["fS"][:, 0:T], F_["th"][:, 0:T], hs1[:, j:j + 1], hs2[:, j:j + 1], r=[F_["th"], hs1, hs2], w=[F_["fS"]])
        aff(F_["kS"][:, 0:T], F_["th"][:, 0:T], hs1n[:, j:j + 1], hs2c[:, j:j + 1], r=[F_["th"], hs1n, hs2c], w=[F_["kS"]])
        cumprod(F_["Pc"], F_["fS"], T, C, NCH, rstH)
        P.add("dve", lambda e, T=T: e.reciprocal(out=F_["rP"][:, 0:T], in_=F_["Pc"][:, 0:T]), r=[F_["Pc"]], w=[F_["rP"]])
        bq = proj(wb, 0, 128, ti, c0, T)
        tt(B_["qt"][:, 0:T], bq[:, 0:T], F_["Pc"][:, 0:T], ALU.mult, r=[bq, F_["Pc"]], w=[B_["qt"]])
        tt(F_["t1"][:, 0:T], F_["kS"][:, 0:T], F_["rP"][:, 0:T], ALU.mult, eng="pool", r=[F_["kS"], F_["rP"]], w=[F_["t1"]])
        cp(B_["kt"][:, 0:T], F_["t1"][:, 0:T], eng="act", r=[F_["t1"]], w=[B_["kt"]])
        tt(chunk3(B_["kh"][:, 0:T], C), chunk3(F_["t1"][:, 0:T], C), lastbc(F_["Pc"], T, C, NCH), ALU.mult, r=[F_["t1"], F_["Pc"]], w=[B_["kh"]])
        bi = proj(wb, 256, 384, ti, c0, T)
        cp(B_["vT"][:, 0:T], bi[:, 0:T], eng="act", r=[bi], w=[B_["vT"]])
        bog = proj(wb, 384, 512, ti, c0, T)
        sigm(F_["t2"][:, 0:T], bog[:, 0:T], r=[bog], w=[F_["t2"]])
        tt(F_["t2"][:, 0:T], bog[:, 0:T], F_["t2"][:, 0:T], ALU.mult, r=[bog, F_["t2"]], w=[F_["t2"]])
        bga = proj(wb, 512, 640, ti, c0, T)
        sigm(F_["Ga"][:, 0:T], bga[:, 0:T], r=[bga], w=[F_["Ga"]])
        tt(F_["Ga"][:, 0:T], F_["Ga"][:, 0:T], F_["t2"][:, 0:T], ALU.mult, eng="pool", r=[F_["Ga"], F_["t2"]], w=[F_["Ga"]])
        to_tm(tm["Vtm"], B_["vT"], T, C, NCH); to_tm(tm["khtm"], B_["kh"], T, C, NCH)
        b_ = bank()
        for c in range(NCH):
            mm(b_[0:C, c * C:(c + 1) * C], B_["kt"][:, c * C:(c + 1) * C], B_["qt"][:, c * C:(c + 1) * C], True, True, w=[b_])
        tt(attnT[0:C, 0:NCH, 0:C], b_[0:C, 0:NCH * C].rearrange("p (c t) -> p c t", t=C),
           M_["incl"].unsqueeze(1).broadcast_to([C, NCH, C]), ALU.mult, r=[b_, M_["incl"]], w=[attnT])
        bo = banks[0]
        for c in range(NCH):
            csl = slice(c * C, (c + 1) * C)
            sbf = S0bf[:, c, :] if is_s else Shbf
            sin = S0all[:, c, :] if is_s else Sh
            sout = Souts[:, c, :] if is_s else Sh
            oo = bo[0:C, c * 128:(c + 1) * 128]
            mm(oo, attnT[0:C, c, 0:C], tm["Vtm"][0:C, c, :], True, False, r=[attnT, tm["Vtm"]], w=[bo])
            mm(oo, B_["qt"][:, csl], sbf, False, True, r=[B_["qt"], S0bf if is_s else Shbf], w=[bo])
            b2 = bank()
            mm(b2[:, 0:128], tm["khtm"][0:C, c, :], tm["Vtm"][0:C, c, :], True, True, r=[tm["khtm"], tm["Vtm"]], w=[b2])
            stt(sout, sin, F_["Pc"][:, (c + 1) * C - 1:(c + 1) * C], b2[:, 0:128], ALU.mult, ALU.add,
                r=[S0all if is_s else Sh, F_["Pc"], b2], w=[Souts if is_s else Sh])
            if not is_s:
                cp(Shbf, Sh, eng="act")
        cp(Osb[0:C, 0:NCH, :], bo.rearrange("p (c f) -> p c f", f=128)[0:C], eng="act", r=[bo], w=[Osb])
        if is_s:
            dma("sp", hs[s0:s0 + 4, j].rearrange("s k v -> k s v"), Souts, r=[Souts], w=[hs])
        elif last:
            dma("sp", hp[j], Sh, r=[Sh], w=[hp])
        tt(Osq[0:C, 0:NCH, :], Osb[0:C, 0:NCH, :], Osb[0:C, 0:NCH, :], ALU.mult, eng="pool", r=[Osb], w=[Osq])
        P.add("dve", lambda e, C=C, NCH=NCH: e.reduce_sum(out=st_[0:C, 0:NCH], in_=Osq[0:C, 0:NCH, :], axis=AX.X), r=[Osq], w=[st_])
        rsqrt_(st_[0:C, 0:NCH], st_[0:C, 0:NCH], 1.0 / 128, RMS_EPS, stA[0:C, 0:NCH], stB[0:C, 0:NCH])
        tt(Osq[0:C, 0:NCH, :], Osb[0:C, 0:NCH, :], st_[0:C, 0:NCH].unsqueeze(2).broadcast_to([C, NCH, 128]), ALU.mult, r=[Osb, st_], w=[Osq])
        tt(tm["onb"][0:C, 0:NCH, :], Osq[0:C, 0:NCH, :], nwbc[0:C, :].unsqueeze(1).broadcast_to([C, NCH, 128]), ALU.mult, r=[Osq, nwbc], w=[tm["onb"]])
        b_ = bank(); bb = b_.bitcast(BF16)
        for c in range(NCH):
            tr(bb[:, c * C:(c + 1) * C], tm["onb"][0:C, c, :], identB[0:C, 0:C], r=[tm["onb"], identB], w=[b_])
        if gi >= 2:
            P.add("need", ("oTr", gi - 2))
        tt(oT[:, 0:T], bb[:, 0:T], F_["Ga"][:, 0:T], ALU.mult, r=[b_, F_["Ga"]], w=[oT])
        P.add("mark", ("oTw", gi))

    def stream(kind):
        gi = 0
        for j in range(NBLK):
            seg = lambda col0: w_in[:, col0 + j * 128:col0 + (j + 1) * 128].rearrange("(k p) c -> p k c", p=128)
            if kind == "H":
                for si, col0 in enumerate((0, 1024, 2048, 3072, 7456)):
                    dma("pool", wbH[:, :, si * 128:(si + 1) * 128], seg(col0), r=[w_in], w=[wbH])
                memset(Sh, 0.0, eng="dve"); memset(Shbf, 0.0, eng="dve")
            elif kind == "RF":
                for si, col0 in enumerate((4096, 5120, 6144, 8480)):
                    dma("pool", wbR[:, :, si * 128:(si + 1) * 128], seg(col0), r=[w_in], w=[wbR])
                memset(prevs[:, 3:6], 0.0, eng="dve", w=[prevs])
            else:
                memset(Sr, 0.0, eng="dve"); memset(Srbf, 0.0, eng="dve")
            for ti, (c0, T, C, NCH, is_s) in enumerate(tiles):
                if debug is not None and 'tsel' in debug and ti not in debug['tsel']:
                    continue
                last = (c0 + T == SEQ)
                if kind == "H":
                    hgrn_tile(j, ti, c0, T, C, NCH, is_s, last, wbH, oTs[gi % 2], gi)
                elif kind == "RF":
                    rwkv_front(j, ti, c0, T, C, NCH, is_s, last, wbR, gi)
                else:
                    rwkv_back(j, ti, c0, T, C, NCH, is_s, last, oTs[gi % 2], gi)
                gi += 1

    opsH = []; opsF = []; opsB = []
    P.sink = opsH; cur[0] = "H"; stream("H")
    P.sink = opsF; cur[0] = "RF"; stream("RF")
    P.sink = opsB; cur[0] = "RB"; stream("RB")
    P.sink = None; cur[0] = None
    _n0 = len(P.ops)
    if debug is not None:
        P.lat_same = debug.get('lat_same', P.lat_same); P.lat_cross = debug.get('lat_cross', P.lat_cross)
    P.merge([opsH, opsF, opsB])
    if debug is not None and debug.get("xdeps"):
        from collections import Counter
        cx = Counter()
        for i in range(_n0, len(P.ops)):
            o = P.ops[i]
            for d in o["deps"]:
                od = P.ops[d]
                if d >= _n0 and od["stream"] != o["stream"]:
                    shared = (set(o["rk"]) | set(o["wk"])) & (set(od["rk"]) | set(od["wk"]))
                    cx[(o["stream"], o["eng"], od["eng"], str(sorted(map(str, shared)))[:80])] += 1
        for k, v in cx.most_common(30):
            print("XDEP", v, k)
    dump("mT", mT.rearrange("p a b -> p (a b)"), [128, 8 * NT])
    for p4 in range(0, 27, 4):
        b_ = bank(); n4 = min(4, 27 - p4); wdt = min(n4 * 128, RC - p4 * 128)
        for q in range(n4):
            tr(b_[0:17, q * 128:(q + 1) * 128], lastc[:, p4 + q, :], identF, r=[lastc, identF], w=[b_])
        cp(srow[:, 0:n4 * 128], b_[0:17, 0:n4 * 128], r=[b_], w=[srow])
        dma("sp", shp[:, p4 * 128:p4 * 128 + wdt], srow[0:1, 0:wdt], r=[srow], w=[shp])
        dma("sp", shs[:, p4 * 128:p4 * 128 + wdt], srow[1:17, 0:wdt], r=[srow], w=[shs])
    if debug is not None and debug.get("_stop") == 2:
        n = P.emit()
        return nc, dbg, n
    print('SBUF phase2 total', SBTOT[0], 'S2', S2.tot)
    S2.close()
    P.barrier(dummy)
    S1 = Scope()
    xT = S1.sb("xT", [128, 8, NT]); h2T = S1.sb("h2T", [128, 8, NT], BF16)
    g2s = S1.sb("g2s", [128, 8, 64]); sh2s = S1.sb("sh2s", [128, 8, 64]); gt1s = S1.sb("gt1s", [128, 8, 64]); gt2s = S1.sb("gt2s", [128, 8, 64])
    mk_mods(gt1s, 2); mk_mods(sh2s, 3); mk_mods(g2s, 4, "n2"); mk_mods(gt2s, 5)
    nbuf = dict(sq=S1.sb("sq", [128, 8, 512], BF16), rstd=S1.sb("rstd", [128, 512]), tmp=S1.sb("ntmp", [128, 8, 512]),
                xtm=[S1.sb("xtm", [128, D]) for _ in range(2)], i=0)
    nbuf["r1"] = nbuf["tmp"][:, 1, :]; nbuf["r2"] = nbuf["tmp"][:, 0, :]
    wo = S1.sb("wo", [128, 8, 128], BF16); wus = [S1.sb("wu", [128, 8, 256], BF16) for _ in range(2)]; wds = [S1.sb("wd", [128, 2, D], BF16)] * 2
    aTs = [S1.sb("aT", [128, 2, 512], BF16) for _ in range(2)]; rTs = [S1.sb("rT", [128, 1, 512], BF16)] * 2; t64 = S1.sb("t64", [128, 64])
    tiles34 = [(i * 512, 512, False) for i in range(4)] + [(SEQ, 64, True)]
    for ti, (c0, T, is_s) in enumerate(tiles34):
        load_xT(xT[:, :, c0:c0 + T], c0, T, ti)

    def resid_add(f, c0, T, is_s, ps, gtp, gts):
        if not is_s:
            stt(xT[:, f, c0:c0 + T], ps[:, 0:T], gtp[:, f:f + 1], xT[:, f, c0:c0 + T], ALU.mult, ALU.add, r=[ps, gtp, xT], w=[xT])
        else:
            tt(t64[:, 0:T], ps[:, 0:T], gts[:, f, :], ALU.mult, r=[ps, gts], w=[t64])
            tt(xT[:, f, c0:c0 + T], xT[:, f, c0:c0 + T], t64[:, 0:T], ALU.add, r=[xT, t64], w=[xT])

    for f in range(8):
        dma("pool", wo, w_out[:, f * 128:(f + 1) * 128].rearrange("(k p) c -> p k c", p=128), r=[w_out], w=[wo])
        for ti, (c0, T, is_s) in enumerate(tiles34):
            b_ = bank()
            for k in range(8):
                mm(b_[:, 0:T], wo[:, k, :], mT[:, k, c0:c0 + T], k == 0, k == 7, r=[wo, mT], w=[b_])
            resid_add(f, c0, T, is_s, b_, gt1p, gt1s)
    for ti, (c0, T, is_s) in enumerate(tiles34):
        norm_to(h2T[:, :, c0:c0 + T], xT[:, :, c0:c0 + T], T, g2p, sh2p, g2s, sh2s, is_s, 0)
    steps = [(g, ti) for g in range(16) for ti in range(len(tiles34))]
    upb = [0]; dnb = [0]

    def load_wu(g):
        wu_ = wus[g % 2]
        dma("pool", wu_, w_up[:, g * 256:(g + 1) * 256].rearrange("(k p) c -> p k c", p=128), r=[w_up], w=[wu_])

    def load_wd(g):
        wd_ = wds[0]
        dma("pool", wd_, w_down[g * 256:(g + 1) * 256, :].rearrange("(q p) c -> p q c", p=128), r=[w_down], w=[wd_])

    def up(k):
        g, ti = steps[k]; c0, T, is_s = tiles34[ti]
        if ti == 0 and g + 1 < 16:
            load_wu(g + 1)
        wu_ = wus[g % 2]; aT_ = aTs[k % 2]; rT_ = rTs[k % 2]
        for q in range(2):
            b_ = banks[upb[0] % 4]; upb[0] += 1
            for kk_ in range(8):
                mm(b_[:, 0:T], wu_[:, kk_, q * 128:(q + 1) * 128], h2T[:, kk_, c0:c0 + T], kk_ == 0, kk_ == 7, r=[wu_, (h2T, 0)], w=[b_])
            act(rT_[:, 0, 0:T], b_[:, 0:T], AF.Relu, r=[b_], w=[rT_])
            tt(aT_[:, q, 0:T], rT_[:, 0, 0:T], rT_[:, 0, 0:T], ALU.mult, eng=("pool" if q == 1 else "dve"), r=[rT_], w=[aT_])

    def down(k):
        g, ti = steps[k]; c0, T, is_s = tiles34[ti]
        wd_ = wds[g % 2]; aT_ = aTs[k % 2]
        for f in range(8):
            b_ = banks[4 + dnb[0] % 4]; dnb[0] += 1
            for q in range(2):
                mm(b_[:, 0:T], wd_[:, q, f * 128:(f + 1) * 128], aT_[:, q, 0:T], q == 0, q == 1, r=[wd_, aT_], w=[b_])
            resid_add(f, c0, T, is_s, b_, gt2p, gt2s)

    load_wu(0); load_wd(0)
    up(0)
    for k in range(len(steps)):
        if k + 1 < len(steps):
            up(k + 1)
        down(k)
        g_, ti_ = steps[k]
        if ti_ == len(tiles34) - 1 and g_ + 1 < 16:
            load_wd(g_ + 1)
    fnp = pv[:, rows["fn"]:rows["fn"] + 8]
    for ti, (c0, T, is_s) in enumerate(tiles34):
        sq = nbuf["sq"]; rstd = nbuf["rstd"]; tmp = nbuf["tmp"]
        xv_ = xT[:, :, c0:c0 + T]
        tt(sq[:, :, 0:T], xv_, xv_, ALU.mult, r=[xT], w=[sq])
        b_ = bank()
        for fc in range(8):
            mm(b_[:, 0:T], onesB, sq[:, fc, 0:T], fc == 0, fc == 7, w=[b_])
        ts(rstd[:, 0:T], b_[:, 0:T], 1.0 / D, RMS_EPS, ALU.mult, ALU.add, r=[b_], w=[rstd])
        r1 = nbuf["r1"]; r2 = nbuf["r2"]
        cp(r1[:, 0:T], rstd[:, 0:T], eng="act", r=[rstd], w=[r1])
        rsqrt_dve(rstd[:, 0:T], r1[:, 0:T], r2[:, 0:T], [rstd, r1, r2])
        tt(tmp[:, :, 0:T], xv_, rstd[:, 0:T].unsqueeze(1).broadcast_to([128, 8, T]), ALU.mult, r=[xT, rstd], w=[tmp])
        tt(tmp[:, :, 0:T], tmp[:, :, 0:T], fnp.unsqueeze(2).broadcast_to([128, 8, T]), ALU.mult, r=[tmp, pv], w=[tmp])
        for sub in range(0, T, 128):
            n = min(128, T - sub)
            ytm = nbuf["xtm"][nbuf["i"] % 2]; nbuf["i"] += 1
            for half in range(2):
                b_ = bank()
                for f4 in range(4):
                    fc = half * 4 + f4
                    tr(b_[0:n, f4 * 128:(f4 + 1) * 128], tmp[:, fc, sub:sub + n], identF, r=[tmp, identF], w=[b_])
                cp(ytm[0:n, half * 512:(half + 1) * 512], b_[0:n, :], eng="act", r=[b_], w=[ytm])
            dst = yp[c0 + sub:c0 + sub + n, :] if not is_s else ys[sub:sub + n, :]
            dma("sp", dst, ytm[0:n, :], r=[ytm], w=[yp if not is_s else ys])
    n = P.emit()
    print('SBUF bytes/partition: G', G.tot, 'max', SBMAX[0])
    S1.close(); G.close()
    return nc, dbg, n


_CACHE = {}


def _core_inputs(inp, c):
    f = lambda a: np.ascontiguousarray(a, dtype=np.float32)
    m = {"xp": inp["x_prompt"][c], "xs": inp["x_sample"][16 * c:16 * c + 16].reshape(64, D),
         "sh": inp["state_hgrn"][0, 16 * c:16 * c + 16], "sr": inp["state_rwkv"][0, 16 * c:16 * c + 16],
         "ss": inp["state_shift"][0, 16 * c:16 * c + 16], "cp": inp["c_prompt"][c:c + 1], "cs": inp["c_sample"][16 * c:16 * c + 16]}
    for k in ("norm1_w", "norm2_w", "rwkv_w0", "rwkv_a0", "rwkv_k_k", "rwkv_k_a", "rwkv_lnx_w", "rwkv_lnx_b", "rwkv_r_k", "final_norm_w"):
        m[k] = inp[k].reshape(8, 128)
    m["ada_w"] = inp["ada_w"][0]; m["ada_b"] = inp["ada_b"].reshape(48, 128); m["w_in"] = inp["w_in"][0]
    m["lb_logits"] = inp["lb_logits"].reshape(16, 128); m["hgrn_norm_w"] = inp["hgrn_norm_w"].reshape(1, 128)
    m["rwkv_mu"] = inp["rwkv_mu"].reshape(1, RC); m["rwkv_w2"] = inp["rwkv_w2"][0]; m["rwkv_a2"] = inp["rwkv_a2"][0]
    m["rwkv_g2"] = inp["rwkv_g2"][0]; m["w_out"] = inp["w_out"][0]; m["w_up"] = inp["w_up"][0]; m["w_down"] = inp["w_down"][0]
    return {k: f(v) for k, v in m.items()}


def kernel(**inputs):
    inp = {k: np.asarray(v) for k, v in inputs.items()}
    if "nc" not in _CACHE:
        _CACHE["nc"] = build()[0]
    nc = _CACHE["nc"]
    in_maps = [_core_inputs(inp, c) for c in range(8)]
    res = run_bass_kernel_spmd(nc, in_maps, core_ids=list(range(8)))
    R = res.results
    g = lambda n: [np.asarray(R[c][n], dtype=np.float32) for c in range(8)]
    y_prompt = np.stack(g("yp"), 0)
    y_sample = np.concatenate([a.reshape(16, 4, D) for a in g("ys")], 0)
    hgrn_p = np.stack(g("hp"), 0)[None]
    rwkv_p = np.stack(g("rp"), 0)[None]
    shift_p = np.concatenate(g("shp"), 0)[None]
    hgrn_s = np.concatenate(g("hs"), 0)[None]
    rwkv_s = np.concatenate(g("rs"), 0)[None]
    shift_s = np.concatenate(g("shs"), 0)[None]
    return (y_prompt, y_sample, hgrn_p, rwkv_p, shift_p, hgrn_s, rwkv_s, shift_s)
```

```python
import contextlib
import numpy as np
import concourse.bass as bass
import concourse.mybir as mybir
from concourse.bass_utils import run_bass_kernel_spmd

F32 = mybir.dt.float32
BF16 = mybir.dt.bfloat16
AF = mybir.ActivationFunctionType
ALU = mybir.AluOpType
AX = mybir.AxisListType

D = 1024
SEQ = 2048
NSEQ_S = 16
TS = 4
NT = SEQ + NSEQ_S * TS
INC = 9504
RC = 3360
NDMASEM = 12
SAME_ENG_RELAX = ()
RMS_EPS = 1e-6
LNX_EPS = 64e-5
C1 = -0.5 * float(np.exp(-0.5))


class Prog:
    def __init__(self, nc):
        self.nc = nc
        self.ops = []
        self.last_w = {}
        self.readers = {}
        self.bar = False
        self.sink = None
        self.lat_same = 0.2
        self.lat_cross = 0.3

    def peek(self, r, w):
        deps = set()
        rk = [self.key(x) for x in r]; wk = [self.key(x) for x in w]
        if self.bar:
            rk.append("__bar")
        for k in rk:
            if k in self.last_w:
                deps.add(self.last_w[k])
        for k in wk:
            if k in self.last_w:
                deps.add(self.last_w[k])
            deps |= self.readers.get(k, set())
        return deps

    @staticmethod
    def dur(eng, dma, n):
        if dma:
            return 0.1
        if eng == "pe":
            return 0.06 + n / 1200.0
        if eng == "dve":
            return 0.10 + n / 960.0
        if eng == "act":
            return 0.25 + n / 1200.0
        if eng == "pool":
            return 0.30 + n / 400.0
        return 0.1

    def merge(self, lists, lead=0):
        L = list(lists); K = len(L)
        done = set(); pos = [0] * K
        end = {}; free = {}
        tot = [max(1, len(x)) for x in L]

        def est_start(x):
            eng, fn, r, w, dma, n = x
            deps = self.peek(r, w)
            t = free.get(eng, 0.0)
            for d in deps:
                if d in end:
                    lat = self.lat_same if self.ops[d]["eng"] == eng else self.lat_cross
                    if self.ops[d]["eng"] == "pe" and eng == "pe":
                        lat = 0.0
                    t = max(t, end[d] + lat)
            return t

        while any(pos[i] < len(L[i]) for i in range(K)):
            progressed = False
            for si in range(K):
                while pos[si] < len(L[si]) and (L[si][pos[si]][0] == "mark" or (L[si][pos[si]][0] == "need" and L[si][pos[si]][1] in done)):
                    if L[si][pos[si]][0] == "mark":
                        done.add(L[si][pos[si]][1])
                    pos[si] += 1; progressed = True
            cand = []
            for si in range(K):
                if pos[si] >= len(L[si]):
                    continue
                x = L[si][pos[si]]
                if x[0] == "need":
                    continue
                cand.append((est_start(x), pos[si] / tot[si], si))
            if not cand:
                if progressed:
                    continue
                if all(pos[i] >= len(L[i]) for i in range(K)):
                    break
                raise RuntimeError("merge deadlock")
            t, _, si = min(cand)
            x = L[si][pos[si]]; pos[si] += 1
            eng, fn, r, w, dma, n = x
            idx = self.add(eng, fn, r, w, dma, n)
            d_ = self.dur(eng, dma, n)
            free[eng] = t + d_
            end[idx] = t + d_ + (2.5 if dma else 0.0)

    @staticmethod
    def key(x):
        if isinstance(x, tuple):
            if isinstance(x[0], str):
                return x
            return (x[0].tensor.name, x[1])
        if isinstance(x, str):
            return x
        return x.tensor.name

    def add(self, eng, fn, r=(), w=(), dma=False, n=256):
        if self.sink is not None:
            if eng in ("mark", "need"):
                self.sink.append((eng, fn))
            else:
                self.sink.append((eng, fn, list(r), list(w), dma, n))
            return None
        deps = set()
        rk = [self.key(x) for x in r]
        wk = [self.key(x) for x in w]
        if self.bar:
            rk.append("__bar")
        for k in rk:
            if k in self.last_w:
                deps.add(self.last_w[k])
        for k in wk:
            if k in self.last_w:
                deps.add(self.last_w[k])
            deps |= self.readers.get(k, set())
        idx = len(self.ops)
        self.ops.append(dict(eng=eng, fn=fn, deps=deps, dma=dma, users=0, stream=getattr(self, "tag", None), rk=rk, wk=wk))
        for k in rk:
            self.readers.setdefault(k, set()).add(idx)
        for k in wk:
            self.last_w[k] = idx
            self.readers[k] = set()
        return idx

    def barrier(self, dummy):
        allk = set(self.last_w.keys()) | set(self.readers.keys())
        allk.discard("__bar")
        self.bar = False
        self.add("dve", lambda e: e.memset(dummy, 0.0), r=(), w=list(allk) + ["__bar"])
        self.last_w = {"__bar": self.last_w["__bar"]}
        self.readers = {}
        self.bar = True

    def emit(self):
        nc = self.nc
        engs = {"pe": nc.tensor, "act": nc.scalar, "dve": nc.vector, "pool": nc.gpsimd, "sp": nc.sync}
        ops = self.ops

        seqn = {}
        for i_, o_ in enumerate(ops):
            if not o_["dma"]:
                seqn[o_["eng"]] = seqn.get(o_["eng"], 0) + 1
                o_["seq"] = seqn[o_["eng"]]

        def skip(d, o):
            if ops[d]["dma"] or o["dma"]:
                return False
            if ops[d]["eng"] != o["eng"]:
                return False
            if o["eng"] == "pe":
                return True
            if o["eng"] in SAME_ENG_RELAX and o["seq"] - ops[d]["seq"] >= 2:
                return True
            return False

        for o in ops:
            for d in o["deps"]:
                if not skip(d, o):
                    ops[d]["users"] += 1
        with contextlib.ExitStack() as st:
            csem = {e: st.enter_context(nc.semaphore("c_" + e)) for e in engs}
            dsem = {e: [st.enter_context(nc.semaphore(f"d_{e}{i}")) for i in range(NDMASEM)]
                    for e in ("act", "pool", "sp")}
            ccnt = {e: 0 for e in engs}
            dcnt = {e: [0] * NDMASEM for e in dsem}
            drr = {e: 0 for e in dsem}
            waited = {e: {} for e in engs}
            sig = {}

            def wait(e, sem, val):
                k = id(sem)
                if val <= 0 or waited[e].get(k, 0) >= val:
                    return
                engs[e].wait_ge(sem, val)
                waited[e][k] = val

            for i, o in enumerate(ops):
                e = o["eng"]
                for d in sorted(o["deps"]):
                    if d not in sig or skip(d, o):
                        continue
                    s, v = sig[d]
                    wait(e, s, v)
                if o["dma"]:
                    j = drr[e]
                    drr[e] = (j + 1) % NDMASEM
                    s = dsem[e][j]
                    wait(e, s, dcnt[e][j])
                    ins = o["fn"](engs[e])
                    dcnt[e][j] += 16
                    ins.then_inc(s, 16)
                    sig[i] = (s, dcnt[e][j])
                else:
                    ins = o["fn"](engs[e])
                    if o["users"] > 0:
                        ccnt[e] += 1
                        ins.then_inc(csem[e], 1)
                        sig[i] = (csem[e], ccnt[e])
            for e in dsem:
                for j in range(NDMASEM):
                    if dcnt[e][j] > 0:
                        nc.sync.wait_ge(dsem[e][j], dcnt[e][j])
        return len(ops)


def build(debug=None):
    nc = bass.Bass("TRN2", target_bir_lowering=False)
    P = Prog(nc)
    din = lambda n, s: nc.dram_tensor(n, list(s), F32, kind="ExternalInput").ap()
    dout = lambda n, s: nc.dram_tensor(n, list(s), F32, kind="ExternalOutput").ap()
    xp = din("xp", [SEQ, D]); xs = din("xs", [64, D])
    sh_in = din("sh", [16, 8, 128, 128]); sr_in = din("sr", [16, 16, 64, 64]); ss_in = din("ss", [16, RC])
    cpr = din("cp", [1, D]); csm = din("cs", [16, D])
    norm1_w = din("norm1_w", [8, 128]); norm2_w = din("norm2_w", [8, 128])
    ada_w = din("ada_w", [D, 6 * D]); ada_b = din("ada_b", [48, 128])
    w_in = din("w_in", [D, INC]); lb_logits = din("lb_logits", [16, 128])
    hgrn_norm_w = din("hgrn_norm_w", [1, 128]); mu = din("rwkv_mu", [1, RC])
    w0 = din("rwkv_w0", [8, 128]); w2 = din("rwkv_w2", [64, D]); a0 = din("rwkv_a0", [8, 128])
    a2 = din("rwkv_a2", [64, D]); g2 = din("rwkv_g2", [160, D]); k_k = din("rwkv_k_k", [8, 128])
    k_a = din("rwkv_k_a", [8, 128]); r_k = din("rwkv_r_k", [8, 128]); lnx_w = din("rwkv_lnx_w", [8, 128])
    lnx_b = din("rwkv_lnx_b", [8, 128]); w_out = din("w_out", [D, D]); w_up = din("w_up", [D, 4 * D])
    w_down = din("w_down", [4 * D, D]); fnw = din("final_norm_w", [8, 128])
    yp = dout("yp", [SEQ, D]); ys = dout("ys", [64, D])
    hp = dout("hp", [8, 128, 128]); rp = dout("rp", [16, 64, 64]); shp = dout("shp", [1, RC])
    hs = dout("hs", [16, 8, 128, 128]); rs = dout("rs", [16, 16, 64, 64]); shs = dout("shs", [16, RC])
    dbg = {}

    cnt = [0]

    def uname(n):
        cnt[0] += 1
        return f"{n}_{cnt[0]}"

    SBTOT = [0]; SBMAX = [0]

    class Scope:
        def __init__(self):
            self.st = contextlib.ExitStack()

        def sb(self, name, shape, dt=F32):
            nb = int(np.prod(shape[1:])) * (4 if dt == F32 else 2)
            SBTOT[0] += nb; self.tot = getattr(self, "tot", 0) + nb
            SBMAX[0] = max(SBMAX[0], SBTOT[0])
            return self.st.enter_context(nc.sbuf_tensor(uname(name), list(shape), dt)).ap()

        def close(self):
            SBTOT[0] -= getattr(self, "tot", 0)
            self.st.close()

    G = Scope()
    banks = [nc.alloc_psum_tensor(f"bank{i}", [128, 512], F32).ap() for i in range(8)]
    bk = [0]

    pools = {None: list(range(8)), "H": [1, 2], "RF": [3, 4], "RB": [6, 7]}
    cur = [None]
    bkp = {"H": 0, "RF": 0, "RB": 0}

    def bank():
        if cur[0] is None:
            b = banks[bk[0] % 8]
            bk[0] += 1
            return b
        pl = pools[cur[0]]
        b = banks[pl[bkp[cur[0]] % len(pl)]]
        bkp[cur[0]] += 1
        return b

    def dump(name, ap, shape, keys=None):
        if debug is None or (name not in debug and "all" not in debug):
            return
        d = nc.dram_tensor("dbg_" + name, list(shape), F32, kind="ExternalOutput").ap()
        dbg[name] = d
        q = "pool" if ap.dtype != F32 else "sp"
        rk = [ap] + [(ap, t_) for t_ in range(16)] if keys is None else keys
        P.add(q, lambda e: e.dma_start(out=d, in_=ap), r=rk, w=[d], dma=True)

    def dma(q, out, in_, r=(), w=()):
        P.add(q, lambda e: e.dma_start(out=out, in_=in_), r=list(r) or [in_], w=list(w) or [out], dma=True)

    def act(out, in_, func, bias=0.0, scale=1.0, r=(), w=(), accum=None):
        rr = list(r) or [in_]
        if not isinstance(bias, float):
            rr.append(bias)
        if not isinstance(scale, float):
            rr.append(scale)
        ww = list(w) or [out]
        if accum is not None:
            ww.append(accum)
        if accum is not None:
            P.add("act", lambda e: e.activation(out=out, in_=in_, func=func, bias=bias, scale=scale, accum_out=accum), r=rr, w=ww, n=out.free_size())
        else:
            P.add("act", lambda e: e.activation(out=out, in_=in_, func=func, bias=bias, scale=scale), r=rr, w=ww, n=out.free_size())

    def tt(out, in0, in1, op, eng="dve", r=(), w=()):
        P.add(eng, lambda e: e.tensor_tensor(out=out, in0=in0, in1=in1, op=op), r=list(r) or [in0, in1], w=list(w) or [out], n=out.free_size())

    def ts(out, in0, s1, s2, op0, op1=None, eng="dve", r=(), w=()):
        rr = list(r) or [in0]
        for s in (s1, s2):
            if s is not None and not isinstance(s, float):
                rr.append(s)
        if op1 is None:
            P.add(eng, lambda e: e.tensor_scalar(out=out, in0=in0, scalar1=s1, scalar2=None, op0=op0), r=rr, w=list(w) or [out], n=out.free_size())
        else:
            P.add(eng, lambda e: e.tensor_scalar(out=out, in0=in0, scalar1=s1, scalar2=s2, op0=op0, op1=op1), r=rr, w=list(w) or [out], n=out.free_size())

    def aff(out, in_, scale, bias, r=(), w=()):
        act(out, in_, AF.Identity, bias=bias, scale=scale, r=r, w=w)

    def stt(out, in0, scalar, in1, op0, op1, r=(), w=()):
        rr = list(r) or [in0, in1]
        if not isinstance(scalar, float):
            rr.append(scalar)
        P.add("dve", lambda e: e.scalar_tensor_tensor(out=out, in0=in0, scalar=scalar, in1=in1, op0=op0, op1=op1), r=rr, w=list(w) or [out], n=out.free_size())

    def cp(out, in_, eng="dve", r=(), w=()):
        if eng == "act":
            act(out, in_, AF.Copy, r=r, w=w)
        else:
            P.add(eng, lambda e: e.tensor_copy(out=out, in_=in_), r=list(r) or [in_], w=list(w) or [out], n=out.free_size())

    def memset(ap, val, eng="pool", w=()):
        P.add(eng, lambda e: e.memset(ap, val), w=list(w) or [ap], n=ap.free_size())

    def mm(out, lhsT, rhs, start, stop, r=(), w=()):
        P.add("pe", lambda e: e.matmul(out, lhsT, rhs, start=start, stop=stop), r=list(r) or [lhsT, rhs], w=list(w) or [out], n=rhs.free_size() + 60)

    def tr(out, in_, ident, r=(), w=()):
        P.add("pe", lambda e: e.transpose(out=out, in_=in_, identity=ident), r=list(r) or [in_, ident], w=list(w) or [out], n=ident.free_size() + 60)

    I32 = mybir.dt.int32

    def rsqrt_dve(y, x, t1, keys):
        P.add("dve", lambda e: e.tensor_scalar(out=y.bitcast(I32), in0=x.bitcast(I32), scalar1=-0.5, scalar2=1597463007.0,
                                               op0=ALU.mult, op1=ALU.add), r=keys, w=keys)
        for _ in range(2):
            tt(t1, y, y, ALU.mult, r=keys, w=keys)
            stt(t1, t1, -0.5, x, ALU.mult, ALU.mult, r=keys, w=keys)
            stt(y, t1, 1.5, y, ALU.add, ALU.mult, r=keys, w=keys)

    def rsqrt_(out, in_, scale, eps, tA, tB):
        ts(tA, in_, scale, eps, ALU.mult, ALU.add)
        rsqrt_dve(out, tA, tB, [out, tA, tB])

    ones128 = G.sb("ones128", [128, 128]); identF = G.sb("identF", [128, 128]); identB = G.sb("identB", [128, 128], BF16)
    memset(ones128, 1.0)
    P.add("pool", lambda e: e.affine_select(out=identF, in_=ones128, pattern=[[1, 128]], compare_op=ALU.is_equal,
                                           fill=0.0, base=0, channel_multiplier=-1), r=[ones128], w=[identF])
    cp(identB, identF)
    onesB = G.sb("onesB", [128, 128], BF16)
    cp(onesB, ones128)
    blkones = G.sb("blkones", [128, 128], BF16)
    memset(blkones, 0.0)
    memset(blkones[0:64, 0:64], 1.0, w=[blkones]); memset(blkones[64:128, 64:128], 1.0, w=[blkones])
    blkneg = G.sb("blkneg", [128, 128])
    memset(blkneg, 0.0)
    memset(blkneg[0:64, 0:64], -1.0, w=[blkneg]); memset(blkneg[64:128, 64:128], -1.0, w=[blkneg])
    blkpos = G.sb("blkpos", [128, 128])
    memset(blkpos, 0.0)
    memset(blkpos[0:64, 0:64], 1.0, w=[blkpos]); memset(blkpos[64:128, 64:128], 1.0, w=[blkpos])
    dummy = G.sb("dummy", [128, 1])
    masks = {}
    for C in (64, 4):
        mi = G.sb(f"mi{C}", [C, C]); ms = G.sb(f"ms{C}", [C, C]); mls = G.sb(f"mls{C}", [C, C])
        P.add("pool", lambda e, mi=mi, C=C: e.affine_select(out=mi, in_=ones128[0:C, 0:C], pattern=[[1, C]], compare_op=ALU.is_ge,
                                                         fill=0.0, base=0, channel_multiplier=-1), r=[ones128], w=[mi])
        P.add("pool", lambda e, ms=ms, C=C: e.affine_select(out=ms, in_=ones128[0:C, 0:C], pattern=[[1, C]], compare_op=ALU.is_gt,
                                                         fill=0.0, base=0, channel_multiplier=-1), r=[ones128], w=[ms])
        P.add("pool", lambda e, mls=mls, C=C: e.affine_select(out=mls, in_=ones128[0:C, 0:C], pattern=[[-1, C]], compare_op=ALU.is_gt,
                                                           fill=0.0, base=0, channel_multiplier=1), r=[ones128], w=[mls])
        msn = G.sb(f"msn{C}", [C, C]); mlsn = G.sb(f"mlsn{C}", [C, C])
        ts(msn, ms, -1.0, None, ALU.mult); ts(mlsn, mls, -1.0, None, ALU.mult)
        masks[C] = dict(incl=mi, strict=ms, nstrict=msn, nlstrict=mlsn)

    stgA = G.sb("stgA", [128, 128]); stgB = G.sb("stgB", [48, 128])
    memset(stgA, 0.0)
    rows = {}
    r0 = 0
    for nm, src, n in (("n1", norm1_w, 8), ("n2", norm2_w, 8), ("lb", lb_logits, 16), ("w0", w0, 8), ("a0", a0, 8),
                       ("kk", k_k, 8), ("ka", k_a, 8), ("rk", r_k, 8), ("lw", lnx_w, 8), ("lbb", lnx_b, 8), ("fn", fnw, 8)):
        dma("sp", stgA[r0:r0 + n, :], src, w=[stgA]); rows[nm] = r0; r0 += n
    dma("sp", stgA[r0:r0 + 26, :], mu[:, 0:3328].rearrange("o (n p) -> (o n) p", p=128), w=[stgA]); rows["mu"] = r0; r0 += 26
    dma("sp", stgA[r0:r0 + 1, 0:32], mu[:, 3328:3360], w=[stgA]); rows["mu2"] = r0; r0 += 1
    assert r0 <= 128
    dma("sp", stgB, ada_b)
    pv = G.sb("pv", [128, 128]); pvB = G.sb("pvB", [128, 48])
    b_ = bank(); tr(b_[:, 0:128], stgA, identF); cp(pv, b_[:, 0:128])
    b_ = bank(); tr(b_[:, 0:48], stgB, identF[0:48, 0:48]); cp(pvB, b_[:, 0:48])
    col = lambda nm, i=0: pv[:, rows[nm] + i: rows[nm] + i + 1]
    pv2 = G.sb("pv2", [128, 128]); ts(pv2, pv, -1.0, 1.0, ALU.mult, ALU.add)
    col2 = lambda nm, i=0: pv2[:, rows[nm] + i: rows[nm] + i + 1]
    hs1 = G.sb("hs1", [128, 8]); hs2 = G.sb("hs2", [128, 8]); hs1n = G.sb("hs1n", [128, 8]); hs2c = G.sb("hs2c", [128, 8])
    tt(hs1, pv[:, rows["lb"]:rows["lb"] + 8], pv[:, rows["lb"] + 8:rows["lb"] + 16], ALU.subtract)
    act(hs2, hs1, AF.Tanh, scale=0.5)
    ts(hs1, hs2, -0.25, 0.25, ALU.mult, ALU.add)
    ts(hs2, hs2, 0.25, 0.75, ALU.mult, ALU.add)
    ts(hs1n, hs1, -1.0, None, ALU.mult)
    ts(hs2c, hs2, -1.0, 1.0, ALU.mult, ALU.add)
    hw0 = G.sb("hw0", [128, 8]); ha0 = G.sb("ha0", [128, 8]); omka = G.sb("omka", [128, 8])
    ts(hw0, pv[:, rows["w0"]:rows["w0"] + 8], 0.5, None, ALU.mult)
    ts(ha0, pv[:, rows["a0"]:rows["a0"] + 8], 0.5, None, ALU.mult)
    ts(omka, pv[:, rows["ka"]:rows["ka"] + 8], -1.0, 1.0, ALU.mult, ALU.add)
    nwbc = G.sb("nwbc", [64, 128]); dma("sp", nwbc, hgrn_norm_w.partition_broadcast(64))

    modT = G.sb("modT", [128, 48, 17])
    g1p = G.sb("g1p", [128, 8]); g2p = G.sb("g2p", [128, 8])
    sh1p = G.sb("sh1p", [128, 8]); sh2p = G.sb("sh2p", [128, 8]); gt1p = G.sb("gt1p", [128, 8]); gt2p = G.sb("gt2p", [128, 8])
    mT = G.sb("mT", [128, 8, NT], BF16)
    S0 = Scope()
    c17 = S0.sb("c17", [17, D]); s17 = S0.sb("s17", [17, D]); s17b = S0.sb("s17b", [17, D], BF16); scT = S0.sb("scT", [128, 8, 32], BF16)
    dma("sp", c17[0:1, :], cpr, w=[c17]); dma("sp", c17[1:17, :], csm, w=[c17])
    act(s17, c17, AF.Tanh, scale=0.5)
    ts(s17, s17, 0.5, 0.5, ALU.mult, ALU.add)
    tt(s17b, s17, c17, ALU.mult)
    b_ = bank(); bb = b_.bitcast(BF16)
    for k in range(8):
        tr(bb[:, k * 32:k * 32 + 17], s17b[:, k * 128:(k + 1) * 128], identB[0:17, 0:17], w=[b_])
    cp(scT[:, :, 0:17], bb[:, 0:256].rearrange("p (k c) -> p k c", c=32)[:, :, 0:17], r=[b_])
    awb = [S0.sb("awb", [128, 8, 512], BF16) for _ in range(2)]
    for g in range(12):
        wt = awb[g % 2]
        dma("pool", wt, ada_w[:, g * 512:(g + 1) * 512].rearrange("(k p) c -> p k c", p=128), r=[ada_w], w=[wt])
        b_ = bank()
        for m in range(4):
            for k in range(8):
                mm(b_[:, m * 32:m * 32 + 17], wt[:, k, m * 128:(m + 1) * 128], scT[:, k, 0:17], k == 0, k == 7, w=[b_])
        for m in range(4):
            ts(modT[:, g * 4 + m, :], b_[:, m * 32:m * 32 + 17], pvB[:, g * 4 + m:g * 4 + m + 1], None, ALU.add, r=[b_, pvB], w=[modT])
    def modp(part):
        return modT[:, part * 8:(part + 1) * 8, 0]
    def mods(part):
        return modT[:, part * 8:(part + 1) * 8, 1:17].unsqueeze(3).broadcast_to([128, 8, 16, 4])
    stt(g1p, modp(1), 1.0, pv[:, rows["n1"]:rows["n1"] + 8], ALU.add, ALU.mult, r=[modT, pv])
    stt(g2p, modp(4), 1.0, pv[:, rows["n2"]:rows["n2"] + 8], ALU.add, ALU.mult, r=[modT, pv])
    for dst, part in ((sh1p, 0), (gt1p, 2), (sh2p, 3), (gt2p, 5)):
        cp(dst, modp(part), r=[modT])
    def mods3(part):
        return modT[:, part * 8:(part + 1) * 8, 1:17]

    def mk_mods(dst, part, nm=None):
        for t4 in range(4):
            if nm is None:
                cp(dst[:, :, t4::4], mods3(part), r=[modT], w=[dst])
            else:
                stt(dst[:, :, t4::4], mods3(part), 1.0, pv[:, rows[nm]:rows[nm] + 8].unsqueeze(2).broadcast_to([128, 8, 16]),
                    ALU.add, ALU.mult, r=[modT, pv], w=[dst])
    S0.close()
    P.barrier(dummy)

    tiles = [(i * 256, 256, 64, 4, False) for i in range(8)] + [(SEQ + 16 * i, 16, 4, 4, True) for i in range(4)]

    def norm_to(hT_dst, xT_t, T, gp, shp_, gs, shs_, is_s, ti, fkeys=False):
        xk_ = [(xT_t, f_) for f_ in range(8)] if fkeys else [xT_t]
        sq = nbuf["sq"]
        tt(sq[:, :, 0:T], xT_t[:, :, 0:T], xT_t[:, :, 0:T], ALU.mult, r=xk_, w=[sq])
        b_ = bank()
        for fc in range(8):
            mm(b_[:, 0:T], onesB, sq[:, fc, 0:T], fc == 0, fc == 7, w=[b_])
        rstd = nbuf["rstd"]
        ts(rstd[:, 0:T], b_[:, 0:T], 1.0 / D, RMS_EPS, ALU.mult, ALU.add, r=[b_], w=[rstd])
        r1 = nbuf["r1"]; r2 = nbuf["r2"]
        cp(r1[:, 0:T], rstd[:, 0:T], eng="act", r=[rstd], w=[r1])
        rsqrt_dve(rstd[:, 0:T], r1[:, 0:T], r2[:, 0:T], [rstd, r1, r2])
        tmp = nbuf["tmp"]
        tt(tmp[:, :, 0:T], xT_t[:, :, 0:T], rstd[:, 0:T].unsqueeze(1).broadcast_to([128, 8, T]), ALU.mult, r=xk_ + [rstd], w=[tmp])
        if not is_s:
            for fc in range(8):
                ts(hT_dst[:, fc, :], tmp[:, fc, 0:T], gp[:, fc:fc + 1], shp_[:, fc:fc + 1], ALU.mult, ALU.add,
                   r=[tmp, gp, shp_], w=[(hT_dst, ti)])
        else:
            tt(tmp[:, :, 0:T], tmp[:, :, 0:T], gs, ALU.mult, r=[tmp, gs], w=[tmp])
            tt(hT_dst, tmp[:, :, 0:T], shs_, ALU.add, r=[tmp, shs_], w=[(hT_dst, ti)])

    def load_xT(xT_t, c0, T, ti, fkeys=False):
        for sub in range(0, T, 128):
            n = min(128, T - sub)
            xtm = nbuf["xtm"][nbuf["i"] % 2]; nbuf["i"] += 1
            src = xp[c0 + sub:c0 + sub + n, :] if c0 < SEQ else xs[c0 - SEQ + sub:c0 - SEQ + sub + n, :]
            dma("sp", xtm[0:n, :], src, w=[xtm])
            for half in range(2):
                b_ = bank()
                for f4 in range(4):
                    fc = half * 4 + f4
                    tr(b_[:, f4 * 128:f4 * 128 + n], xtm[0:n, fc * 128:(fc + 1) * 128], identF[0:n, 0:n], w=[b_])
                cp(xT_t[:, half * 4:half * 4 + 4, sub:sub + n], b_.rearrange("p (f t) -> p f t", t=128)[:, :, 0:n], eng="act", r=[b_],
                   w=([(xT_t, f_) for f_ in range(half * 4, half * 4 + 4)] if fkeys else [xT_t]))

    S2 = Scope()
    hT = S2.sb("hT", [128, 8, NT], BF16)
    S1 = Scope()
    g1s = S1.sb("g1s", [128, 8, 64]); sh1s = S1.sb("sh1s", [128, 8, 64])
    mk_mods(sh1s, 0); mk_mods(g1s, 1, "n1")
    nbuf = dict(sq=S1.sb("sq", [128, 8, 512], BF16), rstd=S1.sb("rstd", [128, 512]), tmp=S1.sb("ntmp", [128, 8, 512]), r1=S1.sb("r1", [128, 512]), r2=S1.sb("r2", [128, 512]),
                xtm=[S1.sb("xtm", [128, D]) for _ in range(2)], i=0)
    xT_ts = [S1.sb("xT_t", [128, 8, 512]) for _ in range(2)]
    for ti, (c0, T, C, NCH, is_s) in enumerate(tiles):
        xT_t = xT_ts[ti % 2]
        load_xT(xT_t, c0, T, ti)
        so = max(c0 - SEQ, 0)
        norm_to(hT[:, :, c0:c0 + T], xT_t, T, g1p, sh1p, g1s[:, :, so:so + min(T, 16)], sh1s[:, :, so:so + min(T, 16)], is_s, ti)
    S1.close()
    P.barrier(dummy)

    W = lambda name, shape, dt=F32: S2.sb(name, shape, dt)
    wbH = W("wbH", [128, 8, 640], BF16); wbR = W("wbR", [128, 8, 512], BF16)
    wl = wbH[:, :, 0:288]
    w2b = W("w2b", [128, D], BF16)
    g2b = W("g2b", [128, D], BF16); g2c = W("g2c", [32, D], BF16)
    TW = W("TW", [128, NT], BF16); SG1 = W("SG1", [128, NT], BF16); SG2 = W("SG2", [32, NT], BF16)
    ssT = W("ssT", [128, 27, 16]); lastc = W("lastc", [128, 27, 17])
    prevs = W("prevs", [128, 8])
    dsh = W("dsh", [128, 256])
    S0all = W("S0all", [128, 4, 128]); S0bf = W("S0bf", [128, 4, 128], BF16); Souts = S0all
    Sh = W("Sh", [128, 128]); Shbf = W("Shbf", [128, 128], BF16)
    R0in = W("R0in", [128, 4, 128]); R0blk = W("R0blk", [128, 4, 128]); R0bf = W("R0bf", [128, 4, 128], BF16); Routs = R0blk
    Sr = W("Sr", [128, 128]); Srbf = W("Srbf", [128, 128], BF16); Srt = W("Srt", [128, 128])
    f32t = {n: W(n, [128, 256]) for n in ("th", "fS", "kS", "Pc", "rP", "Ga", "t1", "t2", "Gmm", "kap", "kmod", "bS", "gS",
                                          "xr", "xk", "xv", "rG", "aS", "u1", "u2", "e1")}
    f32t["n1"] = f32t["kmod"]; f32t["n2"] = f32t["u1"]
    oTs = [W("oT", [128, 256]) for _ in range(2)]
    rstH = W("rstH", [128, 256]); rstR = W("rstR", [128, 256]); memset(rstH, 0.0); memset(rstR, 0.0)
    bft = {n: W(n, [128, 256], BF16) for n in ("qt", "kt", "kh", "vT", "sqb", "kpt", "bt", "bh", "ktl", "khr", "vTr")}
    KR = W("KR", [128, 4, 128], BF16)
    tm = {n: W(n, [64, 4, 128], BF16) for n in ("Vtm", "khtm", "W1", "Un", "Ktp", "onb", "onr")}
    Abm = {n: W(n, [64, 4, 128], BF16) for n in ("Y2", "Z2", "TTa", "TTb")}
    IFs = [dict(**{n: W(n, [64, 4, 128], BF16) for n in ("Z", "Y", "PbT", "PkT", "AkT", "Vtr", "Ktm", "bhtm", "khrtm")},
                rt=W("rt", [128, 256], BF16), **{n: W(n, [128, 256]) for n in ("Gm", "bon", "Gb")}) for _ in range(2)]
    attnT = W("attnT", [64, 4, 64], BF16)
    Osb = W("Osb", [64, 4, 128]); Osq = W("Osq", [64, 4, 128]); st_ = W("st_", [64, 16]); stA = W("stA", [64, 16]); stB = W("stB", [64, 16])
    srow = Osq.rearrange("p c f -> p (c f)")[0:17, :]
    Osb2 = W("Osb2", [64, 4, 128]); Osq2 = W("Osq2", [64, 4, 128]); st3 = W("st3", [64, 16]); st4 = W("st4", [64, 16]); stC = W("stC", [64, 16]); stD = W("stD", [64, 16])
    Mblk = W("Mblk", [128, 4, 128], BF16); Ncomb = W("Ncomb", [128, 4, 128]); RpT = W("RpT", [128, 256], BF16)

    dma("pool", w2b[0:64, :], w2, w=[w2b]); dma("pool", w2b[64:128, :], a2, w=[w2b])
    dma("pool", g2b, g2[0:128, :]); dma("pool", g2c, g2[128:160, :])
    dma("pool", wl, w_in[:, 7168:7456].rearrange("(k p) c -> p k c", p=128), r=[w_in], w=[wl])
    for p4 in range(0, 27, 4):
        n4 = min(4, 27 - p4); wdt = min(n4 * 128, RC - p4 * 128)
        memset(srow, 0.0, eng="dve")
        dma("sp", srow[0:16, 0:wdt], ss_in[:, p4 * 128:p4 * 128 + wdt], r=[ss_in], w=[srow])
        b_ = bank()
        for q in range(n4):
            tr(b_[:, q * 16:(q + 1) * 16], srow[0:16, q * 128:(q + 1) * 128], identF[0:16, 0:16], r=[srow, identF], w=[b_])
        cp(ssT[:, p4:p4 + n4, :], b_[:, 0:n4 * 16].rearrange("p (q s) -> p q s", s=16), r=[b_], w=[ssT])
    memset(prevs, 0.0); memset(lastc, 0.0)

    def proj(wt, c_lo, c_hi, ti, c0, T):
        b_ = bank(); n = c_hi - c_lo
        for k in range(8):
            mm(b_[0:n, 0:T], wt[:, k, c_lo:c_hi], hT[:, k, c0:c0 + T], k == 0, k == 7, r=[wt, (hT, ti)], w=[b_])
        return b_

    def shift_mix(dst, ps, n, T, is_s, mucol, pcol, part, last_tile, s0=0, omucol=None):
        aff(dsh[0:n, 0:T], ps[0:n, 0:T], omucol[0:n, :], 0.0, r=[ps, pv2], w=[dsh])
        if not is_s:
            stt(dst[0:n, 1:T], ps[0:n, 0:T - 1], mucol[0:n, :], dsh[0:n, 1:T], ALU.mult, ALU.add, r=[ps, dsh, pv], w=[dst])
            stt(dst[0:n, 0:1], prevs[0:n, pcol:pcol + 1], mucol[0:n, :], dsh[0:n, 0:1], ALU.mult, ALU.add, r=[prevs, dsh, pv], w=[dst])
            cp(prevs[0:n, pcol:pcol + 1], ps[0:n, T - 1:T], eng="act", r=[ps, dst], w=[prevs])
            if last_tile:
                cp(lastc[0:n, part, 0:1], ps[0:n, T - 1:T], eng="act", r=[ps], w=[lastc])
        else:
            p3 = ps[0:n, 0:T].rearrange("p (s t) -> p s t", t=4); d3 = dsh[0:n, 0:T].rearrange("p (s t) -> p s t", t=4)
            o3 = dst[0:n, 0:T].rearrange("p (s t) -> p s t", t=4)
            stt(o3[:, :, 1:4], p3[:, :, 0:3], mucol[0:n, :], d3[:, :, 1:4], ALU.mult, ALU.add, r=[ps, dsh, pv], w=[dst])
            stt(o3[:, :, 0], ssT[0:n, part, s0:s0 + T // 4], mucol[0:n, :], d3[:, :, 0], ALU.mult, ALU.add, r=[ssT, dsh, pv], w=[dst])
            cp(lastc[0:n, part, 1 + s0:1 + s0 + T // 4], p3[:, :, 3], eng="act", r=[ps], w=[lastc])

    def sigm(dst, src, r=(), w=(), bias=0.0):
        act(dst, src, AF.Tanh, bias=bias, scale=0.5, r=r, w=w)
        aff(dst, dst, 0.5, 0.5)

    for ti, (c0, T, C, NCH, is_s) in enumerate(tiles):
        last = (c0 + T == SEQ)
        b_ = proj(wl, 0, 128, ti, c0, T)
        shift_mix(f32t["t1"], b_, 128, T, is_s, col("mu", 24), 0, 24, last, max(c0 - SEQ, 0) // 4, omucol=col2("mu", 24))
        act(TW[0:64, c0:c0 + T], f32t["t1"][0:64, 0:T], AF.Tanh, r=[f32t["t1"]], w=[(TW, ti)])
        cp(TW[64:128, c0:c0 + T], f32t["t1"][64:128, 0:T], eng="act", r=[f32t["t1"]], w=[(TW, ti)])
        b_ = proj(wl, 128, 256, ti, c0, T)
        shift_mix(f32t["t1"], b_, 128, T, is_s, col("mu", 25), 1, 25, last, max(c0 - SEQ, 0) // 4, omucol=col2("mu", 25))
        sigm(f32t["t2"][:, 0:T], f32t["t1"][:, 0:T], r=[f32t["t1"]], w=[f32t["t2"]])
        cp(SG1[:, c0:c0 + T], f32t["t2"][:, 0:T], r=[f32t["t2"]], w=[(SG1, ti)])
        b_ = proj(wl, 256, 288, ti, c0, T)
        shift_mix(f32t["t1"], b_, 32, T, is_s, col("mu2"), 2, 26, last, max(c0 - SEQ, 0) // 4, omucol=col2("mu2"))
        sigm(f32t["t2"][0:32, 0:T], f32t["t1"][0:32, 0:T], r=[f32t["t1"]], w=[f32t["t2"]])
        cp(SG2[:, c0:c0 + T], f32t["t2"][0:32, 0:T], r=[f32t["t2"]], w=[(SG2, ti)])

    if debug is not None and debug.get("_stop") == "lora":
        dump("TW", TW, [128, NT]); dump("SG1", SG1, [128, NT]); dump("SG2", SG2, [32, NT]); dump("lastc", lastc.rearrange("p a b -> p (a b)"), [128, 27 * 17])
        n = P.emit()
        return nc, dbg, n

    def chunk3(ap, C):
        return ap.rearrange("p (c t) -> p c t", t=C)

    def to_tm(dst, src_bf, T, C, NCH):
        for c4 in range(0, NCH, 8):
            b_ = bank(); bb = b_.bitcast(BF16)
            for c in range(c4, min(c4 + 8, NCH)):
                tr(bb[0:C, (c - c4) * 128:(c - c4 + 1) * 128], src_bf[:, c * C:(c + 1) * C], identB, r=[src_bf, identB], w=[b_])
            n8 = min(8, NCH - c4)
            cp(dst[0:C, c4:c4 + n8, :], bb[0:C, 0:n8 * 128].rearrange("p (c f) -> p c f", f=128), eng="act", r=[b_], w=[dst])

    def cumprod(dst, src, T, C, NCH, rst):
        cp(chunk3(rst[:, 0:T], C)[:, :, 0:1], chunk3(src[:, 0:T], C)[:, :, 0:1], eng="act", r=[src], w=[rst])
        P.add("dve", lambda e: e.tensor_tensor_scan(out=dst[:, 0:T], data0=src[:, 0:T], data1=rst[:, 0:T], initial=1.0,
                                                    op0=ALU.mult, op1=ALU.max), r=[src, rst], w=[dst])
        if C != 64:
            memset(chunk3(rst[:, 0:T], C)[:, :, 0:1], 0.0, eng="pool", w=[rst])

    def lastbc(ap, T, C, NCH):
        return chunk3(ap[:, 0:T], C)[:, :, C - 1:C].broadcast_to([128, NCH, C])


    def rwkv_front(j, ti, c0, T, C, NCH, is_s, last, wb, gi):
        F_ = f32t; B_ = bft; M_ = masks[C]; jc = slice(j * 128, (j + 1) * 128)
        I_ = IFs[gi % 2]
        if gi >= 2:
            P.add("need", ("B", gi - 2))
        X = lambda n: F_[n][:, 0:T]
        Bx = lambda n: B_[n][:, 0:T]
        Gm = I_["Gm"]; bon = I_["bon"]; Gb = I_["Gb"]; rt = I_["rt"]
        so_ = max(c0 - SEQ, 0) // 4
        br = proj(wb, 0, 128, ti, c0, T); shift_mix(F_["xr"], br, 128, T, is_s, col("mu", j), 3, j, last, so_, omucol=col2("mu", j))
        bk_ = proj(wb, 128, 256, ti, c0, T); shift_mix(F_["xk"], bk_, 128, T, is_s, col("mu", 8 + j), 4, 8 + j, last, so_, omucol=col2("mu", 8 + j))
        bv = proj(wb, 256, 384, ti, c0, T); shift_mix(F_["xv"], bv, 128, T, is_s, col("mu", 16 + j), 5, 16 + j, last, so_, omucol=col2("mu", 16 + j))
        bgb = proj(wb, 384, 512, ti, c0, T)
        sigm(Gb[:, 0:T], bgb[:, 0:T], r=[bgb], w=[Gb])
        bw = bank(); mm(bw[:, 0:T], w2b[0:64, jc], TW[0:64, c0:c0 + T], True, True, r=[w2b, (TW, ti)], w=[bw])
        act(X("u1"), bw[:, 0:T], AF.Tanh, bias=hw0[:, j:j + 1], scale=0.5, r=[bw], w=[F_["u1"]])
        act(X("u1"), X("u1"), AF.Exp, bias=C1, scale=C1)
        ba = bank(); mm(ba[:, 0:T], w2b[64:128, jc], TW[64:128, c0:c0 + T], True, True, r=[w2b, (TW, ti)], w=[ba])
        sigm(X("aS"), ba[:, 0:T], r=[ba], w=[F_["aS"]], bias=ha0[:, j:j + 1])
        bg = bank(); mm(bg[:, 0:T], g2b[:, jc], SG1[:, c0:c0 + T], True, False, r=[g2b, (SG1, ti)], w=[bg])
        mm(bg[:, 0:T], g2c[:, jc], SG2[:, c0:c0 + T], False, True, r=[g2c, (SG2, ti)], w=[bg])
        cp(X("gS"), bg[:, 0:T], eng="act", r=[bg], w=[F_["gS"]])
        tt(Gb[:, 0:T], Gb[:, 0:T], X("gS"), ALU.mult, eng="pool", r=[Gb, F_["gS"]], w=[Gb])
        cumprod(Gm, F_["u1"], T, C, NCH, rstR)
        P.add("dve", lambda e: e.reciprocal(out=X("rG"), in_=Gm[:, 0:T]), r=[Gm], w=[F_["rG"]])
        cp(chunk3(X("Gmm"), C)[:, :, 1:C], chunk3(Gm[:, 0:T], C)[:, :, 0:C - 1], r=[Gm], w=[F_["Gmm"]])
        memset(chunk3(X("Gmm"), C)[:, :, 0:1], 1.0, eng="dve", w=[F_["Gmm"]])
        aff(X("kap"), X("xk"), col("kk", j), 0.0, r=[F_["xk"], pv], w=[F_["kap"]])
        tt(Bx("sqb"), X("kap"), X("kap"), ALU.mult, eng="pool")
        bn = bank(); mm(bn[:, 0:T], blkones, Bx("sqb"), True, True, w=[bn])
        ts(X("n1"), bn[:, 0:T], 1.0, 1e-24, ALU.mult, ALU.add, r=[bn], w=[F_["n1"]])
        rsqrt_dve(X("u2"), X("n1"), X("n2"), [F_["u2"], F_["n1"], F_["n2"]])
        tt(X("kap"), X("kap"), X("u2"), ALU.mult)
        aff(X("u2"), X("aS"), col("ka", j), omka[:, j:j + 1], r=[F_["aS"], pv, omka], w=[F_["u2"]])
        tt(X("kmod"), X("xk"), X("u2"), ALU.mult, eng="pool")
        tt(X("bS"), X("aS"), X("kap"), ALU.mult)
        stt(Bx("sqb"), X("xr"), col("rk", j), X("kmod"), ALU.mult, ALU.mult, r=[F_["xr"], F_["kmod"], pv], w=[B_["sqb"]])
        bn2 = bank(); mm(bn2[:, 0:T], blkones, Bx("sqb"), True, True, w=[bn2])
        tt(bon[:, 0:T], bn2[:, 0:T], X("xv"), ALU.mult, r=[bn2, F_["xv"]], w=[bon])
        tt(Bx("kpt"), X("kap"), X("Gmm"), ALU.mult)
        tt(rt[:, 0:T], X("xr"), Gm[:, 0:T], ALU.mult, eng="pool", r=[F_["xr"], Gm], w=[rt])
        cp(KR[:, 0:NCH, 0:C], chunk3(Bx("kpt"), C), eng="pool", r=[B_["kpt"]], w=[KR])
        cp(KR[:, 0:NCH, 64:64 + C], chunk3(rt[:, 0:T], C), eng="pool", r=[rt], w=[KR])
        tt(X("u2"), X("kmod"), X("rG"), ALU.mult); cp(Bx("ktl"), X("u2"), eng="act")
        tt(chunk3(Bx("khr"), C), chunk3(X("u2"), C), lastbc(Gm, T, C, NCH), ALU.mult, r=[F_["u2"], Gm], w=[B_["khr"]])
        tt(X("u2"), X("bS"), X("rG"), ALU.mult); cp(Bx("bt"), X("u2"), eng="act")
        tt(chunk3(Bx("bh"), C), chunk3(X("u2"), C), lastbc(Gm, T, C, NCH), ALU.mult, r=[F_["u2"], Gm], w=[B_["bh"]])
        cp(Bx("vTr"), X("xv"), eng="act")
        to_tm(I_["Vtr"], B_["vTr"], T, C, NCH); to_tm(I_["Ktm"], B_["kpt"], T, C, NCH)
        to_tm(I_["bhtm"], B_["bh"], T, C, NCH); to_tm(I_["khrtm"], B_["khr"], T, C, NCH)
        v4_ = lambda b: b[0:C, :].rearrange("p (c x t) -> p c x t", x=2, t=64)
        mk = lambda m_: m_.unsqueeze(1).broadcast_to([C, NCH, C])
        for h in range(2):
            hs_ = slice(64 * h, 64 * h + 64)
            dv = lambda t_: t_[0:C, 0:NCH, h * 64:h * 64 + C]
            b1 = bank()
            for c in range(NCH):
                mm(b1[0:C, c * 128:(c + 1) * 128], B_["bt"][hs_, c * C:(c + 1) * C], KR[hs_, c, :], True, True, r=[B_["bt"], KR], w=[b1])
            tt(dv(I_["Z"]), v4_(b1)[:, 0:NCH, 0, 0:C], mk(M_["nstrict"]), ALU.mult, r=[b1, M_["nstrict"]], w=[I_["Z"]])
            tt(dv(I_["PbT"]), v4_(b1)[:, 0:NCH, 1, 0:C], mk(M_["incl"]), ALU.mult, r=[b1, M_["incl"]], w=[I_["PbT"]])
            b2 = bank()
            for c in range(NCH):
                mm(b2[0:C, c * 128:(c + 1) * 128], B_["ktl"][hs_, c * C:(c + 1) * C], KR[hs_, c, :], True, True, r=[B_["ktl"], KR], w=[b2])
            tt(dv(I_["AkT"]), v4_(b2)[:, 0:NCH, 0, 0:C], mk(M_["strict"]), ALU.mult, r=[b2, M_["strict"]], w=[I_["AkT"]])
            tt(dv(I_["PkT"]), v4_(b2)[:, 0:NCH, 1, 0:C], mk(M_["incl"]), ALU.mult, r=[b2, M_["incl"]], w=[I_["PkT"]])
            b3 = bank()
            for c in range(NCH):
                mm(b3[0:C, c * 128:c * 128 + C], B_["kpt"][hs_, c * C:(c + 1) * C], B_["bt"][hs_, c * C:(c + 1) * C], True, True, r=[B_["kpt"], B_["bt"]], w=[b3])
            tt(dv(I_["Y"]), v4_(b3)[:, 0:NCH, 0, 0:C], mk(M_["nlstrict"]), ALU.mult, r=[b3, M_["nlstrict"]], w=[I_["Y"]])
        P.add("mark", ("F", gi))

    def rwkv_back(j, ti, c0, T, C, NCH, is_s, last, oT, gi):
        F_ = f32t; M_ = masks[C]
        I_ = IFs[gi % 2]
        L = 6 if C == 64 else 2
        X = lambda n: F_[n][:, 0:T]
        Gm = I_["Gm"]; bon = I_["bon"]; Gb = I_["Gb"]; rt = I_["rt"]
        if is_s:
            s0 = (c0 - SEQ) // 4
            memset(R0in, 0.0)
            dma("sp", R0in[0:64, :, 0:64], sr_in[s0:s0 + 4, 2 * j].rearrange("s i j -> i s j"), r=[sr_in], w=[R0in])
            dma("sp", R0in[64:128, :, 64:128], sr_in[s0:s0 + 4, 2 * j + 1].rearrange("s i j -> i s j"), r=[sr_in], w=[R0in])
            b_ = bank()
            for q in range(4):
                tr(b_[:, q * 128:(q + 1) * 128], R0in[:, q, :], identF, w=[b_])
            cp(R0blk[:, 0:4, :], b_.rearrange("p (q f) -> p q f", f=128), r=[b_], w=[R0blk])
            cp(R0bf, R0blk, eng="pool")
        P.add("need", ("F", gi))
        hv = lambda t_, c_lo, c_hi: t_[0:C, c_lo:c_hi, :].rearrange("p c (h t) -> p c h t", t=64)[:, :, :, 0:C]
        tt(hv(Abm["TTa"], 0, NCH), hv(I_["Z"], 0, NCH), identF[0:C, 0:C].unsqueeze(1).unsqueeze(1).broadcast_to([C, NCH, 2, C]), ALU.add,
           r=[I_["Z"], identF], w=[Abm["TTa"]])
        Yc, Zc, Yn, Zn, TTc, TTn = I_["Y"], I_["Z"], Abm["Y2"], Abm["Z2"], Abm["TTa"], Abm["TTb"]
        pr = [(c, h) for c in range(NCH) for h in range(2)]

        def batch_mm(dst, lhs, rhs, wid_l, wid_r, post):
            for p8 in range(0, len(pr), 8):
                b_ = bank(); grp = pr[p8:p8 + 8]
                for q, (c, h) in enumerate(grp):
                    mm(b_[0:C, q * 64:q * 64 + wid_r], lhs[0:C, c, h * 64:h * 64 + wid_l], rhs[0:C, c, h * 64:h * 64 + wid_r], True, True,
                       r=[lhs, rhs], w=[b_])
                cA = grp[0][0]; cB = grp[-1][0] + 1
                src = b_[0:C, 0:(cB - cA) * 128].rearrange("p (c h t) -> p c h t", h=2, t=64)[:, :, :, 0:wid_r]
                d = dst[0:C, cA:cB, :].rearrange("p c (h t) -> p c h t", t=64)[:, :, :, 0:wid_r]
                post(d, src, b_)

        cpy = lambda d, s_, b_: cp(d, s_, eng="act", r=[b_], w=[d])
        for n in range(1, L):
            batch_mm(Yn, Zc, Yc, C, C, cpy)
            if n < L - 1:
                batch_mm(Zn, Yc, Zc, C, C, cpy)
            for p8 in range(0, len(pr), 8):
                b_ = bank(); grp = pr[p8:p8 + 8]
                for q, (c, h) in enumerate(grp):
                    mm(b_[0:C, q * 64:q * 64 + C], Yn[0:C, c, h * 64:h * 64 + C], TTc[0:C, c, h * 64:h * 64 + C], True, True, r=[Yn, TTc], w=[b_])
                cA = grp[0][0]; cB = grp[-1][0] + 1
                src = b_[0:C, 0:(cB - cA) * 128].rearrange("p (c h t) -> p c h t", h=2, t=64)[:, :, :, 0:C]
                tt(hv(TTn, cA, cB), src, hv(TTc, cA, cB), ALU.add, r=[b_, TTc], w=[TTn])
            Yc, Yn = Yn, Yc; Zc, Zn = Zn, Zc; TTc, TTn = TTn, TTc
        TTf = TTc
        batch_mm(tm["W1"], I_["AkT"], I_["Vtr"], C, 64, cpy)
        batch_mm(tm["Ktp"], TTf, I_["Ktm"], C, 64, cpy)
        neg = lambda d, s_, b_: ts(d, s_, -1.0, None, ALU.mult, r=[b_], w=[d])
        batch_mm(tm["Un"], TTf, tm["W1"], C, 64, neg)
        v3 = lambda b: b.rearrange("p (c f) -> p c f", f=128)
        bM = bank()
        for c in range(NCH):
            mm(bM[:, c * 128:(c + 1) * 128], tm["Ktp"][0:C, c, :], I_["bhtm"][0:C, c, :], True, True, r=[tm["Ktp"], I_["bhtm"]], w=[bM])
        tt(Mblk[:, 0:NCH, :], v3(bM)[:, 0:NCH, :], blkneg.unsqueeze(1).broadcast_to([128, NCH, 128]), ALU.mult, r=[bM, blkneg], w=[Mblk])
        bR = bank()
        for c in range(NCH):
            mm(bR[:, c * 128:(c + 1) * 128], tm["Ktp"][0:C, c, :], I_["PbT"][0:C, c, :], True, True, r=[tm["Ktp"], I_["PbT"]], w=[bR])
        for h in range(2):
            hs_ = slice(64 * h, 64 * h + 64)
            tt(chunk3(RpT[hs_, 0:T], C), chunk3(rt[hs_, 0:T], C), v3(bR)[hs_, 0:NCH, h * 64:h * 64 + C], ALU.subtract, r=[rt, bR], w=[RpT])
        bN = bank()
        for c in range(NCH):
            mm(bN[:, c * 128:(c + 1) * 128], I_["khrtm"][0:C, c, :], I_["Vtr"][0:C, c, :], True, False, r=[I_["khrtm"], I_["Vtr"]], w=[bN])
            mm(bN[:, c * 128:(c + 1) * 128], I_["bhtm"][0:C, c, :], tm["Un"][0:C, c, :], False, True, r=[I_["bhtm"], tm["Un"]], w=[bN])
        tt(Ncomb[:, 0:NCH, :], v3(bN)[:, 0:NCH, :], blkpos.unsqueeze(1).broadcast_to([128, NCH, 128]), ALU.mult, r=[bN, blkpos], w=[Ncomb])
        bo = banks[5]
        for c in range(NCH):
            csl = slice(c * C, (c + 1) * C)
            sbf = R0bf[:, c, :] if is_s else Srbf
            sin = R0blk[:, c, :] if is_s else Sr
            sout = Routs[:, c, :] if is_s else Sr
            kS_ = [R0bf] if is_s else [Srbf]
            oo = bo[0:C, c * 128:(c + 1) * 128]
            mm(oo, RpT[:, csl], sbf, True, False, r=[RpT] + kS_, w=[bo])
            for h in range(2):
                hb = slice(h * 64, (h + 1) * 64)
                mm(oo[:, hb], I_["PkT"][0:C, c, h * 64:h * 64 + C], I_["Vtr"][0:C, c, hb], False, False, r=[I_["PkT"], I_["Vtr"]], w=[bo])
                mm(oo[:, hb], I_["PbT"][0:C, c, h * 64:h * 64 + C], tm["Un"][0:C, c, hb], False, h == 1, r=[I_["PbT"], tm["Un"]], w=[bo])
            b3 = bank()
            mm(b3[:, 0:128], Mblk[:, c, :], sbf, True, True, r=[Mblk] + kS_, w=[b3])
            stt(Srt, sin, Gm[:, (c + 1) * C - 1:(c + 1) * C], Ncomb[:, c, :], ALU.mult, ALU.add,
                r=[R0blk if is_s else Sr, Gm, Ncomb], w=[Srt])
            tt(sout, Srt, b3[:, 0:128], ALU.add, r=[Srt, b3], w=[Routs if is_s else Sr])
            if not is_s:
                cp(Srbf, Sr, eng="act")
        cp(Osb2[0:C, 0:NCH, :], bo.rearrange("p (c f) -> p c f", f=128)[0:C, 0:NCH, :], eng="act", r=[bo], w=[Osb2])
        if is_s:
            b_ = bank()
            for q in range(4):
                tr(b_[:, q * 128:(q + 1) * 128], Routs[:, q, :], identF, w=[b_])
            cp(R0in[:, 0:4, :], b_.rearrange("p (q f) -> p q f", f=128), r=[b_], w=[R0in])
            dma("sp", rs[s0:s0 + 4, 2 * j].rearrange("s i j -> i s j"), R0in[0:64, :, 0:64], r=[R0in], w=[rs])
            dma("sp", rs[s0:s0 + 4, 2 * j + 1].rearrange("s i j -> i s j"), R0in[64:128, :, 64:128], r=[R0in], w=[rs])
        elif last:
            b_ = bank(); tr(b_[:, 0:128], Sr, identF, w=[b_]); cp(Srt, b_[:, 0:128])
            dma("sp", rp[2 * j], Srt[0:64, 0:64], r=[Srt], w=[rp]); dma("sp", rp[2 * j + 1], Srt[64:128, 64:128], r=[Srt], w=[rp])
        G2 = NCH * 2
        og = Osb2[0:C, 0:NCH, :].rearrange("p c (h i) -> p (c h) i", i=64)
        oq = Osq2[0:C, 0:NCH, :].rearrange("p c (h i) -> p (c h) i", i=64)
        P.add("dve", lambda e: e.reduce_sum(out=st3[0:C, 0:G2], in_=og, axis=AX.X), r=[Osb2], w=[st3])
        ts(st3[0:C, 0:G2], st3[0:C, 0:G2], 1.0 / 64, None, ALU.mult)
        tt(og, og, st3[0:C, 0:G2].unsqueeze(2).broadcast_to([C, G2, 64]), ALU.subtract, r=[Osb2, st3], w=[Osb2])
        tt(oq, og, og, ALU.mult, eng="pool", r=[Osb2], w=[Osq2])
        P.add("dve", lambda e: e.reduce_sum(out=st4[0:C, 0:G2], in_=oq, axis=AX.X), r=[Osq2], w=[st4])
        rsqrt_(st4[0:C, 0:G2], st4[0:C, 0:G2], 1.0 / 64, LNX_EPS, stC[0:C, 0:G2], stD[0:C, 0:G2])
        tt(tm["onr"][0:C, 0:NCH, :].rearrange("p c (h i) -> p (c h) i", i=64), og, st4[0:C, 0:G2].unsqueeze(2).broadcast_to([C, G2, 64]), ALU.mult,
           r=[Osb2, st4], w=[tm["onr"]])
        b_ = bank(); bb = b_.bitcast(BF16)
        for c in range(NCH):
            tr(bb[:, c * C:(c + 1) * C], tm["onr"][0:C, c, :], identB[0:C, 0:C], r=[tm["onr"], identB], w=[b_])
        ts(X("e1"), bb[:, 0:T], col("lw", j), col("lbb", j), ALU.mult, ALU.add, r=[b_, pv], w=[F_["e1"]])
        tt(X("e1"), X("e1"), bon[:, 0:T], ALU.add, eng="pool", r=[F_["e1"], bon], w=[F_["e1"]])
        tt(X("e1"), X("e1"), Gb[:, 0:T], ALU.mult, eng="pool", r=[F_["e1"], Gb], w=[F_["e1"]])
        P.add("need", ("oTw", gi))
        tt(mT[:, j, c0:c0 + T], X("e1"), oT[:, 0:T], ALU.add, r=[F_["e1"], oT], w=[(mT, ti)])
        P.add("mark", ("oTr", gi))
        P.add("mark", ("B", gi))

    NBLK = 8 if debug is None else debug.get('nblk', 8)

    def hgrn_tile(j, ti, c0, T, C, NCH, is_s, last, wb, oT, gi):
        M_ = masks[C]; F_ = f32t; B_ = bft
        if is_s:
            s0 = (c0 - SEQ) // 4
            dma("sp", S0all, sh_in[s0:s0 + 4, j].rearrange("s k v -> k s v"), r=[sh_in], w=[S0all])
            cp(S0bf, S0all, eng="pool")
        bf_ = proj(wb, 128, 256, ti, c0, T)
        act(F_["th"][:, 0:T], bf_[:, 0:T], AF.Tanh, scale=0.5, r=[bf_], w=[F_["th"]])
        aff(F_["fS"][:, 0:T], F_["th"][:, 0:T], hs1[:, j:j + 1], hs2[:, j:j + 1], r=[F_["th"], hs1, hs2], w=[F_["fS"]])
        aff(F_["kS"][:, 0:T], F_["th"][:, 0:T], hs1n[:, j:j + 1], hs2c[:, j:j + 1], r=[F_["th"], hs1n, hs2c], w=[F_["kS"]])
        cumprod(F_["Pc"], F_["fS"], T, C, NCH, rstH)
        P.add("dve", lambda e, T=T: e.reciprocal(out=F_["rP"][:, 0:T], in_=F_["Pc"][:, 0:T]), r=[F_["Pc"]], w=[F_["rP"]])
        bq = proj(wb, 0, 128, ti, c0, T)
        tt(B_["qt"][:, 0:T], bq[:, 0:T], F_["Pc"][:, 0:T], ALU.mult, r=[bq, F_["Pc"]], w=[B_["qt"]])
        tt(F_["t1"][:, 0:T], F_["kS"][:, 0:T], F_["rP"][:, 0:T], ALU.mult, eng="pool", r=[F_["kS"], F_["rP"]], w=[F_["t1"]])
        cp(B_["kt"][:, 0:T], F_["t1"][:, 0:T], eng="act", r=[F_["t1"]], w=[B_["kt"]])
        tt(chunk3(B_["kh"][:, 0:T], C), chunk3(F_["t1"][:, 0:T], C), lastbc(F_["Pc"], T, C, NCH), ALU.mult, r=[F_["t1"], F_["Pc"]], w=[B_["kh"]])
        bi = proj(wb, 256, 384, ti, c0, T)
        cp(B_["vT"][:, 0:T], bi[:, 0:T], eng="act", r=[bi], w=[B_["vT"]])
        bog = proj(wb, 384, 512, ti, c0, T)
        sigm(F_["t2"][:, 0:T], bog[:, 0:T], r=[bog], w=[F_["t2"]])
        tt(F_["t2"][:, 0:T], bog[:, 0:T], F_["t2"][:, 0:T], ALU.mult, r=[bog, F_["t2"]], w=[F_["t2"]])
        bga = proj(wb, 512, 640, ti, c0, T)
        sigm(F_["Ga"][:, 0:T], bga[:, 0:T], r=[bga], w=[F_["Ga"]])
        tt(F_["Ga"][:, 0:T], F_["Ga"][:, 0:T], F_["t2"][:, 0:T], ALU.mult, eng="pool", r=[F_["Ga"], F_["t2"]], w=[F_["Ga"]])
        to_tm(tm["Vtm"], B_["vT"], T, C, NCH); to_tm(tm["khtm"], B_["kh"], T, C, NCH)
        b_ = bank()
        for c in range(NCH):
            mm(b_[0:C, c * C:(c + 1) * C], B_["kt"][:, c * C:(c + 1) * C], B_["qt"][:, c * C:(c + 1) * C], True, True, w=[b_])
        tt(attnT[0:C, 0:NCH, 0:C], b_[0:C, 0:NCH * C].rearrange("p (c t) -> p c t", t=C),
           M_["incl"].unsqueeze(1).broadcast_to([C, NCH, C]), ALU.mult, r=[b_, M_["incl"]], w=[attnT])
        bo = banks[0]
        for c in range(NCH):
            csl = slice(c * C, (c + 1) * C)
            sbf = S0bf[:, c, :] if is_s else Shbf
            sin = S0all[:, c, :] if is_s else Sh
            sout = Souts[:, c, :] if is_s else Sh
            oo = bo[0:C, c * 128:(c + 1) * 128]
            mm(oo, attnT[0:C, c, 0:C], tm["Vtm"][0:C, c, :], True, False, r=[attnT, tm["Vtm"]], w=[bo])
            mm(oo, B_["qt"][:, csl], sbf, False, True, r=[B_["qt"], S0bf if is_s else Shbf], w=[bo])
            b2 = bank()
            mm(b2[:, 0:128], tm["khtm"][0:C, c, :], tm["Vtm"][0:C, c, :], True, True, r=[tm["khtm"], tm["Vtm"]], w=[b2])
            stt(sout, sin, F_["Pc"][:, (c + 1) * C - 1:(c + 1) * C], b2[:, 0:128], ALU.mult, ALU.add,
                r=[S0all if is_s else Sh, F_["Pc"], b2], w=[Souts if is_s else Sh])
            if not is_s:
                cp(Shbf, Sh, eng="act")
        cp(Osb[0:C, 0:NCH, :], bo.rearrange("p (c f) -> p c f", f=128)[0:C], eng="act", r=[bo], w=[Osb])
        if is_s:
            dma("sp", hs[s0:s0 + 4, j].rearrange("s k v -> k s v"), Souts, r=[Souts], w=[hs])
        elif last:
            dma("sp", hp[j], Sh, r=[Sh], w=[hp])
        tt(Osq[0:C, 0:NCH, :], Osb[0:C, 0:NCH, :], Osb[0:C, 0:NCH, :], ALU.mult, eng="pool", r=[Osb], w=[Osq])
        P.add("dve", lambda e, C=C, NCH=NCH: e.reduce_sum(out=st_[0:C, 0:NCH], in_=Osq[0:C, 0:NCH, :], axis=AX.X), r=[Osq], w=[st_])
        rsqrt_(st_[0:C, 0:NCH], st_[0:C, 0:NCH], 1.0 / 128, RMS_EPS, stA[0:C, 0:NCH], stB[0:C, 0:NCH])
        tt(Osq[0:C, 0:NCH, :], Osb[0:C, 0:NCH, :], st_[0:C, 0:NCH].unsqueeze(2).broadcast_to([C, NCH, 128]), ALU.mult, r=[Osb, st_], w=[Osq])
        tt(tm["onb"][0:C, 0:NCH, :], Osq[0:C, 0:NCH, :], nwbc[0:C, :].unsqueeze(1).broadcast_to([C, NCH, 128]), ALU.mult, r=[Osq, nwbc], w=[tm["onb"]])
        b_ = bank(); bb = b_.bitcast(BF16)
        for c in range(NCH):
            tr(bb[:, c * C:(c + 1) * C], tm["onb"][0:C, c, :], identB[0:C, 0:C], r=[tm["onb"], identB], w=[b_])
        if gi >= 2:
            P.add("need", ("oTr", gi - 2))
        tt(oT[:, 0:T], bb[:, 0:T], F_["Ga"][:, 0:T], ALU.mult, r=[b_, F_["Ga"]], w=[oT])
        P.add("mark", ("oTw", gi))

    def stream(kind):
        gi = 0
        for j in range(NBLK):
            seg = lambda col0: w_in[:, col0 + j * 128:col0 + (j + 1) * 128].rearrange("(k p) c -> p k c", p=128)
            if kind == "H":
                for si, col0 in enumerate((0, 1024, 2048, 3072, 7456)):
                    dma("pool", wbH[:, :, si * 128:(si + 1) * 128], seg(col0), r=[w_in], w=[wbH])
                memset(Sh, 0.0, eng="dve"); memset(Shbf, 0.0, eng="dve")
            elif kind == "RF":
                for si, col0 in enumerate((4096, 5120, 6144, 8480)):
                    dma("pool", wbR[:, :, si * 128:(si + 1) * 128], seg(col0), r=[w_in], w=[wbR])
                memset(prevs[:, 3:6], 0.0, eng="dve", w=[prevs])
            else:
                memset(Sr, 0.0, eng="dve"); memset(Srbf, 0.0, eng="dve")
            for ti, (c0, T, C, NCH, is_s) in enumerate(tiles):
                if debug is not None and 'tsel' in debug and ti not in debug['tsel']:
                    continue
                last = (c0 + T == SEQ)
                if kind == "H":
                    hgrn_tile(j, ti, c0, T, C, NCH, is_s, last, wbH, oTs[gi % 2], gi)
                elif kind == "RF":
                    rwkv_front(j, ti, c0, T, C, NCH, is_s, last, wbR, gi)
                else:
                    rwkv_back(j, ti, c0, T, C, NCH, is_s, last, oTs[gi % 2], gi)
                gi += 1

    opsH = []; opsF = []; opsB = []
    P.sink = opsH; cur[0] = "H"; stream("H")
    P.sink = opsF; cur[0] = "RF"; stream("RF")
    P.sink = opsB; cur[0] = "RB"; stream("RB")
    P.sink = None; cur[0] = None
    _n0 = len(P.ops)
    if debug is not None:
        P.lat_same = debug.get('lat_same', P.lat_same); P.lat_cross = debug.get('lat_cross', P.lat_cross)
    P.merge([opsH, opsF, opsB])
    if debug is not None and debug.get("xdeps"):
        from collections import Counter
        cx = Counter()
        for i in range(_n0, len(P.ops)):
            o = P.ops[i]
            for d in o["deps"]:
                od = P.ops[d]
                if d >= _n0 and od["stream"] != o["stream"]:
                    shared = (set(o["rk"]) | set(o["wk"])) & (set(od["rk"]) | set(od["wk"]))
                    cx[(o["stream"], o["eng"], od["eng"], str(sorted(map(str, shared)))[:80])] += 1
        for k, v in cx.most_common(30):
            print("XDEP", v, k)
    dump("mT", mT.rearrange("p a b -> p (a b)"), [128, 8 * NT])
    for p4 in range(0, 27, 4):
        b_ = bank(); n4 = min(4, 27 - p4); wdt = min(n4 * 128, RC - p4 * 128)
        for q in range(n4):
            tr(b_[0:17, q * 128:(q + 1) * 128], lastc[:, p4 + q, :], identF, r=[lastc, identF], w=[b_])
        cp(srow[:, 0:n4 * 128], b_[0:17, 0:n4 * 128], r=[b_], w=[srow])
        dma("sp", shp[:, p4 * 128:p4 * 128 + wdt], srow[0:1, 0:wdt], r=[srow], w=[shp])
        dma("sp", shs[:, p4 * 128:p4 * 128 + wdt], srow[1:17, 0:wdt], r=[srow], w=[shs])
    if debug is not None and debug.get("_stop") == 2:
        n = P.emit()
        return nc, dbg, n
    print('SBUF phase2 total', SBTOT[0], 'S2', S2.tot)
    S2.close()
    P.barrier(dummy)
    S1 = Scope()
    xT = S1.sb("xT", [128, 8, NT]); h2T = S1.sb("h2T", [128, 8, NT], BF16)
    g2s = S1.sb("g2s", [128, 8, 64]); sh2s = S1.sb("sh2s", [128, 8, 64]); gt1s = S1.sb("gt1s", [128, 8, 64]); gt2s = S1.sb("gt2s", [128, 8, 64])
    mk_mods(gt1s, 2); mk_mods(sh2s, 3); mk_mods(g2s, 4, "n2"); mk_mods(gt2s, 5)
    nbuf = dict(sq=S1.sb("sq", [128, 8, 512], BF16), rstd=S1.sb("rstd", [128, 512]), tmp=S1.sb("ntmp", [128, 8, 512]),
                xtm=[S1.sb("xtm", [128, D]) for _ in range(2)], i=0)
    nbuf["r1"] = nbuf["tmp"][:, 1, :]; nbuf["r2"] = nbuf["tmp"][:, 0, :]
    wo = S1.sb("wo", [128, 8, 128], BF16); wus = [S1.sb("wu", [128, 8, 256], BF16) for _ in range(2)]; wds = [S1.sb("wd", [128, 2, D], BF16)] * 2
    aTs = [S1.sb("aT", [128, 2, 512], BF16) for _ in range(2)]; rTs = [S1.sb("rT", [128, 1, 512], BF16)] * 2; t64 = S1.sb("t64", [128, 64])
    rtmp = [nbuf["rstd"], nbuf["tmp"][:, 0, :]]; rti = [0]
    tiles34 = [(i * 512, 512, False) for i in range(4)] + [(SEQ, 64, True)]
    for ti, (c0, T, is_s) in enumerate(tiles34):
        load_xT(xT[:, :, c0:c0 + T], c0, T, ti, fkeys=True)

    def resid_add(f, c0, T, is_s, ps, gtp, gts, split=False):
        kx = [(xT, f)]
        if not is_s:
            if split:
                tq = rtmp[rti[0] % len(rtmp)]; rti[0] += 1
                aff(tq[:, 0:T], ps[:, 0:T], gtp[:, f:f + 1], 0.0, r=[ps, gtp], w=[tq])
                tt(xT[:, f, c0:c0 + T], xT[:, f, c0:c0 + T], tq[:, 0:T], ALU.add, eng="pool", r=kx + [tq], w=kx)
            else:
                stt(xT[:, f, c0:c0 + T], ps[:, 0:T], gtp[:, f:f + 1], xT[:, f, c0:c0 + T], ALU.mult, ALU.add, r=[ps, gtp] + kx, w=kx)
        else:
            tt(t64[:, 0:T], ps[:, 0:T], gts[:, f, :], ALU.mult, r=[ps, gts], w=[t64])
            tt(xT[:, f, c0:c0 + T], xT[:, f, c0:c0 + T], t64[:, 0:T], ALU.add, r=kx + [t64], w=kx)

    for f in range(8):
        dma("pool", wo, w_out[:, f * 128:(f + 1) * 128].rearrange("(k p) c -> p k c", p=128), r=[w_out], w=[wo])
        for ti, (c0, T, is_s) in enumerate(tiles34):
            b_ = bank()
            for k in range(8):
                mm(b_[:, 0:T], wo[:, k, :], mT[:, k, c0:c0 + T], k == 0, k == 7, r=[wo, mT], w=[b_])
            resid_add(f, c0, T, is_s, b_, gt1p, gt1s)
    for ti, (c0, T, is_s) in enumerate(tiles34):
        norm_to(h2T[:, :, c0:c0 + T], xT[:, :, c0:c0 + T], T, g2p, sh2p, g2s, sh2s, is_s, 0, fkeys=True)
    steps = [(g, ti) for g in range(16) for ti in range(len(tiles34))]
    upb = [0]; dnb = [0]

    def load_wu(g):
        wu_ = wus[g % 2]
        dma("pool", wu_, w_up[:, g * 256:(g + 1) * 256].rearrange("(k p) c -> p k c", p=128), r=[w_up], w=[wu_])

    def load_wd(g):
        wd_ = wds[0]
        dma("pool", wd_, w_down[g * 256:(g + 1) * 256, :].rearrange("(q p) c -> p q c", p=128), r=[w_down], w=[wd_])

    def up(k):
        g, ti = steps[k]; c0, T, is_s = tiles34[ti]
        if ti == 0 and g + 1 < 16:
            load_wu(g + 1)
        wu_ = wus[g % 2]; aT_ = aTs[k % 2]; rT_ = rTs[k % 2]
        for q in range(2):
            b_ = banks[upb[0] % 4]; upb[0] += 1
            for kk_ in range(8):
                mm(b_[:, 0:T], wu_[:, kk_, q * 128:(q + 1) * 128], h2T[:, kk_, c0:c0 + T], kk_ == 0, kk_ == 7, r=[wu_, (h2T, 0)], w=[b_])
            act(rT_[:, 0, 0:T], b_[:, 0:T], AF.Relu, r=[b_], w=[rT_])
            tt(aT_[:, q, 0:T], rT_[:, 0, 0:T], rT_[:, 0, 0:T], ALU.mult, eng=("pool" if q == 1 else "dve"), r=[rT_], w=[aT_])

    def down(k):
        g, ti = steps[k]; c0, T, is_s = tiles34[ti]
        wd_ = wds[g % 2]; aT_ = aTs[k % 2]
        for f in range(8):
            b_ = banks[4 + dnb[0] % 4]; dnb[0] += 1
            for q in range(2):
                mm(b_[:, 0:T], wd_[:, q, f * 128:(f + 1) * 128], aT_[:, q, 0:T], q == 0, q == 1, r=[wd_, aT_], w=[b_])
            resid_add(f, c0, T, is_s, b_, gt2p, gt2s, split=(f >= 6))

    load_wu(0); load_wd(0)
    up(0)
    for k in range(len(steps)):
        if k + 1 < len(steps):
            up(k + 1)
        down(k)
        g_, ti_ = steps[k]
        if ti_ == len(tiles34) - 1 and g_ + 1 < 16:
            load_wd(g_ + 1)
    fnp = pv[:, rows["fn"]:rows["fn"] + 8]
    for ti, (c0, T, is_s) in enumerate(tiles34):
        sq = nbuf["sq"]; rstd = nbuf["rstd"]; tmp = nbuf["tmp"]
        xv_ = xT[:, :, c0:c0 + T]
        xk8 = [(xT, f_) for f_ in range(8)]
        tt(sq[:, :, 0:T], xv_, xv_, ALU.mult, r=xk8, w=[sq])
        b_ = bank()
        for fc in range(8):
            mm(b_[:, 0:T], onesB, sq[:, fc, 0:T], fc == 0, fc == 7, w=[b_])
        ts(rstd[:, 0:T], b_[:, 0:T], 1.0 / D, RMS_EPS, ALU.mult, ALU.add, r=[b_], w=[rstd])
        r1 = nbuf["r1"]; r2 = nbuf["r2"]
        cp(r1[:, 0:T], rstd[:, 0:T], eng="act", r=[rstd], w=[r1])
        rsqrt_dve(rstd[:, 0:T], r1[:, 0:T], r2[:, 0:T], [rstd, r1, r2])
        tt(tmp[:, :, 0:T], xv_, rstd[:, 0:T].unsqueeze(1).broadcast_to([128, 8, T]), ALU.mult, r=xk8 + [rstd], w=[tmp])
        tt(tmp[:, :, 0:T], tmp[:, :, 0:T], fnp.unsqueeze(2).broadcast_to([128, 8, T]), ALU.mult, r=[tmp, pv], w=[tmp])
        for sub in range(0, T, 128):
            n = min(128, T - sub)
            ytm = nbuf["xtm"][nbuf["i"] % 2]; nbuf["i"] += 1
            for half in range(2):
                b_ = bank()
                for f4 in range(4):
                    fc = half * 4 + f4
                    tr(b_[0:n, f4 * 128:(f4 + 1) * 128], tmp[:, fc, sub:sub + n], identF, r=[tmp, identF], w=[b_])
                cp(ytm[0:n, half * 512:(half + 1) * 512], b_[0:n, :], eng="act", r=[b_], w=[ytm])
            dst = yp[c0 + sub:c0 + sub + n, :] if not is_s else ys[sub:sub + n, :]
            dma("sp", dst, ytm[0:n, :], r=[ytm], w=[yp if not is_s else ys])
    n = P.emit()
    print('SBUF bytes/partition: G', G.tot, 'max', SBMAX[0])
    S1.close(); G.close()
    return nc, dbg, n


_CACHE = {}


def _core_inputs(inp, c):
    f = lambda a: np.ascontiguousarray(a, dtype=np.float32)
    m = {"xp": inp["x_prompt"][c], "xs": inp["x_sample"][16 * c:16 * c + 16].reshape(64, D),
         "sh": inp["state_hgrn"][0, 16 * c:16 * c + 16], "sr": inp["state_rwkv"][0, 16 * c:16 * c + 16],
         "ss": inp["state_shift"][0, 16 * c:16 * c + 16], "cp": inp["c_prompt"][c:c + 1], "cs": inp["c_sample"][16 * c:16 * c + 16]}
    for k in ("norm1_w", "norm2_w", "rwkv_w0", "rwkv_a0", "rwkv_k_k", "rwkv_k_a", "rwkv_lnx_w", "rwkv_lnx_b", "rwkv_r_k", "final_norm_w"):
        m[k] = inp[k].reshape(8, 128)
    m["ada_w"] = inp["ada_w"][0]; m["ada_b"] = inp["ada_b"].reshape(48, 128); m["w_in"] = inp["w_in"][0]
    m["lb_logits"] = inp["lb_logits"].reshape(16, 128); m["hgrn_norm_w"] = inp["hgrn_norm_w"].reshape(1, 128)
    m["rwkv_mu"] = inp["rwkv_mu"].reshape(1, RC); m["rwkv_w2"] = inp["rwkv_w2"][0]; m["rwkv_a2"] = inp["rwkv_a2"][0]
    m["rwkv_g2"] = inp["rwkv_g2"][0]; m["w_out"] = inp["w_out"][0]; m["w_up"] = inp["w_up"][0]; m["w_down"] = inp["w_down"][0]
    return {k: f(v) for k, v in m.items()}


def kernel(**inputs):
    inp = {k: np.asarray(v) for k, v in inputs.items()}
    if "nc" not in _CACHE:
        _CACHE["nc"] = build()[0]
    nc = _CACHE["nc"]
    in_maps = [_core_inputs(inp, c) for c in range(8)]
    res = run_bass_kernel_spmd(nc, in_maps, core_ids=list(range(8)))
    R = res.results
    g = lambda n: [np.asarray(R[c][n], dtype=np.float32) for c in range(8)]
    y_prompt = np.stack(g("yp"), 0)
    y_sample = np.concatenate([a.reshape(16, 4, D) for a in g("ys")], 0)
    hgrn_p = np.stack(g("hp"), 0)[None]
    rwkv_p = np.stack(g("rp"), 0)[None]
    shift_p = np.concatenate(g("shp"), 0)[None]
    hgrn_s = np.concatenate(g("hs"), 0)[None]
    rwkv_s = np.concatenate(g("rs"), 0)[None]
    shift_s = np.concatenate(g("shs"), 0)[None]
    return (y_prompt, y_sample, hgrn_p, rwkv_p, shift_p, hgrn_s, rwkv_s, shift_s)
```

```python
import contextlib
import numpy as np
import concourse.bass as bass
import concourse.mybir as mybir
from concourse.bass_utils import run_bass_kernel_spmd

F32 = mybir.dt.float32
BF16 = mybir.dt.bfloat16
AF = mybir.ActivationFunctionType
ALU = mybir.AluOpType
AX = mybir.AxisListType

D = 1024
SEQ = 2048
NSEQ_S = 16
TS = 4
NT = SEQ + NSEQ_S * TS
INC = 9504
RC = 3360
NDMASEM = 12
SAME_ENG_RELAX = ()
RMS_EPS = 1e-6
LNX_EPS = 64e-5
C1 = -0.5 * float(np.exp(-0.5))


class Prog:
    def __init__(self, nc):
        self.nc = nc
        self.ops = []
        self.last_w = {}
        self.readers = {}
        self.bar = False
        self.sink = None
        self.lat_same = 0.2
        self.lat_cross = 0.3

    def peek(self, r, w):
        deps = set()
        rk = [self.key(x) for x in r]; wk = [self.key(x) for x in w]
        if self.bar:
            rk.append("__bar")
        for k in rk:
            if k in self.last_w:
                deps.add(self.last_w[k])
        for k in wk:
            if k in self.last_w:
                deps.add(self.last_w[k])
            deps |= self.readers.get(k, set())
        return deps

    @staticmethod
    def dur(eng, dma, n):
        if dma:
            return 0.1
        if eng == "pe":
            return 0.06 + n / 1200.0
        if eng == "dve":
            return 0.10 + n / 960.0
        if eng == "act":
            return 0.25 + n / 1200.0
        if eng == "pool":
            return 0.30 + n / 400.0
        return 0.1

    def merge(self, lists, lead=0):
        L = list(lists); K = len(L)
        done = set(); pos = [0] * K
        end = {}; free = {}
        tot = [max(1, len(x)) for x in L]

        def est_start(x):
            eng, fn, r, w, dma, n = x
            deps = self.peek(r, w)
            t = free.get(eng, 0.0)
            for d in deps:
                if d in end:
                    lat = self.lat_same if self.ops[d]["eng"] == eng else self.lat_cross
                    if self.ops[d]["eng"] == "pe" and eng == "pe":
                        lat = 0.0
                    t = max(t, end[d] + lat)
            return t

        while any(pos[i] < len(L[i]) for i in range(K)):
            progressed = False
            for si in range(K):
                while pos[si] < len(L[si]) and (L[si][pos[si]][0] == "mark" or (L[si][pos[si]][0] == "need" and L[si][pos[si]][1] in done)):
                    if L[si][pos[si]][0] == "mark":
                        done.add(L[si][pos[si]][1])
                    pos[si] += 1; progressed = True
            cand = []
            for si in range(K):
                if pos[si] >= len(L[si]):
                    continue
                x = L[si][pos[si]]
                if x[0] == "need":
                    continue
                cand.append((est_start(x), pos[si] / tot[si], si))
            if not cand:
                if progressed:
                    continue
                if all(pos[i] >= len(L[i]) for i in range(K)):
                    break
                raise RuntimeError("merge deadlock")
            t, _, si = min(cand)
            x = L[si][pos[si]]; pos[si] += 1
            eng, fn, r, w, dma, n = x
            idx = self.add(eng, fn, r, w, dma, n)
            d_ = self.dur(eng, dma, n)
            free[eng] = t + d_
            end[idx] = t + d_ + (2.5 if dma else 0.0)

    @staticmethod
    def key(x):
        if isinstance(x, tuple):
            if isinstance(x[0], str):
                return x
            return (x[0].tensor.name, x[1])
        if isinstance(x, str):
            return x
        return x.tensor.name

    def add(self, eng, fn, r=(), w=(), dma=False, n=256):
        if self.sink is not None:
            if eng in ("mark", "need"):
                self.sink.append((eng, fn))
            else:
                self.sink.append((eng, fn, list(r), list(w), dma, n))
            return None
        deps = set()
        rk = [self.key(x) for x in r]
        wk = [self.key(x) for x in w]
        if self.bar:
            rk.append("__bar")
        for k in rk:
            if k in self.last_w:
                deps.add(self.last_w[k])
        for k in wk:
            if k in self.last_w:
                deps.add(self.last_w[k])
            deps |= self.readers.get(k, set())
        idx = len(self.ops)
        self.ops.append(dict(eng=eng, fn=fn, deps=deps, dma=dma, users=0, stream=getattr(self, "tag", None), rk=rk, wk=wk))
        for k in rk:
            self.readers.setdefault(k, set()).add(idx)
        for k in wk:
            self.last_w[k] = idx
            self.readers[k] = set()
        return idx

    def barrier(self, dummy):
        allk = set(self.last_w.keys()) | set(self.readers.keys())
        allk.discard("__bar")
        self.bar = False
        self.add("dve", lambda e: e.memset(dummy, 0.0), r=(), w=list(allk) + ["__bar"])
        self.last_w = {"__bar": self.last_w["__bar"]}
        self.readers = {}
        self.bar = True

    def emit(self):
        nc = self.nc
        engs = {"pe": nc.tensor, "act": nc.scalar, "dve": nc.vector, "pool": nc.gpsimd, "sp": nc.sync}
        ops = self.ops

        seqn = {}
        for i_, o_ in enumerate(ops):
            if not o_["dma"]:
                seqn[o_["eng"]] = seqn.get(o_["eng"], 0) + 1
                o_["seq"] = seqn[o_["eng"]]

        def skip(d, o):
            if ops[d]["dma"] or o["dma"]:
                return False
            if ops[d]["eng"] != o["eng"]:
                return False
            if o["eng"] == "pe":
                return True
            if o["eng"] in SAME_ENG_RELAX and o["seq"] - ops[d]["seq"] >= 2:
                return True
            return False

        for o in ops:
            for d in o["deps"]:
                if not skip(d, o):
                    ops[d]["users"] += 1
        with contextlib.ExitStack() as st:
            csem = {e: st.enter_context(nc.semaphore("c_" + e)) for e in engs}
            dsem = {e: [st.enter_context(nc.semaphore(f"d_{e}{i}")) for i in range(NDMASEM)]
                    for e in ("act", "pool", "sp")}
            ccnt = {e: 0 for e in engs}
            dcnt = {e: [0] * NDMASEM for e in dsem}
            drr = {e: 0 for e in dsem}
            waited = {e: {} for e in engs}
            sig = {}

            def wait(e, sem, val):
                k = id(sem)
                if val <= 0 or waited[e].get(k, 0) >= val:
                    return
                engs[e].wait_ge(sem, val)
                waited[e][k] = val

            for i, o in enumerate(ops):
                e = o["eng"]
                for d in sorted(o["deps"]):
                    if d not in sig or skip(d, o):
                        continue
                    s, v = sig[d]
                    wait(e, s, v)
                if o["dma"]:
                    j = drr[e]
                    drr[e] = (j + 1) % NDMASEM
                    s = dsem[e][j]
                    wait(e, s, dcnt[e][j])
                    ins = o["fn"](engs[e])
                    dcnt[e][j] += 16
                    ins.then_inc(s, 16)
                    sig[i] = (s, dcnt[e][j])
                else:
                    ins = o["fn"](engs[e])
                    if o["users"] > 0:
                        ccnt[e] += 1
                        ins.then_inc(csem[e], 1)
                        sig[i] = (csem[e], ccnt[e])
            for e in dsem:
                for j in range(NDMASEM):
                    if dcnt[e][j] > 0:
                        nc.sync.wait_ge(dsem[e][j], dcnt[e][j])
        return len(ops)


def build(debug=None):
    nc = bass.Bass("TRN2", target_bir_lowering=False)
    P = Prog(nc)
    din = lambda n, s: nc.dram_tensor(n, list(s), F32, kind="ExternalInput").ap()
    dout = lambda n, s: nc.dram_tensor(n, list(s), F32, kind="ExternalOutput").ap()
    xp = din("xp", [SEQ, D]); xs = din("xs", [64, D])
    sh_in = din("sh", [16, 8, 128, 128]); sr_in = din("sr", [16, 16, 64, 64]); ss_in = din("ss", [16, RC])
    cpr = din("cp", [1, D]); csm = din("cs", [16, D])
    norm1_w = din("norm1_w", [8, 128]); norm2_w = din("norm2_w", [8, 128])
    ada_w = din("ada_w", [D, 6 * D]); ada_b = din("ada_b", [48, 128])
    w_in = din("w_in", [D, INC]); lb_logits = din("lb_logits", [16, 128])
    hgrn_norm_w = din("hgrn_norm_w", [1, 128]); mu = din("rwkv_mu", [1, RC])
    w0 = din("rwkv_w0", [8, 128]); w2 = din("rwkv_w2", [64, D]); a0 = din("rwkv_a0", [8, 128])
    a2 = din("rwkv_a2", [64, D]); g2 = din("rwkv_g2", [160, D]); k_k = din("rwkv_k_k", [8, 128])
    k_a = din("rwkv_k_a", [8, 128]); r_k = din("rwkv_r_k", [8, 128]); lnx_w = din("rwkv_lnx_w", [8, 128])
    lnx_b = din("rwkv_lnx_b", [8, 128]); w_out = din("w_out", [D, D]); w_up = din("w_up", [D, 4 * D])
    w_down = din("w_down", [4 * D, D]); fnw = din("final_norm_w", [8, 128])
    yp = dout("yp", [SEQ, D]); ys = dout("ys", [64, D])
    hp = dout("hp", [8, 128, 128]); rp = dout("rp", [16, 64, 64]); shp = dout("shp", [1, RC])
    hs = dout("hs", [16, 8, 128, 128]); rs = dout("rs", [16, 16, 64, 64]); shs = dout("shs", [16, RC])
    dbg = {}

    cnt = [0]

    def uname(n):
        cnt[0] += 1
        return f"{n}_{cnt[0]}"

    SBTOT = [0]; SBMAX = [0]

    class Scope:
        def __init__(self):
            self.st = contextlib.ExitStack()

        def sb(self, name, shape, dt=F32):
            nb = int(np.prod(shape[1:])) * (4 if dt == F32 else 2)
            SBTOT[0] += nb; self.tot = getattr(self, "tot", 0) + nb
            SBMAX[0] = max(SBMAX[0], SBTOT[0])
            return self.st.enter_context(nc.sbuf_tensor(uname(name), list(shape), dt)).ap()

        def close(self):
            SBTOT[0] -= getattr(self, "tot", 0)
            self.st.close()

    G = Scope()
    banks = [nc.alloc_psum_tensor(f"bank{i}", [128, 512], F32).ap() for i in range(8)]
    bk = [0]

    pools = {None: list(range(8)), "H": [1, 2], "RF": [3, 4], "RB": [6, 7]}
    cur = [None]
    bkp = {"H": 0, "RF": 0, "RB": 0}

    def bank():
        if cur[0] is None:
            b = banks[bk[0] % 8]
            bk[0] += 1
            return b
        pl = pools[cur[0]]
        b = banks[pl[bkp[cur[0]] % len(pl)]]
        bkp[cur[0]] += 1
        return b

    def dump(name, ap, shape, keys=None):
        if debug is None or (name not in debug and "all" not in debug):
            return
        d = nc.dram_tensor("dbg_" + name, list(shape), F32, kind="ExternalOutput").ap()
        dbg[name] = d
        q = "pool" if ap.dtype != F32 else "sp"
        rk = [ap] + [(ap, t_) for t_ in range(16)] if keys is None else keys
        P.add(q, lambda e: e.dma_start(out=d, in_=ap), r=rk, w=[d], dma=True)

    def dma(q, out, in_, r=(), w=()):
        P.add(q, lambda e: e.dma_start(out=out, in_=in_), r=list(r) or [in_], w=list(w) or [out], dma=True)

    def act(out, in_, func, bias=0.0, scale=1.0, r=(), w=(), accum=None):
        rr = list(r) or [in_]
        if not isinstance(bias, float):
            rr.append(bias)
        if not isinstance(scale, float):
            rr.append(scale)
        ww = list(w) or [out]
        if accum is not None:
            ww.append(accum)
        if accum is not None:
            P.add("act", lambda e: e.activation(out=out, in_=in_, func=func, bias=bias, scale=scale, accum_out=accum), r=rr, w=ww, n=out.free_size())
        else:
            P.add("act", lambda e: e.activation(out=out, in_=in_, func=func, bias=bias, scale=scale), r=rr, w=ww, n=out.free_size())

    def tt(out, in0, in1, op, eng="dve", r=(), w=()):
        P.add(eng, lambda e: e.tensor_tensor(out=out, in0=in0, in1=in1, op=op), r=list(r) or [in0, in1], w=list(w) or [out], n=out.free_size())

    def ts(out, in0, s1, s2, op0, op1=None, eng="dve", r=(), w=()):
        rr = list(r) or [in0]
        for s in (s1, s2):
            if s is not None and not isinstance(s, float):
                rr.append(s)
        if op1 is None:
            P.add(eng, lambda e: e.tensor_scalar(out=out, in0=in0, scalar1=s1, scalar2=None, op0=op0), r=rr, w=list(w) or [out], n=out.free_size())
        else:
            P.add(eng, lambda e: e.tensor_scalar(out=out, in0=in0, scalar1=s1, scalar2=s2, op0=op0, op1=op1), r=rr, w=list(w) or [out], n=out.free_size())

    def aff(out, in_, scale, bias, r=(), w=()):
        act(out, in_, AF.Identity, bias=bias, scale=scale, r=r, w=w)

    def stt(out, in0, scalar, in1, op0, op1, r=(), w=()):
        rr = list(r) or [in0, in1]
        if not isinstance(scalar, float):
            rr.append(scalar)
        P.add("dve", lambda e: e.scalar_tensor_tensor(out=out, in0=in0, scalar=scalar, in1=in1, op0=op0, op1=op1), r=rr, w=list(w) or [out], n=out.free_size())

    def cp(out, in_, eng="dve", r=(), w=()):
        if eng == "act":
            act(out, in_, AF.Copy, r=r, w=w)
        else:
            P.add(eng, lambda e: e.tensor_copy(out=out, in_=in_), r=list(r) or [in_], w=list(w) or [out], n=out.free_size())

    def memset(ap, val, eng="pool", w=()):
        P.add(eng, lambda e: e.memset(ap, val), w=list(w) or [ap], n=ap.free_size())

    def mm(out, lhsT, rhs, start, stop, r=(), w=()):
        P.add("pe", lambda e: e.matmul(out, lhsT, rhs, start=start, stop=stop), r=list(r) or [lhsT, rhs], w=list(w) or [out], n=rhs.free_size() + 60)

    def tr(out, in_, ident, r=(), w=()):
        P.add("pe", lambda e: e.transpose(out=out, in_=in_, identity=ident), r=list(r) or [in_, ident], w=list(w) or [out], n=ident.free_size() + 60)

    I32 = mybir.dt.int32

    def rsqrt_dve(y, x, t1, keys):
        P.add("dve", lambda e: e.tensor_scalar(out=y.bitcast(I32), in0=x.bitcast(I32), scalar1=-0.5, scalar2=1597463007.0,
                                               op0=ALU.mult, op1=ALU.add), r=keys, w=keys)
        for _ in range(2):
            tt(t1, y, y, ALU.mult, r=keys, w=keys)
            stt(t1, t1, -0.5, x, ALU.mult, ALU.mult, r=keys, w=keys)
            stt(y, t1, 1.5, y, ALU.add, ALU.mult, r=keys, w=keys)

    def rsqrt_(out, in_, scale, eps, tA, tB):
        ts(tA, in_, scale, eps, ALU.mult, ALU.add)
        rsqrt_dve(out, tA, tB, [out, tA, tB])

    ones128 = G.sb("ones128", [128, 128]); identF = G.sb("identF", [128, 128]); identB = G.sb("identB", [128, 128], BF16)
    memset(ones128, 1.0)
    P.add("pool", lambda e: e.affine_select(out=identF, in_=ones128, pattern=[[1, 128]], compare_op=ALU.is_equal,
                                           fill=0.0, base=0, channel_multiplier=-1), r=[ones128], w=[identF])
    cp(identB, identF)
    onesB = G.sb("onesB", [128, 128], BF16)
    cp(onesB, ones128)
    blkones = G.sb("blkones", [128, 128], BF16)
    memset(blkones, 0.0)
    memset(blkones[0:64, 0:64], 1.0, w=[blkones]); memset(blkones[64:128, 64:128], 1.0, w=[blkones])
    blkneg = G.sb("blkneg", [128, 128])
    memset(blkneg, 0.0)
    memset(blkneg[0:64, 0:64], -1.0, w=[blkneg]); memset(blkneg[64:128, 64:128], -1.0, w=[blkneg])
    blkpos = G.sb("blkpos", [128, 128])
    memset(blkpos, 0.0)
    memset(blkpos[0:64, 0:64], 1.0, w=[blkpos]); memset(blkpos[64:128, 64:128], 1.0, w=[blkpos])
    dummy = G.sb("dummy", [128, 1])
    masks = {}
    for C in (64, 4):
        mi = G.sb(f"mi{C}", [C, C]); ms = G.sb(f"ms{C}", [C, C]); mls = G.sb(f"mls{C}", [C, C])
        P.add("pool", lambda e, mi=mi, C=C: e.affine_select(out=mi, in_=ones128[0:C, 0:C], pattern=[[1, C]], compare_op=ALU.is_ge,
                                                         fill=0.0, base=0, channel_multiplier=-1), r=[ones128], w=[mi])
        P.add("pool", lambda e, ms=ms, C=C: e.affine_select(out=ms, in_=ones128[0:C, 0:C], pattern=[[1, C]], compare_op=ALU.is_gt,
                                                         fill=0.0, base=0, channel_multiplier=-1), r=[ones128], w=[ms])
        P.add("pool", lambda e, mls=mls, C=C: e.affine_select(out=mls, in_=ones128[0:C, 0:C], pattern=[[-1, C]], compare_op=ALU.is_gt,
                                                           fill=0.0, base=0, channel_multiplier=1), r=[ones128], w=[mls])
        msn = G.sb(f"msn{C}", [C, C]); mlsn = G.sb(f"mlsn{C}", [C, C])
        ts(msn, ms, -1.0, None, ALU.mult); ts(mlsn, mls, -1.0, None, ALU.mult)
        masks[C] = dict(incl=mi, strict=ms, nstrict=msn, nlstrict=mlsn)

    stgA = G.sb("stgA", [128, 128]); stgB = G.sb("stgB", [48, 128])
    memset(stgA, 0.0)
    rows = {}
    r0 = 0
    for nm, src, n in (("n1", norm1_w, 8), ("n2", norm2_w, 8), ("lb", lb_logits, 16), ("w0", w0, 8), ("a0", a0, 8),
                       ("kk", k_k, 8), ("ka", k_a, 8), ("rk", r_k, 8), ("lw", lnx_w, 8), ("lbb", lnx_b, 8), ("fn", fnw, 8)):
        dma("sp", stgA[r0:r0 + n, :], src, w=[stgA]); rows[nm] = r0; r0 += n
    dma("sp", stgA[r0:r0 + 26, :], mu[:, 0:3328].rearrange("o (n p) -> (o n) p", p=128), w=[stgA]); rows["mu"] = r0; r0 += 26
    dma("sp", stgA[r0:r0 + 1, 0:32], mu[:, 3328:3360], w=[stgA]); rows["mu2"] = r0; r0 += 1
    assert r0 <= 128
    dma("sp", stgB, ada_b)
    pv = G.sb("pv", [128, 128]); pvB = G.sb("pvB", [128, 48])
    b_ = bank(); tr(b_[:, 0:128], stgA, identF); cp(pv, b_[:, 0:128])
    b_ = bank(); tr(b_[:, 0:48], stgB, identF[0:48, 0:48]); cp(pvB, b_[:, 0:48])
    col = lambda nm, i=0: pv[:, rows[nm] + i: rows[nm] + i + 1]
    pv2 = G.sb("pv2", [128, 128]); ts(pv2, pv, -1.0, 1.0, ALU.mult, ALU.add)
    col2 = lambda nm, i=0: pv2[:, rows[nm] + i: rows[nm] + i + 1]
    hs1 = G.sb("hs1", [128, 8]); hs2 = G.sb("hs2", [128, 8]); hs1n = G.sb("hs1n", [128, 8]); hs2c = G.sb("hs2c", [128, 8])
    tt(hs1, pv[:, rows["lb"]:rows["lb"] + 8], pv[:, rows["lb"] + 8:rows["lb"] + 16], ALU.subtract)
    act(hs2, hs1, AF.Tanh, scale=0.5)
    ts(hs1, hs2, -0.25, 0.25, ALU.mult, ALU.add)
    ts(hs2, hs2, 0.25, 0.75, ALU.mult, ALU.add)
    ts(hs1n, hs1, -1.0, None, ALU.mult)
    ts(hs2c, hs2, -1.0, 1.0, ALU.mult, ALU.add)
    hw0 = G.sb("hw0", [128, 8]); ha0 = G.sb("ha0", [128, 8]); omka = G.sb("omka", [128, 8])
    ts(hw0, pv[:, rows["w0"]:rows["w0"] + 8], 0.5, None, ALU.mult)
    ts(ha0, pv[:, rows["a0"]:rows["a0"] + 8], 0.5, None, ALU.mult)
    ts(omka, pv[:, rows["ka"]:rows["ka"] + 8], -1.0, 1.0, ALU.mult, ALU.add)
    nwbc = G.sb("nwbc", [64, 128]); dma("sp", nwbc, hgrn_norm_w.partition_broadcast(64))

    modT = G.sb("modT", [128, 48, 17])
    g1p = G.sb("g1p", [128, 8]); g2p = G.sb("g2p", [128, 8])
    sh1p = G.sb("sh1p", [128, 8]); sh2p = G.sb("sh2p", [128, 8]); gt1p = G.sb("gt1p", [128, 8]); gt2p = G.sb("gt2p", [128, 8])
    mT = G.sb("mT", [128, 8, NT], BF16)
    S2 = Scope()
    hT = S2.sb("hT", [128, 8, NT], BF16)
    S0 = Scope()
    c17 = S0.sb("c17", [17, D]); s17 = S0.sb("s17", [17, D]); s17b = S0.sb("s17b", [17, D], BF16); scT = S0.sb("scT", [128, 8, 32], BF16)
    dma("sp", c17[0:1, :], cpr, w=[c17]); dma("sp", c17[1:17, :], csm, w=[c17])
    act(s17, c17, AF.Tanh, scale=0.5)
    ts(s17, s17, 0.5, 0.5, ALU.mult, ALU.add)
    tt(s17b, s17, c17, ALU.mult)
    b_ = bank(); bb = b_.bitcast(BF16)
    for k in range(8):
        tr(bb[:, k * 32:k * 32 + 17], s17b[:, k * 128:(k + 1) * 128], identB[0:17, 0:17], w=[b_])
    cp(scT[:, :, 0:17], bb[:, 0:256].rearrange("p (k c) -> p k c", c=32)[:, :, 0:17], r=[b_])
    awb = [S0.sb("awb", [128, 8, 512], BF16) for _ in range(2)]
    def ada_group(g):
            wt = awb[g % 2]
            dma("pool", wt, ada_w[:, g * 512:(g + 1) * 512].rearrange("(k p) c -> p k c", p=128), r=[ada_w], w=[wt])
            b_ = bank()
            for m in range(4):
                for k in range(8):
                    mm(b_[:, m * 32:m * 32 + 17], wt[:, k, m * 128:(m + 1) * 128], scT[:, k, 0:17], k == 0, k == 7, w=[b_])
            for m in range(4):
                ts(modT[:, g * 4 + m, :], b_[:, m * 32:m * 32 + 17], pvB[:, g * 4 + m:g * 4 + m + 1], None, ALU.add, r=[b_, pvB], w=[modT])

    def modp(part):
        return modT[:, part * 8:(part + 1) * 8, 0]
    def mods(part):
        return modT[:, part * 8:(part + 1) * 8, 1:17].unsqueeze(3).broadcast_to([128, 8, 16, 4])
    def mods3(part):
        return modT[:, part * 8:(part + 1) * 8, 1:17]

    def mk_mods(dst, part, nm=None):
        for t4 in range(4):
            if nm is None:
                cp(dst[:, :, t4::4], mods3(part), r=[modT], w=[dst])
            else:
                stt(dst[:, :, t4::4], mods3(part), 1.0, pv[:, rows[nm]:rows[nm] + 8].unsqueeze(2).broadcast_to([128, 8, 16]),
                    ALU.add, ALU.mult, r=[modT, pv], w=[dst])

    tiles = [(i * 256, 256, 64, 4, False) for i in range(8)] + [(SEQ + 16 * i, 16, 4, 4, True) for i in range(4)]

    def norm_to(hT_dst, xT_t, T, gp, shp_, gs, shs_, is_s, ti, fkeys=False):
        xk_ = [(xT_t, f_) for f_ in range(8)] if fkeys else [xT_t]
        sq = nbuf["sq"]
        tt(sq[:, :, 0:T], xT_t[:, :, 0:T], xT_t[:, :, 0:T], ALU.mult, r=xk_, w=[sq])
        b_ = bank()
        for fc in range(8):
            mm(b_[:, 0:T], onesB, sq[:, fc, 0:T], fc == 0, fc == 7, w=[b_])
        rstd = nbuf["rstd"]
        ts(rstd[:, 0:T], b_[:, 0:T], 1.0 / D, RMS_EPS, ALU.mult, ALU.add, r=[b_], w=[rstd])
        r1 = nbuf["r1"]; r2 = nbuf["r2"]
        cp(r1[:, 0:T], rstd[:, 0:T], eng="act", r=[rstd], w=[r1])
        rsqrt_dve(rstd[:, 0:T], r1[:, 0:T], r2[:, 0:T], [rstd, r1, r2])
        tmp = nbuf["tmp"]
        tt(tmp[:, :, 0:T], xT_t[:, :, 0:T], rstd[:, 0:T].unsqueeze(1).broadcast_to([128, 8, T]), ALU.mult, r=xk_ + [rstd], w=[tmp])
        if not is_s:
            for fc in range(8):
                ts(hT_dst[:, fc, :], tmp[:, fc, 0:T], gp[:, fc:fc + 1], shp_[:, fc:fc + 1], ALU.mult, ALU.add,
                   r=[tmp, gp, shp_], w=[(hT_dst, ti)])
        else:
            tt(tmp[:, :, 0:T], tmp[:, :, 0:T], gs, ALU.mult, r=[tmp, gs], w=[tmp])
            tt(hT_dst, tmp[:, :, 0:T], shs_, ALU.add, r=[tmp, shs_], w=[(hT_dst, ti)])

    def load_xT(xT_t, c0, T, ti, fkeys=False):
        for sub in range(0, T, 128):
            n = min(128, T - sub)
            xtm = nbuf["xtm"][nbuf["i"] % 2]; nbuf["i"] += 1
            src = xp[c0 + sub:c0 + sub + n, :] if c0 < SEQ else xs[c0 - SEQ + sub:c0 - SEQ + sub + n, :]
            dma("sp", xtm[0:n, :], src, w=[xtm])
            for half in range(2):
                b_ = bank()
                for f4 in range(4):
                    fc = half * 4 + f4
                    tr(b_[:, f4 * 128:f4 * 128 + n], xtm[0:n, fc * 128:(fc + 1) * 128], identF[0:n, 0:n], w=[b_])
                cp(xT_t[:, half * 4:half * 4 + 4, sub:sub + n], b_.rearrange("p (f t) -> p f t", t=128)[:, :, 0:n], eng="act", r=[b_],
                   w=([(xT_t, f_) for f_ in range(half * 4, half * 4 + 4)] if fkeys else [xT_t]))

    for g in range(4):
        ada_group(g)
    stt(g1p, modp(1), 1.0, pv[:, rows["n1"]:rows["n1"] + 8], ALU.add, ALU.mult, r=[modT, pv])
    cp(sh1p, modp(0), r=[modT])
    S1 = Scope()
    g1s = S1.sb("g1s", [128, 8, 64]); sh1s = S1.sb("sh1s", [128, 8, 64])
    mk_mods(sh1s, 0); mk_mods(g1s, 1, "n1")
    nbuf = dict(sq=S1.sb("sq", [128, 8, 512], BF16), rstd=S1.sb("rstd", [128, 512]), tmp=S1.sb("ntmp", [128, 8, 512]), r1=S1.sb("r1", [128, 512]), r2=S1.sb("r2", [128, 512]),
                xtm=[S1.sb("xtm", [128, D]) for _ in range(2)], i=0)
    xT_ts = [S1.sb("xT_t", [128, 8, 512]) for _ in range(2)]
    gnext = 4
    for ti, (c0, T, C, NCH, is_s) in enumerate(tiles):
        xT_t = xT_ts[ti % 2]
        load_xT(xT_t, c0, T, ti)
        so = max(c0 - SEQ, 0)
        norm_to(hT[:, :, c0:c0 + T], xT_t, T, g1p, sh1p, g1s[:, :, so:so + min(T, 16)], sh1s[:, :, so:so + min(T, 16)], is_s, ti)
        if gnext < 12:
            ada_group(gnext); gnext += 1
    while gnext < 12:
        ada_group(gnext); gnext += 1
    stt(g2p, modp(4), 1.0, pv[:, rows["n2"]:rows["n2"] + 8], ALU.add, ALU.mult, r=[modT, pv])
    for dst, part in ((gt1p, 2), (sh2p, 3), (gt2p, 5)):
        cp(dst, modp(part), r=[modT])
    S1.close()
    S0.close()
    P.barrier(dummy)

    W = lambda name, shape, dt=F32: S2.sb(name, shape, dt)
    wbH = W("wbH", [128, 8, 640], BF16); wbR = W("wbR", [128, 8, 512], BF16)
    wl = wbH[:, :, 0:288]
    w2b = W("w2b", [128, D], BF16)
    g2b = W("g2b", [128, D], BF16); g2c = W("g2c", [32, D], BF16)
    TW = W("TW", [128, NT], BF16); SG1 = W("SG1", [128, NT], BF16); SG2 = W("SG2", [32, NT], BF16)
    ssT = W("ssT", [128, 27, 16]); lastc = W("lastc", [128, 27, 17])
    prevs = W("prevs", [128, 8])
    dsh = W("dsh", [128, 256])
    S0all = W("S0all", [128, 4, 128]); S0bf = W("S0bf", [128, 4, 128], BF16); Souts = S0all
    Sh = W("Sh", [128, 128]); Shbf = W("Shbf", [128, 128], BF16)
    R0in = W("R0in", [128, 4, 128]); R0blk = W("R0blk", [128, 4, 128]); R0bf = W("R0bf", [128, 4, 128], BF16); Routs = R0blk
    Sr = W("Sr", [128, 128]); Srbf = W("Srbf", [128, 128], BF16); Srt = W("Srt", [128, 128])
    f32t = {n: W(n, [128, 256]) for n in ("th", "fS", "kS", "Pc", "rP", "Ga", "t1", "t2", "Gmm", "kap", "kmod", "bS", "gS",
                                          "xr", "xk", "xv", "rG", "aS", "u1", "u2", "e1")}
    f32t["n1"] = f32t["kmod"]; f32t["n2"] = f32t["u1"]
    oTs = [W("oT", [128, 256]) for _ in range(2)]
    rstH = W("rstH", [128, 256]); rstR = W("rstR", [128, 256]); memset(rstH, 0.0); memset(rstR, 0.0)
    bft = {n: W(n, [128, 256], BF16) for n in ("qt", "kt", "kh", "vT", "sqb", "kpt", "bt", "bh", "ktl", "khr", "vTr")}
    KR = W("KR", [128, 4, 128], BF16)
    tm = {n: W(n, [64, 4, 128], BF16) for n in ("Vtm", "khtm", "W1", "Un", "Ktp", "onb", "onr")}
    Abm = {n: W(n, [64, 4, 128], BF16) for n in ("Y2", "Z2", "TTa", "TTb")}
    IFs = [dict(**{n: W(n, [64, 4, 128], BF16) for n in ("Z", "Y", "PbT", "PkT", "AkT", "Vtr", "Ktm", "bhtm", "khrtm")},
                rt=W("rt", [128, 256], BF16), **{n: W(n, [128, 256]) for n in ("Gm", "bon", "Gb")}) for _ in range(2)]
    attnT = W("attnT", [64, 4, 64], BF16)
    Osb = W("Osb", [64, 4, 128]); Osq = W("Osq", [64, 4, 128]); st_ = W("st_", [64, 16]); stA = W("stA", [64, 16]); stB = W("stB", [64, 16])
    srow = Osq.rearrange("p c f -> p (c f)")[0:17, :]
    Osb2 = W("Osb2", [64, 4, 128]); Osq2 = W("Osq2", [64, 4, 128]); st3 = W("st3", [64, 16]); st4 = W("st4", [64, 16]); stC = W("stC", [64, 16]); stD = W("stD", [64, 16])
    Mblk = W("Mblk", [128, 4, 128], BF16); Ncomb = W("Ncomb", [128, 4, 128]); RpT = W("RpT", [128, 256], BF16)

    dma("pool", w2b[0:64, :], w2, w=[w2b]); dma("pool", w2b[64:128, :], a2, w=[w2b])
    dma("pool", g2b, g2[0:128, :]); dma("pool", g2c, g2[128:160, :])
    dma("pool", wl, w_in[:, 7168:7456].rearrange("(k p) c -> p k c", p=128), r=[w_in], w=[wl])
    for p4 in range(0, 27, 4):
        n4 = min(4, 27 - p4); wdt = min(n4 * 128, RC - p4 * 128)
        memset(srow, 0.0, eng="dve")
        dma("sp", srow[0:16, 0:wdt], ss_in[:, p4 * 128:p4 * 128 + wdt], r=[ss_in], w=[srow])
        b_ = bank()
        for q in range(n4):
            tr(b_[:, q * 16:(q + 1) * 16], srow[0:16, q * 128:(q + 1) * 128], identF[0:16, 0:16], r=[srow, identF], w=[b_])
        cp(ssT[:, p4:p4 + n4, :], b_[:, 0:n4 * 16].rearrange("p (q s) -> p q s", s=16), r=[b_], w=[ssT])
    memset(prevs, 0.0); memset(lastc, 0.0)

    def proj(wt, c_lo, c_hi, ti, c0, T):
        b_ = bank(); n = c_hi - c_lo
        for k in range(8):
            mm(b_[0:n, 0:T], wt[:, k, c_lo:c_hi], hT[:, k, c0:c0 + T], k == 0, k == 7, r=[wt, (hT, ti)], w=[b_])
        return b_

    def shift_mix(dst, ps, n, T, is_s, mucol, pcol, part, last_tile, s0=0, omucol=None):
        aff(dsh[0:n, 0:T], ps[0:n, 0:T], omucol[0:n, :], 0.0, r=[ps, pv2], w=[dsh])
        if not is_s:
            stt(dst[0:n, 1:T], ps[0:n, 0:T - 1], mucol[0:n, :], dsh[0:n, 1:T], ALU.mult, ALU.add, r=[ps, dsh, pv], w=[dst])
            stt(dst[0:n, 0:1], prevs[0:n, pcol:pcol + 1], mucol[0:n, :], dsh[0:n, 0:1], ALU.mult, ALU.add, r=[prevs, dsh, pv], w=[dst])
            cp(prevs[0:n, pcol:pcol + 1], ps[0:n, T - 1:T], eng="act", r=[ps, dst], w=[prevs])
            if last_tile:
                cp(lastc[0:n, part, 0:1], ps[0:n, T - 1:T], eng="act", r=[ps], w=[lastc])
        else:
            p3 = ps[0:n, 0:T].rearrange("p (s t) -> p s t", t=4); d3 = dsh[0:n, 0:T].rearrange("p (s t) -> p s t", t=4)
            o3 = dst[0:n, 0:T].rearrange("p (s t) -> p s t", t=4)
            stt(o3[:, :, 1:4], p3[:, :, 0:3], mucol[0:n, :], d3[:, :, 1:4], ALU.mult, ALU.add, r=[ps, dsh, pv], w=[dst])
            stt(o3[:, :, 0], ssT[0:n, part, s0:s0 + T // 4], mucol[0:n, :], d3[:, :, 0], ALU.mult, ALU.add, r=[ssT, dsh, pv], w=[dst])
            cp(lastc[0:n, part, 1 + s0:1 + s0 + T // 4], p3[:, :, 3], eng="act", r=[ps], w=[lastc])

    def sigm(dst, src, r=(), w=(), bias=0.0):
        act(dst, src, AF.Tanh, bias=bias, scale=0.5, r=r, w=w)
        aff(dst, dst, 0.5, 0.5)

    for ti, (c0, T, C, NCH, is_s) in enumerate(tiles):
        last = (c0 + T == SEQ)
        b_ = proj(wl, 0, 128, ti, c0, T)
        shift_mix(f32t["t1"], b_, 128, T, is_s, col("mu", 24), 0, 24, last, max(c0 - SEQ, 0) // 4, omucol=col2("mu", 24))
        act(TW[0:64, c0:c0 + T], f32t["t1"][0:64, 0:T], AF.Tanh, r=[f32t["t1"]], w=[(TW, ti)])
        cp(TW[64:128, c0:c0 + T], f32t["t1"][64:128, 0:T], eng="act", r=[f32t["t1"]], w=[(TW, ti)])
        b_ = proj(wl, 128, 256, ti, c0, T)
        shift_mix(f32t["t1"], b_, 128, T, is_s, col("mu", 25), 1, 25, last, max(c0 - SEQ, 0) // 4, omucol=col2("mu", 25))
        sigm(f32t["t2"][:, 0:T], f32t["t1"][:, 0:T], r=[f32t["t1"]], w=[f32t["t2"]])
        cp(SG1[:, c0:c0 + T], f32t["t2"][:, 0:T], r=[f32t["t2"]], w=[(SG1, ti)])
        b_ = proj(wl, 256, 288, ti, c0, T)
        shift_mix(f32t["t1"], b_, 32, T, is_s, col("mu2"), 2, 26, last, max(c0 - SEQ, 0) // 4, omucol=col2("mu2"))
        sigm(f32t["t2"][0:32, 0:T], f32t["t1"][0:32, 0:T], r=[f32t["t1"]], w=[f32t["t2"]])
        cp(SG2[:, c0:c0 + T], f32t["t2"][0:32, 0:T], r=[f32t["t2"]], w=[(SG2, ti)])

    if debug is not None and debug.get("_stop") == "lora":
        dump("TW", TW, [128, NT]); dump("SG1", SG1, [128, NT]); dump("SG2", SG2, [32, NT]); dump("lastc", lastc.rearrange("p a b -> p (a b)"), [128, 27 * 17])
        n = P.emit()
        return nc, dbg, n

    def chunk3(ap, C):
        return ap.rearrange("p (c t) -> p c t", t=C)

    def to_tm(dst, src_bf, T, C, NCH):
        for c4 in range(0, NCH, 8):
            b_ = bank(); bb = b_.bitcast(BF16)
            for c in range(c4, min(c4 + 8, NCH)):
                tr(bb[0:C, (c - c4) * 128:(c - c4 + 1) * 128], src_bf[:, c * C:(c + 1) * C], identB, r=[src_bf, identB], w=[b_])
            n8 = min(8, NCH - c4)
            cp(dst[0:C, c4:c4 + n8, :], bb[0:C, 0:n8 * 128].rearrange("p (c f) -> p c f", f=128), eng="act", r=[b_], w=[dst])

    def cumprod(dst, src, T, C, NCH, rst):
        cp(chunk3(rst[:, 0:T], C)[:, :, 0:1], chunk3(src[:, 0:T], C)[:, :, 0:1], eng="act", r=[src], w=[rst])
        P.add("dve", lambda e: e.tensor_tensor_scan(out=dst[:, 0:T], data0=src[:, 0:T], data1=rst[:, 0:T], initial=1.0,
                                                    op0=ALU.mult, op1=ALU.max), r=[src, rst], w=[dst])
        if C != 64:
            memset(chunk3(rst[:, 0:T], C)[:, :, 0:1], 0.0, eng="pool", w=[rst])

    def lastbc(ap, T, C, NCH):
        return chunk3(ap[:, 0:T], C)[:, :, C - 1:C].broadcast_to([128, NCH, C])


    def rwkv_front(j, ti, c0, T, C, NCH, is_s, last, wb, gi):
        F_ = f32t; B_ = bft; M_ = masks[C]; jc = slice(j * 128, (j + 1) * 128)
        I_ = IFs[gi % 2]
        if gi >= 2:
            P.add("need", ("B", gi - 2))
        X = lambda n: F_[n][:, 0:T]
        Bx = lambda n: B_[n][:, 0:T]
        Gm = I_["Gm"]; bon = I_["bon"]; Gb = I_["Gb"]; rt = I_["rt"]
        so_ = max(c0 - SEQ, 0) // 4
        br = proj(wb, 0, 128, ti, c0, T); shift_mix(F_["xr"], br, 128, T, is_s, col("mu", j), 3, j, last, so_, omucol=col2("mu", j))
        bk_ = proj(wb, 128, 256, ti, c0, T); shift_mix(F_["xk"], bk_, 128, T, is_s, col("mu", 8 + j), 4, 8 + j, last, so_, omucol=col2("mu", 8 + j))
        bv = proj(wb, 256, 384, ti, c0, T); shift_mix(F_["xv"], bv, 128, T, is_s, col("mu", 16 + j), 5, 16 + j, last, so_, omucol=col2("mu", 16 + j))
        bgb = proj(wb, 384, 512, ti, c0, T)
        sigm(Gb[:, 0:T], bgb[:, 0:T], r=[bgb], w=[Gb])
        bw = bank(); mm(bw[:, 0:T], w2b[0:64, jc], TW[0:64, c0:c0 + T], True, True, r=[w2b, (TW, ti)], w=[bw])
        act(X("u1"), bw[:, 0:T], AF.Tanh, bias=hw0[:, j:j + 1], scale=0.5, r=[bw], w=[F_["u1"]])
        act(X("u1"), X("u1"), AF.Exp, bias=C1, scale=C1)
        ba = bank(); mm(ba[:, 0:T], w2b[64:128, jc], TW[64:128, c0:c0 + T], True, True, r=[w2b, (TW, ti)], w=[ba])
        sigm(X("aS"), ba[:, 0:T], r=[ba], w=[F_["aS"]], bias=ha0[:, j:j + 1])
        bg = bank(); mm(bg[:, 0:T], g2b[:, jc], SG1[:, c0:c0 + T], True, False, r=[g2b, (SG1, ti)], w=[bg])
        mm(bg[:, 0:T], g2c[:, jc], SG2[:, c0:c0 + T], False, True, r=[g2c, (SG2, ti)], w=[bg])
        cp(X("gS"), bg[:, 0:T], eng="act", r=[bg], w=[F_["gS"]])
        tt(Gb[:, 0:T], Gb[:, 0:T], X("gS"), ALU.mult, eng="pool", r=[Gb, F_["gS"]], w=[Gb])
        cumprod(Gm, F_["u1"], T, C, NCH, rstR)
        P.add("dve", lambda e: e.reciprocal(out=X("rG"), in_=Gm[:, 0:T]), r=[Gm], w=[F_["rG"]])
        cp(chunk3(X("Gmm"), C)[:, :, 1:C], chunk3(Gm[:, 0:T], C)[:, :, 0:C - 1], r=[Gm], w=[F_["Gmm"]])
        memset(chunk3(X("Gmm"), C)[:, :, 0:1], 1.0, eng="dve", w=[F_["Gmm"]])
        aff(X("kap"), X("xk"), col("kk", j), 0.0, r=[F_["xk"], pv], w=[F_["kap"]])
        tt(Bx("sqb"), X("kap"), X("kap"), ALU.mult, eng="pool")
        bn = bank(); mm(bn[:, 0:T], blkones, Bx("sqb"), True, True, w=[bn])
        ts(X("n1"), bn[:, 0:T], 1.0, 1e-24, ALU.mult, ALU.add, r=[bn], w=[F_["n1"]])
        rsqrt_dve(X("u2"), X("n1"), X("n2"), [F_["u2"], F_["n1"], F_["n2"]])
        tt(X("kap"), X("kap"), X("u2"), ALU.mult)
        aff(X("u2"), X("aS"), col("ka", j), omka[:, j:j + 1], r=[F_["aS"], pv, omka], w=[F_["u2"]])
        tt(X("kmod"), X("xk"), X("u2"), ALU.mult, eng="pool")
        tt(X("bS"), X("aS"), X("kap"), ALU.mult)
        stt(Bx("sqb"), X("xr"), col("rk", j), X("kmod"), ALU.mult, ALU.mult, r=[F_["xr"], F_["kmod"], pv], w=[B_["sqb"]])
        bn2 = bank(); mm(bn2[:, 0:T], blkones, Bx("sqb"), True, True, w=[bn2])
        tt(bon[:, 0:T], bn2[:, 0:T], X("xv"), ALU.mult, r=[bn2, F_["xv"]], w=[bon])
        tt(Bx("kpt"), X("kap"), X("Gmm"), ALU.mult)
        tt(rt[:, 0:T], X("xr"), Gm[:, 0:T], ALU.mult, eng="pool", r=[F_["xr"], Gm], w=[rt])
        cp(KR[:, 0:NCH, 0:C], chunk3(Bx("kpt"), C), eng="pool", r=[B_["kpt"]], w=[KR])
        cp(KR[:, 0:NCH, 64:64 + C], chunk3(rt[:, 0:T], C), eng="pool", r=[rt], w=[KR])
        tt(X("u2"), X("kmod"), X("rG"), ALU.mult); cp(Bx("ktl"), X("u2"), eng="act")
        tt(chunk3(Bx("khr"), C), chunk3(X("u2"), C), lastbc(Gm, T, C, NCH), ALU.mult, r=[F_["u2"], Gm], w=[B_["khr"]])
        tt(X("u2"), X("bS"), X("rG"), ALU.mult); cp(Bx("bt"), X("u2"), eng="act")
        tt(chunk3(Bx("bh"), C), chunk3(X("u2"), C), lastbc(Gm, T, C, NCH), ALU.mult, r=[F_["u2"], Gm], w=[B_["bh"]])
        cp(Bx("vTr"), X("xv"), eng="act")
        to_tm(I_["Vtr"], B_["vTr"], T, C, NCH); to_tm(I_["Ktm"], B_["kpt"], T, C, NCH)
        to_tm(I_["bhtm"], B_["bh"], T, C, NCH); to_tm(I_["khrtm"], B_["khr"], T, C, NCH)
        v4_ = lambda b: b[0:C, :].rearrange("p (c x t) -> p c x t", x=2, t=64)
        mk = lambda m_: m_.unsqueeze(1).broadcast_to([C, NCH, C])
        for h in range(2):
            hs_ = slice(64 * h, 64 * h + 64)
            dv = lambda t_: t_[0:C, 0:NCH, h * 64:h * 64 + C]
            b1 = bank()
            for c in range(NCH):
                mm(b1[0:C, c * 128:(c + 1) * 128], B_["bt"][hs_, c * C:(c + 1) * C], KR[hs_, c, :], True, True, r=[B_["bt"], KR], w=[b1])
            tt(dv(I_["Z"]), v4_(b1)[:, 0:NCH, 0, 0:C], mk(M_["nstrict"]), ALU.mult, r=[b1, M_["nstrict"]], w=[I_["Z"]])
            tt(dv(I_["PbT"]), v4_(b1)[:, 0:NCH, 1, 0:C], mk(M_["incl"]), ALU.mult, r=[b1, M_["incl"]], w=[I_["PbT"]])
            b2 = bank()
            for c in range(NCH):
                mm(b2[0:C, c * 128:(c + 1) * 128], B_["ktl"][hs_, c * C:(c + 1) * C], KR[hs_, c, :], True, True, r=[B_["ktl"], KR], w=[b2])
            tt(dv(I_["AkT"]), v4_(b2)[:, 0:NCH, 0, 0:C], mk(M_["strict"]), ALU.mult, r=[b2, M_["strict"]], w=[I_["AkT"]])
            tt(dv(I_["PkT"]), v4_(b2)[:, 0:NCH, 1, 0:C], mk(M_["incl"]), ALU.mult, r=[b2, M_["incl"]], w=[I_["PkT"]])
            b3 = bank()
            for c in range(NCH):
                mm(b3[0:C, c * 128:c * 128 + C], B_["kpt"][hs_, c * C:(c + 1) * C], B_["bt"][hs_, c * C:(c + 1) * C], True, True, r=[B_["kpt"], B_["bt"]], w=[b3])
            tt(dv(I_["Y"]), v4_(b3)[:, 0:NCH, 0, 0:C], mk(M_["nlstrict"]), ALU.mult, r=[b3, M_["nlstrict"]], w=[I_["Y"]])
        P.add("mark", ("F", gi))

    def rwkv_back(j, ti, c0, T, C, NCH, is_s, last, oT, gi):
        F_ = f32t; M_ = masks[C]
        I_ = IFs[gi % 2]
        L = 6 if C == 64 else 2
        X = lambda n: F_[n][:, 0:T]
        Gm = I_["Gm"]; bon = I_["bon"]; Gb = I_["Gb"]; rt = I_["rt"]
        if is_s:
            s0 = (c0 - SEQ) // 4
            memset(R0in, 0.0)
            dma("sp", R0in[0:64, :, 0:64], sr_in[s0:s0 + 4, 2 * j].rearrange("s i j -> i s j"), r=[sr_in], w=[R0in])
            dma("sp", R0in[64:128, :, 64:128], sr_in[s0:s0 + 4, 2 * j + 1].rearrange("s i j -> i s j"), r=[sr_in], w=[R0in])
            b_ = bank()
            for q in range(4):
                tr(b_[:, q * 128:(q + 1) * 128], R0in[:, q, :], identF, w=[b_])
            cp(R0blk[:, 0:4, :], b_.rearrange("p (q f) -> p q f", f=128), r=[b_], w=[R0blk])
            cp(R0bf, R0blk, eng="pool")
        P.add("need", ("F", gi))
        hv = lambda t_, c_lo, c_hi: t_[0:C, c_lo:c_hi, :].rearrange("p c (h t) -> p c h t", t=64)[:, :, :, 0:C]
        tt(hv(Abm["TTa"], 0, NCH), hv(I_["Z"], 0, NCH), identF[0:C, 0:C].unsqueeze(1).unsqueeze(1).broadcast_to([C, NCH, 2, C]), ALU.add,
           r=[I_["Z"], identF], w=[Abm["TTa"]])
        Yc, Zc, Yn, Zn, TTc, TTn = I_["Y"], I_["Z"], Abm["Y2"], Abm["Z2"], Abm["TTa"], Abm["TTb"]
        pr = [(c, h) for c in range(NCH) for h in range(2)]

        def batch_mm(dst, lhs, rhs, wid_l, wid_r, post):
            for p8 in range(0, len(pr), 8):
                b_ = bank(); grp = pr[p8:p8 + 8]
                for q, (c, h) in enumerate(grp):
                    mm(b_[0:C, q * 64:q * 64 + wid_r], lhs[0:C, c, h * 64:h * 64 + wid_l], rhs[0:C, c, h * 64:h * 64 + wid_r], True, True,
                       r=[lhs, rhs], w=[b_])
                cA = grp[0][0]; cB = grp[-1][0] + 1
                src = b_[0:C, 0:(cB - cA) * 128].rearrange("p (c h t) -> p c h t", h=2, t=64)[:, :, :, 0:wid_r]
                d = dst[0:C, cA:cB, :].rearrange("p c (h t) -> p c h t", t=64)[:, :, :, 0:wid_r]
                post(d, src, b_)

        cpy = lambda d, s_, b_: cp(d, s_, eng="act", r=[b_], w=[d])
        for n in range(1, L):
            batch_mm(Yn, Zc, Yc, C, C, cpy)
            if n < L - 1:
                batch_mm(Zn, Yc, Zc, C, C, cpy)
            for p8 in range(0, len(pr), 8):
                b_ = bank(); grp = pr[p8:p8 + 8]
                for q, (c, h) in enumerate(grp):
                    mm(b_[0:C, q * 64:q * 64 + C], Yn[0:C, c, h * 64:h * 64 + C], TTc[0:C, c, h * 64:h * 64 + C], True, True, r=[Yn, TTc], w=[b_])
                cA = grp[0][0]; cB = grp[-1][0] + 1
                src = b_[0:C, 0:(cB - cA) * 128].rearrange("p (c h t) -> p c h t", h=2, t=64)[:, :, :, 0:C]
                tt(hv(TTn, cA, cB), src, hv(TTc, cA, cB), ALU.add, r=[b_, TTc], w=[TTn])
            Yc, Yn = Yn, Yc; Zc, Zn = Zn, Zc; TTc, TTn = TTn, TTc
        TTf = TTc
        batch_mm(tm["W1"], I_["AkT"], I_["Vtr"], C, 64, cpy)
        batch_mm(tm["Ktp"], TTf, I_["Ktm"], C, 64, cpy)
        neg = lambda d, s_, b_: ts(d, s_, -1.0, None, ALU.mult, r=[b_], w=[d])
        batch_mm(tm["Un"], TTf, tm["W1"], C, 64, neg)
        v3 = lambda b: b.rearrange("p (c f) -> p c f", f=128)
        bM = bank()
        for c in range(NCH):
            mm(bM[:, c * 128:(c + 1) * 128], tm["Ktp"][0:C, c, :], I_["bhtm"][0:C, c, :], True, True, r=[tm["Ktp"], I_["bhtm"]], w=[bM])
        tt(Mblk[:, 0:NCH, :], v3(bM)[:, 0:NCH, :], blkneg.unsqueeze(1).broadcast_to([128, NCH, 128]), ALU.mult, r=[bM, blkneg], w=[Mblk])
        bR = bank()
        for c in range(NCH):
            mm(bR[:, c * 128:(c + 1) * 128], tm["Ktp"][0:C, c, :], I_["PbT"][0:C, c, :], True, True, r=[tm["Ktp"], I_["PbT"]], w=[bR])
        for h in range(2):
            hs_ = slice(64 * h, 64 * h + 64)
            tt(chunk3(RpT[hs_, 0:T], C), chunk3(rt[hs_, 0:T], C), v3(bR)[hs_, 0:NCH, h * 64:h * 64 + C], ALU.subtract, r=[rt, bR], w=[RpT])
        bN = bank()
        for c in range(NCH):
            mm(bN[:, c * 128:(c + 1) * 128], I_["khrtm"][0:C, c, :], I_["Vtr"][0:C, c, :], True, False, r=[I_["khrtm"], I_["Vtr"]], w=[bN])
            mm(bN[:, c * 128:(c + 1) * 128], I_["bhtm"][0:C, c, :], tm["Un"][0:C, c, :], False, True, r=[I_["bhtm"], tm["Un"]], w=[bN])
        tt(Ncomb[:, 0:NCH, :], v3(bN)[:, 0:NCH, :], blkpos.unsqueeze(1).broadcast_to([128, NCH, 128]), ALU.mult, r=[bN, blkpos], w=[Ncomb])
        bo = banks[5]
        for c in range(NCH):
            csl = slice(c * C, (c + 1) * C)
            sbf = R0bf[:, c, :] if is_s else Srbf
            sin = R0blk[:, c, :] if is_s else Sr
            sout = Routs[:, c, :] if is_s else Sr
            kS_ = [R0bf] if is_s else [Srbf]
            oo = bo[0:C, c * 128:(c + 1) * 128]
            mm(oo, RpT[:, csl], sbf, True, False, r=[RpT] + kS_, w=[bo])
            for h in range(2):
                hb = slice(h * 64, (h + 1) * 64)
                mm(oo[:, hb], I_["PkT"][0:C, c, h * 64:h * 64 + C], I_["Vtr"][0:C, c, hb], False, False, r=[I_["PkT"], I_["Vtr"]], w=[bo])
                mm(oo[:, hb], I_["PbT"][0:C, c, h * 64:h * 64 + C], tm["Un"][0:C, c, hb], False, h == 1, r=[I_["PbT"], tm["Un"]], w=[bo])
            b3 = bank()
            mm(b3[:, 0:128], Mblk[:, c, :], sbf, True, True, r=[Mblk] + kS_, w=[b3])
            stt(Srt, sin, Gm[:, (c + 1) * C - 1:(c + 1) * C], Ncomb[:, c, :], ALU.mult, ALU.add,
                r=[R0blk if is_s else Sr, Gm, Ncomb], w=[Srt])
            tt(sout, Srt, b3[:, 0:128], ALU.add, r=[Srt, b3], w=[Routs if is_s else Sr])
            if not is_s:
                cp(Srbf, Sr, eng="act")
        cp(Osb2[0:C, 0:NCH, :], bo.rearrange("p (c f) -> p c f", f=128)[0:C, 0:NCH, :], eng="act", r=[bo], w=[Osb2])
        if is_s:
            b_ = bank()
            for q in range(4):
                tr(b_[:, q * 128:(q + 1) * 128], Routs[:, q, :], identF, w=[b_])
            cp(R0in[:, 0:4, :], b_.rearrange("p (q f) -> p q f", f=128), r=[b_], w=[R0in])
            dma("sp", rs[s0:s0 + 4, 2 * j].rearrange("s i j -> i s j"), R0in[0:64, :, 0:64], r=[R0in], w=[rs])
            dma("sp", rs[s0:s0 + 4, 2 * j + 1].rearrange("s i j -> i s j"), R0in[64:128, :, 64:128], r=[R0in], w=[rs])
        elif last:
            b_ = bank(); tr(b_[:, 0:128], Sr, identF, w=[b_]); cp(Srt, b_[:, 0:128])
            dma("sp", rp[2 * j], Srt[0:64, 0:64], r=[Srt], w=[rp]); dma("sp", rp[2 * j + 1], Srt[64:128, 64:128], r=[Srt], w=[rp])
        G2 = NCH * 2
        og = Osb2[0:C, 0:NCH, :].rearrange("p c (h i) -> p (c h) i", i=64)
        oq = Osq2[0:C, 0:NCH, :].rearrange("p c (h i) -> p (c h) i", i=64)
        P.add("dve", lambda e: e.reduce_sum(out=st3[0:C, 0:G2], in_=og, axis=AX.X), r=[Osb2], w=[st3])
        ts(st3[0:C, 0:G2], st3[0:C, 0:G2], 1.0 / 64, None, ALU.mult)
        tt(og, og, st3[0:C, 0:G2].unsqueeze(2).broadcast_to([C, G2, 64]), ALU.subtract, r=[Osb2, st3], w=[Osb2])
        tt(oq, og, og, ALU.mult, eng="pool", r=[Osb2], w=[Osq2])
        P.add("dve", lambda e: e.reduce_sum(out=st4[0:C, 0:G2], in_=oq, axis=AX.X), r=[Osq2], w=[st4])
        rsqrt_(st4[0:C, 0:G2], st4[0:C, 0:G2], 1.0 / 64, LNX_EPS, stC[0:C, 0:G2], stD[0:C, 0:G2])
        tt(tm["onr"][0:C, 0:NCH, :].rearrange("p c (h i) -> p (c h) i", i=64), og, st4[0:C, 0:G2].unsqueeze(2).broadcast_to([C, G2, 64]), ALU.mult,
           r=[Osb2, st4], w=[tm["onr"]])
        b_ = bank(); bb = b_.bitcast(BF16)
        for c in range(NCH):
            tr(bb[:, c * C:(c + 1) * C], tm["onr"][0:C, c, :], identB[0:C, 0:C], r=[tm["onr"], identB], w=[b_])
        ts(X("e1"), bb[:, 0:T], col("lw", j), col("lbb", j), ALU.mult, ALU.add, r=[b_, pv], w=[F_["e1"]])
        tt(X("e1"), X("e1"), bon[:, 0:T], ALU.add, eng="pool", r=[F_["e1"], bon], w=[F_["e1"]])
        tt(X("e1"), X("e1"), Gb[:, 0:T], ALU.mult, eng="pool", r=[F_["e1"], Gb], w=[F_["e1"]])
        P.add("need", ("oTw", gi))
        tt(mT[:, j, c0:c0 + T], X("e1"), oT[:, 0:T], ALU.add, r=[F_["e1"], oT], w=[(mT, ti)])
        P.add("mark", ("oTr", gi))
        P.add("mark", ("B", gi))

    NBLK = 8 if debug is None else debug.get('nblk', 8)

    def hgrn_tile(j, ti, c0, T, C, NCH, is_s, last, wb, oT, gi):
        M_ = masks[C]; F_ = f32t; B_ = bft
        if is_s:
            s0 = (c0 - SEQ) // 4
            dma("sp", S0all, sh_in[s0:s0 + 4, j].rearrange("s k v -> k s v"), r=[sh_in], w=[S0all])
            cp(S0bf, S0all, eng="pool")
        bf_ = proj(wb, 128, 256, ti, c0, T)
        act(F_["th"][:, 0:T], bf_[:, 0:T], AF.Tanh, scale=0.5, r=[bf_], w=[F_["th"]])
        aff(F_["fS"][:, 0:T], F_["th"][:, 0:T], hs1[:, j:j + 1], hs2[:, j:j + 1], r=[F_["th"], hs1, hs2], w=[F_["fS"]])
        aff(F_["kS"][:, 0:T], F_["th"][:, 0:T], hs1n[:, j:j + 1], hs2c[:, j:j + 1], r=[F_["th"], hs1n, hs2c], w=[F_["kS"]])
        cumprod(F_["Pc"], F_["fS"], T, C, NCH, rstH)
        P.add("dve", lambda e, T=T: e.reciprocal(out=F_["rP"][:, 0:T], in_=F_["Pc"][:, 0:T]), r=[F_["Pc"]], w=[F_["rP"]])
        bq = proj(wb, 0, 128, ti, c0, T)
        tt(B_["qt"][:, 0:T], bq[:, 0:T], F_["Pc"][:, 0:T], ALU.mult, r=[bq, F_["Pc"]], w=[B_["qt"]])
        tt(F_["t1"][:, 0:T], F_["kS"][:, 0:T], F_["rP"][:, 0:T], ALU.mult, eng="pool", r=[F_["kS"], F_["rP"]], w=[F_["t1"]])
        cp(B_["kt"][:, 0:T], F_["t1"][:, 0:T], eng="act", r=[F_["t1"]], w=[B_["kt"]])
        tt(chunk3(B_["kh"][:, 0:T], C), chunk3(F_["t1"][:, 0:T], C), lastbc(F_["Pc"], T, C, NCH), ALU.mult, r=[F_["t1"], F_["Pc"]], w=[B_["kh"]])
        bi = proj(wb, 256, 384, ti, c0, T)
        cp(B_["vT"][:, 0:T], bi[:, 0:T], eng="act", r=[bi], w=[B_["vT"]])
        bog = proj(wb, 384, 512, ti, c0, T)
        sigm(F_["t2"][:, 0:T], bog[:, 0:T], r=[bog], w=[F_["t2"]])
        tt(F_["t2"][:, 0:T], bog[:, 0:T], F_["t2"][:, 0:T], ALU.mult, r=[bog, F_["t2"]], w=[F_["t2"]])
        bga = proj(wb, 512, 640, ti, c0, T)
        sigm(F_["Ga"][:, 0:T], bga[:, 0:T], r=[bga], w=[F_["Ga"]])
        tt(F_["Ga"][:, 0:T], F_["Ga"][:, 0:T], F_["t2"][:, 0:T], ALU.mult, eng="pool", r=[F_["Ga"], F_["t2"]], w=[F_["Ga"]])
        to_tm(tm["Vtm"], B_["vT"], T, C, NCH); to_tm(tm["khtm"], B_["kh"], T, C, NCH)
        b_ = bank()
        for c in range(NCH):
            mm(b_[0:C, c * C:(c + 1) * C], B_["kt"][:, c * C:(c + 1) * C], B_["qt"][:, c * C:(c + 1) * C], True, True, w=[b_])
        tt(attnT[0:C, 0:NCH, 0:C], b_[0:C, 0:NCH * C].rearrange("p (c t) -> p c t", t=C),
           M_["incl"].unsqueeze(1).broadcast_to([C, NCH, C]), ALU.mult, r=[b_, M_["incl"]], w=[attnT])
        bo = banks[0]
        for c in range(NCH):
            csl = slice(c * C, (c + 1) * C)
            sbf = S0bf[:, c, :] if is_s else Shbf
            sin = S0all[:, c, :] if is_s else Sh
            sout = Souts[:, c, :] if is_s else Sh
            oo = bo[0:C, c * 128:(c + 1) * 128]
            mm(oo, attnT[0:C, c, 0:C], tm["Vtm"][0:C, c, :], True, False, r=[attnT, tm["Vtm"]], w=[bo])
            mm(oo, B_["qt"][:, csl], sbf, False, True, r=[B_["qt"], S0bf if is_s else Shbf], w=[bo])
            b2 = bank()
            mm(b2[:, 0:128], tm["khtm"][0:C, c, :], tm["Vtm"][0:C, c, :], True, True, r=[tm["khtm"], tm["Vtm"]], w=[b2])
            stt(sout, sin, F_["Pc"][:, (c + 1) * C - 1:(c + 1) * C], b2[:, 0:128], ALU.mult, ALU.add,
                r=[S0all if is_s else Sh, F_["Pc"], b2], w=[Souts if is_s else Sh])
            if not is_s:
                cp(Shbf, Sh, eng="act")
        cp(Osb[0:C, 0:NCH, :], bo.rearrange("p (c f) -> p c f", f=128)[0:C], eng="act", r=[bo], w=[Osb])
        if is_s:
            dma("sp", hs[s0:s0 + 4, j].rearrange("s k v -> k s v"), Souts, r=[Souts], w=[hs])
        elif last:
            dma("sp", hp[j], Sh, r=[Sh], w=[hp])
        tt(Osq[0:C, 0:NCH, :], Osb[0:C, 0:NCH, :], Osb[0:C, 0:NCH, :], ALU.mult, eng="pool", r=[Osb], w=[Osq])
        P.add("dve", lambda e, C=C, NCH=NCH: e.reduce_sum(out=st_[0:C, 0:NCH], in_=Osq[0:C, 0:NCH, :], axis=AX.X), r=[Osq], w=[st_])
        rsqrt_(st_[0:C, 0:NCH], st_[0:C, 0:NCH], 1.0 / 128, RMS_EPS, stA[0:C, 0:NCH], stB[0:C, 0:NCH])
        tt(Osq[0:C, 0:NCH, :], Osb[0:C, 0:NCH, :], st_[0:C, 0:NCH].unsqueeze(2).broadcast_to([C, NCH, 128]), ALU.mult, r=[Osb, st_], w=[Osq])
        tt(tm["onb"][0:C, 0:NCH, :], Osq[0:C, 0:NCH, :], nwbc[0:C, :].unsqueeze(1).broadcast_to([C, NCH, 128]), ALU.mult, r=[Osq, nwbc], w=[tm["onb"]])
        b_ = bank(); bb = b_.bitcast(BF16)
        for c in range(NCH):
            tr(bb[:, c * C:(c + 1) * C], tm["onb"][0:C, c, :], identB[0:C, 0:C], r=[tm["onb"], identB], w=[b_])
        if gi >= 2:
            P.add("need", ("oTr", gi - 2))
        tt(oT[:, 0:T], bb[:, 0:T], F_["Ga"][:, 0:T], ALU.mult, r=[b_, F_["Ga"]], w=[oT])
        P.add("mark", ("oTw", gi))

    def stream(kind):
        gi = 0
        for j in range(NBLK):
            seg = lambda col0: w_in[:, col0 + j * 128:col0 + (j + 1) * 128].rearrange("(k p) c -> p k c", p=128)
            if kind == "H":
                for si, col0 in enumerate((0, 1024, 2048, 3072, 7456)):
                    dma("pool", wbH[:, :, si * 128:(si + 1) * 128], seg(col0), r=[w_in], w=[wbH])
                memset(Sh, 0.0, eng="dve"); memset(Shbf, 0.0, eng="dve")
            elif kind == "RF":
                for si, col0 in enumerate((4096, 5120, 6144, 8480)):
                    dma("pool", wbR[:, :, si * 128:(si + 1) * 128], seg(col0), r=[w_in], w=[wbR])
                memset(prevs[:, 3:6], 0.0, eng="dve", w=[prevs])
            else:
                memset(Sr, 0.0, eng="dve"); memset(Srbf, 0.0, eng="dve")
            for ti, (c0, T, C, NCH, is_s) in enumerate(tiles):
                if debug is not None and 'tsel' in debug and ti not in debug['tsel']:
                    continue
                last = (c0 + T == SEQ)
                if kind == "H":
                    hgrn_tile(j, ti, c0, T, C, NCH, is_s, last, wbH, oTs[gi % 2], gi)
                elif kind == "RF":
                    rwkv_front(j, ti, c0, T, C, NCH, is_s, last, wbR, gi)
                else:
                    rwkv_back(j, ti, c0, T, C, NCH, is_s, last, oTs[gi % 2], gi)
                gi += 1

    opsH = []; opsF = []; opsB = []
    P.sink = opsH; cur[0] = "H"; stream("H")
    P.sink = opsF; cur[0] = "RF"; stream("RF")
    P.sink = opsB; cur[0] = "RB"; stream("RB")
    P.sink = None; cur[0] = None
    _n0 = len(P.ops)
    if debug is not None:
        P.lat_same = debug.get('lat_same', P.lat_same); P.lat_cross = debug.get('lat_cross', P.lat_cross)
    P.merge([opsH, opsF, opsB])
    if debug is not None and debug.get("xdeps"):
        from collections import Counter
        cx = Counter()
        for i in range(_n0, len(P.ops)):
            o = P.ops[i]
            for d in o["deps"]:
                od = P.ops[d]
                if d >= _n0 and od["stream"] != o["stream"]:
                    shared = (set(o["rk"]) | set(o["wk"])) & (set(od["rk"]) | set(od["wk"]))
                    cx[(o["stream"], o["eng"], od["eng"], str(sorted(map(str, shared)))[:80])] += 1
        for k, v in cx.most_common(30):
            print("XDEP", v, k)
    dump("mT", mT.rearrange("p a b -> p (a b)"), [128, 8 * NT])
    for p4 in range(0, 27, 4):
        b_ = bank(); n4 = min(4, 27 - p4); wdt = min(n4 * 128, RC - p4 * 128)
        for q in range(n4):
            tr(b_[0:17, q * 128:(q + 1) * 128], lastc[:, p4 + q, :], identF, r=[lastc, identF], w=[b_])
        cp(srow[:, 0:n4 * 128], b_[0:17, 0:n4 * 128], r=[b_], w=[srow])
        dma("sp", shp[:, p4 * 128:p4 * 128 + wdt], srow[0:1, 0:wdt], r=[srow], w=[shp])
        dma("sp", shs[:, p4 * 128:p4 * 128 + wdt], srow[1:17, 0:wdt], r=[srow], w=[shs])
    if debug is not None and debug.get("_stop") == 2:
        n = P.emit()
        return nc, dbg, n
    print('SBUF phase2 total', SBTOT[0], 'S2', S2.tot)
    S2.close()
    P.barrier(dummy)
    S1 = Scope()
    xT = S1.sb("xT", [128, 8, NT]); h2T = S1.sb("h2T", [128, 8, NT], BF16)
    g2s = S1.sb("g2s", [128, 8, 64]); sh2s = S1.sb("sh2s", [128, 8, 64]); gt1s = S1.sb("gt1s", [128, 8, 64]); gt2s = S1.sb("gt2s", [128, 8, 64])
    mk_mods(gt1s, 2); mk_mods(sh2s, 3); mk_mods(g2s, 4, "n2"); mk_mods(gt2s, 5)
    nbuf = dict(sq=S1.sb("sq", [128, 8, 512], BF16), rstd=S1.sb("rstd", [128, 512]), tmp=S1.sb("ntmp", [128, 8, 512]),
                xtm=[S1.sb("xtm", [128, D]) for _ in range(2)], i=0)
    nbuf["r1"] = nbuf["tmp"][:, 1, :]; nbuf["r2"] = nbuf["tmp"][:, 0, :]
    wo = S1.sb("wo", [128, 8, 128], BF16); wus = [S1.sb("wu", [128, 8, 256], BF16) for _ in range(2)]; wds = [S1.sb("wd", [128, 2, D], BF16)] * 2
    aTs = [S1.sb("aT", [128, 2, 512], BF16) for _ in range(2)]; rTs = [S1.sb("rT", [128, 1, 512], BF16)] * 2; t64 = S1.sb("t64", [128, 64])
    rtmp = [nbuf["rstd"], nbuf["tmp"][:, 0, :]]; rti = [0]
    tiles34 = [(i * 512, 512, False) for i in range(4)] + [(SEQ, 64, True)]
    for ti, (c0, T, is_s) in enumerate(tiles34):
        load_xT(xT[:, :, c0:c0 + T], c0, T, ti, fkeys=True)

    def resid_add(f, c0, T, is_s, ps, gtp, gts, split=False):
        kx = [(xT, f)]
        if not is_s:
            if split:
                tq = rtmp[rti[0] % len(rtmp)]; rti[0] += 1
                aff(tq[:, 0:T], ps[:, 0:T], gtp[:, f:f + 1], 0.0, r=[ps, gtp], w=[tq])
                tt(xT[:, f, c0:c0 + T], xT[:, f, c0:c0 + T], tq[:, 0:T], ALU.add, eng="pool", r=kx + [tq], w=kx)
            else:
                stt(xT[:, f, c0:c0 + T], ps[:, 0:T], gtp[:, f:f + 1], xT[:, f, c0:c0 + T], ALU.mult, ALU.add, r=[ps, gtp] + kx, w=kx)
        else:
            tt(t64[:, 0:T], ps[:, 0:T], gts[:, f, :], ALU.mult, r=[ps, gts], w=[t64])
            tt(xT[:, f, c0:c0 + T], xT[:, f, c0:c0 + T], t64[:, 0:T], ALU.add, r=kx + [t64], w=kx)

    for f in range(8):
        dma("pool", wo, w_out[:, f * 128:(f + 1) * 128].rearrange("(k p) c -> p k c", p=128), r=[w_out], w=[wo])
        for ti, (c0, T, is_s) in enumerate(tiles34):
            b_ = bank()
            for k in range(8):
                mm(b_[:, 0:T], wo[:, k, :], mT[:, k, c0:c0 + T], k == 0, k == 7, r=[wo, mT], w=[b_])
            resid_add(f, c0, T, is_s, b_, gt1p, gt1s)
    for ti, (c0, T, is_s) in enumerate(tiles34):
        norm_to(h2T[:, :, c0:c0 + T], xT[:, :, c0:c0 + T], T, g2p, sh2p, g2s, sh2s, is_s, 0, fkeys=True)
    steps = [(g, ti) for g in range(16) for ti in range(len(tiles34))]
    upb = [0]; dnb = [0]

    def load_wu(g):
        wu_ = wus[g % 2]
        dma("pool", wu_, w_up[:, g * 256:(g + 1) * 256].rearrange("(k p) c -> p k c", p=128), r=[w_up], w=[wu_])

    def load_wd(g):
        wd_ = wds[0]
        dma("pool", wd_, w_down[g * 256:(g + 1) * 256, :].rearrange("(q p) c -> p q c", p=128), r=[w_down], w=[wd_])

    def up(k):
        g, ti = steps[k]; c0, T, is_s = tiles34[ti]
        if ti == 0 and g + 1 < 16:
            load_wu(g + 1)
        wu_ = wus[g % 2]; aT_ = aTs[k % 2]; rT_ = rTs[k % 2]
        for q in range(2):
            b_ = banks[upb[0] % 4]; upb[0] += 1
            for kk_ in range(8):
                mm(b_[:, 0:T], wu_[:, kk_, q * 128:(q + 1) * 128], h2T[:, kk_, c0:c0 + T], kk_ == 0, kk_ == 7, r=[wu_, (h2T, 0)], w=[b_])
            act(rT_[:, 0, 0:T], b_[:, 0:T], AF.Relu, r=[b_], w=[rT_])
            tt(aT_[:, q, 0:T], rT_[:, 0, 0:T], rT_[:, 0, 0:T], ALU.mult, eng=("pool" if q == 1 else "dve"), r=[rT_], w=[aT_])

    def down(k):
        g, ti = steps[k]; c0, T, is_s = tiles34[ti]
        wd_ = wds[g % 2]; aT_ = aTs[k % 2]
        for f in range(8):
            b_ = banks[4 + dnb[0] % 4]; dnb[0] += 1
            for q in range(2):
                mm(b_[:, 0:T], wd_[:, q, f * 128:(f + 1) * 128], aT_[:, q, 0:T], q == 0, q == 1, r=[wd_, aT_], w=[b_])
            resid_add(f, c0, T, is_s, b_, gt2p, gt2s, split=(f >= 6))

    load_wu(0); load_wd(0)
    up(0)
    for k in range(len(steps)):
        if k + 1 < len(steps):
            up(k + 1)
        down(k)
        g_, ti_ = steps[k]
        if ti_ == len(tiles34) - 1 and g_ + 1 < 16:
            load_wd(g_ + 1)
    fnp = pv[:, rows["fn"]:rows["fn"] + 8]
    for ti, (c0, T, is_s) in enumerate(tiles34):
        sq = nbuf["sq"]; rstd = nbuf["rstd"]; tmp = nbuf["tmp"]
        xv_ = xT[:, :, c0:c0 + T]
        xk8 = [(xT, f_) for f_ in range(8)]
        tt(sq[:, :, 0:T], xv_, xv_, ALU.mult, r=xk8, w=[sq])
        b_ = bank()
        for fc in range(8):
            mm(b_[:, 0:T], onesB, sq[:, fc, 0:T], fc == 0, fc == 7, w=[b_])
        ts(rstd[:, 0:T], b_[:, 0:T], 1.0 / D, RMS_EPS, ALU.mult, ALU.add, r=[b_], w=[rstd])
        r1 = nbuf["r1"]; r2 = nbuf["r2"]
        cp(r1[:, 0:T], rstd[:, 0:T], eng="act", r=[rstd], w=[r1])
        rsqrt_dve(rstd[:, 0:T], r1[:, 0:T], r2[:, 0:T], [rstd, r1, r2])
        tt(tmp[:, :, 0:T], xv_, rstd[:, 0:T].unsqueeze(1).broadcast_to([128, 8, T]), ALU.mult, r=xk8 + [rstd], w=[tmp])
        tt(tmp[:, :, 0:T], tmp[:, :, 0:T], fnp.unsqueeze(2).broadcast_to([128, 8, T]), ALU.mult, r=[tmp, pv], w=[tmp])
        for sub in range(0, T, 128):
            n = min(128, T - sub)
            ytm = nbuf["xtm"][nbuf["i"] % 2]; nbuf["i"] += 1
            for half in range(2):
                b_ = bank()
                for f4 in range(4):
                    fc = half * 4 + f4
                    tr(b_[0:n, f4 * 128:(f4 + 1) * 128], tmp[:, fc, sub:sub + n], identF, r=[tmp, identF], w=[b_])
                cp(ytm[0:n, half * 512:(half + 1) * 512], b_[0:n, :], eng="act", r=[b_], w=[ytm])
            dst = yp[c0 + sub:c0 + sub + n, :] if not is_s else ys[sub:sub + n, :]
            dma("sp", dst, ytm[0:n, :], r=[ytm], w=[yp if not is_s else ys])
    n = P.emit()
    print('SBUF bytes/partition: G', G.tot, 'max', SBMAX[0])
    S1.close(); G.close()
    return nc, dbg, n


_CACHE = {}


def _core_inputs(inp, c):
    f = lambda a: np.ascontiguousarray(a, dtype=np.float32)
    m = {"xp": inp["x_prompt"][c], "xs": inp["x_sample"][16 * c:16 * c + 16].reshape(64, D),
         "sh": inp["state_hgrn"][0, 16 * c:16 * c + 16], "sr": inp["state_rwkv"][0, 16 * c:16 * c + 16],
         "ss": inp["state_shift"][0, 16 * c:16 * c + 16], "cp": inp["c_prompt"][c:c + 1], "cs": inp["c_sample"][16 * c:16 * c + 16]}
    for k in ("norm1_w", "norm2_w", "rwkv_w0", "rwkv_a0", "rwkv_k_k", "rwkv_k_a", "rwkv_lnx_w", "rwkv_lnx_b", "rwkv_r_k", "final_norm_w"):
        m[k] = inp[k].reshape(8, 128)
    m["ada_w"] = inp["ada_w"][0]; m["ada_b"] = inp["ada_b"].reshape(48, 128); m["w_in"] = inp["w_in"][0]
    m["lb_logits"] = inp["lb_logits"].reshape(16, 128); m["hgrn_norm_w"] = inp["hgrn_norm_w"].reshape(1, 128)
    m["rwkv_mu"] = inp["rwkv_mu"].reshape(1, RC); m["rwkv_w2"] = inp["rwkv_w2"][0]; m["rwkv_a2"] = inp["rwkv_a2"][0]
    m["rwkv_g2"] = inp["rwkv_g2"][0]; m["w_out"] = inp["w_out"][0]; m["w_up"] = inp["w_up"][0]; m["w_down"] = inp["w_down"][0]
    return {k: f(v) for k, v in m.items()}


def kernel(**inputs):
    inp = {k: np.asarray(v) for k, v in inputs.items()}
    if "nc" not in _CACHE:
        _CACHE["nc"] = build()[0]
    nc = _CACHE["nc"]
    in_maps = [_core_inputs(inp, c) for c in range(8)]
    res = run_bass_kernel_spmd(nc, in_maps, core_ids=list(range(8)))
    R = res.results
    g = lambda n: [np.asarray(R[c][n], dtype=np.float32) for c in range(8)]
    y_prompt = np.stack(g("yp"), 0)
    y_sample = np.concatenate([a.reshape(16, 4, D) for a in g("ys")], 0)
    hgrn_p = np.stack(g("hp"), 0)[None]
    rwkv_p = np.stack(g("rp"), 0)[None]
    shift_p = np.concatenate(g("shp"), 0)[None]
    hgrn_s = np.concatenate(g("hs"), 0)[None]
    rwkv_s = np.concatenate(g("rs"), 0)[None]
    shift_s = np.concatenate(g("shs"), 0)[None]
    return (y_prompt, y_sample, hgrn_p, rwkv_p, shift_p, hgrn_s, rwkv_s, shift_s)
```

```python
import contextlib
import numpy as np
import concourse.bass as bass
import concourse.mybir as mybir
from concourse.bass_utils import run_bass_kernel_spmd

F32 = mybir.dt.float32
BF16 = mybir.dt.bfloat16
AF = mybir.ActivationFunctionType
ALU = mybir.AluOpType
AX = mybir.AxisListType

D = 1024
SEQ = 2048
NSEQ_S = 16
TS = 4
NT = SEQ + NSEQ_S * TS
INC = 9504
RC = 3360
NDMASEM = 12
SAME_ENG_RELAX = ()
RMS_EPS = 1e-6
LNX_EPS = 64e-5
C1 = -0.5 * float(np.exp(-0.5))


class Prog:
    def __init__(self, nc):
        self.nc = nc
        self.ops = []
        self.last_w = {}
        self.readers = {}
        self.bar = False
        self.sink = None
        self.lat_same = 0.2
        self.lat_cross = 0.3

    def peek(self, r, w):
        deps = set()
        rk = [self.key(x) for x in r]; wk = [self.key(x) for x in w]
        if self.bar:
            rk.append("__bar")
        for k in rk:
            if k in self.last_w:
                deps.add(self.last_w[k])
        for k in wk:
            if k in self.last_w:
                deps.add(self.last_w[k])
            deps |= self.readers.get(k, set())
        return deps

    @staticmethod
    def dur(eng, dma, n):
        if dma:
            return 0.1
        if eng == "pe":
            return 0.06 + n / 1200.0
        if eng == "dve":
            return 0.10 + n / 960.0
        if eng == "act":
            return 0.25 + n / 1200.0
        if eng == "pool":
            return 0.30 + n / 400.0
        return 0.1

    def merge(self, lists, lead=0):
        L = list(lists); K = len(L)
        done = set(); pos = [0] * K
        end = {}; free = {}
        tot = [max(1, len(x)) for x in L]

        def est_start(x):
            eng, fn, r, w, dma, n = x
            deps = self.peek(r, w)
            t = free.get(eng, 0.0)
            for d in deps:
                if d in end:
                    lat = self.lat_same if self.ops[d]["eng"] == eng else self.lat_cross
                    if self.ops[d]["eng"] == "pe" and eng == "pe":
                        lat = 0.0
                    t = max(t, end[d] + lat)
            return t

        while any(pos[i] < len(L[i]) for i in range(K)):
            progressed = False
            for si in range(K):
                while pos[si] < len(L[si]) and (L[si][pos[si]][0] == "mark" or (L[si][pos[si]][0] == "need" and L[si][pos[si]][1] in done)):
                    if L[si][pos[si]][0] == "mark":
                        done.add(L[si][pos[si]][1])
                    pos[si] += 1; progressed = True
            cand = []
            for si in range(K):
                if pos[si] >= len(L[si]):
                    continue
                x = L[si][pos[si]]
                if x[0] == "need":
                    continue
                cand.append((est_start(x), pos[si] / tot[si], si))
            if not cand:
                if progressed:
                    continue
                if all(pos[i] >= len(L[i]) for i in range(K)):
                    break
                raise RuntimeError("merge deadlock")
            t, _, si = min(cand)
            x = L[si][pos[si]]; pos[si] += 1
            eng, fn, r, w, dma, n = x
            idx = self.add(eng, fn, r, w, dma, n)
            d_ = self.dur(eng, dma, n)
            free[eng] = t + d_
            end[idx] = t + d_ + (2.5 if dma else 0.0)

    @staticmethod
    def key(x):
        if isinstance(x, tuple):
            if isinstance(x[0], str):
                return x
            return (x[0].tensor.name, x[1])
        if isinstance(x, str):
            return x
        return x.tensor.name

    def add(self, eng, fn, r=(), w=(), dma=False, n=256):
        if self.sink is not None:
            if eng in ("mark", "need"):
                self.sink.append((eng, fn))
            else:
                self.sink.append((eng, fn, list(r), list(w), dma, n))
            return None
        deps = set()
        rk = [self.key(x) for x in r]
        wk = [self.key(x) for x in w]
        if self.bar:
            rk.append("__bar")
        for k in rk:
            if k in self.last_w:
                deps.add(self.last_w[k])
        for k in wk:
            if k in self.last_w:
                deps.add(self.last_w[k])
            deps |= self.readers.get(k, set())
        idx = len(self.ops)
        self.ops.append(dict(eng=eng, fn=fn, deps=deps, dma=dma, users=0, stream=getattr(self, "tag", None), rk=rk, wk=wk))
        for k in rk:
            self.readers.setdefault(k, set()).add(idx)
        for k in wk:
            self.last_w[k] = idx
            self.readers[k] = set()
        return idx

    def barrier(self, dummy):
        allk = set(self.last_w.keys()) | set(self.readers.keys())
        allk.discard("__bar")
        self.bar = False
        self.add("dve", lambda e: e.memset(dummy, 0.0), r=(), w=list(allk) + ["__bar"])
        self.last_w = {"__bar": self.last_w["__bar"]}
        self.readers = {}
        self.bar = True

    def emit(self):
        nc = self.nc
        engs = {"pe": nc.tensor, "act": nc.scalar, "dve": nc.vector, "pool": nc.gpsimd, "sp": nc.sync}
        ops = self.ops

        seqn = {}
        for i_, o_ in enumerate(ops):
            if not o_["dma"]:
                seqn[o_["eng"]] = seqn.get(o_["eng"], 0) + 1
                o_["seq"] = seqn[o_["eng"]]

        def skip(d, o):
            if ops[d]["dma"] or o["dma"]:
                return False
            if ops[d]["eng"] != o["eng"]:
                return False
            if o["eng"] == "pe":
                return True
            if o["eng"] in SAME_ENG_RELAX and o["seq"] - ops[d]["seq"] >= 2:
                return True
            return False

        for o in ops:
            for d in o["deps"]:
                if not skip(d, o):
                    ops[d]["users"] += 1
        with contextlib.ExitStack() as st:
            csem = {e: st.enter_context(nc.semaphore("c_" + e)) for e in engs}
            dsem = {e: [st.enter_context(nc.semaphore(f"d_{e}{i}")) for i in range(NDMASEM)]
                    for e in ("act", "pool", "sp")}
            ccnt = {e: 0 for e in engs}
            dcnt = {e: [0] * NDMASEM for e in dsem}
            drr = {e: 0 for e in dsem}
            waited = {e: {} for e in engs}
            sig = {}

            def wait(e, sem, val):
                k = id(sem)
                if val <= 0 or waited[e].get(k, 0) >= val:
                    return
                engs[e].wait_ge(sem, val)
                waited[e][k] = val

            for i, o in enumerate(ops):
                e = o["eng"]
                for d in sorted(o["deps"]):
                    if d not in sig or skip(d, o):
                        continue
                    s, v = sig[d]
                    wait(e, s, v)
                if o["dma"]:
                    j = drr[e]
                    drr[e] = (j + 1) % NDMASEM
                    s = dsem[e][j]
                    wait(e, s, dcnt[e][j])
                    ins = o["fn"](engs[e])
                    dcnt[e][j] += 16
                    ins.then_inc(s, 16)
                    sig[i] = (s, dcnt[e][j])
                else:
                    ins = o["fn"](engs[e])
                    if o["users"] > 0:
                        ccnt[e] += 1
                        ins.then_inc(csem[e], 1)
                        sig[i] = (csem[e], ccnt[e])
            for e in dsem:
                for j in range(NDMASEM):
                    if dcnt[e][j] > 0:
                        nc.sync.wait_ge(dsem[e][j], dcnt[e][j])
        return len(ops)


def build(debug=None):
    nc = bass.Bass("TRN2", target_bir_lowering=False)
    P = Prog(nc)
    din = lambda n, s: nc.dram_tensor(n, list(s), F32, kind="ExternalInput").ap()
    dout = lambda n, s: nc.dram_tensor(n, list(s), F32, kind="ExternalOutput").ap()
    xp = din("xp", [SEQ, D]); xs = din("xs", [64, D])
    sh_in = din("sh", [16, 8, 128, 128]); sr_in = din("sr", [16, 16, 64, 64]); ss_in = din("ss", [16, RC])
    cpr = din("cp", [1, D]); csm = din("cs", [16, D])
    norm1_w = din("norm1_w", [8, 128]); norm2_w = din("norm2_w", [8, 128])
    ada_w = din("ada_w", [D, 6 * D]); ada_b = din("ada_b", [48, 128])
    w_in = din("w_in", [D, INC]); lb_logits = din("lb_logits", [16, 128])
    hgrn_norm_w = din("hgrn_norm_w", [1, 128]); mu = din("rwkv_mu", [1, RC])
    w0 = din("rwkv_w0", [8, 128]); w2 = din("rwkv_w2", [64, D]); a0 = din("rwkv_a0", [8, 128])
    a2 = din("rwkv_a2", [64, D]); g2 = din("rwkv_g2", [160, D]); k_k = din("rwkv_k_k", [8, 128])
    k_a = din("rwkv_k_a", [8, 128]); r_k = din("rwkv_r_k", [8, 128]); lnx_w = din("rwkv_lnx_w", [8, 128])
    lnx_b = din("rwkv_lnx_b", [8, 128]); w_out = din("w_out", [D, D]); w_up = din("w_up", [D, 4 * D])
    w_down = din("w_down", [4 * D, D]); fnw = din("final_norm_w", [8, 128])
    yp = dout("yp", [SEQ, D]); ys = dout("ys", [64, D])
    hp = dout("hp", [8, 128, 128]); rp = dout("rp", [16, 64, 64]); shp = dout("shp", [1, RC])
    hs = dout("hs", [16, 8, 128, 128]); rs = dout("rs", [16, 16, 64, 64]); shs = dout("shs", [16, RC])
    dbg = {}

    cnt = [0]

    def uname(n):
        cnt[0] += 1
        return f"{n}_{cnt[0]}"

    SBTOT = [0]; SBMAX = [0]

    class Scope:
        def __init__(self):
            self.st = contextlib.ExitStack()

        def sb(self, name, shape, dt=F32):
            nb = int(np.prod(shape[1:])) * (4 if dt == F32 else 2)
            SBTOT[0] += nb; self.tot = getattr(self, "tot", 0) + nb
            SBMAX[0] = max(SBMAX[0], SBTOT[0])
            return self.st.enter_context(nc.sbuf_tensor(uname(name), list(shape), dt)).ap()

        def close(self):
            SBTOT[0] -= getattr(self, "tot", 0)
            self.st.close()

    G = Scope()
    banks = [nc.alloc_psum_tensor(f"bank{i}", [128, 512], F32).ap() for i in range(8)]
    bk = [0]

    pools = {None: list(range(8)), "H": [1, 2], "RF": [3, 4], "RB": [6, 7]}
    cur = [None]
    bkp = {"H": 0, "RF": 0, "RB": 0}

    def bank():
        if cur[0] is None:
            b = banks[bk[0] % 8]
            bk[0] += 1
            return b
        pl = pools[cur[0]]
        b = banks[pl[bkp[cur[0]] % len(pl)]]
        bkp[cur[0]] += 1
        return b

    def dump(name, ap, shape, keys=None):
        if debug is None or (name not in debug and "all" not in debug):
            return
        d = nc.dram_tensor("dbg_" + name, list(shape), F32, kind="ExternalOutput").ap()
        dbg[name] = d
        q = "pool" if ap.dtype != F32 else "sp"
        rk = [ap] + [(ap, t_) for t_ in range(16)] if keys is None else keys
        P.add(q, lambda e: e.dma_start(out=d, in_=ap), r=rk, w=[d], dma=True)

    def dma(q, out, in_, r=(), w=()):
        P.add(q, lambda e: e.dma_start(out=out, in_=in_), r=list(r) or [in_], w=list(w) or [out], dma=True)

    def act(out, in_, func, bias=0.0, scale=1.0, r=(), w=(), accum=None):
        rr = list(r) or [in_]
        if not isinstance(bias, float):
            rr.append(bias)
        if not isinstance(scale, float):
            rr.append(scale)
        ww = list(w) or [out]
        if accum is not None:
            ww.append(accum)
        if accum is not None:
            P.add("act", lambda e: e.activation(out=out, in_=in_, func=func, bias=bias, scale=scale, accum_out=accum), r=rr, w=ww, n=out.free_size())
        else:
            P.add("act", lambda e: e.activation(out=out, in_=in_, func=func, bias=bias, scale=scale), r=rr, w=ww, n=out.free_size())

    def tt(out, in0, in1, op, eng="dve", r=(), w=()):
        P.add(eng, lambda e: e.tensor_tensor(out=out, in0=in0, in1=in1, op=op), r=list(r) or [in0, in1], w=list(w) or [out], n=out.free_size())

    def ts(out, in0, s1, s2, op0, op1=None, eng="dve", r=(), w=()):
        rr = list(r) or [in0]
        for s in (s1, s2):
            if s is not None and not isinstance(s, float):
                rr.append(s)
        if op1 is None:
            P.add(eng, lambda e: e.tensor_scalar(out=out, in0=in0, scalar1=s1, scalar2=None, op0=op0), r=rr, w=list(w) or [out], n=out.free_size())
        else:
            P.add(eng, lambda e: e.tensor_scalar(out=out, in0=in0, scalar1=s1, scalar2=s2, op0=op0, op1=op1), r=rr, w=list(w) or [out], n=out.free_size())

    def aff(out, in_, scale, bias, r=(), w=()):
        act(out, in_, AF.Identity, bias=bias, scale=scale, r=r, w=w)

    def stt(out, in0, scalar, in1, op0, op1, r=(), w=()):
        rr = list(r) or [in0, in1]
        if not isinstance(scalar, float):
            rr.append(scalar)
        P.add("dve", lambda e: e.scalar_tensor_tensor(out=out, in0=in0, scalar=scalar, in1=in1, op0=op0, op1=op1), r=rr, w=list(w) or [out], n=out.free_size())

    def cp(out, in_, eng="dve", r=(), w=()):
        if eng == "act":
            act(out, in_, AF.Copy, r=r, w=w)
        else:
            P.add(eng, lambda e: e.tensor_copy(out=out, in_=in_), r=list(r) or [in_], w=list(w) or [out], n=out.free_size())

    def memset(ap, val, eng="pool", w=()):
        P.add(eng, lambda e: e.memset(ap, val), w=list(w) or [ap], n=ap.free_size())

    def mm(out, lhsT, rhs, start, stop, r=(), w=()):
        P.add("pe", lambda e: e.matmul(out, lhsT, rhs, start=start, stop=stop), r=list(r) or [lhsT, rhs], w=list(w) or [out], n=rhs.free_size() + 60)

    def tr(out, in_, ident, r=(), w=()):
        P.add("pe", lambda e: e.transpose(out=out, in_=in_, identity=ident), r=list(r) or [in_, ident], w=list(w) or [out], n=ident.free_size() + 60)

    I32 = mybir.dt.int32

    def rsqrt_dve(y, x, t1, keys):
        P.add("dve", lambda e: e.tensor_scalar(out=y.bitcast(I32), in0=x.bitcast(I32), scalar1=-0.5, scalar2=1597463007.0,
                                               op0=ALU.mult, op1=ALU.add), r=keys, w=keys)
        for _ in range(2):
            tt(t1, y, y, ALU.mult, r=keys, w=keys)
            stt(t1, t1, -0.5, x, ALU.mult, ALU.mult, r=keys, w=keys)
            stt(y, t1, 1.5, y, ALU.add, ALU.mult, r=keys, w=keys)

    def rsqrt_(out, in_, scale, eps, tA, tB):
        ts(tA, in_, scale, eps, ALU.mult, ALU.add)
        rsqrt_dve(out, tA, tB, [out, tA, tB])

    ones128 = G.sb("ones128", [128, 128]); identF = G.sb("identF", [128, 128]); identB = G.sb("identB", [128, 128], BF16)
    memset(ones128, 1.0)
    P.add("pool", lambda e: e.affine_select(out=identF, in_=ones128, pattern=[[1, 128]], compare_op=ALU.is_equal,
                                           fill=0.0, base=0, channel_multiplier=-1), r=[ones128], w=[identF])
    cp(identB, identF)
    onesB = G.sb("onesB", [128, 128], BF16)
    cp(onesB, ones128)
    blkones = G.sb("blkones", [128, 128], BF16)
    memset(blkones, 0.0)
    memset(blkones[0:64, 0:64], 1.0, w=[blkones]); memset(blkones[64:128, 64:128], 1.0, w=[blkones])
    blkneg = G.sb("blkneg", [128, 128])
    memset(blkneg, 0.0)
    memset(blkneg[0:64, 0:64], -1.0, w=[blkneg]); memset(blkneg[64:128, 64:128], -1.0, w=[blkneg])
    blkpos = G.sb("blkpos", [128, 128])
    memset(blkpos, 0.0)
    memset(blkpos[0:64, 0:64], 1.0, w=[blkpos]); memset(blkpos[64:128, 64:128], 1.0, w=[blkpos])
    dummy = G.sb("dummy", [128, 1])
    masks = {}
    for C in (64, 4):
        mi = G.sb(f"mi{C}", [C, C]); ms = G.sb(f"ms{C}", [C, C]); mls = G.sb(f"mls{C}", [C, C])
        P.add("pool", lambda e, mi=mi, C=C: e.affine_select(out=mi, in_=ones128[0:C, 0:C], pattern=[[1, C]], compare_op=ALU.is_ge,
                                                         fill=0.0, base=0, channel_multiplier=-1), r=[ones128], w=[mi])
        P.add("pool", lambda e, ms=ms, C=C: e.affine_select(out=ms, in_=ones128[0:C, 0:C], pattern=[[1, C]], compare_op=ALU.is_gt,
                                                         fill=0.0, base=0, channel_multiplier=-1), r=[ones128], w=[ms])
        P.add("pool", lambda e, mls=mls, C=C: e.affine_select(out=mls, in_=ones128[0:C, 0:C], pattern=[[-1, C]], compare_op=ALU.is_gt,
                                                           fill=0.0, base=0, channel_multiplier=1), r=[ones128], w=[mls])
        msn = G.sb(f"msn{C}", [C, C]); mlsn = G.sb(f"mlsn{C}", [C, C])
        ts(msn, ms, -1.0, None, ALU.mult); ts(mlsn, mls, -1.0, None, ALU.mult)
        masks[C] = dict(incl=mi, strict=ms, nstrict=msn, nlstrict=mlsn)

    stgA = G.sb("stgA", [128, 128]); stgB = G.sb("stgB", [48, 128])
    memset(stgA, 0.0)
    rows = {}
    r0 = 0
    for nm, src, n in (("n1", norm1_w, 8), ("n2", norm2_w, 8), ("lb", lb_logits, 16), ("w0", w0, 8), ("a0", a0, 8),
                       ("kk", k_k, 8), ("ka", k_a, 8), ("rk", r_k, 8), ("lw", lnx_w, 8), ("lbb", lnx_b, 8), ("fn", fnw, 8)):
        dma("sp", stgA[r0:r0 + n, :], src, w=[stgA]); rows[nm] = r0; r0 += n
    dma("sp", stgA[r0:r0 + 26, :], mu[:, 0:3328].rearrange("o (n p) -> (o n) p", p=128), w=[stgA]); rows["mu"] = r0; r0 += 26
    dma("sp", stgA[r0:r0 + 1, 0:32], mu[:, 3328:3360], w=[stgA]); rows["mu2"] = r0; r0 += 1
    assert r0 <= 128
    dma("sp", stgB, ada_b)
    pv = G.sb("pv", [128, 128]); pvB = G.sb("pvB", [128, 48])
    b_ = bank(); tr(b_[:, 0:128], stgA, identF); cp(pv, b_[:, 0:128])
    b_ = bank(); tr(b_[:, 0:48], stgB, identF[0:48, 0:48]); cp(pvB, b_[:, 0:48])
    col = lambda nm, i=0: pv[:, rows[nm] + i: rows[nm] + i + 1]
    pv2 = G.sb("pv2", [128, 128]); ts(pv2, pv, -1.0, 1.0, ALU.mult, ALU.add)
    col2 = lambda nm, i=0: pv2[:, rows[nm] + i: rows[nm] + i + 1]
    hs1 = G.sb("hs1", [128, 8]); hs2 = G.sb("hs2", [128, 8]); hs1n = G.sb("hs1n", [128, 8]); hs2c = G.sb("hs2c", [128, 8])
    tt(hs1, pv[:, rows["lb"]:rows["lb"] + 8], pv[:, rows["lb"] + 8:rows["lb"] + 16], ALU.subtract)
    act(hs2, hs1, AF.Tanh, scale=0.5)
    ts(hs1, hs2, -0.25, 0.25, ALU.mult, ALU.add)
    ts(hs2, hs2, 0.25, 0.75, ALU.mult, ALU.add)
    ts(hs1n, hs1, -1.0, None, ALU.mult)
    ts(hs2c, hs2, -1.0, 1.0, ALU.mult, ALU.add)
    hw0 = G.sb("hw0", [128, 8]); ha0 = G.sb("ha0", [128, 8]); omka = G.sb("omka", [128, 8])
    ts(hw0, pv[:, rows["w0"]:rows["w0"] + 8], 0.5, None, ALU.mult)
    ts(ha0, pv[:, rows["a0"]:rows["a0"] + 8], 0.5, None, ALU.mult)
    ts(omka, pv[:, rows["ka"]:rows["ka"] + 8], -1.0, 1.0, ALU.mult, ALU.add)
    nwbc = G.sb("nwbc", [64, 128]); dma("sp", nwbc, hgrn_norm_w.partition_broadcast(64))

    modT = G.sb("modT", [128, 48, 17])
    g1p = G.sb("g1p", [128, 8]); g2p = G.sb("g2p", [128, 8])
    sh1p = G.sb("sh1p", [128, 8]); sh2p = G.sb("sh2p", [128, 8]); gt1p = G.sb("gt1p", [128, 8]); gt2p = G.sb("gt2p", [128, 8])
    mT = G.sb("mT", [128, 8, NT], BF16)
    S2 = Scope()
    hT = S2.sb("hT", [128, 8, NT], BF16)
    S0 = Scope()
    c17 = S0.sb("c17", [17, D]); s17 = S0.sb("s17", [17, D]); s17b = S0.sb("s17b", [17, D], BF16); scT = S0.sb("scT", [128, 8, 32], BF16)
    dma("sp", c17[0:1, :], cpr, w=[c17]); dma("sp", c17[1:17, :], csm, w=[c17])
    act(s17, c17, AF.Tanh, scale=0.5)
    ts(s17, s17, 0.5, 0.5, ALU.mult, ALU.add)
    tt(s17b, s17, c17, ALU.mult)
    b_ = bank(); bb = b_.bitcast(BF16)
    for k in range(8):
        tr(bb[:, k * 32:k * 32 + 17], s17b[:, k * 128:(k + 1) * 128], identB[0:17, 0:17], w=[b_])
    cp(scT[:, :, 0:17], bb[:, 0:256].rearrange("p (k c) -> p k c", c=32)[:, :, 0:17], r=[b_])
    awb = [S0.sb("awb", [128, 8, 512], BF16) for _ in range(2)]
    def ada_group(g):
            wt = awb[g % 2]
            dma("pool", wt, ada_w[:, g * 512:(g + 1) * 512].rearrange("(k p) c -> p k c", p=128), r=[ada_w], w=[wt])
            b_ = bank()
            for m in range(4):
                for k in range(8):
                    mm(b_[:, m * 32:m * 32 + 17], wt[:, k, m * 128:(m + 1) * 128], scT[:, k, 0:17], k == 0, k == 7, w=[b_])
            for m in range(4):
                ts(modT[:, g * 4 + m, :], b_[:, m * 32:m * 32 + 17], pvB[:, g * 4 + m:g * 4 + m + 1], None, ALU.add, r=[b_, pvB], w=[modT])

    def modp(part):
        return modT[:, part * 8:(part + 1) * 8, 0]
    def mods(part):
        return modT[:, part * 8:(part + 1) * 8, 1:17].unsqueeze(3).broadcast_to([128, 8, 16, 4])
    def mods3(part):
        return modT[:, part * 8:(part + 1) * 8, 1:17]

    def mk_mods(dst, part, nm=None):
        for t4 in range(4):
            if nm is None:
                cp(dst[:, :, t4::4], mods3(part), r=[modT], w=[dst])
            else:
                stt(dst[:, :, t4::4], mods3(part), 1.0, pv[:, rows[nm]:rows[nm] + 8].unsqueeze(2).broadcast_to([128, 8, 16]),
                    ALU.add, ALU.mult, r=[modT, pv], w=[dst])

    tiles = [(i * 256, 256, 64, 4, False) for i in range(8)] + [(SEQ + 16 * i, 16, 4, 4, True) for i in range(4)]

    def norm_to(hT_dst, xT_t, T, gp, shp_, gs, shs_, is_s, ti, fkeys=False):
        xk_ = [(xT_t, f_) for f_ in range(8)] if fkeys else [xT_t]
        sq = nbuf["sq"]
        tt(sq[:, :, 0:T], xT_t[:, :, 0:T], xT_t[:, :, 0:T], ALU.mult, r=xk_, w=[sq])
        b_ = bank()
        for fc in range(8):
            mm(b_[:, 0:T], onesB, sq[:, fc, 0:T], fc == 0, fc == 7, w=[b_])
        rstd = nbuf["rstd"]
        ts(rstd[:, 0:T], b_[:, 0:T], 1.0 / D, RMS_EPS, ALU.mult, ALU.add, r=[b_], w=[rstd])
        r1 = nbuf["r1"]; r2 = nbuf["r2"]
        cp(r1[:, 0:T], rstd[:, 0:T], eng="act", r=[rstd], w=[r1])
        rsqrt_dve(rstd[:, 0:T], r1[:, 0:T], r2[:, 0:T], [rstd, r1, r2])
        tmp = nbuf["tmp"]
        tt(tmp[:, :, 0:T], xT_t[:, :, 0:T], rstd[:, 0:T].unsqueeze(1).broadcast_to([128, 8, T]), ALU.mult, r=xk_ + [rstd], w=[tmp])
        if not is_s:
            for fc in range(8):
                ts(hT_dst[:, fc, :], tmp[:, fc, 0:T], gp[:, fc:fc + 1], shp_[:, fc:fc + 1], ALU.mult, ALU.add,
                   r=[tmp, gp, shp_], w=[(hT_dst, ti)])
        else:
            tt(tmp[:, :, 0:T], tmp[:, :, 0:T], gs, ALU.mult, r=[tmp, gs], w=[tmp])
            tt(hT_dst, tmp[:, :, 0:T], shs_, ALU.add, r=[tmp, shs_], w=[(hT_dst, ti)])

    def load_xT(xT_t, c0, T, ti, fkeys=False):
        for sub in range(0, T, 128):
            n = min(128, T - sub)
            xtm = nbuf["xtm"][nbuf["i"] % 2]; nbuf["i"] += 1
            src = xp[c0 + sub:c0 + sub + n, :] if c0 < SEQ else xs[c0 - SEQ + sub:c0 - SEQ + sub + n, :]
            dma("sp", xtm[0:n, :], src, w=[xtm])
            for half in range(2):
                b_ = bank()
                for f4 in range(4):
                    fc = half * 4 + f4
                    tr(b_[:, f4 * 128:f4 * 128 + n], xtm[0:n, fc * 128:(fc + 1) * 128], identF[0:n, 0:n], w=[b_])
                cp(xT_t[:, half * 4:half * 4 + 4, sub:sub + n], b_.rearrange("p (f t) -> p f t", t=128)[:, :, 0:n], eng="act", r=[b_],
                   w=([(xT_t, f_) for f_ in range(half * 4, half * 4 + 4)] if fkeys else [xT_t]))

    for g in range(4):
        ada_group(g)
    stt(g1p, modp(1), 1.0, pv[:, rows["n1"]:rows["n1"] + 8], ALU.add, ALU.mult, r=[modT, pv])
    cp(sh1p, modp(0), r=[modT])
    S1 = Scope()
    g1s = S1.sb("g1s", [128, 8, 64]); sh1s = S1.sb("sh1s", [128, 8, 64])
    mk_mods(sh1s, 0); mk_mods(g1s, 1, "n1")
    nbuf = dict(sq=S1.sb("sq", [128, 8, 512], BF16), rstd=S1.sb("rstd", [128, 512]), tmp=S1.sb("ntmp", [128, 8, 512]), r1=S1.sb("r1", [128, 512]), r2=S1.sb("r2", [128, 512]),
                xtm=[S1.sb("xtm", [128, D]) for _ in range(2)], i=0)
    xT_ts = [S1.sb("xT_t", [128, 8, 512]) for _ in range(2)]
    gnext = 4
    for ti, (c0, T, C, NCH, is_s) in enumerate(tiles):
        xT_t = xT_ts[ti % 2]
        load_xT(xT_t, c0, T, ti)
        so = max(c0 - SEQ, 0)
        norm_to(hT[:, :, c0:c0 + T], xT_t, T, g1p, sh1p, g1s[:, :, so:so + min(T, 16)], sh1s[:, :, so:so + min(T, 16)], is_s, ti)
        if gnext < 12:
            ada_group(gnext); gnext += 1
    while gnext < 12:
        ada_group(gnext); gnext += 1
    stt(g2p, modp(4), 1.0, pv[:, rows["n2"]:rows["n2"] + 8], ALU.add, ALU.mult, r=[modT, pv])
    for dst, part in ((gt1p, 2), (sh2p, 3), (gt2p, 5)):
        cp(dst, modp(part), r=[modT])
    S1.close()
    S0.close()
    P.barrier(dummy)

    W = lambda name, shape, dt=F32: S2.sb(name, shape, dt)
    wbH = W("wbH", [128, 8, 640], BF16); wbR = W("wbR", [128, 8, 512], BF16)
    wl = wbH[:, :, 0:288]
    w2b = W("w2b", [128, D], BF16)
    g2b = W("g2b", [128, D], BF16); g2c = W("g2c", [32, D], BF16)
    TW = W("TW", [128, NT], BF16); SG1 = W("SG1", [128, NT], BF16); SG2 = W("SG2", [32, NT], BF16)
    ssT = W("ssT", [128, 27, 16]); lastc = W("lastc", [128, 27, 17])
    prevs = W("prevs", [128, 8])
    dsh = W("dsh", [128, 256])
    S0all = W("S0all", [128, 4, 128]); S0bf = W("S0bf", [128, 4, 128], BF16); Souts = S0all
    Sh = W("Sh", [128, 128]); Shbf = W("Shbf", [128, 128], BF16)
    R0in = W("R0in", [128, 4, 128]); R0blk = W("R0blk", [128, 4, 128]); R0bf = W("R0bf", [128, 4, 128], BF16); Routs = R0blk
    Sr = W("Sr", [128, 128]); Srbf = W("Srbf", [128, 128], BF16); Srt = W("Srt", [128, 128])
    f32t = {n: W(n, [128, 256]) for n in ("th", "fS", "kS", "Pc", "rP", "Ga", "t1", "t2", "Gmm", "kap", "kmod", "bS", "gS",
                                          "xr", "xk", "xv", "rG", "aS", "u1", "u2", "e1")}
    f32t["n1"] = f32t["kmod"]; f32t["n2"] = f32t["u1"]
    oTs = [W("oT", [128, 256]) for _ in range(2)]
    rstH = W("rstH", [128, 256]); rstR = W("rstR", [128, 256]); memset(rstH, 0.0); memset(rstR, 0.0)
    bft = {n: W(n, [128, 256], BF16) for n in ("qt", "kt", "kh", "vT", "sqb", "kpt", "bt", "bh", "ktl", "khr", "vTr")}
    KR = W("KR", [128, 4, 128], BF16)
    tm = {n: W(n, [64, 4, 128], BF16) for n in ("Vtm", "khtm", "W1", "Un", "Ktp", "onb", "onr")}
    Abm = {n: W(n, [64, 4, 128], BF16) for n in ("Y2", "Z2", "TTa", "TTb")}
    IFs = [dict(**{n: W(n, [64, 4, 128], BF16) for n in ("Z", "Y", "PbT", "PkT", "AkT", "Vtr", "Ktm", "bhtm", "khrtm")},
                rt=W("rt", [128, 256], BF16), **{n: W(n, [128, 256]) for n in ("Gm", "bon", "Gb")}) for _ in range(2)]
    attnT = W("attnT", [64, 4, 64], BF16)
    Osb = W("Osb", [64, 4, 128]); Osq = W("Osq", [64, 4, 128]); st_ = W("st_", [64, 16]); stA = W("stA", [64, 16]); stB = W("stB", [64, 16])
    srow = Osq.rearrange("p c f -> p (c f)")[0:17, :]
    Osb2 = W("Osb2", [64, 4, 128]); Osq2 = W("Osq2", [64, 4, 128]); st3 = W("st3", [64, 16]); st4 = W("st4", [64, 16]); stC = W("stC", [64, 16]); stD = W("stD", [64, 16])
    Mblk = W("Mblk", [128, 4, 128], BF16); Ncomb = W("Ncomb", [128, 4, 128]); RpT = W("RpT", [128, 256], BF16)

    dma("pool", w2b[0:64, :], w2, w=[w2b]); dma("pool", w2b[64:128, :], a2, w=[w2b])
    dma("pool", g2b, g2[0:128, :]); dma("pool", g2c, g2[128:160, :])
    dma("pool", wl, w_in[:, 7168:7456].rearrange("(k p) c -> p k c", p=128), r=[w_in], w=[wl])
    for p4 in range(0, 27, 4):
        n4 = min(4, 27 - p4); wdt = min(n4 * 128, RC - p4 * 128)
        memset(srow, 0.0, eng="dve")
        dma("sp", srow[0:16, 0:wdt], ss_in[:, p4 * 128:p4 * 128 + wdt], r=[ss_in], w=[srow])
        b_ = bank()
        for q in range(n4):
            tr(b_[:, q * 16:(q + 1) * 16], srow[0:16, q * 128:(q + 1) * 128], identF[0:16, 0:16], r=[srow, identF], w=[b_])
        cp(ssT[:, p4:p4 + n4, :], b_[:, 0:n4 * 16].rearrange("p (q s) -> p q s", s=16), r=[b_], w=[ssT])
    memset(prevs, 0.0); memset(lastc, 0.0)

    def proj(wt, c_lo, c_hi, ti, c0, T):
        b_ = bank(); n = c_hi - c_lo
        for k in range(8):
            mm(b_[0:n, 0:T], wt[:, k, c_lo:c_hi], hT[:, k, c0:c0 + T], k == 0, k == 7, r=[wt, (hT, ti)], w=[b_])
        return b_

    def shift_mix(dst, ps, n, T, is_s, mucol, pcol, part, last_tile, s0=0, omucol=None):
        aff(dsh[0:n, 0:T], ps[0:n, 0:T], omucol[0:n, :], 0.0, r=[ps, pv2], w=[dsh])
        if not is_s:
            stt(dst[0:n, 1:T], ps[0:n, 0:T - 1], mucol[0:n, :], dsh[0:n, 1:T], ALU.mult, ALU.add, r=[ps, dsh, pv], w=[dst])
            stt(dst[0:n, 0:1], prevs[0:n, pcol:pcol + 1], mucol[0:n, :], dsh[0:n, 0:1], ALU.mult, ALU.add, r=[prevs, dsh, pv], w=[dst])
            cp(prevs[0:n, pcol:pcol + 1], ps[0:n, T - 1:T], eng="act", r=[ps, dst], w=[prevs])
            if last_tile:
                cp(lastc[0:n, part, 0:1], ps[0:n, T - 1:T], eng="act", r=[ps], w=[lastc])
        else:
            p3 = ps[0:n, 0:T].rearrange("p (s t) -> p s t", t=4); d3 = dsh[0:n, 0:T].rearrange("p (s t) -> p s t", t=4)
            o3 = dst[0:n, 0:T].rearrange("p (s t) -> p s t", t=4)
            stt(o3[:, :, 1:4], p3[:, :, 0:3], mucol[0:n, :], d3[:, :, 1:4], ALU.mult, ALU.add, r=[ps, dsh, pv], w=[dst])
            stt(o3[:, :, 0], ssT[0:n, part, s0:s0 + T // 4], mucol[0:n, :], d3[:, :, 0], ALU.mult, ALU.add, r=[ssT, dsh, pv], w=[dst])
            cp(lastc[0:n, part, 1 + s0:1 + s0 + T // 4], p3[:, :, 3], eng="act", r=[ps], w=[lastc])

    def sigm(dst, src, r=(), w=(), bias=0.0):
        act(dst, src, AF.Tanh, bias=bias, scale=0.5, r=r, w=w)
        aff(dst, dst, 0.5, 0.5)

    for ti, (c0, T, C, NCH, is_s) in enumerate(tiles):
        last = (c0 + T == SEQ)
        b_ = proj(wl, 0, 128, ti, c0, T)
        shift_mix(f32t["t1"], b_, 128, T, is_s, col("mu", 24), 0, 24, last, max(c0 - SEQ, 0) // 4, omucol=col2("mu", 24))
        act(TW[0:64, c0:c0 + T], f32t["t1"][0:64, 0:T], AF.Tanh, r=[f32t["t1"]], w=[(TW, ti)])
        cp(TW[64:128, c0:c0 + T], f32t["t1"][64:128, 0:T], eng="act", r=[f32t["t1"]], w=[(TW, ti)])
        b_ = proj(wl, 128, 256, ti, c0, T)
        shift_mix(f32t["t1"], b_, 128, T, is_s, col("mu", 25), 1, 25, last, max(c0 - SEQ, 0) // 4, omucol=col2("mu", 25))
        sigm(f32t["t2"][:, 0:T], f32t["t1"][:, 0:T], r=[f32t["t1"]], w=[f32t["t2"]])
        cp(SG1[:, c0:c0 + T], f32t["t2"][:, 0:T], r=[f32t["t2"]], w=[(SG1, ti)])
        b_ = proj(wl, 256, 288, ti, c0, T)
        shift_mix(f32t["t1"], b_, 32, T, is_s, col("mu2"), 2, 26, last, max(c0 - SEQ, 0) // 4, omucol=col2("mu2"))
        sigm(f32t["t2"][0:32, 0:T], f32t["t1"][0:32, 0:T], r=[f32t["t1"]], w=[f32t["t2"]])
        cp(SG2[:, c0:c0 + T], f32t["t2"][0:32, 0:T], r=[f32t["t2"]], w=[(SG2, ti)])

    if debug is not None and debug.get("_stop") == "lora":
        dump("TW", TW, [128, NT]); dump("SG1", SG1, [128, NT]); dump("SG2", SG2, [32, NT]); dump("lastc", lastc.rearrange("p a b -> p (a b)"), [128, 27 * 17])
        n = P.emit()
        return nc, dbg, n

    def chunk3(ap, C):
        return ap.rearrange("p (c t) -> p c t", t=C)

    def to_tm(dst, src_bf, T, C, NCH):
        for c4 in range(0, NCH, 8):
            b_ = bank(); bb = b_.bitcast(BF16)
            for c in range(c4, min(c4 + 8, NCH)):
                tr(bb[0:C, (c - c4) * 128:(c - c4 + 1) * 128], src_bf[:, c * C:(c + 1) * C], identB, r=[src_bf, identB], w=[b_])
            n8 = min(8, NCH - c4)
            cp(dst[0:C, c4:c4 + n8, :], bb[0:C, 0:n8 * 128].rearrange("p (c f) -> p c f", f=128), eng="act", r=[b_], w=[dst])

    def cumprod(dst, src, T, C, NCH, rst):
        cp(chunk3(rst[:, 0:T], C)[:, :, 0:1], chunk3(src[:, 0:T], C)[:, :, 0:1], eng="act", r=[src], w=[rst])
        P.add("dve", lambda e: e.tensor_tensor_scan(out=dst[:, 0:T], data0=src[:, 0:T], data1=rst[:, 0:T], initial=1.0,
                                                    op0=ALU.mult, op1=ALU.max), r=[src, rst], w=[dst])
        if C != 64:
            memset(chunk3(rst[:, 0:T], C)[:, :, 0:1], 0.0, eng="pool", w=[rst])

    def lastbc(ap, T, C, NCH):
        return chunk3(ap[:, 0:T], C)[:, :, C - 1:C].broadcast_to([128, NCH, C])


    def rwkv_front(j, ti, c0, T, C, NCH, is_s, last, wb, gi):
        F_ = f32t; B_ = bft; M_ = masks[C]; jc = slice(j * 128, (j + 1) * 128)
        I_ = IFs[gi % 2]
        if gi >= 2:
            P.add("need", ("B", gi - 2))
        X = lambda n: F_[n][:, 0:T]
        Bx = lambda n: B_[n][:, 0:T]
        Gm = I_["Gm"]; bon = I_["bon"]; Gb = I_["Gb"]; rt = I_["rt"]
        so_ = max(c0 - SEQ, 0) // 4
        br = proj(wb, 0, 128, ti, c0, T); shift_mix(F_["xr"], br, 128, T, is_s, col("mu", j), 3, j, last, so_, omucol=col2("mu", j))
        bk_ = proj(wb, 128, 256, ti, c0, T); shift_mix(F_["xk"], bk_, 128, T, is_s, col("mu", 8 + j), 4, 8 + j, last, so_, omucol=col2("mu", 8 + j))
        bv = proj(wb, 256, 384, ti, c0, T); shift_mix(F_["xv"], bv, 128, T, is_s, col("mu", 16 + j), 5, 16 + j, last, so_, omucol=col2("mu", 16 + j))
        bgb = proj(wb, 384, 512, ti, c0, T)
        sigm(Gb[:, 0:T], bgb[:, 0:T], r=[bgb], w=[Gb])
        bw = bank(); mm(bw[:, 0:T], w2b[0:64, jc], TW[0:64, c0:c0 + T], True, True, r=[w2b, (TW, ti)], w=[bw])
        act(X("u1"), bw[:, 0:T], AF.Tanh, bias=hw0[:, j:j + 1], scale=0.5, r=[bw], w=[F_["u1"]])
        act(X("u1"), X("u1"), AF.Exp, bias=C1, scale=C1)
        ba = bank(); mm(ba[:, 0:T], w2b[64:128, jc], TW[64:128, c0:c0 + T], True, True, r=[w2b, (TW, ti)], w=[ba])
        sigm(X("aS"), ba[:, 0:T], r=[ba], w=[F_["aS"]], bias=ha0[:, j:j + 1])
        bg = bank(); mm(bg[:, 0:T], g2b[:, jc], SG1[:, c0:c0 + T], True, False, r=[g2b, (SG1, ti)], w=[bg])
        mm(bg[:, 0:T], g2c[:, jc], SG2[:, c0:c0 + T], False, True, r=[g2c, (SG2, ti)], w=[bg])
        cp(X("gS"), bg[:, 0:T], eng="act", r=[bg], w=[F_["gS"]])
        tt(Gb[:, 0:T], Gb[:, 0:T], X("gS"), ALU.mult, eng="pool", r=[Gb, F_["gS"]], w=[Gb])
        cumprod(Gm, F_["u1"], T, C, NCH, rstR)
        P.add("dve", lambda e: e.reciprocal(out=X("rG"), in_=Gm[:, 0:T]), r=[Gm], w=[F_["rG"]])
        cp(chunk3(X("Gmm"), C)[:, :, 1:C], chunk3(Gm[:, 0:T], C)[:, :, 0:C - 1], r=[Gm], w=[F_["Gmm"]])
        memset(chunk3(X("Gmm"), C)[:, :, 0:1], 1.0, eng="dve", w=[F_["Gmm"]])
        aff(X("kap"), X("xk"), col("kk", j), 0.0, r=[F_["xk"], pv], w=[F_["kap"]])
        tt(Bx("sqb"), X("kap"), X("kap"), ALU.mult, eng="pool")
        bn = bank(); mm(bn[:, 0:T], blkones, Bx("sqb"), True, True, w=[bn])
        ts(X("n1"), bn[:, 0:T], 1.0, 1e-24, ALU.mult, ALU.add, r=[bn], w=[F_["n1"]])
        rsqrt_dve(X("u2"), X("n1"), X("n2"), [F_["u2"], F_["n1"], F_["n2"]])
        tt(X("kap"), X("kap"), X("u2"), ALU.mult)
        aff(X("u2"), X("aS"), col("ka", j), omka[:, j:j + 1], r=[F_["aS"], pv, omka], w=[F_["u2"]])
        tt(X("kmod"), X("xk"), X("u2"), ALU.mult, eng="pool")
        tt(X("bS"), X("aS"), X("kap"), ALU.mult)
        stt(Bx("sqb"), X("xr"), col("rk", j), X("kmod"), ALU.mult, ALU.mult, r=[F_["xr"], F_["kmod"], pv], w=[B_["sqb"]])
        bn2 = bank(); mm(bn2[:, 0:T], blkones, Bx("sqb"), True, True, w=[bn2])
        tt(bon[:, 0:T], bn2[:, 0:T], X("xv"), ALU.mult, r=[bn2, F_["xv"]], w=[bon])
        tt(Bx("kpt"), X("kap"), X("Gmm"), ALU.mult)
        tt(rt[:, 0:T], X("xr"), Gm[:, 0:T], ALU.mult, eng="pool", r=[F_["xr"], Gm], w=[rt])
        cp(KR[:, 0:NCH, 0:C], chunk3(Bx("kpt"), C), eng="pool", r=[B_["kpt"]], w=[KR])
        cp(KR[:, 0:NCH, 64:64 + C], chunk3(rt[:, 0:T], C), eng="pool", r=[rt], w=[KR])
        tt(X("u2"), X("kmod"), X("rG"), ALU.mult); cp(Bx("ktl"), X("u2"), eng="act")
        tt(chunk3(Bx("khr"), C), chunk3(X("u2"), C), lastbc(Gm, T, C, NCH), ALU.mult, r=[F_["u2"], Gm], w=[B_["khr"]])
        tt(X("u2"), X("bS"), X("rG"), ALU.mult); cp(Bx("bt"), X("u2"), eng="act")
        tt(chunk3(Bx("bh"), C), chunk3(X("u2"), C), lastbc(Gm, T, C, NCH), ALU.mult, r=[F_["u2"], Gm], w=[B_["bh"]])
        cp(Bx("vTr"), X("xv"), eng="act")
        to_tm(I_["Vtr"], B_["vTr"], T, C, NCH); to_tm(I_["Ktm"], B_["kpt"], T, C, NCH)
        to_tm(I_["bhtm"], B_["bh"], T, C, NCH); to_tm(I_["khrtm"], B_["khr"], T, C, NCH)
        v4_ = lambda b: b[0:C, :].rearrange("p (c x t) -> p c x t", x=2, t=64)
        mk = lambda m_: m_.unsqueeze(1).broadcast_to([C, NCH, C])
        for h in range(2):
            hs_ = slice(64 * h, 64 * h + 64)
            dv = lambda t_: t_[0:C, 0:NCH, h * 64:h * 64 + C]
            b1 = bank()
            for c in range(NCH):
                mm(b1[0:C, c * 128:(c + 1) * 128], B_["bt"][hs_, c * C:(c + 1) * C], KR[hs_, c, :], True, True, r=[B_["bt"], KR], w=[b1])
            tt(dv(I_["Z"]), v4_(b1)[:, 0:NCH, 0, 0:C], mk(M_["nstrict"]), ALU.mult, r=[b1, M_["nstrict"]], w=[I_["Z"]])
            tt(dv(I_["PbT"]), v4_(b1)[:, 0:NCH, 1, 0:C], mk(M_["incl"]), ALU.mult, r=[b1, M_["incl"]], w=[I_["PbT"]])
            b2 = bank()
            for c in range(NCH):
                mm(b2[0:C, c * 128:(c + 1) * 128], B_["ktl"][hs_, c * C:(c + 1) * C], KR[hs_, c, :], True, True, r=[B_["ktl"], KR], w=[b2])
            tt(dv(I_["AkT"]), v4_(b2)[:, 0:NCH, 0, 0:C], mk(M_["strict"]), ALU.mult, r=[b2, M_["strict"]], w=[I_["AkT"]])
            tt(dv(I_["PkT"]), v4_(b2)[:, 0:NCH, 1, 0:C], mk(M_["incl"]), ALU.mult, r=[b2, M_["incl"]], w=[I_["PkT"]])
            b3 = bank()
            for c in range(NCH):
                mm(b3[0:C, c * 128:c * 128 + C], B_["kpt"][hs_, c * C:(c + 1) * C], B_["bt"][hs_, c * C:(c + 1) * C], True, True, r=[B_["kpt"], B_["bt"]], w=[b3])
            tt(dv(I_["Y"]), v4_(b3)[:, 0:NCH, 0, 0:C], mk(M_["nlstrict"]), ALU.mult, r=[b3, M_["nlstrict"]], w=[I_["Y"]])
        P.add("mark", ("F", gi))

    def rwkv_back(j, ti, c0, T, C, NCH, is_s, last, oT, gi):
        F_ = f32t; M_ = masks[C]
        I_ = IFs[gi % 2]
        L = 6 if C == 64 else 2
        X = lambda n: F_[n][:, 0:T]
        Gm = I_["Gm"]; bon = I_["bon"]; Gb = I_["Gb"]; rt = I_["rt"]
        if is_s:
            s0 = (c0 - SEQ) // 4
            memset(R0in, 0.0)
            dma("sp", R0in[0:64, :, 0:64], sr_in[s0:s0 + 4, 2 * j].rearrange("s i j -> i s j"), r=[sr_in], w=[R0in])
            dma("sp", R0in[64:128, :, 64:128], sr_in[s0:s0 + 4, 2 * j + 1].rearrange("s i j -> i s j"), r=[sr_in], w=[R0in])
            b_ = bank()
            for q in range(4):
                tr(b_[:, q * 128:(q + 1) * 128], R0in[:, q, :], identF, w=[b_])
            cp(R0blk[:, 0:4, :], b_.rearrange("p (q f) -> p q f", f=128), r=[b_], w=[R0blk])
            cp(R0bf, R0blk, eng="pool")
        P.add("need", ("F", gi))
        hv = lambda t_, c_lo, c_hi: t_[0:C, c_lo:c_hi, :].rearrange("p c (h t) -> p c h t", t=64)[:, :, :, 0:C]
        tt(hv(Abm["TTa"], 0, NCH), hv(I_["Z"], 0, NCH), identF[0:C, 0:C].unsqueeze(1).unsqueeze(1).broadcast_to([C, NCH, 2, C]), ALU.add,
           r=[I_["Z"], identF], w=[Abm["TTa"]])
        Yc, Zc, Yn, Zn, TTc, TTn = I_["Y"], I_["Z"], Abm["Y2"], Abm["Z2"], Abm["TTa"], Abm["TTb"]
        pr = [(c, h) for c in range(NCH) for h in range(2)]

        def batch_mm(dst, lhs, rhs, wid_l, wid_r, post):
            for p8 in range(0, len(pr), 8):
                b_ = bank(); grp = pr[p8:p8 + 8]
                for q, (c, h) in enumerate(grp):
                    mm(b_[0:C, q * 64:q * 64 + wid_r], lhs[0:C, c, h * 64:h * 64 + wid_l], rhs[0:C, c, h * 64:h * 64 + wid_r], True, True,
                       r=[lhs, rhs], w=[b_])
                cA = grp[0][0]; cB = grp[-1][0] + 1
                src = b_[0:C, 0:(cB - cA) * 128].rearrange("p (c h t) -> p c h t", h=2, t=64)[:, :, :, 0:wid_r]
                d = dst[0:C, cA:cB, :].rearrange("p c (h t) -> p c h t", t=64)[:, :, :, 0:wid_r]
                post(d, src, b_)

        cpy = lambda d, s_, b_: cp(d, s_, eng="act", r=[b_], w=[d])
        for n in range(1, L):
            batch_mm(Yn, Zc, Yc, C, C, cpy)
            if n < L - 1:
                batch_mm(Zn, Yc, Zc, C, C, cpy)
            for p8 in range(0, len(pr), 8):
                b_ = bank(); grp = pr[p8:p8 + 8]
                for q, (c, h) in enumerate(grp):
                    mm(b_[0:C, q * 64:q * 64 + C], Yn[0:C, c, h * 64:h * 64 + C], TTc[0:C, c, h * 64:h * 64 + C], True, True, r=[Yn, TTc], w=[b_])
                cA = grp[0][0]; cB = grp[-1][0] + 1
                src = b_[0:C, 0:(cB - cA) * 128].rearrange("p (c h t) -> p c h t", h=2, t=64)[:, :, :, 0:C]
                tt(hv(TTn, cA, cB), src, hv(TTc, cA, cB), ALU.add, r=[b_, TTc], w=[TTn])
            Yc, Yn = Yn, Yc; Zc, Zn = Zn, Zc; TTc, TTn = TTn, TTc
        TTf = TTc
        batch_mm(tm["W1"], I_["AkT"], I_["Vtr"], C, 64, cpy)
        batch_mm(tm["Ktp"], TTf, I_["Ktm"], C, 64, cpy)
        neg = lambda d, s_, b_: ts(d, s_, -1.0, None, ALU.mult, r=[b_], w=[d])
        batch_mm(tm["Un"], TTf, tm["W1"], C, 64, neg)
        v3 = lambda b: b.rearrange("p (c f) -> p c f", f=128)
        bM = bank()
        for c in range(NCH):
            mm(bM[:, c * 128:(c + 1) * 128], tm["Ktp"][0:C, c, :], I_["bhtm"][0:C, c, :], True, True, r=[tm["Ktp"], I_["bhtm"]], w=[bM])
        tt(Mblk[:, 0:NCH, :], v3(bM)[:, 0:NCH, :], blkneg.unsqueeze(1).broadcast_to([128, NCH, 128]), ALU.mult, r=[bM, blkneg], w=[Mblk])
        bR = bank()
        for c in range(NCH):
            mm(bR[:, c * 128:(c + 1) * 128], tm["Ktp"][0:C, c, :], I_["PbT"][0:C, c, :], True, True, r=[tm["Ktp"], I_["PbT"]], w=[bR])
        for h in range(2):
            hs_ = slice(64 * h, 64 * h + 64)
            tt(chunk3(RpT[hs_, 0:T], C), chunk3(rt[hs_, 0:T], C), v3(bR)[hs_, 0:NCH, h * 64:h * 64 + C], ALU.subtract, r=[rt, bR], w=[RpT])
        bN = bank()
        for c in range(NCH):
            mm(bN[:, c * 128:(c + 1) * 128], I_["khrtm"][0:C, c, :], I_["Vtr"][0:C, c, :], True, False, r=[I_["khrtm"], I_["Vtr"]], w=[bN])
            mm(bN[:, c * 128:(c + 1) * 128], I_["bhtm"][0:C, c, :], tm["Un"][0:C, c, :], False, True, r=[I_["bhtm"], tm["Un"]], w=[bN])
        tt(Ncomb[:, 0:NCH, :], v3(bN)[:, 0:NCH, :], blkpos.unsqueeze(1).broadcast_to([128, NCH, 128]), ALU.mult, r=[bN, blkpos], w=[Ncomb])
        bo = banks[5]
        for c in range(NCH):
            csl = slice(c * C, (c + 1) * C)
            sbf = R0bf[:, c, :] if is_s else Srbf
            sin = R0blk[:, c, :] if is_s else Sr
            sout = Routs[:, c, :] if is_s else Sr
            kS_ = [R0bf] if is_s else [Srbf]
            oo = bo[0:C, c * 128:(c + 1) * 128]
            mm(oo, RpT[:, csl], sbf, True, False, r=[RpT] + kS_, w=[bo])
            for h in range(2):
                hb = slice(h * 64, (h + 1) * 64)
                mm(oo[:, hb], I_["PkT"][0:C, c, h * 64:h * 64 + C], I_["Vtr"][0:C, c, hb], False, False, r=[I_["PkT"], I_["Vtr"]], w=[bo])
                mm(oo[:, hb], I_["PbT"][0:C, c, h * 64:h * 64 + C], tm["Un"][0:C, c, hb], False, h == 1, r=[I_["PbT"], tm["Un"]], w=[bo])
            b3 = bank()
            mm(b3[:, 0:128], Mblk[:, c, :], sbf, True, True, r=[Mblk] + kS_, w=[b3])
            stt(Srt, sin, Gm[:, (c + 1) * C - 1:(c + 1) * C], Ncomb[:, c, :], ALU.mult, ALU.add,
                r=[R0blk if is_s else Sr, Gm, Ncomb], w=[Srt])
            if not is_s:
                tt(Srbf, Srt, b3[:, 0:128], ALU.add, r=[Srt, b3], w=[Srbf])
            tt(sout, Srt, b3[:, 0:128], ALU.add, r=[Srt, b3], w=[Routs if is_s else Sr])
        cp(Osb2[0:C, 0:NCH, :], bo.rearrange("p (c f) -> p c f", f=128)[0:C, 0:NCH, :], eng="act", r=[bo], w=[Osb2])
        if is_s:
            b_ = bank()
            for q in range(4):
                tr(b_[:, q * 128:(q + 1) * 128], Routs[:, q, :], identF, w=[b_])
            cp(R0in[:, 0:4, :], b_.rearrange("p (q f) -> p q f", f=128), r=[b_], w=[R0in])
            dma("sp", rs[s0:s0 + 4, 2 * j].rearrange("s i j -> i s j"), R0in[0:64, :, 0:64], r=[R0in], w=[rs])
            dma("sp", rs[s0:s0 + 4, 2 * j + 1].rearrange("s i j -> i s j"), R0in[64:128, :, 64:128], r=[R0in], w=[rs])
        elif last:
            b_ = bank(); tr(b_[:, 0:128], Sr, identF, w=[b_]); cp(Srt, b_[:, 0:128])
            dma("sp", rp[2 * j], Srt[0:64, 0:64], r=[Srt], w=[rp]); dma("sp", rp[2 * j + 1], Srt[64:128, 64:128], r=[Srt], w=[rp])
        G2 = NCH * 2
        og = Osb2[0:C, 0:NCH, :].rearrange("p c (h i) -> p (c h) i", i=64)
        oq = Osq2[0:C, 0:NCH, :].rearrange("p c (h i) -> p (c h) i", i=64)
        P.add("dve", lambda e: e.reduce_sum(out=st3[0:C, 0:G2], in_=og, axis=AX.X), r=[Osb2], w=[st3])
        ts(st3[0:C, 0:G2], st3[0:C, 0:G2], 1.0 / 64, None, ALU.mult)
        tt(og, og, st3[0:C, 0:G2].unsqueeze(2).broadcast_to([C, G2, 64]), ALU.subtract, r=[Osb2, st3], w=[Osb2])
        tt(oq, og, og, ALU.mult, eng="pool", r=[Osb2], w=[Osq2])
        P.add("dve", lambda e: e.reduce_sum(out=st4[0:C, 0:G2], in_=oq, axis=AX.X), r=[Osq2], w=[st4])
        rsqrt_(st4[0:C, 0:G2], st4[0:C, 0:G2], 1.0 / 64, LNX_EPS, stC[0:C, 0:G2], stD[0:C, 0:G2])
        tt(tm["onr"][0:C, 0:NCH, :].rearrange("p c (h i) -> p (c h) i", i=64), og, st4[0:C, 0:G2].unsqueeze(2).broadcast_to([C, G2, 64]), ALU.mult,
           r=[Osb2, st4], w=[tm["onr"]])
        b_ = bank(); bb = b_.bitcast(BF16)
        for c in range(NCH):
            tr(bb[:, c * C:(c + 1) * C], tm["onr"][0:C, c, :], identB[0:C, 0:C], r=[tm["onr"], identB], w=[b_])
        ts(X("e1"), bb[:, 0:T], col("lw", j), col("lbb", j), ALU.mult, ALU.add, r=[b_, pv], w=[F_["e1"]])
        tt(X("e1"), X("e1"), bon[:, 0:T], ALU.add, eng="pool", r=[F_["e1"], bon], w=[F_["e1"]])
        tt(X("e1"), X("e1"), Gb[:, 0:T], ALU.mult, eng="pool", r=[F_["e1"], Gb], w=[F_["e1"]])
        P.add("need", ("oTw", gi))
        tt(mT[:, j, c0:c0 + T], X("e1"), oT[:, 0:T], ALU.add, r=[F_["e1"], oT], w=[(mT, ti)])
        P.add("mark", ("oTr", gi))
        P.add("mark", ("B", gi))

    NBLK = 8 if debug is None else debug.get('nblk', 8)

    def hgrn_tile(j, ti, c0, T, C, NCH, is_s, last, wb, oT, gi):
        M_ = masks[C]; F_ = f32t; B_ = bft
        if is_s:
            s0 = (c0 - SEQ) // 4
            dma("sp", S0all, sh_in[s0:s0 + 4, j].rearrange("s k v -> k s v"), r=[sh_in], w=[S0all])
            cp(S0bf, S0all, eng="pool")
        bf_ = proj(wb, 128, 256, ti, c0, T)
        act(F_["th"][:, 0:T], bf_[:, 0:T], AF.Tanh, scale=0.5, r=[bf_], w=[F_["th"]])
        aff(F_["fS"][:, 0:T], F_["th"][:, 0:T], hs1[:, j:j + 1], hs2[:, j:j + 1], r=[F_["th"], hs1, hs2], w=[F_["fS"]])
        aff(F_["kS"][:, 0:T], F_["th"][:, 0:T], hs1n[:, j:j + 1], hs2c[:, j:j + 1], r=[F_["th"], hs1n, hs2c], w=[F_["kS"]])
        cumprod(F_["Pc"], F_["fS"], T, C, NCH, rstH)
        P.add("dve", lambda e, T=T: e.reciprocal(out=F_["rP"][:, 0:T], in_=F_["Pc"][:, 0:T]), r=[F_["Pc"]], w=[F_["rP"]])
        bq = proj(wb, 0, 128, ti, c0, T)
        tt(B_["qt"][:, 0:T], bq[:, 0:T], F_["Pc"][:, 0:T], ALU.mult, r=[bq, F_["Pc"]], w=[B_["qt"]])
        tt(F_["t1"][:, 0:T], F_["kS"][:, 0:T], F_["rP"][:, 0:T], ALU.mult, eng="pool", r=[F_["kS"], F_["rP"]], w=[F_["t1"]])
        cp(B_["kt"][:, 0:T], F_["t1"][:, 0:T], eng="act", r=[F_["t1"]], w=[B_["kt"]])
        tt(chunk3(B_["kh"][:, 0:T], C), chunk3(F_["t1"][:, 0:T], C), lastbc(F_["Pc"], T, C, NCH), ALU.mult, r=[F_["t1"], F_["Pc"]], w=[B_["kh"]])
        bi = proj(wb, 256, 384, ti, c0, T)
        cp(B_["vT"][:, 0:T], bi[:, 0:T], eng="act", r=[bi], w=[B_["vT"]])
        bog = proj(wb, 384, 512, ti, c0, T)
        sigm(F_["t2"][:, 0:T], bog[:, 0:T], r=[bog], w=[F_["t2"]])
        tt(F_["t2"][:, 0:T], bog[:, 0:T], F_["t2"][:, 0:T], ALU.mult, r=[bog, F_["t2"]], w=[F_["t2"]])
        bga = proj(wb, 512, 640, ti, c0, T)
        sigm(F_["Ga"][:, 0:T], bga[:, 0:T], r=[bga], w=[F_["Ga"]])
        tt(F_["Ga"][:, 0:T], F_["Ga"][:, 0:T], F_["t2"][:, 0:T], ALU.mult, eng="pool", r=[F_["Ga"], F_["t2"]], w=[F_["Ga"]])
        to_tm(tm["Vtm"], B_["vT"], T, C, NCH); to_tm(tm["khtm"], B_["kh"], T, C, NCH)
        b_ = bank()
        for c in range(NCH):
            mm(b_[0:C, c * C:(c + 1) * C], B_["kt"][:, c * C:(c + 1) * C], B_["qt"][:, c * C:(c + 1) * C], True, True, w=[b_])
        tt(attnT[0:C, 0:NCH, 0:C], b_[0:C, 0:NCH * C].rearrange("p (c t) -> p c t", t=C),
           M_["incl"].unsqueeze(1).broadcast_to([C, NCH, C]), ALU.mult, r=[b_, M_["incl"]], w=[attnT])
        bo = banks[0]
        for c in range(NCH):
            csl = slice(c * C, (c + 1) * C)
            sbf = S0bf[:, c, :] if is_s else Shbf
            sin = S0all[:, c, :] if is_s else Sh
            sout = Souts[:, c, :] if is_s else Sh
            oo = bo[0:C, c * 128:(c + 1) * 128]
            mm(oo, attnT[0:C, c, 0:C], tm["Vtm"][0:C, c, :], True, False, r=[attnT, tm["Vtm"]], w=[bo])
            mm(oo, B_["qt"][:, csl], sbf, False, True, r=[B_["qt"], S0bf if is_s else Shbf], w=[bo])
            b2 = bank()
            mm(b2[:, 0:128], tm["khtm"][0:C, c, :], tm["Vtm"][0:C, c, :], True, True, r=[tm["khtm"], tm["Vtm"]], w=[b2])
            if not is_s:
                stt(Shbf, sin, F_["Pc"][:, (c + 1) * C - 1:(c + 1) * C], b2[:, 0:128], ALU.mult, ALU.add,
                    r=[Sh, F_["Pc"], b2], w=[Shbf])
            stt(sout, sin, F_["Pc"][:, (c + 1) * C - 1:(c + 1) * C], b2[:, 0:128], ALU.mult, ALU.add,
                r=[S0all if is_s else Sh, F_["Pc"], b2], w=[Souts if is_s else Sh])
        cp(Osb[0:C, 0:NCH, :], bo.rearrange("p (c f) -> p c f", f=128)[0:C], eng="act", r=[bo], w=[Osb])
        if is_s:
            dma("sp", hs[s0:s0 + 4, j].rearrange("s k v -> k s v"), Souts, r=[Souts], w=[hs])
        elif last:
            dma("sp", hp[j], Sh, r=[Sh], w=[hp])
        tt(Osq[0:C, 0:NCH, :], Osb[0:C, 0:NCH, :], Osb[0:C, 0:NCH, :], ALU.mult, eng="pool", r=[Osb], w=[Osq])
        P.add("dve", lambda e, C=C, NCH=NCH: e.reduce_sum(out=st_[0:C, 0:NCH], in_=Osq[0:C, 0:NCH, :], axis=AX.X), r=[Osq], w=[st_])
        rsqrt_(st_[0:C, 0:NCH], st_[0:C, 0:NCH], 1.0 / 128, RMS_EPS, stA[0:C, 0:NCH], stB[0:C, 0:NCH])
        tt(Osq[0:C, 0:NCH, :], Osb[0:C, 0:NCH, :], st_[0:C, 0:NCH].unsqueeze(2).broadcast_to([C, NCH, 128]), ALU.mult, r=[Osb, st_], w=[Osq])
        tt(tm["onb"][0:C, 0:NCH, :], Osq[0:C, 0:NCH, :], nwbc[0:C, :].unsqueeze(1).broadcast_to([C, NCH, 128]), ALU.mult, r=[Osq, nwbc], w=[tm["onb"]])
        b_ = bank(); bb = b_.bitcast(BF16)
        for c in range(NCH):
            tr(bb[:, c * C:(c + 1) * C], tm["onb"][0:C, c, :], identB[0:C, 0:C], r=[tm["onb"], identB], w=[b_])
        if gi >= 2:
            P.add("need", ("oTr", gi - 2))
        tt(oT[:, 0:T], bb[:, 0:T], F_["Ga"][:, 0:T], ALU.mult, r=[b_, F_["Ga"]], w=[oT])
        P.add("mark", ("oTw", gi))

    def stream(kind):
        gi = 0
        for j in range(NBLK):
            seg = lambda col0: w_in[:, col0 + j * 128:col0 + (j + 1) * 128].rearrange("(k p) c -> p k c", p=128)
            if kind == "H":
                for si, col0 in enumerate((0, 1024, 2048, 3072, 7456)):
                    dma("pool", wbH[:, :, si * 128:(si + 1) * 128], seg(col0), r=[w_in], w=[wbH])
                memset(Sh, 0.0, eng="dve"); memset(Shbf, 0.0, eng="dve")
            elif kind == "RF":
                for si, col0 in enumerate((4096, 5120, 6144, 8480)):
                    dma("pool", wbR[:, :, si * 128:(si + 1) * 128], seg(col0), r=[w_in], w=[wbR])
                memset(prevs[:, 3:6], 0.0, eng="dve", w=[prevs])
            else:
                memset(Sr, 0.0, eng="dve"); memset(Srbf, 0.0, eng="dve")
            for ti, (c0, T, C, NCH, is_s) in enumerate(tiles):
                if debug is not None and 'tsel' in debug and ti not in debug['tsel']:
                    continue
                last = (c0 + T == SEQ)
                if kind == "H":
                    hgrn_tile(j, ti, c0, T, C, NCH, is_s, last, wbH, oTs[gi % 2], gi)
                elif kind == "RF":
                    rwkv_front(j, ti, c0, T, C, NCH, is_s, last, wbR, gi)
                else:
                    rwkv_back(j, ti, c0, T, C, NCH, is_s, last, oTs[gi % 2], gi)
                gi += 1

    opsH = []; opsF = []; opsB = []
    P.sink = opsH; cur[0] = "H"; stream("H")
    P.sink = opsF; cur[0] = "RF"; stream("RF")
    P.sink = opsB; cur[0] = "RB"; stream("RB")
    P.sink = None; cur[0] = None
    _n0 = len(P.ops)
    if debug is not None:
        P.lat_same = debug.get('lat_same', P.lat_same); P.lat_cross = debug.get('lat_cross', P.lat_cross)
    P.merge([opsH, opsF, opsB])
    if debug is not None and debug.get("xdeps"):
        from collections import Counter
        cx = Counter()
        for i in range(_n0, len(P.ops)):
            o = P.ops[i]
            for d in o["deps"]:
                od = P.ops[d]
                if d >= _n0 and od["stream"] != o["stream"]:
                    shared = (set(o["rk"]) | set(o["wk"])) & (set(od["rk"]) | set(od["wk"]))
                    cx[(o["stream"], o["eng"], od["eng"], str(sorted(map(str, shared)))[:80])] += 1
        for k, v in cx.most_common(30):
            print("XDEP", v, k)
    dump("mT", mT.rearrange("p a b -> p (a b)"), [128, 8 * NT])
    for p4 in range(0, 27, 4):
        b_ = bank(); n4 = min(4, 27 - p4); wdt = min(n4 * 128, RC - p4 * 128)
        for q in range(n4):
            tr(b_[0:17, q * 128:(q + 1) * 128], lastc[:, p4 + q, :], identF, r=[lastc, identF], w=[b_])
        cp(srow[:, 0:n4 * 128], b_[0:17, 0:n4 * 128], r=[b_], w=[srow])
        dma("sp", shp[:, p4 * 128:p4 * 128 + wdt], srow[0:1, 0:wdt], r=[srow], w=[shp])
        dma("sp", shs[:, p4 * 128:p4 * 128 + wdt], srow[1:17, 0:wdt], r=[srow], w=[shs])
    if debug is not None and debug.get("_stop") == 2:
        n = P.emit()
        return nc, dbg, n
    print('SBUF phase2 total', SBTOT[0], 'S2', S2.tot)
    S2.close()
    P.barrier(dummy)
    S1 = Scope()
    xT = S1.sb("xT", [128, 8, NT]); h2T = S1.sb("h2T", [128, 8, NT], BF16)
    g2s = S1.sb("g2s", [128, 8, 64]); sh2s = S1.sb("sh2s", [128, 8, 64]); gt1s = S1.sb("gt1s", [128, 8, 64]); gt2s = S1.sb("gt2s", [128, 8, 64])
    mk_mods(gt1s, 2); mk_mods(sh2s, 3); mk_mods(g2s, 4, "n2"); mk_mods(gt2s, 5)
    nbuf = dict(sq=S1.sb("sq", [128, 8, 512], BF16), rstd=S1.sb("rstd", [128, 512]), tmp=S1.sb("ntmp", [128, 8, 512]),
                xtm=[S1.sb("xtm", [128, D]) for _ in range(2)], i=0)
    nbuf["r1"] = nbuf["tmp"][:, 1, :]; nbuf["r2"] = nbuf["tmp"][:, 0, :]
    wo = S1.sb("wo", [128, 8, 128], BF16); wus = [S1.sb("wu", [128, 8, 256], BF16) for _ in range(2)]; wds = [S1.sb("wd", [128, 2, D], BF16)] * 2
    aTs = [S1.sb("aT", [128, 2, 512], BF16) for _ in range(2)]; rTs = [S1.sb("rT", [128, 1, 512], BF16)] * 2; t64 = S1.sb("t64", [128, 64])
    rtmp = [nbuf["rstd"], nbuf["tmp"][:, 0, :]]; rti = [0]
    tiles34 = [(i * 512, 512, False) for i in range(4)] + [(SEQ, 64, True)]
    for ti, (c0, T, is_s) in enumerate(tiles34):
        load_xT(xT[:, :, c0:c0 + T], c0, T, ti, fkeys=True)

    def resid_add(f, c0, T, is_s, ps, gtp, gts, split=False):
        kx = [(xT, f)]
        if not is_s:
            if split:
                tq = rtmp[rti[0] % len(rtmp)]; rti[0] += 1
                aff(tq[:, 0:T], ps[:, 0:T], gtp[:, f:f + 1], 0.0, r=[ps, gtp], w=[tq])
                tt(xT[:, f, c0:c0 + T], xT[:, f, c0:c0 + T], tq[:, 0:T], ALU.add, eng="pool", r=kx + [tq], w=kx)
            else:
                stt(xT[:, f, c0:c0 + T], ps[:, 0:T], gtp[:, f:f + 1], xT[:, f, c0:c0 + T], ALU.mult, ALU.add, r=[ps, gtp] + kx, w=kx)
        else:
            tt(t64[:, 0:T], ps[:, 0:T], gts[:, f, :], ALU.mult, r=[ps, gts], w=[t64])
            tt(xT[:, f, c0:c0 + T], xT[:, f, c0:c0 + T], t64[:, 0:T], ALU.add, r=kx + [t64], w=kx)

    for f in range(8):
        dma("pool", wo, w_out[:, f * 128:(f + 1) * 128].rearrange("(k p) c -> p k c", p=128), r=[w_out], w=[wo])
        for ti, (c0, T, is_s) in enumerate(tiles34):
            b_ = bank()
            for k in range(8):
                mm(b_[:, 0:T], wo[:, k, :], mT[:, k, c0:c0 + T], k == 0, k == 7, r=[wo, mT], w=[b_])
            resid_add(f, c0, T, is_s, b_, gt1p, gt1s)
    for ti, (c0, T, is_s) in enumerate(tiles34):
        norm_to(h2T[:, :, c0:c0 + T], xT[:, :, c0:c0 + T], T, g2p, sh2p, g2s, sh2s, is_s, 0, fkeys=True)
    steps = [(g, ti) for g in range(16) for ti in range(len(tiles34))]
    upb = [0]; dnb = [0]

    def load_wu(g):
        wu_ = wus[g % 2]
        dma("pool", wu_, w_up[:, g * 256:(g + 1) * 256].rearrange("(k p) c -> p k c", p=128), r=[w_up], w=[wu_])

    def load_wd(g):
        wd_ = wds[0]
        dma("pool", wd_, w_down[g * 256:(g + 1) * 256, :].rearrange("(q p) c -> p q c", p=128), r=[w_down], w=[wd_])

    def up(k):
        g, ti = steps[k]; c0, T, is_s = tiles34[ti]
        if ti == 0 and g + 1 < 16:
            load_wu(g + 1)
        wu_ = wus[g % 2]; aT_ = aTs[k % 2]; rT_ = rTs[k % 2]
        for q in range(2):
            b_ = banks[upb[0] % 4]; upb[0] += 1
            for kk_ in range(8):
                mm(b_[:, 0:T], wu_[:, kk_, q * 128:(q + 1) * 128], h2T[:, kk_, c0:c0 + T], kk_ == 0, kk_ == 7, r=[wu_, (h2T, 0)], w=[b_])
            act(rT_[:, 0, 0:T], b_[:, 0:T], AF.Relu, r=[b_], w=[rT_])
            tt(aT_[:, q, 0:T], rT_[:, 0, 0:T], rT_[:, 0, 0:T], ALU.mult, eng=("pool" if q == 1 else "dve"), r=[rT_], w=[aT_])

    def down(k):
        g, ti = steps[k]; c0, T, is_s = tiles34[ti]
        wd_ = wds[g % 2]; aT_ = aTs[k % 2]
        for f in range(8):
            b_ = banks[4 + dnb[0] % 4]; dnb[0] += 1
            for q in range(2):
                mm(b_[:, 0:T], wd_[:, q, f * 128:(f + 1) * 128], aT_[:, q, 0:T], q == 0, q == 1, r=[wd_, aT_], w=[b_])
            resid_add(f, c0, T, is_s, b_, gt2p, gt2s, split=(f >= 6))

    load_wu(0); load_wd(0)
    up(0)
    for k in range(len(steps)):
        if k + 1 < len(steps):
            up(k + 1)
        down(k)
        g_, ti_ = steps[k]
        if ti_ == len(tiles34) - 1 and g_ + 1 < 16:
            load_wd(g_ + 1)
    fnp = pv[:, rows["fn"]:rows["fn"] + 8]
    for ti, (c0, T, is_s) in enumerate(tiles34):
        sq = nbuf["sq"]; rstd = nbuf["rstd"]; tmp = nbuf["tmp"]
        xv_ = xT[:, :, c0:c0 + T]
        xk8 = [(xT, f_) for f_ in range(8)]
        tt(sq[:, :, 0:T], xv_, xv_, ALU.mult, r=xk8, w=[sq])
        b_ = bank()
        for fc in range(8):
            mm(b_[:, 0:T], onesB, sq[:, fc, 0:T], fc == 0, fc == 7, w=[b_])
        ts(rstd[:, 0:T], b_[:, 0:T], 1.0 / D, RMS_EPS, ALU.mult, ALU.add, r=[b_], w=[rstd])
        r1 = nbuf["r1"]; r2 = nbuf["r2"]
        cp(r1[:, 0:T], rstd[:, 0:T], eng="act", r=[rstd], w=[r1])
        rsqrt_dve(rstd[:, 0:T], r1[:, 0:T], r2[:, 0:T], [rstd, r1, r2])
        tt(tmp[:, :, 0:T], xv_, rstd[:, 0:T].unsqueeze(1).broadcast_to([128, 8, T]), ALU.mult, r=xk8 + [rstd], w=[tmp])
        tt(tmp[:, :, 0:T], tmp[:, :, 0:T], fnp.unsqueeze(2).broadcast_to([128, 8, T]), ALU.mult, r=[tmp, pv], w=[tmp])
        for sub in range(0, T, 128):
            n = min(128, T - sub)
            ytm = nbuf["xtm"][nbuf["i"] % 2]; nbuf["i"] += 1
            for half in range(2):
                b_ = bank()
                for f4 in range(4):
                    fc = half * 4 + f4
                    tr(b_[0:n, f4 * 128:(f4 + 1) * 128], tmp[:, fc, sub:sub + n], identF, r=[tmp, identF], w=[b_])
                cp(ytm[0:n, half * 512:(half + 1) * 512], b_[0:n, :], eng="act", r=[b_], w=[ytm])
            dst = yp[c0 + sub:c0 + sub + n, :] if not is_s else ys[sub:sub + n, :]
            dma("sp", dst, ytm[0:n, :], r=[ytm], w=[yp if not is_s else ys])
    n = P.emit()
    print('SBUF bytes/partition: G', G.tot, 'max', SBMAX[0])
    S1.close(); G.close()
    return nc, dbg, n


_CACHE = {}


def _core_inputs(inp, c):
    f = lambda a: np.ascontiguousarray(a, dtype=np.float32)
    m = {"xp": inp["x_prompt"][c], "xs": inp["x_sample"][16 * c:16 * c + 16].reshape(64, D),
         "sh": inp["state_hgrn"][0, 16 * c:16 * c + 16], "sr": inp["state_rwkv"][0, 16 * c:16 * c + 16],
         "ss": inp["state_shift"][0, 16 * c:16 * c + 16], "cp": inp["c_prompt"][c:c + 1], "cs": inp["c_sample"][16 * c:16 * c + 16]}
    for k in ("norm1_w", "norm2_w", "rwkv_w0", "rwkv_a0", "rwkv_k_k", "rwkv_k_a", "rwkv_lnx_w", "rwkv_lnx_b", "rwkv_r_k", "final_norm_w"):
        m[k] = inp[k].reshape(8, 128)
    m["ada_w"] = inp["ada_w"][0]; m["ada_b"] = inp["ada_b"].reshape(48, 128); m["w_in"] = inp["w_in"][0]
    m["lb_logits"] = inp["lb_logits"].reshape(16, 128); m["hgrn_norm_w"] = inp["hgrn_norm_w"].reshape(1, 128)
    m["rwkv_mu"] = inp["rwkv_mu"].reshape(1, RC); m["rwkv_w2"] = inp["rwkv_w2"][0]; m["rwkv_a2"] = inp["rwkv_a2"][0]
    m["rwkv_g2"] = inp["rwkv_g2"][0]; m["w_out"] = inp["w_out"][0]; m["w_up"] = inp["w_up"][0]; m["w_down"] = inp["w_down"][0]
    return {k: f(v) for k, v in m.items()}


def kernel(**inputs):
    inp = {k: np.asarray(v) for k, v in inputs.items()}
    if "nc" not in _CACHE:
        _CACHE["nc"] = build()[0]
    nc = _CACHE["nc"]
    in_maps = [_core_inputs(inp, c) for c in range(8)]
    res = run_bass_kernel_spmd(nc, in_maps, core_ids=list(range(8)))
    R = res.results
    g = lambda n: [np.asarray(R[c][n], dtype=np.float32) for c in range(8)]
    y_prompt = np.stack(g("yp"), 0)
    y_sample = np.concatenate([a.reshape(16, 4, D) for a in g("ys")], 0)
    hgrn_p = np.stack(g("hp"), 0)[None]
    rwkv_p = np.stack(g("rp"), 0)[None]
    shift_p = np.concatenate(g("shp"), 0)[None]
    hgrn_s = np.concatenate(g("hs"), 0)[None]
    rwkv_s = np.concatenate(g("rs"), 0)[None]
    shift_s = np.concatenate(g("shs"), 0)[None]
    return (y_prompt, y_sample, hgrn_p, rwkv_p, shift_p, hgrn_s, rwkv_s, shift_s)
```

```python
import contextlib
import numpy as np
import concourse.bass as bass
import concourse.mybir as mybir
from concourse.bass_utils import run_bass_kernel_spmd

F32 = mybir.dt.float32
BF16 = mybir.dt.bfloat16
AF = mybir.ActivationFunctionType
ALU = mybir.AluOpType
AX = mybir.AxisListType

D = 1024
SEQ = 2048
NSEQ_S = 16
TS = 4
NT = SEQ + NSEQ_S * TS
INC = 9504
RC = 3360
NDMASEM = 12
SAME_ENG_RELAX = ()
RMS_EPS = 1e-6
LNX_EPS = 64e-5
C1 = -0.5 * float(np.exp(-0.5))


class Prog:
    def __init__(self, nc):
        self.nc = nc
        self.ops = []
        self.last_w = {}
        self.readers = {}
        self.bar = False
        self.sink = None
        self.lat_same = 0.2
        self.lat_cross = 0.3

    def peek(self, r, w):
        deps = set()
        rk = [self.key(x) for x in r]; wk = [self.key(x) for x in w]
        if self.bar:
            rk.append("__bar")
        for k in rk:
            if k in self.last_w:
                deps.add(self.last_w[k])
        for k in wk:
            if k in self.last_w:
                deps.add(self.last_w[k])
            deps |= self.readers.get(k, set())
        return deps

    @staticmethod
    def dur(eng, dma, n):
        if dma:
            return 0.1
        if eng == "pe":
            return 0.06 + n / 1200.0
        if eng == "dve":
            return 0.10 + n / 960.0
        if eng == "act":
            return 0.25 + n / 1200.0
        if eng == "pool":
            return 0.30 + n / 400.0
        return 0.1

    def merge(self, lists, lead=0):
        L = list(lists); K = len(L)
        done = set(); pos = [0] * K
        end = {}; free = {}
        tot = [max(1, len(x)) for x in L]

        def est_start(x):
            eng, fn, r, w, dma, n = x
            deps = self.peek(r, w)
            t = free.get(eng, 0.0)
            for d in deps:
                if d in end:
                    lat = self.lat_same if self.ops[d]["eng"] == eng else self.lat_cross
                    if self.ops[d]["eng"] == "pe" and eng == "pe":
                        lat = 0.0
                    t = max(t, end[d] + lat)
            return t

        while any(pos[i] < len(L[i]) for i in range(K)):
            progressed = False
            for si in range(K):
                while pos[si] < len(L[si]) and (L[si][pos[si]][0] == "mark" or (L[si][pos[si]][0] == "need" and L[si][pos[si]][1] in done)):
                    if L[si][pos[si]][0] == "mark":
                        done.add(L[si][pos[si]][1])
                    pos[si] += 1; progressed = True
            cand = []
            for si in range(K):
                if pos[si] >= len(L[si]):
                    continue
                x = L[si][pos[si]]
                if x[0] == "need":
                    continue
                cand.append((est_start(x), pos[si] / tot[si], si))
            if not cand:
                if progressed:
                    continue
                if all(pos[i] >= len(L[i]) for i in range(K)):
                    break
                raise RuntimeError("merge deadlock")
            t, _, si = min(cand)
            x = L[si][pos[si]]; pos[si] += 1
            eng, fn, r, w, dma, n = x
            idx = self.add(eng, fn, r, w, dma, n)
            d_ = self.dur(eng, dma, n)
            free[eng] = t + d_
            end[idx] = t + d_ + (2.5 if dma else 0.0)

    @staticmethod
    def key(x):
        if isinstance(x, tuple):
            if isinstance(x[0], str):
                return x
            return (x[0].tensor.name, x[1])
        if isinstance(x, str):
            return x
        return x.tensor.name

    def add(self, eng, fn, r=(), w=(), dma=False, n=256):
        if self.sink is not None:
            if eng in ("mark", "need"):
                self.sink.append((eng, fn))
            else:
                self.sink.append((eng, fn, list(r), list(w), dma, n))
            return None
        deps = set()
        rk = [self.key(x) for x in r]
        wk = [self.key(x) for x in w]
        if self.bar:
            rk.append("__bar")
        for k in rk:
            if k in self.last_w:
                deps.add(self.last_w[k])
        for k in wk:
            if k in self.last_w:
                deps.add(self.last_w[k])
            deps |= self.readers.get(k, set())
        idx = len(self.ops)
        self.ops.append(dict(eng=eng, fn=fn, deps=deps, dma=dma, users=0, stream=getattr(self, "tag", None), rk=rk, wk=wk))
        for k in rk:
            self.readers.setdefault(k, set()).add(idx)
        for k in wk:
            self.last_w[k] = idx
            self.readers[k] = set()
        return idx

    def barrier(self, dummy):
        allk = set(self.last_w.keys()) | set(self.readers.keys())
        allk.discard("__bar")
        self.bar = False
        self.add("dve", lambda e: e.memset(dummy, 0.0), r=(), w=list(allk) + ["__bar"])
        self.last_w = {"__bar": self.last_w["__bar"]}
        self.readers = {}
        self.bar = True

    def emit(self):
        nc = self.nc
        engs = {"pe": nc.tensor, "act": nc.scalar, "dve": nc.vector, "pool": nc.gpsimd, "sp": nc.sync}
        ops = self.ops

        seqn = {}
        for i_, o_ in enumerate(ops):
            if not o_["dma"]:
                seqn[o_["eng"]] = seqn.get(o_["eng"], 0) + 1
                o_["seq"] = seqn[o_["eng"]]

        def skip(d, o):
            if ops[d]["dma"] or o["dma"]:
                return False
            if ops[d]["eng"] != o["eng"]:
                return False
            if o["eng"] == "pe":
                return True
            if o["eng"] in SAME_ENG_RELAX and o["seq"] - ops[d]["seq"] >= 2:
                return True
            return False

        for o in ops:
            for d in o["deps"]:
                if not skip(d, o):
                    ops[d]["users"] += 1
        with contextlib.ExitStack() as st:
            csem = {e: st.enter_context(nc.semaphore("c_" + e)) for e in engs}
            dsem = {e: [st.enter_context(nc.semaphore(f"d_{e}{i}")) for i in range(NDMASEM)]
                    for e in ("act", "pool", "sp")}
            ccnt = {e: 0 for e in engs}
            dcnt = {e: [0] * NDMASEM for e in dsem}
            drr = {e: 0 for e in dsem}
            waited = {e: {} for e in engs}
            sig = {}

            def wait(e, sem, val):
                k = id(sem)
                if val <= 0 or waited[e].get(k, 0) >= val:
                    return
                engs[e].wait_ge(sem, val)
                waited[e][k] = val

            for i, o in enumerate(ops):
                e = o["eng"]
                for d in sorted(o["deps"]):
                    if d not in sig or skip(d, o):
                        continue
                    s, v = sig[d]
                    wait(e, s, v)
                if o["dma"]:
                    j = drr[e]
                    drr[e] = (j + 1) % NDMASEM
                    s = dsem[e][j]
                    wait(e, s, dcnt[e][j])
                    ins = o["fn"](engs[e])
                    dcnt[e][j] += 16
                    ins.then_inc(s, 16)
                    sig[i] = (s, dcnt[e][j])
                else:
                    ins = o["fn"](engs[e])
                    if o["users"] > 0:
                        ccnt[e] += 1
                        ins.then_inc(csem[e], 1)
                        sig[i] = (csem[e], ccnt[e])
            for e in dsem:
                for j in range(NDMASEM):
                    if dcnt[e][j] > 0:
                        nc.sync.wait_ge(dsem[e][j], dcnt[e][j])
        return len(ops)


def build(debug=None):
    nc = bass.Bass("TRN2", target_bir_lowering=False)
    P = Prog(nc)
    din = lambda n, s: nc.dram_tensor(n, list(s), F32, kind="ExternalInput").ap()
    dout = lambda n, s: nc.dram_tensor(n, list(s), F32, kind="ExternalOutput").ap()
    xp = din("xp", [SEQ, D]); xs = din("xs", [64, D])
    sh_in = din("sh", [16, 8, 128, 128]); sr_in = din("sr", [16, 16, 64, 64]); ss_in = din("ss", [16, RC])
    cpr = din("cp", [1, D]); csm = din("cs", [16, D])
    norm1_w = din("norm1_w", [8, 128]); norm2_w = din("norm2_w", [8, 128])
    ada_w = din("ada_w", [D, 6 * D]); ada_b = din("ada_b", [48, 128])
    w_in = din("w_in", [D, INC]); lb_logits = din("lb_logits", [16, 128])
    hgrn_norm_w = din("hgrn_norm_w", [1, 128]); mu = din("rwkv_mu", [1, RC])
    w0 = din("rwkv_w0", [8, 128]); w2 = din("rwkv_w2", [64, D]); a0 = din("rwkv_a0", [8, 128])
    a2 = din("rwkv_a2", [64, D]); g2 = din("rwkv_g2", [160, D]); k_k = din("rwkv_k_k", [8, 128])
    k_a = din("rwkv_k_a", [8, 128]); r_k = din("rwkv_r_k", [8, 128]); lnx_w = din("rwkv_lnx_w", [8, 128])
    lnx_b = din("rwkv_lnx_b", [8, 128]); w_out = din("w_out", [D, D]); w_up = din("w_up", [D, 4 * D])
    w_down = din("w_down", [4 * D, D]); fnw = din("final_norm_w", [8, 128])
    yp = dout("yp", [SEQ, D]); ys = dout("ys", [64, D])
    hp = dout("hp", [8, 128, 128]); rp = dout("rp", [16, 64, 64]); shp = dout("shp", [1, RC])
    hs = dout("hs", [16, 8, 128, 128]); rs = dout("rs", [16, 16, 64, 64]); shs = dout("shs", [16, RC])
    dbg = {}

    cnt = [0]

    def uname(n):
        cnt[0] += 1
        return f"{n}_{cnt[0]}"

    SBTOT = [0]; SBMAX = [0]

    class Scope:
        def __init__(self):
            self.st = contextlib.ExitStack()

        def sb(self, name, shape, dt=F32):
            nb = int(np.prod(shape[1:])) * (4 if dt == F32 else 2)
            SBTOT[0] += nb; self.tot = getattr(self, "tot", 0) + nb
            SBMAX[0] = max(SBMAX[0], SBTOT[0])
            return self.st.enter_context(nc.sbuf_tensor(uname(name), list(shape), dt)).ap()

        def close(self):
            SBTOT[0] -= getattr(self, "tot", 0)
            self.st.close()

    G = Scope()
    banks = [nc.alloc_psum_tensor(f"bank{i}", [128, 512], F32).ap() for i in range(8)]
    bk = [0]

    pools = {None: list(range(8)), "H": [1, 2], "RF": [3, 4], "RB": [6, 7]}
    cur = [None]
    bkp = {"H": 0, "RF": 0, "RB": 0}

    def bank():
        if cur[0] is None:
            b = banks[bk[0] % 8]
            bk[0] += 1
            return b
        pl = pools[cur[0]]
        b = banks[pl[bkp[cur[0]] % len(pl)]]
        bkp[cur[0]] += 1
        return b

    def dump(name, ap, shape, keys=None):
        if debug is None or (name not in debug and "all" not in debug):
            return
        d = nc.dram_tensor("dbg_" + name, list(shape), F32, kind="ExternalOutput").ap()
        dbg[name] = d
        q = "pool" if ap.dtype != F32 else "sp"
        rk = [ap] + [(ap, t_) for t_ in range(16)] if keys is None else keys
        P.add(q, lambda e: e.dma_start(out=d, in_=ap), r=rk, w=[d], dma=True)

    def dma(q, out, in_, r=(), w=()):
        P.add(q, lambda e: e.dma_start(out=out, in_=in_), r=list(r) or [in_], w=list(w) or [out], dma=True)

    def act(out, in_, func, bias=0.0, scale=1.0, r=(), w=(), accum=None):
        rr = list(r) or [in_]
        if not isinstance(bias, float):
            rr.append(bias)
        if not isinstance(scale, float):
            rr.append(scale)
        ww = list(w) or [out]
        if accum is not None:
            ww.append(accum)
        if accum is not None:
            P.add("act", lambda e: e.activation(out=out, in_=in_, func=func, bias=bias, scale=scale, accum_out=accum), r=rr, w=ww, n=out.free_size())
        else:
            P.add("act", lambda e: e.activation(out=out, in_=in_, func=func, bias=bias, scale=scale), r=rr, w=ww, n=out.free_size())

    def tt(out, in0, in1, op, eng="dve", r=(), w=()):
        P.add(eng, lambda e: e.tensor_tensor(out=out, in0=in0, in1=in1, op=op), r=list(r) or [in0, in1], w=list(w) or [out], n=out.free_size())

    def ts(out, in0, s1, s2, op0, op1=None, eng="dve", r=(), w=()):
        rr = list(r) or [in0]
        for s in (s1, s2):
            if s is not None and not isinstance(s, float):
                rr.append(s)
        if op1 is None:
            P.add(eng, lambda e: e.tensor_scalar(out=out, in0=in0, scalar1=s1, scalar2=None, op0=op0), r=rr, w=list(w) or [out], n=out.free_size())
        else:
            P.add(eng, lambda e: e.tensor_scalar(out=out, in0=in0, scalar1=s1, scalar2=s2, op0=op0, op1=op1), r=rr, w=list(w) or [out], n=out.free_size())

    def aff(out, in_, scale, bias, r=(), w=()):
        act(out, in_, AF.Identity, bias=bias, scale=scale, r=r, w=w)

    def stt(out, in0, scalar, in1, op0, op1, r=(), w=()):
        rr = list(r) or [in0, in1]
        if not isinstance(scalar, float):
            rr.append(scalar)
        P.add("dve", lambda e: e.scalar_tensor_tensor(out=out, in0=in0, scalar=scalar, in1=in1, op0=op0, op1=op1), r=rr, w=list(w) or [out], n=out.free_size())

    def cp(out, in_, eng="dve", r=(), w=()):
        if eng == "act":
            act(out, in_, AF.Copy, r=r, w=w)
        else:
            P.add(eng, lambda e: e.tensor_copy(out=out, in_=in_), r=list(r) or [in_], w=list(w) or [out], n=out.free_size())

    def memset(ap, val, eng="pool", w=()):
        P.add(eng, lambda e: e.memset(ap, val), w=list(w) or [ap], n=ap.free_size())

    def mm(out, lhsT, rhs, start, stop, r=(), w=()):
        P.add("pe", lambda e: e.matmul(out, lhsT, rhs, start=start, stop=stop), r=list(r) or [lhsT, rhs], w=list(w) or [out], n=rhs.free_size() + 60)

    def tr(out, in_, ident, r=(), w=()):
        P.add("pe", lambda e: e.transpose(out=out, in_=in_, identity=ident), r=list(r) or [in_, ident], w=list(w) or [out], n=ident.free_size() + 60)

    I32 = mybir.dt.int32

    def rsqrt_dve(y, x, t1, keys):
        P.add("dve", lambda e: e.tensor_scalar(out=y.bitcast(I32), in0=x.bitcast(I32), scalar1=-0.5, scalar2=1597463007.0,
                                               op0=ALU.mult, op1=ALU.add), r=keys, w=keys)
        for _ in range(2):
            tt(t1, y, y, ALU.mult, r=keys, w=keys)
            stt(t1, t1, -0.5, x, ALU.mult, ALU.mult, r=keys, w=keys)
            stt(y, t1, 1.5, y, ALU.add, ALU.mult, r=keys, w=keys)

    def rsqrt_(out, in_, scale, eps, tA, tB):
        ts(tA, in_, scale, eps, ALU.mult, ALU.add)
        rsqrt_dve(out, tA, tB, [out, tA, tB])

    ones128 = G.sb("ones128", [128, 128]); identF = G.sb("identF", [128, 128]); identB = G.sb("identB", [128, 128], BF16)
    memset(ones128, 1.0)
    P.add("pool", lambda e: e.affine_select(out=identF, in_=ones128, pattern=[[1, 128]], compare_op=ALU.is_equal,
                                           fill=0.0, base=0, channel_multiplier=-1), r=[ones128], w=[identF])
    cp(identB, identF)
    onesB = G.sb("onesB", [128, 128], BF16)
    cp(onesB, ones128)
    blkones = G.sb("blkones", [128, 128], BF16)
    memset(blkones, 0.0)
    memset(blkones[0:64, 0:64], 1.0, w=[blkones]); memset(blkones[64:128, 64:128], 1.0, w=[blkones])
    blkneg = G.sb("blkneg", [128, 128])
    memset(blkneg, 0.0)
    memset(blkneg[0:64, 0:64], -1.0, w=[blkneg]); memset(blkneg[64:128, 64:128], -1.0, w=[blkneg])
    blkpos = G.sb("blkpos", [128, 128])
    memset(blkpos, 0.0)
    memset(blkpos[0:64, 0:64], 1.0, w=[blkpos]); memset(blkpos[64:128, 64:128], 1.0, w=[blkpos])
    dummy = G.sb("dummy", [128, 1])
    masks = {}
    for C in (64, 4):
        mi = G.sb(f"mi{C}", [C, C]); ms = G.sb(f"ms{C}", [C, C]); mls = G.sb(f"mls{C}", [C, C])
        P.add("pool", lambda e, mi=mi, C=C: e.affine_select(out=mi, in_=ones128[0:C, 0:C], pattern=[[1, C]], compare_op=ALU.is_ge,
                                                         fill=0.0, base=0, channel_multiplier=-1), r=[ones128], w=[mi])
        P.add("pool", lambda e, ms=ms, C=C: e.affine_select(out=ms, in_=ones128[0:C, 0:C], pattern=[[1, C]], compare_op=ALU.is_gt,
                                                         fill=0.0, base=0, channel_multiplier=-1), r=[ones128], w=[ms])
        P.add("pool", lambda e, mls=mls, C=C: e.affine_select(out=mls, in_=ones128[0:C, 0:C], pattern=[[-1, C]], compare_op=ALU.is_gt,
                                                           fill=0.0, base=0, channel_multiplier=1), r=[ones128], w=[mls])
        msn = G.sb(f"msn{C}", [C, C]); mlsn = G.sb(f"mlsn{C}", [C, C])
        ts(msn, ms, -1.0, None, ALU.mult); ts(mlsn, mls, -1.0, None, ALU.mult)
        masks[C] = dict(incl=mi, strict=ms, nstrict=msn, nlstrict=mlsn)

    stgA = G.sb("stgA", [128, 128]); stgB = G.sb("stgB", [48, 128])
    memset(stgA, 0.0)
    rows = {}
    r0 = 0
    for nm, src, n in (("n1", norm1_w, 8), ("n2", norm2_w, 8), ("lb", lb_logits, 16), ("w0", w0, 8), ("a0", a0, 8),
                       ("kk", k_k, 8), ("ka", k_a, 8), ("rk", r_k, 8), ("lw", lnx_w, 8), ("lbb", lnx_b, 8), ("fn", fnw, 8)):
        dma("sp", stgA[r0:r0 + n, :], src, w=[stgA]); rows[nm] = r0; r0 += n
    dma("sp", stgA[r0:r0 + 26, :], mu[:, 0:3328].rearrange("o (n p) -> (o n) p", p=128), w=[stgA]); rows["mu"] = r0; r0 += 26
    dma("sp", stgA[r0:r0 + 1, 0:32], mu[:, 3328:3360], w=[stgA]); rows["mu2"] = r0; r0 += 1
    assert r0 <= 128
    dma("sp", stgB, ada_b)
    pv = G.sb("pv", [128, 128]); pvB = G.sb("pvB", [128, 48])
    b_ = bank(); tr(b_[:, 0:128], stgA, identF); cp(pv, b_[:, 0:128])
    b_ = bank(); tr(b_[:, 0:48], stgB, identF[0:48, 0:48]); cp(pvB, b_[:, 0:48])
    col = lambda nm, i=0: pv[:, rows[nm] + i: rows[nm] + i + 1]
    pv2 = G.sb("pv2", [128, 128]); ts(pv2, pv, -1.0, 1.0, ALU.mult, ALU.add)
    col2 = lambda nm, i=0: pv2[:, rows[nm] + i: rows[nm] + i + 1]
    hs1 = G.sb("hs1", [128, 8]); hs2 = G.sb("hs2", [128, 8]); hs1n = G.sb("hs1n", [128, 8]); hs2c = G.sb("hs2c", [128, 8])
    tt(hs1, pv[:, rows["lb"]:rows["lb"] + 8], pv[:, rows["lb"] + 8:rows["lb"] + 16], ALU.subtract)
    act(hs2, hs1, AF.Tanh, scale=0.5)
    ts(hs1, hs2, -0.25, 0.25, ALU.mult, ALU.add)
    ts(hs2, hs2, 0.25, 0.75, ALU.mult, ALU.add)
    ts(hs1n, hs1, -1.0, None, ALU.mult)
    ts(hs2c, hs2, -1.0, 1.0, ALU.mult, ALU.add)
    hw0 = G.sb("hw0", [128, 8]); ha0 = G.sb("ha0", [128, 8]); omka = G.sb("omka", [128, 8])
    ts(hw0, pv[:, rows["w0"]:rows["w0"] + 8], 0.5, None, ALU.mult)
    ts(ha0, pv[:, rows["a0"]:rows["a0"] + 8], 0.5, None, ALU.mult)
    ts(omka, pv[:, rows["ka"]:rows["ka"] + 8], -1.0, 1.0, ALU.mult, ALU.add)
    nwbc = G.sb("nwbc", [64, 128]); dma("sp", nwbc, hgrn_norm_w.partition_broadcast(64))

    modT = G.sb("modT", [128, 48, 17])
    g1p = G.sb("g1p", [128, 8]); g2p = G.sb("g2p", [128, 8])
    sh1p = G.sb("sh1p", [128, 8]); sh2p = G.sb("sh2p", [128, 8]); gt1p = G.sb("gt1p", [128, 8]); gt2p = G.sb("gt2p", [128, 8])
    mT = G.sb("mT", [128, 8, NT], BF16)
    S2 = Scope()
    hT = S2.sb("hT", [128, 8, NT], BF16)
    S0 = Scope()
    c17 = S0.sb("c17", [17, D]); s17 = S0.sb("s17", [17, D]); s17b = S0.sb("s17b", [17, D], BF16); scT = S0.sb("scT", [128, 8, 32], BF16)
    dma("sp", c17[0:1, :], cpr, w=[c17]); dma("sp", c17[1:17, :], csm, w=[c17])
    act(s17, c17, AF.Tanh, scale=0.5)
    ts(s17, s17, 0.5, 0.5, ALU.mult, ALU.add)
    tt(s17b, s17, c17, ALU.mult)
    b_ = bank(); bb = b_.bitcast(BF16)
    for k in range(8):
        tr(bb[:, k * 32:k * 32 + 17], s17b[:, k * 128:(k + 1) * 128], identB[0:17, 0:17], w=[b_])
    cp(scT[:, :, 0:17], bb[:, 0:256].rearrange("p (k c) -> p k c", c=32)[:, :, 0:17], r=[b_])
    awb = [S0.sb("awb", [128, 8, 512], BF16) for _ in range(2)]
    def ada_group(g):
            wt = awb[g % 2]
            dma("pool", wt, ada_w[:, g * 512:(g + 1) * 512].rearrange("(k p) c -> p k c", p=128), r=[ada_w], w=[wt])
            b_ = bank()
            for m in range(4):
                for k in range(8):
                    mm(b_[:, m * 32:m * 32 + 17], wt[:, k, m * 128:(m + 1) * 128], scT[:, k, 0:17], k == 0, k == 7, w=[b_])
            for m in range(4):
                ts(modT[:, g * 4 + m, :], b_[:, m * 32:m * 32 + 17], pvB[:, g * 4 + m:g * 4 + m + 1], None, ALU.add, r=[b_, pvB], w=[modT])

    def modp(part):
        return modT[:, part * 8:(part + 1) * 8, 0]
    def mods(part):
        return modT[:, part * 8:(part + 1) * 8, 1:17].unsqueeze(3).broadcast_to([128, 8, 16, 4])
    def mods3(part):
        return modT[:, part * 8:(part + 1) * 8, 1:17]

    def mk_mods(dst, part, nm=None):
        for t4 in range(4):
            if nm is None:
                cp(dst[:, :, t4::4], mods3(part), r=[modT], w=[dst])
            else:
                stt(dst[:, :, t4::4], mods3(part), 1.0, pv[:, rows[nm]:rows[nm] + 8].unsqueeze(2).broadcast_to([128, 8, 16]),
                    ALU.add, ALU.mult, r=[modT, pv], w=[dst])

    tiles = [(i * 256, 256, 64, 4, False) for i in range(8)] + [(SEQ + 16 * i, 16, 4, 4, True) for i in range(4)]

    def norm_to(hT_dst, xT_t, T, gp, shp_, gs, shs_, is_s, ti, fkeys=False):
        xk_ = [(xT_t, f_) for f_ in range(8)] if fkeys else [xT_t]
        sq = nbuf["sq"]
        tt(sq[:, :, 0:T], xT_t[:, :, 0:T], xT_t[:, :, 0:T], ALU.mult, r=xk_, w=[sq])
        b_ = bank()
        for fc in range(8):
            mm(b_[:, 0:T], onesB, sq[:, fc, 0:T], fc == 0, fc == 7, w=[b_])
        rstd = nbuf["rstd"]
        ts(rstd[:, 0:T], b_[:, 0:T], 1.0 / D, RMS_EPS, ALU.mult, ALU.add, r=[b_], w=[rstd])
        r1 = nbuf["r1"]; r2 = nbuf["r2"]
        cp(r1[:, 0:T], rstd[:, 0:T], eng="act", r=[rstd], w=[r1])
        rsqrt_dve(rstd[:, 0:T], r1[:, 0:T], r2[:, 0:T], [rstd, r1, r2])
        tmp = nbuf["tmp"]
        tt(tmp[:, :, 0:T], xT_t[:, :, 0:T], rstd[:, 0:T].unsqueeze(1).broadcast_to([128, 8, T]), ALU.mult, r=xk_ + [rstd], w=[tmp])
        if not is_s:
            for fc in range(8):
                ts(hT_dst[:, fc, :], tmp[:, fc, 0:T], gp[:, fc:fc + 1], shp_[:, fc:fc + 1], ALU.mult, ALU.add,
                   r=[tmp, gp, shp_], w=[(hT_dst, ti)])
        else:
            tt(tmp[:, :, 0:T], tmp[:, :, 0:T], gs, ALU.mult, r=[tmp, gs], w=[tmp])
            tt(hT_dst, tmp[:, :, 0:T], shs_, ALU.add, r=[tmp, shs_], w=[(hT_dst, ti)])

    def load_xT(xT_t, c0, T, ti, fkeys=False):
        for sub in range(0, T, 128):
            n = min(128, T - sub)
            xtm = nbuf["xtm"][nbuf["i"] % 2]; nbuf["i"] += 1
            src = xp[c0 + sub:c0 + sub + n, :] if c0 < SEQ else xs[c0 - SEQ + sub:c0 - SEQ + sub + n, :]
            dma("sp", xtm[0:n, :], src, w=[xtm])
            for half in range(2):
                b_ = bank()
                for f4 in range(4):
                    fc = half * 4 + f4
                    tr(b_[:, f4 * 128:f4 * 128 + n], xtm[0:n, fc * 128:(fc + 1) * 128], identF[0:n, 0:n], w=[b_])
                cp(xT_t[:, half * 4:half * 4 + 4, sub:sub + n], b_.rearrange("p (f t) -> p f t", t=128)[:, :, 0:n], eng="act", r=[b_],
                   w=([(xT_t, f_) for f_ in range(half * 4, half * 4 + 4)] if fkeys else [xT_t]))

    for g in range(4):
        ada_group(g)
    stt(g1p, modp(1), 1.0, pv[:, rows["n1"]:rows["n1"] + 8], ALU.add, ALU.mult, r=[modT, pv])
    cp(sh1p, modp(0), r=[modT])
    S1 = Scope()
    g1s = S1.sb("g1s", [128, 8, 64]); sh1s = S1.sb("sh1s", [128, 8, 64])
    mk_mods(sh1s, 0); mk_mods(g1s, 1, "n1")
    nbuf = dict(sq=S1.sb("sq", [128, 8, 512], BF16), rstd=S1.sb("rstd", [128, 512]), tmp=S1.sb("ntmp", [128, 8, 512]), r1=S1.sb("r1", [128, 512]), r2=S1.sb("r2", [128, 512]),
                xtm=[S1.sb("xtm", [128, D]) for _ in range(2)], i=0)
    xT_ts = [S1.sb("xT_t", [128, 8, 512]) for _ in range(2)]
    gnext = 4
    for ti, (c0, T, C, NCH, is_s) in enumerate(tiles):
        xT_t = xT_ts[ti % 2]
        load_xT(xT_t, c0, T, ti)
        so = max(c0 - SEQ, 0)
        norm_to(hT[:, :, c0:c0 + T], xT_t, T, g1p, sh1p, g1s[:, :, so:so + min(T, 16)], sh1s[:, :, so:so + min(T, 16)], is_s, ti)
        if gnext < 12:
            ada_group(gnext); gnext += 1
    while gnext < 12:
        ada_group(gnext); gnext += 1
    stt(g2p, modp(4), 1.0, pv[:, rows["n2"]:rows["n2"] + 8], ALU.add, ALU.mult, r=[modT, pv])
    for dst, part in ((gt1p, 2), (sh2p, 3), (gt2p, 5)):
        cp(dst, modp(part), r=[modT])
    S1.close()
    S0.close()
    P.barrier(dummy)

    W = lambda name, shape, dt=F32: S2.sb(name, shape, dt)
    wbH = W("wbH", [128, 8, 640], BF16); wbR = W("wbR", [128, 8, 512], BF16)
    wl = wbH[:, :, 0:288]
    w2b = W("w2b", [128, D], BF16)
    g2b = W("g2b", [128, D], BF16); g2c = W("g2c", [32, D], BF16)
    TW = W("TW", [128, NT], BF16); SG1 = W("SG1", [128, NT], BF16); SG2 = W("SG2", [32, NT], BF16)
    ssT = W("ssT", [128, 27, 16]); lastc = W("lastc", [128, 27, 17])
    prevs = W("prevs", [128, 8])
    dsh = W("dsh", [128, 256])
    S0all = W("S0all", [128, 4, 128]); S0bf = W("S0bf", [128, 4, 128], BF16); Souts = S0all
    Sh = W("Sh", [128, 128]); Shbf = W("Shbf", [128, 128], BF16)
    R0in = W("R0in", [128, 4, 128]); R0blk = W("R0blk", [128, 4, 128]); R0bf = W("R0bf", [128, 4, 128], BF16); Routs = R0blk
    Sr = W("Sr", [128, 128]); Srbf = W("Srbf", [128, 128], BF16); Srt = W("Srt", [128, 128])
    f32t = {n: W(n, [128, 256]) for n in ("th", "fS", "kS", "Pc", "rP", "Ga", "t1", "t2", "Gmm", "kap", "kmod", "bS", "gS",
                                          "xr", "xk", "xv", "rG", "aS", "u1", "u2")}
    f32t["n1"] = f32t["kmod"]; f32t["n2"] = f32t["u1"]
    oTs = [W("oT", [128, 256]) for _ in range(2)]
    PsmpH = W("PsmpH", [128, 5, 64]); PsmpR = W("PsmpR", [128, 4, 64])
    rstH = W("rstH", [128, 256]); rstR = W("rstR", [128, 256]); memset(rstH, 0.0); memset(rstR, 0.0)
    bft = {n: W(n, [128, 256], BF16) for n in ("qt", "kt", "kh", "vT", "sqb", "kpt", "bt", "bh", "ktl", "khr", "vTr")}
    KR = W("KR", [128, 4, 128], BF16)
    tm = {n: W(n, [64, 4, 128], BF16) for n in ("Vtm", "khtm", "W1", "Un", "Ktp", "onb")}
    Abm = {n: W(n, [64, 4, 128], BF16) for n in ("Y2", "Z2", "TTa", "TTb")}
    IFs = [dict(**{n: W(n, [64, 4, 128], BF16) for n in ("Z", "Y", "PbT", "PkT", "AkT", "Vtr", "Ktm", "bhtm", "khrtm")},
                rt=W("rt", [128, 256], BF16), **{n: W(n, [128, 256]) for n in ("Gm", "bon", "Gb")}) for _ in range(2)]
    attnT = W("attnT", [64, 4, 64], BF16)
    Osb = W("Osb", [64, 4, 128]); Osq = W("Osq", [64, 4, 128]); st_ = W("st_", [64, 16]); stA = W("stA", [64, 16]); stB = W("stB", [64, 16])
    srow = Osq.rearrange("p c f -> p (c f)")[0:17, :]
    Osb2 = W("Osb2", [64, 4, 128]); st3 = W("st3", [64, 16]); st4 = W("st4", [64, 16]); stC = W("stC", [64, 16]); stD = W("stD", [64, 16])
    Mblk = W("Mblk", [128, 4, 128], BF16); Ncomb = W("Ncomb", [128, 4, 128]); RpT = W("RpT", [128, 256], BF16)

    dma("pool", w2b[0:64, :], w2, w=[w2b]); dma("pool", w2b[64:128, :], a2, w=[w2b])
    dma("pool", g2b, g2[0:128, :]); dma("pool", g2c, g2[128:160, :])
    dma("pool", wl, w_in[:, 7168:7456].rearrange("(k p) c -> p k c", p=128), r=[w_in], w=[wl])
    for p4 in range(0, 27, 4):
        n4 = min(4, 27 - p4); wdt = min(n4 * 128, RC - p4 * 128)
        memset(srow, 0.0, eng="dve")
        dma("sp", srow[0:16, 0:wdt], ss_in[:, p4 * 128:p4 * 128 + wdt], r=[ss_in], w=[srow])
        b_ = bank()
        for q in range(n4):
            tr(b_[:, q * 16:(q + 1) * 16], srow[0:16, q * 128:(q + 1) * 128], identF[0:16, 0:16], r=[srow, identF], w=[b_])
        cp(ssT[:, p4:p4 + n4, :], b_[:, 0:n4 * 16].rearrange("p (q s) -> p q s", s=16), r=[b_], w=[ssT])
    memset(prevs, 0.0); memset(lastc, 0.0)

    def proj(wt, c_lo, c_hi, ti, c0, T):
        b_ = bank(); n = c_hi - c_lo
        for k in range(8):
            mm(b_[0:n, 0:T], wt[:, k, c_lo:c_hi], hT[:, k, c0:c0 + T], k == 0, k == 7, r=[wt, (hT, ti)], w=[b_])
        return b_

    def proj_samples(wt, nseg, dstP):
        for sg in range(nseg):
            b_ = bank()
            for k in range(8):
                mm(b_[:, 0:64], wt[:, k, sg * 128:(sg + 1) * 128], hT[:, k, SEQ:SEQ + 64], k == 0, k == 7,
                   r=[wt] + [(hT, t_) for t_ in range(8, 12)], w=[b_])
            cp(dstP[:, sg, :], b_[:, 0:64], eng="act", r=[b_], w=[dstP])

    def shift_mix(dst, ps, n, T, is_s, mucol, pcol, part, last_tile, s0=0, omucol=None):
        aff(dsh[0:n, 0:T], ps[0:n, 0:T], omucol[0:n, :], 0.0, r=[ps, pv2], w=[dsh])
        if not is_s:
            stt(dst[0:n, 1:T], ps[0:n, 0:T - 1], mucol[0:n, :], dsh[0:n, 1:T], ALU.mult, ALU.add, r=[ps, dsh, pv], w=[dst])
            stt(dst[0:n, 0:1], prevs[0:n, pcol:pcol + 1], mucol[0:n, :], dsh[0:n, 0:1], ALU.mult, ALU.add, r=[prevs, dsh, pv], w=[dst])
            cp(prevs[0:n, pcol:pcol + 1], ps[0:n, T - 1:T], eng="act", r=[ps, dst], w=[prevs])
            if last_tile:
                cp(lastc[0:n, part, 0:1], ps[0:n, T - 1:T], eng="act", r=[ps], w=[lastc])
        else:
            p3 = ps[0:n, 0:T].rearrange("p (s t) -> p s t", t=4); d3 = dsh[0:n, 0:T].rearrange("p (s t) -> p s t", t=4)
            o3 = dst[0:n, 0:T].rearrange("p (s t) -> p s t", t=4)
            stt(o3[:, :, 1:4], p3[:, :, 0:3], mucol[0:n, :], d3[:, :, 1:4], ALU.mult, ALU.add, r=[ps, dsh, pv], w=[dst])
            stt(o3[:, :, 0], ssT[0:n, part, s0:s0 + T // 4], mucol[0:n, :], d3[:, :, 0], ALU.mult, ALU.add, r=[ssT, dsh, pv], w=[dst])
            cp(lastc[0:n, part, 1 + s0:1 + s0 + T // 4], p3[:, :, 3], eng="act", r=[ps], w=[lastc])

    def sigm(dst, src, r=(), w=(), bias=0.0):
        act(dst, src, AF.Tanh, bias=bias, scale=0.5, r=r, w=w)
        aff(dst, dst, 0.5, 0.5)

    for ti, (c0, T, C, NCH, is_s) in enumerate(tiles):
        last = (c0 + T == SEQ)
        b_ = proj(wl, 0, 128, ti, c0, T)
        shift_mix(f32t["t1"], b_, 128, T, is_s, col("mu", 24), 0, 24, last, max(c0 - SEQ, 0) // 4, omucol=col2("mu", 24))
        act(TW[0:64, c0:c0 + T], f32t["t1"][0:64, 0:T], AF.Tanh, r=[f32t["t1"]], w=[(TW, ti)])
        cp(TW[64:128, c0:c0 + T], f32t["t1"][64:128, 0:T], eng="act", r=[f32t["t1"]], w=[(TW, ti)])
        b_ = proj(wl, 128, 256, ti, c0, T)
        shift_mix(f32t["t1"], b_, 128, T, is_s, col("mu", 25), 1, 25, last, max(c0 - SEQ, 0) // 4, omucol=col2("mu", 25))
        sigm(f32t["t2"][:, 0:T], f32t["t1"][:, 0:T], r=[f32t["t1"]], w=[f32t["t2"]])
        cp(SG1[:, c0:c0 + T], f32t["t2"][:, 0:T], r=[f32t["t2"]], w=[(SG1, ti)])
        b_ = proj(wl, 256, 288, ti, c0, T)
        shift_mix(f32t["t1"], b_, 32, T, is_s, col("mu2"), 2, 26, last, max(c0 - SEQ, 0) // 4, omucol=col2("mu2"))
        sigm(f32t["t2"][0:32, 0:T], f32t["t1"][0:32, 0:T], r=[f32t["t1"]], w=[f32t["t2"]])
        cp(SG2[:, c0:c0 + T], f32t["t2"][0:32, 0:T], r=[f32t["t2"]], w=[(SG2, ti)])

    if debug is not None and debug.get("_stop") == "lora":
        dump("TW", TW, [128, NT]); dump("SG1", SG1, [128, NT]); dump("SG2", SG2, [32, NT]); dump("lastc", lastc.rearrange("p a b -> p (a b)"), [128, 27 * 17])
        n = P.emit()
        return nc, dbg, n

    def chunk3(ap, C):
        return ap.rearrange("p (c t) -> p c t", t=C)

    def to_tm(dst, src_bf, T, C, NCH):
        for c4 in range(0, NCH, 8):
            b_ = bank(); bb = b_.bitcast(BF16)
            for c in range(c4, min(c4 + 8, NCH)):
                tr(bb[0:C, (c - c4) * 128:(c - c4 + 1) * 128], src_bf[:, c * C:(c + 1) * C], identB, r=[src_bf, identB], w=[b_])
            n8 = min(8, NCH - c4)
            cp(dst[0:C, c4:c4 + n8, :], bb[0:C, 0:n8 * 128].rearrange("p (c f) -> p c f", f=128), eng="act", r=[b_], w=[dst])

    def cumprod(dst, src, T, C, NCH, rst):
        cp(chunk3(rst[:, 0:T], C)[:, :, 0:1], chunk3(src[:, 0:T], C)[:, :, 0:1], eng="act", r=[src], w=[rst])
        P.add("dve", lambda e: e.tensor_tensor_scan(out=dst[:, 0:T], data0=src[:, 0:T], data1=rst[:, 0:T], initial=1.0,
                                                    op0=ALU.mult, op1=ALU.max), r=[src, rst], w=[dst])
        if C != 64:
            memset(chunk3(rst[:, 0:T], C)[:, :, 0:1], 0.0, eng="pool", w=[rst])

    def lastbc(ap, T, C, NCH):
        return chunk3(ap[:, 0:T], C)[:, :, C - 1:C].broadcast_to([128, NCH, C])


    def rwkv_front(j, ti, c0, T, C, NCH, is_s, last, wb, gi):
        F_ = f32t; B_ = bft; M_ = masks[C]; jc = slice(j * 128, (j + 1) * 128)
        I_ = IFs[gi % 2]
        if gi >= 2:
            P.add("need", ("B", gi - 2))
        X = lambda n: F_[n][:, 0:T]
        Bx = lambda n: B_[n][:, 0:T]
        Gm = I_["Gm"]; bon = I_["bon"]; Gb = I_["Gb"]; rt = I_["rt"]
        so_ = max(c0 - SEQ, 0) // 4
        if is_s and so_ == 0:
            proj_samples(wb, 4, PsmpR)
        PJ = (lambda lo, hi: PsmpR[:, lo // 128, so_ * 4:so_ * 4 + T]) if is_s else (lambda lo, hi: proj(wb, lo, hi, ti, c0, T)[:, 0:T])
        br = PJ(0, 128); shift_mix(F_["xr"], br, 128, T, is_s, col("mu", j), 3, j, last, so_, omucol=col2("mu", j))
        bk_ = PJ(128, 256); shift_mix(F_["xk"], bk_, 128, T, is_s, col("mu", 8 + j), 4, 8 + j, last, so_, omucol=col2("mu", 8 + j))
        bv = PJ(256, 384); shift_mix(F_["xv"], bv, 128, T, is_s, col("mu", 16 + j), 5, 16 + j, last, so_, omucol=col2("mu", 16 + j))
        bgb = PJ(384, 512)
        sigm(Gb[:, 0:T], bgb[:, 0:T], r=[bgb], w=[Gb])
        bw = bank(); mm(bw[:, 0:T], w2b[0:64, jc], TW[0:64, c0:c0 + T], True, True, r=[w2b, (TW, ti)], w=[bw])
        act(X("u1"), bw[:, 0:T], AF.Tanh, bias=hw0[:, j:j + 1], scale=0.5, r=[bw], w=[F_["u1"]])
        act(X("u1"), X("u1"), AF.Exp, bias=C1, scale=C1)
        ba = bank(); mm(ba[:, 0:T], w2b[64:128, jc], TW[64:128, c0:c0 + T], True, True, r=[w2b, (TW, ti)], w=[ba])
        sigm(X("aS"), ba[:, 0:T], r=[ba], w=[F_["aS"]], bias=ha0[:, j:j + 1])
        bg = bank(); mm(bg[:, 0:T], g2b[:, jc], SG1[:, c0:c0 + T], True, False, r=[g2b, (SG1, ti)], w=[bg])
        mm(bg[:, 0:T], g2c[:, jc], SG2[:, c0:c0 + T], False, True, r=[g2c, (SG2, ti)], w=[bg])
        cp(X("gS"), bg[:, 0:T], eng="act", r=[bg], w=[F_["gS"]])
        tt(Gb[:, 0:T], Gb[:, 0:T], X("gS"), ALU.mult, eng="pool", r=[Gb, F_["gS"]], w=[Gb])
        cumprod(Gm, F_["u1"], T, C, NCH, rstR)
        P.add("dve", lambda e: e.reciprocal(out=X("rG"), in_=Gm[:, 0:T]), r=[Gm], w=[F_["rG"]])
        cp(chunk3(X("Gmm"), C)[:, :, 1:C], chunk3(Gm[:, 0:T], C)[:, :, 0:C - 1], r=[Gm], w=[F_["Gmm"]])
        memset(chunk3(X("Gmm"), C)[:, :, 0:1], 1.0, eng="dve", w=[F_["Gmm"]])
        aff(X("kap"), X("xk"), col("kk", j), 0.0, r=[F_["xk"], pv], w=[F_["kap"]])
        tt(Bx("sqb"), X("kap"), X("kap"), ALU.mult, eng="pool")
        bn = bank(); mm(bn[:, 0:T], blkones, Bx("sqb"), True, True, w=[bn])
        ts(X("n1"), bn[:, 0:T], 1.0, 1e-24, ALU.mult, ALU.add, r=[bn], w=[F_["n1"]])
        rsqrt_dve(X("u2"), X("n1"), X("n2"), [F_["u2"], F_["n1"], F_["n2"]])
        tt(X("kap"), X("kap"), X("u2"), ALU.mult)
        aff(X("u2"), X("aS"), col("ka", j), omka[:, j:j + 1], r=[F_["aS"], pv, omka], w=[F_["u2"]])
        tt(X("kmod"), X("xk"), X("u2"), ALU.mult, eng="pool")
        tt(X("bS"), X("aS"), X("kap"), ALU.mult)
        stt(Bx("sqb"), X("xr"), col("rk", j), X("kmod"), ALU.mult, ALU.mult, r=[F_["xr"], F_["kmod"], pv], w=[B_["sqb"]])
        bn2 = bank(); mm(bn2[:, 0:T], blkones, Bx("sqb"), True, True, w=[bn2])
        tt(bon[:, 0:T], bn2[:, 0:T], X("xv"), ALU.mult, r=[bn2, F_["xv"]], w=[bon])
        tt(Bx("kpt"), X("kap"), X("Gmm"), ALU.mult)
        tt(rt[:, 0:T], X("xr"), Gm[:, 0:T], ALU.mult, eng="pool", r=[F_["xr"], Gm], w=[rt])
        cp(KR[:, 0:NCH, 0:C], chunk3(Bx("kpt"), C), eng="pool", r=[B_["kpt"]], w=[KR])
        cp(KR[:, 0:NCH, 64:64 + C], chunk3(rt[:, 0:T], C), eng="pool", r=[rt], w=[KR])
        tt(X("u2"), X("kmod"), X("rG"), ALU.mult); cp(Bx("ktl"), X("u2"), eng="act")
        tt(chunk3(Bx("khr"), C), chunk3(X("u2"), C), lastbc(Gm, T, C, NCH), ALU.mult, r=[F_["u2"], Gm], w=[B_["khr"]])
        tt(X("u2"), X("bS"), X("rG"), ALU.mult); cp(Bx("bt"), X("u2"), eng="act")
        tt(chunk3(Bx("bh"), C), chunk3(X("u2"), C), lastbc(Gm, T, C, NCH), ALU.mult, r=[F_["u2"], Gm], w=[B_["bh"]])
        cp(Bx("vTr"), X("xv"), eng="act")
        to_tm(I_["Vtr"], B_["vTr"], T, C, NCH); to_tm(I_["Ktm"], B_["kpt"], T, C, NCH)
        to_tm(I_["bhtm"], B_["bh"], T, C, NCH); to_tm(I_["khrtm"], B_["khr"], T, C, NCH)
        v4_ = lambda b: b[0:C, :].rearrange("p (c x t) -> p c x t", x=2, t=64)
        mk = lambda m_: m_.unsqueeze(1).broadcast_to([C, NCH, C])
        for h in range(2):
            hs_ = slice(64 * h, 64 * h + 64)
            dv = lambda t_: t_[0:C, 0:NCH, h * 64:h * 64 + C]
            b1 = bank()
            for c in range(NCH):
                mm(b1[0:C, c * 128:(c + 1) * 128], B_["bt"][hs_, c * C:(c + 1) * C], KR[hs_, c, :], True, True, r=[B_["bt"], KR], w=[b1])
            tt(dv(I_["Z"]), v4_(b1)[:, 0:NCH, 0, 0:C], mk(M_["nstrict"]), ALU.mult, r=[b1, M_["nstrict"]], w=[I_["Z"]])
            tt(dv(I_["PbT"]), v4_(b1)[:, 0:NCH, 1, 0:C], mk(M_["incl"]), ALU.mult, r=[b1, M_["incl"]], w=[I_["PbT"]])
            b2 = bank()
            for c in range(NCH):
                mm(b2[0:C, c * 128:(c + 1) * 128], B_["ktl"][hs_, c * C:(c + 1) * C], KR[hs_, c, :], True, True, r=[B_["ktl"], KR], w=[b2])
            tt(dv(I_["AkT"]), v4_(b2)[:, 0:NCH, 0, 0:C], mk(M_["strict"]), ALU.mult, r=[b2, M_["strict"]], w=[I_["AkT"]])
            tt(dv(I_["PkT"]), v4_(b2)[:, 0:NCH, 1, 0:C], mk(M_["incl"]), ALU.mult, r=[b2, M_["incl"]], w=[I_["PkT"]])
            b3 = bank()
            for c in range(NCH):
                mm(b3[0:C, c * 128:c * 128 + C], B_["kpt"][hs_, c * C:(c + 1) * C], B_["bt"][hs_, c * C:(c + 1) * C], True, True, r=[B_["kpt"], B_["bt"]], w=[b3])
            tt(dv(I_["Y"]), v4_(b3)[:, 0:NCH, 0, 0:C], mk(M_["nlstrict"]), ALU.mult, r=[b3, M_["nlstrict"]], w=[I_["Y"]])
        P.add("mark", ("F", gi))

    def rwkv_back(j, ti, c0, T, C, NCH, is_s, last, oT, gi):
        F_ = f32t; M_ = masks[C]
        I_ = IFs[gi % 2]
        L = 6 if C == 64 else 2
        X = lambda n: F_[n][:, 0:T]
        Gm = I_["Gm"]; bon = I_["bon"]; Gb = I_["Gb"]; rt = I_["rt"]
        Osq2 = Ncomb; onr = tm["Un"]; e1 = Mblk.bitcast(F32).rearrange("p c f -> p (c f)")
        if is_s:
            s0 = (c0 - SEQ) // 4
            memset(R0in, 0.0)
            dma("sp", R0in[0:64, :, 0:64], sr_in[s0:s0 + 4, 2 * j].rearrange("s i j -> i s j"), r=[sr_in], w=[R0in])
            dma("sp", R0in[64:128, :, 64:128], sr_in[s0:s0 + 4, 2 * j + 1].rearrange("s i j -> i s j"), r=[sr_in], w=[R0in])
            b_ = bank()
            for q in range(4):
                tr(b_[:, q * 128:(q + 1) * 128], R0in[:, q, :], identF, w=[b_])
            cp(R0blk[:, 0:4, :], b_.rearrange("p (q f) -> p q f", f=128), r=[b_], w=[R0blk])
            cp(R0bf, R0blk, eng="pool")
        P.add("need", ("F", gi))
        hv = lambda t_, c_lo, c_hi: t_[0:C, c_lo:c_hi, :].rearrange("p c (h t) -> p c h t", t=64)[:, :, :, 0:C]
        tt(hv(Abm["TTa"], 0, NCH), hv(I_["Z"], 0, NCH), identF[0:C, 0:C].unsqueeze(1).unsqueeze(1).broadcast_to([C, NCH, 2, C]), ALU.add,
           r=[I_["Z"], identF], w=[Abm["TTa"]])
        Yc, Zc, Yn, Zn, TTc, TTn = I_["Y"], I_["Z"], Abm["Y2"], Abm["Z2"], Abm["TTa"], Abm["TTb"]
        pr = [(c, h) for c in range(NCH) for h in range(2)]

        def batch_mm(dst, lhs, rhs, wid_l, wid_r, post):
            for p8 in range(0, len(pr), 8):
                b_ = bank(); grp = pr[p8:p8 + 8]
                for q, (c, h) in enumerate(grp):
                    mm(b_[0:C, q * 64:q * 64 + wid_r], lhs[0:C, c, h * 64:h * 64 + wid_l], rhs[0:C, c, h * 64:h * 64 + wid_r], True, True,
                       r=[lhs, rhs], w=[b_])
                cA = grp[0][0]; cB = grp[-1][0] + 1
                src = b_[0:C, 0:(cB - cA) * 128].rearrange("p (c h t) -> p c h t", h=2, t=64)[:, :, :, 0:wid_r]
                d = dst[0:C, cA:cB, :].rearrange("p c (h t) -> p c h t", t=64)[:, :, :, 0:wid_r]
                post(d, src, b_)

        cpy = lambda d, s_, b_: cp(d, s_, eng="act", r=[b_], w=[d])
        for n in range(1, L):
            batch_mm(Yn, Zc, Yc, C, C, cpy)
            if n < L - 1:
                batch_mm(Zn, Yc, Zc, C, C, cpy)
            for p8 in range(0, len(pr), 8):
                b_ = bank(); grp = pr[p8:p8 + 8]
                for q, (c, h) in enumerate(grp):
                    mm(b_[0:C, q * 64:q * 64 + C], Yn[0:C, c, h * 64:h * 64 + C], TTc[0:C, c, h * 64:h * 64 + C], True, True, r=[Yn, TTc], w=[b_])
                cA = grp[0][0]; cB = grp[-1][0] + 1
                src = b_[0:C, 0:(cB - cA) * 128].rearrange("p (c h t) -> p c h t", h=2, t=64)[:, :, :, 0:C]
                tt(hv(TTn, cA, cB), src, hv(TTc, cA, cB), ALU.add, r=[b_, TTc], w=[TTn])
            Yc, Yn = Yn, Yc; Zc, Zn = Zn, Zc; TTc, TTn = TTn, TTc
        TTf = TTc
        batch_mm(tm["W1"], I_["AkT"], I_["Vtr"], C, 64, cpy)
        batch_mm(tm["Ktp"], TTf, I_["Ktm"], C, 64, cpy)
        neg = lambda d, s_, b_: ts(d, s_, -1.0, None, ALU.mult, r=[b_], w=[d])
        batch_mm(tm["Un"], TTf, tm["W1"], C, 64, neg)
        v3 = lambda b: b.rearrange("p (c f) -> p c f", f=128)
        bM = bank()
        for c in range(NCH):
            mm(bM[:, c * 128:(c + 1) * 128], tm["Ktp"][0:C, c, :], I_["bhtm"][0:C, c, :], True, True, r=[tm["Ktp"], I_["bhtm"]], w=[bM])
        tt(Mblk[:, 0:NCH, :], v3(bM)[:, 0:NCH, :], blkneg.unsqueeze(1).broadcast_to([128, NCH, 128]), ALU.mult, r=[bM, blkneg], w=[Mblk])
        bR = bank()
        for c in range(NCH):
            mm(bR[:, c * 128:(c + 1) * 128], tm["Ktp"][0:C, c, :], I_["PbT"][0:C, c, :], True, True, r=[tm["Ktp"], I_["PbT"]], w=[bR])
        for h in range(2):
            hs_ = slice(64 * h, 64 * h + 64)
            tt(chunk3(RpT[hs_, 0:T], C), chunk3(rt[hs_, 0:T], C), v3(bR)[hs_, 0:NCH, h * 64:h * 64 + C], ALU.subtract, r=[rt, bR], w=[RpT])
        bN = bank()
        for c in range(NCH):
            mm(bN[:, c * 128:(c + 1) * 128], I_["khrtm"][0:C, c, :], I_["Vtr"][0:C, c, :], True, False, r=[I_["khrtm"], I_["Vtr"]], w=[bN])
            mm(bN[:, c * 128:(c + 1) * 128], I_["bhtm"][0:C, c, :], tm["Un"][0:C, c, :], False, True, r=[I_["bhtm"], tm["Un"]], w=[bN])
        tt(Ncomb[:, 0:NCH, :], v3(bN)[:, 0:NCH, :], blkpos.unsqueeze(1).broadcast_to([128, NCH, 128]), ALU.mult, r=[bN, blkpos], w=[Ncomb])
        bo = banks[5]
        for c in range(NCH):
            csl = slice(c * C, (c + 1) * C)
            sbf = R0bf[:, c, :] if is_s else Srbf
            sin = R0blk[:, c, :] if is_s else Sr
            sout = Routs[:, c, :] if is_s else Sr
            kS_ = [R0bf] if is_s else [Srbf]
            oo = bo[0:C, c * 128:(c + 1) * 128]
            mm(oo, RpT[:, csl], sbf, True, False, r=[RpT] + kS_, w=[bo])
            for h in range(2):
                hb = slice(h * 64, (h + 1) * 64)
                mm(oo[:, hb], I_["PkT"][0:C, c, h * 64:h * 64 + C], I_["Vtr"][0:C, c, hb], False, False, r=[I_["PkT"], I_["Vtr"]], w=[bo])
                mm(oo[:, hb], I_["PbT"][0:C, c, h * 64:h * 64 + C], tm["Un"][0:C, c, hb], False, h == 1, r=[I_["PbT"], tm["Un"]], w=[bo])
            b3 = bank()
            mm(b3[:, 0:128], Mblk[:, c, :], sbf, True, True, r=[Mblk] + kS_, w=[b3])
            stt(Srt, sin, Gm[:, (c + 1) * C - 1:(c + 1) * C], Ncomb[:, c, :], ALU.mult, ALU.add,
                r=[R0blk if is_s else Sr, Gm, Ncomb], w=[Srt])
            tt(sout, Srt, b3[:, 0:128], ALU.add, r=[Srt, b3], w=[Routs if is_s else Sr])
            if not is_s:
                cp(Srbf, Sr, eng="act")
        cp(Osb2[0:C, 0:NCH, :], bo.rearrange("p (c f) -> p c f", f=128)[0:C, 0:NCH, :], eng="act", r=[bo], w=[Osb2])
        if is_s:
            b_ = bank()
            for q in range(4):
                tr(b_[:, q * 128:(q + 1) * 128], Routs[:, q, :], identF, w=[b_])
            cp(R0in[:, 0:4, :], b_.rearrange("p (q f) -> p q f", f=128), r=[b_], w=[R0in])
            dma("sp", rs[s0:s0 + 4, 2 * j].rearrange("s i j -> i s j"), R0in[0:64, :, 0:64], r=[R0in], w=[rs])
            dma("sp", rs[s0:s0 + 4, 2 * j + 1].rearrange("s i j -> i s j"), R0in[64:128, :, 64:128], r=[R0in], w=[rs])
        elif last:
            b_ = bank(); tr(b_[:, 0:128], Sr, identF, w=[b_]); cp(Srt, b_[:, 0:128])
            dma("sp", rp[2 * j], Srt[0:64, 0:64], r=[Srt], w=[rp]); dma("sp", rp[2 * j + 1], Srt[64:128, 64:128], r=[Srt], w=[rp])
        G2 = NCH * 2
        og = Osb2[0:C, 0:NCH, :].rearrange("p c (h i) -> p (c h) i", i=64)
        oq = Osq2[0:C, 0:NCH, :].rearrange("p c (h i) -> p (c h) i", i=64)
        P.add("dve", lambda e: e.reduce_sum(out=st3[0:C, 0:G2], in_=og, axis=AX.X), r=[Osb2], w=[st3])
        ts(st3[0:C, 0:G2], st3[0:C, 0:G2], 1.0 / 64, None, ALU.mult)
        tt(og, og, st3[0:C, 0:G2].unsqueeze(2).broadcast_to([C, G2, 64]), ALU.subtract, r=[Osb2, st3], w=[Osb2])
        tt(oq, og, og, ALU.mult, eng="pool", r=[Osb2], w=[Osq2])
        P.add("dve", lambda e: e.reduce_sum(out=st4[0:C, 0:G2], in_=oq, axis=AX.X), r=[Osq2], w=[st4])
        rsqrt_(st4[0:C, 0:G2], st4[0:C, 0:G2], 1.0 / 64, LNX_EPS, stC[0:C, 0:G2], stD[0:C, 0:G2])
        tt(onr[0:C, 0:NCH, :].rearrange("p c (h i) -> p (c h) i", i=64), og, st4[0:C, 0:G2].unsqueeze(2).broadcast_to([C, G2, 64]), ALU.mult,
           r=[Osb2, st4], w=[onr])
        b_ = bank(); bb = b_.bitcast(BF16)
        for c in range(NCH):
            tr(bb[:, c * C:(c + 1) * C], onr[0:C, c, :], identB[0:C, 0:C], r=[onr, identB], w=[b_])
        ts(e1[:, 0:T], bb[:, 0:T], col("lw", j), col("lbb", j), ALU.mult, ALU.add, r=[b_, pv], w=[Mblk])
        tt(e1[:, 0:T], e1[:, 0:T], bon[:, 0:T], ALU.add, eng="pool", r=[Mblk, bon], w=[Mblk])
        tt(e1[:, 0:T], e1[:, 0:T], Gb[:, 0:T], ALU.mult, eng="pool", r=[Mblk, Gb], w=[Mblk])
        P.add("need", ("oTw", gi))
        tt(mT[:, j, c0:c0 + T], e1[:, 0:T], oT[:, 0:T], ALU.add, r=[Mblk, oT], w=[(mT, ti)])
        P.add("mark", ("oTr", gi))
        P.add("mark", ("B", gi))

    NBLK = 8 if debug is None else debug.get('nblk', 8)

    def hgrn_tile(j, ti, c0, T, C, NCH, is_s, last, wb, oT, gi):
        M_ = masks[C]; F_ = f32t; B_ = bft
        if is_s:
            s0 = (c0 - SEQ) // 4
            dma("sp", S0all, sh_in[s0:s0 + 4, j].rearrange("s k v -> k s v"), r=[sh_in], w=[S0all])
            cp(S0bf, S0all, eng="pool")
        so_ = max(c0 - SEQ, 0)
        if is_s and so_ == 0:
            proj_samples(wb, 5, PsmpH)
        PJ = (lambda lo, hi: PsmpH[:, lo // 128, so_:so_ + T]) if is_s else (lambda lo, hi: proj(wb, lo, hi, ti, c0, T)[:, 0:T])
        bf_ = PJ(128, 256)
        act(F_["th"][:, 0:T], bf_[:, 0:T], AF.Tanh, scale=0.5, r=[bf_], w=[F_["th"]])
        aff(F_["fS"][:, 0:T], F_["th"][:, 0:T], hs1[:, j:j + 1], hs2[:, j:j + 1], r=[F_["th"], hs1, hs2], w=[F_["fS"]])
        aff(F_["kS"][:, 0:T], F_["th"][:, 0:T], hs1n[:, j:j + 1], hs2c[:, j:j + 1], r=[F_["th"], hs1n, hs2c], w=[F_["kS"]])
        cumprod(F_["Pc"], F_["fS"], T, C, NCH, rstH)
        P.add("dve", lambda e, T=T: e.reciprocal(out=F_["rP"][:, 0:T], in_=F_["Pc"][:, 0:T]), r=[F_["Pc"]], w=[F_["rP"]])
        bq = PJ(0, 128)
        tt(B_["qt"][:, 0:T], bq[:, 0:T], F_["Pc"][:, 0:T], ALU.mult, r=[bq, F_["Pc"]], w=[B_["qt"]])
        tt(F_["t1"][:, 0:T], F_["kS"][:, 0:T], F_["rP"][:, 0:T], ALU.mult, eng="pool", r=[F_["kS"], F_["rP"]], w=[F_["t1"]])
        cp(B_["kt"][:, 0:T], F_["t1"][:, 0:T], eng="act", r=[F_["t1"]], w=[B_["kt"]])
        tt(chunk3(B_["kh"][:, 0:T], C), chunk3(F_["t1"][:, 0:T], C), lastbc(F_["Pc"], T, C, NCH), ALU.mult, r=[F_["t1"], F_["Pc"]], w=[B_["kh"]])
        bi = PJ(256, 384)
        cp(B_["vT"][:, 0:T], bi[:, 0:T], eng="act", r=[bi], w=[B_["vT"]])
        bog = PJ(384, 512)
        sigm(F_["t2"][:, 0:T], bog[:, 0:T], r=[bog], w=[F_["t2"]])
        tt(F_["t2"][:, 0:T], bog[:, 0:T], F_["t2"][:, 0:T], ALU.mult, r=[bog, F_["t2"]], w=[F_["t2"]])
        bga = PJ(512, 640)
        sigm(F_["Ga"][:, 0:T], bga[:, 0:T], r=[bga], w=[F_["Ga"]])
        tt(F_["Ga"][:, 0:T], F_["Ga"][:, 0:T], F_["t2"][:, 0:T], ALU.mult, eng="pool", r=[F_["Ga"], F_["t2"]], w=[F_["Ga"]])
        to_tm(tm["Vtm"], B_["vT"], T, C, NCH); to_tm(tm["khtm"], B_["kh"], T, C, NCH)
        b_ = bank()
        for c in range(NCH):
            mm(b_[0:C, c * C:(c + 1) * C], B_["kt"][:, c * C:(c + 1) * C], B_["qt"][:, c * C:(c + 1) * C], True, True, w=[b_])
        tt(attnT[0:C, 0:NCH, 0:C], b_[0:C, 0:NCH * C].rearrange("p (c t) -> p c t", t=C),
           M_["incl"].unsqueeze(1).broadcast_to([C, NCH, C]), ALU.mult, r=[b_, M_["incl"]], w=[attnT])
        bo = banks[0]
        for c in range(NCH):
            csl = slice(c * C, (c + 1) * C)
            sbf = S0bf[:, c, :] if is_s else Shbf
            sin = S0all[:, c, :] if is_s else Sh
            sout = Souts[:, c, :] if is_s else Sh
            oo = bo[0:C, c * 128:(c + 1) * 128]
            mm(oo, attnT[0:C, c, 0:C], tm["Vtm"][0:C, c, :], True, False, r=[attnT, tm["Vtm"]], w=[bo])
            mm(oo, B_["qt"][:, csl], sbf, False, True, r=[B_["qt"], S0bf if is_s else Shbf], w=[bo])
            b2 = bank()
            mm(b2[:, 0:128], tm["khtm"][0:C, c, :], tm["Vtm"][0:C, c, :], True, True, r=[tm["khtm"], tm["Vtm"]], w=[b2])
            stt(sout, sin, F_["Pc"][:, (c + 1) * C - 1:(c + 1) * C], b2[:, 0:128], ALU.mult, ALU.add,
                r=[S0all if is_s else Sh, F_["Pc"], b2], w=[Souts if is_s else Sh])
            if not is_s:
                cp(Shbf, Sh, eng="act")
        cp(Osb[0:C, 0:NCH, :], bo.rearrange("p (c f) -> p c f", f=128)[0:C], eng="act", r=[bo], w=[Osb])
        if is_s:
            dma("sp", hs[s0:s0 + 4, j].rearrange("s k v -> k s v"), Souts, r=[Souts], w=[hs])
        elif last:
            dma("sp", hp[j], Sh, r=[Sh], w=[hp])
        tt(Osq[0:C, 0:NCH, :], Osb[0:C, 0:NCH, :], Osb[0:C, 0:NCH, :], ALU.mult, eng="pool", r=[Osb], w=[Osq])
        P.add("dve", lambda e, C=C, NCH=NCH: e.reduce_sum(out=st_[0:C, 0:NCH], in_=Osq[0:C, 0:NCH, :], axis=AX.X), r=[Osq], w=[st_])
        rsqrt_(st_[0:C, 0:NCH], st_[0:C, 0:NCH], 1.0 / 128, RMS_EPS, stA[0:C, 0:NCH], stB[0:C, 0:NCH])
        tt(Osq[0:C, 0:NCH, :], Osb[0:C, 0:NCH, :], st_[0:C, 0:NCH].unsqueeze(2).broadcast_to([C, NCH, 128]), ALU.mult, r=[Osb, st_], w=[Osq])
        tt(tm["onb"][0:C, 0:NCH, :], Osq[0:C, 0:NCH, :], nwbc[0:C, :].unsqueeze(1).broadcast_to([C, NCH, 128]), ALU.mult, r=[Osq, nwbc], w=[tm["onb"]])
        b_ = bank(); bb = b_.bitcast(BF16)
        for c in range(NCH):
            tr(bb[:, c * C:(c + 1) * C], tm["onb"][0:C, c, :], identB[0:C, 0:C], r=[tm["onb"], identB], w=[b_])
        if gi >= 2:
            P.add("need", ("oTr", gi - 2))
        tt(oT[:, 0:T], bb[:, 0:T], F_["Ga"][:, 0:T], ALU.mult, r=[b_, F_["Ga"]], w=[oT])
        P.add("mark", ("oTw", gi))

    def stream(kind):
        gi = 0
        for j in range(NBLK):
            seg = lambda col0: w_in[:, col0 + j * 128:col0 + (j + 1) * 128].rearrange("(k p) c -> p k c", p=128)
            if kind == "H":
                for si, col0 in enumerate((0, 1024, 2048, 3072, 7456)):
                    dma("pool", wbH[:, :, si * 128:(si + 1) * 128], seg(col0), r=[w_in], w=[wbH])
                memset(Sh, 0.0, eng="dve"); memset(Shbf, 0.0, eng="dve")
            elif kind == "RF":
                for si, col0 in enumerate((4096, 5120, 6144, 8480)):
                    dma("pool", wbR[:, :, si * 128:(si + 1) * 128], seg(col0), r=[w_in], w=[wbR])
                memset(prevs[:, 3:6], 0.0, eng="dve", w=[prevs])
            else:
                memset(Sr, 0.0, eng="dve"); memset(Srbf, 0.0, eng="dve")
            for ti, (c0, T, C, NCH, is_s) in enumerate(tiles):
                if debug is not None and 'tsel' in debug and ti not in debug['tsel']:
                    continue
                last = (c0 + T == SEQ)
                if kind == "H":
                    hgrn_tile(j, ti, c0, T, C, NCH, is_s, last, wbH, oTs[gi % 2], gi)
                elif kind == "RF":
                    rwkv_front(j, ti, c0, T, C, NCH, is_s, last, wbR, gi)
                else:
                    rwkv_back(j, ti, c0, T, C, NCH, is_s, last, oTs[gi % 2], gi)
                gi += 1

    opsH = []; opsF = []; opsB = []
    P.sink = opsH; cur[0] = "H"; stream("H")
    P.sink = opsF; cur[0] = "RF"; stream("RF")
    P.sink = opsB; cur[0] = "RB"; stream("RB")
    P.sink = None; cur[0] = None
    _n0 = len(P.ops)
    if debug is not None:
        P.lat_same = debug.get('lat_same', P.lat_same); P.lat_cross = debug.get('lat_cross', P.lat_cross)
    P.merge([opsH, opsF, opsB])
    if debug is not None and debug.get("xdeps"):
        from collections import Counter
        cx = Counter()
        for i in range(_n0, len(P.ops)):
            o = P.ops[i]
            for d in o["deps"]:
                od = P.ops[d]
                if d >= _n0 and od["stream"] != o["stream"]:
                    shared = (set(o["rk"]) | set(o["wk"])) & (set(od["rk"]) | set(od["wk"]))
                    cx[(o["stream"], o["eng"], od["eng"], str(sorted(map(str, shared)))[:80])] += 1
        for k, v in cx.most_common(30):
            print("XDEP", v, k)
    dump("mT", mT.rearrange("p a b -> p (a b)"), [128, 8 * NT])
    for p4 in range(0, 27, 4):
        b_ = bank(); n4 = min(4, 27 - p4); wdt = min(n4 * 128, RC - p4 * 128)
        for q in range(n4):
            tr(b_[0:17, q * 128:(q + 1) * 128], lastc[:, p4 + q, :], identF, r=[lastc, identF], w=[b_])
        cp(srow[:, 0:n4 * 128], b_[0:17, 0:n4 * 128], r=[b_], w=[srow])
        dma("sp", shp[:, p4 * 128:p4 * 128 + wdt], srow[0:1, 0:wdt], r=[srow], w=[shp])
        dma("sp", shs[:, p4 * 128:p4 * 128 + wdt], srow[1:17, 0:wdt], r=[srow], w=[shs])
    if debug is not None and debug.get("_stop") == 2:
        n = P.emit()
        return nc, dbg, n
    print('SBUF phase2 total', SBTOT[0], 'S2', S2.tot)
    S2.close()
    P.barrier(dummy)
    S1 = Scope()
    xT = S1.sb("xT", [128, 8, NT]); h2T = S1.sb("h2T", [128, 8, NT], BF16)
    g2s = S1.sb("g2s", [128, 8, 64]); sh2s = S1.sb("sh2s", [128, 8, 64]); gt1s = S1.sb("gt1s", [128, 8, 64]); gt2s = S1.sb("gt2s", [128, 8, 64])
    mk_mods(gt1s, 2); mk_mods(sh2s, 3); mk_mods(g2s, 4, "n2"); mk_mods(gt2s, 5)
    nbuf = dict(sq=S1.sb("sq", [128, 8, 512], BF16), rstd=S1.sb("rstd", [128, 512]), tmp=S1.sb("ntmp", [128, 8, 512]),
                xtm=[S1.sb("xtm", [128, D]) for _ in range(2)], i=0)
    nbuf["r1"] = nbuf["tmp"][:, 1, :]; nbuf["r2"] = nbuf["tmp"][:, 0, :]
    wo = S1.sb("wo", [128, 8, 128], BF16); wus = [S1.sb("wu", [128, 8, 256], BF16) for _ in range(2)]; wds = [S1.sb("wd", [128, 2, D], BF16)] * 2
    aTs = [S1.sb("aT", [128, 2, 512], BF16) for _ in range(2)]; rTs = [S1.sb("rT", [128, 1, 512], BF16)] * 2; t64 = S1.sb("t64", [128, 64])
    rtmp = [nbuf["rstd"], nbuf["tmp"][:, 0, :]]; rti = [0]
    tiles34 = [(i * 512, 512, False) for i in range(4)] + [(SEQ, 64, True)]
    for ti, (c0, T, is_s) in enumerate(tiles34):
        load_xT(xT[:, :, c0:c0 + T], c0, T, ti, fkeys=True)

    def resid_add(f, c0, T, is_s, ps, gtp, gts, split=False):
        kx = [(xT, f)]
        if not is_s:
            if split:
                tq = rtmp[rti[0] % len(rtmp)]; rti[0] += 1
                aff(tq[:, 0:T], ps[:, 0:T], gtp[:, f:f + 1], 0.0, r=[ps, gtp], w=[tq])
                tt(xT[:, f, c0:c0 + T], xT[:, f, c0:c0 + T], tq[:, 0:T], ALU.add, eng="pool", r=kx + [tq], w=kx)
            else:
                stt(xT[:, f, c0:c0 + T], ps[:, 0:T], gtp[:, f:f + 1], xT[:, f, c0:c0 + T], ALU.mult, ALU.add, r=[ps, gtp] + kx, w=kx)
        else:
            tt(t64[:, 0:T], ps[:, 0:T], gts[:, f, :], ALU.mult, r=[ps, gts], w=[t64])
            tt(xT[:, f, c0:c0 + T], xT[:, f, c0:c0 + T], t64[:, 0:T], ALU.add, r=kx + [t64], w=kx)

    for f in range(8):
        dma("pool", wo, w_out[:, f * 128:(f + 1) * 128].rearrange("(k p) c -> p k c", p=128), r=[w_out], w=[wo])
        for ti, (c0, T, is_s) in enumerate(tiles34):
            b_ = bank()
            for k in range(8):
                mm(b_[:, 0:T], wo[:, k, :], mT[:, k, c0:c0 + T], k == 0, k == 7, r=[wo, mT], w=[b_])
            resid_add(f, c0, T, is_s, b_, gt1p, gt1s)
    for ti, (c0, T, is_s) in enumerate(tiles34):
        norm_to(h2T[:, :, c0:c0 + T], xT[:, :, c0:c0 + T], T, g2p, sh2p, g2s, sh2s, is_s, 0, fkeys=True)
    steps = [(g, ti) for g in range(16) for ti in range(len(tiles34))]
    upb = [0]; dnb = [0]

    def load_wu(g):
        wu_ = wus[g % 2]
        dma("pool", wu_, w_up[:, g * 256:(g + 1) * 256].rearrange("(k p) c -> p k c", p=128), r=[w_up], w=[wu_])

    def load_wd(g):
        wd_ = wds[0]
        dma("pool", wd_, w_down[g * 256:(g + 1) * 256, :].rearrange("(q p) c -> p q c", p=128), r=[w_down], w=[wd_])

    def up(k):
        g, ti = steps[k]; c0, T, is_s = tiles34[ti]
        if ti == 0 and g + 1 < 16:
            load_wu(g + 1)
        wu_ = wus[g % 2]; aT_ = aTs[k % 2]; rT_ = rTs[k % 2]
        for q in range(2):
            b_ = banks[upb[0] % 4]; upb[0] += 1
            for kk_ in range(8):
                mm(b_[:, 0:T], wu_[:, kk_, q * 128:(q + 1) * 128], h2T[:, kk_, c0:c0 + T], kk_ == 0, kk_ == 7, r=[wu_, (h2T, 0)], w=[b_])
            act(rT_[:, 0, 0:T], b_[:, 0:T], AF.Relu, r=[b_], w=[rT_])
            tt(aT_[:, q, 0:T], rT_[:, 0, 0:T], rT_[:, 0, 0:T], ALU.mult, eng=("pool" if q == 1 else "dve"), r=[rT_], w=[aT_])

    def down(k):
        g, ti = steps[k]; c0, T, is_s = tiles34[ti]
        wd_ = wds[g % 2]; aT_ = aTs[k % 2]
        for f in range(8):
            b_ = banks[4 + dnb[0] % 4]; dnb[0] += 1
            for q in range(2):
                mm(b_[:, 0:T], wd_[:, q, f * 128:(f + 1) * 128], aT_[:, q, 0:T], q == 0, q == 1, r=[wd_, aT_], w=[b_])
            resid_add(f, c0, T, is_s, b_, gt2p, gt2s, split=(f >= 6))

    load_wu(0); load_wd(0)
    up(0)
    for k in range(len(steps)):
        if k + 1 < len(steps):
            up(k + 1)
        down(k)
        g_, ti_ = steps[k]
        if ti_ == len(tiles34) - 1 and g_ + 1 < 16:
            load_wd(g_ + 1)
    fnp = pv[:, rows["fn"]:rows["fn"] + 8]
    for ti, (c0, T, is_s) in enumerate(tiles34):
        sq = nbuf["sq"]; rstd = nbuf["rstd"]; tmp = nbuf["tmp"]
        xv_ = xT[:, :, c0:c0 + T]
        xk8 = [(xT, f_) for f_ in range(8)]
        tt(sq[:, :, 0:T], xv_, xv_, ALU.mult, r=xk8, w=[sq])
        b_ = bank()
        for fc in range(8):
            mm(b_[:, 0:T], onesB, sq[:, fc, 0:T], fc == 0, fc == 7, w=[b_])
        ts(rstd[:, 0:T], b_[:, 0:T], 1.0 / D, RMS_EPS, ALU.mult, ALU.add, r=[b_], w=[rstd])
        r1 = nbuf["r1"]; r2 = nbuf["r2"]
        cp(r1[:, 0:T], rstd[:, 0:T], eng="act", r=[rstd], w=[r1])
        rsqrt_dve(rstd[:, 0:T], r1[:, 0:T], r2[:, 0:T], [rstd, r1, r2])
        tt(tmp[:, :, 0:T], xv_, rstd[:, 0:T].unsqueeze(1).broadcast_to([128, 8, T]), ALU.mult, r=xk8 + [rstd], w=[tmp])
        tt(tmp[:, :, 0:T], tmp[:, :, 0:T], fnp.unsqueeze(2).broadcast_to([128, 8, T]), ALU.mult, r=[tmp, pv], w=[tmp])
        for sub in range(0, T, 128):
            n = min(128, T - sub)
            ytm = nbuf["xtm"][nbuf["i"] % 2]; nbuf["i"] += 1
            for half in range(2):
                b_ = bank()
                for f4 in range(4):
                    fc = half * 4 + f4
                    tr(b_[0:n, f4 * 128:(f4 + 1) * 128], tmp[:, fc, sub:sub + n], identF, r=[tmp, identF], w=[b_])
                cp(ytm[0:n, half * 512:(half + 1) * 512], b_[0:n, :], eng="act", r=[b_], w=[ytm])
            dst = yp[c0 + sub:c0 + sub + n, :] if not is_s else ys[sub:sub + n, :]
            dma("sp", dst, ytm[0:n, :], r=[ytm], w=[yp if not is_s else ys])
    n = P.emit()
    print('SBUF bytes/partition: G', G.tot, 'max', SBMAX[0])
    S1.close(); G.close()
    return nc, dbg, n


_CACHE = {}


def _core_inputs(inp, c):
    f = lambda a: np.ascontiguousarray(a, dtype=np.float32)
    m = {"xp": inp["x_prompt"][c], "xs": inp["x_sample"][16 * c:16 * c + 16].reshape(64, D),
         "sh": inp["state_hgrn"][0, 16 * c:16 * c + 16], "sr": inp["state_rwkv"][0, 16 * c:16 * c + 16],
         "ss": inp["state_shift"][0, 16 * c:16 * c + 16], "cp": inp["c_prompt"][c:c + 1], "cs": inp["c_sample"][16 * c:16 * c + 16]}
    for k in ("norm1_w", "norm2_w", "rwkv_w0", "rwkv_a0", "rwkv_k_k", "rwkv_k_a", "rwkv_lnx_w", "rwkv_lnx_b", "rwkv_r_k", "final_norm_w"):
        m[k] = inp[k].reshape(8, 128)
    m["ada_w"] = inp["ada_w"][0]; m["ada_b"] = inp["ada_b"].reshape(48, 128); m["w_in"] = inp["w_in"][0]
    m["lb_logits"] = inp["lb_logits"].reshape(16, 128); m["hgrn_norm_w"] = inp["hgrn_norm_w"].reshape(1, 128)
    m["rwkv_mu"] = inp["rwkv_mu"].reshape(1, RC); m["rwkv_w2"] = inp["rwkv_w2"][0]; m["rwkv_a2"] = inp["rwkv_a2"][0]
    m["rwkv_g2"] = inp["rwkv_g2"][0]; m["w_out"] = inp["w_out"][0]; m["w_up"] = inp["w_up"][0]; m["w_down"] = inp["w_down"][0]
    return {k: f(v) for k, v in m.items()}


def kernel(**inputs):
    inp = {k: np.asarray(v) for k, v in inputs.items()}
    if "nc" not in _CACHE:
        _CACHE["nc"] = build()[0]
    nc = _CACHE["nc"]
    in_maps = [_core_inputs(inp, c) for c in range(8)]
    res = run_bass_kernel_spmd(nc, in_maps, core_ids=list(range(8)))
    R = res.results
    g = lambda n: [np.asarray(R[c][n], dtype=np.float32) for c in range(8)]
    y_prompt = np.stack(g("yp"), 0)
    y_sample = np.concatenate([a.reshape(16, 4, D) for a in g("ys")], 0)
    hgrn_p = np.stack(g("hp"), 0)[None]
    rwkv_p = np.stack(g("rp"), 0)[None]
    shift_p = np.concatenate(g("shp"), 0)[None]
    hgrn_s = np.concatenate(g("hs"), 0)[None]
    rwkv_s = np.concatenate(g("rs"), 0)[None]
    shift_s = np.concatenate(g("shs"), 0)[None]
    return (y_prompt, y_sample, hgrn_p, rwkv_p, shift_p, hgrn_s, rwkv_s, shift_s)
```

```python
import contextlib
import numpy as np
import concourse.bass as bass
import concourse.mybir as mybir
from concourse.bass_utils import run_bass_kernel_spmd

F32 = mybir.dt.float32
BF16 = mybir.dt.bfloat16
AF = mybir.ActivationFunctionType
ALU = mybir.AluOpType
AX = mybir.AxisListType

D = 1024
SEQ = 2048
NSEQ_S = 16
TS = 4
NT = SEQ + NSEQ_S * TS
INC = 9504
RC = 3360
NDMASEM = 12
PE_FIX = 0.03
PE_RATE = 3200.0
SAME_ENG_RELAX = ()
RMS_EPS = 1e-6
LNX_EPS = 64e-5
C1 = -0.5 * float(np.exp(-0.5))


class Prog:
    def __init__(self, nc):
        self.nc = nc
        self.ops = []
        self.last_w = {}
        self.readers = {}
        self.bar = False
        self.sink = None
        self.lat_same = 0.2
        self.lat_cross = 0.3

    def peek(self, r, w):
        deps = set()
        rk = [self.key(x) for x in r]; wk = [self.key(x) for x in w]
        if self.bar:
            rk.append("__bar")
        for k in rk:
            if k in self.last_w:
                deps.add(self.last_w[k])
        for k in wk:
            if k in self.last_w:
                deps.add(self.last_w[k])
            deps |= self.readers.get(k, set())
        return deps

    @staticmethod
    def dur(eng, dma, n):
        if dma:
            return 0.1
        if eng == "pe":
            return PE_FIX + n / PE_RATE
        if eng == "dve":
            return 0.10 + n / 960.0
        if eng == "act":
            return 0.25 + n / 1200.0
        if eng == "pool":
            return 0.30 + n / 400.0
        return 0.1

    def merge(self, lists, lead=0):
        L = list(lists); K = len(L)
        done = set(); pos = [0] * K
        end = {}; free = {}
        tot = [max(1, len(x)) for x in L]

        def est_start(x):
            eng, fn, r, w, dma, n = x
            deps = self.peek(r, w)
            t = free.get(eng, 0.0)
            for d in deps:
                if d in end:
                    lat = self.lat_same if self.ops[d]["eng"] == eng else self.lat_cross
                    if self.ops[d]["eng"] == "pe" and eng == "pe":
                        lat = 0.0
                    t = max(t, end[d] + lat)
            return t

        while any(pos[i] < len(L[i]) for i in range(K)):
            progressed = False
            for si in range(K):
                while pos[si] < len(L[si]) and (L[si][pos[si]][0] == "mark" or (L[si][pos[si]][0] == "need" and L[si][pos[si]][1] in done)):
                    if L[si][pos[si]][0] == "mark":
                        done.add(L[si][pos[si]][1])
                    pos[si] += 1; progressed = True
            cand = []
            for si in range(K):
                if pos[si] >= len(L[si]):
                    continue
                x = L[si][pos[si]]
                if x[0] == "need":
                    continue
                cand.append((est_start(x), pos[si] / tot[si], si))
            if not cand:
                if progressed:
                    continue
                if all(pos[i] >= len(L[i]) for i in range(K)):
                    break
                raise RuntimeError("merge deadlock")
            t, _, si = min(cand)
            x = L[si][pos[si]]; pos[si] += 1
            eng, fn, r, w, dma, n = x
            idx = self.add(eng, fn, r, w, dma, n)
            d_ = self.dur(eng, dma, n)
            free[eng] = t + d_
            end[idx] = t + d_ + (2.5 if dma else 0.0)

    @staticmethod
    def key(x):
        if isinstance(x, tuple):
            if isinstance(x[0], str):
                return x
            return (x[0].tensor.name, x[1])
        if isinstance(x, str):
            return x
        return x.tensor.name

    def add(self, eng, fn, r=(), w=(), dma=False, n=256):
        if self.sink is not None:
            if eng in ("mark", "need"):
                self.sink.append((eng, fn))
            else:
                self.sink.append((eng, fn, list(r), list(w), dma, n))
            return None
        deps = set()
        rk = [self.key(x) for x in r]
        wk = [self.key(x) for x in w]
        if self.bar:
            rk.append("__bar")
        for k in rk:
            if k in self.last_w:
                deps.add(self.last_w[k])
        for k in wk:
            if k in self.last_w:
                deps.add(self.last_w[k])
            deps |= self.readers.get(k, set())
        idx = len(self.ops)
        self.ops.append(dict(eng=eng, fn=fn, deps=deps, dma=dma, users=0, stream=getattr(self, "tag", None), rk=rk, wk=wk))
        for k in rk:
            self.readers.setdefault(k, set()).add(idx)
        for k in wk:
            self.last_w[k] = idx
            self.readers[k] = set()
        return idx

    def barrier(self, dummy):
        allk = set(self.last_w.keys()) | set(self.readers.keys())
        allk.discard("__bar")
        self.bar = False
        self.add("dve", lambda e: e.memset(dummy, 0.0), r=(), w=list(allk) + ["__bar"])
        self.last_w = {"__bar": self.last_w["__bar"]}
        self.readers = {}
        self.bar = True

    def emit(self):
        nc = self.nc
        engs = {"pe": nc.tensor, "act": nc.scalar, "dve": nc.vector, "pool": nc.gpsimd, "sp": nc.sync}
        ops = self.ops

        seqn = {}
        for i_, o_ in enumerate(ops):
            if not o_["dma"]:
                seqn[o_["eng"]] = seqn.get(o_["eng"], 0) + 1
                o_["seq"] = seqn[o_["eng"]]

        def skip(d, o):
            if ops[d]["dma"] or o["dma"]:
                return False
            if ops[d]["eng"] != o["eng"]:
                return False
            if o["eng"] == "pe":
                return True
            if o["eng"] in SAME_ENG_RELAX and o["seq"] - ops[d]["seq"] >= 2:
                return True
            return False

        for o in ops:
            for d in o["deps"]:
                if not skip(d, o):
                    ops[d]["users"] += 1
        with contextlib.ExitStack() as st:
            csem = {e: st.enter_context(nc.semaphore("c_" + e)) for e in engs}
            dsem = {e: [st.enter_context(nc.semaphore(f"d_{e}{i}")) for i in range(NDMASEM)]
                    for e in ("act", "pool", "sp")}
            ccnt = {e: 0 for e in engs}
            dcnt = {e: [0] * NDMASEM for e in dsem}
            drr = {e: 0 for e in dsem}
            waited = {e: {} for e in engs}
            sig = {}

            def wait(e, sem, val):
                k = id(sem)
                if val <= 0 or waited[e].get(k, 0) >= val:
                    return
                engs[e].wait_ge(sem, val)
                waited[e][k] = val

            for i, o in enumerate(ops):
                e = o["eng"]
                for d in sorted(o["deps"]):
                    if d not in sig or skip(d, o):
                        continue
                    s, v = sig[d]
                    wait(e, s, v)
                if o["dma"]:
                    j = drr[e]
                    drr[e] = (j + 1) % NDMASEM
                    s = dsem[e][j]
                    wait(e, s, dcnt[e][j])
                    ins = o["fn"](engs[e])
                    dcnt[e][j] += 16
                    ins.then_inc(s, 16)
                    sig[i] = (s, dcnt[e][j])
                else:
                    ins = o["fn"](engs[e])
                    if o["users"] > 0:
                        ccnt[e] += 1
                        ins.then_inc(csem[e], 1)
                        sig[i] = (csem[e], ccnt[e])
            for e in dsem:
                for j in range(NDMASEM):
                    if dcnt[e][j] > 0:
                        nc.sync.wait_ge(dsem[e][j], dcnt[e][j])
        return len(ops)


def build(debug=None):
    nc = bass.Bass("TRN2", target_bir_lowering=False)
    P = Prog(nc)
    din = lambda n, s: nc.dram_tensor(n, list(s), F32, kind="ExternalInput").ap()
    dout = lambda n, s: nc.dram_tensor(n, list(s), F32, kind="ExternalOutput").ap()
    xp = din("xp", [SEQ, D]); xs = din("xs", [64, D])
    sh_in = din("sh", [16, 8, 128, 128]); sr_in = din("sr", [16, 16, 64, 64]); ss_in = din("ss", [16, RC])
    cpr = din("cp", [1, D]); csm = din("cs", [16, D])
    norm1_w = din("norm1_w", [8, 128]); norm2_w = din("norm2_w", [8, 128])
    ada_w = din("ada_w", [D, 6 * D]); ada_b = din("ada_b", [48, 128])
    w_in = din("w_in", [D, INC]); lb_logits = din("lb_logits", [16, 128])
    hgrn_norm_w = din("hgrn_norm_w", [1, 128]); mu = din("rwkv_mu", [1, RC])
    w0 = din("rwkv_w0", [8, 128]); w2 = din("rwkv_w2", [64, D]); a0 = din("rwkv_a0", [8, 128])
    a2 = din("rwkv_a2", [64, D]); g2 = din("rwkv_g2", [160, D]); k_k = din("rwkv_k_k", [8, 128])
    k_a = din("rwkv_k_a", [8, 128]); r_k = din("rwkv_r_k", [8, 128]); lnx_w = din("rwkv_lnx_w", [8, 128])
    lnx_b = din("rwkv_lnx_b", [8, 128]); w_out = din("w_out", [D, D]); w_up = din("w_up", [D, 4 * D])
    w_down = din("w_down", [4 * D, D]); fnw = din("final_norm_w", [8, 128])
    yp = dout("yp", [SEQ, D]); ys = dout("ys", [64, D])
    hp = dout("hp", [8, 128, 128]); rp = dout("rp", [16, 64, 64]); shp = dout("shp", [1, RC])
    hs = dout("hs", [16, 8, 128, 128]); rs = dout("rs", [16, 16, 64, 64]); shs = dout("shs", [16, RC])
    dbg = {}

    cnt = [0]

    def uname(n):
        cnt[0] += 1
        return f"{n}_{cnt[0]}"

    SBTOT = [0]; SBMAX = [0]

    class Scope:
        def __init__(self):
            self.st = contextlib.ExitStack()

        def sb(self, name, shape, dt=F32):
            nb = int(np.prod(shape[1:])) * (4 if dt == F32 else 2)
            SBTOT[0] += nb; self.tot = getattr(self, "tot", 0) + nb
            SBMAX[0] = max(SBMAX[0], SBTOT[0])
            return self.st.enter_context(nc.sbuf_tensor(uname(name), list(shape), dt)).ap()

        def close(self):
            SBTOT[0] -= getattr(self, "tot", 0)
            self.st.close()

    G = Scope()
    banks = [nc.alloc_psum_tensor(f"bank{i}", [128, 512], F32).ap() for i in range(8)]
    bk = [0]

    pools = {None: list(range(8)), "H": [1, 2], "RF": [3, 4], "RB": [6, 7]}
    cur = [None]
    bkp = {"H": 0, "RF": 0, "RB": 0}

    def bank():
        if cur[0] is None:
            b = banks[bk[0] % 8]
            bk[0] += 1
            return b
        pl = pools[cur[0]]
        b = banks[pl[bkp[cur[0]] % len(pl)]]
        bkp[cur[0]] += 1
        return b

    def dump(name, ap, shape, keys=None):
        if debug is None or (name not in debug and "all" not in debug):
            return
        d = nc.dram_tensor("dbg_" + name, list(shape), F32, kind="ExternalOutput").ap()
        dbg[name] = d
        q = "pool" if ap.dtype != F32 else "sp"
        rk = [ap] + [(ap, t_) for t_ in range(16)] if keys is None else keys
        P.add(q, lambda e: e.dma_start(out=d, in_=ap), r=rk, w=[d], dma=True)

    def dma(q, out, in_, r=(), w=()):
        P.add(q, lambda e: e.dma_start(out=out, in_=in_), r=list(r) or [in_], w=list(w) or [out], dma=True)

    def act(out, in_, func, bias=0.0, scale=1.0, r=(), w=(), accum=None):
        rr = list(r) or [in_]
        if not isinstance(bias, float):
            rr.append(bias)
        if not isinstance(scale, float):
            rr.append(scale)
        ww = list(w) or [out]
        if accum is not None:
            ww.append(accum)
        if accum is not None:
            P.add("act", lambda e: e.activation(out=out, in_=in_, func=func, bias=bias, scale=scale, accum_out=accum), r=rr, w=ww, n=out.free_size())
        else:
            P.add("act", lambda e: e.activation(out=out, in_=in_, func=func, bias=bias, scale=scale), r=rr, w=ww, n=out.free_size())

    def tt(out, in0, in1, op, eng="dve", r=(), w=()):
        P.add(eng, lambda e: e.tensor_tensor(out=out, in0=in0, in1=in1, op=op), r=list(r) or [in0, in1], w=list(w) or [out], n=out.free_size())

    def ts(out, in0, s1, s2, op0, op1=None, eng="dve", r=(), w=()):
        rr = list(r) or [in0]
        for s in (s1, s2):
            if s is not None and not isinstance(s, float):
                rr.append(s)
        if op1 is None:
            P.add(eng, lambda e: e.tensor_scalar(out=out, in0=in0, scalar1=s1, scalar2=None, op0=op0), r=rr, w=list(w) or [out], n=out.free_size())
        else:
            P.add(eng, lambda e: e.tensor_scalar(out=out, in0=in0, scalar1=s1, scalar2=s2, op0=op0, op1=op1), r=rr, w=list(w) or [out], n=out.free_size())

    def aff(out, in_, scale, bias, r=(), w=()):
        act(out, in_, AF.Identity, bias=bias, scale=scale, r=r, w=w)

    def stt(out, in0, scalar, in1, op0, op1, r=(), w=()):
        rr = list(r) or [in0, in1]
        if not isinstance(scalar, float):
            rr.append(scalar)
        P.add("dve", lambda e: e.scalar_tensor_tensor(out=out, in0=in0, scalar=scalar, in1=in1, op0=op0, op1=op1), r=rr, w=list(w) or [out], n=out.free_size())

    def cp(out, in_, eng="dve", r=(), w=()):
        if eng == "act":
            act(out, in_, AF.Copy, r=r, w=w)
        else:
            P.add(eng, lambda e: e.tensor_copy(out=out, in_=in_), r=list(r) or [in_], w=list(w) or [out], n=out.free_size())

    def memset(ap, val, eng="pool", w=()):
        P.add(eng, lambda e: e.memset(ap, val), w=list(w) or [ap], n=ap.free_size())

    def mm(out, lhsT, rhs, start, stop, r=(), w=()):
        P.add("pe", lambda e: e.matmul(out, lhsT, rhs, start=start, stop=stop), r=list(r) or [lhsT, rhs], w=list(w) or [out], n=rhs.free_size() + 60)

    def tr(out, in_, ident, r=(), w=()):
        P.add("pe", lambda e: e.transpose(out=out, in_=in_, identity=ident), r=list(r) or [in_, ident], w=list(w) or [out], n=ident.free_size() + 60)

    I32 = mybir.dt.int32

    def rsqrt_dve(y, x, t1, keys):
        P.add("dve", lambda e: e.tensor_scalar(out=y.bitcast(I32), in0=x.bitcast(I32), scalar1=-0.5, scalar2=1597463007.0,
                                               op0=ALU.mult, op1=ALU.add), r=keys, w=keys)
        for _ in range(2):
            tt(t1, y, y, ALU.mult, r=keys, w=keys)
            stt(t1, t1, -0.5, x, ALU.mult, ALU.mult, r=keys, w=keys)
            stt(y, t1, 1.5, y, ALU.add, ALU.mult, r=keys, w=keys)

    def rsqrt_(out, in_, scale, eps, tA, tB):
        ts(tA, in_, scale, eps, ALU.mult, ALU.add)
        rsqrt_dve(out, tA, tB, [out, tA, tB])

    ones128 = G.sb("ones128", [128, 128]); identF = G.sb("identF", [128, 128]); identB = G.sb("identB", [128, 128], BF16)
    memset(ones128, 1.0)
    P.add("pool", lambda e: e.affine_select(out=identF, in_=ones128, pattern=[[1, 128]], compare_op=ALU.is_equal,
                                           fill=0.0, base=0, channel_multiplier=-1), r=[ones128], w=[identF])
    cp(identB, identF)
    onesB = G.sb("onesB", [128, 128], BF16)
    cp(onesB, ones128)
    blkones = G.sb("blkones", [128, 128], BF16)
    memset(blkones, 0.0)
    memset(blkones[0:64, 0:64], 1.0, w=[blkones]); memset(blkones[64:128, 64:128], 1.0, w=[blkones])
    blkneg = G.sb("blkneg", [128, 128])
    memset(blkneg, 0.0)
    memset(blkneg[0:64, 0:64], -1.0, w=[blkneg]); memset(blkneg[64:128, 64:128], -1.0, w=[blkneg])
    blkpos = G.sb("blkpos", [128, 128])
    memset(blkpos, 0.0)
    memset(blkpos[0:64, 0:64], 1.0, w=[blkpos]); memset(blkpos[64:128, 64:128], 1.0, w=[blkpos])
    dummy = G.sb("dummy", [128, 1])
    masks = {}
    for C in (64, 4):
        mi = G.sb(f"mi{C}", [C, C]); ms = G.sb(f"ms{C}", [C, C]); mls = G.sb(f"mls{C}", [C, C])
        P.add("pool", lambda e, mi=mi, C=C: e.affine_select(out=mi, in_=ones128[0:C, 0:C], pattern=[[1, C]], compare_op=ALU.is_ge,
                                                         fill=0.0, base=0, channel_multiplier=-1), r=[ones128], w=[mi])
        P.add("pool", lambda e, ms=ms, C=C: e.affine_select(out=ms, in_=ones128[0:C, 0:C], pattern=[[1, C]], compare_op=ALU.is_gt,
                                                         fill=0.0, base=0, channel_multiplier=-1), r=[ones128], w=[ms])
        P.add("pool", lambda e, mls=mls, C=C: e.affine_select(out=mls, in_=ones128[0:C, 0:C], pattern=[[-1, C]], compare_op=ALU.is_gt,
                                                           fill=0.0, base=0, channel_multiplier=1), r=[ones128], w=[mls])
        msn = G.sb(f"msn{C}", [C, C]); mlsn = G.sb(f"mlsn{C}", [C, C])
        ts(msn, ms, -1.0, None, ALU.mult); ts(mlsn, mls, -1.0, None, ALU.mult)
        masks[C] = dict(incl=mi, strict=ms, nstrict=msn, nlstrict=mlsn)

    stgA = G.sb("stgA", [128, 128]); stgB = G.sb("stgB", [48, 128])
    memset(stgA, 0.0)
    rows = {}
    r0 = 0
    for nm, src, n in (("n1", norm1_w, 8), ("n2", norm2_w, 8), ("lb", lb_logits, 16), ("w0", w0, 8), ("a0", a0, 8),
                       ("kk", k_k, 8), ("ka", k_a, 8), ("rk", r_k, 8), ("lw", lnx_w, 8), ("lbb", lnx_b, 8), ("fn", fnw, 8)):
        dma("sp", stgA[r0:r0 + n, :], src, w=[stgA]); rows[nm] = r0; r0 += n
    dma("sp", stgA[r0:r0 + 26, :], mu[:, 0:3328].rearrange("o (n p) -> (o n) p", p=128), w=[stgA]); rows["mu"] = r0; r0 += 26
    dma("sp", stgA[r0:r0 + 1, 0:32], mu[:, 3328:3360], w=[stgA]); rows["mu2"] = r0; r0 += 1
    assert r0 <= 128
    dma("sp", stgB, ada_b)
    pv = G.sb("pv", [128, 128]); pvB = G.sb("pvB", [128, 48])
    b_ = bank(); tr(b_[:, 0:128], stgA, identF); cp(pv, b_[:, 0:128])
    b_ = bank(); tr(b_[:, 0:48], stgB, identF[0:48, 0:48]); cp(pvB, b_[:, 0:48])
    col = lambda nm, i=0: pv[:, rows[nm] + i: rows[nm] + i + 1]
    pv2 = G.sb("pv2", [128, 128]); ts(pv2, pv, -1.0, 1.0, ALU.mult, ALU.add)
    col2 = lambda nm, i=0: pv2[:, rows[nm] + i: rows[nm] + i + 1]
    hs1 = G.sb("hs1", [128, 8]); hs2 = G.sb("hs2", [128, 8]); hs1n = G.sb("hs1n", [128, 8]); hs2c = G.sb("hs2c", [128, 8])
    tt(hs1, pv[:, rows["lb"]:rows["lb"] + 8], pv[:, rows["lb"] + 8:rows["lb"] + 16], ALU.subtract)
    act(hs2, hs1, AF.Tanh, scale=0.5)
    ts(hs1, hs2, -0.25, 0.25, ALU.mult, ALU.add)
    ts(hs2, hs2, 0.25, 0.75, ALU.mult, ALU.add)
    ts(hs1n, hs1, -1.0, None, ALU.mult)
    ts(hs2c, hs2, -1.0, 1.0, ALU.mult, ALU.add)
    hw0 = G.sb("hw0", [128, 8]); ha0 = G.sb("ha0", [128, 8]); omka = G.sb("omka", [128, 8])
    ts(hw0, pv[:, rows["w0"]:rows["w0"] + 8], 0.5, None, ALU.mult)
    ts(ha0, pv[:, rows["a0"]:rows["a0"] + 8], 0.5, None, ALU.mult)
    ts(omka, pv[:, rows["ka"]:rows["ka"] + 8], -1.0, 1.0, ALU.mult, ALU.add)
    nwbc = G.sb("nwbc", [64, 128]); dma("sp", nwbc, hgrn_norm_w.partition_broadcast(64))

    modT = G.sb("modT", [128, 48, 17])
    g1p = G.sb("g1p", [128, 8]); g2p = G.sb("g2p", [128, 8])
    sh1p = G.sb("sh1p", [128, 8]); sh2p = G.sb("sh2p", [128, 8]); gt1p = G.sb("gt1p", [128, 8]); gt2p = G.sb("gt2p", [128, 8])
    mT = G.sb("mT", [128, 8, NT], BF16)
    S2 = Scope()
    hT = S2.sb("hT", [128, 8, NT], BF16)
    S0 = Scope()
    c17 = S0.sb("c17", [17, D]); s17 = S0.sb("s17", [17, D]); s17b = S0.sb("s17b", [17, D], BF16); scT = S0.sb("scT", [128, 8, 32], BF16)
    dma("sp", c17[0:1, :], cpr, w=[c17]); dma("sp", c17[1:17, :], csm, w=[c17])
    act(s17, c17, AF.Tanh, scale=0.5)
    ts(s17, s17, 0.5, 0.5, ALU.mult, ALU.add)
    tt(s17b, s17, c17, ALU.mult)
    b_ = bank(); bb = b_.bitcast(BF16)
    for k in range(8):
        tr(bb[:, k * 32:k * 32 + 17], s17b[:, k * 128:(k + 1) * 128], identB[0:17, 0:17], w=[b_])
    cp(scT[:, :, 0:17], bb[:, 0:256].rearrange("p (k c) -> p k c", c=32)[:, :, 0:17], r=[b_])
    awb = [S0.sb("awb", [128, 8, 512], BF16) for _ in range(2)]
    def ada_group(g):
            wt = awb[g % 2]
            dma("pool", wt, ada_w[:, g * 512:(g + 1) * 512].rearrange("(k p) c -> p k c", p=128), r=[ada_w], w=[wt])
            b_ = bank()
            for m in range(4):
                for k in range(8):
                    mm(b_[:, m * 32:m * 32 + 17], wt[:, k, m * 128:(m + 1) * 128], scT[:, k, 0:17], k == 0, k == 7, w=[b_])
            for m in range(4):
                ts(modT[:, g * 4 + m, :], b_[:, m * 32:m * 32 + 17], pvB[:, g * 4 + m:g * 4 + m + 1], None, ALU.add, r=[b_, pvB], w=[modT])

    def modp(part):
        return modT[:, part * 8:(part + 1) * 8, 0]
    def mods(part):
        return modT[:, part * 8:(part + 1) * 8, 1:17].unsqueeze(3).broadcast_to([128, 8, 16, 4])
    def mods3(part):
        return modT[:, part * 8:(part + 1) * 8, 1:17]

    def mk_mods(dst, part, nm=None):
        for t4 in range(4):
            if nm is None:
                cp(dst[:, :, t4::4], mods3(part), r=[modT], w=[dst])
            else:
                stt(dst[:, :, t4::4], mods3(part), 1.0, pv[:, rows[nm]:rows[nm] + 8].unsqueeze(2).broadcast_to([128, 8, 16]),
                    ALU.add, ALU.mult, r=[modT, pv], w=[dst])

    tiles = [(i * 256, 256, 64, 4, False) for i in range(8)] + [(SEQ + 16 * i, 16, 4, 4, True) for i in range(4)]

    def norm_to(hT_dst, xT_t, T, gp, shp_, gs, shs_, is_s, ti, fkeys=False):
        xk_ = [(xT_t, f_) for f_ in range(8)] if fkeys else [xT_t]
        sq = nbuf["sq"]
        tt(sq[:, :, 0:T], xT_t[:, :, 0:T], xT_t[:, :, 0:T], ALU.mult, r=xk_, w=[sq])
        b_ = bank()
        for fc in range(8):
            mm(b_[:, 0:T], onesB, sq[:, fc, 0:T], fc == 0, fc == 7, w=[b_])
        rstd = nbuf["rstd"]
        ts(rstd[:, 0:T], b_[:, 0:T], 1.0 / D, RMS_EPS, ALU.mult, ALU.add, r=[b_], w=[rstd])
        r1 = nbuf["r1"]; r2 = nbuf["r2"]
        cp(r1[:, 0:T], rstd[:, 0:T], eng="act", r=[rstd], w=[r1])
        rsqrt_dve(rstd[:, 0:T], r1[:, 0:T], r2[:, 0:T], [rstd, r1, r2])
        tmp = nbuf["tmp"]
        tt(tmp[:, :, 0:T], xT_t[:, :, 0:T], rstd[:, 0:T].unsqueeze(1).broadcast_to([128, 8, T]), ALU.mult, r=xk_ + [rstd], w=[tmp])
        if not is_s:
            for fc in range(8):
                ts(hT_dst[:, fc, :], tmp[:, fc, 0:T], gp[:, fc:fc + 1], shp_[:, fc:fc + 1], ALU.mult, ALU.add,
                   r=[tmp, gp, shp_], w=[(hT_dst, ti)])
        else:
            tt(tmp[:, :, 0:T], tmp[:, :, 0:T], gs, ALU.mult, r=[tmp, gs], w=[tmp])
            tt(hT_dst, tmp[:, :, 0:T], shs_, ALU.add, r=[tmp, shs_], w=[(hT_dst, ti)])

    def load_xT(xT_t, c0, T, ti, fkeys=False):
        for sub in range(0, T, 128):
            n = min(128, T - sub)
            xtm = nbuf["xtm"][nbuf["i"] % 2]; nbuf["i"] += 1
            src = xp[c0 + sub:c0 + sub + n, :] if c0 < SEQ else xs[c0 - SEQ + sub:c0 - SEQ + sub + n, :]
            dma("sp", xtm[0:n, :], src, w=[xtm])
            for half in range(2):
                b_ = bank()
                for f4 in range(4):
                    fc = half * 4 + f4
                    tr(b_[:, f4 * 128:f4 * 128 + n], xtm[0:n, fc * 128:(fc + 1) * 128], identF[0:n, 0:n], w=[b_])
                cp(xT_t[:, half * 4:half * 4 + 4, sub:sub + n], b_.rearrange("p (f t) -> p f t", t=128)[:, :, 0:n], eng="act", r=[b_],
                   w=([(xT_t, f_) for f_ in range(half * 4, half * 4 + 4)] if fkeys else [xT_t]))

    for g in range(4):
        ada_group(g)
    stt(g1p, modp(1), 1.0, pv[:, rows["n1"]:rows["n1"] + 8], ALU.add, ALU.mult, r=[modT, pv])
    cp(sh1p, modp(0), r=[modT])
    S1 = Scope()
    g1s = S1.sb("g1s", [128, 8, 64]); sh1s = S1.sb("sh1s", [128, 8, 64])
    mk_mods(sh1s, 0); mk_mods(g1s, 1, "n1")
    nbuf = dict(sq=S1.sb("sq", [128, 8, 512], BF16), rstd=S1.sb("rstd", [128, 512]), tmp=S1.sb("ntmp", [128, 8, 512]), r1=S1.sb("r1", [128, 512]), r2=S1.sb("r2", [128, 512]),
                xtm=[S1.sb("xtm", [128, D]) for _ in range(2)], i=0)
    xT_ts = [S1.sb("xT_t", [128, 8, 512]) for _ in range(2)]
    gnext = 4
    for ti, (c0, T, C, NCH, is_s) in enumerate(tiles):
        xT_t = xT_ts[ti % 2]
        load_xT(xT_t, c0, T, ti)
        so = max(c0 - SEQ, 0)
        norm_to(hT[:, :, c0:c0 + T], xT_t, T, g1p, sh1p, g1s[:, :, so:so + min(T, 16)], sh1s[:, :, so:so + min(T, 16)], is_s, ti)
        if gnext < 12:
            ada_group(gnext); gnext += 1
    while gnext < 12:
        ada_group(gnext); gnext += 1
    stt(g2p, modp(4), 1.0, pv[:, rows["n2"]:rows["n2"] + 8], ALU.add, ALU.mult, r=[modT, pv])
    for dst, part in ((gt1p, 2), (sh2p, 3), (gt2p, 5)):
        cp(dst, modp(part), r=[modT])
    S1.close()
    S0.close()
    P.barrier(dummy)

    W = lambda name, shape, dt=F32: S2.sb(name, shape, dt)
    wbH = W("wbH", [128, 8, 640], BF16); wbR = W("wbR", [128, 8, 512], BF16)
    wl = wbH[:, :, 0:288]
    w2b = W("w2b", [128, D], BF16)
    g2b = W("g2b", [128, D], BF16); g2c = W("g2c", [32, D], BF16)
    TW = W("TW", [128, NT], BF16); SG1 = W("SG1", [128, NT], BF16); SG2 = W("SG2", [32, NT], BF16)
    ssT = W("ssT", [128, 27, 16]); lastc = W("lastc", [128, 27, 17])
    prevs = W("prevs", [128, 8])
    dsh = W("dsh", [128, 256])
    S0all = W("S0all", [128, 4, 128]); S0bf = W("S0bf", [128, 4, 128], BF16); Souts = S0all
    Sh = W("Sh", [128, 128]); Shbf = W("Shbf", [128, 128], BF16)
    R0in = W("R0in", [128, 4, 128]); R0blk = W("R0blk", [128, 4, 128]); R0bf = W("R0bf", [128, 4, 128], BF16); Routs = R0blk
    Sr = W("Sr", [128, 128]); Srbf = W("Srbf", [128, 128], BF16); Srt = W("Srt", [128, 128])
    f32t = {n: W(n, [128, 256]) for n in ("th", "fS", "kS", "Pc", "rP", "Ga", "t1", "t2", "Gmm", "kap", "kmod", "bS", "gS",
                                          "xr", "xk", "xv", "rG", "aS", "u1", "u2")}
    f32t["n1"] = f32t["kmod"]; f32t["n2"] = f32t["u1"]
    oTs = [W("oT", [128, 256]) for _ in range(2)]
    PsmpH = W("PsmpH", [128, 5, 64]); PsmpR = W("PsmpR", [128, 4, 64])
    rstH = W("rstH", [128, 256]); rstR = W("rstR", [128, 256]); memset(rstH, 0.0); memset(rstR, 0.0)
    bft = {n: W(n, [128, 256], BF16) for n in ("qt", "kt", "kh", "vT", "sqb", "kpt", "bt", "bh", "ktl", "khr", "vTr")}
    KR = W("KR", [128, 4, 128], BF16)
    tm = {n: W(n, [64, 4, 128], BF16) for n in ("Vtm", "khtm", "W1", "Un", "Ktp", "onb")}
    Abm = {n: W(n, [64, 4, 128], BF16) for n in ("Y2", "Z2", "TTa", "TTb")}
    IFs = [dict(**{n: W(n, [64, 4, 128], BF16) for n in ("Z", "Y", "PbT", "PkT", "AkT", "Vtr", "Ktm", "bhtm", "khrtm")},
                rt=W("rt", [128, 256], BF16), **{n: W(n, [128, 256]) for n in ("Gm", "bon", "Gb")}) for _ in range(2)]
    attnT = W("attnT", [64, 4, 64], BF16)
    Osb = W("Osb", [64, 4, 128]); Osq = W("Osq", [64, 4, 128]); st_ = W("st_", [64, 16]); stA = W("stA", [64, 16]); stB = W("stB", [64, 16])
    srow = Osq.rearrange("p c f -> p (c f)")[0:17, :]
    Osb2 = W("Osb2", [64, 4, 128]); st3 = W("st3", [64, 16]); st4 = W("st4", [64, 16]); stC = W("stC", [64, 16]); stD = W("stD", [64, 16])
    Mblk = W("Mblk", [128, 4, 128], BF16); Ncomb = W("Ncomb", [128, 4, 128]); RpT = W("RpT", [128, 256], BF16)

    dma("pool", w2b[0:64, :], w2, w=[w2b]); dma("pool", w2b[64:128, :], a2, w=[w2b])
    dma("pool", g2b, g2[0:128, :]); dma("pool", g2c, g2[128:160, :])
    dma("pool", wl, w_in[:, 7168:7456].rearrange("(k p) c -> p k c", p=128), r=[w_in], w=[wl])
    for p4 in range(0, 27, 4):
        n4 = min(4, 27 - p4); wdt = min(n4 * 128, RC - p4 * 128)
        memset(srow, 0.0, eng="dve")
        dma("sp", srow[0:16, 0:wdt], ss_in[:, p4 * 128:p4 * 128 + wdt], r=[ss_in], w=[srow])
        b_ = bank()
        for q in range(n4):
            tr(b_[:, q * 16:(q + 1) * 16], srow[0:16, q * 128:(q + 1) * 128], identF[0:16, 0:16], r=[srow, identF], w=[b_])
        cp(ssT[:, p4:p4 + n4, :], b_[:, 0:n4 * 16].rearrange("p (q s) -> p q s", s=16), r=[b_], w=[ssT])
    memset(prevs, 0.0); memset(lastc, 0.0)

    def proj(wt, c_lo, c_hi, ti, c0, T):
        b_ = bank(); n = c_hi - c_lo
        for k in range(8):
            mm(b_[0:n, 0:T], wt[:, k, c_lo:c_hi], hT[:, k, c0:c0 + T], k == 0, k == 7, r=[wt, (hT, ti)], w=[b_])
        return b_

    def proj_samples(wt, nseg, dstP):
        for sg in range(nseg):
            b_ = bank()
            for k in range(8):
                mm(b_[:, 0:64], wt[:, k, sg * 128:(sg + 1) * 128], hT[:, k, SEQ:SEQ + 64], k == 0, k == 7,
                   r=[wt] + [(hT, t_) for t_ in range(8, 12)], w=[b_])
            cp(dstP[:, sg, :], b_[:, 0:64], eng="act", r=[b_], w=[dstP])

    def shift_mix(dst, ps, n, T, is_s, mucol, pcol, part, last_tile, s0=0, omucol=None):
        aff(dsh[0:n, 0:T], ps[0:n, 0:T], omucol[0:n, :], 0.0, r=[ps, pv2], w=[dsh])
        if not is_s:
            stt(dst[0:n, 1:T], ps[0:n, 0:T - 1], mucol[0:n, :], dsh[0:n, 1:T], ALU.mult, ALU.add, r=[ps, dsh, pv], w=[dst])
            stt(dst[0:n, 0:1], prevs[0:n, pcol:pcol + 1], mucol[0:n, :], dsh[0:n, 0:1], ALU.mult, ALU.add, r=[prevs, dsh, pv], w=[dst])
            cp(prevs[0:n, pcol:pcol + 1], ps[0:n, T - 1:T], eng="act", r=[ps, dst], w=[prevs])
            if last_tile:
                cp(lastc[0:n, part, 0:1], ps[0:n, T - 1:T], eng="act", r=[ps], w=[lastc])
        else:
            p3 = ps[0:n, 0:T].rearrange("p (s t) -> p s t", t=4); d3 = dsh[0:n, 0:T].rearrange("p (s t) -> p s t", t=4)
            o3 = dst[0:n, 0:T].rearrange("p (s t) -> p s t", t=4)
            stt(o3[:, :, 1:4], p3[:, :, 0:3], mucol[0:n, :], d3[:, :, 1:4], ALU.mult, ALU.add, r=[ps, dsh, pv], w=[dst])
            stt(o3[:, :, 0], ssT[0:n, part, s0:s0 + T // 4], mucol[0:n, :], d3[:, :, 0], ALU.mult, ALU.add, r=[ssT, dsh, pv], w=[dst])
            cp(lastc[0:n, part, 1 + s0:1 + s0 + T // 4], p3[:, :, 3], eng="act", r=[ps], w=[lastc])

    def sigm(dst, src, r=(), w=(), bias=0.0):
        act(dst, src, AF.Tanh, bias=bias, scale=0.5, r=r, w=w)
        aff(dst, dst, 0.5, 0.5)

    for ti, (c0, T, C, NCH, is_s) in enumerate(tiles):
        last = (c0 + T == SEQ)
        b_ = proj(wl, 0, 128, ti, c0, T)
        shift_mix(f32t["t1"], b_, 128, T, is_s, col("mu", 24), 0, 24, last, max(c0 - SEQ, 0) // 4, omucol=col2("mu", 24))
        act(TW[0:64, c0:c0 + T], f32t["t1"][0:64, 0:T], AF.Tanh, r=[f32t["t1"]], w=[(TW, ti)])
        cp(TW[64:128, c0:c0 + T], f32t["t1"][64:128, 0:T], eng="act", r=[f32t["t1"]], w=[(TW, ti)])
        b_ = proj(wl, 128, 256, ti, c0, T)
        shift_mix(f32t["t1"], b_, 128, T, is_s, col("mu", 25), 1, 25, last, max(c0 - SEQ, 0) // 4, omucol=col2("mu", 25))
        sigm(f32t["t2"][:, 0:T], f32t["t1"][:, 0:T], r=[f32t["t1"]], w=[f32t["t2"]])
        cp(SG1[:, c0:c0 + T], f32t["t2"][:, 0:T], r=[f32t["t2"]], w=[(SG1, ti)])
        b_ = proj(wl, 256, 288, ti, c0, T)
        shift_mix(f32t["t1"], b_, 32, T, is_s, col("mu2"), 2, 26, last, max(c0 - SEQ, 0) // 4, omucol=col2("mu2"))
        sigm(f32t["t2"][0:32, 0:T], f32t["t1"][0:32, 0:T], r=[f32t["t1"]], w=[f32t["t2"]])
        cp(SG2[:, c0:c0 + T], f32t["t2"][0:32, 0:T], r=[f32t["t2"]], w=[(SG2, ti)])

    if debug is not None and debug.get("_stop") == "lora":
        dump("TW", TW, [128, NT]); dump("SG1", SG1, [128, NT]); dump("SG2", SG2, [32, NT]); dump("lastc", lastc.rearrange("p a b -> p (a b)"), [128, 27 * 17])
        n = P.emit()
        return nc, dbg, n

    def chunk3(ap, C):
        return ap.rearrange("p (c t) -> p c t", t=C)

    def to_tm(dst, src_bf, T, C, NCH):
        for c4 in range(0, NCH, 8):
            b_ = bank(); bb = b_.bitcast(BF16)
            for c in range(c4, min(c4 + 8, NCH)):
                tr(bb[0:C, (c - c4) * 128:(c - c4 + 1) * 128], src_bf[:, c * C:(c + 1) * C], identB, r=[src_bf, identB], w=[b_])
            n8 = min(8, NCH - c4)
            cp(dst[0:C, c4:c4 + n8, :], bb[0:C, 0:n8 * 128].rearrange("p (c f) -> p c f", f=128), eng="act", r=[b_], w=[dst])

    def cumprod(dst, src, T, C, NCH, rst):
        cp(chunk3(rst[:, 0:T], C)[:, :, 0:1], chunk3(src[:, 0:T], C)[:, :, 0:1], eng="act", r=[src], w=[rst])
        P.add("dve", lambda e: e.tensor_tensor_scan(out=dst[:, 0:T], data0=src[:, 0:T], data1=rst[:, 0:T], initial=1.0,
                                                    op0=ALU.mult, op1=ALU.max), r=[src, rst], w=[dst])
        if C != 64:
            memset(chunk3(rst[:, 0:T], C)[:, :, 0:1], 0.0, eng="pool", w=[rst])

    def lastbc(ap, T, C, NCH):
        return chunk3(ap[:, 0:T], C)[:, :, C - 1:C].broadcast_to([128, NCH, C])


    def rwkv_front(j, ti, c0, T, C, NCH, is_s, last, wb, gi):
        F_ = f32t; B_ = bft; M_ = masks[C]; jc = slice(j * 128, (j + 1) * 128)
        I_ = IFs[gi % 2]
        if gi >= 2:
            P.add("need", ("B", gi - 2))
        X = lambda n: F_[n][:, 0:T]
        Bx = lambda n: B_[n][:, 0:T]
        Gm = I_["Gm"]; bon = I_["bon"]; Gb = I_["Gb"]; rt = I_["rt"]
        so_ = max(c0 - SEQ, 0) // 4
        if is_s and so_ == 0:
            proj_samples(wb, 4, PsmpR)
        PJ = (lambda lo, hi: PsmpR[:, lo // 128, so_ * 4:so_ * 4 + T]) if is_s else (lambda lo, hi: proj(wb, lo, hi, ti, c0, T)[:, 0:T])
        br = PJ(0, 128); shift_mix(F_["xr"], br, 128, T, is_s, col("mu", j), 3, j, last, so_, omucol=col2("mu", j))
        bk_ = PJ(128, 256); shift_mix(F_["xk"], bk_, 128, T, is_s, col("mu", 8 + j), 4, 8 + j, last, so_, omucol=col2("mu", 8 + j))
        bv = PJ(256, 384); shift_mix(F_["xv"], bv, 128, T, is_s, col("mu", 16 + j), 5, 16 + j, last, so_, omucol=col2("mu", 16 + j))
        bgb = PJ(384, 512)
        sigm(Gb[:, 0:T], bgb[:, 0:T], r=[bgb], w=[Gb])
        bw = bank(); mm(bw[:, 0:T], w2b[0:64, jc], TW[0:64, c0:c0 + T], True, True, r=[w2b, (TW, ti)], w=[bw])
        act(X("u1"), bw[:, 0:T], AF.Tanh, bias=hw0[:, j:j + 1], scale=0.5, r=[bw], w=[F_["u1"]])
        act(X("u1"), X("u1"), AF.Exp, bias=C1, scale=C1)
        ba = bank(); mm(ba[:, 0:T], w2b[64:128, jc], TW[64:128, c0:c0 + T], True, True, r=[w2b, (TW, ti)], w=[ba])
        sigm(X("aS"), ba[:, 0:T], r=[ba], w=[F_["aS"]], bias=ha0[:, j:j + 1])
        bg = bank(); mm(bg[:, 0:T], g2b[:, jc], SG1[:, c0:c0 + T], True, False, r=[g2b, (SG1, ti)], w=[bg])
        mm(bg[:, 0:T], g2c[:, jc], SG2[:, c0:c0 + T], False, True, r=[g2c, (SG2, ti)], w=[bg])
        cp(X("gS"), bg[:, 0:T], eng="act", r=[bg], w=[F_["gS"]])
        tt(Gb[:, 0:T], Gb[:, 0:T], X("gS"), ALU.mult, eng="pool", r=[Gb, F_["gS"]], w=[Gb])
        cumprod(Gm, F_["u1"], T, C, NCH, rstR)
        P.add("dve", lambda e: e.reciprocal(out=X("rG"), in_=Gm[:, 0:T]), r=[Gm], w=[F_["rG"]])
        cp(chunk3(X("Gmm"), C)[:, :, 1:C], chunk3(Gm[:, 0:T], C)[:, :, 0:C - 1], r=[Gm], w=[F_["Gmm"]])
        memset(chunk3(X("Gmm"), C)[:, :, 0:1], 1.0, eng="dve", w=[F_["Gmm"]])
        aff(X("kap"), X("xk"), col("kk", j), 0.0, r=[F_["xk"], pv], w=[F_["kap"]])
        tt(Bx("sqb"), X("kap"), X("kap"), ALU.mult, eng="pool")
        bn = bank(); mm(bn[:, 0:T], blkones, Bx("sqb"), True, True, w=[bn])
        ts(X("n1"), bn[:, 0:T], 1.0, 1e-24, ALU.mult, ALU.add, r=[bn], w=[F_["n1"]])
        rsqrt_dve(X("u2"), X("n1"), X("n2"), [F_["u2"], F_["n1"], F_["n2"]])
        tt(X("kap"), X("kap"), X("u2"), ALU.mult)
        aff(X("u2"), X("aS"), col("ka", j), omka[:, j:j + 1], r=[F_["aS"], pv, omka], w=[F_["u2"]])
        tt(X("kmod"), X("xk"), X("u2"), ALU.mult, eng="pool")
        tt(X("bS"), X("aS"), X("kap"), ALU.mult)
        stt(Bx("sqb"), X("xr"), col("rk", j), X("kmod"), ALU.mult, ALU.mult, r=[F_["xr"], F_["kmod"], pv], w=[B_["sqb"]])
        bn2 = bank(); mm(bn2[:, 0:T], blkones, Bx("sqb"), True, True, w=[bn2])
        tt(bon[:, 0:T], bn2[:, 0:T], X("xv"), ALU.mult, r=[bn2, F_["xv"]], w=[bon])
        tt(Bx("kpt"), X("kap"), X("Gmm"), ALU.mult)
        tt(rt[:, 0:T], X("xr"), Gm[:, 0:T], ALU.mult, eng="pool", r=[F_["xr"], Gm], w=[rt])
        cp(KR[:, 0:NCH, 0:C], chunk3(Bx("kpt"), C), eng="pool", r=[B_["kpt"]], w=[KR])
        cp(KR[:, 0:NCH, 64:64 + C], chunk3(rt[:, 0:T], C), eng="pool", r=[rt], w=[KR])
        tt(X("u2"), X("kmod"), X("rG"), ALU.mult); cp(Bx("ktl"), X("u2"), eng="act")
        tt(chunk3(Bx("khr"), C), chunk3(X("u2"), C), lastbc(Gm, T, C, NCH), ALU.mult, r=[F_["u2"], Gm], w=[B_["khr"]])
        tt(X("u2"), X("bS"), X("rG"), ALU.mult); cp(Bx("bt"), X("u2"), eng="act")
        tt(chunk3(Bx("bh"), C), chunk3(X("u2"), C), lastbc(Gm, T, C, NCH), ALU.mult, r=[F_["u2"], Gm], w=[B_["bh"]])
        cp(Bx("vTr"), X("xv"), eng="act")
        to_tm(I_["Vtr"], B_["vTr"], T, C, NCH); to_tm(I_["Ktm"], B_["kpt"], T, C, NCH)
        to_tm(I_["bhtm"], B_["bh"], T, C, NCH); to_tm(I_["khrtm"], B_["khr"], T, C, NCH)
        v4_ = lambda b: b[0:C, :].rearrange("p (c x t) -> p c x t", x=2, t=64)
        mk = lambda m_: m_.unsqueeze(1).broadcast_to([C, NCH, C])
        for h in range(2):
            hs_ = slice(64 * h, 64 * h + 64)
            dv = lambda t_: t_[0:C, 0:NCH, h * 64:h * 64 + C]
            b1 = bank()
            for c in range(NCH):
                mm(b1[0:C, c * 128:(c + 1) * 128], B_["bt"][hs_, c * C:(c + 1) * C], KR[hs_, c, :], True, True, r=[B_["bt"], KR], w=[b1])
            tt(dv(I_["Z"]), v4_(b1)[:, 0:NCH, 0, 0:C], mk(M_["nstrict"]), ALU.mult, r=[b1, M_["nstrict"]], w=[I_["Z"]])
            tt(dv(I_["PbT"]), v4_(b1)[:, 0:NCH, 1, 0:C], mk(M_["incl"]), ALU.mult, r=[b1, M_["incl"]], w=[I_["PbT"]])
            b2 = bank()
            for c in range(NCH):
                mm(b2[0:C, c * 128:(c + 1) * 128], B_["ktl"][hs_, c * C:(c + 1) * C], KR[hs_, c, :], True, True, r=[B_["ktl"], KR], w=[b2])
            tt(dv(I_["AkT"]), v4_(b2)[:, 0:NCH, 0, 0:C], mk(M_["strict"]), ALU.mult, r=[b2, M_["strict"]], w=[I_["AkT"]])
            tt(dv(I_["PkT"]), v4_(b2)[:, 0:NCH, 1, 0:C], mk(M_["incl"]), ALU.mult, r=[b2, M_["incl"]], w=[I_["PkT"]])
            b3 = bank()
            for c in range(NCH):
                mm(b3[0:C, c * 128:c * 128 + C], B_["kpt"][hs_, c * C:(c + 1) * C], B_["bt"][hs_, c * C:(c + 1) * C], True, True, r=[B_["kpt"], B_["bt"]], w=[b3])
            tt(dv(I_["Y"]), v4_(b3)[:, 0:NCH, 0, 0:C], mk(M_["nlstrict"]), ALU.mult, r=[b3, M_["nlstrict"]], w=[I_["Y"]])
        P.add("mark", ("F", gi))

    def rwkv_back(j, ti, c0, T, C, NCH, is_s, last, oT, gi):
        F_ = f32t; M_ = masks[C]
        I_ = IFs[gi % 2]
        L = 6 if C == 64 else 2
        X = lambda n: F_[n][:, 0:T]
        Gm = I_["Gm"]; bon = I_["bon"]; Gb = I_["Gb"]; rt = I_["rt"]
        Osq2 = Ncomb; onr = tm["Un"]; e1 = Mblk.bitcast(F32).rearrange("p c f -> p (c f)")
        if is_s:
            s0 = (c0 - SEQ) // 4
            memset(R0in, 0.0)
            dma("sp", R0in[0:64, :, 0:64], sr_in[s0:s0 + 4, 2 * j].rearrange("s i j -> i s j"), r=[sr_in], w=[R0in])
            dma("sp", R0in[64:128, :, 64:128], sr_in[s0:s0 + 4, 2 * j + 1].rearrange("s i j -> i s j"), r=[sr_in], w=[R0in])
            b_ = bank()
            for q in range(4):
                tr(b_[:, q * 128:(q + 1) * 128], R0in[:, q, :], identF, w=[b_])
            cp(R0blk[:, 0:4, :], b_.rearrange("p (q f) -> p q f", f=128), r=[b_], w=[R0blk])
            cp(R0bf, R0blk, eng="pool")
        P.add("need", ("F", gi))
        hv = lambda t_, c_lo, c_hi: t_[0:C, c_lo:c_hi, :].rearrange("p c (h t) -> p c h t", t=64)[:, :, :, 0:C]
        tt(hv(Abm["TTa"], 0, NCH), hv(I_["Z"], 0, NCH), identF[0:C, 0:C].unsqueeze(1).unsqueeze(1).broadcast_to([C, NCH, 2, C]), ALU.add,
           r=[I_["Z"], identF], w=[Abm["TTa"]])
        Yc, Zc, Yn, Zn, TTc, TTn = I_["Y"], I_["Z"], Abm["Y2"], Abm["Z2"], Abm["TTa"], Abm["TTb"]
        pr = [(c, h) for c in range(NCH) for h in range(2)]

        def batch_mm(dst, lhs, rhs, wid_l, wid_r, post):
            for p8 in range(0, len(pr), 8):
                b_ = bank(); grp = pr[p8:p8 + 8]
                for q, (c, h) in enumerate(grp):
                    mm(b_[0:C, q * 64:q * 64 + wid_r], lhs[0:C, c, h * 64:h * 64 + wid_l], rhs[0:C, c, h * 64:h * 64 + wid_r], True, True,
                       r=[lhs, rhs], w=[b_])
                cA = grp[0][0]; cB = grp[-1][0] + 1
                src = b_[0:C, 0:(cB - cA) * 128].rearrange("p (c h t) -> p c h t", h=2, t=64)[:, :, :, 0:wid_r]
                d = dst[0:C, cA:cB, :].rearrange("p c (h t) -> p c h t", t=64)[:, :, :, 0:wid_r]
                post(d, src, b_)

        cpy = lambda d, s_, b_: cp(d, s_, eng="act", r=[b_], w=[d])
        for n in range(1, L):
            batch_mm(Yn, Zc, Yc, C, C, cpy)
            if n < L - 1:
                batch_mm(Zn, Yc, Zc, C, C, cpy)
            for p8 in range(0, len(pr), 8):
                b_ = bank(); grp = pr[p8:p8 + 8]
                for q, (c, h) in enumerate(grp):
                    mm(b_[0:C, q * 64:q * 64 + C], Yn[0:C, c, h * 64:h * 64 + C], TTc[0:C, c, h * 64:h * 64 + C], True, True, r=[Yn, TTc], w=[b_])
                cA = grp[0][0]; cB = grp[-1][0] + 1
                src = b_[0:C, 0:(cB - cA) * 128].rearrange("p (c h t) -> p c h t", h=2, t=64)[:, :, :, 0:C]
                tt(hv(TTn, cA, cB), src, hv(TTc, cA, cB), ALU.add, r=[b_, TTc], w=[TTn])
            Yc, Yn = Yn, Yc; Zc, Zn = Zn, Zc; TTc, TTn = TTn, TTc
        TTf = TTc
        batch_mm(tm["W1"], I_["AkT"], I_["Vtr"], C, 64, cpy)
        batch_mm(tm["Ktp"], TTf, I_["Ktm"], C, 64, cpy)
        neg = lambda d, s_, b_: ts(d, s_, -1.0, None, ALU.mult, r=[b_], w=[d])
        batch_mm(tm["Un"], TTf, tm["W1"], C, 64, neg)
        v3 = lambda b: b.rearrange("p (c f) -> p c f", f=128)
        bM = bank()
        for c in range(NCH):
            mm(bM[:, c * 128:(c + 1) * 128], tm["Ktp"][0:C, c, :], I_["bhtm"][0:C, c, :], True, True, r=[tm["Ktp"], I_["bhtm"]], w=[bM])
        tt(Mblk[:, 0:NCH, :], v3(bM)[:, 0:NCH, :], blkneg.unsqueeze(1).broadcast_to([128, NCH, 128]), ALU.mult, r=[bM, blkneg], w=[Mblk])
        bR = bank()
        for c in range(NCH):
            mm(bR[:, c * 128:(c + 1) * 128], tm["Ktp"][0:C, c, :], I_["PbT"][0:C, c, :], True, True, r=[tm["Ktp"], I_["PbT"]], w=[bR])
        for h in range(2):
            hs_ = slice(64 * h, 64 * h + 64)
            tt(chunk3(RpT[hs_, 0:T], C), chunk3(rt[hs_, 0:T], C), v3(bR)[hs_, 0:NCH, h * 64:h * 64 + C], ALU.subtract, r=[rt, bR], w=[RpT])
        bN = bank()
        for c in range(NCH):
            mm(bN[:, c * 128:(c + 1) * 128], I_["khrtm"][0:C, c, :], I_["Vtr"][0:C, c, :], True, False, r=[I_["khrtm"], I_["Vtr"]], w=[bN])
            mm(bN[:, c * 128:(c + 1) * 128], I_["bhtm"][0:C, c, :], tm["Un"][0:C, c, :], False, True, r=[I_["bhtm"], tm["Un"]], w=[bN])
        tt(Ncomb[:, 0:NCH, :], v3(bN)[:, 0:NCH, :], blkpos.unsqueeze(1).broadcast_to([128, NCH, 128]), ALU.mult, r=[bN, blkpos], w=[Ncomb])
        bo = banks[5]
        for c in range(NCH):
            csl = slice(c * C, (c + 1) * C)
            sbf = R0bf[:, c, :] if is_s else Srbf
            sin = R0blk[:, c, :] if is_s else Sr
            sout = Routs[:, c, :] if is_s else Sr
            kS_ = [R0bf] if is_s else [Srbf]
            oo = bo[0:C, c * 128:(c + 1) * 128]
            mm(oo, RpT[:, csl], sbf, True, False, r=[RpT] + kS_, w=[bo])
            for h in range(2):
                hb = slice(h * 64, (h + 1) * 64)
                mm(oo[:, hb], I_["PkT"][0:C, c, h * 64:h * 64 + C], I_["Vtr"][0:C, c, hb], False, False, r=[I_["PkT"], I_["Vtr"]], w=[bo])
                mm(oo[:, hb], I_["PbT"][0:C, c, h * 64:h * 64 + C], tm["Un"][0:C, c, hb], False, h == 1, r=[I_["PbT"], tm["Un"]], w=[bo])
            b3 = bank()
            mm(b3[:, 0:128], Mblk[:, c, :], sbf, True, True, r=[Mblk] + kS_, w=[b3])
            stt(Srt, sin, Gm[:, (c + 1) * C - 1:(c + 1) * C], Ncomb[:, c, :], ALU.mult, ALU.add,
                r=[R0blk if is_s else Sr, Gm, Ncomb], w=[Srt])
            tt(sout, Srt, b3[:, 0:128], ALU.add, r=[Srt, b3], w=[Routs if is_s else Sr])
            if not is_s:
                cp(Srbf, Sr, eng="act")
        cp(Osb2[0:C, 0:NCH, :], bo.rearrange("p (c f) -> p c f", f=128)[0:C, 0:NCH, :], eng="act", r=[bo], w=[Osb2])
        if is_s:
            b_ = bank()
            for q in range(4):
                tr(b_[:, q * 128:(q + 1) * 128], Routs[:, q, :], identF, w=[b_])
            cp(R0in[:, 0:4, :], b_.rearrange("p (q f) -> p q f", f=128), r=[b_], w=[R0in])
            dma("sp", rs[s0:s0 + 4, 2 * j].rearrange("s i j -> i s j"), R0in[0:64, :, 0:64], r=[R0in], w=[rs])
            dma("sp", rs[s0:s0 + 4, 2 * j + 1].rearrange("s i j -> i s j"), R0in[64:128, :, 64:128], r=[R0in], w=[rs])
        elif last:
            b_ = bank(); tr(b_[:, 0:128], Sr, identF, w=[b_]); cp(Srt, b_[:, 0:128])
            dma("sp", rp[2 * j], Srt[0:64, 0:64], r=[Srt], w=[rp]); dma("sp", rp[2 * j + 1], Srt[64:128, 64:128], r=[Srt], w=[rp])
        G2 = NCH * 2
        og = Osb2[0:C, 0:NCH, :].rearrange("p c (h i) -> p (c h) i", i=64)
        oq = Osq2[0:C, 0:NCH, :].rearrange("p c (h i) -> p (c h) i", i=64)
        P.add("dve", lambda e: e.reduce_sum(out=st3[0:C, 0:G2], in_=og, axis=AX.X), r=[Osb2], w=[st3])
        ts(st3[0:C, 0:G2], st3[0:C, 0:G2], 1.0 / 64, None, ALU.mult)
        tt(og, og, st3[0:C, 0:G2].unsqueeze(2).broadcast_to([C, G2, 64]), ALU.subtract, r=[Osb2, st3], w=[Osb2])
        tt(oq, og, og, ALU.mult, eng="pool", r=[Osb2], w=[Osq2])
        P.add("dve", lambda e: e.reduce_sum(out=st4[0:C, 0:G2], in_=oq, axis=AX.X), r=[Osq2], w=[st4])
        rsqrt_(st4[0:C, 0:G2], st4[0:C, 0:G2], 1.0 / 64, LNX_EPS, stC[0:C, 0:G2], stD[0:C, 0:G2])
        tt(onr[0:C, 0:NCH, :].rearrange("p c (h i) -> p (c h) i", i=64), og, st4[0:C, 0:G2].unsqueeze(2).broadcast_to([C, G2, 64]), ALU.mult,
           r=[Osb2, st4], w=[onr])
        b_ = bank(); bb = b_.bitcast(BF16)
        for c in range(NCH):
            tr(bb[:, c * C:(c + 1) * C], onr[0:C, c, :], identB[0:C, 0:C], r=[onr, identB], w=[b_])
        ts(e1[:, 0:T], bb[:, 0:T], col("lw", j), col("lbb", j), ALU.mult, ALU.add, r=[b_, pv], w=[Mblk])
        tt(e1[:, 0:T], e1[:, 0:T], bon[:, 0:T], ALU.add, eng="pool", r=[Mblk, bon], w=[Mblk])
        tt(e1[:, 0:T], e1[:, 0:T], Gb[:, 0:T], ALU.mult, eng="pool", r=[Mblk, Gb], w=[Mblk])
        P.add("need", ("oTw", gi))
        tt(mT[:, j, c0:c0 + T], e1[:, 0:T], oT[:, 0:T], ALU.add, r=[Mblk, oT], w=[(mT, ti)])
        P.add("mark", ("oTr", gi))
        P.add("mark", ("B", gi))

    NBLK = 8 if debug is None else debug.get('nblk', 8)

    def hgrn_tile(j, ti, c0, T, C, NCH, is_s, last, wb, oT, gi):
        M_ = masks[C]; F_ = f32t; B_ = bft
        if is_s:
            s0 = (c0 - SEQ) // 4
            dma("sp", S0all, sh_in[s0:s0 + 4, j].rearrange("s k v -> k s v"), r=[sh_in], w=[S0all])
            cp(S0bf, S0all, eng="pool")
        so_ = max(c0 - SEQ, 0)
        if is_s and so_ == 0:
            proj_samples(wb, 5, PsmpH)
        PJ = (lambda lo, hi: PsmpH[:, lo // 128, so_:so_ + T]) if is_s else (lambda lo, hi: proj(wb, lo, hi, ti, c0, T)[:, 0:T])
        bf_ = PJ(128, 256)
        act(F_["th"][:, 0:T], bf_[:, 0:T], AF.Tanh, scale=0.5, r=[bf_], w=[F_["th"]])
        aff(F_["fS"][:, 0:T], F_["th"][:, 0:T], hs1[:, j:j + 1], hs2[:, j:j + 1], r=[F_["th"], hs1, hs2], w=[F_["fS"]])
        aff(F_["kS"][:, 0:T], F_["th"][:, 0:T], hs1n[:, j:j + 1], hs2c[:, j:j + 1], r=[F_["th"], hs1n, hs2c], w=[F_["kS"]])
        cumprod(F_["Pc"], F_["fS"], T, C, NCH, rstH)
        P.add("dve", lambda e, T=T: e.reciprocal(out=F_["rP"][:, 0:T], in_=F_["Pc"][:, 0:T]), r=[F_["Pc"]], w=[F_["rP"]])
        bq = PJ(0, 128)
        tt(B_["qt"][:, 0:T], bq[:, 0:T], F_["Pc"][:, 0:T], ALU.mult, r=[bq, F_["Pc"]], w=[B_["qt"]])
        tt(F_["t1"][:, 0:T], F_["kS"][:, 0:T], F_["rP"][:, 0:T], ALU.mult, eng="pool", r=[F_["kS"], F_["rP"]], w=[F_["t1"]])
        cp(B_["kt"][:, 0:T], F_["t1"][:, 0:T], eng="act", r=[F_["t1"]], w=[B_["kt"]])
        tt(chunk3(B_["kh"][:, 0:T], C), chunk3(F_["t1"][:, 0:T], C), lastbc(F_["Pc"], T, C, NCH), ALU.mult, r=[F_["t1"], F_["Pc"]], w=[B_["kh"]])
        bi = PJ(256, 384)
        cp(B_["vT"][:, 0:T], bi[:, 0:T], eng="act", r=[bi], w=[B_["vT"]])
        bog = PJ(384, 512)
        sigm(F_["t2"][:, 0:T], bog[:, 0:T], r=[bog], w=[F_["t2"]])
        tt(F_["t2"][:, 0:T], bog[:, 0:T], F_["t2"][:, 0:T], ALU.mult, r=[bog, F_["t2"]], w=[F_["t2"]])
        bga = PJ(512, 640)
        sigm(F_["Ga"][:, 0:T], bga[:, 0:T], r=[bga], w=[F_["Ga"]])
        tt(F_["Ga"][:, 0:T], F_["Ga"][:, 0:T], F_["t2"][:, 0:T], ALU.mult, eng="pool", r=[F_["Ga"], F_["t2"]], w=[F_["Ga"]])
        to_tm(tm["Vtm"], B_["vT"], T, C, NCH); to_tm(tm["khtm"], B_["kh"], T, C, NCH)
        b_ = bank()
        for c in range(NCH):
            mm(b_[0:C, c * C:(c + 1) * C], B_["kt"][:, c * C:(c + 1) * C], B_["qt"][:, c * C:(c + 1) * C], True, True, w=[b_])
        tt(attnT[0:C, 0:NCH, 0:C], b_[0:C, 0:NCH * C].rearrange("p (c t) -> p c t", t=C),
           M_["incl"].unsqueeze(1).broadcast_to([C, NCH, C]), ALU.mult, r=[b_, M_["incl"]], w=[attnT])
        bo = banks[0]
        for c in range(NCH):
            csl = slice(c * C, (c + 1) * C)
            sbf = S0bf[:, c, :] if is_s else Shbf
            sin = S0all[:, c, :] if is_s else Sh
            sout = Souts[:, c, :] if is_s else Sh
            oo = bo[0:C, c * 128:(c + 1) * 128]
            mm(oo, attnT[0:C, c, 0:C], tm["Vtm"][0:C, c, :], True, False, r=[attnT, tm["Vtm"]], w=[bo])
            mm(oo, B_["qt"][:, csl], sbf, False, True, r=[B_["qt"], S0bf if is_s else Shbf], w=[bo])
            b2 = bank()
            mm(b2[:, 0:128], tm["khtm"][0:C, c, :], tm["Vtm"][0:C, c, :], True, True, r=[tm["khtm"], tm["Vtm"]], w=[b2])
            stt(sout, sin, F_["Pc"][:, (c + 1) * C - 1:(c + 1) * C], b2[:, 0:128], ALU.mult, ALU.add,
                r=[S0all if is_s else Sh, F_["Pc"], b2], w=[Souts if is_s else Sh])
            if not is_s:
                cp(Shbf, Sh, eng="act")
        cp(Osb[0:C, 0:NCH, :], bo.rearrange("p (c f) -> p c f", f=128)[0:C], eng="act", r=[bo], w=[Osb])
        if is_s:
            dma("sp", hs[s0:s0 + 4, j].rearrange("s k v -> k s v"), Souts, r=[Souts], w=[hs])
        elif last:
            dma("sp", hp[j], Sh, r=[Sh], w=[hp])
        tt(Osq[0:C, 0:NCH, :], Osb[0:C, 0:NCH, :], Osb[0:C, 0:NCH, :], ALU.mult, eng="pool", r=[Osb], w=[Osq])
        P.add("dve", lambda e, C=C, NCH=NCH: e.reduce_sum(out=st_[0:C, 0:NCH], in_=Osq[0:C, 0:NCH, :], axis=AX.X), r=[Osq], w=[st_])
        rsqrt_(st_[0:C, 0:NCH], st_[0:C, 0:NCH], 1.0 / 128, RMS_EPS, stA[0:C, 0:NCH], stB[0:C, 0:NCH])
        tt(Osq[0:C, 0:NCH, :], Osb[0:C, 0:NCH, :], st_[0:C, 0:NCH].unsqueeze(2).broadcast_to([C, NCH, 128]), ALU.mult, r=[Osb, st_], w=[Osq])
        tt(tm["onb"][0:C, 0:NCH, :], Osq[0:C, 0:NCH, :], nwbc[0:C, :].unsqueeze(1).broadcast_to([C, NCH, 128]), ALU.mult, r=[Osq, nwbc], w=[tm["onb"]])
        b_ = bank(); bb = b_.bitcast(BF16)
        for c in range(NCH):
            tr(bb[:, c * C:(c + 1) * C], tm["onb"][0:C, c, :], identB[0:C, 0:C], r=[tm["onb"], identB], w=[b_])
        if gi >= 2:
            P.add("need", ("oTr", gi - 2))
        tt(oT[:, 0:T], bb[:, 0:T], F_["Ga"][:, 0:T], ALU.mult, r=[b_, F_["Ga"]], w=[oT])
        P.add("mark", ("oTw", gi))

    def stream(kind):
        gi = 0
        for j in range(NBLK):
            seg = lambda col0: w_in[:, col0 + j * 128:col0 + (j + 1) * 128].rearrange("(k p) c -> p k c", p=128)
            if kind == "H":
                for si, col0 in enumerate((0, 1024, 2048, 3072, 7456)):
                    dma("pool", wbH[:, :, si * 128:(si + 1) * 128], seg(col0), r=[w_in], w=[wbH])
                memset(Sh, 0.0, eng="dve"); memset(Shbf, 0.0, eng="dve")
            elif kind == "RF":
                for si, col0 in enumerate((4096, 5120, 6144, 8480)):
                    dma("pool", wbR[:, :, si * 128:(si + 1) * 128], seg(col0), r=[w_in], w=[wbR])
                memset(prevs[:, 3:6], 0.0, eng="dve", w=[prevs])
            else:
                memset(Sr, 0.0, eng="dve"); memset(Srbf, 0.0, eng="dve")
            for ti, (c0, T, C, NCH, is_s) in enumerate(tiles):
                if debug is not None and 'tsel' in debug and ti not in debug['tsel']:
                    continue
                last = (c0 + T == SEQ)
                if kind == "H":
                    hgrn_tile(j, ti, c0, T, C, NCH, is_s, last, wbH, oTs[gi % 2], gi)
                elif kind == "RF":
                    rwkv_front(j, ti, c0, T, C, NCH, is_s, last, wbR, gi)
                else:
                    rwkv_back(j, ti, c0, T, C, NCH, is_s, last, oTs[gi % 2], gi)
                gi += 1

    opsH = []; opsF = []; opsB = []
    P.sink = opsH; cur[0] = "H"; stream("H")
    P.sink = opsF; cur[0] = "RF"; stream("RF")
    P.sink = opsB; cur[0] = "RB"; stream("RB")
    P.sink = None; cur[0] = None
    _n0 = len(P.ops)
    if debug is not None:
        P.lat_same = debug.get('lat_same', P.lat_same); P.lat_cross = debug.get('lat_cross', P.lat_cross)
        global PE_FIX, PE_RATE
        PE_FIX = debug.get('pe_fix', PE_FIX); PE_RATE = debug.get('pe_rate', PE_RATE)
    P.merge([opsH, opsF, opsB])
    if debug is not None and debug.get("xdeps"):
        from collections import Counter
        cx = Counter()
        for i in range(_n0, len(P.ops)):
            o = P.ops[i]
            for d in o["deps"]:
                od = P.ops[d]
                if d >= _n0 and od["stream"] != o["stream"]:
                    shared = (set(o["rk"]) | set(o["wk"])) & (set(od["rk"]) | set(od["wk"]))
                    cx[(o["stream"], o["eng"], od["eng"], str(sorted(map(str, shared)))[:80])] += 1
        for k, v in cx.most_common(30):
            print("XDEP", v, k)
    dump("mT", mT.rearrange("p a b -> p (a b)"), [128, 8 * NT])
    for p4 in range(0, 27, 4):
        b_ = bank(); n4 = min(4, 27 - p4); wdt = min(n4 * 128, RC - p4 * 128)
        for q in range(n4):
            tr(b_[0:17, q * 128:(q + 1) * 128], lastc[:, p4 + q, :], identF, r=[lastc, identF], w=[b_])
        cp(srow[:, 0:n4 * 128], b_[0:17, 0:n4 * 128], r=[b_], w=[srow])
        dma("sp", shp[:, p4 * 128:p4 * 128 + wdt], srow[0:1, 0:wdt], r=[srow], w=[shp])
        dma("sp", shs[:, p4 * 128:p4 * 128 + wdt], srow[1:17, 0:wdt], r=[srow], w=[shs])
    if debug is not None and debug.get("_stop") == 2:
        n = P.emit()
        return nc, dbg, n
    print('SBUF phase2 total', SBTOT[0], 'S2', S2.tot)
    S2.close()
    P.barrier(dummy)
    S1 = Scope()
    xT = S1.sb("xT", [128, 8, NT]); h2T = S1.sb("h2T", [128, 8, NT], BF16)
    g2s = S1.sb("g2s", [128, 8, 64]); sh2s = S1.sb("sh2s", [128, 8, 64]); gt1s = S1.sb("gt1s", [128, 8, 64]); gt2s = S1.sb("gt2s", [128, 8, 64])
    mk_mods(gt1s, 2); mk_mods(sh2s, 3); mk_mods(g2s, 4, "n2"); mk_mods(gt2s, 5)
    nbuf = dict(sq=S1.sb("sq", [128, 8, 512], BF16), rstd=S1.sb("rstd", [128, 512]), tmp=S1.sb("ntmp", [128, 8, 512]),
                xtm=[S1.sb("xtm", [128, D]) for _ in range(2)], i=0)
    nbuf["r1"] = nbuf["tmp"][:, 1, :]; nbuf["r2"] = nbuf["tmp"][:, 0, :]
    wo = S1.sb("wo", [128, 8, 128], BF16); wus = [S1.sb("wu", [128, 8, 256], BF16) for _ in range(2)]; wds = [S1.sb("wd", [128, 2, D], BF16)] * 2
    aTs = [S1.sb("aT", [128, 2, 512], BF16) for _ in range(2)]; rTs = [S1.sb("rT", [128, 1, 512], BF16)] * 2; t64 = S1.sb("t64", [128, 64])
    rtmp = [nbuf["rstd"], nbuf["tmp"][:, 0, :]]; rti = [0]
    tiles34 = [(i * 512, 512, False) for i in range(4)] + [(SEQ, 64, True)]
    for ti, (c0, T, is_s) in enumerate(tiles34):
        load_xT(xT[:, :, c0:c0 + T], c0, T, ti, fkeys=True)

    def resid_add(f, c0, T, is_s, ps, gtp, gts, split=False):
        kx = [(xT, f)]
        if not is_s:
            if split:
                tq = rtmp[rti[0] % len(rtmp)]; rti[0] += 1
                aff(tq[:, 0:T], ps[:, 0:T], gtp[:, f:f + 1], 0.0, r=[ps, gtp], w=[tq])
                tt(xT[:, f, c0:c0 + T], xT[:, f, c0:c0 + T], tq[:, 0:T], ALU.add, eng="pool", r=kx + [tq], w=kx)
            else:
                stt(xT[:, f, c0:c0 + T], ps[:, 0:T], gtp[:, f:f + 1], xT[:, f, c0:c0 + T], ALU.mult, ALU.add, r=[ps, gtp] + kx, w=kx)
        else:
            tt(t64[:, 0:T], ps[:, 0:T], gts[:, f, :], ALU.mult, r=[ps, gts], w=[t64])
            tt(xT[:, f, c0:c0 + T], xT[:, f, c0:c0 + T], t64[:, 0:T], ALU.add, r=kx + [t64], w=kx)

    for f in range(8):
        dma("pool", wo, w_out[:, f * 128:(f + 1) * 128].rearrange("(k p) c -> p k c", p=128), r=[w_out], w=[wo])
        for ti, (c0, T, is_s) in enumerate(tiles34):
            b_ = bank()
            for k in range(8):
                mm(b_[:, 0:T], wo[:, k, :], mT[:, k, c0:c0 + T], k == 0, k == 7, r=[wo, mT], w=[b_])
            resid_add(f, c0, T, is_s, b_, gt1p, gt1s)
    for ti, (c0, T, is_s) in enumerate(tiles34):
        norm_to(h2T[:, :, c0:c0 + T], xT[:, :, c0:c0 + T], T, g2p, sh2p, g2s, sh2s, is_s, 0, fkeys=True)
    steps = [(g, ti) for g in range(16) for ti in range(len(tiles34))]
    upb = [0]; dnb = [0]

    def load_wu(g):
        wu_ = wus[g % 2]
        dma("pool", wu_, w_up[:, g * 256:(g + 1) * 256].rearrange("(k p) c -> p k c", p=128), r=[w_up], w=[wu_])

    def load_wd(g):
        wd_ = wds[0]
        dma("pool", wd_, w_down[g * 256:(g + 1) * 256, :].rearrange("(q p) c -> p q c", p=128), r=[w_down], w=[wd_])

    def up(k):
        g, ti = steps[k]; c0, T, is_s = tiles34[ti]
        if ti == 0 and g + 1 < 16:
            load_wu(g + 1)
        wu_ = wus[g % 2]; aT_ = aTs[k % 2]; rT_ = rTs[k % 2]
        for q in range(2):
            b_ = banks[upb[0] % 4]; upb[0] += 1
            for kk_ in range(8):
                mm(b_[:, 0:T], wu_[:, kk_, q * 128:(q + 1) * 128], h2T[:, kk_, c0:c0 + T], kk_ == 0, kk_ == 7, r=[wu_, (h2T, 0)], w=[b_])
            act(rT_[:, 0, 0:T], b_[:, 0:T], AF.Relu, r=[b_], w=[rT_])
            tt(aT_[:, q, 0:T], rT_[:, 0, 0:T], rT_[:, 0, 0:T], ALU.mult, eng=("pool" if q == 1 else "dve"), r=[rT_], w=[aT_])

    def down(k):
        g, ti = steps[k]; c0, T, is_s = tiles34[ti]
        wd_ = wds[g % 2]; aT_ = aTs[k % 2]
        for f in range(8):
            b_ = banks[4 + dnb[0] % 4]; dnb[0] += 1
            for q in range(2):
                mm(b_[:, 0:T], wd_[:, q, f * 128:(f + 1) * 128], aT_[:, q, 0:T], q == 0, q == 1, r=[wd_, aT_], w=[b_])
            resid_add(f, c0, T, is_s, b_, gt2p, gt2s, split=(f >= 6))

    load_wu(0); load_wd(0)
    up(0)
    for k in range(len(steps)):
        if k + 1 < len(steps):
            up(k + 1)
        down(k)
        g_, ti_ = steps[k]
        if ti_ == len(tiles34) - 1 and g_ + 1 < 16:
            load_wd(g_ + 1)
    fnp = pv[:, rows["fn"]:rows["fn"] + 8]
    for ti, (c0, T, is_s) in enumerate(tiles34):
        sq = nbuf["sq"]; rstd = nbuf["rstd"]; tmp = nbuf["tmp"]
        xv_ = xT[:, :, c0:c0 + T]
        xk8 = [(xT, f_) for f_ in range(8)]
        tt(sq[:, :, 0:T], xv_, xv_, ALU.mult, r=xk8, w=[sq])
        b_ = bank()
        for fc in range(8):
            mm(b_[:, 0:T], onesB, sq[:, fc, 0:T], fc == 0, fc == 7, w=[b_])
        ts(rstd[:, 0:T], b_[:, 0:T], 1.0 / D, RMS_EPS, ALU.mult, ALU.add, r=[b_], w=[rstd])
        r1 = nbuf["r1"]; r2 = nbuf["r2"]
        cp(r1[:, 0:T], rstd[:, 0:T], eng="act", r=[rstd], w=[r1])
        rsqrt_dve(rstd[:, 0:T], r1[:, 0:T], r2[:, 0:T], [rstd, r1, r2])
        tt(tmp[:, :, 0:T], xv_, rstd[:, 0:T].unsqueeze(1).broadcast_to([128, 8, T]), ALU.mult, r=xk8 + [rstd], w=[tmp])
        tt(tmp[:, :, 0:T], tmp[:, :, 0:T], fnp.unsqueeze(2).broadcast_to([128, 8, T]), ALU.mult, r=[tmp, pv], w=[tmp])
        for sub in range(0, T, 128):
            n = min(128, T - sub)
            ytm = nbuf["xtm"][nbuf["i"] % 2]; nbuf["i"] += 1
            for half in range(2):
                b_ = bank()
                for f4 in range(4):
                    fc = half * 4 + f4
                    tr(b_[0:n, f4 * 128:(f4 + 1) * 128], tmp[:, fc, sub:sub + n], identF, r=[tmp, identF], w=[b_])
                cp(ytm[0:n, half * 512:(half + 1) * 512], b_[0:n, :], eng="act", r=[b_], w=[ytm])
            dst = yp[c0 + sub:c0 + sub + n, :] if not is_s else ys[sub:sub + n, :]
            dma("sp", dst, ytm[0:n, :], r=[ytm], w=[yp if not is_s else ys])
    n = P.emit()
    print('SBUF bytes/partition: G', G.tot, 'max', SBMAX[0])
    S1.close(); G.close()
    return nc, dbg, n


_CACHE = {}


def _core_inputs(inp, c):
    f = lambda a: np.ascontiguousarray(a, dtype=np.float32)
    m = {"xp": inp["x_prompt"][c], "xs": inp["x_sample"][16 * c:16 * c + 16].reshape(64, D),
         "sh": inp["state_hgrn"][0, 16 * c:16 * c + 16], "sr": inp["state_rwkv"][0, 16 * c:16 * c + 16],
         "ss": inp["state_shift"][0, 16 * c:16 * c + 16], "cp": inp["c_prompt"][c:c + 1], "cs": inp["c_sample"][16 * c:16 * c + 16]}
    for k in ("norm1_w", "norm2_w", "rwkv_w0", "rwkv_a0", "rwkv_k_k", "rwkv_k_a", "rwkv_lnx_w", "rwkv_lnx_b", "rwkv_r_k", "final_norm_w"):
        m[k] = inp[k].reshape(8, 128)
    m["ada_w"] = inp["ada_w"][0]; m["ada_b"] = inp["ada_b"].reshape(48, 128); m["w_in"] = inp["w_in"][0]
    m["lb_logits"] = inp["lb_logits"].reshape(16, 128); m["hgrn_norm_w"] = inp["hgrn_norm_w"].reshape(1, 128)
    m["rwkv_mu"] = inp["rwkv_mu"].reshape(1, RC); m["rwkv_w2"] = inp["rwkv_w2"][0]; m["rwkv_a2"] = inp["rwkv_a2"][0]
    m["rwkv_g2"] = inp["rwkv_g2"][0]; m["w_out"] = inp["w_out"][0]; m["w_up"] = inp["w_up"][0]; m["w_down"] = inp["w_down"][0]
    return {k: f(v) for k, v in m.items()}


def kernel(**inputs):
    inp = {k: np.asarray(v) for k, v in inputs.items()}
    if "nc" not in _CACHE:
        _CACHE["nc"] = build()[0]
    nc = _CACHE["nc"]
    in_maps = [_core_inputs(inp, c) for c in range(8)]
    res = run_bass_kernel_spmd(nc, in_maps, core_ids=list(range(8)))
    R = res.results
    g = lambda n: [np.asarray(R[c][n], dtype=np.float32) for c in range(8)]
    y_prompt = np.stack(g("yp"), 0)
    y_sample = np.concatenate([a.reshape(16, 4, D) for a in g("ys")], 0)
    hgrn_p = np.stack(g("hp"), 0)[None]
    rwkv_p = np.stack(g("rp"), 0)[None]
    shift_p = np.concatenate(g("shp"), 0)[None]
    hgrn_s = np.concatenate(g("hs"), 0)[None]
    rwkv_s = np.concatenate(g("rs"), 0)[None]
    shift_s = np.concatenate(g("shs"), 0)[None]
    return (y_prompt, y_sample, hgrn_p, rwkv_p, shift_p, hgrn_s, rwkv_s, shift_s)
```

```python
import contextlib
import numpy as np
import concourse.bass as bass
import concourse.mybir as mybir
from concourse.bass_utils import run_bass_kernel_spmd

F32 = mybir.dt.float32
BF16 = mybir.dt.bfloat16
AF = mybir.ActivationFunctionType
ALU = mybir.AluOpType
AX = mybir.AxisListType

D = 1024
SEQ = 2048
NSEQ_S = 16
TS = 4
NT = SEQ + NSEQ_S * TS
INC = 9504
RC = 3360
NDMASEM = 12
PE_FIX = 0.03
DVE_FIX = 0.10
ACT_FIX = 0.25
POOL_FIX = 0.30
PE_RATE = 3200.0
SAME_ENG_RELAX = ()
RMS_EPS = 1e-6
LNX_EPS = 64e-5
C1 = -0.5 * float(np.exp(-0.5))


class Prog:
    def __init__(self, nc):
        self.nc = nc
        self.ops = []
        self.last_w = {}
        self.readers = {}
        self.bar = False
        self.sink = None
        self.lat_same = 0.4
        self.lat_cross = 1.0

    def peek(self, r, w):
        deps = set()
        rk = [self.key(x) for x in r]; wk = [self.key(x) for x in w]
        if self.bar:
            rk.append("__bar")
        for k in rk:
            if k in self.last_w:
                deps.add(self.last_w[k])
        for k in wk:
            if k in self.last_w:
                deps.add(self.last_w[k])
            deps |= self.readers.get(k, set())
        return deps

    @staticmethod
    def dur(eng, dma, n):
        if dma:
            return 0.1
        if eng == "pe":
            return PE_FIX + n / PE_RATE
        if eng == "dve":
            return DVE_FIX + n / 960.0
        if eng == "act":
            return ACT_FIX + n / 1200.0
        if eng == "pool":
            return POOL_FIX + n / 400.0
        return 0.1

    def merge(self, lists, lead=0):
        L = list(lists); K = len(L)
        done = set(); pos = [0] * K
        end = {}; free = {}
        tot = [max(1, len(x)) for x in L]

        def est_start(x):
            eng, fn, r, w, dma, n = x
            deps = self.peek(r, w)
            t = free.get(eng, 0.0)
            for d in deps:
                if d in end:
                    lat = self.lat_same if self.ops[d]["eng"] == eng else self.lat_cross
                    if self.ops[d]["eng"] == "pe" and eng == "pe":
                        lat = 0.0
                    t = max(t, end[d] + lat)
            return t

        while any(pos[i] < len(L[i]) for i in range(K)):
            progressed = False
            for si in range(K):
                while pos[si] < len(L[si]) and (L[si][pos[si]][0] == "mark" or (L[si][pos[si]][0] == "need" and L[si][pos[si]][1] in done)):
                    if L[si][pos[si]][0] == "mark":
                        done.add(L[si][pos[si]][1])
                    pos[si] += 1; progressed = True
            cand = []
            for si in range(K):
                if pos[si] >= len(L[si]):
                    continue
                x = L[si][pos[si]]
                if x[0] == "need":
                    continue
                cand.append((est_start(x), pos[si] / tot[si], si))
            if not cand:
                if progressed:
                    continue
                if all(pos[i] >= len(L[i]) for i in range(K)):
                    break
                raise RuntimeError("merge deadlock")
            t, _, si = min(cand)
            x = L[si][pos[si]]; pos[si] += 1
            eng, fn, r, w, dma, n = x
            idx = self.add(eng, fn, r, w, dma, n)
            d_ = self.dur(eng, dma, n)
            free[eng] = t + d_
            end[idx] = t + d_ + (2.5 if dma else 0.0)

    @staticmethod
    def key(x):
        if isinstance(x, tuple):
            if isinstance(x[0], str):
                return x
            return (x[0].tensor.name, x[1])
        if isinstance(x, str):
            return x
        return x.tensor.name

    def add(self, eng, fn, r=(), w=(), dma=False, n=256):
        if self.sink is not None:
            if eng in ("mark", "need"):
                self.sink.append((eng, fn))
            else:
                self.sink.append((eng, fn, list(r), list(w), dma, n))
            return None
        deps = set()
        rk = [self.key(x) for x in r]
        wk = [self.key(x) for x in w]
        if self.bar:
            rk.append("__bar")
        for k in rk:
            if k in self.last_w:
                deps.add(self.last_w[k])
        for k in wk:
            if k in self.last_w:
                deps.add(self.last_w[k])
            deps |= self.readers.get(k, set())
        idx = len(self.ops)
        self.ops.append(dict(eng=eng, fn=fn, deps=deps, dma=dma, users=0, stream=getattr(self, "tag", None), rk=rk, wk=wk))
        for k in rk:
            self.readers.setdefault(k, set()).add(idx)
        for k in wk:
            self.last_w[k] = idx
            self.readers[k] = set()
        return idx

    def barrier(self, dummy):
        allk = set(self.last_w.keys()) | set(self.readers.keys())
        allk.discard("__bar")
        self.bar = False
        self.add("dve", lambda e: e.memset(dummy, 0.0), r=(), w=list(allk) + ["__bar"])
        self.last_w = {"__bar": self.last_w["__bar"]}
        self.readers = {}
        self.bar = True

    def emit(self):
        nc = self.nc
        engs = {"pe": nc.tensor, "act": nc.scalar, "dve": nc.vector, "pool": nc.gpsimd, "sp": nc.sync}
        ops = self.ops

        seqn = {}
        for i_, o_ in enumerate(ops):
            if not o_["dma"]:
                seqn[o_["eng"]] = seqn.get(o_["eng"], 0) + 1
                o_["seq"] = seqn[o_["eng"]]

        def skip(d, o):
            if ops[d]["dma"] or o["dma"]:
                return False
            if ops[d]["eng"] != o["eng"]:
                return False
            if o["eng"] == "pe":
                return True
            if o["eng"] in SAME_ENG_RELAX and o["seq"] - ops[d]["seq"] >= 2:
                return True
            return False

        for o in ops:
            for d in o["deps"]:
                if not skip(d, o):
                    ops[d]["users"] += 1
        with contextlib.ExitStack() as st:
            csem = {e: st.enter_context(nc.semaphore("c_" + e)) for e in engs}
            dsem = {e: [st.enter_context(nc.semaphore(f"d_{e}{i}")) for i in range(NDMASEM)]
                    for e in ("act", "pool", "sp")}
            ccnt = {e: 0 for e in engs}
            dcnt = {e: [0] * NDMASEM for e in dsem}
            drr = {e: 0 for e in dsem}
            waited = {e: {} for e in engs}
            sig = {}

            def wait(e, sem, val):
                k = id(sem)
                if val <= 0 or waited[e].get(k, 0) >= val:
                    return
                engs[e].wait_ge(sem, val)
                waited[e][k] = val

            for i, o in enumerate(ops):
                e = o["eng"]
                for d in sorted(o["deps"]):
                    if d not in sig or skip(d, o):
                        continue
                    s, v = sig[d]
                    wait(e, s, v)
                if o["dma"]:
                    j = drr[e]
                    drr[e] = (j + 1) % NDMASEM
                    s = dsem[e][j]
                    wait(e, s, dcnt[e][j])
                    ins = o["fn"](engs[e])
                    dcnt[e][j] += 16
                    ins.then_inc(s, 16)
                    sig[i] = (s, dcnt[e][j])
                else:
                    ins = o["fn"](engs[e])
                    if o["users"] > 0:
                        ccnt[e] += 1
                        ins.then_inc(csem[e], 1)
                        sig[i] = (csem[e], ccnt[e])
            for e in dsem:
                for j in range(NDMASEM):
                    if dcnt[e][j] > 0:
                        nc.sync.wait_ge(dsem[e][j], dcnt[e][j])
        return len(ops)


def build(debug=None):
    nc = bass.Bass("TRN2", target_bir_lowering=False)
    P = Prog(nc)
    din = lambda n, s: nc.dram_tensor(n, list(s), F32, kind="ExternalInput").ap()
    dout = lambda n, s: nc.dram_tensor(n, list(s), F32, kind="ExternalOutput").ap()
    xp = din("xp", [SEQ, D]); xs = din("xs", [64, D])
    sh_in = din("sh", [16, 8, 128, 128]); sr_in = din("sr", [16, 16, 64, 64]); ss_in = din("ss", [16, RC])
    cpr = din("cp", [1, D]); csm = din("cs", [16, D])
    norm1_w = din("norm1_w", [8, 128]); norm2_w = din("norm2_w", [8, 128])
    ada_w = din("ada_w", [D, 6 * D]); ada_b = din("ada_b", [48, 128])
    w_in = din("w_in", [D, INC]); lb_logits = din("lb_logits", [16, 128])
    hgrn_norm_w = din("hgrn_norm_w", [1, 128]); mu = din("rwkv_mu", [1, RC])
    w0 = din("rwkv_w0", [8, 128]); w2 = din("rwkv_w2", [64, D]); a0 = din("rwkv_a0", [8, 128])
    a2 = din("rwkv_a2", [64, D]); g2 = din("rwkv_g2", [160, D]); k_k = din("rwkv_k_k", [8, 128])
    k_a = din("rwkv_k_a", [8, 128]); r_k = din("rwkv_r_k", [8, 128]); lnx_w = din("rwkv_lnx_w", [8, 128])
    lnx_b = din("rwkv_lnx_b", [8, 128]); w_out = din("w_out", [D, D]); w_up = din("w_up", [D, 4 * D])
    w_down = din("w_down", [4 * D, D]); fnw = din("final_norm_w", [8, 128])
    yp = dout("yp", [SEQ, D]); ys = dout("ys", [64, D])
    hp = dout("hp", [8, 128, 128]); rp = dout("rp", [16, 64, 64]); shp = dout("shp", [1, RC])
    hs = dout("hs", [16, 8, 128, 128]); rs = dout("rs", [16, 16, 64, 64]); shs = dout("shs", [16, RC])
    dbg = {}

    cnt = [0]

    def uname(n):
        cnt[0] += 1
        return f"{n}_{cnt[0]}"

    SBTOT = [0]; SBMAX = [0]

    class Scope:
        def __init__(self):
            self.st = contextlib.ExitStack()

        def sb(self, name, shape, dt=F32):
            nb = int(np.prod(shape[1:])) * (4 if dt == F32 else 2)
            SBTOT[0] += nb; self.tot = getattr(self, "tot", 0) + nb
            SBMAX[0] = max(SBMAX[0], SBTOT[0])
            return self.st.enter_context(nc.sbuf_tensor(uname(name), list(shape), dt)).ap()

        def close(self):
            SBTOT[0] -= getattr(self, "tot", 0)
            self.st.close()

    G = Scope()
    banks = [nc.alloc_psum_tensor(f"bank{i}", [128, 512], F32).ap() for i in range(8)]
    bk = [0]

    pools = {None: list(range(8)), "H": [1, 2], "RF": [3, 4], "RB": [6, 7]}
    cur = [None]
    bkp = {"H": 0, "RF": 0, "RB": 0}

    def bank():
        if cur[0] is None:
            b = banks[bk[0] % 8]
            bk[0] += 1
            return b
        pl = pools[cur[0]]
        b = banks[pl[bkp[cur[0]] % len(pl)]]
        bkp[cur[0]] += 1
        return b

    def dump(name, ap, shape, keys=None):
        if debug is None or (name not in debug and "all" not in debug):
            return
        d = nc.dram_tensor("dbg_" + name, list(shape), F32, kind="ExternalOutput").ap()
        dbg[name] = d
        q = "pool" if ap.dtype != F32 else "sp"
        rk = [ap] + [(ap, t_) for t_ in range(16)] if keys is None else keys
        P.add(q, lambda e: e.dma_start(out=d, in_=ap), r=rk, w=[d], dma=True)

    def dma(q, out, in_, r=(), w=()):
        P.add(q, lambda e: e.dma_start(out=out, in_=in_), r=list(r) or [in_], w=list(w) or [out], dma=True)

    def act(out, in_, func, bias=0.0, scale=1.0, r=(), w=(), accum=None):
        rr = list(r) or [in_]
        if not isinstance(bias, float):
            rr.append(bias)
        if not isinstance(scale, float):
            rr.append(scale)
        ww = list(w) or [out]
        if accum is not None:
            ww.append(accum)
        if accum is not None:
            P.add("act", lambda e: e.activation(out=out, in_=in_, func=func, bias=bias, scale=scale, accum_out=accum), r=rr, w=ww, n=out.free_size())
        else:
            P.add("act", lambda e: e.activation(out=out, in_=in_, func=func, bias=bias, scale=scale), r=rr, w=ww, n=out.free_size())

    def tt(out, in0, in1, op, eng="dve", r=(), w=()):
        P.add(eng, lambda e: e.tensor_tensor(out=out, in0=in0, in1=in1, op=op), r=list(r) or [in0, in1], w=list(w) or [out], n=out.free_size())

    def ts(out, in0, s1, s2, op0, op1=None, eng="dve", r=(), w=()):
        rr = list(r) or [in0]
        for s in (s1, s2):
            if s is not None and not isinstance(s, float):
                rr.append(s)
        if op1 is None:
            P.add(eng, lambda e: e.tensor_scalar(out=out, in0=in0, scalar1=s1, scalar2=None, op0=op0), r=rr, w=list(w) or [out], n=out.free_size())
        else:
            P.add(eng, lambda e: e.tensor_scalar(out=out, in0=in0, scalar1=s1, scalar2=s2, op0=op0, op1=op1), r=rr, w=list(w) or [out], n=out.free_size())

    def aff(out, in_, scale, bias, r=(), w=()):
        act(out, in_, AF.Identity, bias=bias, scale=scale, r=r, w=w)

    def stt(out, in0, scalar, in1, op0, op1, r=(), w=()):
        rr = list(r) or [in0, in1]
        if not isinstance(scalar, float):
            rr.append(scalar)
        P.add("dve", lambda e: e.scalar_tensor_tensor(out=out, in0=in0, scalar=scalar, in1=in1, op0=op0, op1=op1), r=rr, w=list(w) or [out], n=out.free_size())

    def cp(out, in_, eng="dve", r=(), w=()):
        if eng == "act":
            act(out, in_, AF.Copy, r=r, w=w)
        else:
            P.add(eng, lambda e: e.tensor_copy(out=out, in_=in_), r=list(r) or [in_], w=list(w) or [out], n=out.free_size())

    def memset(ap, val, eng="pool", w=()):
        P.add(eng, lambda e: e.memset(ap, val), w=list(w) or [ap], n=ap.free_size())

    def mm(out, lhsT, rhs, start, stop, r=(), w=()):
        P.add("pe", lambda e: e.matmul(out, lhsT, rhs, start=start, stop=stop), r=list(r) or [lhsT, rhs], w=list(w) or [out], n=rhs.free_size() + 60)

    def tr(out, in_, ident, r=(), w=()):
        P.add("pe", lambda e: e.transpose(out=out, in_=in_, identity=ident), r=list(r) or [in_, ident], w=list(w) or [out], n=ident.free_size() + 60)

    I32 = mybir.dt.int32

    def rsqrt_dve(y, x, t1, keys):
        P.add("dve", lambda e: e.tensor_scalar(out=y.bitcast(I32), in0=x.bitcast(I32), scalar1=-0.5, scalar2=1597463007.0,
                                               op0=ALU.mult, op1=ALU.add), r=keys, w=keys)
        for _ in range(2):
            tt(t1, y, y, ALU.mult, r=keys, w=keys)
            stt(t1, t1, -0.5, x, ALU.mult, ALU.mult, r=keys, w=keys)
            stt(y, t1, 1.5, y, ALU.add, ALU.mult, r=keys, w=keys)

    def rsqrt_(out, in_, scale, eps, tA, tB):
        ts(tA, in_, scale, eps, ALU.mult, ALU.add)
        rsqrt_dve(out, tA, tB, [out, tA, tB])

    ones128 = G.sb("ones128", [128, 128]); identF = G.sb("identF", [128, 128]); identB = G.sb("identB", [128, 128], BF16)
    memset(ones128, 1.0)
    P.add("pool", lambda e: e.affine_select(out=identF, in_=ones128, pattern=[[1, 128]], compare_op=ALU.is_equal,
                                           fill=0.0, base=0, channel_multiplier=-1), r=[ones128], w=[identF])
    cp(identB, identF)
    onesB = G.sb("onesB", [128, 128], BF16)
    cp(onesB, ones128)
    blkones = G.sb("blkones", [128, 128], BF16)
    memset(blkones, 0.0)
    memset(blkones[0:64, 0:64], 1.0, w=[blkones]); memset(blkones[64:128, 64:128], 1.0, w=[blkones])
    blkneg = G.sb("blkneg", [128, 128])
    memset(blkneg, 0.0)
    memset(blkneg[0:64, 0:64], -1.0, w=[blkneg]); memset(blkneg[64:128, 64:128], -1.0, w=[blkneg])
    blkpos = G.sb("blkpos", [128, 128])
    memset(blkpos, 0.0)
    memset(blkpos[0:64, 0:64], 1.0, w=[blkpos]); memset(blkpos[64:128, 64:128], 1.0, w=[blkpos])
    dummy = G.sb("dummy", [128, 1])
    masks = {}
    for C in (64, 4):
        mi = G.sb(f"mi{C}", [C, C]); ms = G.sb(f"ms{C}", [C, C]); mls = G.sb(f"mls{C}", [C, C])
        P.add("pool", lambda e, mi=mi, C=C: e.affine_select(out=mi, in_=ones128[0:C, 0:C], pattern=[[1, C]], compare_op=ALU.is_ge,
                                                         fill=0.0, base=0, channel_multiplier=-1), r=[ones128], w=[mi])
        P.add("pool", lambda e, ms=ms, C=C: e.affine_select(out=ms, in_=ones128[0:C, 0:C], pattern=[[1, C]], compare_op=ALU.is_gt,
                                                         fill=0.0, base=0, channel_multiplier=-1), r=[ones128], w=[ms])
        P.add("pool", lambda e, mls=mls, C=C: e.affine_select(out=mls, in_=ones128[0:C, 0:C], pattern=[[-1, C]], compare_op=ALU.is_gt,
                                                           fill=0.0, base=0, channel_multiplier=1), r=[ones128], w=[mls])
        msn = G.sb(f"msn{C}", [C, C]); mlsn = G.sb(f"mlsn{C}", [C, C])
        ts(msn, ms, -1.0, None, ALU.mult); ts(mlsn, mls, -1.0, None, ALU.mult)
        masks[C] = dict(incl=mi, strict=ms, nstrict=msn, nlstrict=mlsn)

    stgA = G.sb("stgA", [128, 128]); stgB = G.sb("stgB", [48, 128])
    memset(stgA, 0.0)
    rows = {}
    r0 = 0
    for nm, src, n in (("n1", norm1_w, 8), ("n2", norm2_w, 8), ("lb", lb_logits, 16), ("w0", w0, 8), ("a0", a0, 8),
                       ("kk", k_k, 8), ("ka", k_a, 8), ("rk", r_k, 8), ("lw", lnx_w, 8), ("lbb", lnx_b, 8), ("fn", fnw, 8)):
        dma("sp", stgA[r0:r0 + n, :], src, w=[stgA]); rows[nm] = r0; r0 += n
    dma("sp", stgA[r0:r0 + 26, :], mu[:, 0:3328].rearrange("o (n p) -> (o n) p", p=128), w=[stgA]); rows["mu"] = r0; r0 += 26
    dma("sp", stgA[r0:r0 + 1, 0:32], mu[:, 3328:3360], w=[stgA]); rows["mu2"] = r0; r0 += 1
    assert r0 <= 128
    dma("sp", stgB, ada_b)
    pv = G.sb("pv", [128, 128]); pvB = G.sb("pvB", [128, 48])
    b_ = bank(); tr(b_[:, 0:128], stgA, identF); cp(pv, b_[:, 0:128])
    b_ = bank(); tr(b_[:, 0:48], stgB, identF[0:48, 0:48]); cp(pvB, b_[:, 0:48])
    col = lambda nm, i=0: pv[:, rows[nm] + i: rows[nm] + i + 1]
    pv2 = G.sb("pv2", [128, 128]); ts(pv2, pv, -1.0, 1.0, ALU.mult, ALU.add)
    col2 = lambda nm, i=0: pv2[:, rows[nm] + i: rows[nm] + i + 1]
    hs1 = G.sb("hs1", [128, 8]); hs2 = G.sb("hs2", [128, 8]); hs1n = G.sb("hs1n", [128, 8]); hs2c = G.sb("hs2c", [128, 8])
    tt(hs1, pv[:, rows["lb"]:rows["lb"] + 8], pv[:, rows["lb"] + 8:rows["lb"] + 16], ALU.subtract)
    act(hs2, hs1, AF.Tanh, scale=0.5)
    ts(hs1, hs2, -0.25, 0.25, ALU.mult, ALU.add)
    ts(hs2, hs2, 0.25, 0.75, ALU.mult, ALU.add)
    ts(hs1n, hs1, -1.0, None, ALU.mult)
    ts(hs2c, hs2, -1.0, 1.0, ALU.mult, ALU.add)
    hw0 = G.sb("hw0", [128, 8]); ha0 = G.sb("ha0", [128, 8]); omka = G.sb("omka", [128, 8])
    ts(hw0, pv[:, rows["w0"]:rows["w0"] + 8], 0.5, None, ALU.mult)
    ts(ha0, pv[:, rows["a0"]:rows["a0"] + 8], 0.5, None, ALU.mult)
    ts(omka, pv[:, rows["ka"]:rows["ka"] + 8], -1.0, 1.0, ALU.mult, ALU.add)
    nwbc = G.sb("nwbc", [64, 128]); dma("sp", nwbc, hgrn_norm_w.partition_broadcast(64))

    modT = G.sb("modT", [128, 48, 17])
    g1p = G.sb("g1p", [128, 8]); g2p = G.sb("g2p", [128, 8])
    sh1p = G.sb("sh1p", [128, 8]); sh2p = G.sb("sh2p", [128, 8]); gt1p = G.sb("gt1p", [128, 8]); gt2p = G.sb("gt2p", [128, 8])
    mT = G.sb("mT", [128, 8, NT], BF16)
    S2 = Scope()
    hT = S2.sb("hT", [128, 8, NT], BF16)
    S0 = Scope()
    c17 = S0.sb("c17", [17, D]); s17 = S0.sb("s17", [17, D]); s17b = S0.sb("s17b", [17, D], BF16); scT = S0.sb("scT", [128, 8, 32], BF16)
    dma("sp", c17[0:1, :], cpr, w=[c17]); dma("sp", c17[1:17, :], csm, w=[c17])
    act(s17, c17, AF.Tanh, scale=0.5)
    ts(s17, s17, 0.5, 0.5, ALU.mult, ALU.add)
    tt(s17b, s17, c17, ALU.mult)
    b_ = bank(); bb = b_.bitcast(BF16)
    for k in range(8):
        tr(bb[:, k * 32:k * 32 + 17], s17b[:, k * 128:(k + 1) * 128], identB[0:17, 0:17], w=[b_])
    cp(scT[:, :, 0:17], bb[:, 0:256].rearrange("p (k c) -> p k c", c=32)[:, :, 0:17], r=[b_])
    awb = [S0.sb("awb", [128, 8, 512], BF16) for _ in range(2)]
    def ada_group(g):
            wt = awb[g % 2]
            dma("pool", wt, ada_w[:, g * 512:(g + 1) * 512].rearrange("(k p) c -> p k c", p=128), r=[ada_w], w=[wt])
            b_ = bank()
            for m in range(4):
                for k in range(8):
                    mm(b_[:, m * 32:m * 32 + 17], wt[:, k, m * 128:(m + 1) * 128], scT[:, k, 0:17], k == 0, k == 7, w=[b_])
            for m in range(4):
                ts(modT[:, g * 4 + m, :], b_[:, m * 32:m * 32 + 17], pvB[:, g * 4 + m:g * 4 + m + 1], None, ALU.add, r=[b_, pvB], w=[modT])

    def modp(part):
        return modT[:, part * 8:(part + 1) * 8, 0]
    def mods(part):
        return modT[:, part * 8:(part + 1) * 8, 1:17].unsqueeze(3).broadcast_to([128, 8, 16, 4])
    def mods3(part):
        return modT[:, part * 8:(part + 1) * 8, 1:17]

    def mk_mods(dst, part, nm=None):
        for t4 in range(4):
            if nm is None:
                cp(dst[:, :, t4::4], mods3(part), r=[modT], w=[dst])
            else:
                stt(dst[:, :, t4::4], mods3(part), 1.0, pv[:, rows[nm]:rows[nm] + 8].unsqueeze(2).broadcast_to([128, 8, 16]),
                    ALU.add, ALU.mult, r=[modT, pv], w=[dst])

    tiles = [(i * 256, 256, 64, 4, False) for i in range(8)] + [(SEQ + 16 * i, 16, 4, 4, True) for i in range(4)]

    def norm_to(hT_dst, xT_t, T, gp, shp_, gs, shs_, is_s, ti, fkeys=False):
        xk_ = [(xT_t, f_) for f_ in range(8)] if fkeys else [xT_t]
        sq = nbuf["sq"]
        tt(sq[:, :, 0:T], xT_t[:, :, 0:T], xT_t[:, :, 0:T], ALU.mult, r=xk_, w=[sq])
        b_ = bank()
        for fc in range(8):
            mm(b_[:, 0:T], onesB, sq[:, fc, 0:T], fc == 0, fc == 7, w=[b_])
        rstd = nbuf["rstd"]
        ts(rstd[:, 0:T], b_[:, 0:T], 1.0 / D, RMS_EPS, ALU.mult, ALU.add, r=[b_], w=[rstd])
        r1 = nbuf["r1"]; r2 = nbuf["r2"]
        cp(r1[:, 0:T], rstd[:, 0:T], eng="act", r=[rstd], w=[r1])
        rsqrt_dve(rstd[:, 0:T], r1[:, 0:T], r2[:, 0:T], [rstd, r1, r2])
        tmp = nbuf["tmp"]
        tt(tmp[:, :, 0:T], xT_t[:, :, 0:T], rstd[:, 0:T].unsqueeze(1).broadcast_to([128, 8, T]), ALU.mult, r=xk_ + [rstd], w=[tmp])
        if not is_s:
            for fc in range(8):
                ts(hT_dst[:, fc, :], tmp[:, fc, 0:T], gp[:, fc:fc + 1], shp_[:, fc:fc + 1], ALU.mult, ALU.add,
                   r=[tmp, gp, shp_], w=[(hT_dst, ti)])
        else:
            tt(tmp[:, :, 0:T], tmp[:, :, 0:T], gs, ALU.mult, r=[tmp, gs], w=[tmp])
            tt(hT_dst, tmp[:, :, 0:T], shs_, ALU.add, r=[tmp, shs_], w=[(hT_dst, ti)])

    def load_xT(xT_t, c0, T, ti, fkeys=False):
        for sub in range(0, T, 128):
            n = min(128, T - sub)
            xtm = nbuf["xtm"][nbuf["i"] % 2]; nbuf["i"] += 1
            src = xp[c0 + sub:c0 + sub + n, :] if c0 < SEQ else xs[c0 - SEQ + sub:c0 - SEQ + sub + n, :]
            dma("sp", xtm[0:n, :], src, w=[xtm])
            for half in range(2):
                b_ = bank()
                for f4 in range(4):
                    fc = half * 4 + f4
                    tr(b_[:, f4 * 128:f4 * 128 + n], xtm[0:n, fc * 128:(fc + 1) * 128], identF[0:n, 0:n], w=[b_])
                cp(xT_t[:, half * 4:half * 4 + 4, sub:sub + n], b_.rearrange("p (f t) -> p f t", t=128)[:, :, 0:n], eng="act", r=[b_],
                   w=([(xT_t, f_) for f_ in range(half * 4, half * 4 + 4)] if fkeys else [xT_t]))

    for g in range(4):
        ada_group(g)
    stt(g1p, modp(1), 1.0, pv[:, rows["n1"]:rows["n1"] + 8], ALU.add, ALU.mult, r=[modT, pv])
    cp(sh1p, modp(0), r=[modT])
    S1 = Scope()
    g1s = S1.sb("g1s", [128, 8, 64]); sh1s = S1.sb("sh1s", [128, 8, 64])
    mk_mods(sh1s, 0); mk_mods(g1s, 1, "n1")
    nbuf = dict(sq=S1.sb("sq", [128, 8, 512], BF16), rstd=S1.sb("rstd", [128, 512]), tmp=S1.sb("ntmp", [128, 8, 512]), r1=S1.sb("r1", [128, 512]), r2=S1.sb("r2", [128, 512]),
                xtm=[S1.sb("xtm", [128, D]) for _ in range(2)], i=0)
    xT_ts = [S1.sb("xT_t", [128, 8, 512]) for _ in range(2)]
    gnext = 4
    for ti, (c0, T, C, NCH, is_s) in enumerate(tiles):
        xT_t = xT_ts[ti % 2]
        load_xT(xT_t, c0, T, ti)
        so = max(c0 - SEQ, 0)
        norm_to(hT[:, :, c0:c0 + T], xT_t, T, g1p, sh1p, g1s[:, :, so:so + min(T, 16)], sh1s[:, :, so:so + min(T, 16)], is_s, ti)
        if gnext < 12:
            ada_group(gnext); gnext += 1
    while gnext < 12:
        ada_group(gnext); gnext += 1
    stt(g2p, modp(4), 1.0, pv[:, rows["n2"]:rows["n2"] + 8], ALU.add, ALU.mult, r=[modT, pv])
    for dst, part in ((gt1p, 2), (sh2p, 3), (gt2p, 5)):
        cp(dst, modp(part), r=[modT])
    S1.close()
    S0.close()
    P.barrier(dummy)

    W = lambda name, shape, dt=F32: S2.sb(name, shape, dt)
    wbH = W("wbH", [128, 8, 640], BF16); wbR = W("wbR", [128, 8, 512], BF16)
    wl = wbH[:, :, 0:288]
    w2b = W("w2b", [128, D], BF16)
    g2b = W("g2b", [128, D], BF16); g2c = W("g2c", [32, D], BF16)
    TW = W("TW", [128, NT], BF16); SG1 = W("SG1", [128, NT], BF16); SG2 = W("SG2", [32, NT], BF16)
    ssT = W("ssT", [128, 27, 16]); lastc = W("lastc", [128, 27, 17])
    prevs = W("prevs", [128, 8])
    dsh = W("dsh", [128, 256])
    S0all = W("S0all", [128, 4, 128]); S0bf = W("S0bf", [128, 4, 128], BF16); Souts = S0all
    Sh = W("Sh", [128, 128]); Shbf = W("Shbf", [128, 128], BF16)
    R0in = W("R0in", [128, 4, 128]); R0blk = W("R0blk", [128, 4, 128]); R0bf = W("R0bf", [128, 4, 128], BF16); Routs = R0blk
    Sr = W("Sr", [128, 128]); Srbf = W("Srbf", [128, 128], BF16); Srt = W("Srt", [128, 128])
    f32t = {n: W(n, [128, 256]) for n in ("th", "fS", "kS", "Pc", "rP", "Ga", "t1", "t2", "Gmm", "kap", "kmod", "bS", "gS",
                                          "xr", "xk", "xv", "rG", "aS", "u1", "u2")}
    f32t["n1"] = f32t["kmod"]; f32t["n2"] = f32t["u1"]
    oTs = [W("oT", [128, 256]) for _ in range(2)]
    PsmpH = W("PsmpH", [128, 5, 64]); PsmpR = W("PsmpR", [128, 4, 64])
    rstH = W("rstH", [128, 256]); rstR = W("rstR", [128, 256]); memset(rstH, 0.0); memset(rstR, 0.0)
    bft = {n: W(n, [128, 256], BF16) for n in ("qt", "kt", "kh", "vT", "sqb", "kpt", "bt", "bh", "ktl", "khr", "vTr")}
    KR = W("KR", [128, 4, 128], BF16)
    tm = {n: W(n, [64, 4, 128], BF16) for n in ("Vtm", "khtm", "W1", "Un", "Ktp", "onb")}
    Abm = {n: W(n, [64, 4, 128], BF16) for n in ("Y2", "Z2", "TTa", "TTb")}
    IFs = [dict(**{n: W(n, [64, 4, 128], BF16) for n in ("Z", "Y", "PbT", "PkT", "AkT", "Vtr", "Ktm", "bhtm", "khrtm")},
                rt=W("rt", [128, 256], BF16), **{n: W(n, [128, 256]) for n in ("Gm", "bon", "Gb")}) for _ in range(2)]
    attnT = W("attnT", [64, 4, 64], BF16)
    Osb = W("Osb", [64, 4, 128]); Osq = W("Osq", [64, 4, 128]); st_ = W("st_", [64, 16]); stA = W("stA", [64, 16]); stB = W("stB", [64, 16])
    srow = Osq.rearrange("p c f -> p (c f)")[0:17, :]
    Osb2 = W("Osb2", [64, 4, 128]); st3 = W("st3", [64, 16]); st4 = W("st4", [64, 16]); stC = W("stC", [64, 16]); stD = W("stD", [64, 16])
    Mblk = W("Mblk", [128, 4, 128], BF16); Ncomb = W("Ncomb", [128, 4, 128]); RpT = W("RpT", [128, 256], BF16)

    dma("pool", w2b[0:64, :], w2, w=[w2b]); dma("pool", w2b[64:128, :], a2, w=[w2b])
    dma("pool", g2b, g2[0:128, :]); dma("pool", g2c, g2[128:160, :])
    dma("pool", wl, w_in[:, 7168:7456].rearrange("(k p) c -> p k c", p=128), r=[w_in], w=[wl])
    for p4 in range(0, 27, 4):
        n4 = min(4, 27 - p4); wdt = min(n4 * 128, RC - p4 * 128)
        memset(srow, 0.0, eng="dve")
        dma("sp", srow[0:16, 0:wdt], ss_in[:, p4 * 128:p4 * 128 + wdt], r=[ss_in], w=[srow])
        b_ = bank()
        for q in range(n4):
            tr(b_[:, q * 16:(q + 1) * 16], srow[0:16, q * 128:(q + 1) * 128], identF[0:16, 0:16], r=[srow, identF], w=[b_])
        cp(ssT[:, p4:p4 + n4, :], b_[:, 0:n4 * 16].rearrange("p (q s) -> p q s", s=16), r=[b_], w=[ssT])
    memset(prevs, 0.0); memset(lastc, 0.0)

    def proj(wt, c_lo, c_hi, ti, c0, T):
        b_ = bank(); n = c_hi - c_lo
        for k in range(8):
            mm(b_[0:n, 0:T], wt[:, k, c_lo:c_hi], hT[:, k, c0:c0 + T], k == 0, k == 7, r=[wt, (hT, ti)], w=[b_])
        return b_

    def proj_samples(wt, nseg, dstP):
        for sg in range(nseg):
            b_ = bank()
            for k in range(8):
                mm(b_[:, 0:64], wt[:, k, sg * 128:(sg + 1) * 128], hT[:, k, SEQ:SEQ + 64], k == 0, k == 7,
                   r=[wt] + [(hT, t_) for t_ in range(8, 12)], w=[b_])
            cp(dstP[:, sg, :], b_[:, 0:64], eng="act", r=[b_], w=[dstP])

    def shift_mix(dst, ps, n, T, is_s, mucol, pcol, part, last_tile, s0=0, omucol=None):
        aff(dsh[0:n, 0:T], ps[0:n, 0:T], omucol[0:n, :], 0.0, r=[ps, pv2], w=[dsh])
        if not is_s:
            stt(dst[0:n, 1:T], ps[0:n, 0:T - 1], mucol[0:n, :], dsh[0:n, 1:T], ALU.mult, ALU.add, r=[ps, dsh, pv], w=[dst])
            stt(dst[0:n, 0:1], prevs[0:n, pcol:pcol + 1], mucol[0:n, :], dsh[0:n, 0:1], ALU.mult, ALU.add, r=[prevs, dsh, pv], w=[dst])
            cp(prevs[0:n, pcol:pcol + 1], ps[0:n, T - 1:T], eng="act", r=[ps, dst], w=[prevs])
            if last_tile:
                cp(lastc[0:n, part, 0:1], ps[0:n, T - 1:T], eng="act", r=[ps], w=[lastc])
        else:
            p3 = ps[0:n, 0:T].rearrange("p (s t) -> p s t", t=4); d3 = dsh[0:n, 0:T].rearrange("p (s t) -> p s t", t=4)
            o3 = dst[0:n, 0:T].rearrange("p (s t) -> p s t", t=4)
            stt(o3[:, :, 1:4], p3[:, :, 0:3], mucol[0:n, :], d3[:, :, 1:4], ALU.mult, ALU.add, r=[ps, dsh, pv], w=[dst])
            stt(o3[:, :, 0], ssT[0:n, part, s0:s0 + T // 4], mucol[0:n, :], d3[:, :, 0], ALU.mult, ALU.add, r=[ssT, dsh, pv], w=[dst])
            cp(lastc[0:n, part, 1 + s0:1 + s0 + T // 4], p3[:, :, 3], eng="act", r=[ps], w=[lastc])

    def sigm(dst, src, r=(), w=(), bias=0.0):
        act(dst, src, AF.Tanh, bias=bias, scale=0.5, r=r, w=w)
        aff(dst, dst, 0.5, 0.5)

    for ti, (c0, T, C, NCH, is_s) in enumerate(tiles):
        last = (c0 + T == SEQ)
        b_ = proj(wl, 0, 128, ti, c0, T)
        shift_mix(f32t["t1"], b_, 128, T, is_s, col("mu", 24), 0, 24, last, max(c0 - SEQ, 0) // 4, omucol=col2("mu", 24))
        act(TW[0:64, c0:c0 + T], f32t["t1"][0:64, 0:T], AF.Tanh, r=[f32t["t1"]], w=[(TW, ti)])
        cp(TW[64:128, c0:c0 + T], f32t["t1"][64:128, 0:T], eng="act", r=[f32t["t1"]], w=[(TW, ti)])
        b_ = proj(wl, 128, 256, ti, c0, T)
        shift_mix(f32t["t1"], b_, 128, T, is_s, col("mu", 25), 1, 25, last, max(c0 - SEQ, 0) // 4, omucol=col2("mu", 25))
        sigm(f32t["t2"][:, 0:T], f32t["t1"][:, 0:T], r=[f32t["t1"]], w=[f32t["t2"]])
        cp(SG1[:, c0:c0 + T], f32t["t2"][:, 0:T], r=[f32t["t2"]], w=[(SG1, ti)])
        b_ = proj(wl, 256, 288, ti, c0, T)
        shift_mix(f32t["t1"], b_, 32, T, is_s, col("mu2"), 2, 26, last, max(c0 - SEQ, 0) // 4, omucol=col2("mu2"))
        sigm(f32t["t2"][0:32, 0:T], f32t["t1"][0:32, 0:T], r=[f32t["t1"]], w=[f32t["t2"]])
        cp(SG2[:, c0:c0 + T], f32t["t2"][0:32, 0:T], r=[f32t["t2"]], w=[(SG2, ti)])

    if debug is not None and debug.get("_stop") == "lora":
        dump("TW", TW, [128, NT]); dump("SG1", SG1, [128, NT]); dump("SG2", SG2, [32, NT]); dump("lastc", lastc.rearrange("p a b -> p (a b)"), [128, 27 * 17])
        n = P.emit()
        return nc, dbg, n

    def chunk3(ap, C):
        return ap.rearrange("p (c t) -> p c t", t=C)

    def to_tm(dst, src_bf, T, C, NCH):
        for c4 in range(0, NCH, 8):
            b_ = bank(); bb = b_.bitcast(BF16)
            for c in range(c4, min(c4 + 8, NCH)):
                tr(bb[0:C, (c - c4) * 128:(c - c4 + 1) * 128], src_bf[:, c * C:(c + 1) * C], identB, r=[src_bf, identB], w=[b_])
            n8 = min(8, NCH - c4)
            cp(dst[0:C, c4:c4 + n8, :], bb[0:C, 0:n8 * 128].rearrange("p (c f) -> p c f", f=128), eng="act", r=[b_], w=[dst])

    def cumprod(dst, src, T, C, NCH, rst):
        cp(chunk3(rst[:, 0:T], C)[:, :, 0:1], chunk3(src[:, 0:T], C)[:, :, 0:1], eng="act", r=[src], w=[rst])
        P.add("dve", lambda e: e.tensor_tensor_scan(out=dst[:, 0:T], data0=src[:, 0:T], data1=rst[:, 0:T], initial=1.0,
                                                    op0=ALU.mult, op1=ALU.max), r=[src, rst], w=[dst])
        if C != 64:
            memset(chunk3(rst[:, 0:T], C)[:, :, 0:1], 0.0, eng="pool", w=[rst])

    def lastbc(ap, T, C, NCH):
        return chunk3(ap[:, 0:T], C)[:, :, C - 1:C].broadcast_to([128, NCH, C])


    def rwkv_front(j, ti, c0, T, C, NCH, is_s, last, wb, gi):
        F_ = f32t; B_ = bft; M_ = masks[C]; jc = slice(j * 128, (j + 1) * 128)
        I_ = IFs[gi % 2]
        if gi >= 2:
            P.add("need", ("B", gi - 2))
        X = lambda n: F_[n][:, 0:T]
        Bx = lambda n: B_[n][:, 0:T]
        Gm = I_["Gm"]; bon = I_["bon"]; Gb = I_["Gb"]; rt = I_["rt"]
        so_ = max(c0 - SEQ, 0) // 4
        if is_s and so_ == 0:
            proj_samples(wb, 4, PsmpR)
        PJ = (lambda lo, hi: PsmpR[:, lo // 128, so_ * 4:so_ * 4 + T]) if is_s else (lambda lo, hi: proj(wb, lo, hi, ti, c0, T)[:, 0:T])
        br = PJ(0, 128); shift_mix(F_["xr"], br, 128, T, is_s, col("mu", j), 3, j, last, so_, omucol=col2("mu", j))
        bk_ = PJ(128, 256); shift_mix(F_["xk"], bk_, 128, T, is_s, col("mu", 8 + j), 4, 8 + j, last, so_, omucol=col2("mu", 8 + j))
        bv = PJ(256, 384); shift_mix(F_["xv"], bv, 128, T, is_s, col("mu", 16 + j), 5, 16 + j, last, so_, omucol=col2("mu", 16 + j))
        bgb = PJ(384, 512)
        sigm(Gb[:, 0:T], bgb[:, 0:T], r=[bgb], w=[Gb])
        bw = bank(); mm(bw[:, 0:T], w2b[0:64, jc], TW[0:64, c0:c0 + T], True, True, r=[w2b, (TW, ti)], w=[bw])
        act(X("u1"), bw[:, 0:T], AF.Tanh, bias=hw0[:, j:j + 1], scale=0.5, r=[bw], w=[F_["u1"]])
        act(X("u1"), X("u1"), AF.Exp, bias=C1, scale=C1)
        ba = bank(); mm(ba[:, 0:T], w2b[64:128, jc], TW[64:128, c0:c0 + T], True, True, r=[w2b, (TW, ti)], w=[ba])
        sigm(X("aS"), ba[:, 0:T], r=[ba], w=[F_["aS"]], bias=ha0[:, j:j + 1])
        bg = bank(); mm(bg[:, 0:T], g2b[:, jc], SG1[:, c0:c0 + T], True, False, r=[g2b, (SG1, ti)], w=[bg])
        mm(bg[:, 0:T], g2c[:, jc], SG2[:, c0:c0 + T], False, True, r=[g2c, (SG2, ti)], w=[bg])
        cp(X("gS"), bg[:, 0:T], eng="act", r=[bg], w=[F_["gS"]])
        tt(Gb[:, 0:T], Gb[:, 0:T], X("gS"), ALU.mult, eng="pool", r=[Gb, F_["gS"]], w=[Gb])
        cumprod(Gm, F_["u1"], T, C, NCH, rstR)
        P.add("dve", lambda e: e.reciprocal(out=X("rG"), in_=Gm[:, 0:T]), r=[Gm], w=[F_["rG"]])
        cp(chunk3(X("Gmm"), C)[:, :, 1:C], chunk3(Gm[:, 0:T], C)[:, :, 0:C - 1], r=[Gm], w=[F_["Gmm"]])
        memset(chunk3(X("Gmm"), C)[:, :, 0:1], 1.0, eng="dve", w=[F_["Gmm"]])
        aff(X("kap"), X("xk"), col("kk", j), 0.0, r=[F_["xk"], pv], w=[F_["kap"]])
        tt(Bx("sqb"), X("kap"), X("kap"), ALU.mult, eng="pool")
        bn = bank(); mm(bn[:, 0:T], blkones, Bx("sqb"), True, True, w=[bn])
        ts(X("n1"), bn[:, 0:T], 1.0, 1e-24, ALU.mult, ALU.add, r=[bn], w=[F_["n1"]])
        rsqrt_dve(X("u2"), X("n1"), X("n2"), [F_["u2"], F_["n1"], F_["n2"]])
        tt(X("kap"), X("kap"), X("u2"), ALU.mult)
        aff(X("u2"), X("aS"), col("ka", j), omka[:, j:j + 1], r=[F_["aS"], pv, omka], w=[F_["u2"]])
        tt(X("kmod"), X("xk"), X("u2"), ALU.mult, eng="pool")
        tt(X("bS"), X("aS"), X("kap"), ALU.mult)
        stt(Bx("sqb"), X("xr"), col("rk", j), X("kmod"), ALU.mult, ALU.mult, r=[F_["xr"], F_["kmod"], pv], w=[B_["sqb"]])
        bn2 = bank(); mm(bn2[:, 0:T], blkones, Bx("sqb"), True, True, w=[bn2])
        tt(bon[:, 0:T], bn2[:, 0:T], X("xv"), ALU.mult, r=[bn2, F_["xv"]], w=[bon])
        tt(Bx("kpt"), X("kap"), X("Gmm"), ALU.mult)
        tt(rt[:, 0:T], X("xr"), Gm[:, 0:T], ALU.mult, eng="pool", r=[F_["xr"], Gm], w=[rt])
        cp(KR[:, 0:NCH, 0:C], chunk3(Bx("kpt"), C), eng="pool", r=[B_["kpt"]], w=[KR])
        cp(KR[:, 0:NCH, 64:64 + C], chunk3(rt[:, 0:T], C), eng="pool", r=[rt], w=[KR])
        tt(X("u2"), X("kmod"), X("rG"), ALU.mult); cp(Bx("ktl"), X("u2"), eng="act")
        tt(chunk3(Bx("khr"), C), chunk3(X("u2"), C), lastbc(Gm, T, C, NCH), ALU.mult, r=[F_["u2"], Gm], w=[B_["khr"]])
        tt(X("u2"), X("bS"), X("rG"), ALU.mult); cp(Bx("bt"), X("u2"), eng="act")
        tt(chunk3(Bx("bh"), C), chunk3(X("u2"), C), lastbc(Gm, T, C, NCH), ALU.mult, r=[F_["u2"], Gm], w=[B_["bh"]])
        cp(Bx("vTr"), X("xv"), eng="act")
        to_tm(I_["Vtr"], B_["vTr"], T, C, NCH); to_tm(I_["Ktm"], B_["kpt"], T, C, NCH)
        to_tm(I_["bhtm"], B_["bh"], T, C, NCH); to_tm(I_["khrtm"], B_["khr"], T, C, NCH)
        v4_ = lambda b: b[0:C, :].rearrange("p (c x t) -> p c x t", x=2, t=64)
        mk = lambda m_: m_.unsqueeze(1).broadcast_to([C, NCH, C])
        for h in range(2):
            hs_ = slice(64 * h, 64 * h + 64)
            dv = lambda t_: t_[0:C, 0:NCH, h * 64:h * 64 + C]
            b1 = bank()
            for c in range(NCH):
                mm(b1[0:C, c * 128:(c + 1) * 128], B_["bt"][hs_, c * C:(c + 1) * C], KR[hs_, c, :], True, True, r=[B_["bt"], KR], w=[b1])
            tt(dv(I_["Z"]), v4_(b1)[:, 0:NCH, 0, 0:C], mk(M_["nstrict"]), ALU.mult, r=[b1, M_["nstrict"]], w=[I_["Z"]])
            tt(dv(I_["PbT"]), v4_(b1)[:, 0:NCH, 1, 0:C], mk(M_["incl"]), ALU.mult, r=[b1, M_["incl"]], w=[I_["PbT"]])
            b2 = bank()
            for c in range(NCH):
                mm(b2[0:C, c * 128:(c + 1) * 128], B_["ktl"][hs_, c * C:(c + 1) * C], KR[hs_, c, :], True, True, r=[B_["ktl"], KR], w=[b2])
            tt(dv(I_["AkT"]), v4_(b2)[:, 0:NCH, 0, 0:C], mk(M_["strict"]), ALU.mult, r=[b2, M_["strict"]], w=[I_["AkT"]])
            tt(dv(I_["PkT"]), v4_(b2)[:, 0:NCH, 1, 0:C], mk(M_["incl"]), ALU.mult, r=[b2, M_["incl"]], w=[I_["PkT"]])
            b3 = bank()
            for c in range(NCH):
                mm(b3[0:C, c * 128:c * 128 + C], B_["kpt"][hs_, c * C:(c + 1) * C], B_["bt"][hs_, c * C:(c + 1) * C], True, True, r=[B_["kpt"], B_["bt"]], w=[b3])
            tt(dv(I_["Y"]), v4_(b3)[:, 0:NCH, 0, 0:C], mk(M_["nlstrict"]), ALU.mult, r=[b3, M_["nlstrict"]], w=[I_["Y"]])
        P.add("mark", ("F", gi))

    def rwkv_back(j, ti, c0, T, C, NCH, is_s, last, oT, gi):
        F_ = f32t; M_ = masks[C]
        I_ = IFs[gi % 2]
        L = 6 if C == 64 else 2
        X = lambda n: F_[n][:, 0:T]
        Gm = I_["Gm"]; bon = I_["bon"]; Gb = I_["Gb"]; rt = I_["rt"]
        Osq2 = Ncomb; onr = tm["Un"]; e1 = Mblk.bitcast(F32).rearrange("p c f -> p (c f)")
        if is_s:
            s0 = (c0 - SEQ) // 4
            memset(R0in, 0.0)
            dma("sp", R0in[0:64, :, 0:64], sr_in[s0:s0 + 4, 2 * j].rearrange("s i j -> i s j"), r=[sr_in], w=[R0in])
            dma("sp", R0in[64:128, :, 64:128], sr_in[s0:s0 + 4, 2 * j + 1].rearrange("s i j -> i s j"), r=[sr_in], w=[R0in])
            b_ = bank()
            for q in range(4):
                tr(b_[:, q * 128:(q + 1) * 128], R0in[:, q, :], identF, w=[b_])
            cp(R0blk[:, 0:4, :], b_.rearrange("p (q f) -> p q f", f=128), r=[b_], w=[R0blk])
            cp(R0bf, R0blk, eng="pool")
        P.add("need", ("F", gi))
        hv = lambda t_, c_lo, c_hi: t_[0:C, c_lo:c_hi, :].rearrange("p c (h t) -> p c h t", t=64)[:, :, :, 0:C]
        tt(hv(Abm["TTa"], 0, NCH), hv(I_["Z"], 0, NCH), identF[0:C, 0:C].unsqueeze(1).unsqueeze(1).broadcast_to([C, NCH, 2, C]), ALU.add,
           r=[I_["Z"], identF], w=[Abm["TTa"]])
        Yc, Zc, Yn, Zn, TTc, TTn = I_["Y"], I_["Z"], Abm["Y2"], Abm["Z2"], Abm["TTa"], Abm["TTb"]
        pr = [(c, h) for c in range(NCH) for h in range(2)]

        def batch_mm(dst, lhs, rhs, wid_l, wid_r, post):
            for p8 in range(0, len(pr), 8):
                b_ = bank(); grp = pr[p8:p8 + 8]
                for q, (c, h) in enumerate(grp):
                    mm(b_[0:C, q * 64:q * 64 + wid_r], lhs[0:C, c, h * 64:h * 64 + wid_l], rhs[0:C, c, h * 64:h * 64 + wid_r], True, True,
                       r=[lhs, rhs], w=[b_])
                cA = grp[0][0]; cB = grp[-1][0] + 1
                src = b_[0:C, 0:(cB - cA) * 128].rearrange("p (c h t) -> p c h t", h=2, t=64)[:, :, :, 0:wid_r]
                d = dst[0:C, cA:cB, :].rearrange("p c (h t) -> p c h t", t=64)[:, :, :, 0:wid_r]
                post(d, src, b_)

        cpy = lambda d, s_, b_: cp(d, s_, eng="act", r=[b_], w=[d])
        for n in range(1, L):
            batch_mm(Yn, Zc, Yc, C, C, cpy)
            if n < L - 1:
                batch_mm(Zn, Yc, Zc, C, C, cpy)
            for p8 in range(0, len(pr), 8):
                b_ = bank(); grp = pr[p8:p8 + 8]
                for q, (c, h) in enumerate(grp):
                    mm(b_[0:C, q * 64:q * 64 + C], Yn[0:C, c, h * 64:h * 64 + C], TTc[0:C, c, h * 64:h * 64 + C], True, True, r=[Yn, TTc], w=[b_])
                cA = grp[0][0]; cB = grp[-1][0] + 1
                src = b_[0:C, 0:(cB - cA) * 128].rearrange("p (c h t) -> p c h t", h=2, t=64)[:, :, :, 0:C]
                tt(hv(TTn, cA, cB), src, hv(TTc, cA, cB), ALU.add, r=[b_, TTc], w=[TTn])
            Yc, Yn = Yn, Yc; Zc, Zn = Zn, Zc; TTc, TTn = TTn, TTc
        TTf = TTc
        batch_mm(tm["W1"], I_["AkT"], I_["Vtr"], C, 64, cpy)
        batch_mm(tm["Ktp"], TTf, I_["Ktm"], C, 64, cpy)
        neg = lambda d, s_, b_: ts(d, s_, -1.0, None, ALU.mult, r=[b_], w=[d])
        batch_mm(tm["Un"], TTf, tm["W1"], C, 64, neg)
        v3 = lambda b: b.rearrange("p (c f) -> p c f", f=128)
        bM = bank()
        for c in range(NCH):
            mm(bM[:, c * 128:(c + 1) * 128], tm["Ktp"][0:C, c, :], I_["bhtm"][0:C, c, :], True, True, r=[tm["Ktp"], I_["bhtm"]], w=[bM])
        tt(Mblk[:, 0:NCH, :], v3(bM)[:, 0:NCH, :], blkneg.unsqueeze(1).broadcast_to([128, NCH, 128]), ALU.mult, r=[bM, blkneg], w=[Mblk])
        bR = bank()
        for c in range(NCH):
            mm(bR[:, c * 128:(c + 1) * 128], tm["Ktp"][0:C, c, :], I_["PbT"][0:C, c, :], True, True, r=[tm["Ktp"], I_["PbT"]], w=[bR])
        for h in range(2):
            hs_ = slice(64 * h, 64 * h + 64)
            tt(chunk3(RpT[hs_, 0:T], C), chunk3(rt[hs_, 0:T], C), v3(bR)[hs_, 0:NCH, h * 64:h * 64 + C], ALU.subtract, r=[rt, bR], w=[RpT])
        bN = bank()
        for c in range(NCH):
            mm(bN[:, c * 128:(c + 1) * 128], I_["khrtm"][0:C, c, :], I_["Vtr"][0:C, c, :], True, False, r=[I_["khrtm"], I_["Vtr"]], w=[bN])
            mm(bN[:, c * 128:(c + 1) * 128], I_["bhtm"][0:C, c, :], tm["Un"][0:C, c, :], False, True, r=[I_["bhtm"], tm["Un"]], w=[bN])
        tt(Ncomb[:, 0:NCH, :], v3(bN)[:, 0:NCH, :], blkpos.unsqueeze(1).broadcast_to([128, NCH, 128]), ALU.mult, r=[bN, blkpos], w=[Ncomb])
        bo = banks[5]
        for c in range(NCH):
            csl = slice(c * C, (c + 1) * C)
            sbf = R0bf[:, c, :] if is_s else Srbf
            sin = R0blk[:, c, :] if is_s else Sr
            sout = Routs[:, c, :] if is_s else Sr
            kS_ = [R0bf] if is_s else [Srbf]
            oo = bo[0:C, c * 128:(c + 1) * 128]
            mm(oo, RpT[:, csl], sbf, True, False, r=[RpT] + kS_, w=[bo])
            for h in range(2):
                hb = slice(h * 64, (h + 1) * 64)
                mm(oo[:, hb], I_["PkT"][0:C, c, h * 64:h * 64 + C], I_["Vtr"][0:C, c, hb], False, False, r=[I_["PkT"], I_["Vtr"]], w=[bo])
                mm(oo[:, hb], I_["PbT"][0:C, c, h * 64:h * 64 + C], tm["Un"][0:C, c, hb], False, h == 1, r=[I_["PbT"], tm["Un"]], w=[bo])
            b3 = bank()
            mm(b3[:, 0:128], Mblk[:, c, :], sbf, True, True, r=[Mblk] + kS_, w=[b3])
            stt(Srt, sin, Gm[:, (c + 1) * C - 1:(c + 1) * C], Ncomb[:, c, :], ALU.mult, ALU.add,
                r=[R0blk if is_s else Sr, Gm, Ncomb], w=[Srt])
            tt(sout, Srt, b3[:, 0:128], ALU.add, r=[Srt, b3], w=[Routs if is_s else Sr])
            if not is_s:
                cp(Srbf, Sr, eng="act")
        cp(Osb2[0:C, 0:NCH, :], bo.rearrange("p (c f) -> p c f", f=128)[0:C, 0:NCH, :], eng="act", r=[bo], w=[Osb2])
        if is_s:
            b_ = bank()
            for q in range(4):
                tr(b_[:, q * 128:(q + 1) * 128], Routs[:, q, :], identF, w=[b_])
            cp(R0in[:, 0:4, :], b_.rearrange("p (q f) -> p q f", f=128), r=[b_], w=[R0in])
            dma("sp", rs[s0:s0 + 4, 2 * j].rearrange("s i j -> i s j"), R0in[0:64, :, 0:64], r=[R0in], w=[rs])
            dma("sp", rs[s0:s0 + 4, 2 * j + 1].rearrange("s i j -> i s j"), R0in[64:128, :, 64:128], r=[R0in], w=[rs])
        elif last:
            b_ = bank(); tr(b_[:, 0:128], Sr, identF, w=[b_]); cp(Srt, b_[:, 0:128])
            dma("sp", rp[2 * j], Srt[0:64, 0:64], r=[Srt], w=[rp]); dma("sp", rp[2 * j + 1], Srt[64:128, 64:128], r=[Srt], w=[rp])
        G2 = NCH * 2
        og = Osb2[0:C, 0:NCH, :].rearrange("p c (h i) -> p (c h) i", i=64)
        oq = Osq2[0:C, 0:NCH, :].rearrange("p c (h i) -> p (c h) i", i=64)
        P.add("dve", lambda e: e.reduce_sum(out=st3[0:C, 0:G2], in_=og, axis=AX.X), r=[Osb2], w=[st3])
        ts(st3[0:C, 0:G2], st3[0:C, 0:G2], 1.0 / 64, None, ALU.mult)
        tt(og, og, st3[0:C, 0:G2].unsqueeze(2).broadcast_to([C, G2, 64]), ALU.subtract, r=[Osb2, st3], w=[Osb2])
        tt(oq, og, og, ALU.mult, eng="pool", r=[Osb2], w=[Osq2])
        P.add("dve", lambda e: e.reduce_sum(out=st4[0:C, 0:G2], in_=oq, axis=AX.X), r=[Osq2], w=[st4])
        rsqrt_(st4[0:C, 0:G2], st4[0:C, 0:G2], 1.0 / 64, LNX_EPS, stC[0:C, 0:G2], stD[0:C, 0:G2])
        tt(onr[0:C, 0:NCH, :].rearrange("p c (h i) -> p (c h) i", i=64), og, st4[0:C, 0:G2].unsqueeze(2).broadcast_to([C, G2, 64]), ALU.mult,
           r=[Osb2, st4], w=[onr])
        b_ = bank(); bb = b_.bitcast(BF16)
        for c in range(NCH):
            tr(bb[:, c * C:(c + 1) * C], onr[0:C, c, :], identB[0:C, 0:C], r=[onr, identB], w=[b_])
        ts(e1[:, 0:T], bb[:, 0:T], col("lw", j), col("lbb", j), ALU.mult, ALU.add, r=[b_, pv], w=[Mblk])
        tt(e1[:, 0:T], e1[:, 0:T], bon[:, 0:T], ALU.add, eng="pool", r=[Mblk, bon], w=[Mblk])
        tt(e1[:, 0:T], e1[:, 0:T], Gb[:, 0:T], ALU.mult, eng="pool", r=[Mblk, Gb], w=[Mblk])
        P.add("need", ("oTw", gi))
        tt(mT[:, j, c0:c0 + T], e1[:, 0:T], oT[:, 0:T], ALU.add, r=[Mblk, oT], w=[(mT, ti)])
        P.add("mark", ("oTr", gi))
        P.add("mark", ("B", gi))

    NBLK = 8 if debug is None else debug.get('nblk', 8)

    def hgrn_tile(j, ti, c0, T, C, NCH, is_s, last, wb, oT, gi):
        M_ = masks[C]; F_ = f32t; B_ = bft
        if is_s:
            s0 = (c0 - SEQ) // 4
            dma("sp", S0all, sh_in[s0:s0 + 4, j].rearrange("s k v -> k s v"), r=[sh_in], w=[S0all])
            cp(S0bf, S0all, eng="pool")
        so_ = max(c0 - SEQ, 0)
        if is_s and so_ == 0:
            proj_samples(wb, 5, PsmpH)
        PJ = (lambda lo, hi: PsmpH[:, lo // 128, so_:so_ + T]) if is_s else (lambda lo, hi: proj(wb, lo, hi, ti, c0, T)[:, 0:T])
        bf_ = PJ(128, 256)
        act(F_["th"][:, 0:T], bf_[:, 0:T], AF.Tanh, scale=0.5, r=[bf_], w=[F_["th"]])
        aff(F_["fS"][:, 0:T], F_["th"][:, 0:T], hs1[:, j:j + 1], hs2[:, j:j + 1], r=[F_["th"], hs1, hs2], w=[F_["fS"]])
        aff(F_["kS"][:, 0:T], F_["th"][:, 0:T], hs1n[:, j:j + 1], hs2c[:, j:j + 1], r=[F_["th"], hs1n, hs2c], w=[F_["kS"]])
        cumprod(F_["Pc"], F_["fS"], T, C, NCH, rstH)
        P.add("dve", lambda e, T=T: e.reciprocal(out=F_["rP"][:, 0:T], in_=F_["Pc"][:, 0:T]), r=[F_["Pc"]], w=[F_["rP"]])
        bq = PJ(0, 128)
        tt(B_["qt"][:, 0:T], bq[:, 0:T], F_["Pc"][:, 0:T], ALU.mult, r=[bq, F_["Pc"]], w=[B_["qt"]])
        tt(F_["t1"][:, 0:T], F_["kS"][:, 0:T], F_["rP"][:, 0:T], ALU.mult, eng="pool", r=[F_["kS"], F_["rP"]], w=[F_["t1"]])
        cp(B_["kt"][:, 0:T], F_["t1"][:, 0:T], eng="act", r=[F_["t1"]], w=[B_["kt"]])
        tt(chunk3(B_["kh"][:, 0:T], C), chunk3(F_["t1"][:, 0:T], C), lastbc(F_["Pc"], T, C, NCH), ALU.mult, r=[F_["t1"], F_["Pc"]], w=[B_["kh"]])
        bi = PJ(256, 384)
        cp(B_["vT"][:, 0:T], bi[:, 0:T], eng="act", r=[bi], w=[B_["vT"]])
        bog = PJ(384, 512)
        sigm(F_["t2"][:, 0:T], bog[:, 0:T], r=[bog], w=[F_["t2"]])
        tt(F_["t2"][:, 0:T], bog[:, 0:T], F_["t2"][:, 0:T], ALU.mult, r=[bog, F_["t2"]], w=[F_["t2"]])
        bga = PJ(512, 640)
        sigm(F_["Ga"][:, 0:T], bga[:, 0:T], r=[bga], w=[F_["Ga"]])
        tt(F_["Ga"][:, 0:T], F_["Ga"][:, 0:T], F_["t2"][:, 0:T], ALU.mult, eng="pool", r=[F_["Ga"], F_["t2"]], w=[F_["Ga"]])
        to_tm(tm["Vtm"], B_["vT"], T, C, NCH); to_tm(tm["khtm"], B_["kh"], T, C, NCH)
        b_ = bank()
        for c in range(NCH):
            mm(b_[0:C, c * C:(c + 1) * C], B_["kt"][:, c * C:(c + 1) * C], B_["qt"][:, c * C:(c + 1) * C], True, True, w=[b_])
        tt(attnT[0:C, 0:NCH, 0:C], b_[0:C, 0:NCH * C].rearrange("p (c t) -> p c t", t=C),
           M_["incl"].unsqueeze(1).broadcast_to([C, NCH, C]), ALU.mult, r=[b_, M_["incl"]], w=[attnT])
        bo = banks[0]
        for c in range(NCH):
            csl = slice(c * C, (c + 1) * C)
            sbf = S0bf[:, c, :] if is_s else Shbf
            sin = S0all[:, c, :] if is_s else Sh
            sout = Souts[:, c, :] if is_s else Sh
            oo = bo[0:C, c * 128:(c + 1) * 128]
            mm(oo, attnT[0:C, c, 0:C], tm["Vtm"][0:C, c, :], True, False, r=[attnT, tm["Vtm"]], w=[bo])
            mm(oo, B_["qt"][:, csl], sbf, False, True, r=[B_["qt"], S0bf if is_s else Shbf], w=[bo])
            b2 = bank()
            mm(b2[:, 0:128], tm["khtm"][0:C, c, :], tm["Vtm"][0:C, c, :], True, True, r=[tm["khtm"], tm["Vtm"]], w=[b2])
            stt(sout, sin, F_["Pc"][:, (c + 1) * C - 1:(c + 1) * C], b2[:, 0:128], ALU.mult, ALU.add,
                r=[S0all if is_s else Sh, F_["Pc"], b2], w=[Souts if is_s else Sh])
            if not is_s:
                cp(Shbf, Sh, eng="act")
        cp(Osb[0:C, 0:NCH, :], bo.rearrange("p (c f) -> p c f", f=128)[0:C], eng="act", r=[bo], w=[Osb])
        if is_s:
            dma("sp", hs[s0:s0 + 4, j].rearrange("s k v -> k s v"), Souts, r=[Souts], w=[hs])
        elif last:
            dma("sp", hp[j], Sh, r=[Sh], w=[hp])
        tt(Osq[0:C, 0:NCH, :], Osb[0:C, 0:NCH, :], Osb[0:C, 0:NCH, :], ALU.mult, eng="pool", r=[Osb], w=[Osq])
        P.add("dve", lambda e, C=C, NCH=NCH: e.reduce_sum(out=st_[0:C, 0:NCH], in_=Osq[0:C, 0:NCH, :], axis=AX.X), r=[Osq], w=[st_])
        rsqrt_(st_[0:C, 0:NCH], st_[0:C, 0:NCH], 1.0 / 128, RMS_EPS, stA[0:C, 0:NCH], stB[0:C, 0:NCH])
        tt(Osq[0:C, 0:NCH, :], Osb[0:C, 0:NCH, :], st_[0:C, 0:NCH].unsqueeze(2).broadcast_to([C, NCH, 128]), ALU.mult, r=[Osb, st_], w=[Osq])
        tt(tm["onb"][0:C, 0:NCH, :], Osq[0:C, 0:NCH, :], nwbc[0:C, :].unsqueeze(1).broadcast_to([C, NCH, 128]), ALU.mult, r=[Osq, nwbc], w=[tm["onb"]])
        b_ = bank(); bb = b_.bitcast(BF16)
        for c in range(NCH):
            tr(bb[:, c * C:(c + 1) * C], tm["onb"][0:C, c, :], identB[0:C, 0:C], r=[tm["onb"], identB], w=[b_])
        if gi >= 2:
            P.add("need", ("oTr", gi - 2))
        tt(oT[:, 0:T], bb[:, 0:T], F_["Ga"][:, 0:T], ALU.mult, r=[b_, F_["Ga"]], w=[oT])
        P.add("mark", ("oTw", gi))

    def stream(kind):
        gi = 0
        for j in range(NBLK):
            seg = lambda col0: w_in[:, col0 + j * 128:col0 + (j + 1) * 128].rearrange("(k p) c -> p k c", p=128)
            if kind == "H":
                for si, col0 in enumerate((0, 1024, 2048, 3072, 7456)):
                    dma("pool", wbH[:, :, si * 128:(si + 1) * 128], seg(col0), r=[w_in], w=[wbH])
                memset(Sh, 0.0, eng="dve"); memset(Shbf, 0.0, eng="dve")
            elif kind == "RF":
                for si, col0 in enumerate((4096, 5120, 6144, 8480)):
                    dma("pool", wbR[:, :, si * 128:(si + 1) * 128], seg(col0), r=[w_in], w=[wbR])
                memset(prevs[:, 3:6], 0.0, eng="dve", w=[prevs])
            else:
                memset(Sr, 0.0, eng="dve"); memset(Srbf, 0.0, eng="dve")
            for ti, (c0, T, C, NCH, is_s) in enumerate(tiles):
                if debug is not None and 'tsel' in debug and ti not in debug['tsel']:
                    continue
                last = (c0 + T == SEQ)
                if kind == "H":
                    hgrn_tile(j, ti, c0, T, C, NCH, is_s, last, wbH, oTs[gi % 2], gi)
                elif kind == "RF":
                    rwkv_front(j, ti, c0, T, C, NCH, is_s, last, wbR, gi)
                else:
                    rwkv_back(j, ti, c0, T, C, NCH, is_s, last, oTs[gi % 2], gi)
                gi += 1

    opsH = []; opsF = []; opsB = []
    P.sink = opsH; cur[0] = "H"; stream("H")
    P.sink = opsF; cur[0] = "RF"; stream("RF")
    P.sink = opsB; cur[0] = "RB"; stream("RB")
    P.sink = None; cur[0] = None
    _n0 = len(P.ops)
    if debug is not None:
        P.lat_same = debug.get('lat_same', P.lat_same); P.lat_cross = debug.get('lat_cross', P.lat_cross)
        global PE_FIX, PE_RATE, DVE_FIX, ACT_FIX, POOL_FIX
        DVE_FIX = debug.get('dve_fix', DVE_FIX); ACT_FIX = debug.get('act_fix', ACT_FIX); POOL_FIX = debug.get('pool_fix', POOL_FIX)
        PE_FIX = debug.get('pe_fix', PE_FIX); PE_RATE = debug.get('pe_rate', PE_RATE)
    P.merge([opsH, opsF, opsB])
    if debug is not None and debug.get("xdeps"):
        from collections import Counter
        cx = Counter()
        for i in range(_n0, len(P.ops)):
            o = P.ops[i]
            for d in o["deps"]:
                od = P.ops[d]
                if d >= _n0 and od["stream"] != o["stream"]:
                    shared = (set(o["rk"]) | set(o["wk"])) & (set(od["rk"]) | set(od["wk"]))
                    cx[(o["stream"], o["eng"], od["eng"], str(sorted(map(str, shared)))[:80])] += 1
        for k, v in cx.most_common(30):
            print("XDEP", v, k)
    dump("mT", mT.rearrange("p a b -> p (a b)"), [128, 8 * NT])
    for p4 in range(0, 27, 4):
        b_ = bank(); n4 = min(4, 27 - p4); wdt = min(n4 * 128, RC - p4 * 128)
        for q in range(n4):
            tr(b_[0:17, q * 128:(q + 1) * 128], lastc[:, p4 + q, :], identF, r=[lastc, identF], w=[b_])
        cp(srow[:, 0:n4 * 128], b_[0:17, 0:n4 * 128], r=[b_], w=[srow])
        dma("sp", shp[:, p4 * 128:p4 * 128 + wdt], srow[0:1, 0:wdt], r=[srow], w=[shp])
        dma("sp", shs[:, p4 * 128:p4 * 128 + wdt], srow[1:17, 0:wdt], r=[srow], w=[shs])
    if debug is not None and debug.get("_stop") == 2:
        n = P.emit()
        return nc, dbg, n
    print('SBUF phase2 total', SBTOT[0], 'S2', S2.tot)
    S2.close()
    P.barrier(dummy)
    S1 = Scope()
    xT = S1.sb("xT", [128, 8, NT]); h2T = S1.sb("h2T", [128, 8, NT], BF16)
    g2s = S1.sb("g2s", [128, 8, 64]); sh2s = S1.sb("sh2s", [128, 8, 64]); gt1s = S1.sb("gt1s", [128, 8, 64]); gt2s = S1.sb("gt2s", [128, 8, 64])
    mk_mods(gt1s, 2); mk_mods(sh2s, 3); mk_mods(g2s, 4, "n2"); mk_mods(gt2s, 5)
    nbuf = dict(sq=S1.sb("sq", [128, 8, 512], BF16), rstd=S1.sb("rstd", [128, 512]), tmp=S1.sb("ntmp", [128, 8, 512]),
                xtm=[S1.sb("xtm", [128, D]) for _ in range(2)], i=0)
    nbuf["r1"] = nbuf["tmp"][:, 1, :]; nbuf["r2"] = nbuf["tmp"][:, 0, :]
    wo = S1.sb("wo", [128, 8, 128], BF16); wus = [S1.sb("wu", [128, 8, 256], BF16) for _ in range(2)]; wds = [S1.sb("wd", [128, 2, D], BF16)] * 2
    aTs = [S1.sb("aT", [128, 2, 512], BF16) for _ in range(2)]; rTs = [S1.sb("rT", [128, 1, 512], BF16)] * 2; t64 = S1.sb("t64", [128, 64])
    rtmp = [nbuf["rstd"], nbuf["tmp"][:, 0, :]]; rti = [0]
    tiles34 = [(i * 512, 512, False) for i in range(4)] + [(SEQ, 64, True)]
    for ti, (c0, T, is_s) in enumerate(tiles34):
        load_xT(xT[:, :, c0:c0 + T], c0, T, ti, fkeys=True)

    def resid_add(f, c0, T, is_s, ps, gtp, gts, split=False):
        kx = [(xT, f)]
        if not is_s:
            if split:
                tq = rtmp[rti[0] % len(rtmp)]; rti[0] += 1
                aff(tq[:, 0:T], ps[:, 0:T], gtp[:, f:f + 1], 0.0, r=[ps, gtp], w=[tq])
                tt(xT[:, f, c0:c0 + T], xT[:, f, c0:c0 + T], tq[:, 0:T], ALU.add, eng="pool", r=kx + [tq], w=kx)
            else:
                stt(xT[:, f, c0:c0 + T], ps[:, 0:T], gtp[:, f:f + 1], xT[:, f, c0:c0 + T], ALU.mult, ALU.add, r=[ps, gtp] + kx, w=kx)
        else:
            tt(t64[:, 0:T], ps[:, 0:T], gts[:, f, :], ALU.mult, r=[ps, gts], w=[t64])
            tt(xT[:, f, c0:c0 + T], xT[:, f, c0:c0 + T], t64[:, 0:T], ALU.add, r=kx + [t64], w=kx)

    for f in range(8):
        dma("pool", wo, w_out[:, f * 128:(f + 1) * 128].rearrange("(k p) c -> p k c", p=128), r=[w_out], w=[wo])
        for ti, (c0, T, is_s) in enumerate(tiles34):
            b_ = bank()
            for k in range(8):
                mm(b_[:, 0:T], wo[:, k, :], mT[:, k, c0:c0 + T], k == 0, k == 7, r=[wo, mT], w=[b_])
            resid_add(f, c0, T, is_s, b_, gt1p, gt1s)
    for ti, (c0, T, is_s) in enumerate(tiles34):
        norm_to(h2T[:, :, c0:c0 + T], xT[:, :, c0:c0 + T], T, g2p, sh2p, g2s, sh2s, is_s, 0, fkeys=True)
    steps = [(g, ti) for g in range(16) for ti in range(len(tiles34))]
    upb = [0]; dnb = [0]

    def load_wu(g):
        wu_ = wus[g % 2]
        dma("pool", wu_, w_up[:, g * 256:(g + 1) * 256].rearrange("(k p) c -> p k c", p=128), r=[w_up], w=[wu_])

    def load_wd(g):
        wd_ = wds[0]
        dma("pool", wd_, w_down[g * 256:(g + 1) * 256, :].rearrange("(q p) c -> p q c", p=128), r=[w_down], w=[wd_])

    def up(k):
        g, ti = steps[k]; c0, T, is_s = tiles34[ti]
        if ti == 0 and g + 1 < 16:
            load_wu(g + 1)
        wu_ = wus[g % 2]; aT_ = aTs[k % 2]; rT_ = rTs[k % 2]
        for q in range(2):
            b_ = banks[upb[0] % 4]; upb[0] += 1
            for kk_ in range(8):
                mm(b_[:, 0:T], wu_[:, kk_, q * 128:(q + 1) * 128], h2T[:, kk_, c0:c0 + T], kk_ == 0, kk_ == 7, r=[wu_, (h2T, 0)], w=[b_])
            act(rT_[:, 0, 0:T], b_[:, 0:T], AF.Relu, r=[b_], w=[rT_])
            tt(aT_[:, q, 0:T], rT_[:, 0, 0:T], rT_[:, 0, 0:T], ALU.mult, eng=("pool" if q == 1 else "dve"), r=[rT_], w=[aT_])

    def down(k):
        g, ti = steps[k]; c0, T, is_s = tiles34[ti]
        wd_ = wds[g % 2]; aT_ = aTs[k % 2]
        for f in range(8):
            b_ = banks[4 + dnb[0] % 4]; dnb[0] += 1
            for q in range(2):
                mm(b_[:, 0:T], wd_[:, q, f * 128:(f + 1) * 128], aT_[:, q, 0:T], q == 0, q == 1, r=[wd_, aT_], w=[b_])
            resid_add(f, c0, T, is_s, b_, gt2p, gt2s, split=(f >= 6))

    load_wu(0); load_wd(0)
    up(0)
    for k in range(len(steps)):
        if k + 1 < len(steps):
            up(k + 1)
        down(k)
        g_, ti_ = steps[k]
        if ti_ == len(tiles34) - 1 and g_ + 1 < 16:
            load_wd(g_ + 1)
    fnp = pv[:, rows["fn"]:rows["fn"] + 8]
    for ti, (c0, T, is_s) in enumerate(tiles34):
        sq = nbuf["sq"]; rstd = nbuf["rstd"]; tmp = nbuf["tmp"]
        xv_ = xT[:, :, c0:c0 + T]
        xk8 = [(xT, f_) for f_ in range(8)]
        tt(sq[:, :, 0:T], xv_, xv_, ALU.mult, r=xk8, w=[sq])
        b_ = bank()
        for fc in range(8):
            mm(b_[:, 0:T], onesB, sq[:, fc, 0:T], fc == 0, fc == 7, w=[b_])
        ts(rstd[:, 0:T], b_[:, 0:T], 1.0 / D, RMS_EPS, ALU.mult, ALU.add, r=[b_], w=[rstd])
        r1 = nbuf["r1"]; r2 = nbuf["r2"]
        cp(r1[:, 0:T], rstd[:, 0:T], eng="act", r=[rstd], w=[r1])
        rsqrt_dve(rstd[:, 0:T], r1[:, 0:T], r2[:, 0:T], [rstd, r1, r2])
        tt(tmp[:, :, 0:T], xv_, rstd[:, 0:T].unsqueeze(1).broadcast_to([128, 8, T]), ALU.mult, r=xk8 + [rstd], w=[tmp])
        tt(tmp[:, :, 0:T], tmp[:, :, 0:T], fnp.unsqueeze(2).broadcast_to([128, 8, T]), ALU.mult, r=[tmp, pv], w=[tmp])
        for sub in range(0, T, 128):
            n = min(128, T - sub)
            ytm = nbuf["xtm"][nbuf["i"] % 2]; nbuf["i"] += 1
            for half in range(2):
                b_ = bank()
                for f4 in range(4):
                    fc = half * 4 + f4
                    tr(b_[0:n, f4 * 128:(f4 + 1) * 128], tmp[:, fc, sub:sub + n], identF, r=[tmp, identF], w=[b_])
                cp(ytm[0:n, half * 512:(half + 1) * 512], b_[0:n, :], eng="act", r=[b_], w=[ytm])
            dst = yp[c0 + sub:c0 + sub + n, :] if not is_s else ys[sub:sub + n, :]
            dma("sp", dst, ytm[0:n, :], r=[ytm], w=[yp if not is_s else ys])
    n = P.emit()
    print('SBUF bytes/partition: G', G.tot, 'max', SBMAX[0])
    S1.close(); G.close()
    return nc, dbg, n


_CACHE = {}


def _core_inputs(inp, c):
    f = lambda a: np.ascontiguousarray(a, dtype=np.float32)
    m = {"xp": inp["x_prompt"][c], "xs": inp["x_sample"][16 * c:16 * c + 16].reshape(64, D),
         "sh": inp["state_hgrn"][0, 16 * c:16 * c + 16], "sr": inp["state_rwkv"][0, 16 * c:16 * c + 16],
         "ss": inp["state_shift"][0, 16 * c:16 * c + 16], "cp": inp["c_prompt"][c:c + 1], "cs": inp["c_sample"][16 * c:16 * c + 16]}
    for k in ("norm1_w", "norm2_w", "rwkv_w0", "rwkv_a0", "rwkv_k_k", "rwkv_k_a", "rwkv_lnx_w", "rwkv_lnx_b", "rwkv_r_k", "final_norm_w"):
        m[k] = inp[k].reshape(8, 128)
    m["ada_w"] = inp["ada_w"][0]; m["ada_b"] = inp["ada_b"].reshape(48, 128); m["w_in"] = inp["w_in"][0]
    m["lb_logits"] = inp["lb_logits"].reshape(16, 128); m["hgrn_norm_w"] = inp["hgrn_norm_w"].reshape(1, 128)
    m["rwkv_mu"] = inp["rwkv_mu"].reshape(1, RC); m["rwkv_w2"] = inp["rwkv_w2"][0]; m["rwkv_a2"] = inp["rwkv_a2"][0]
    m["rwkv_g2"] = inp["rwkv_g2"][0]; m["w_out"] = inp["w_out"][0]; m["w_up"] = inp["w_up"][0]; m["w_down"] = inp["w_down"][0]
    return {k: f(v) for k, v in m.items()}


def kernel(**inputs):
    inp = {k: np.asarray(v) for k, v in inputs.items()}
    if "nc" not in _CACHE:
        _CACHE["nc"] = build()[0]
    nc = _CACHE["nc"]
    in_maps = [_core_inputs(inp, c) for c in range(8)]
    res = run_bass_kernel_spmd(nc, in_maps, core_ids=list(range(8)))
    R = res.results
    g = lambda n: [np.asarray(R[c][n], dtype=np.float32) for c in range(8)]
    y_prompt = np.stack(g("yp"), 0)
    y_sample = np.concatenate([a.reshape(16, 4, D) for a in g("ys")], 0)
    hgrn_p = np.stack(g("hp"), 0)[None]
    rwkv_p = np.stack(g("rp"), 0)[None]
    shift_p = np.concatenate(g("shp"), 0)[None]
    hgrn_s = np.concatenate(g("hs"), 0)[None]
    rwkv_s = np.concatenate(g("rs"), 0)[None]
    shift_s = np.concatenate(g("shs"), 0)[None]
    return (y_prompt, y_sample, hgrn_p, rwkv_p, shift_p, hgrn_s, rwkv_s, shift_s)
```
